# Optimizing a Trainium2 kernel written in Bass

```python
import math
import jax, jax.numpy as jnp
from jax import lax
import numpy as np

D_MODEL = 1024
BATCH = 8
SEQ = 2048
DEPTH = 1
DEC_BATCH = 128
DEC_SEQ = 4
PAST_LEN = 16384
PAGE_SIZE = 128

D_MIX = D_MODEL
D_SSD = D_MIX // 2
SSD_HEAD_DIM = 64
SSD_HEADS = D_SSD // SSD_HEAD_DIM
SSD_GROUPS = 2
SSD_STATE = 128
SSD_CONV_W = 4
SSD_CHUNK = 128
SSD_CONV_DIM = D_SSD + 2 * SSD_GROUPS * SSD_STATE
D_S5 = D_MIX - D_SSD
S5_CH = 16
S5_GROUPS = D_S5 // S5_CH
S5_STATE = 64
S5_DT_MIN = 0.001
S5_DT_MAX = 0.1
SSD_DT_MIN = 0.001
SSD_DT_MAX = 0.1
D_FF = -(-8 * D_MODEL // (3 * 256)) * 256
IN_PROJ = D_SSD + SSD_CONV_DIM + SSD_HEADS + D_S5
N_ADA = 6
EPS = 1e-6

kernel_name = "hymba_ssd_s5_adaln_decode_step"


def _rmsnorm(x, g):
    xf = x.astype(jnp.float32)
    y = xf * lax.rsqrt(jnp.mean(xf * xf, axis=-1, keepdims=True) + EPS)
    return (y * g.astype(jnp.float32)).astype(x.dtype)


def _segsum(a):
    T = a.shape[-1]
    aa = jnp.broadcast_to(a[..., None], a.shape + (T,))
    aa = jnp.where(jnp.tril(jnp.ones((T, T), bool), -1), aa, 0.0)
    ss = jnp.cumsum(aa, axis=-2)
    return jnp.where(jnp.tril(jnp.ones((T, T), bool)), ss, -jnp.inf)


def _ssd_chunked(X, dA, Bm, Cm, h0):
    b, L, H, P = X.shape
    N = Bm.shape[-1]
    T = min(SSD_CHUNK, L)
    nc = L // T
    X = X.reshape(b, nc, T, H, P)
    Bm = Bm.reshape(b, nc, T, H, N)
    Cm = Cm.reshape(b, nc, T, H, N)
    A = dA.reshape(b, nc, T, H).transpose(0, 3, 1, 2)
    A_cs = jnp.cumsum(A, axis=-1)
    Lmat = jnp.exp(_segsum(A))
    scores = jnp.einsum("bclhn,bcshn->bhcls", Cm, Bm) * Lmat
    y_diag = jnp.einsum("bhcls,bcshp->bclhp", scores, X)
    decay_states = jnp.exp(A_cs[..., -1:] - A_cs).transpose(0, 2, 3, 1)
    states = jnp.einsum("bclhn,bclhp->bchpn", Bm * decay_states[..., None], X)
    states = jnp.concatenate([h0[:, None], states], axis=1)
    chunk_tot = jnp.pad(A_cs[..., -1], ((0, 0), (0, 0), (1, 0)))
    decay_chunk = jnp.exp(_segsum(chunk_tot))
    new_states = jnp.einsum("bhzc,bchpn->bzhpn", decay_chunk, states)
    y_off = jnp.einsum("bclhn,bchpn->bclhp", Cm, new_states[:, :-1]) * \
        jnp.exp(A_cs).transpose(0, 2, 3, 1)[..., None]
    y = (y_diag + y_off).reshape(b, L, H, P)
    return y, new_states[:, -1]


def _ssd_mixer(z, xbc, dt_raw, conv_buf, h0, conv_w, conv_b, dt_bias, A_log, D_skip, norm_g):
    b, L, _ = xbc.shape
    xbc_full = jnp.concatenate([conv_buf.astype(xbc.dtype), xbc], axis=1)
    conv = conv_b + sum(xbc_full[:, k:k + L] * conv_w[k] for k in range(SSD_CONV_W))
    conv_new = xbc_full[:, L:]
    act = jax.nn.silu(conv.astype(jnp.float32))
    xs = act[..., :D_SSD].reshape(b, L, SSD_HEADS, SSD_HEAD_DIM)
    rep = SSD_HEADS // SSD_GROUPS
    Bs = jnp.repeat(act[..., D_SSD:D_SSD + SSD_GROUPS * SSD_STATE].reshape(b, L, SSD_GROUPS, SSD_STATE), rep, axis=2)
    Cs = jnp.repeat(act[..., D_SSD + SSD_GROUPS * SSD_STATE:].reshape(b, L, SSD_GROUPS, SSD_STATE), rep, axis=2)
    dt = jax.nn.softplus(dt_raw.astype(jnp.float32) + dt_bias.astype(jnp.float32))
    A = -jnp.exp(A_log.astype(jnp.float32))
    y, h_new = _ssd_chunked(xs * dt[..., None], dt * A, Bs, Cs, h0.astype(jnp.float32))
    y = y + D_skip.astype(jnp.float32)[:, None] * xs
    y = y.reshape(b, L, D_SSD) * jax.nn.silu(z.astype(jnp.float32))
    yg = y.reshape(b, L, SSD_GROUPS, D_SSD // SSD_GROUPS)
    yg = yg * lax.rsqrt(jnp.mean(yg * yg, axis=-1, keepdims=True) + EPS)
    y = yg.reshape(b, L, D_SSD) * norm_g.astype(jnp.float32)
    return y, h_new, conv_new


def _s5_combine(e1, e2):
    a1r, a1i, b1r, b1i = e1
    a2r, a2i, b2r, b2i = e2
    return (a2r * a1r - a2i * a1i,
            a2r * a1i + a2i * a1r,
            a2r * b1r - a2i * b1i + b2r,
            a2r * b1i + a2i * b1r + b2i)


def _s5_mixer(u5, h0_re, h0_im, A_re, A_im, log_step, B_re, B_im, C_re, C_im, D_skip, w_glu, b_glu):
    b, L, _ = u5.shape
    f32 = jnp.float32
    uc = u5.astype(f32).reshape(b, L, S5_GROUPS, S5_CH)
    lr, li = A_re.astype(f32), A_im.astype(f32)
    step = jnp.exp(log_step.astype(f32))[:, None]
    mag = jnp.exp(lr * step)
    ab_re, ab_im = mag * jnp.cos(li * step), mag * jnp.sin(li * step)
    nr, ni = ab_re - 1.0, ab_im
    den = lr * lr + li * li
    f_re = (nr * lr + ni * li) / den
    f_im = (ni * lr - nr * li) / den
    Br, Bi = B_re.astype(f32), B_im.astype(f32)
    Bb_re = f_re[..., None] * Br - f_im[..., None] * Bi
    Bb_im = f_re[..., None] * Bi + f_im[..., None] * Br
    bu_re = jnp.einsum("gpc,blgc->blgp", Bb_re, uc)
    bu_im = jnp.einsum("gpc,blgc->blgp", Bb_im, uc)
    a_re = jnp.broadcast_to(ab_re, bu_re.shape)
    a_im = jnp.broadcast_to(ab_im, bu_im.shape)
    Ar, Ai, Hr, Hi = lax.associative_scan(_s5_combine, (a_re, a_im, bu_re, bu_im), axis=1)
    r0, i0 = h0_re.astype(f32)[:, None], h0_im.astype(f32)[:, None]
    h_re = Hr + Ar * r0 - Ai * i0
    h_im = Hi + Ar * i0 + Ai * r0
    y = jnp.einsum("gcp,blgp->blgc", C_re.astype(f32), h_re) - \
        jnp.einsum("gcp,blgp->blgc", C_im.astype(f32), h_im)
    y = y.reshape(b, L, D_S5) + D_skip.astype(f32) * uc.reshape(b, L, D_S5)
    g = jax.nn.gelu(y, approximate=False)
    out = g * jax.nn.sigmoid(g @ w_glu.astype(f32) + b_glu.astype(f32))
    return out, h_re[:, -1], h_im[:, -1]


def _layer(x, c, conv_buf, h_ssd0, s5_re0, s5_im0,
           w_ada, b_ada, norm1_g, w_in, conv_w, conv_b, ssd_dt_bias, ssd_A_log, ssd_D, ssd_norm_g,
           s5_A_re, s5_A_im, s5_log_step, s5_B_re, s5_B_im, s5_C_re, s5_C_im, s5_D, w_glu, b_glu,
           w_out, norm2_g, w_ffn_gate, w_ffn_up, w_ffn_down):
    mod = (jax.nn.silu(c) @ w_ada + b_ada)[:, None, :]
    sh1, sc1, g1, sh2, sc2, g2 = jnp.split(mod, N_ADA, axis=-1)
    u = _rmsnorm(x, norm1_g) * (1 + sc1) + sh1
    proj = u @ w_in
    o1 = D_SSD
    o2 = o1 + SSD_CONV_DIM
    o3 = o2 + SSD_HEADS
    y_ssd, h_ssd, conv_new = _ssd_mixer(proj[..., :o1], proj[..., o1:o2], proj[..., o2:o3], conv_buf, h_ssd0,
                                        conv_w, conv_b, ssd_dt_bias, ssd_A_log, ssd_D, ssd_norm_g)
    y_s5, re_new, im_new = _s5_mixer(proj[..., o3:], s5_re0, s5_im0, s5_A_re, s5_A_im, s5_log_step,
                                     s5_B_re, s5_B_im, s5_C_re, s5_C_im, s5_D, w_glu, b_glu)
    mix = jnp.concatenate([y_ssd, y_s5], axis=-1).astype(x.dtype)
    x = x + g1 * (mix @ w_out)
    v = _rmsnorm(x, norm2_g) * (1 + sc2) + sh2
    ff = (jax.nn.silu(v @ w_ffn_gate) * (v @ w_ffn_up)) @ w_ffn_down
    x = x + g2 * ff
    return x, h_ssd, conv_new, re_new, im_new


def setup_inputs(seed: int = 0) -> dict:
    key = jax.random.key(seed)
    ks = iter(jax.random.split(key, 48))

    def nrm(shape, scale):
        return jax.random.normal(next(ks), shape, jnp.float32) * scale

    def unif(shape, lo, hi):
        return jax.random.uniform(next(ks), shape, jnp.float32, lo, hi)

    Dp = DEPTH
    d = {}
    d["x_prompt"] = nrm((BATCH, SEQ, D_MODEL), 1.0)
    d["x_sample"] = nrm((DEC_BATCH, DEC_SEQ, D_MODEL), 1.0)
    d["c_prompt"] = nrm((BATCH, D_MODEL), 1.0)
    d["c_sample"] = nrm((DEC_BATCH, D_MODEL), 1.0)
    d["state_ssd"] = nrm((Dp, DEC_BATCH, SSD_HEADS, SSD_HEAD_DIM, SSD_STATE), 0.1)
    d["state_conv"] = nrm((Dp, DEC_BATCH, SSD_CONV_W - 1, SSD_CONV_DIM), 1.0)
    d["state_s5_re"] = nrm((Dp, DEC_BATCH, S5_GROUPS, S5_STATE), 0.1)
    d["state_s5_im"] = nrm((Dp, DEC_BATCH, S5_GROUPS, S5_STATE), 0.1)
    d["w_ada"] = nrm((Dp, D_MODEL, N_ADA * D_MODEL), 0.5 * D_MODEL ** -0.5)
    d["b_ada"] = nrm((Dp, N_ADA * D_MODEL), 0.02)
    d["norm1_g"] = 1.0 + nrm((Dp, D_MODEL), 0.02)
    d["w_in"] = nrm((Dp, D_MODEL, IN_PROJ), D_MODEL ** -0.5)
    d["conv_w"] = nrm((Dp, SSD_CONV_W, SSD_CONV_DIM), SSD_CONV_W ** -0.5)
    d["conv_b"] = nrm((Dp, SSD_CONV_DIM), 0.02)
    dt0 = jnp.exp(unif((Dp, SSD_HEADS), math.log(SSD_DT_MIN), math.log(SSD_DT_MAX)))
    d["ssd_dt_bias"] = dt0 + jnp.log(-jnp.expm1(-dt0))
    d["ssd_A_log"] = jnp.log(unif((Dp, SSD_HEADS), 1.0, 16.0))
    d["ssd_D"] = 1.0 + nrm((Dp, SSD_HEADS), 0.02)
    d["ssd_norm_g"] = 1.0 + nrm((Dp, D_SSD), 0.02)
    n_idx = jnp.arange(S5_STATE, dtype=jnp.float32)
    d["s5_A_re"] = -0.5 + nrm((Dp, S5_GROUPS, S5_STATE), 0.01)
    d["s5_A_im"] = jnp.broadcast_to(math.pi * n_idx, (Dp, S5_GROUPS, S5_STATE)) + nrm((Dp, S5_GROUPS, S5_STATE), 0.01)
    d["s5_log_step"] = unif((Dp, S5_GROUPS), math.log(S5_DT_MIN), math.log(S5_DT_MAX))
    d["s5_B_re"] = nrm((Dp, S5_GROUPS, S5_STATE, S5_CH), (2.0 * S5_CH) ** -0.5)
    d["s5_B_im"] = nrm((Dp, S5_GROUPS, S5_STATE, S5_CH), (2.0 * S5_CH) ** -0.5)
    d["s5_C_re"] = nrm((Dp, S5_GROUPS, S5_CH, S5_STATE), (2.0 * S5_STATE) ** -0.5)
    d["s5_C_im"] = nrm((Dp, S5_GROUPS, S5_CH, S5_STATE), (2.0 * S5_STATE) ** -0.5)
    d["s5_D"] = nrm((Dp, D_S5), 1.0)
    d["w_glu"] = nrm((Dp, D_S5, D_S5), D_S5 ** -0.5)
    d["b_glu"] = nrm((Dp, D_S5), 0.02)
    d["w_out"] = nrm((Dp, D_MIX, D_MODEL), D_MIX ** -0.5)
    d["norm2_g"] = 1.0 + nrm((Dp, D_MODEL), 0.02)
    d["w_ffn_gate"] = nrm((Dp, D_MODEL, D_FF), D_MODEL ** -0.5)
    d["w_ffn_up"] = nrm((Dp, D_MODEL, D_FF), D_MODEL ** -0.5)
    d["w_ffn_down"] = nrm((Dp, D_FF, D_MODEL), D_FF ** -0.5)
    d["w_ada_f"] = nrm((D_MODEL, 2 * D_MODEL), 0.5 * D_MODEL ** -0.5)
    d["b_ada_f"] = nrm((2 * D_MODEL,), 0.02)
    d["normf_g"] = 1.0 + nrm((D_MODEL,), 0.02)
    return d


def reference(x_prompt, x_sample, c_prompt, c_sample, state_ssd, state_conv, state_s5_re, state_s5_im,
              w_ada, b_ada, norm1_g, w_in, conv_w, conv_b, ssd_dt_bias, ssd_A_log, ssd_D, ssd_norm_g,
              s5_A_re, s5_A_im, s5_log_step, s5_B_re, s5_B_im, s5_C_re, s5_C_im, s5_D, w_glu, b_glu,
              w_out, norm2_g, w_ffn_gate, w_ffn_up, w_ffn_down, w_ada_f, b_ada_f, normf_g):
    layer_w = (w_ada, b_ada, norm1_g, w_in, conv_w, conv_b, ssd_dt_bias, ssd_A_log, ssd_D, ssd_norm_g,
               s5_A_re, s5_A_im, s5_log_step, s5_B_re, s5_B_im, s5_C_re, s5_C_im, s5_D, w_glu, b_glu,
               w_out, norm2_g, w_ffn_gate, w_ffn_up, w_ffn_down)

    def run(x, c, ssd0, conv0, re0, im0):
        ssd_n, conv_n, re_n, im_n = [], [], [], []
        for l in range(DEPTH):
            x, h, cb, r, i = _layer(x, c, conv0[l], ssd0[l], re0[l], im0[l], *[w[l] for w in layer_w])
            ssd_n.append(h)
            conv_n.append(cb)
            re_n.append(r)
            im_n.append(i)
        mod = (jax.nn.silu(c) @ w_ada_f + b_ada_f)[:, None, :]
        shf, scf = jnp.split(mod, 2, axis=-1)
        y = _rmsnorm(x, normf_g) * (1 + scf) + shf
        return y, jnp.stack(ssd_n), jnp.stack(conv_n), jnp.stack(re_n), jnp.stack(im_n)

    bp = x_prompt.shape[0]
    f32 = jnp.float32
    y_prompt, ssd_p, conv_p, re_p, im_p = run(
        x_prompt, c_prompt,
        jnp.zeros((DEPTH, bp, SSD_HEADS, SSD_HEAD_DIM, SSD_STATE), f32),
        jnp.zeros((DEPTH, bp, SSD_CONV_W - 1, SSD_CONV_DIM), x_prompt.dtype),
        jnp.zeros((DEPTH, bp, S5_GROUPS, S5_STATE), f32),
        jnp.zeros((DEPTH, bp, S5_GROUPS, S5_STATE), f32))
    y_sample, ssd_s, conv_s, re_s, im_s = run(
        x_sample, c_sample, state_ssd, state_conv, state_s5_re, state_s5_im)
    return (y_prompt, y_sample, ssd_p, ssd_s, conv_p, conv_s, re_p, re_s, im_p, im_s)
```

```python
import math
import numpy as np
from contextlib import ExitStack
import concourse.bass as bass
import concourse.mybir as mybir
from concourse.bass_utils import run_bass_kernel_spmd

F32 = mybir.dt.float32
BF16 = mybir.dt.bfloat16
I32 = mybir.dt.int32
AF = mybir.ActivationFunctionType
ALU = mybir.AluOpType

NCORES = 8
D = 1024
SEQ = 2048
NS = 16
LS = 4
NTOK = SEQ + NS * LS
DFF = 2816
NJ = DFF // 128
INP = 2056
EPS = 1e-6
T5 = 32
TILES = [(0, 512), (512, 512), (1024, 512), (1536, 512), (2048, 64)]
PI = math.pi


class Buf:
    def __init__(self, name):
        self.name = name
        self.w = None
        self.r = []
        self.dsem = None
        self.dcnt = 0


class TL:
    def __init__(self, t, name):
        self.t = t
        self.name = name
        self.b = Buf(name)
        self.subs = {}

    def sub(self, k):
        if getattr(self, "nosub", False):
            return self.b
        if k not in self.subs:
            self.subs[k] = Buf("%s_%s" % (self.name, k))
        return self.subs[k]

    def allb(self):
        return [self.b] + list(self.subs.values())

    def __getitem__(self, k):
        return self.t[k]


class Sched:
    ENG = ["pe", "act", "dve", "pool", "sp"]

    def __init__(self, nc, es):
        self.nc = nc
        self.es = es
        self.eobj = {"pe": nc.tensor, "act": nc.scalar, "dve": nc.vector, "pool": nc.gpsimd, "sp": nc.sync}
        self.cnt = {e: 0 for e in self.ENG}
        self.sem = {e: es.enter_context(nc.semaphore("s_" + e)) for e in self.ENG}
        self.seen = {e: {} for e in self.ENG}
        self.dbufs = []
        self.ninst = 0
        self.dead = False
        self.pe_pending = None

    def _flush_pe(self):
        if self.pe_pending is not None:
            self.pe_pending.then_inc(self.sem["pe"], 1)
            self.cnt["pe"] += 1
            self.pe_pending = None

    def _deps(self, eng, reads, writes):
        deps = []
        for b in reads:
            if b.w is not None:
                deps.append(b.w)
        for b in writes:
            if b.w is not None:
                deps.append(b.w)
            deps.extend(b.r)
        waits = {}
        for (sem, val, key) in deps:
            if key == "pe" and eng == "pe":
                continue
            if self.seen[eng].get(key, 0) >= val:
                continue
            if key == "pe" and val > self.cnt["pe"]:
                self._flush_pe()
            if key not in waits or waits[key][1] < val:
                waits[key] = (sem, val)
        for key, (sem, val) in waits.items():
            self.seen[eng][key] = val
        return list(waits.values())

    def op(self, eng, fn, reads=(), writes=()):
        if self.dead:
            return None
        xr = [b for b in reads if getattr(b, "excl", False)]
        if xr:
            reads = [b for b in reads if not getattr(b, "excl", False)]
            writes = list(writes) + xr
        waits = self._deps(eng, reads, writes)
        e = self.eobj[eng]
        for (s_, v_) in waits:
            e.wait_ge(s_, v_)
        if eng == "pe":
            self.pe_pending = fn(e)
            tok = (self.sem[eng], self.cnt[eng] + 1, eng)
        else:
            self.cnt[eng] += 1
            tok = (self.sem[eng], self.cnt[eng], eng)
            fn(e).then_inc(self.sem[eng], 1)
        for b in reads:
            b.r.append(tok)
        for b in writes:
            b.w = tok
            b.r = []
        self.ninst += 1
        return tok

    def dma(self, eng, out, in_, reads=(), writes=(), buf=None, **kw):
        if self.dead:
            return None
        waits = self._deps(eng, reads, writes)
        if buf is None:
            buf = writes[0] if writes else reads[0]
        if buf.dsem is None:
            buf.dsem = self.es.enter_context(self.nc.semaphore("d_" + buf.name))
            self.dbufs.append(buf)
        buf.dcnt += 16
        tok = (buf.dsem, buf.dcnt, "d_" + buf.name)
        e = self.eobj[eng]
        for (s_, v_) in waits:
            e.wait_ge(s_, v_)
        e.dma_start(out=out, in_=in_, **kw).then_inc(buf.dsem, 16)
        for b in reads:
            b.r.append(tok)
        for b in writes:
            b.w = tok
            b.r = []
        self.ninst += 1
        return tok

    def barrier(self):
        if self.dead:
            return
        self._flush_pe()
        for e in self.ENG:
            waits = []
            for o in self.ENG:
                if o != e and self.cnt[o] > self.seen[e].get(o, 0):
                    waits.append((self.sem[o], self.cnt[o]))
                    self.seen[e][o] = self.cnt[o]
            for b in self.dbufs:
                key = "d_" + b.name
                if b.dcnt > self.seen[e].get(key, 0):
                    waits.append((b.dsem, b.dcnt))
                    self.seen[e][key] = b.dcnt
            for (s_, v_) in waits:
                self.eobj[e].wait_ge(s_, v_)

    def emit(self):
        pass


C_ID = 0
C_TRI = 128
C_NEG = 256
C_TRI64 = 384
C_NEG64 = 512
C_SEG64 = 640
C_SEGI = 768
CST_W = 784

P_BMOD = 0
P_GAIN = 64
P_CONV = 88
P_SSDFM = 128
P_S5P = 136
P_S5M = 184
P_SSD8 = 192
PRM_W = 194


def _consts():
    c = np.zeros((128, CST_W), np.float32)
    c[:, C_ID:C_ID + 128] = np.eye(128, dtype=np.float32)
    s = np.arange(128)[:, None]
    l = np.arange(128)[None, :]
    c[:, C_TRI:C_TRI + 128] = (s <= l).astype(np.float32)
    c[:, C_NEG:C_NEG + 128] = np.where(l >= s, 0.0, -30000.0)
    same = (s // LS == l // LS) & (s < 64) & (l < 64)
    c[:, C_TRI64:C_TRI64 + 128] = ((s <= l) & same).astype(np.float32)
    c[:, C_NEG64:C_NEG64 + 128] = np.where((l >= s) & same, 0.0, -30000.0)
    c[:, C_SEG64:C_SEG64 + 128] = same.astype(np.float32)
    j = np.arange(16)[None, :]
    c[:, C_SEGI:C_SEGI + 16] = ((s // LS == j) & (s < 64)).astype(np.float32)
    return c


def _fm(v, nt):
    return np.ascontiguousarray(np.asarray(v, np.float32).reshape(nt, 128).T)


def _params(inp):
    p = np.zeros((128, PRM_W), np.float32)
    p[:, P_BMOD:P_BMOD + 48] = _fm(inp["b_ada"][0], 48)
    p[:, P_BMOD + 48:P_BMOD + 64] = _fm(inp["b_ada_f"], 16)
    p[:, P_GAIN:P_GAIN + 8] = _fm(inp["norm1_g"][0], 8)
    p[:, P_GAIN + 8:P_GAIN + 16] = _fm(inp["norm2_g"][0], 8)
    p[:, P_GAIN + 16:P_GAIN + 24] = _fm(inp["normf_g"], 8)
    cw = inp["conv_w"][0]
    cv = np.zeros((128, 8, 5), np.float32)
    for k in range(4):
        cv[:, :, k] = _fm(cw[k], 8)
    cv[:, :, 4] = _fm(inp["conv_b"][0], 8)
    p[:, P_CONV:P_CONV + 40] = cv.reshape(128, 40)
    Dh = inp["ssd_D"][0]
    dfm = np.zeros((128, 4), np.float32)
    for pr in range(4):
        dfm[0:64, pr] = Dh[2 * pr]
        dfm[64:128, pr] = Dh[2 * pr + 1]
    p[:, P_SSDFM:P_SSDFM + 4] = dfm
    p[:, P_SSDFM + 4:P_SSDFM + 8] = _fm(inp["ssd_norm_g"][0], 4)

    def st(a):
        return np.ascontiguousarray(np.asarray(a, np.float32).reshape(16, 128).T)
    p[:, P_S5P:P_S5P + 16] = st(inp["s5_A_re"][0])
    p[:, P_S5P + 16:P_S5P + 32] = st(inp["s5_A_im"][0])
    p[:, P_S5P + 32:P_S5P + 48] = st(np.repeat(inp["s5_log_step"][0][:, None], 64, axis=1))
    p[:, P_S5M:P_S5M + 4] = _fm(inp["s5_D"][0], 4)
    p[:, P_S5M + 4:P_S5M + 8] = _fm(inp["b_glu"][0], 4)
    p[0:8, P_SSD8] = inp["ssd_dt_bias"][0]
    p[0:8, P_SSD8 + 1] = inp["ssd_A_log"][0]
    return p


def _s5mats(inp):
    Br, Bi = inp["s5_B_re"][0], inp["s5_B_im"][0]
    Cr, Ci = inp["s5_C_re"][0], inp["s5_C_im"][0]
    BT = np.zeros((128, 2, 16, 128), np.float32)
    CT = np.zeros((128, 2, 16, 32), np.float32)
    for s in range(16):
        for gl in range(2):
            g = 2 * s + gl
            r0 = (g % 8) * 16
            BT[r0:r0 + 16, 0, s, gl * 64:(gl + 1) * 64] = Br[g].T
            BT[r0:r0 + 16, 1, s, gl * 64:(gl + 1) * 64] = Bi[g].T
            CT[gl * 64:(gl + 1) * 64, 0, s, gl * 16:(gl + 1) * 16] = Cr[g].T
            CT[gl * 64:(gl + 1) * 64, 1, s, gl * 16:(gl + 1) * 16] = Ci[g].T
    return BT, CT


class Arena:
    def __init__(self, nc, es, words):
        self.t = es.enter_context(nc.sbuf_tensor("arena", [128, words], F32))
        self.words = words
        self.lo = 0
        self.hi = words

    def alloc(self, name, shape, dt, top=False):
        n = 1
        for d in shape[1:]:
            n *= d
        w = n if dt == F32 or dt == I32 else (n + 1) // 2
        w = (w + 3) // 4 * 4
        if top:
            self.hi -= w
            off = self.hi
        else:
            off = self.lo
            self.lo += w
        assert self.lo <= self.hi, "arena overflow at %s: lo=%d hi=%d" % (name, self.lo, self.hi)
        ap = self.t[:, off:off + w]
        if dt != F32:
            ap = ap.bitcast(dt)
        ap = ap[:, 0:n]
        if len(shape) == 3:
            ap = ap.rearrange("p (a b) -> p a b", b=shape[2])
        elif len(shape) == 4:
            ap = ap.rearrange("p (a b c) -> p a b c", b=shape[2], c=shape[3])
        if shape[0] < 128:
            ap = ap[0:shape[0]]
        return TL(ap, name)


class StopBuild(Exception):
    pass


def build(dbg=None, stop_after=None):
    nc = bass.Bass("TRN2", target_bir_lowering=False)

    SH = []

    def ckpt(name):
        if stop_after == name:
            SH[0].barrier()
            SH[0].dead = True
    dt_in = lambda name, shape: nc.dram_tensor(name, list(shape), F32, kind="ExternalInput").ap()
    dt_out = lambda name, shape: nc.dram_tensor(name, list(shape), F32, kind="ExternalOutput").ap()
    xin = dt_in("xin", [NTOK, D])
    cin = dt_in("cin", [17, D])
    wada = dt_in("wada", [D, 6144])
    wadaf = dt_in("wadaf", [D, 2048])
    win = dt_in("win", [D, INP])
    wglu = dt_in("wglu", [512, 512])
    wout = dt_in("wout", [D, D])
    wg = dt_in("wg", [D, DFF])
    wu = dt_in("wu", [D, DFF])
    wd = dt_in("wd", [DFF, D])
    cst_d = dt_in("cst", [128, CST_W])
    prm_d = dt_in("prm", [128, PRM_W])
    s5bt_d = dt_in("s5bt", [128, 2 * 16 * 128])
    s5ct_d = dt_in("s5ct", [128, 2 * 16 * 32])
    stssd_d = dt_in("stssd", [NS, 8, 64, 128])
    stconv_d = dt_in("stconv", [128, 8 * NS * 3])
    sts5_d = dt_in("sts5", [128, 2 * 16 * NS])
    yout = dt_out("yout", [NTOK, D])
    o_ssdp = dt_out("o_ssdp", [128, 512])
    o_ssds = dt_out("o_ssds", [NS, 8, 64, 128])
    o_conv = dt_out("o_conv", [128, 8 * 17 * 3])
    o_s5 = dt_out("o_s5", [128, 2 * 16 * 17])
    mixd = nc.dram_tensor("mixd", [128, 8, NTOK], BF16, kind="Internal").ap()
    dumps = {}

    with ExitStack() as es:
        S = Sched(nc, es)
        SH.append(S)
        A = Arena(nc, es, 53200)
        outbufs = []

        def dump(name, ap, shape, reads):
            if dbg is None or name not in dbg:
                return
            d = dt_out("dbg_" + name, shape)
            dumps[name] = shape
            b = Buf("dbg_" + name)
            S.dma("sp" if ap.dtype == F32 else "pool", d, ap, reads=reads, buf=b)
            outbufs.append(b)

        PB = [TL(es.enter_context(nc.psum_tensor("pb%d" % i, [128, 512], F32)), "pb%d" % i) for i in range(8)]
        for pb_ in PB:
            pb_.b.excl = True
            pb_.nosub = True

        def pbf(i):
            return PB[i].t[:].bitcast(BF16)

        cst = A.alloc("cst", [128, CST_W], F32)
        prm = A.alloc("prm", [128, PRM_W], F32)
        identb = A.alloc("identb", [128, 128], BF16)
        onesf = A.alloc("onesf", [128, 128], F32)
        mod = A.alloc("mod", [128, 64, 17], F32)
        amod = A.alloc("amod", [128, 24, 17], F32)
        s5fin = A.alloc("s5fin", [128, 2, 16, 17], F32)
        scT = A.alloc("scT", [128, 8, 17], BF16)
        LO_GLOBAL = A.lo

        ident = cst.t[:, C_ID:C_ID + 128]
        S.dma("sp", cst.t[:], cst_d, writes=[cst.b])
        S.dma("sp", prm.t[:], prm_d, writes=[prm.b])
        S.op("act", lambda e: e.activation(out=identb.t[:], in_=ident, func=AF.Copy), reads=[cst.b], writes=[identb.b])
        S.op("dve", lambda e: e.memset(onesf.t[:], 1.0), writes=[onesf.b])

        def chunkmod(i):
            return mod.t[:, 8 * i:8 * i + 8, :]

        cs = A.alloc("cs", [17, D], F32)
        slabs = [A.alloc("adaslab%d" % i, [128, 8, 512], BF16) for i in range(3)]
        S.dma("sp", cs.t[:], cin, writes=[cs.b])
        S.op("act", lambda e: e.activation(out=cs.t[:], in_=cs.t[:], func=AF.Silu), reads=[cs.b], writes=[cs.b])
        for kt in range(8):
            S.op("pe", lambda e, kt=kt: e.transpose(PB[2].t[:, kt * 17:(kt + 1) * 17], cs.t[:, kt * 128:(kt + 1) * 128],
                                                    cst.t[0:17, C_ID:C_ID + 17]),
                 reads=[cs.b, cst.b], writes=[PB[2].b])
        S.op("act", lambda e: e.activation(out=scT.t[:].rearrange("p k s -> p (k s)"), in_=PB[2].t[:, 0:136], func=AF.Copy),
             reads=[PB[2].b], writes=[scT.b])
        wada_v = wada.rearrange("(kt p) n -> p kt n", p=128)
        wadaf_v = wadaf.rearrange("(kt p) n -> p kt n", p=128)

        def slab_src(i):
            if i < 12:
                return wada_v[:, :, i * 512:(i + 1) * 512]
            return wadaf_v[:, :, (i - 12) * 512:(i - 11) * 512]

        def load_slab(i):
            sl = slabs[i % 3]
            for kh in range(2):
                S.dma("pool", sl.t[:, 4 * kh:4 * kh + 4, :], slab_src(i)[:, 4 * kh:4 * kh + 4, :], writes=[sl.b])
        load_slab(0)
        load_slab(1)
        for i in range(4):
            if i + 2 < 4:
                load_slab(i + 2)
            sl = slabs[i % 3]
            pb = PB[i % 2]
            for fc in range(4):
                for kt in range(8):
                    S.op("pe", lambda e, fc=fc, kt=kt, sl=sl, pb=pb: e.matmul(
                        pb.t[:, fc * 17:(fc + 1) * 17], sl.t[:, kt, fc * 128:(fc + 1) * 128], scT.t[:, kt, :],
                        start=(kt == 0), stop=(kt == 7)), reads=[sl.b, scT.b], writes=[pb.b])
            S.op("dve", lambda e, i=i, pb=pb: e.tensor_tensor(
                out=mod.t[:, 4 * i:4 * i + 4, :], in0=pb.t[:, 0:68].rearrange("p (c s) -> p c s", s=17),
                in1=prm.t[:, P_BMOD + 4 * i:P_BMOD + 4 * i + 4].unsqueeze(2).to_broadcast([128, 4, 17]), op=ALU.add),
                reads=[pb.b, prm.b], writes=[mod.b])
        def make_amod(lst):
          for k, (sci, gi) in lst:
            S.op("dve", lambda e, k=k, sci=sci, gi=gi: e.scalar_tensor_tensor(
                out=amod.t[:, 8 * k:8 * k + 8, :], in0=chunkmod(sci), scalar=1.0,
                in1=prm.t[:, P_GAIN + 8 * gi:P_GAIN + 8 * gi + 8].unsqueeze(2).to_broadcast([128, 8, 17]),
                op0=ALU.add, op1=ALU.mult), reads=[mod.b, prm.b], writes=[amod.b])
        make_amod([(0, (1, 0))])
        dump("mod", mod.t[:].rearrange("p c s -> p (c s)"), [128, 64 * 17], [mod.b])
        S.barrier()
        S.emit()
        A.lo = LO_GLOBAL

        MOD_SH1, MOD_G1, MOD_SH2, MOD_G2, MOD_SHF = 0, 2, 3, 5, 6

        def expand_mod(name, src_ap, srcbufs):
            t = A.alloc(name, [128, 8, 64], F32)
            S.op("dve", lambda e: e.tensor_copy(out=t.t[:].rearrange("p k (s b) -> p k s b", b=LS),
                                                in_=src_ap.unsqueeze(3).to_broadcast([128, 8, NS, LS])),
                 reads=srcbufs, writes=[t.b])
            return t

        LO_P1 = A.lo
        mixt = [A.alloc("mixt%d" % i, [128, 8, 256], BF16) for i in range(2)]
        mixdb = [Buf("mixd%d" % i) for i in range(9)]
        win_sb = A.alloc("win_sb", [128, 8, INP], BF16)
        wglu_sb = A.alloc("wglu_sb", [128, 4, 512], BF16)
        s5BT = A.alloc("s5BT", [128, 2, 16, 128], BF16)
        s5CT = A.alloc("s5CT", [128, 2, 16, 32], BF16)
        win_v = win.rearrange("(kt p) n -> p kt n", p=128)
        for kh in range(4):
            for ch in range(2):
                S.dma("pool", win_sb.t[:, 2 * kh:2 * kh + 2, ch * 1028:(ch + 1) * 1028],
                      win_v[:, 2 * kh:2 * kh + 2, ch * 1028:(ch + 1) * 1028], writes=[win_sb.b])
        for a_ in range(4):
            S.dma("pool", s5BT.t[:].rearrange("p a s c -> p (a s c)")[:, a_ * 1024:(a_ + 1) * 1024],
                  s5bt_d[:, a_ * 1024:(a_ + 1) * 1024], writes=[s5BT.b])
        S.dma("pool", s5CT.t[:].rearrange("p a s c -> p (a s c)"), s5ct_d, writes=[s5CT.b])
        S.dma("pool", wglu_sb.t[:], wglu.rearrange("(kt p) n -> p kt n", p=128), writes=[wglu_sb.b])
        S.op("dve", lambda e: e.tensor_scalar(out=s5CT.t[:, 1], in0=s5CT.t[:, 1], scalar1=-1.0, scalar2=None, op0=ALU.mult),
             reads=[s5CT.b], writes=[s5CT.b])

        a1x = A.alloc("a1x", [128, 8, 64], F32)
        sh1x = A.alloc("sh1x", [128, 8, 64], F32)

        def fill_x(t, src_ap, srcbufs):
            S.op("dve", lambda e: e.tensor_copy(out=t.t[:].rearrange("p k (s b) -> p k s b", b=LS),
                                                in_=src_ap.unsqueeze(3).to_broadcast([128, 8, NS, LS])),
                 reads=srcbufs, writes=[t.b])
        adab = [TL(a1x.t[:].rearrange("p k t -> p (k t)").bitcast(BF16).rearrange("p (k c) -> p k c", c=128), "adab0"),
                TL(sh1x.t[:].rearrange("p k t -> p (k t)").bitcast(BF16).rearrange("p (k c) -> p k c", c=128), "adab1")]
        adab[0].b = a1x.b
        adab[1].b = sh1x.b
        ADA_CH = list(range(16, 64))

        def ada_load(ci):
            c = ADA_CH[ci]
            src = wada_v[:, :, c * 128:(c + 1) * 128] if c < 48 else wadaf_v[:, :, (c - 48) * 128:(c - 47) * 128]
            S.dma("pool", adab[ci % 2].t[:], src, writes=[adab[ci % 2].b])

        def ada_compute(ci):
            c = ADA_CH[ci]
            sl = adab[ci % 2]
            pb = next_pb()
            for kt in range(8):
                S.op("pe", lambda e, kt=kt: e.matmul(pb.t[:, 0:17], sl.t[:, kt, :], scT.t[:, kt, :], start=(kt == 0), stop=(kt == 7)),
                     reads=[sl.b, scT.b], writes=[pb.b])
            S.op("dve", lambda e: e.tensor_scalar(out=mod.t[:, c, :], in0=pb.t[:, 0:17], scalar1=prm.t[:, P_BMOD + c:P_BMOD + c + 1],
                                                  scalar2=None, op0=ALU.add), reads=[pb.b, prm.b], writes=[mod.b])
        ada_state = [0, 0]

        def ada_step():
            if ada_state[1] >= len(ADA_CH):
                return
            while ada_state[0] < min(len(ADA_CH), ada_state[1] + 2):
                ada_load(ada_state[0])
                ada_state[0] += 1
            ada_compute(ada_state[1])
            ada_state[1] += 1

        ssd8 = A.alloc("ssd8", [8, 4], F32)
        S.op("act", lambda e: e.activation(out=ssd8.t[:, 1:2], in_=prm.t[0:8, P_SSD8 + 1:P_SSD8 + 2], func=AF.Exp),
             reads=[prm.b], writes=[ssd8.b])
        S.op("dve", lambda e: e.tensor_scalar(out=ssd8.t[:, 1:2], in0=ssd8.t[:, 1:2], scalar1=-1.0, scalar2=None, op0=ALU.mult),
             reads=[ssd8.b], writes=[ssd8.b])
        S.op("dve", lambda e: e.tensor_copy(out=ssd8.t[:, 0:1], in_=prm.t[0:8, P_SSD8:P_SSD8 + 1]), reads=[prm.b], writes=[ssd8.b])

        Ptab = A.alloc("Ptab", [128, 2, 16, T5], F32)
        Qtab = A.alloc("Qtab", [128, 2, 16, T5], F32)
        s5t = [A.alloc("s5t%d" % i, [128, 512], F32) for i in range(2)]

        def alias(name, ap, buf):
            tl = TL(ap, name)
            tl.b = buf
            return tl
        sw = alias("s5work", s5t[1].t[:, 0:384].rearrange("p (a b) -> p a b", b=16), s5t[1].b)
        tmpA = alias("tmpA", s5t[0].t[:, 0:256].rearrange("p (a b) -> p a b", b=T5 // 2), s5t[0].b)
        tmpB = alias("tmpB", s5t[0].t[:, 256:512].rearrange("p (a b) -> p a b", b=T5 // 2), s5t[0].b)
        mask32 = A.alloc("mask32", [128, 16, T5], BF16)
        s5v = [A.alloc("s5v%d" % i, [128, 512], F32) for i in range(2)]
        qtmp = alias("qtmp", s5v[0].t[:].rearrange("p (s t) -> p s t", t=T5), s5v[0].b)
        mask4 = A.alloc("mask4", [128, 128, LS], BF16)
        s5cr = A.alloc("s5cr", [128, 2, 16], F32)
        W = lambda i: sw.t[:, i, :]
        pv = lambda i: prm.t[:, P_S5P + 16 * i:P_S5P + 16 * (i + 1)]
        swb = [sw.b, prm.b]

        def dv(fn):
            S.op("dve", fn, reads=swb, writes=[sw.b])

        def act(fn):
            S.op("act", fn, reads=swb, writes=[sw.b])
        TT = lambda e, o, a, b, op: e.tensor_tensor(out=o, in0=a, in1=b, op=op)
        def exp_acc(dst, src):
            dv(lambda e: e.tensor_scalar(out=W(22), in0=src, scalar1=1.0 / 16, scalar2=None, op0=ALU.mult))
            dv(lambda e: e.tensor_scalar(out=dst, in0=W(22), scalar1=1.0 / 7, scalar2=1.0, op0=ALU.mult, op1=ALU.add))
            for k in (6, 5, 4, 3, 2, 1):
                dv(lambda e: TT(e, dst, dst, W(22), ALU.mult))
                dv(lambda e, k=k: e.tensor_scalar(out=dst, in0=dst, scalar1=1.0 / k, scalar2=1.0, op0=ALU.mult, op1=ALU.add))
            for _ in range(4):
                dv(lambda e: TT(e, dst, dst, dst, ALU.mult))
        exp_acc(W(0), pv(2))
        dv(lambda e: TT(e, W(1), pv(0), W(0), ALU.mult))
        dv(lambda e: TT(e, W(2), pv(1), W(0), ALU.mult))
        exp_acc(W(3), W(1))

        def range_reduce(dst, src, add):
            ki = A_ki
            dv(lambda e: e.tensor_scalar(out=W(20), in0=src, scalar1=float(add), scalar2=1.0 / (2 * PI), op0=ALU.add, op1=ALU.mult))
            S.op("dve", lambda e: e.tensor_copy(out=ki.t[:], in_=W(20)), reads=swb, writes=[ki.b])
            S.op("dve", lambda e: e.tensor_copy(out=W(21), in_=ki.t[:]), reads=[ki.b], writes=[sw.b])
            dv(lambda e: e.tensor_scalar(out=W(20), in0=src, scalar1=float(add), scalar2=None, op0=ALU.add))
            dv(lambda e: e.scalar_tensor_tensor(out=dst, in0=W(21), scalar=-2 * PI, in1=W(20), op0=ALU.mult, op1=ALU.add))
            dv(lambda e: e.tensor_scalar(out=dst, in0=dst, scalar1=PI, scalar2=-PI, op0=ALU.min, op1=ALU.max))
        A_ki = A.alloc("s5ki", [128, 16], I32)
        range_reduce(W(4), W(2), 0.0)
        range_reduce(W(5), W(2), PI / 2)
        act(lambda e: e.activation(out=W(6), in_=W(4), func=AF.Sin))
        act(lambda e: e.activation(out=W(7), in_=W(5), func=AF.Sin))
        dv(lambda e: TT(e, W(8), W(3), W(7), ALU.mult))
        dv(lambda e: TT(e, W(9), W(3), W(6), ALU.mult))
        dv(lambda e: e.tensor_scalar(out=W(10), in0=W(8), scalar1=-1.0, scalar2=None, op0=ALU.add))
        dv(lambda e: TT(e, W(11), pv(0), pv(0), ALU.mult))
        dv(lambda e: TT(e, W(12), pv(1), pv(1), ALU.mult))
        dv(lambda e: TT(e, W(11), W(11), W(12), ALU.add))
        dv(lambda e: e.reciprocal(out=W(11), in_=W(11)))
        dv(lambda e: TT(e, W(12), W(10), pv(0), ALU.mult))
        dv(lambda e: TT(e, W(13), W(9), pv(1), ALU.mult))
        dv(lambda e: TT(e, W(12), W(12), W(13), ALU.add))
        dv(lambda e: TT(e, W(14), W(12), W(11), ALU.mult))
        dv(lambda e: TT(e, W(12), W(9), pv(0), ALU.mult))
        dv(lambda e: TT(e, W(13), W(10), pv(1), ALU.mult))
        dv(lambda e: TT(e, W(12), W(12), W(13), ALU.subtract))
        dv(lambda e: TT(e, W(15), W(12), W(11), ALU.mult))
        dv(lambda e: TT(e, W(12), W(8), W(8), ALU.mult))
        dv(lambda e: TT(e, W(13), W(9), W(9), ALU.mult))
        dv(lambda e: TT(e, W(12), W(12), W(13), ALU.add))
        dv(lambda e: e.reciprocal(out=W(12), in_=W(12)))
        dv(lambda e: TT(e, W(16), W(8), W(12), ALU.mult))
        dv(lambda e: e.scalar_tensor_tensor(out=W(17), in0=W(9), scalar=-1.0, in1=W(12), op0=ALU.mult, op1=ALU.mult))

        def build_pow(tab, br, bi):
            tb = [tab.b, sw.b, tmpA.b, tmpB.b]
            S.op("dve", lambda e: e.tensor_copy(out=tab.t[:, 0, :, 0], in_=br), reads=tb, writes=[tab.b])
            S.op("dve", lambda e: e.tensor_copy(out=tab.t[:, 1, :, 0], in_=bi), reads=tb, writes=[tab.b])
            n = 1
            while n < T5:
                ar, ai = tab.t[:, 0, :, 0:n], tab.t[:, 1, :, 0:n]
                sr = tab.t[:, 0, :, n - 1:n].to_broadcast([128, 16, n])
                si = tab.t[:, 1, :, n - 1:n].to_broadcast([128, 16, n])
                tA, tB = tmpA.t[:, :, 0:n], tmpB.t[:, :, 0:n]
                orr, oi = tab.t[:, 0, :, n:2 * n], tab.t[:, 1, :, n:2 * n]
                ops = [(tA, ar, sr, ALU.mult), (tB, ai, si, ALU.mult), (orr, tA, tB, ALU.subtract),
                       (tA, ar, si, ALU.mult), (tB, ai, sr, ALU.mult), (oi, tA, tB, ALU.add)]
                for (o, a, b, op) in ops:
                    S.op("dve", lambda e, o=o, a=a, b=b, op=op: TT(e, o, a, b, op), reads=tb, writes=tb[0:1] + tb[2:4])
                n *= 2
        build_pow(Ptab, W(8), W(9))
        build_pow(Qtab, W(16), W(17))
        tq = [Qtab.b, sw.b, tmpA.b, tmpB.b]
        for half in range(2):
            hs = slice(half * (T5 // 2), (half + 1) * (T5 // 2))
            qr, qi = Qtab.t[:, 0, :, hs], Qtab.t[:, 1, :, hs]
            fr = W(14).unsqueeze(2).to_broadcast([128, 16, T5 // 2])
            fi = W(15).unsqueeze(2).to_broadcast([128, 16, T5 // 2])
            ops = [(tmpA.t[:], qr, fr, ALU.mult), (tmpB.t[:], qi, fi, ALU.mult), ("R", tmpA.t[:], tmpB.t[:], ALU.subtract),
                   (tmpA.t[:], qr, fi, ALU.mult), (tmpB.t[:], qi, fr, ALU.mult), (qi, tmpA.t[:], tmpB.t[:], ALU.add)]
            for (o, a, b, op) in ops:
                if isinstance(o, str):
                    o = qtmp.t[:, :, hs]
                S.op("dve", lambda e, o=o, a=a, b=b, op=op: TT(e, o, a, b, op), reads=tq + [qtmp.b], writes=tq + [qtmp.b])
            S.op("dve", lambda e, qr=qr, hs=hs: e.tensor_copy(out=qr, in_=qtmp.t[:, :, hs]), reads=[qtmp.b], writes=[Qtab.b])
        S.op("dve", lambda e: e.memset(mask32.t[:], 1.0), reads=[Qtab.b], writes=[mask32.b])
        S.op("dve", lambda e: e.memset(mask32.t[:, :, 0:1], 0.0), writes=[mask32.b])
        S.op("dve", lambda e: e.memset(mask4.t[:], 1.0), writes=[mask4.b])
        S.op("dve", lambda e: e.memset(mask4.t[:, :, 0:1], 0.0), writes=[mask4.b])
        S.op("dve", lambda e: e.memset(s5cr.t[:], 0.0), writes=[s5cr.b])
        dump("Ptab", Ptab.t[:].rearrange("p a s t -> p (a s t)"), [128, 2 * 16 * T5], [Ptab.b])
        dump("Qtab", Qtab.t[:].rearrange("p a s t -> p (a s t)"), [128, 2 * 16 * T5], [Qtab.b])

        ckpt("setup0")
        NTM = 256
        xtm = A.alloc("xtm", [128, 2, D], F32)
        xn = A.alloc("xn", [128, 2, D], BF16)
        nstat = A.alloc("nstat", [128, 4], F32)
        uT = A.alloc("uT", [128, 8, NTM], BF16)
        xpad = A.alloc("xpad", [128, 8, NTM + 3], F32)
        xsT = A.alloc("xsT", [128, 4, NTM], F32)
        BCT = A.alloc("BCT", [128, 4, NTM], BF16)
        szT = A.alloc("szT", [128, 4, NTM], BF16)
        u5Ts = [A.alloc("u5T%d" % i, [128, 4, NTM], BF16) for i in range(2)]
        dtT = A.alloc("dtT", [8, 2, NTM], F32)
        cacc = [A.alloc("cacc0", [128, NTM], F32)] * 2
        y5pre = A.alloc("y5pre", [128, 4, NTM], F32)
        g5 = A.alloc("g5", [128, 4, NTM], BF16)
        sgl = A.alloc("sgl", [128, NTM], F32)
        dtm_l = [A.alloc("dtm%d" % i, [128, 16], F32) for i in range(2)]
        acs_l = [A.alloc("acs%d" % i, [128, 8], F32) for i in range(2)]
        dec_l = [A.alloc("dec%d" % i, [128, 8], F32) for i in range(2)]
        dtdec_l = [A.alloc("dtdec%d" % i, [128, 8], F32) for i in range(2)]
        Xtm = A.alloc("Xtm", [128, 8, 64], BF16)
        Xdec = A.alloc("Xdec", [128, 8, 64], BF16)
        Btm = A.alloc("Btm", [128, 2, 128], BF16)
        big1 = A.alloc("big1", [128, 8, 128], F32)
        big2 = A.alloc("big2", [128, 8, 128], F32)
        MT = A.alloc("MT", [128, 8, 128], BF16)
        eA = A.alloc("eA", [128, 8, 128], F32)
        CdT = A.alloc("CdT", [128, 8, 128], BF16)
        ST = A.alloc("ST", [128, 8, 64], F32)
        STb = A.alloc("STb", [128, 8, 64], BF16)
        sts5 = alias("sts5", ST.t[:].rearrange("p h q -> p (h q)").rearrange("p (a s q) -> p a s q", a=2, s=16), ST.b)
        yg = A.alloc("yg", [128, 4, 128], F32)
        ysq = alias("ysq", big1.t[:, 4:8, :], big1.b)
        rsb = A.alloc("rsb", [128, 2, 128], F32)
        h0n = [alias("h0n0", xtm.t[:, 1, 0:512].rearrange("p (a n) -> p a n", n=128), xtm.sub(1))] * 2
        h0T = [A.alloc("h0T%d" % i, [128, 8, 64], BF16) for i in range(2)]
        Bj = [A.alloc("Bj%d" % i, [128, 2, 128], BF16) for i in range(2)]
        hn = [alias("hn0", xtm.t[:, 1, 512:1024].rearrange("p (a n) -> p a n", n=128), xtm.sub(1))] * 2
        decfm = A.alloc("decfm", [128, 4, 16], F32)
        dAx = alias("dAx", big1.t[:, 0:4, :].rearrange("p a (b c) -> p (a b) c", c=64), big1.b)
        s5g = [[A.alloc("s5g%d%d" % (j, i), [128, 512], F32) for i in range(2)] for j in range(2)]
        s5t34 = [A.alloc("s5t%d" % i, [128, 512], F32) for i in (2, 3)]
        s5vb = [A.alloc("s5vb%d" % i, [128, 512], F32) for i in range(2)]
        s5k = [0]
        s5o = [A.alloc("s5o%d" % i, [128, 512], F32) for i in range(2)]
        s5h = [[A.alloc("s5h%d%d" % (j, i), [128, 512], BF16) for i in range(2)] for j in range(2)]
        s5c = A.alloc("s5c", [128, 4, 16], F32)
        busd = [[A.alloc("bus%d%d" % (j, i), [128, 512], F32) for i in range(2)] for j in range(2)]
        dg5 = A.alloc("dg5", [128, 4, 128], BF16)
        for q_ in range(4):
            S.op("act", lambda e, q_=q_: e.activation(out=dg5.t[:, q_, :], in_=ident, func=AF.Copy,
                                                      scale=prm.t[:, P_S5M + q_:P_S5M + q_ + 1]),
                 reads=[cst.b, prm.b], writes=[dg5.b])
        print("arena after p1a allocs: lo=%d hi=%d (words)" % (A.lo, A.hi))

        S.op("dve", lambda e: e.memset(xpad.t[:, :, 0:3], 0.0), writes=[xpad.b])
        S.op("dve", lambda e: e.memset(ST.t[:], 0.0), writes=[ST.b])
        S.op("dve", lambda e: e.memset(STb.t[:], 0.0), writes=[STb.b])

        TILES_A = [(i * 256, 256, False) for i in range(8)] + [(SEQ, 64, True)]

        def load_x(ti):
            t0, NT, is_s = TILES_A[ti]
            for blk in range((NT + 127) // 128):
                rows = min(128, NT - blk * 128)
                S.dma("sp", xtm.t[0:rows, blk, :], xin[t0 + blk * 128:t0 + blk * 128 + rows, :], writes=[xtm.sub(blk)])

        a1 = lambda kt: amod.t[:, kt, 0:1]
        sh1 = lambda kt: mod.t[:, 8 * MOD_SH1 + kt, 0:1]
        cw = lambda ct, k: prm.t[:, P_CONV + 5 * ct + k:P_CONV + 5 * ct + k + 1]
        IN_CHUNKS = [("dt", 0, 1536, 8)] + [("z", i, i * 128, 128) for i in range(4)] + \
                    [("xbc", i, 512 + i * 128, 128) for i in range(8)] + [("u5", i, 1544 + i * 128, 128) for i in range(4)]

        load_x(0)
        pbi = [0]

        def next_pb():
            pbi[0] ^= 1
            return PB[pbi[0]]

        ckpt("pre")
        def chain1(ti):
            t0, NT, is_s = TILES_A[ti]
            u5T = u5Ts[ti % 2]
            nblk = (NT + 127) // 128
            T = 128 if not is_s else 64
            tri = cst.t[0:T, C_TRI:C_TRI + T] if not is_s else cst.t[0:T, C_TRI64:C_TRI64 + T]
            neg = cst.t[0:T, C_NEG:C_NEG + T] if not is_s else cst.t[0:T, C_NEG64:C_NEG64 + T]
            sego = onesf.t[0:T, 0:T] if not is_s else cst.t[0:T, C_SEG64:C_SEG64 + T]
            segi = cst.t[0:64, C_SEGI:C_SEGI + 16]

            def dt_prep(ck):
                c0 = ck * T
                cs_ = slice(c0, c0 + T)
                dtm, acs, dec, dtdec = dtm_l[ck], acs_l[ck], dec_l[ck], dtdec_l[ck]
                pc = 0 if ck == 0 else 480
                S.op("pe", lambda e: e.transpose(PB[4].t[0:T, pc:pc + 8], dtT.t[:, 0, cs_], cst.t[0:8, C_ID:C_ID + 8]),
                     reads=[dtT.b, cst.b], writes=[PB[4].sub("sm")])
                S.op("pe", lambda e: e.transpose(PB[4].t[0:T, pc + 8:pc + 16], dtT.t[:, 1, cs_], cst.t[0:8, C_ID:C_ID + 8]),
                     reads=[dtT.b, cst.b], writes=[PB[4].sub("sm")])
                S.op("act", lambda e: e.activation(out=dtm.t[0:T, :], in_=PB[4].t[0:T, pc:pc + 16], func=AF.Copy),
                     reads=[PB[4].sub("sm")], writes=[dtm.b])
                S.op("pe", lambda e: e.matmul(PB[4].t[0:T, pc + 16:pc + 24], tri, dtm.t[0:T, 8:16], start=True, stop=True),
                     reads=[dtm.b, cst.b], writes=[PB[4].sub("sm")])
                S.op("pe", lambda e: e.matmul(PB[4].t[0:T, pc + 24:pc + 32], sego, dtm.t[0:T, 8:16], start=True, stop=True),
                     reads=[dtm.b, cst.b, onesf.b], writes=[PB[4].sub("sm")])
                S.op("act", lambda e: e.activation(out=acs.t[0:T, :], in_=PB[4].t[0:T, pc + 16:pc + 24], func=AF.Copy),
                     reads=[PB[4].sub("sm")], writes=[acs.b])
                S.op("dve", lambda e: TT(e, dec.t[0:T, :], PB[4].t[0:T, pc + 24:pc + 32], acs.t[0:T, :], ALU.subtract),
                     reads=[PB[4].sub("sm"), acs.b], writes=[dec.b])
                S.op("act", lambda e: e.activation(out=dec.t[0:T, :], in_=dec.t[0:T, :], func=AF.Exp), reads=[dec.b], writes=[dec.b])
                S.op("dve", lambda e: TT(e, dtdec.t[0:T, :], dtm.t[0:T, 0:8], dec.t[0:T, :], ALU.mult),
                     reads=[dtm.b, dec.b], writes=[dtdec.b])
            for blk in range(nblk):
                rows = min(128, NT - blk * 128)
                xb = xtm.sub(blk)
                S.op("act", lambda e, blk=blk, rows=rows: e.activation(
                    out=xn.t[0:rows, blk, :], in_=xtm.t[0:rows, blk, :], func=AF.Square, accum_out=nstat.t[0:rows, blk:blk + 1]),
                    reads=[xb], writes=[xn.sub(blk), nstat.sub(blk)])
                S.op("act", lambda e, blk=blk, rows=rows: e.activation(
                    out=nstat.t[0:rows, 2 + blk:3 + blk], in_=nstat.t[0:rows, blk:blk + 1], func=AF.Sqrt, scale=1.0 / D, bias=EPS),
                    reads=[nstat.sub(blk)], writes=[nstat.sub(blk)])
                S.op("dve", lambda e, blk=blk, rows=rows: e.reciprocal(out=nstat.t[0:rows, 2 + blk:3 + blk],
                                                                        in_=nstat.t[0:rows, 2 + blk:3 + blk]),
                     reads=[nstat.sub(blk)], writes=[nstat.sub(blk)])
                S.op("act", lambda e, blk=blk, rows=rows: e.activation(
                    out=xn.t[0:rows, blk, :], in_=xtm.t[0:rows, blk, :], func=AF.Copy, scale=nstat.t[0:rows, 2 + blk:3 + blk]),
                    reads=[xb, nstat.sub(blk)], writes=[xn.sub(blk)])
            ckpt("Aa%d" % ti)
            if ti + 1 < len(TILES_A):
                load_x(ti + 1)
            ckpt("Ab%d" % ti)
            for kt in range(8):
                xb_ = 2 + (kt % 2)
                pslot = PB[xb_].b
                for blk in range(nblk):
                    rows = min(128, NT - blk * 128)
                    S.op("pe", lambda e, kt=kt, blk=blk, rows=rows: e.transpose(
                        pbf(xb_)[:, blk * 128:blk * 128 + rows],
                        xn.t[0:rows, blk, kt * 128:(kt + 1) * 128], identb.t[0:rows, 0:rows]),
                        reads=[xn.sub(blk), identb.b], writes=[pslot])
                src = pbf(xb_)[:, 0:NT]
                if not is_s:
                    S.op("act", lambda e, kt=kt, src=src: e.activation(out=uT.t[:, kt, 0:NT], in_=src, func=AF.Identity,
                                                                       scale=a1(kt), bias=sh1(kt)),
                         reads=[pslot, amod.b, mod.b], writes=[uT.sub(kt)])
                else:
                    S.op("dve", lambda e, kt=kt, src=src: TT(e, cacc[0].t[:, 0:NT], src, a1x.t[:, kt, :], ALU.mult),
                         reads=[pslot, a1x.b], writes=[cacc[0].b])
                    S.op("dve", lambda e, kt=kt: TT(e, uT.t[:, kt, 0:NT], cacc[0].t[:, 0:NT], sh1x.t[:, kt, :], ALU.add),
                         reads=[cacc[0].b, sh1x.b], writes=[uT.sub(kt)])
            ckpt("A%d" % ti)
            if ti == 0:
                dump("uT", uT.t[:].rearrange("p k t -> p (k t)"), [128, 8 * NTM], uT.allb())

            yield
            if is_s:
                xps = xpad.t[:, :, 0:NS * 7].rearrange("p c (s k) -> p c s k", k=7)
                scv = stconv_d.rearrange("p (c s k) -> p c s k", s=NS, k=3)
                for ct in range(8):
                    S.dma("sp", xps[:, ct, :, 0:3], scv[:, ct], writes=[xpad.b])
            for (kind, i, c0, M) in IN_CHUNKS:
                yield
                pb = next_pb()
                for kt in range(8):
                    S.op("pe", lambda e, kt=kt, c0=c0, M=M, pb=pb: e.matmul(
                        pb.t[0:M, 0:NT], win_sb.t[:, kt, c0:c0 + M], uT.t[:, kt, 0:NT], start=(kt == 0), stop=(kt == 7)),
                        reads=[win_sb.b, uT.sub(kt)], writes=[pb.b])
                if kind == "z":
                    S.op("act", lambda e, i=i, pb=pb: e.activation(out=szT.t[:, i, 0:NT], in_=pb.t[:, 0:NT], func=AF.Silu),
                         reads=[pb.b], writes=[szT.b])
                elif kind == "xbc":
                    if not is_s:
                        S.op("act", lambda e, i=i, pb=pb: e.activation(out=xpad.t[:, i, 3:3 + NT], in_=pb.t[:, 0:NT], func=AF.Copy),
                             reads=[pb.b], writes=[xpad.b])
                    else:
                        S.op("act", lambda e, i=i, pb=pb: e.activation(
                            out=xps[:, i, :, 3:7], in_=pb.t[:, 0:NT].rearrange("p (s k) -> p s k", k=LS), func=AF.Copy),
                            reads=[pb.b], writes=[xpad.b])
                elif kind == "dt":
                    S.op("act", lambda e, pb=pb: e.activation(out=dtT.t[:, 1, 0:NT], in_=pb.t[0:8, 0:NT], func=AF.Exp,
                                                              bias=ssd8.t[:, 0:1]), reads=[pb.b, ssd8.b], writes=[dtT.b])
                    S.op("act", lambda e: e.activation(out=dtT.t[:, 0, 0:NT], in_=dtT.t[:, 1, 0:NT], func=AF.Ln, bias=1.0),
                         reads=[dtT.b], writes=[dtT.b])
                    S.op("dve", lambda e: e.tensor_scalar(out=dtT.t[:, 1, 0:NT], in0=dtT.t[:, 0, 0:NT], scalar1=ssd8.t[:, 1:2],
                                                          scalar2=None, op0=ALU.mult), reads=[dtT.b, ssd8.b], writes=[dtT.b])
                    for ck_ in range(NT // T):
                        yield
                        dt_prep(ck_)
                else:
                    S.op("act", lambda e, i=i, pb=pb: e.activation(out=u5T.t[:, i, 0:NT], in_=pb.t[:, 0:NT], func=AF.Copy),
                         reads=[pb.b], writes=[u5T.b])

            ckpt("B%d" % ti)
            for ct in range(8):
                yield
                ca = cacc[ct % 2]
                if not is_s:
                    xin_k = lambda k, ct=ct: xpad.t[:, ct, k:k + NT]
                    cav = ca.t[:, 0:NT]
                    dst = xsT.t[:, ct, 0:NT] if ct < 4 else BCT.t[:, ct - 4, 0:NT]
                else:
                    xin_k = lambda k, ct=ct: xps[:, ct, :, k:k + LS]
                    cav = ca.t[:, 0:NT].rearrange("p (s k) -> p s k", k=LS)
                    dst = (xsT.t[:, ct, 0:NT] if ct < 4 else BCT.t[:, ct - 4, 0:NT]).rearrange("p (s k) -> p s k", k=LS)
                S.op("dve", lambda e, ct=ct, cav=cav, xin_k=xin_k: e.tensor_scalar(
                    out=cav, in0=xin_k(0), scalar1=cw(ct, 0), scalar2=cw(ct, 4), op0=ALU.mult, op1=ALU.add),
                    reads=[xpad.b, prm.b], writes=[ca.b])
                for k in range(1, 4):
                    S.op("dve", lambda e, ct=ct, k=k, cav=cav, xin_k=xin_k: e.scalar_tensor_tensor(
                        out=cav, in0=xin_k(k), scalar=cw(ct, k), in1=cav, op0=ALU.mult, op1=ALU.add),
                        reads=[xpad.b, prm.b, ca.b], writes=[ca.b])
                S.op("act", lambda e, cav=cav, dst=dst: e.activation(out=dst, in_=cav, func=AF.Silu),
                     reads=[ca.b], writes=[xsT.b if ct < 4 else BCT.b])
            ocv = o_conv.rearrange("p (c s k) -> p c s k", s=17, k=3)
            if is_s:
                for ct in range(8):
                    S.dma("sp", ocv[:, ct, 1:17, :], xps[:, ct, :, 4:7], reads=[xpad.b], buf=xpad.b)
                outbufs.append(xpad.b)
            elif ti == 7:
                S.dma("sp", ocv[:, :, 0, :], xpad.t[:, :, NT:NT + 3], reads=[xpad.b], buf=xpad.b)
            if not is_s:
                S.op("dve", lambda e: e.tensor_copy(out=xpad.t[:, :, 0:3], in_=xpad.t[:, :, NT:NT + 3]),
                     reads=[xpad.b], writes=[xpad.b])
            if is_s:
                dump("xsS", xsT.t[:, :, 0:64], [128, 4, 64], [xsT.b])
                dump("ygS", yg.t[:, :, 0:64], [128, 4, 64], [yg.b])
            if ti == 0:
                dump("xsT", xsT.t[:].rearrange("p k t -> p (k t)"), [128, 4 * NTM], [xsT.b])
                dump("dtT", dtT.t[:].rearrange("p k t -> p (k t)"), [8, 2 * NTM], [dtT.b])

            ckpt("C%d" % ti)
            for ck in range(NT // T):
                c0 = ck * T
                cs_ = slice(c0, c0 + T)
                dtm, acs, dec, dtdec = dtm_l[ck], acs_l[ck], dec_l[ck], dtdec_l[ck]
                yield
                for pr in range(4):
                    S.op("pe", lambda e, pr=pr, cs_=cs_: e.transpose(PB[3].t[0:T, pr * 128:(pr + 1) * 128], xsT.t[:, pr, cs_], ident),
                         reads=[xsT.b, cst.b], writes=[PB[3].b])
                pxs = PB[3].t[0:T, :].rearrange("p (h q) -> p h q", q=64)
                S.op("dve", lambda e: TT(e, Xtm.t[0:T], pxs, dtm.t[0:T, 0:8].unsqueeze(2).to_broadcast([T, 8, 64]), ALU.mult),
                     reads=[PB[3].b, dtm.b], writes=[Xtm.b])
                S.op("dve", lambda e: TT(e, Xdec.t[0:T], pxs, dtdec.t[0:T, :].unsqueeze(2).to_broadcast([T, 8, 64]), ALU.mult),
                     reads=[PB[3].b, dtdec.b], writes=[Xdec.b])
                for g in range(2):
                    S.op("pe", lambda e, g=g, cs_=cs_: e.transpose(pbf(2)[0:T, g * 128:(g + 1) * 128], BCT.t[:, g, cs_], identb.t[:]),
                         reads=[BCT.b, identb.b], writes=[PB[2].sub(0)])
                S.op("act", lambda e: e.activation(out=Btm.t[0:T].rearrange("p g n -> p (g n)"), in_=pbf(2)[0:T, 0:256], func=AF.Copy),
                     reads=[PB[2].sub(0)], writes=[Btm.b])
                yield
                S.op("dve", lambda e: TT(e, big1.t[0:T, :, 0:T], tri.unsqueeze(1).to_broadcast([T, 8, T]),
                                         dtm.t[0:T, 8:16].unsqueeze(2).to_broadcast([T, 8, T]), ALU.mult),
                     reads=[cst.b, dtm.b], writes=[big1.b])
                for half in range(2):
                    S.op("pe", lambda e, half=half: e.matmul(
                        PB[3].t[:, 0:4 * T].rearrange("p (h l) -> p h l", l=T), onesf.t[0:T, :],
                        big1.t[0:T, 4 * half:4 * half + 4, 0:T], start=True, stop=True),
                        reads=[big1.b, onesf.b], writes=[PB[3].b])
                    for h in range(4 * half, 4 * half + 4):
                        S.op("dve", lambda e, h=h: e.scalar_tensor_tensor(
                            out=big2.t[0:T, h, 0:T], in0=PB[3].t[0:T, (h % 4) * T:(h % 4 + 1) * T], scalar=acs.t[0:T, h:h + 1],
                            in1=neg, op0=ALU.subtract, op1=ALU.min), reads=[PB[3].b, acs.b, cst.b], writes=[big2.b])
                    S.op("act", lambda e, half=half: e.activation(
                        out=eA.t[:, 4 * half:4 * half + 4, 0:T], in_=PB[3].t[:, 0:4 * T].rearrange("p (h l) -> p h l", l=T),
                        func=AF.Exp), reads=[PB[3].b], writes=[eA.b])
                    yield
                S.op("act", lambda e: e.activation(out=big2.t[0:T, :, 0:T], in_=big2.t[0:T, :, 0:T], func=AF.Exp),
                     reads=[big2.b], writes=[big2.b])
                yield
                for g in range(2):
                    S.op("pe", lambda e, g=g, cs_=cs_: e.matmul(PB[4].t[0:T, 32 + g * 128:32 + g * 128 + T], BCT.t[:, g, cs_],
                                                                 BCT.t[:, 2 + g, cs_], start=True, stop=True),
                         reads=[BCT.b], writes=[PB[4].sub("cb")])
                cbv = PB[4].t[0:T, 32:288].rearrange("p (g l) -> p g l", l=128)[:, :, 0:T]
                S.op("dve", lambda e: TT(e, MT.t[0:T, :, 0:T].rearrange("p (g h) l -> p g h l", h=4),
                                         cbv.unsqueeze(2).to_broadcast([T, 2, 4, T]),
                                         big2.t[0:T, :, 0:T].rearrange("p (g h) l -> p g h l", h=4), ALU.mult),
                     reads=[PB[4].sub("cb"), big2.b], writes=[MT.b])
                yield
                S.op("pool", lambda e, cs_=cs_: TT(e, CdT.t[:, :, 0:T].rearrange("p (g h) l -> p g h l", h=4),
                                                   BCT.t[:, 2:4, cs_].unsqueeze(2).to_broadcast([128, 2, 4, T]),
                                                   eA.t[:, :, 0:T].rearrange("p (g h) l -> p g h l", h=4), ALU.mult),
                     reads=[BCT.b, eA.b], writes=[CdT.b])
                yield
                ypb = PB[7]
                if is_s:
                    S.op("dve", lambda e: e.tensor_copy(out=dAx.t[0:T], in_=dtm.t[0:T, 8:16].unsqueeze(2).to_broadcast([T, 8, 64])),
                         reads=[dtm.b], writes=[dAx.b])
                    for pr in range(4):
                        S.op("pe", lambda e, pr=pr: e.matmul(PB[4].t[:, 288 + pr * 16:288 + (pr + 1) * 16],
                                                             dAx.t[0:T, 2 * pr:2 * pr + 2, :], segi, start=True, stop=True),
                             reads=[dAx.b, cst.b], writes=[PB[4].sub("dec")])
                    S.op("act", lambda e: e.activation(out=decfm.t[:].rearrange("p a s -> p (a s)"), in_=PB[4].t[:, 288:352], func=AF.Exp),
                         reads=[PB[4].sub("dec")], writes=[decfm.b])
                    stv = stssd_d.rearrange("j (pr hl) p n -> j (hl p) pr n", hl=2)
                    osv = o_ssds.rearrange("j (pr hl) p n -> j (hl p) pr n", hl=2)
                    for j in range(NS):
                        yield
                        jj = j % 2
                        S.dma("sp", h0n[jj].t[:], stv[j], writes=[h0n[jj].b])
                        pbt = PB[jj]
                        for pr in range(4):
                            S.op("pe", lambda e, pr=pr, jj=jj, pbt=pbt: e.transpose(pbt.t[:, pr * 128:(pr + 1) * 128], h0n[jj].t[:, pr, :], ident),
                                 reads=[h0n[jj].b, cst.b], writes=[pbt.b])
                        S.op("act", lambda e, jj=jj, pbt=pbt: e.activation(out=h0T[jj].t[:].rearrange("p h q -> p (h q)"), in_=pbt.t[:, :], func=AF.Copy),
                             reads=[pbt.b], writes=[h0T[jj].b])
                        for h in range(8):
                            pr, hl = h // 2, h % 2
                            S.op("pe", lambda e, h=h, pr=pr, hl=hl, jj=jj, j=j: e.matmul(
                                ypb.t[64 * hl:64 * hl + 64, pr * T + LS * j:pr * T + LS * j + LS], h0T[jj].t[:, h, :],
                                CdT.t[:, h, LS * j:LS * j + LS], start=(j == 0 and pr == 0), stop=False, skip_group_check=True),
                                reads=[h0T[jj].b, CdT.b], writes=[ypb.b])
                        S.op("dve", lambda e, jj=jj, j=j: e.tensor_scalar(out=Bj[jj].t[0:T], in0=Btm.t[0:T], scalar1=segi[:, j:j + 1],
                                                                          scalar2=None, op0=ALU.mult),
                             reads=[Btm.b, cst.b], writes=[Bj[jj].b])
                        pby = PB[3]
                        for pr in range(4):
                            S.op("pe", lambda e, pr=pr, jj=jj, pby=pby: e.matmul(
                                pby.t[:, pr * 128:(pr + 1) * 128], Xdec.t[0:T, 2 * pr:2 * pr + 2, :], Bj[jj].t[0:T, pr // 2, :],
                                start=True, stop=True), reads=[Xdec.b, Bj[jj].b], writes=[pby.b])
                        S.op("dve", lambda e, jj=jj, j=j: TT(e, hn[jj].t[:], h0n[jj].t[:],
                                                             decfm.t[:, :, j:j + 1].to_broadcast([128, 4, 128]), ALU.mult),
                             reads=[h0n[jj].b, decfm.b], writes=[hn[jj].b])
                        S.op("dve", lambda e, jj=jj, pby=pby: TT(e, hn[jj].t[:], hn[jj].t[:],
                                                                 pby.t[:, :].rearrange("p (a n) -> p a n", n=128), ALU.add),
                             reads=[hn[jj].b, pby.b], writes=[hn[jj].b])
                        S.dma("sp", osv[j], hn[jj].t[:], reads=[hn[jj].b], buf=hn[jj].b)
                    outbufs.extend([hn[0].b, hn[1].b])
                for h in range(8):
                    pr, hl = h // 2, h % 2
                    out = ypb.t[64 * hl:64 * hl + 64, pr * T:(pr + 1) * T]
                    S.op("pe", lambda e, h=h, out=out, pr=pr: e.matmul(out, Xtm.t[0:T, h, :], MT.t[0:T, h, 0:T],
                                                                       start=(pr == 0 and not is_s), stop=is_s, skip_group_check=True),
                         reads=[Xtm.b, MT.b], writes=[ypb.b])
                    if not is_s:
                        S.op("pe", lambda e, h=h, out=out: e.matmul(out, STb.t[:, h, :], CdT.t[:, h, 0:T], start=False, stop=True,
                                                                    skip_group_check=True),
                             reads=[STb.b, CdT.b], writes=[ypb.b])
                yield
                for pr in range(4):
                    S.op("dve", lambda e, pr=pr, cs_=cs_: e.scalar_tensor_tensor(
                        out=yg.t[:, pr, 0:T], in0=xsT.t[:, pr, cs_], scalar=prm.t[:, P_SSDFM + pr:P_SSDFM + pr + 1],
                        in1=ypb.t[:, pr * T:(pr + 1) * T], op0=ALU.mult, op1=ALU.add),
                        reads=[xsT.b, prm.b, ypb.b], writes=[yg.b])
                S.op("pool", lambda e, cs_=cs_: TT(e, yg.t[:, :, 0:T], yg.t[:, :, 0:T], szT.t[:, :, cs_], ALU.mult),
                     reads=[yg.b, szT.b], writes=[yg.b])
                S.op("act", lambda e: e.activation(out=ysq.t[:, :, 0:T], in_=yg.t[:, :, 0:T], func=AF.Square),
                     reads=[yg.b], writes=[ysq.b])
                for g in range(2):
                    for k in range(2):
                        S.op("pe", lambda e, g=g, k=k: e.matmul(PB[3].t[:, g * T:(g + 1) * T], onesf.t[:], ysq.t[:, 2 * g + k, 0:T],
                                                                start=(k == 0), stop=(k == 1)),
                             reads=[onesf.b, ysq.b], writes=[PB[3].b])
                S.op("act", lambda e: e.activation(out=rsb.t[:, :, 0:T], in_=PB[3].t[:, 0:2 * T].rearrange("p (g l) -> p g l", l=T),
                                                   func=AF.Sqrt, scale=1.0 / 256, bias=EPS), reads=[PB[3].b], writes=[rsb.b])
                S.op("dve", lambda e: e.reciprocal(out=rsb.t[:, :, 0:T], in_=rsb.t[:, :, 0:T]), reads=[rsb.b], writes=[rsb.b])
                for pr in range(4):
                    S.op("dve", lambda e, pr=pr: e.scalar_tensor_tensor(
                        out=mixt[ti % 2].t[:, pr, c0:c0 + T], in0=yg.t[:, pr, 0:T],
                        scalar=prm.t[:, P_SSDFM + 4 + pr:P_SSDFM + 5 + pr], in1=rsb.t[:, pr // 2, 0:T], op0=ALU.mult, op1=ALU.mult),
                        reads=[yg.b, prm.b, rsb.b], writes=[mixt[ti % 2].sub("ssd")])
                yield
                if not is_s:
                    for g in range(2):
                        S.op("pe", lambda e, g=g: e.matmul(PB[6].t[:, g * 256:(g + 1) * 256], Btm.t[0:T, g, :],
                                                           Xdec.t[0:T, 4 * g:4 * g + 4, :], start=True, stop=True),
                             reads=[Btm.b, Xdec.b], writes=[PB[6].b])
                    S.op("dve", lambda e: TT(e, ST.t[:], ST.t[:], eA.t[:, :, T - 1:T].to_broadcast([128, 8, 64]), ALU.mult),
                         reads=[ST.b, eA.b], writes=[ST.b])
                    S.op("dve", lambda e: TT(e, ST.t[:], ST.t[:], PB[6].t[:, :].rearrange("p (h q) -> p h q", q=64), ALU.add),
                         reads=[ST.b, PB[6].b], writes=[ST.b])
                    S.op("act", lambda e: e.activation(out=STb.t[:], in_=ST.t[:], func=AF.Copy), reads=[ST.b], writes=[STb.b])
            if ti == 7:
                S.dma("sp", o_ssdp, ST.t[:].rearrange("p h q -> p (h q)"), reads=[ST.b], buf=ST.b)
                outbufs.append(ST.b)

            ckpt("D%d" % ti)
            yield

        def chain2(ti):
            t0, NT, is_s = TILES_A[ti]
            u5T = u5Ts[ti % 2]
            if is_s:
                S.dma("sp", sts5.t[:].rearrange("p a s q -> p (a s q)"), sts5_d, writes=[sts5.b])
            if not is_s:
                groups = [(list(range(16)), k * T5, T5) for k in range(NT // T5)]
            else:
                groups = [(list(range(8)), 0, 64), (list(range(8, 16)), 0, 64)]
            def emit_bu(g_):
                slist_, tk0_, ntok_ = groups[g_]
                bus = busd[g_ % 2]
                for part, pb in ((0, PB[5]), (1, PB[6])):
                    for idx, s in enumerate(slist_):
                        S.op("pe", lambda e, part=part, pb=pb, idx=idx, s=s: e.matmul(
                            pb.t[:, idx * ntok_:(idx + 1) * ntok_], s5BT.t[:, part, s, :], u5T.t[:, s // 4, tk0_:tk0_ + ntok_],
                            start=True, stop=True), reads=[s5BT.b, u5T.b], writes=[pb.b])
                S.op("act", lambda e: e.activation(out=bus[0].t[:], in_=PB[5].t[:, :], func=AF.Copy), reads=[PB[5].b], writes=[bus[0].b])
                S.op("act", lambda e: e.activation(out=bus[1].t[:], in_=PB[6].t[:, :], func=AF.Copy), reads=[PB[6].b], writes=[bus[1].b])
            def views(g_):
                slist_, tk0_, ntok_ = groups[g_]
                s0_ = slist_[0]
                if not is_s:
                    V3 = lambda ap: ap.rearrange("p (s t) -> p s t", t=T5)
                    QR, QI = Qtab.t[:, 0], Qtab.t[:, 1]
                    PR_, PI_ = Ptab.t[:, 0], Ptab.t[:, 1]
                    msk = mask32.t[:].rearrange("p s t -> p (s t)")
                    first = lambda ap: V3(ap)[:, :, 0]
                    cin_r, cin_i = s5cr.t[:, 0, :], s5cr.t[:, 1, :]
                else:
                    V3 = lambda ap: ap.rearrange("p (s q b) -> p s q b", q=NS, b=LS)
                    bc = lambda ap: ap.unsqueeze(2).to_broadcast([128, 8, NS, LS])
                    QR, QI = bc(Qtab.t[:, 0, s0_:s0_ + 8, 0:LS]), bc(Qtab.t[:, 1, s0_:s0_ + 8, 0:LS])
                    PR_, PI_ = bc(Ptab.t[:, 0, s0_:s0_ + 8, 0:LS]), bc(Ptab.t[:, 1, s0_:s0_ + 8, 0:LS])
                    msk = mask4.t[:].rearrange("p s t -> p (s t)")
                    first = lambda ap: V3(ap)[:, :, :, 0]
                    cin_r, cin_i = sts5.t[:, 0, s0_:s0_ + 8, :], sts5.t[:, 1, s0_:s0_ + 8, :]
                return V3, QR, QI, PR_, PI_, msk, first, cin_r, cin_i
            vsets = [[s5v[0], s5v[1]], [s5vb[0], s5vb[1]]]

            def mults_adds(g_):
                V3, QR, QI, PR_, PI_, msk, first, cin_r, cin_i = views(g_)
                bus = busd[g_ % 2]
                br, bi = V3(bus[0].t[:]), V3(bus[1].t[:])
                t1, t2, t3, t4 = s5t[0], s5t[1], s5t34[0], s5t34[1]
                vr, vi = vsets[g_ % 2]
                tb = [Qtab.b]
                for (o, a, b_, rd) in ((t1, QR, br, bus[0].b), (t2, QI, bi, bus[1].b), (t3, QR, bi, bus[1].b), (t4, QI, br, bus[0].b)):
                    S.op("dve", lambda e, o=o, a=a, b_=b_: TT(e, V3(o.t[:]), a, b_, ALU.mult), reads=tb + [rd], writes=[o.b])
                S.op("pool", lambda e: TT(e, vr.t[:], t1.t[:], t2.t[:], ALU.subtract), reads=[t1.b, t2.b], writes=[vr.b])
                S.op("pool", lambda e: TT(e, vi.t[:], t3.t[:], t4.t[:], ALU.add), reads=[t3.b, t4.b], writes=[vi.b])
            emit_bu(0)
            if len(groups) > 1:
                emit_bu(1)
            mults_adds(0)
            pend_y5 = [None]
            for gi_, (slist, tk0, ntok) in enumerate(groups):
                yield
                ns = len(slist)
                s0 = slist[0]
                V3, QR, QI, PR_, PI_, msk, first, cin_r, cin_i = views(gi_)
                vr, vi = vsets[gi_ % 2]
                if gi_ + 1 < len(groups):
                    mults_adds(gi_ + 1)
                    yield
                if gi_ + 2 < len(groups):
                    emit_bu(gi_ + 2)
                S.op("dve", lambda e: TT(e, first(vr.t[:]), first(vr.t[:]), cin_r, ALU.add), reads=[vr.b, s5cr.b, sts5.b], writes=[vr.b])
                S.op("dve", lambda e: TT(e, first(vi.t[:]), first(vi.t[:]), cin_i, ALU.add), reads=[vi.b, s5cr.b, sts5.b], writes=[vi.b])
                yield
                s5k[0] ^= 1
                gr, gi2 = s5g[s5k[0]][0], s5g[s5k[0]][1]
                S.op("dve", lambda e: e.tensor_tensor_scan(out=gr.t[:], data0=msk, data1=vr.t[:], initial=0.0, op0=ALU.mult, op1=ALU.add),
                     reads=[vr.b, mask32.b, mask4.b], writes=[gr.b])
                S.op("dve", lambda e: e.tensor_tensor_scan(out=gi2.t[:], data0=msk, data1=vi.t[:], initial=0.0, op0=ALU.mult, op1=ALU.add),
                     reads=[vi.b, mask32.b, mask4.b], writes=[gi2.b])
                yield
                o1, o2 = s5o[0], s5o[1]
                hr, hi = s5h[gi_ % 2][0], s5h[gi_ % 2][1]
                seq = [(o1, PR_, gr, ALU.mult), (o2, PI_, gi2, ALU.mult), (hr, o1, o2, ALU.subtract),
                       (o1, PR_, gi2, ALU.mult), (o2, PI_, gr, ALU.mult), (hi, o1, o2, ALU.add)]
                for (o, a, b, op) in seq:
                    a3 = a if not isinstance(a, TL) else V3(a.t[:])
                    rd = [Ptab.b, b.b] + ([a.b] if isinstance(a, TL) else [])
                    S.op("pool", lambda e, o=o, a3=a3, b=b, op=op: TT(e, V3(o.t[:]), a3, V3(b.t[:]), op), reads=rd, writes=[o.b])
                yield
                if not is_s:
                    glr, gli = V3(gr.t[:])[:, :, T5 - 1], V3(gi2.t[:])[:, :, T5 - 1]
                    plr, pli = Ptab.t[:, 0, :, T5 - 1], Ptab.t[:, 1, :, T5 - 1]
                    c_ = lambda i: s5c.t[:, i, :]
                    outr, outi = s5cr.t[:, 0, :], s5cr.t[:, 1, :]
                else:
                    glr, gli = V3(gr.t[:])[:, :, :, LS - 1], V3(gi2.t[:])[:, :, :, LS - 1]
                    plr = Ptab.t[:, 0, s0:s0 + 8, LS - 1:LS].to_broadcast([128, 8, NS])
                    pli = Ptab.t[:, 1, s0:s0 + 8, LS - 1:LS].to_broadcast([128, 8, NS])
                    c_ = lambda i: hn[0].t[:, i, :].rearrange("p (s q) -> p s q", q=NS)
                    outr, outi = s5fin.t[:, 0, s0:s0 + 8, 1:17], s5fin.t[:, 1, s0:s0 + 8, 1:17]
                cb_ = [s5c.b, hn[0].b]
                cseq = [(c_(0), plr, glr, ALU.mult), (c_(1), pli, gli, ALU.mult), (c_(2), plr, gli, ALU.mult), (c_(3), pli, glr, ALU.mult)]
                for (o, a, b, op) in cseq:
                    S.op("dve", lambda e, o=o, a=a, b=b, op=op: TT(e, o, a, b, op), reads=[Ptab.b, gr.b, gi2.b] + cb_, writes=cb_)
                S.op("dve", lambda e: TT(e, outr, c_(0), c_(1), ALU.subtract), reads=cb_, writes=[s5cr.b, s5fin.b])
                S.op("dve", lambda e: TT(e, outi, c_(2), c_(3), ALU.add), reads=cb_, writes=[s5cr.b, s5fin.b])
                yield
                def emit_y5(gi_=gi_, slist=slist, tk0=tk0, ntok=ntok, hr=hr, hi=hi):
                    y5c0 = 352
                    nq = 4 if not is_s else 2
                    for qi in range(nq):
                        q = qi if not is_s else 2 * gi_ + qi
                        S.op("pe", lambda e, q=q, qi=qi: e.matmul(PB[4].t[:, y5c0 + qi * ntok:y5c0 + (qi + 1) * ntok], dg5.t[:, q, :],
                                                                  u5T.t[:, q, tk0:tk0 + ntok], start=(qi == 0), stop=False, skip_group_check=True),
                             reads=[dg5.b, u5T.b], writes=[PB[4].sub("y5")])
                    for idx, s in enumerate(slist):
                        qi = (s // 4) if not is_s else (s // 4 - 2 * gi_)
                        out = PB[4].t[32 * (s % 4):32 * (s % 4) + 32, y5c0 + qi * ntok:y5c0 + (qi + 1) * ntok]
                        S.op("pe", lambda e, out=out, s=s, idx=idx: e.matmul(out, s5CT.t[:, 0, s, :], hr.t[:, idx * ntok:(idx + 1) * ntok],
                                                                             start=False, stop=False, skip_group_check=True,
                                                                             tile_position=(0, 32 * (s % 4))),
                             reads=[s5CT.b, hr.b], writes=[PB[4].sub("y5")])
                        S.op("pe", lambda e, out=out, s=s, idx=idx: e.matmul(out, s5CT.t[:, 1, s, :], hi.t[:, idx * ntok:(idx + 1) * ntok],
                                                                             start=False, stop=True, skip_group_check=True,
                                                                             tile_position=(0, 32 * (s % 4))),
                             reads=[s5CT.b, hi.b], writes=[PB[4].sub("y5")])
                    q0 = 0 if not is_s else 2 * gi_
                    S.op("act", lambda e: e.activation(out=y5pre.t[:, q0:q0 + nq, tk0:tk0 + ntok],
                                                       in_=PB[4].t[:, y5c0:y5c0 + nq * ntok].rearrange("p (q t) -> p q t", t=ntok), func=AF.Copy),
                         reads=[PB[4].sub("y5")], writes=[y5pre.b])
                if pend_y5[0] is not None:
                    pend_y5[0]()
                    yield
                pend_y5[0] = emit_y5
            if pend_y5[0] is not None:
                pend_y5[0]()
                pend_y5[0] = None
                yield
            if ti == 7:
                S.op("dve", lambda e: e.tensor_copy(out=s5fin.t[:, :, :, 0], in_=s5cr.t[:]), reads=[s5cr.b], writes=[s5fin.b])
            if is_s:
                S.dma("sp", o_s5, s5fin.t[:].rearrange("p a s q -> p (a s q)"), reads=[s5fin.b], buf=s5fin.b)
                outbufs.append(s5fin.b)
            if ti == 0:
                dump("y5pre", y5pre.t[:].rearrange("p k t -> p (k t)"), [128, 4 * NTM], [y5pre.b])
            ckpt("E%d" % ti)
            yield
            S.op("act", lambda e: e.activation(out=g5.t[:, :, 0:NT], in_=y5pre.t[:, :, 0:NT], func=AF.Gelu), reads=[y5pre.b], writes=[g5.b])
            for m in range(4):
                yield
                pb = next_pb()
                for q in range(4):
                    S.op("pe", lambda e, m=m, q=q, pb=pb: e.matmul(pb.t[:, 0:NT], wglu_sb.t[:, q, m * 128:(m + 1) * 128], g5.t[:, q, 0:NT],
                                                                   start=(q == 0), stop=(q == 3)),
                         reads=[wglu_sb.b, g5.b], writes=[pb.b])
                S.op("act", lambda e, m=m, pb=pb: e.activation(out=sgl.t[:, 0:NT], in_=pb.t[:, 0:NT], func=AF.Sigmoid,
                                                               bias=prm.t[:, P_S5M + 4 + m:P_S5M + 5 + m]),
                     reads=[pb.b, prm.b], writes=[sgl.b])
                S.op("dve", lambda e, m=m: TT(e, mixt[ti % 2].t[:, 4 + m, 0:NT], g5.t[:, m, 0:NT], sgl.t[:, 0:NT], ALU.mult),
                     reads=[g5.b, sgl.b], writes=[mixt[ti % 2].sub("s5")])
            S.dma("sp", mixd[:, :, t0:t0 + NT], mixt[ti % 2].t[:, :, 0:NT], reads=mixt[ti % 2].allb(), writes=[mixdb[ti]], buf=mixdb[ti])
            ckpt("T%d" % ti)
            if ti == 0:
                dump("mix0", mixt[0].t[:, :, 0:NTM], [128, 8, NTM], mixt[0].allb())
            yield

        import os as _os
        RATIO = int(_os.environ.get("K_RATIO", "1"))

        def drive(gens, ada_every=0):
            gens = [g for g in gens if g is not None]
            n = 0
            while gens:
                for gi__, g in enumerate(list(gens)):
                    for _ in range((RATIO if gi__ == 0 else 1) if RATIO > 0 else (-RATIO if gi__ == 1 else 1)):
                        try:
                            next(g)
                        except StopIteration:
                            if g in gens:
                                gens.remove(g)
                            break
                n += 1
                if ada_every and n % ada_every == 0:
                    ada_step()
        ada_state[0] = 0
        drive([chain1(0)], ada_every=12)
        for ti_ in range(len(TILES_A)):
            if ti_ == 7:
                while ada_state[1] < len(ADA_CH):
                    ada_step()
                fill_x(a1x, amod.t[:, 0:8, 1:17], [amod.b])
                fill_x(sh1x, chunkmod(MOD_SH1)[:, :, 1:17], [mod.b])
                make_amod([(1, (4, 1)), (2, (7, 2))])
            drive([chain2(ti_), chain1(ti_ + 1) if ti_ + 1 < len(TILES_A) else None], ada_every=(10 if ti_ < 7 else 0))
        dump("mixS", mixt[0].t[:, :, 0:64], [128, 8, 64], mixt[0].allb())
        S.barrier()
        ckpt("1a")
        A.lo = LO_P1
        x1T = A.alloc("x1T", [128, 8, NTOK], F32, top=True)
        vT = A.alloc("vT", [128, 8, NTOK], BF16, top=True)
        wout_sb = A.alloc("wout_sb", [128, 8, D], BF16)
        wout_v = wout.rearrange("(kt p) n -> p kt n", p=128)
        for kh in range(4):
            S.dma("pool", wout_sb.t[:, 2 * kh:2 * kh + 2, :], wout_v[:, 2 * kh:2 * kh + 2, :], writes=[wout_sb.b])
        mixb = [A.alloc("mixb%d" % i, [128, 8, 512], BF16) for i in range(2)]

        def load_mix(ti):
            t0, NT, is_s = TILES_B[ti]
            tiles_a = [i for i, (a0, n0, s0_) in enumerate(TILES_A) if a0 >= t0 and a0 < t0 + NT]
            S.dma("sp", mixb[ti % 2].t[:, :, 0:NT], mixd[:, :, t0:t0 + NT], reads=[mixdb[i] for i in tiles_a], writes=[mixb[ti % 2].b])
        xtm2 = A.alloc("xtm2", [128, 4, D], F32)
        xTm = [A.alloc("xTm%d" % i, [128, 512], F32) for i in range(2)]
        sqb = [A.alloc("sqb%d" % i, [128, 512], BF16) for i in range(2)]
        onesb = A.alloc("onesb", [128, 128], BF16)
        S.op("dve", lambda e: e.memset(onesb.t[:], 1.0), writes=[onesb.b])
        tmp2 = [A.alloc("tmp2_%d" % i, [128, 512], F32) for i in range(2)]
        rstdb = [A.alloc("rstdb%d" % i, [128, 512], F32) for i in range(2)]
        g1x = expand_mod("g1x", chunkmod(MOD_G1)[:, :, 1:17], [mod.b])
        a2x = expand_mod("a2x", amod.t[:, 8:16, 1:17], [amod.b])
        sh2x = expand_mod("sh2x", chunkmod(MOD_SH2)[:, :, 1:17], [mod.b])
        print("arena p1b: lo=%d hi=%d" % (A.lo, A.hi))
        TILES_B = [(i * 512, 512, False) for i in range(4)] + [(SEQ, 64, True)]

        def load_x2(ti):
            t0, NT, is_s = TILES_B[ti]
            for blk in range((NT + 127) // 128):
                rows = min(128, NT - blk * 128)
                S.dma("sp", xtm2.t[0:rows, blk, :], xin[t0 + blk * 128:t0 + blk * 128 + rows, :], writes=[xtm2.sub(blk)])
        load_x2(0)
        load_mix(0)

        def stat_accum(src_ap, m, NT, pbs):
            sq = sqb[m % 2]
            S.op("act", lambda e: e.activation(out=sq.t[:, 0:NT], in_=src_ap, func=AF.Square), reads=[x1T.sub(m)], writes=[sq.b])
            S.op("pe", lambda e: e.matmul(pbs.t[:, 0:NT], onesb.t[:], sq.t[:, 0:NT], start=(m == 0), stop=(m == 7)),
                 reads=[onesb.b, sq.b], writes=[pbs.b])

        def stat_finish(NT, pbs, rs):
            S.op("act", lambda e: e.activation(out=rs.t[:, 0:NT], in_=pbs.t[:, 0:NT], func=AF.Sqrt, scale=1.0 / D, bias=EPS),
                 reads=[pbs.b], writes=[rs.b])
            S.op("dve", lambda e: e.reciprocal(out=rs.t[:, 0:NT], in_=rs.t[:, 0:NT]), reads=[rs.b], writes=[rs.b])

        def b_part1(ti):
            t0, NT, is_s = TILES_B[ti]
            nblk = (NT + 127) // 128
            tsl = slice(t0, t0 + NT)
            pbs = PB[4 + ti % 2]
            for m in range(8):
                pbx = PB[2 + m % 2]
                xm = xTm[m % 2]
                for blk in range(nblk):
                    rows = min(128, NT - blk * 128)
                    S.op("pe", lambda e, blk=blk, rows=rows: e.transpose(
                        pbx.t[:, blk * 128:blk * 128 + rows], xtm2.t[0:rows, blk, m * 128:(m + 1) * 128], cst.t[0:rows, C_ID:C_ID + rows]),
                        reads=[xtm2.sub(blk), cst.b], writes=[pbx.b])
                S.op("act", lambda e: e.activation(out=xm.t[:, 0:NT], in_=pbx.t[:, 0:NT], func=AF.Copy), reads=[pbx.b], writes=[xm.b])
                pb = next_pb()
                for kt in range(8):
                    S.op("pe", lambda e, kt=kt: e.matmul(pb.t[:, 0:NT], wout_sb.t[:, kt, m * 128:(m + 1) * 128], mixb[ti % 2].t[:, kt, 0:NT],
                                                         start=(kt == 0), stop=(kt == 7)),
                         reads=[wout_sb.b, mixb[ti % 2].b], writes=[pb.b])
                if m == 0 and ti + 1 < len(TILES_B):
                    load_mix(ti + 1)
                if not is_s:
                    S.op("dve", lambda e: e.scalar_tensor_tensor(
                        out=x1T.t[:, m, tsl], in0=pb.t[:, 0:NT], scalar=mod.t[:, 8 * MOD_G1 + m, 0:1], in1=xm.t[:, 0:NT],
                        op0=ALU.mult, op1=ALU.add), reads=[pb.b, mod.b, xm.b], writes=[x1T.sub(m)])
                else:
                    S.op("dve", lambda e: TT(e, tmp2[0].t[:, 0:NT], pb.t[:, 0:NT], g1x.t[:, m, :], ALU.mult),
                         reads=[pb.b, g1x.b], writes=[tmp2[0].b])
                    S.op("dve", lambda e: TT(e, x1T.t[:, m, tsl], tmp2[0].t[:, 0:NT], xm.t[:, 0:NT], ALU.add),
                         reads=[tmp2[0].b, xm.b], writes=[x1T.sub(m)])
                stat_accum(x1T.t[:, m, tsl], m, NT, pbs)
                yield
            if ti + 1 < len(TILES_B):
                load_x2(ti + 1)
            yield

        def b_part2(ti):
            t0, NT, is_s = TILES_B[ti]
            tsl = slice(t0, t0 + NT)
            rs = rstdb[ti % 2]
            stat_finish(NT, PB[4 + ti % 2], rs)
            yield
            for m in range(8):
                tq = tmp2[m % 2]
                S.op("dve", lambda e: TT(e, tq.t[:, 0:NT], x1T.t[:, m, tsl], rs.t[:, 0:NT], ALU.mult),
                     reads=[x1T.sub(m), rs.b], writes=[tq.b])
                if not is_s:
                    S.op("act", lambda e: e.activation(out=vT.t[:, m, tsl], in_=tq.t[:, 0:NT], func=AF.Identity,
                                                       scale=amod.t[:, 8 + m, 0:1], bias=mod.t[:, 8 * MOD_SH2 + m, 0:1]),
                         reads=[tq.b, amod.b, mod.b], writes=[vT.sub(m)])
                else:
                    S.op("dve", lambda e: TT(e, tq.t[:, 0:NT], tq.t[:, 0:NT], a2x.t[:, m, :], ALU.mult),
                         reads=[tq.b, a2x.b], writes=[tq.b])
                    S.op("dve", lambda e: TT(e, vT.t[:, m, tsl], tq.t[:, 0:NT], sh2x.t[:, m, :], ALU.add),
                         reads=[tq.b, sh2x.b], writes=[vT.sub(m)])
                yield
            if ti == 0:
                dump("x1p", x1T.t[:, :, 0:256], [128, 8, 256], x1T.allb())
                dump("vp", vT.t[:, :, 0:256], [128, 8, 256], vT.allb())
        drive([b_part1(0)])
        for ti_ in range(len(TILES_B)):
            drive([b_part2(ti_), b_part1(ti_ + 1) if ti_ + 1 < len(TILES_B) else None])
        S.barrier()
        ckpt("1b")

        A.lo = LO_GLOBAL
        tmp2 = [A.alloc("tmp3_%d" % i, [128, 512], F32) for i in range(2)]
        rstdb = [A.alloc("rstd3_%d" % i, [128, 512], F32) for i in range(2)]
        sqb = [A.alloc("sqb3_%d" % i, [128, 512], BF16) for i in range(2)]
        onesb = A.alloc("onesb3", [128, 128], BF16)
        S.op("dve", lambda e: e.memset(onesb.t[:], 1.0), writes=[onesb.b])
        g2x = expand_mod("g2x", chunkmod(MOD_G2)[:, :, 1:17], [mod.b])
        afx = expand_mod("afx", amod.t[:, 16:24, 1:17], [amod.b])
        shfx = expand_mod("shfx", chunkmod(MOD_SHF)[:, :, 1:17], [mod.b])
        LO_P2 = A.lo
        hT = A.alloc("hT", [128, 6, NTOK], BF16)
        wgs = [A.alloc("wgs%d" % i, [128, 8, 256], BF16) for i in range(3)]
        wus = [A.alloc("wus%d" % i, [128, 8, 256], BF16) for i in range(3)]
        wds = [A.alloc("wds%d" % i, [128, 6, D], BF16) for i in range(2)]
        sgt = [A.alloc("sgt%d" % i, [128, 512], BF16) for i in range(2)]
        print("arena p2: lo=%d hi=%d" % (A.lo, A.hi))
        wg_v = wg.rearrange("(kt p) n -> p kt n", p=128)
        wu_v = wu.rearrange("(kt p) n -> p kt n", p=128)
        wd_v = wd.rearrange("(j p) n -> p j n", p=128)
        QUARTERS = [(0, 6), (6, 12), (12, 18), (18, 22)]
        SLABS = [(q, ja + 2 * s) for q, (ja, jb) in enumerate(QUARTERS) for s in range((jb - ja) // 2)]

        def load_gu(si):
            q, j0 = SLABS[si]
            S.dma("pool", wgs[si % 3].t[:], wg_v[:, :, j0 * 128:(j0 + 2) * 128], writes=[wgs[si % 3].b])
            S.dma("pool", wus[si % 3].t[:], wu_v[:, :, j0 * 128:(j0 + 2) * 128], writes=[wus[si % 3].b])

        def load_wd(q):
            ja, jb = QUARTERS[q]
            for jh in range(0, jb - ja, 2):
                S.dma("pool", wds[q % 2].t[:, jh:jh + 2, :], wd_v[:, ja + jh:ja + jh + 2, :], writes=[wds[q % 2].b])
        load_gu(0)
        load_gu(1)
        load_wd(0)
        gbank = [0]
        si = 0
        for q, (ja, jb) in enumerate(QUARTERS):
            if q + 1 < 4:
                load_wd(q + 1)
            for s in range((jb - ja) // 2):
                if si + 2 < len(SLABS):
                    load_gu(si + 2)
                wgt, wut = wgs[si % 3], wus[si % 3]
                for jc in range(2):
                    jj = 2 * s + jc
                    for (t0, NT, is_s) in TILES_B:
                        tsl = slice(t0, t0 + NT)
                        gbank[0] ^= 1
                        pbg, pbu = PB[gbank[0]], PB[2 + gbank[0]]
                        for (wt, pb_) in ((wgt, pbg), (wut, pbu)):
                            for kt in range(8):
                                S.op("pe", lambda e, kt=kt, wt=wt, pb_=pb_: e.matmul(
                                    pb_.t[:, 0:NT], wt.t[:, kt, jc * 128:(jc + 1) * 128], vT.t[:, kt, tsl], start=(kt == 0), stop=(kt == 7)),
                                    reads=[wt.b] + vT.allb(), writes=[pb_.b])
                        sg_ = sgt[gbank[0]]
                        S.op("act", lambda e, pbg=pbg, sg_=sg_: e.activation(out=sg_.t[:, 0:NT], in_=pbg.t[:, 0:NT], func=AF.Silu),
                             reads=[pbg.b], writes=[sg_.b])
                        S.op("dve", lambda e, pbu=pbu, sg_=sg_: TT(e, hT.t[:, jj, tsl], sg_.t[:, 0:NT], pbu.t[:, 0:NT], ALU.mult),
                             reads=[sg_.b, pbu.b], writes=[hT.sub(jj)])
                si += 1
            nj = jb - ja
            wdt = wds[q % 2]
            for (t0, NT, is_s) in TILES_B:
                tsl = slice(t0, t0 + NT)
                for m in range(8):
                    pb = PB[4 + m % 2]
                    for jj in range(nj):
                        S.op("pe", lambda e, jj=jj, m=m, pb=pb: e.matmul(pb.t[:, 0:NT], wdt.t[:, jj, m * 128:(m + 1) * 128], hT.t[:, jj, tsl],
                                                                         start=(jj == 0), stop=(jj == nj - 1)),
                             reads=[wdt.b, hT.sub(jj)], writes=[pb.b])
                    if not is_s:
                        S.op("dve", lambda e, m=m, pb=pb: e.scalar_tensor_tensor(
                            out=x1T.t[:, m, tsl], in0=pb.t[:, 0:NT], scalar=mod.t[:, 8 * MOD_G2 + m, 0:1], in1=x1T.t[:, m, tsl],
                            op0=ALU.mult, op1=ALU.add), reads=[pb.b, mod.b, x1T.sub(m)], writes=[x1T.sub(m)])
                    else:
                        S.op("dve", lambda e, m=m, pb=pb: TT(e, tmp2[0].t[:, 0:NT], pb.t[:, 0:NT], g2x.t[:, m, :], ALU.mult),
                             reads=[pb.b, g2x.b], writes=[tmp2[0].b])
                        S.op("dve", lambda e, m=m: TT(e, x1T.t[:, m, tsl], tmp2[0].t[:, 0:NT], x1T.t[:, m, tsl], ALU.add),
                             reads=[tmp2[0].b, x1T.sub(m)], writes=[x1T.sub(m)])
        S.barrier()
        ckpt("ffn")
        A.lo = LO_P2
        yTs = [A.alloc("yT%d" % i, [128, 8, 512], F32) for i in range(2)]
        ytm = [A.alloc("ytm%d" % i, [128, D], F32) for i in range(2)]
        print("arena final: lo=%d hi=%d" % (A.lo, A.hi))
        oi = [0]

        def f_part1(ti):
            t0, NT, is_s = TILES_B[ti]
            tsl = slice(t0, t0 + NT)
            yT = yTs[ti % 2]
            pbs = PB[6 + ti % 2]
            rs = rstdb[ti % 2]
            for m in range(8):
                stat_accum(x1T.t[:, m, tsl], m, NT, pbs)
                if m % 2 == 1:
                    yield
            stat_finish(NT, pbs, rs)
            yield
            for m in range(8):
                tq = tmp2[m % 2]
                S.op("dve", lambda e: TT(e, tq.t[:, 0:NT], x1T.t[:, m, tsl], rs.t[:, 0:NT], ALU.mult),
                     reads=[x1T.sub(m), rs.b], writes=[tq.b])
                if not is_s:
                    S.op("act", lambda e: e.activation(out=yT.t[:, m, 0:NT], in_=tq.t[:, 0:NT], func=AF.Identity,
                                                       scale=amod.t[:, 16 + m, 0:1], bias=mod.t[:, 8 * MOD_SHF + m, 0:1]),
                         reads=[tq.b, amod.b, mod.b], writes=[yT.sub(m)])
                else:
                    S.op("dve", lambda e: TT(e, tq.t[:, 0:NT], tq.t[:, 0:NT], afx.t[:, m, :], ALU.mult),
                         reads=[tq.b, afx.b], writes=[tq.b])
                    S.op("dve", lambda e: TT(e, yT.t[:, m, 0:NT], tq.t[:, 0:NT], shfx.t[:, m, :], ALU.add),
                         reads=[tq.b, shfx.b], writes=[yT.sub(m)])
                yield

        def f_part2(ti):
            t0, NT, is_s = TILES_B[ti]
            yT = yTs[ti % 2]
            for blk in range((NT + 127) // 128):
                rows = min(128, NT - blk * 128)
                yo = ytm[oi[0] % 2]
                oi[0] += 1
                for half in range(2):
                    pbt = PB[half]
                    for k4 in range(4):
                        kt = 4 * half + k4
                        S.op("pe", lambda e, kt=kt, k4=k4: e.transpose(
                            pbt.t[0:rows, k4 * 128:(k4 + 1) * 128], yT.t[:, kt, blk * 128:blk * 128 + rows], ident),
                            reads=[yT.sub(kt), cst.b], writes=[pbt.b])
                    if half == 0:
                        S.op("act", lambda e: e.activation(out=yo.t[0:rows, 0:512], in_=pbt.t[0:rows, :], func=AF.Copy),
                             reads=[pbt.b], writes=[yo.b])
                    else:
                        S.op("dve", lambda e: e.tensor_copy(out=yo.t[0:rows, 512:1024], in_=pbt.t[0:rows, :]),
                             reads=[pbt.b], writes=[yo.b])
                    yield
                S.dma("sp", yout[t0 + blk * 128:t0 + blk * 128 + rows, :], yo.t[0:rows, :], reads=[yo.b], buf=yo.b)
        import os as _os2
        if True:
            for ti_ in range(len(TILES_B)):
                drive([f_part1(ti_)])
                drive([f_part2(ti_)])
        else:
            drive([f_part1(0)])
            for ti_ in range(len(TILES_B)):
                drive([f_part2(ti_), f_part1(ti_ + 1) if ti_ + 1 < len(TILES_B) else None])
        S.barrier()
    return nc, dumps


def _prep_inputs(inp):
    cstv = _consts()
    prmv = _params(inp)
    BT, CT = _s5mats(inp)
    maps = []
    for i in range(NCORES):
        m = {}
        m["xin"] = np.ascontiguousarray(np.concatenate(
            [inp["x_prompt"][i], inp["x_sample"][NS * i:NS * (i + 1)].reshape(NS * LS, D)], axis=0), dtype=np.float32)
        m["cin"] = np.ascontiguousarray(np.concatenate(
            [inp["c_prompt"][i:i + 1], inp["c_sample"][NS * i:NS * (i + 1)]], axis=0), dtype=np.float32)
        m["wada"] = np.ascontiguousarray(inp["w_ada"][0], dtype=np.float32)
        m["wadaf"] = np.ascontiguousarray(inp["w_ada_f"], dtype=np.float32)
        m["win"] = np.ascontiguousarray(inp["w_in"][0], dtype=np.float32)
        m["wglu"] = np.ascontiguousarray(inp["w_glu"][0], dtype=np.float32)
        m["wout"] = np.ascontiguousarray(inp["w_out"][0], dtype=np.float32)
        m["wg"] = np.ascontiguousarray(inp["w_ffn_gate"][0], dtype=np.float32)
        m["wu"] = np.ascontiguousarray(inp["w_ffn_up"][0], dtype=np.float32)
        m["wd"] = np.ascontiguousarray(inp["w_ffn_down"][0], dtype=np.float32)
        m["cst"] = cstv
        m["prm"] = prmv
        m["s5bt"] = BT.reshape(128, -1)
        m["s5ct"] = CT.reshape(128, -1)
        m["stssd"] = np.ascontiguousarray(inp["state_ssd"][0, NS * i:NS * (i + 1)], dtype=np.float32)
        sc = inp["state_conv"][0, NS * i:NS * (i + 1)]
        m["stconv"] = np.ascontiguousarray(
            sc.reshape(NS, 3, 8, 128).transpose(3, 2, 0, 1).reshape(128, -1), dtype=np.float32)
        sr = inp["state_s5_re"][0, NS * i:NS * (i + 1)]
        si = inp["state_s5_im"][0, NS * i:NS * (i + 1)]
        st = np.stack([sr, si], 0).reshape(2, NS, 16, 128).transpose(3, 0, 2, 1)
        m["sts5"] = np.ascontiguousarray(st.reshape(128, -1), dtype=np.float32)
        maps.append(m)
    return maps


_CACHE = {}


def kernel(**inputs):
    inp = {k: np.asarray(v) for k, v in inputs.items()}
    if "nc" not in _CACHE:
        _CACHE["nc"] = build()[0]
    nc = _CACHE["nc"]
    maps = _prep_inputs(inp)
    res = run_bass_kernel_spmd(nc, maps, core_ids=list(range(NCORES)))
    R = res.results
    y_p = np.stack([R[i]["yout"][:SEQ] for i in range(NCORES)], 0)
    y_s = np.concatenate([R[i]["yout"][SEQ:].reshape(NS, LS, D) for i in range(NCORES)], 0)
    ssd_p = np.stack([R[i]["o_ssdp"].reshape(128, 8, 64).transpose(1, 2, 0) for i in range(NCORES)], 0)[None]
    ssd_s = np.concatenate([R[i]["o_ssds"] for i in range(NCORES)], 0)[None]
    conv = [R[i]["o_conv"].reshape(128, 8, 17, 3).transpose(2, 3, 1, 0).reshape(17, 3, 1024) for i in range(NCORES)]
    conv_p = np.stack([c[0] for c in conv], 0)[None]
    conv_s = np.concatenate([c[1:] for c in conv], 0)[None]
    s5 = [R[i]["o_s5"].reshape(128, 2, 16, 17).transpose(1, 3, 2, 0).reshape(2, 17, 32, 64) for i in range(NCORES)]
    re_p = np.stack([s[0, 0] for s in s5], 0)[None]
    re_s = np.concatenate([s[0, 1:] for s in s5], 0)[None]
    im_p = np.stack([s[1, 0] for s in s5], 0)[None]
    im_s = np.concatenate([s[1, 1:] for s in s5], 0)[None]
    f = lambda a: np.ascontiguousarray(a, dtype=np.float32)
    return (f(y_p), f(y_s), f(ssd_p), f(ssd_s), f(conv_p), f(conv_s), f(re_p), f(re_s), f(im_p), f(im_s))
```

```python
import math
import numpy as np
from contextlib import ExitStack
import concourse.bass as bass
import concourse.mybir as mybir
from concourse.bass_utils import run_bass_kernel_spmd

F32 = mybir.dt.float32
BF16 = mybir.dt.bfloat16
I32 = mybir.dt.int32
AF = mybir.ActivationFunctionType
ALU = mybir.AluOpType

NCORES = 8
D = 1024
SEQ = 2048
NS = 16
LS = 4
NTOK = SEQ + NS * LS
DFF = 2816
NJ = DFF // 128
INP = 2056
EPS = 1e-6
T5 = 32
TILES = [(0, 512), (512, 512), (1024, 512), (1536, 512), (2048, 64)]
PI = math.pi


class Buf:
    def __init__(self, name):
        self.name = name
        self.w = None
        self.r = []
        self.dsem = None
        self.dcnt = 0


class TL:
    def __init__(self, t, name):
        self.t = t
        self.name = name
        self.b = Buf(name)
        self.subs = {}

    def sub(self, k):
        if getattr(self, "nosub", False):
            return self.b
        if k not in self.subs:
            self.subs[k] = Buf("%s_%s" % (self.name, k))
        return self.subs[k]

    def allb(self):
        return [self.b] + list(self.subs.values())

    def __getitem__(self, k):
        return self.t[k]


class Sched:
    ENG = ["pe", "act", "dve", "pool", "sp"]

    def __init__(self, nc, es):
        self.nc = nc
        self.es = es
        self.eobj = {"pe": nc.tensor, "act": nc.scalar, "dve": nc.vector, "pool": nc.gpsimd, "sp": nc.sync}
        self.cnt = {e: 0 for e in self.ENG}
        self.sem = {e: es.enter_context(nc.semaphore("s_" + e)) for e in self.ENG}
        self.seen = {e: {} for e in self.ENG}
        self.dbufs = []
        self.ninst = 0
        self.dead = False
        self.pe_pending = None

    def _flush_pe(self):
        if self.pe_pending is not None:
            self.pe_pending.then_inc(self.sem["pe"], 1)
            self.cnt["pe"] += 1
            self.pe_pending = None

    def _deps(self, eng, reads, writes):
        deps = []
        for b in reads:
            if b.w is not None:
                deps.append(b.w)
        for b in writes:
            if b.w is not None:
                deps.append(b.w)
            deps.extend(b.r)
        waits = {}
        for (sem, val, key) in deps:
            if key == "pe" and eng == "pe":
                continue
            if self.seen[eng].get(key, 0) >= val:
                continue
            if key == "pe" and val > self.cnt["pe"]:
                self._flush_pe()
            if key not in waits or waits[key][1] < val:
                waits[key] = (sem, val)
        for key, (sem, val) in waits.items():
            self.seen[eng][key] = val
        return list(waits.values())

    def op(self, eng, fn, reads=(), writes=()):
        if self.dead:
            return None
        xr = [b for b in reads if getattr(b, "excl", False)]
        if xr:
            reads = [b for b in reads if not getattr(b, "excl", False)]
            writes = list(writes) + xr
        waits = self._deps(eng, reads, writes)
        e = self.eobj[eng]
        for (s_, v_) in waits:
            e.wait_ge(s_, v_)
        if eng == "pe":
            self.pe_pending = fn(e)
            tok = (self.sem[eng], self.cnt[eng] + 1, eng)
        else:
            self.cnt[eng] += 1
            tok = (self.sem[eng], self.cnt[eng], eng)
            fn(e).then_inc(self.sem[eng], 1)
        for b in reads:
            b.r.append(tok)
        for b in writes:
            b.w = tok
            b.r = []
        self.ninst += 1
        return tok

    def dma(self, eng, out, in_, reads=(), writes=(), buf=None, **kw):
        if self.dead:
            return None
        waits = self._deps(eng, reads, writes)
        if buf is None:
            buf = writes[0] if writes else reads[0]
        if buf.dsem is None:
            buf.dsem = self.es.enter_context(self.nc.semaphore("d_" + buf.name))
            self.dbufs.append(buf)
        buf.dcnt += 16
        tok = (buf.dsem, buf.dcnt, "d_" + buf.name)
        e = self.eobj[eng]
        for (s_, v_) in waits:
            e.wait_ge(s_, v_)
        e.dma_start(out=out, in_=in_, **kw).then_inc(buf.dsem, 16)
        for b in reads:
            b.r.append(tok)
        for b in writes:
            b.w = tok
            b.r = []
        self.ninst += 1
        return tok

    def barrier(self):
        if self.dead:
            return
        self._flush_pe()
        for e in self.ENG:
            waits = []
            for o in self.ENG:
                if o != e and self.cnt[o] > self.seen[e].get(o, 0):
                    waits.append((self.sem[o], self.cnt[o]))
                    self.seen[e][o] = self.cnt[o]
            for b in self.dbufs:
                key = "d_" + b.name
                if b.dcnt > self.seen[e].get(key, 0):
                    waits.append((b.dsem, b.dcnt))
                    self.seen[e][key] = b.dcnt
            for (s_, v_) in waits:
                self.eobj[e].wait_ge(s_, v_)

    def emit(self):
        pass


C_ID = 0
C_TRI = 128
C_NEG = 256
C_TRI64 = 384
C_NEG64 = 512
C_SEG64 = 640
C_SEGI = 768
CST_W = 784

P_BMOD = 0
P_GAIN = 64
P_CONV = 88
P_SSDFM = 128
P_S5P = 136
P_S5M = 184
P_SSD8 = 192
PRM_W = 194


def _consts():
    c = np.zeros((128, CST_W), np.float32)
    c[:, C_ID:C_ID + 128] = np.eye(128, dtype=np.float32)
    s = np.arange(128)[:, None]
    l = np.arange(128)[None, :]
    c[:, C_TRI:C_TRI + 128] = (s <= l).astype(np.float32)
    c[:, C_NEG:C_NEG + 128] = np.where(l >= s, 0.0, -30000.0)
    same = (s // LS == l // LS) & (s < 64) & (l < 64)
    c[:, C_TRI64:C_TRI64 + 128] = ((s <= l) & same).astype(np.float32)
    c[:, C_NEG64:C_NEG64 + 128] = np.where((l >= s) & same, 0.0, -30000.0)
    c[:, C_SEG64:C_SEG64 + 128] = same.astype(np.float32)
    j = np.arange(16)[None, :]
    c[:, C_SEGI:C_SEGI + 16] = ((s // LS == j) & (s < 64)).astype(np.float32)
    return c


def _fm(v, nt):
    return np.ascontiguousarray(np.asarray(v, np.float32).reshape(nt, 128).T)


def _params(inp):
    p = np.zeros((128, PRM_W), np.float32)
    p[:, P_BMOD:P_BMOD + 48] = _fm(inp["b_ada"][0], 48)
    p[:, P_BMOD + 48:P_BMOD + 64] = _fm(inp["b_ada_f"], 16)
    p[:, P_GAIN:P_GAIN + 8] = _fm(inp["norm1_g"][0], 8)
    p[:, P_GAIN + 8:P_GAIN + 16] = _fm(inp["norm2_g"][0], 8)
    p[:, P_GAIN + 16:P_GAIN + 24] = _fm(inp["normf_g"], 8)
    cw = inp["conv_w"][0]
    cv = np.zeros((128, 8, 5), np.float32)
    for k in range(4):
        cv[:, :, k] = _fm(cw[k], 8)
    cv[:, :, 4] = _fm(inp["conv_b"][0], 8)
    p[:, P_CONV:P_CONV + 40] = cv.reshape(128, 40)
    Dh = inp["ssd_D"][0]
    dfm = np.zeros((128, 4), np.float32)
    for pr in range(4):
        dfm[0:64, pr] = Dh[2 * pr]
        dfm[64:128, pr] = Dh[2 * pr + 1]
    p[:, P_SSDFM:P_SSDFM + 4] = dfm
    p[:, P_SSDFM + 4:P_SSDFM + 8] = _fm(inp["ssd_norm_g"][0], 4)

    def st(a):
        return np.ascontiguousarray(np.asarray(a, np.float32).reshape(16, 128).T)
    p[:, P_S5P:P_S5P + 16] = st(inp["s5_A_re"][0])
    p[:, P_S5P + 16:P_S5P + 32] = st(inp["s5_A_im"][0])
    p[:, P_S5P + 32:P_S5P + 48] = st(np.repeat(inp["s5_log_step"][0][:, None], 64, axis=1))
    p[:, P_S5M:P_S5M + 4] = _fm(inp["s5_D"][0], 4)
    p[:, P_S5M + 4:P_S5M + 8] = _fm(inp["b_glu"][0], 4)
    p[0:8, P_SSD8] = inp["ssd_dt_bias"][0]
    p[0:8, P_SSD8 + 1] = inp["ssd_A_log"][0]
    return p


def _s5mats(inp):
    Br, Bi = inp["s5_B_re"][0], inp["s5_B_im"][0]
    Cr, Ci = inp["s5_C_re"][0], inp["s5_C_im"][0]
    BT = np.zeros((128, 2, 16, 128), np.float32)
    CT = np.zeros((128, 2, 16, 32), np.float32)
    for s in range(16):
        for gl in range(2):
            g = 2 * s + gl
            r0 = (g % 8) * 16
            BT[r0:r0 + 16, 0, s, gl * 64:(gl + 1) * 64] = Br[g].T
            BT[r0:r0 + 16, 1, s, gl * 64:(gl + 1) * 64] = Bi[g].T
            CT[gl * 64:(gl + 1) * 64, 0, s, gl * 16:(gl + 1) * 16] = Cr[g].T
            CT[gl * 64:(gl + 1) * 64, 1, s, gl * 16:(gl + 1) * 16] = Ci[g].T
    return BT, CT


class Arena:
    def __init__(self, nc, es, words):
        self.t = es.enter_context(nc.sbuf_tensor("arena", [128, words], F32))
        self.words = words
        self.lo = 0
        self.hi = words

    def alloc(self, name, shape, dt, top=False):
        n = 1
        for d in shape[1:]:
            n *= d
        w = n if dt == F32 or dt == I32 else (n + 1) // 2
        w = (w + 3) // 4 * 4
        if top:
            self.hi -= w
            off = self.hi
        else:
            off = self.lo
            self.lo += w
        assert self.lo <= self.hi, "arena overflow at %s: lo=%d hi=%d" % (name, self.lo, self.hi)
        ap = self.t[:, off:off + w]
        if dt != F32:
            ap = ap.bitcast(dt)
        ap = ap[:, 0:n]
        if len(shape) == 3:
            ap = ap.rearrange("p (a b) -> p a b", b=shape[2])
        elif len(shape) == 4:
            ap = ap.rearrange("p (a b c) -> p a b c", b=shape[2], c=shape[3])
        if shape[0] < 128:
            ap = ap[0:shape[0]]
        return TL(ap, name)


class StopBuild(Exception):
    pass


def build(dbg=None, stop_after=None):
    nc = bass.Bass("TRN2", target_bir_lowering=False)

    SH = []

    def ckpt(name):
        if stop_after == name:
            SH[0].barrier()
            SH[0].dead = True
    dt_in = lambda name, shape: nc.dram_tensor(name, list(shape), F32, kind="ExternalInput").ap()
    dt_out = lambda name, shape: nc.dram_tensor(name, list(shape), F32, kind="ExternalOutput").ap()
    xin = dt_in("xin", [NTOK, D])
    cin = dt_in("cin", [17, D])
    wada = dt_in("wada", [D, 6144])
    wadaf = dt_in("wadaf", [D, 2048])
    win = dt_in("win", [D, INP])
    wglu = dt_in("wglu", [512, 512])
    wout = dt_in("wout", [D, D])
    wg = dt_in("wg", [D, DFF])
    wu = dt_in("wu", [D, DFF])
    wd = dt_in("wd", [DFF, D])
    cst_d = dt_in("cst", [128, CST_W])
    prm_d = dt_in("prm", [128, PRM_W])
    s5bt_d = dt_in("s5bt", [128, 2 * 16 * 128])
    s5ct_d = dt_in("s5ct", [128, 2 * 16 * 32])
    stssd_d = dt_in("stssd", [NS, 8, 64, 128])
    stconv_d = dt_in("stconv", [128, 8 * NS * 3])
    sts5_d = dt_in("sts5", [128, 2 * 16 * NS])
    yout = dt_out("yout", [NTOK, D])
    o_ssdp = dt_out("o_ssdp", [128, 512])
    o_ssds = dt_out("o_ssds", [NS, 8, 64, 128])
    o_conv = dt_out("o_conv", [128, 8 * 17 * 3])
    o_s5 = dt_out("o_s5", [128, 2 * 16 * 17])
    mixd = nc.dram_tensor("mixd", [128, 8, NTOK], BF16, kind="Internal").ap()
    dumps = {}

    with ExitStack() as es:
        S = Sched(nc, es)
        SH.append(S)
        A = Arena(nc, es, 53200)
        outbufs = []

        def dump(name, ap, shape, reads):
            if dbg is None or name not in dbg:
                return
            d = dt_out("dbg_" + name, shape)
            dumps[name] = shape
            b = Buf("dbg_" + name)
            S.dma("sp" if ap.dtype == F32 else "pool", d, ap, reads=reads, buf=b)
            outbufs.append(b)

        PB = [TL(es.enter_context(nc.psum_tensor("pb%d" % i, [128, 512], F32)), "pb%d" % i) for i in range(8)]
        for pb_ in PB:
            pb_.b.excl = True
            pb_.nosub = True

        def pbf(i):
            return PB[i].t[:].bitcast(BF16)

        cst = A.alloc("cst", [128, CST_W], F32)
        prm = A.alloc("prm", [128, PRM_W], F32)
        identb = A.alloc("identb", [128, 128], BF16)
        onesf = A.alloc("onesf", [128, 128], F32)
        mod = A.alloc("mod", [128, 64, 17], F32)
        amod = A.alloc("amod", [128, 24, 17], F32)
        s5fin = A.alloc("s5fin", [128, 2, 16, 17], F32)
        scT = A.alloc("scT", [128, 8, 17], BF16)
        LO_GLOBAL = A.lo

        ident = cst.t[:, C_ID:C_ID + 128]
        S.dma("sp", cst.t[:], cst_d, writes=[cst.b])
        S.dma("sp", prm.t[:], prm_d, writes=[prm.b])
        S.op("act", lambda e: e.activation(out=identb.t[:], in_=ident, func=AF.Copy), reads=[cst.b], writes=[identb.b])
        S.op("dve", lambda e: e.memset(onesf.t[:], 1.0), writes=[onesf.b])

        def chunkmod(i):
            return mod.t[:, 8 * i:8 * i + 8, :]

        cs = A.alloc("cs", [17, D], F32)
        slabs = [A.alloc("adaslab%d" % i, [128, 8, 512], BF16) for i in range(3)]
        S.dma("sp", cs.t[:], cin, writes=[cs.b])
        S.op("act", lambda e: e.activation(out=cs.t[:], in_=cs.t[:], func=AF.Silu), reads=[cs.b], writes=[cs.b])
        for kt in range(8):
            S.op("pe", lambda e, kt=kt: e.transpose(PB[2].t[:, kt * 17:(kt + 1) * 17], cs.t[:, kt * 128:(kt + 1) * 128],
                                                    cst.t[0:17, C_ID:C_ID + 17]),
                 reads=[cs.b, cst.b], writes=[PB[2].b])
        S.op("act", lambda e: e.activation(out=scT.t[:].rearrange("p k s -> p (k s)"), in_=PB[2].t[:, 0:136], func=AF.Copy),
             reads=[PB[2].b], writes=[scT.b])
        wada_v = wada.rearrange("(kt p) n -> p kt n", p=128)
        wadaf_v = wadaf.rearrange("(kt p) n -> p kt n", p=128)

        def slab_src(i):
            if i < 12:
                return wada_v[:, :, i * 512:(i + 1) * 512]
            return wadaf_v[:, :, (i - 12) * 512:(i - 11) * 512]

        def load_slab(i):
            sl = slabs[i % 3]
            for kh in range(2):
                S.dma("pool", sl.t[:, 4 * kh:4 * kh + 4, :], slab_src(i)[:, 4 * kh:4 * kh + 4, :], writes=[sl.b])
        load_slab(0)
        load_slab(1)
        for i in range(4):
            if i + 2 < 4:
                load_slab(i + 2)
            sl = slabs[i % 3]
            pb = PB[i % 2]
            for fc in range(4):
                for kt in range(8):
                    S.op("pe", lambda e, fc=fc, kt=kt, sl=sl, pb=pb: e.matmul(
                        pb.t[:, fc * 17:(fc + 1) * 17], sl.t[:, kt, fc * 128:(fc + 1) * 128], scT.t[:, kt, :],
                        start=(kt == 0), stop=(kt == 7)), reads=[sl.b, scT.b], writes=[pb.b])
            S.op("dve", lambda e, i=i, pb=pb: e.tensor_tensor(
                out=mod.t[:, 4 * i:4 * i + 4, :], in0=pb.t[:, 0:68].rearrange("p (c s) -> p c s", s=17),
                in1=prm.t[:, P_BMOD + 4 * i:P_BMOD + 4 * i + 4].unsqueeze(2).to_broadcast([128, 4, 17]), op=ALU.add),
                reads=[pb.b, prm.b], writes=[mod.b])
        def make_amod(lst):
          for k, (sci, gi) in lst:
            S.op("dve", lambda e, k=k, sci=sci, gi=gi: e.scalar_tensor_tensor(
                out=amod.t[:, 8 * k:8 * k + 8, :], in0=chunkmod(sci), scalar=1.0,
                in1=prm.t[:, P_GAIN + 8 * gi:P_GAIN + 8 * gi + 8].unsqueeze(2).to_broadcast([128, 8, 17]),
                op0=ALU.add, op1=ALU.mult), reads=[mod.b, prm.b], writes=[amod.b])
        make_amod([(0, (1, 0))])
        dump("mod", mod.t[:].rearrange("p c s -> p (c s)"), [128, 64 * 17], [mod.b])
        S.barrier()
        S.emit()
        A.lo = LO_GLOBAL

        MOD_SH1, MOD_G1, MOD_SH2, MOD_G2, MOD_SHF = 0, 2, 3, 5, 6

        def expand_mod(name, src_ap, srcbufs):
            t = A.alloc(name, [128, 8, 64], F32)
            S.op("dve", lambda e: e.tensor_copy(out=t.t[:].rearrange("p k (s b) -> p k s b", b=LS),
                                                in_=src_ap.unsqueeze(3).to_broadcast([128, 8, NS, LS])),
                 reads=srcbufs, writes=[t.b])
            return t

        LO_P1 = A.lo
        mixt = [A.alloc("mixt%d" % i, [128, 8, 256], BF16) for i in range(2)]
        mixdb = [Buf("mixd%d" % i) for i in range(9)]
        win_sb = A.alloc("win_sb", [128, 8, INP], BF16)
        wglu_sb = A.alloc("wglu_sb", [128, 4, 512], BF16)
        s5BT = A.alloc("s5BT", [128, 2, 16, 128], BF16)
        s5CT = A.alloc("s5CT", [128, 2, 16, 32], BF16)
        win_v = win.rearrange("(kt p) n -> p kt n", p=128)
        for kh in range(4):
            for ch in range(2):
                S.dma("pool", win_sb.t[:, 2 * kh:2 * kh + 2, ch * 1028:(ch + 1) * 1028],
                      win_v[:, 2 * kh:2 * kh + 2, ch * 1028:(ch + 1) * 1028], writes=[win_sb.b])
        for a_ in range(4):
            S.dma("pool", s5BT.t[:].rearrange("p a s c -> p (a s c)")[:, a_ * 1024:(a_ + 1) * 1024],
                  s5bt_d[:, a_ * 1024:(a_ + 1) * 1024], writes=[s5BT.b])
        S.dma("pool", s5CT.t[:].rearrange("p a s c -> p (a s c)"), s5ct_d, writes=[s5CT.b])
        S.dma("pool", wglu_sb.t[:], wglu.rearrange("(kt p) n -> p kt n", p=128), writes=[wglu_sb.b])
        S.op("dve", lambda e: e.tensor_scalar(out=s5CT.t[:, 1], in0=s5CT.t[:, 1], scalar1=-1.0, scalar2=None, op0=ALU.mult),
             reads=[s5CT.b], writes=[s5CT.b])

        a1x = A.alloc("a1x", [128, 8, 64], F32)
        sh1x = A.alloc("sh1x", [128, 8, 64], F32)

        def fill_x(t, src_ap, srcbufs):
            S.op("dve", lambda e: e.tensor_copy(out=t.t[:].rearrange("p k (s b) -> p k s b", b=LS),
                                                in_=src_ap.unsqueeze(3).to_broadcast([128, 8, NS, LS])),
                 reads=srcbufs, writes=[t.b])
        adab = [TL(a1x.t[:].rearrange("p k t -> p (k t)").bitcast(BF16).rearrange("p (k c) -> p k c", c=128), "adab0"),
                TL(sh1x.t[:].rearrange("p k t -> p (k t)").bitcast(BF16).rearrange("p (k c) -> p k c", c=128), "adab1")]
        adab[0].b = a1x.b
        adab[1].b = sh1x.b
        ADA_CH = list(range(16, 64))

        def ada_load(ci):
            c = ADA_CH[ci]
            src = wada_v[:, :, c * 128:(c + 1) * 128] if c < 48 else wadaf_v[:, :, (c - 48) * 128:(c - 47) * 128]
            S.dma("pool", adab[ci % 2].t[:], src, writes=[adab[ci % 2].b])

        def ada_compute(ci):
            c = ADA_CH[ci]
            sl = adab[ci % 2]
            pb = next_pb()
            for kt in range(8):
                S.op("pe", lambda e, kt=kt: e.matmul(pb.t[:, 0:17], sl.t[:, kt, :], scT.t[:, kt, :], start=(kt == 0), stop=(kt == 7)),
                     reads=[sl.b, scT.b], writes=[pb.b])
            S.op("dve", lambda e: e.tensor_scalar(out=mod.t[:, c, :], in0=pb.t[:, 0:17], scalar1=prm.t[:, P_BMOD + c:P_BMOD + c + 1],
                                                  scalar2=None, op0=ALU.add), reads=[pb.b, prm.b], writes=[mod.b])
        ada_state = [0, 0]

        def ada_step():
            if ada_state[1] >= len(ADA_CH):
                return
            while ada_state[0] < min(len(ADA_CH), ada_state[1] + 2):
                ada_load(ada_state[0])
                ada_state[0] += 1
            ada_compute(ada_state[1])
            ada_state[1] += 1

        ssd8 = A.alloc("ssd8", [8, 4], F32)
        S.op("act", lambda e: e.activation(out=ssd8.t[:, 1:2], in_=prm.t[0:8, P_SSD8 + 1:P_SSD8 + 2], func=AF.Exp),
             reads=[prm.b], writes=[ssd8.b])
        S.op("dve", lambda e: e.tensor_scalar(out=ssd8.t[:, 1:2], in0=ssd8.t[:, 1:2], scalar1=-1.0, scalar2=None, op0=ALU.mult),
             reads=[ssd8.b], writes=[ssd8.b])
        S.op("dve", lambda e: e.tensor_copy(out=ssd8.t[:, 0:1], in_=prm.t[0:8, P_SSD8:P_SSD8 + 1]), reads=[prm.b], writes=[ssd8.b])

        Ptab = A.alloc("Ptab", [128, 2, 16, T5], F32)
        Qtab = A.alloc("Qtab", [128, 2, 16, T5], F32)
        s5t = [A.alloc("s5t%d" % i, [128, 512], F32) for i in range(2)]

        def alias(name, ap, buf):
            tl = TL(ap, name)
            tl.b = buf
            return tl
        sw = alias("s5work", s5t[1].t[:, 0:384].rearrange("p (a b) -> p a b", b=16), s5t[1].b)
        tmpA = alias("tmpA", s5t[0].t[:, 0:256].rearrange("p (a b) -> p a b", b=T5 // 2), s5t[0].b)
        tmpB = alias("tmpB", s5t[0].t[:, 256:512].rearrange("p (a b) -> p a b", b=T5 // 2), s5t[0].b)
        mask32 = A.alloc("mask32", [128, 16, T5], BF16)
        s5v = [A.alloc("s5v%d" % i, [128, 512], F32) for i in range(2)]
        qtmp = alias("qtmp", s5v[0].t[:].rearrange("p (s t) -> p s t", t=T5), s5v[0].b)
        mask4 = A.alloc("mask4", [128, 128, LS], BF16)
        s5cr = A.alloc("s5cr", [128, 2, 16], F32)
        W = lambda i: sw.t[:, i, :]
        pv = lambda i: prm.t[:, P_S5P + 16 * i:P_S5P + 16 * (i + 1)]
        swb = [sw.b, prm.b]

        def dv(fn):
            S.op("dve", fn, reads=swb, writes=[sw.b])

        def act(fn):
            S.op("act", fn, reads=swb, writes=[sw.b])
        TT = lambda e, o, a, b, op: e.tensor_tensor(out=o, in0=a, in1=b, op=op)
        def exp_acc(dst, src):
            dv(lambda e: e.tensor_scalar(out=W(22), in0=src, scalar1=1.0 / 16, scalar2=None, op0=ALU.mult))
            dv(lambda e: e.tensor_scalar(out=dst, in0=W(22), scalar1=1.0 / 7, scalar2=1.0, op0=ALU.mult, op1=ALU.add))
            for k in (6, 5, 4, 3, 2, 1):
                dv(lambda e: TT(e, dst, dst, W(22), ALU.mult))
                dv(lambda e, k=k: e.tensor_scalar(out=dst, in0=dst, scalar1=1.0 / k, scalar2=1.0, op0=ALU.mult, op1=ALU.add))
            for _ in range(4):
                dv(lambda e: TT(e, dst, dst, dst, ALU.mult))
        exp_acc(W(0), pv(2))
        dv(lambda e: TT(e, W(1), pv(0), W(0), ALU.mult))
        dv(lambda e: TT(e, W(2), pv(1), W(0), ALU.mult))
        exp_acc(W(3), W(1))

        def range_reduce(dst, src, add):
            ki = A_ki
            dv(lambda e: e.tensor_scalar(out=W(20), in0=src, scalar1=float(add), scalar2=1.0 / (2 * PI), op0=ALU.add, op1=ALU.mult))
            S.op("dve", lambda e: e.tensor_copy(out=ki.t[:], in_=W(20)), reads=swb, writes=[ki.b])
            S.op("dve", lambda e: e.tensor_copy(out=W(21), in_=ki.t[:]), reads=[ki.b], writes=[sw.b])
            dv(lambda e: e.tensor_scalar(out=W(20), in0=src, scalar1=float(add), scalar2=None, op0=ALU.add))
            dv(lambda e: e.scalar_tensor_tensor(out=dst, in0=W(21), scalar=-2 * PI, in1=W(20), op0=ALU.mult, op1=ALU.add))
            dv(lambda e: e.tensor_scalar(out=dst, in0=dst, scalar1=PI, scalar2=-PI, op0=ALU.min, op1=ALU.max))
        A_ki = A.alloc("s5ki", [128, 16], I32)
        range_reduce(W(4), W(2), 0.0)
        range_reduce(W(5), W(2), PI / 2)
        act(lambda e: e.activation(out=W(6), in_=W(4), func=AF.Sin))
        act(lambda e: e.activation(out=W(7), in_=W(5), func=AF.Sin))
        dv(lambda e: TT(e, W(8), W(3), W(7), ALU.mult))
        dv(lambda e: TT(e, W(9), W(3), W(6), ALU.mult))
        dv(lambda e: e.tensor_scalar(out=W(10), in0=W(8), scalar1=-1.0, scalar2=None, op0=ALU.add))
        dv(lambda e: TT(e, W(11), pv(0), pv(0), ALU.mult))
        dv(lambda e: TT(e, W(12), pv(1), pv(1), ALU.mult))
        dv(lambda e: TT(e, W(11), W(11), W(12), ALU.add))
        dv(lambda e: e.reciprocal(out=W(11), in_=W(11)))
        dv(lambda e: TT(e, W(12), W(10), pv(0), ALU.mult))
        dv(lambda e: TT(e, W(13), W(9), pv(1), ALU.mult))
        dv(lambda e: TT(e, W(12), W(12), W(13), ALU.add))
        dv(lambda e: TT(e, W(14), W(12), W(11), ALU.mult))
        dv(lambda e: TT(e, W(12), W(9), pv(0), ALU.mult))
        dv(lambda e: TT(e, W(13), W(10), pv(1), ALU.mult))
        dv(lambda e: TT(e, W(12), W(12), W(13), ALU.subtract))
        dv(lambda e: TT(e, W(15), W(12), W(11), ALU.mult))
        dv(lambda e: TT(e, W(12), W(8), W(8), ALU.mult))
        dv(lambda e: TT(e, W(13), W(9), W(9), ALU.mult))
        dv(lambda e: TT(e, W(12), W(12), W(13), ALU.add))
        dv(lambda e: e.reciprocal(out=W(12), in_=W(12)))
        dv(lambda e: TT(e, W(16), W(8), W(12), ALU.mult))
        dv(lambda e: e.scalar_tensor_tensor(out=W(17), in0=W(9), scalar=-1.0, in1=W(12), op0=ALU.mult, op1=ALU.mult))

        def build_pow(tab, br, bi):
            tb = [tab.b, sw.b, tmpA.b, tmpB.b]
            S.op("dve", lambda e: e.tensor_copy(out=tab.t[:, 0, :, 0], in_=br), reads=tb, writes=[tab.b])
            S.op("dve", lambda e: e.tensor_copy(out=tab.t[:, 1, :, 0], in_=bi), reads=tb, writes=[tab.b])
            n = 1
            while n < T5:
                ar, ai = tab.t[:, 0, :, 0:n], tab.t[:, 1, :, 0:n]
                sr = tab.t[:, 0, :, n - 1:n].to_broadcast([128, 16, n])
                si = tab.t[:, 1, :, n - 1:n].to_broadcast([128, 16, n])
                tA, tB = tmpA.t[:, :, 0:n], tmpB.t[:, :, 0:n]
                orr, oi = tab.t[:, 0, :, n:2 * n], tab.t[:, 1, :, n:2 * n]
                ops = [(tA, ar, sr, ALU.mult), (tB, ai, si, ALU.mult), (orr, tA, tB, ALU.subtract),
                       (tA, ar, si, ALU.mult), (tB, ai, sr, ALU.mult), (oi, tA, tB, ALU.add)]
                for (o, a, b, op) in ops:
                    S.op("dve", lambda e, o=o, a=a, b=b, op=op: TT(e, o, a, b, op), reads=tb, writes=tb[0:1] + tb[2:4])
                n *= 2
        build_pow(Ptab, W(8), W(9))
        build_pow(Qtab, W(16), W(17))
        tq = [Qtab.b, sw.b, tmpA.b, tmpB.b]
        for half in range(2):
            hs = slice(half * (T5 // 2), (half + 1) * (T5 // 2))
            qr, qi = Qtab.t[:, 0, :, hs], Qtab.t[:, 1, :, hs]
            fr = W(14).unsqueeze(2).to_broadcast([128, 16, T5 // 2])
            fi = W(15).unsqueeze(2).to_broadcast([128, 16, T5 // 2])
            ops = [(tmpA.t[:], qr, fr, ALU.mult), (tmpB.t[:], qi, fi, ALU.mult), ("R", tmpA.t[:], tmpB.t[:], ALU.subtract),
                   (tmpA.t[:], qr, fi, ALU.mult), (tmpB.t[:], qi, fr, ALU.mult), (qi, tmpA.t[:], tmpB.t[:], ALU.add)]
            for (o, a, b, op) in ops:
                if isinstance(o, str):
                    o = qtmp.t[:, :, hs]
                S.op("dve", lambda e, o=o, a=a, b=b, op=op: TT(e, o, a, b, op), reads=tq + [qtmp.b], writes=tq + [qtmp.b])
            S.op("dve", lambda e, qr=qr, hs=hs: e.tensor_copy(out=qr, in_=qtmp.t[:, :, hs]), reads=[qtmp.b], writes=[Qtab.b])
        S.op("dve", lambda e: e.memset(mask32.t[:], 1.0), reads=[Qtab.b], writes=[mask32.b])
        S.op("dve", lambda e: e.memset(mask32.t[:, :, 0:1], 0.0), writes=[mask32.b])
        S.op("dve", lambda e: e.memset(mask4.t[:], 1.0), writes=[mask4.b])
        S.op("dve", lambda e: e.memset(mask4.t[:, :, 0:1], 0.0), writes=[mask4.b])
        S.op("dve", lambda e: e.memset(s5cr.t[:], 0.0), writes=[s5cr.b])
        dump("Ptab", Ptab.t[:].rearrange("p a s t -> p (a s t)"), [128, 2 * 16 * T5], [Ptab.b])
        dump("Qtab", Qtab.t[:].rearrange("p a s t -> p (a s t)"), [128, 2 * 16 * T5], [Qtab.b])

        ckpt("setup0")
        NTM = 256
        xtm = A.alloc("xtm", [128, 2, D], F32)
        xn = A.alloc("xn", [128, 2, D], BF16)
        nstat = A.alloc("nstat", [128, 4], F32)
        uT = A.alloc("uT", [128, 8, NTM], BF16)
        xpad = A.alloc("xpad", [128, 8, NTM + 3], F32)
        xsT = A.alloc("xsT", [128, 4, NTM], F32)
        BCT = A.alloc("BCT", [128, 4, NTM], BF16)
        szT = A.alloc("szT", [128, 4, NTM], BF16)
        u5Ts = [A.alloc("u5T%d" % i, [128, 4, NTM], BF16) for i in range(2)]
        dtT = A.alloc("dtT", [8, 2, NTM], F32)
        cacc = [A.alloc("cacc0", [128, NTM], F32)] * 2
        y5pre = A.alloc("y5pre", [128, 4, NTM], F32)
        g5 = A.alloc("g5", [128, 4, NTM], BF16)
        sgl = A.alloc("sgl", [128, NTM], F32)
        dtm_l = [A.alloc("dtm%d" % i, [128, 16], F32) for i in range(2)]
        acs_l = [A.alloc("acs%d" % i, [128, 8], F32) for i in range(2)]
        dec_l = [A.alloc("dec%d" % i, [128, 8], F32) for i in range(2)]
        dtdec_l = [A.alloc("dtdec%d" % i, [128, 8], F32) for i in range(2)]
        Xtm = A.alloc("Xtm", [128, 8, 64], BF16)
        Xdec = A.alloc("Xdec", [128, 8, 64], BF16)
        Btm = A.alloc("Btm", [128, 2, 128], BF16)
        big1 = A.alloc("big1", [128, 8, 128], F32)
        big2 = A.alloc("big2", [128, 8, 128], F32)
        MT = A.alloc("MT", [128, 8, 128], BF16)
        eA = A.alloc("eA", [128, 8, 128], F32)
        CdT = A.alloc("CdT", [128, 8, 128], BF16)
        ST = A.alloc("ST", [128, 8, 64], F32)
        STb = A.alloc("STb", [128, 8, 64], BF16)
        sts5 = alias("sts5", ST.t[:].rearrange("p h q -> p (h q)").rearrange("p (a s q) -> p a s q", a=2, s=16), ST.b)
        yg = A.alloc("yg", [128, 4, 128], F32)
        ysq = alias("ysq", big1.t[:, 4:8, :], big1.b)
        rsb = A.alloc("rsb", [128, 2, 128], F32)
        h0n = [alias("h0n0", xtm.t[:, 1, 0:512].rearrange("p (a n) -> p a n", n=128), xtm.sub(1)),
               alias("h0n1", xtm.t[:, 0, 0:512].rearrange("p (a n) -> p a n", n=128), xtm.sub(0))]
        h0T = [A.alloc("h0T%d" % i, [128, 8, 64], BF16) for i in range(2)]
        Bj = [A.alloc("Bj%d" % i, [128, 2, 128], BF16) for i in range(2)]
        hn = [alias("hn0", xtm.t[:, 1, 512:1024].rearrange("p (a n) -> p a n", n=128), xtm.sub(1)),
              alias("hn1", xtm.t[:, 0, 512:1024].rearrange("p (a n) -> p a n", n=128), xtm.sub(0))]
        decfm = A.alloc("decfm", [128, 4, 16], F32)
        dAx = alias("dAx", big1.t[:, 0:4, :].rearrange("p a (b c) -> p (a b) c", c=64), big1.b)
        s5g = [[A.alloc("s5g%d%d" % (j, i), [128, 512], F32) for i in range(2)] for j in range(2)]
        s5t34 = [A.alloc("s5t%d" % i, [128, 512], F32) for i in (2, 3)]
        s5vb = [A.alloc("s5vb%d" % i, [128, 512], F32) for i in range(2)]
        s5k = [0]
        s5o = [A.alloc("s5o%d" % i, [128, 512], F32) for i in range(2)]
        s5h = [[A.alloc("s5h%d%d" % (j, i), [128, 512], BF16) for i in range(2)] for j in range(2)]
        s5c = A.alloc("s5c", [128, 4, 16], F32)
        busd = [[A.alloc("bus%d%d" % (j, i), [128, 512], F32) for i in range(2)] for j in range(2)]
        dg5 = A.alloc("dg5", [128, 4, 128], BF16)
        for q_ in range(4):
            S.op("act", lambda e, q_=q_: e.activation(out=dg5.t[:, q_, :], in_=ident, func=AF.Copy,
                                                      scale=prm.t[:, P_S5M + q_:P_S5M + q_ + 1]),
                 reads=[cst.b, prm.b], writes=[dg5.b])
        print("arena after p1a allocs: lo=%d hi=%d (words)" % (A.lo, A.hi))

        S.op("dve", lambda e: e.memset(xpad.t[:, :, 0:3], 0.0), writes=[xpad.b])
        S.op("dve", lambda e: e.memset(ST.t[:], 0.0), writes=[ST.b])
        S.op("dve", lambda e: e.memset(STb.t[:], 0.0), writes=[STb.b])

        import os as _os3
        ENG_OUTROT = _os3.environ.get("K_OUTROT", "dve")
        ENG_ADDS = _os3.environ.get("K_ADDS", "dve")
        TILES_A = [(i * 256, 256, False) for i in range(8)] + [(SEQ, 64, True)]

        def load_x(ti):
            t0, NT, is_s = TILES_A[ti]
            for blk in range((NT + 127) // 128):
                rows = min(128, NT - blk * 128)
                S.dma("sp", xtm.t[0:rows, blk, :], xin[t0 + blk * 128:t0 + blk * 128 + rows, :], writes=[xtm.sub(blk)])

        a1 = lambda kt: amod.t[:, kt, 0:1]
        sh1 = lambda kt: mod.t[:, 8 * MOD_SH1 + kt, 0:1]
        cw = lambda ct, k: prm.t[:, P_CONV + 5 * ct + k:P_CONV + 5 * ct + k + 1]
        IN_CHUNKS = [("dt", 0, 1536, 8)] + [("z", i, i * 128, 128) for i in range(4)] + \
                    [("xbc", i, 512 + i * 128, 128) for i in range(8)] + [("u5", i, 1544 + i * 128, 128) for i in range(4)]

        load_x(0)
        pbi = [0]

        def next_pb():
            pbi[0] ^= 1
            return PB[pbi[0]]

        ckpt("pre")
        def chain1(ti):
            t0, NT, is_s = TILES_A[ti]
            u5T = u5Ts[ti % 2]
            nblk = (NT + 127) // 128
            T = 128 if not is_s else 64
            tri = cst.t[0:T, C_TRI:C_TRI + T] if not is_s else cst.t[0:T, C_TRI64:C_TRI64 + T]
            neg = cst.t[0:T, C_NEG:C_NEG + T] if not is_s else cst.t[0:T, C_NEG64:C_NEG64 + T]
            sego = onesf.t[0:T, 0:T] if not is_s else cst.t[0:T, C_SEG64:C_SEG64 + T]
            segi = cst.t[0:64, C_SEGI:C_SEGI + 16]

            def dt_prep(ck):
                c0 = ck * T
                cs_ = slice(c0, c0 + T)
                dtm, acs, dec, dtdec = dtm_l[ck], acs_l[ck], dec_l[ck], dtdec_l[ck]
                pc = 0 if ck == 0 else 480
                S.op("pe", lambda e: e.transpose(PB[4].t[0:T, pc:pc + 8], dtT.t[:, 0, cs_], cst.t[0:8, C_ID:C_ID + 8]),
                     reads=[dtT.b, cst.b], writes=[PB[4].sub("sm")])
                S.op("pe", lambda e: e.transpose(PB[4].t[0:T, pc + 8:pc + 16], dtT.t[:, 1, cs_], cst.t[0:8, C_ID:C_ID + 8]),
                     reads=[dtT.b, cst.b], writes=[PB[4].sub("sm")])
                S.op("act", lambda e: e.activation(out=dtm.t[0:T, :], in_=PB[4].t[0:T, pc:pc + 16], func=AF.Copy),
                     reads=[PB[4].sub("sm")], writes=[dtm.b])
                S.op("pe", lambda e: e.matmul(PB[4].t[0:T, pc + 16:pc + 24], tri, dtm.t[0:T, 8:16], start=True, stop=True),
                     reads=[dtm.b, cst.b], writes=[PB[4].sub("sm")])
                S.op("pe", lambda e: e.matmul(PB[4].t[0:T, pc + 24:pc + 32], sego, dtm.t[0:T, 8:16], start=True, stop=True),
                     reads=[dtm.b, cst.b, onesf.b], writes=[PB[4].sub("sm")])
                S.op("act", lambda e: e.activation(out=acs.t[0:T, :], in_=PB[4].t[0:T, pc + 16:pc + 24], func=AF.Copy),
                     reads=[PB[4].sub("sm")], writes=[acs.b])
                S.op("dve", lambda e: TT(e, dec.t[0:T, :], PB[4].t[0:T, pc + 24:pc + 32], acs.t[0:T, :], ALU.subtract),
                     reads=[PB[4].sub("sm"), acs.b], writes=[dec.b])
                S.op("act", lambda e: e.activation(out=dec.t[0:T, :], in_=dec.t[0:T, :], func=AF.Exp), reads=[dec.b], writes=[dec.b])
                S.op("dve", lambda e: TT(e, dtdec.t[0:T, :], dtm.t[0:T, 0:8], dec.t[0:T, :], ALU.mult),
                     reads=[dtm.b, dec.b], writes=[dtdec.b])
            for blk in range(nblk):
                rows = min(128, NT - blk * 128)
                xb = xtm.sub(blk)
                S.op("act", lambda e, blk=blk, rows=rows: e.activation(
                    out=xn.t[0:rows, blk, :], in_=xtm.t[0:rows, blk, :], func=AF.Square, accum_out=nstat.t[0:rows, blk:blk + 1]),
                    reads=[xb], writes=[xn.sub(blk), nstat.sub(blk)])
                S.op("act", lambda e, blk=blk, rows=rows: e.activation(
                    out=nstat.t[0:rows, 2 + blk:3 + blk], in_=nstat.t[0:rows, blk:blk + 1], func=AF.Sqrt, scale=1.0 / D, bias=EPS),
                    reads=[nstat.sub(blk)], writes=[nstat.sub(blk)])
                S.op("dve", lambda e, blk=blk, rows=rows: e.reciprocal(out=nstat.t[0:rows, 2 + blk:3 + blk],
                                                                        in_=nstat.t[0:rows, 2 + blk:3 + blk]),
                     reads=[nstat.sub(blk)], writes=[nstat.sub(blk)])
                S.op("act", lambda e, blk=blk, rows=rows: e.activation(
                    out=xn.t[0:rows, blk, :], in_=xtm.t[0:rows, blk, :], func=AF.Copy, scale=nstat.t[0:rows, 2 + blk:3 + blk]),
                    reads=[xb, nstat.sub(blk)], writes=[xn.sub(blk)])
            ckpt("Aa%d" % ti)
            if ti + 1 < len(TILES_A):
                load_x(ti + 1)
            ckpt("Ab%d" % ti)
            for kt in range(8):
                xb_ = 2 + (kt % 2)
                pslot = PB[xb_].b
                for blk in range(nblk):
                    rows = min(128, NT - blk * 128)
                    S.op("pe", lambda e, kt=kt, blk=blk, rows=rows: e.transpose(
                        pbf(xb_)[:, blk * 128:blk * 128 + rows],
                        xn.t[0:rows, blk, kt * 128:(kt + 1) * 128], identb.t[0:rows, 0:rows]),
                        reads=[xn.sub(blk), identb.b], writes=[pslot])
                src = pbf(xb_)[:, 0:NT]
                if not is_s:
                    S.op("act", lambda e, kt=kt, src=src: e.activation(out=uT.t[:, kt, 0:NT], in_=src, func=AF.Identity,
                                                                       scale=a1(kt), bias=sh1(kt)),
                         reads=[pslot, amod.b, mod.b], writes=[uT.sub(kt)])
                else:
                    S.op("dve", lambda e, kt=kt, src=src: TT(e, cacc[0].t[:, 0:NT], src, a1x.t[:, kt, :], ALU.mult),
                         reads=[pslot, a1x.b], writes=[cacc[0].b])
                    S.op("dve", lambda e, kt=kt: TT(e, uT.t[:, kt, 0:NT], cacc[0].t[:, 0:NT], sh1x.t[:, kt, :], ALU.add),
                         reads=[cacc[0].b, sh1x.b], writes=[uT.sub(kt)])
            ckpt("A%d" % ti)
            if ti == 0:
                dump("uT", uT.t[:].rearrange("p k t -> p (k t)"), [128, 8 * NTM], uT.allb())

            yield
            if is_s:
                xps = xpad.t[:, :, 0:NS * 7].rearrange("p c (s k) -> p c s k", k=7)
                scv = stconv_d.rearrange("p (c s k) -> p c s k", s=NS, k=3)
                for ct in range(8):
                    S.dma("sp", xps[:, ct, :, 0:3], scv[:, ct], writes=[xpad.b])
            for (kind, i, c0, M) in IN_CHUNKS:
                yield
                pb = next_pb()
                for kt in range(8):
                    S.op("pe", lambda e, kt=kt, c0=c0, M=M, pb=pb: e.matmul(
                        pb.t[0:M, 0:NT], win_sb.t[:, kt, c0:c0 + M], uT.t[:, kt, 0:NT], start=(kt == 0), stop=(kt == 7)),
                        reads=[win_sb.b, uT.sub(kt)], writes=[pb.b])
                if kind == "z":
                    S.op("act", lambda e, i=i, pb=pb: e.activation(out=szT.t[:, i, 0:NT], in_=pb.t[:, 0:NT], func=AF.Silu),
                         reads=[pb.b], writes=[szT.b])
                elif kind == "xbc":
                    if not is_s:
                        S.op("act", lambda e, i=i, pb=pb: e.activation(out=xpad.t[:, i, 3:3 + NT], in_=pb.t[:, 0:NT], func=AF.Copy),
                             reads=[pb.b], writes=[xpad.b])
                    else:
                        S.op("act", lambda e, i=i, pb=pb: e.activation(
                            out=xps[:, i, :, 3:7], in_=pb.t[:, 0:NT].rearrange("p (s k) -> p s k", k=LS), func=AF.Copy),
                            reads=[pb.b], writes=[xpad.b])
                elif kind == "dt":
                    S.op("act", lambda e, pb=pb: e.activation(out=dtT.t[:, 1, 0:NT], in_=pb.t[0:8, 0:NT], func=AF.Exp,
                                                              bias=ssd8.t[:, 0:1]), reads=[pb.b, ssd8.b], writes=[dtT.b])
                    S.op("act", lambda e: e.activation(out=dtT.t[:, 0, 0:NT], in_=dtT.t[:, 1, 0:NT], func=AF.Ln, bias=1.0),
                         reads=[dtT.b], writes=[dtT.b])
                    S.op("dve", lambda e: e.tensor_scalar(out=dtT.t[:, 1, 0:NT], in0=dtT.t[:, 0, 0:NT], scalar1=ssd8.t[:, 1:2],
                                                          scalar2=None, op0=ALU.mult), reads=[dtT.b, ssd8.b], writes=[dtT.b])
                    for ck_ in range(NT // T):
                        yield
                        dt_prep(ck_)
                else:
                    S.op("act", lambda e, i=i, pb=pb: e.activation(out=u5T.t[:, i, 0:NT], in_=pb.t[:, 0:NT], func=AF.Copy),
                         reads=[pb.b], writes=[u5T.b])

            ckpt("B%d" % ti)
            for ct in range(8):
                yield
                ca = cacc[ct % 2]
                if not is_s:
                    xin_k = lambda k, ct=ct: xpad.t[:, ct, k:k + NT]
                    cav = ca.t[:, 0:NT]
                    dst = xsT.t[:, ct, 0:NT] if ct < 4 else BCT.t[:, ct - 4, 0:NT]
                else:
                    xin_k = lambda k, ct=ct: xps[:, ct, :, k:k + LS]
                    cav = ca.t[:, 0:NT].rearrange("p (s k) -> p s k", k=LS)
                    dst = (xsT.t[:, ct, 0:NT] if ct < 4 else BCT.t[:, ct - 4, 0:NT]).rearrange("p (s k) -> p s k", k=LS)
                S.op("dve", lambda e, ct=ct, cav=cav, xin_k=xin_k: e.tensor_scalar(
                    out=cav, in0=xin_k(0), scalar1=cw(ct, 0), scalar2=cw(ct, 4), op0=ALU.mult, op1=ALU.add),
                    reads=[xpad.b, prm.b], writes=[ca.b])
                for k in range(1, 4):
                    S.op("dve", lambda e, ct=ct, k=k, cav=cav, xin_k=xin_k: e.scalar_tensor_tensor(
                        out=cav, in0=xin_k(k), scalar=cw(ct, k), in1=cav, op0=ALU.mult, op1=ALU.add),
                        reads=[xpad.b, prm.b, ca.b], writes=[ca.b])
                S.op("act", lambda e, cav=cav, dst=dst: e.activation(out=dst, in_=cav, func=AF.Silu),
                     reads=[ca.b], writes=[xsT.b if ct < 4 else BCT.b])
            ocv = o_conv.rearrange("p (c s k) -> p c s k", s=17, k=3)
            if is_s:
                for ct in range(8):
                    S.dma("sp", ocv[:, ct, 1:17, :], xps[:, ct, :, 4:7], reads=[xpad.b], buf=xpad.b)
                outbufs.append(xpad.b)
            elif ti == 7:
                S.dma("sp", ocv[:, :, 0, :], xpad.t[:, :, NT:NT + 3], reads=[xpad.b], buf=xpad.b)
            if not is_s:
                S.op("dve", lambda e: e.tensor_copy(out=xpad.t[:, :, 0:3], in_=xpad.t[:, :, NT:NT + 3]),
                     reads=[xpad.b], writes=[xpad.b])
            if is_s:
                dump("xsS", xsT.t[:, :, 0:64], [128, 4, 64], [xsT.b])
                dump("ygS", yg.t[:, :, 0:64], [128, 4, 64], [yg.b])
            if ti == 0:
                dump("xsT", xsT.t[:].rearrange("p k t -> p (k t)"), [128, 4 * NTM], [xsT.b])
                dump("dtT", dtT.t[:].rearrange("p k t -> p (k t)"), [8, 2 * NTM], [dtT.b])

            ckpt("C%d" % ti)
            for ck in range(NT // T):
                c0 = ck * T
                cs_ = slice(c0, c0 + T)
                dtm, acs, dec, dtdec = dtm_l[ck], acs_l[ck], dec_l[ck], dtdec_l[ck]
                yield
                for pr in range(4):
                    S.op("pe", lambda e, pr=pr, cs_=cs_: e.transpose(PB[3].t[0:T, pr * 128:(pr + 1) * 128], xsT.t[:, pr, cs_], ident),
                         reads=[xsT.b, cst.b], writes=[PB[3].b])
                pxs = PB[3].t[0:T, :].rearrange("p (h q) -> p h q", q=64)
                S.op("dve", lambda e: TT(e, Xtm.t[0:T], pxs, dtm.t[0:T, 0:8].unsqueeze(2).to_broadcast([T, 8, 64]), ALU.mult),
                     reads=[PB[3].b, dtm.b], writes=[Xtm.b])
                S.op("dve", lambda e: TT(e, Xdec.t[0:T], pxs, dtdec.t[0:T, :].unsqueeze(2).to_broadcast([T, 8, 64]), ALU.mult),
                     reads=[PB[3].b, dtdec.b], writes=[Xdec.b])
                for g in range(2):
                    S.op("pe", lambda e, g=g, cs_=cs_: e.transpose(pbf(2)[0:T, g * 128:(g + 1) * 128], BCT.t[:, g, cs_], identb.t[:]),
                         reads=[BCT.b, identb.b], writes=[PB[2].sub(0)])
                S.op("act", lambda e: e.activation(out=Btm.t[0:T].rearrange("p g n -> p (g n)"), in_=pbf(2)[0:T, 0:256], func=AF.Copy),
                     reads=[PB[2].sub(0)], writes=[Btm.b])
                yield
                S.op("dve", lambda e: TT(e, big1.t[0:T, :, 0:T], tri.unsqueeze(1).to_broadcast([T, 8, T]),
                                         dtm.t[0:T, 8:16].unsqueeze(2).to_broadcast([T, 8, T]), ALU.mult),
                     reads=[cst.b, dtm.b], writes=[big1.b])
                for half in range(2):
                    S.op("pe", lambda e, half=half: e.matmul(
                        PB[3].t[:, 0:4 * T].rearrange("p (h l) -> p h l", l=T), onesf.t[0:T, :],
                        big1.t[0:T, 4 * half:4 * half + 4, 0:T], start=True, stop=True),
                        reads=[big1.b, onesf.b], writes=[PB[3].b])
                    for h in range(4 * half, 4 * half + 4):
                        S.op("dve", lambda e, h=h: e.scalar_tensor_tensor(
                            out=big2.t[0:T, h, 0:T], in0=PB[3].t[0:T, (h % 4) * T:(h % 4 + 1) * T], scalar=acs.t[0:T, h:h + 1],
                            in1=neg, op0=ALU.subtract, op1=ALU.min), reads=[PB[3].b, acs.b, cst.b], writes=[big2.b])
                    S.op("act", lambda e, half=half: e.activation(
                        out=eA.t[:, 4 * half:4 * half + 4, 0:T], in_=PB[3].t[:, 0:4 * T].rearrange("p (h l) -> p h l", l=T),
                        func=AF.Exp), reads=[PB[3].b], writes=[eA.b])
                    yield
                S.op("act", lambda e: e.activation(out=big2.t[0:T, :, 0:T], in_=big2.t[0:T, :, 0:T], func=AF.Exp),
                     reads=[big2.b], writes=[big2.b])
                yield
                for g in range(2):
                    S.op("pe", lambda e, g=g, cs_=cs_: e.matmul(PB[4].t[0:T, 32 + g * 128:32 + g * 128 + T], BCT.t[:, g, cs_],
                                                                 BCT.t[:, 2 + g, cs_], start=True, stop=True),
                         reads=[BCT.b], writes=[PB[4].sub("cb")])
                cbv = PB[4].t[0:T, 32:288].rearrange("p (g l) -> p g l", l=128)[:, :, 0:T]
                S.op("dve", lambda e: TT(e, MT.t[0:T, :, 0:T].rearrange("p (g h) l -> p g h l", h=4),
                                         cbv.unsqueeze(2).to_broadcast([T, 2, 4, T]),
                                         big2.t[0:T, :, 0:T].rearrange("p (g h) l -> p g h l", h=4), ALU.mult),
                     reads=[PB[4].sub("cb"), big2.b], writes=[MT.b])
                yield
                S.op("pool", lambda e, cs_=cs_: TT(e, CdT.t[:, :, 0:T].rearrange("p (g h) l -> p g h l", h=4),
                                                   BCT.t[:, 2:4, cs_].unsqueeze(2).to_broadcast([128, 2, 4, T]),
                                                   eA.t[:, :, 0:T].rearrange("p (g h) l -> p g h l", h=4), ALU.mult),
                     reads=[BCT.b, eA.b], writes=[CdT.b])
                yield
                ypb = PB[7]
                if is_s:
                    S.op("dve", lambda e: e.tensor_copy(out=dAx.t[0:T], in_=dtm.t[0:T, 8:16].unsqueeze(2).to_broadcast([T, 8, 64])),
                         reads=[dtm.b], writes=[dAx.b])
                    for pr in range(4):
                        S.op("pe", lambda e, pr=pr: e.matmul(PB[4].t[:, 288 + pr * 16:288 + (pr + 1) * 16],
                                                             dAx.t[0:T, 2 * pr:2 * pr + 2, :], segi, start=True, stop=True),
                             reads=[dAx.b, cst.b], writes=[PB[4].sub("dec")])
                    S.op("act", lambda e: e.activation(out=decfm.t[:].rearrange("p a s -> p (a s)"), in_=PB[4].t[:, 288:352], func=AF.Exp),
                         reads=[PB[4].sub("dec")], writes=[decfm.b])
                    stv = stssd_d.rearrange("j (pr hl) p n -> j (hl p) pr n", hl=2)
                    osv = o_ssds.rearrange("j (pr hl) p n -> j (hl p) pr n", hl=2)
                    S.dma("sp", h0n[0].t[:], stv[0], writes=[h0n[0].b])
                    for j in range(NS):
                        yield
                        jj = j % 2
                        if j + 1 < NS:
                            S.dma("sp", h0n[1 - jj].t[:], stv[j + 1], writes=[h0n[1 - jj].b])
                        pbt = PB[jj]
                        for pr in range(4):
                            S.op("pe", lambda e, pr=pr, jj=jj, pbt=pbt: e.transpose(pbt.t[:, pr * 128:(pr + 1) * 128], h0n[jj].t[:, pr, :], ident),
                                 reads=[h0n[jj].b, cst.b], writes=[pbt.b])
                        S.op("act", lambda e, jj=jj, pbt=pbt: e.activation(out=h0T[jj].t[:].rearrange("p h q -> p (h q)"), in_=pbt.t[:, :], func=AF.Copy),
                             reads=[pbt.b], writes=[h0T[jj].b])
                        for h in range(8):
                            pr, hl = h // 2, h % 2
                            S.op("pe", lambda e, h=h, pr=pr, hl=hl, jj=jj, j=j: e.matmul(
                                ypb.t[64 * hl:64 * hl + 64, pr * T + LS * j:pr * T + LS * j + LS], h0T[jj].t[:, h, :],
                                CdT.t[:, h, LS * j:LS * j + LS], start=(j == 0 and pr == 0), stop=False, skip_group_check=True),
                                reads=[h0T[jj].b, CdT.b], writes=[ypb.b])
                        S.op("dve", lambda e, jj=jj, j=j: e.tensor_scalar(out=Bj[jj].t[0:T], in0=Btm.t[0:T], scalar1=segi[:, j:j + 1],
                                                                          scalar2=None, op0=ALU.mult),
                             reads=[Btm.b, cst.b], writes=[Bj[jj].b])
                        pby = PB[3]
                        for pr in range(4):
                            S.op("pe", lambda e, pr=pr, jj=jj, pby=pby: e.matmul(
                                pby.t[:, pr * 128:(pr + 1) * 128], Xdec.t[0:T, 2 * pr:2 * pr + 2, :], Bj[jj].t[0:T, pr // 2, :],
                                start=True, stop=True), reads=[Xdec.b, Bj[jj].b], writes=[pby.b])
                        S.op("dve", lambda e, jj=jj, j=j: TT(e, hn[jj].t[:], h0n[jj].t[:],
                                                             decfm.t[:, :, j:j + 1].to_broadcast([128, 4, 128]), ALU.mult),
                             reads=[h0n[jj].b, decfm.b], writes=[hn[jj].b])
                        S.op("dve", lambda e, jj=jj, pby=pby: TT(e, hn[jj].t[:], hn[jj].t[:],
                                                                 pby.t[:, :].rearrange("p (a n) -> p a n", n=128), ALU.add),
                             reads=[hn[jj].b, pby.b], writes=[hn[jj].b])
                        S.dma("sp", osv[j], hn[jj].t[:], reads=[hn[jj].b], buf=hn[jj].b)
                    outbufs.extend([hn[0].b, hn[1].b])
                for h in range(8):
                    pr, hl = h // 2, h % 2
                    out = ypb.t[64 * hl:64 * hl + 64, pr * T:(pr + 1) * T]
                    S.op("pe", lambda e, h=h, out=out, pr=pr: e.matmul(out, Xtm.t[0:T, h, :], MT.t[0:T, h, 0:T],
                                                                       start=(pr == 0 and not is_s), stop=is_s, skip_group_check=True),
                         reads=[Xtm.b, MT.b], writes=[ypb.b])
                    if not is_s:
                        S.op("pe", lambda e, h=h, out=out: e.matmul(out, STb.t[:, h, :], CdT.t[:, h, 0:T], start=False, stop=True,
                                                                    skip_group_check=True),
                             reads=[STb.b, CdT.b], writes=[ypb.b])
                yield
                for pr in range(4):
                    S.op("dve", lambda e, pr=pr, cs_=cs_: e.scalar_tensor_tensor(
                        out=yg.t[:, pr, 0:T], in0=xsT.t[:, pr, cs_], scalar=prm.t[:, P_SSDFM + pr:P_SSDFM + pr + 1],
                        in1=ypb.t[:, pr * T:(pr + 1) * T], op0=ALU.mult, op1=ALU.add),
                        reads=[xsT.b, prm.b, ypb.b], writes=[yg.b])
                S.op("pool", lambda e, cs_=cs_: TT(e, yg.t[:, :, 0:T], yg.t[:, :, 0:T], szT.t[:, :, cs_], ALU.mult),
                     reads=[yg.b, szT.b], writes=[yg.b])
                S.op("act", lambda e: e.activation(out=ysq.t[:, :, 0:T], in_=yg.t[:, :, 0:T], func=AF.Square),
                     reads=[yg.b], writes=[ysq.b])
                for g in range(2):
                    for k in range(2):
                        S.op("pe", lambda e, g=g, k=k: e.matmul(PB[3].t[:, g * T:(g + 1) * T], onesf.t[:], ysq.t[:, 2 * g + k, 0:T],
                                                                start=(k == 0), stop=(k == 1)),
                             reads=[onesf.b, ysq.b], writes=[PB[3].b])
                S.op("act", lambda e: e.activation(out=rsb.t[:, :, 0:T], in_=PB[3].t[:, 0:2 * T].rearrange("p (g l) -> p g l", l=T),
                                                   func=AF.Sqrt, scale=1.0 / 256, bias=EPS), reads=[PB[3].b], writes=[rsb.b])
                S.op("dve", lambda e: e.reciprocal(out=rsb.t[:, :, 0:T], in_=rsb.t[:, :, 0:T]), reads=[rsb.b], writes=[rsb.b])
                for pr in range(4):
                    S.op("dve", lambda e, pr=pr: e.scalar_tensor_tensor(
                        out=mixt[ti % 2].t[:, pr, c0:c0 + T], in0=yg.t[:, pr, 0:T],
                        scalar=prm.t[:, P_SSDFM + 4 + pr:P_SSDFM + 5 + pr], in1=rsb.t[:, pr // 2, 0:T], op0=ALU.mult, op1=ALU.mult),
                        reads=[yg.b, prm.b, rsb.b], writes=[mixt[ti % 2].sub("ssd")])
                yield
                if not is_s:
                    for g in range(2):
                        S.op("pe", lambda e, g=g: e.matmul(PB[6].t[:, g * 256:(g + 1) * 256], Btm.t[0:T, g, :],
                                                           Xdec.t[0:T, 4 * g:4 * g + 4, :], start=True, stop=True),
                             reads=[Btm.b, Xdec.b], writes=[PB[6].b])
                    S.op("dve", lambda e: TT(e, ST.t[:], ST.t[:], eA.t[:, :, T - 1:T].to_broadcast([128, 8, 64]), ALU.mult),
                         reads=[ST.b, eA.b], writes=[ST.b])
                    S.op("dve", lambda e: TT(e, ST.t[:], ST.t[:], PB[6].t[:, :].rearrange("p (h q) -> p h q", q=64), ALU.add),
                         reads=[ST.b, PB[6].b], writes=[ST.b])
                    S.op("act", lambda e: e.activation(out=STb.t[:], in_=ST.t[:], func=AF.Copy), reads=[ST.b], writes=[STb.b])
            if ti == 7:
                S.dma("sp", o_ssdp, ST.t[:].rearrange("p h q -> p (h q)"), reads=[ST.b], buf=ST.b)
                outbufs.append(ST.b)

            ckpt("D%d" % ti)
            yield

        def chain2(ti):
            t0, NT, is_s = TILES_A[ti]
            u5T = u5Ts[ti % 2]
            if is_s:
                S.dma("sp", sts5.t[:].rearrange("p a s q -> p (a s q)"), sts5_d, writes=[sts5.b])
            if not is_s:
                groups = [(list(range(16)), k * T5, T5) for k in range(NT // T5)]
            else:
                groups = [(list(range(8)), 0, 64), (list(range(8, 16)), 0, 64)]
            def emit_bu(g_):
                slist_, tk0_, ntok_ = groups[g_]
                bus = busd[g_ % 2]
                for part, pb in ((0, PB[5]), (1, PB[6])):
                    for idx, s in enumerate(slist_):
                        S.op("pe", lambda e, part=part, pb=pb, idx=idx, s=s: e.matmul(
                            pb.t[:, idx * ntok_:(idx + 1) * ntok_], s5BT.t[:, part, s, :], u5T.t[:, s // 4, tk0_:tk0_ + ntok_],
                            start=True, stop=True), reads=[s5BT.b, u5T.b], writes=[pb.b])
                S.op("act", lambda e: e.activation(out=bus[0].t[:], in_=PB[5].t[:, :], func=AF.Copy), reads=[PB[5].b], writes=[bus[0].b])
                S.op("act", lambda e: e.activation(out=bus[1].t[:], in_=PB[6].t[:, :], func=AF.Copy), reads=[PB[6].b], writes=[bus[1].b])
            def views(g_):
                slist_, tk0_, ntok_ = groups[g_]
                s0_ = slist_[0]
                if not is_s:
                    V3 = lambda ap: ap.rearrange("p (s t) -> p s t", t=T5)
                    QR, QI = Qtab.t[:, 0], Qtab.t[:, 1]
                    PR_, PI_ = Ptab.t[:, 0], Ptab.t[:, 1]
                    msk = mask32.t[:].rearrange("p s t -> p (s t)")
                    first = lambda ap: V3(ap)[:, :, 0]
                    cin_r, cin_i = s5cr.t[:, 0, :], s5cr.t[:, 1, :]
                else:
                    V3 = lambda ap: ap.rearrange("p (s q b) -> p s q b", q=NS, b=LS)
                    bc = lambda ap: ap.unsqueeze(2).to_broadcast([128, 8, NS, LS])
                    QR, QI = bc(Qtab.t[:, 0, s0_:s0_ + 8, 0:LS]), bc(Qtab.t[:, 1, s0_:s0_ + 8, 0:LS])
                    PR_, PI_ = bc(Ptab.t[:, 0, s0_:s0_ + 8, 0:LS]), bc(Ptab.t[:, 1, s0_:s0_ + 8, 0:LS])
                    msk = mask4.t[:].rearrange("p s t -> p (s t)")
                    first = lambda ap: V3(ap)[:, :, :, 0]
                    cin_r, cin_i = sts5.t[:, 0, s0_:s0_ + 8, :], sts5.t[:, 1, s0_:s0_ + 8, :]
                return V3, QR, QI, PR_, PI_, msk, first, cin_r, cin_i
            vsets = [[s5v[0], s5v[1]], [s5vb[0], s5vb[1]]]

            def mults_adds(g_):
                V3, QR, QI, PR_, PI_, msk, first, cin_r, cin_i = views(g_)
                bus = busd[g_ % 2]
                br, bi = V3(bus[0].t[:]), V3(bus[1].t[:])
                t1, t2, t3, t4 = s5t[0], s5t[1], s5t34[0], s5t34[1]
                vr, vi = vsets[g_ % 2]
                tb = [Qtab.b]
                for (o, a, b_, rd) in ((t1, QR, br, bus[0].b), (t2, QI, bi, bus[1].b), (t3, QR, bi, bus[1].b), (t4, QI, br, bus[0].b)):
                    S.op("dve", lambda e, o=o, a=a, b_=b_: TT(e, V3(o.t[:]), a, b_, ALU.mult), reads=tb + [rd], writes=[o.b])
                S.op(ENG_ADDS, lambda e: TT(e, vr.t[:], t1.t[:], t2.t[:], ALU.subtract), reads=[t1.b, t2.b], writes=[vr.b])
                S.op(ENG_ADDS, lambda e: TT(e, vi.t[:], t3.t[:], t4.t[:], ALU.add), reads=[t3.b, t4.b], writes=[vi.b])
            emit_bu(0)
            if len(groups) > 1:
                emit_bu(1)
            mults_adds(0)
            pend_y5 = [None]
            for gi_, (slist, tk0, ntok) in enumerate(groups):
                yield
                ns = len(slist)
                s0 = slist[0]
                V3, QR, QI, PR_, PI_, msk, first, cin_r, cin_i = views(gi_)
                vr, vi = vsets[gi_ % 2]
                if gi_ + 1 < len(groups):
                    mults_adds(gi_ + 1)
                    yield
                if gi_ + 2 < len(groups):
                    emit_bu(gi_ + 2)
                S.op("dve", lambda e: TT(e, first(vr.t[:]), first(vr.t[:]), cin_r, ALU.add), reads=[vr.b, s5cr.b, sts5.b], writes=[vr.b])
                S.op("dve", lambda e: TT(e, first(vi.t[:]), first(vi.t[:]), cin_i, ALU.add), reads=[vi.b, s5cr.b, sts5.b], writes=[vi.b])
                yield
                s5k[0] ^= 1
                gr, gi2 = s5g[s5k[0]][0], s5g[s5k[0]][1]
                S.op("dve", lambda e: e.tensor_tensor_scan(out=gr.t[:], data0=msk, data1=vr.t[:], initial=0.0, op0=ALU.mult, op1=ALU.add),
                     reads=[vr.b, mask32.b, mask4.b], writes=[gr.b])
                S.op("dve", lambda e: e.tensor_tensor_scan(out=gi2.t[:], data0=msk, data1=vi.t[:], initial=0.0, op0=ALU.mult, op1=ALU.add),
                     reads=[vi.b, mask32.b, mask4.b], writes=[gi2.b])
                yield
                o1, o2 = s5o[0], s5o[1]
                hr, hi = s5h[gi_ % 2][0], s5h[gi_ % 2][1]
                seq = [(o1, PR_, gr, ALU.mult), (o2, PI_, gi2, ALU.mult), (hr, o1, o2, ALU.subtract),
                       (o1, PR_, gi2, ALU.mult), (o2, PI_, gr, ALU.mult), (hi, o1, o2, ALU.add)]
                for (o, a, b, op) in seq:
                    a3 = a if not isinstance(a, TL) else V3(a.t[:])
                    rd = [Ptab.b, b.b] + ([a.b] if isinstance(a, TL) else [])
                    S.op(ENG_OUTROT, lambda e, o=o, a3=a3, b=b, op=op: TT(e, V3(o.t[:]), a3, V3(b.t[:]), op), reads=rd, writes=[o.b])
                yield
                if not is_s:
                    glr, gli = V3(gr.t[:])[:, :, T5 - 1], V3(gi2.t[:])[:, :, T5 - 1]
                    plr, pli = Ptab.t[:, 0, :, T5 - 1], Ptab.t[:, 1, :, T5 - 1]
                    c_ = lambda i: s5c.t[:, i, :]
                    outr, outi = s5cr.t[:, 0, :], s5cr.t[:, 1, :]
                else:
                    glr, gli = V3(gr.t[:])[:, :, :, LS - 1], V3(gi2.t[:])[:, :, :, LS - 1]
                    plr = Ptab.t[:, 0, s0:s0 + 8, LS - 1:LS].to_broadcast([128, 8, NS])
                    pli = Ptab.t[:, 1, s0:s0 + 8, LS - 1:LS].to_broadcast([128, 8, NS])
                    c_ = lambda i: hn[0].t[:, i, :].rearrange("p (s q) -> p s q", q=NS)
                    outr, outi = s5fin.t[:, 0, s0:s0 + 8, 1:17], s5fin.t[:, 1, s0:s0 + 8, 1:17]
                cb_ = [s5c.b, hn[0].b]
                cseq = [(c_(0), plr, glr, ALU.mult), (c_(1), pli, gli, ALU.mult), (c_(2), plr, gli, ALU.mult), (c_(3), pli, glr, ALU.mult)]
                for (o, a, b, op) in cseq:
                    S.op("dve", lambda e, o=o, a=a, b=b, op=op: TT(e, o, a, b, op), reads=[Ptab.b, gr.b, gi2.b] + cb_, writes=cb_)
                S.op("dve", lambda e: TT(e, outr, c_(0), c_(1), ALU.subtract), reads=cb_, writes=[s5cr.b, s5fin.b])
                S.op("dve", lambda e: TT(e, outi, c_(2), c_(3), ALU.add), reads=cb_, writes=[s5cr.b, s5fin.b])
                yield
                def emit_y5(gi_=gi_, slist=slist, tk0=tk0, ntok=ntok, hr=hr, hi=hi):
                    y5c0 = 352
                    nq = 4 if not is_s else 2
                    for qi in range(nq):
                        q = qi if not is_s else 2 * gi_ + qi
                        S.op("pe", lambda e, q=q, qi=qi: e.matmul(PB[4].t[:, y5c0 + qi * ntok:y5c0 + (qi + 1) * ntok], dg5.t[:, q, :],
                                                                  u5T.t[:, q, tk0:tk0 + ntok], start=(qi == 0), stop=False, skip_group_check=True),
                             reads=[dg5.b, u5T.b], writes=[PB[4].sub("y5")])
                    for idx, s in enumerate(slist):
                        qi = (s // 4) if not is_s else (s // 4 - 2 * gi_)
                        out = PB[4].t[32 * (s % 4):32 * (s % 4) + 32, y5c0 + qi * ntok:y5c0 + (qi + 1) * ntok]
                        S.op("pe", lambda e, out=out, s=s, idx=idx: e.matmul(out, s5CT.t[:, 0, s, :], hr.t[:, idx * ntok:(idx + 1) * ntok],
                                                                             start=False, stop=False, skip_group_check=True,
                                                                             tile_position=(0, 32 * (s % 4))),
                             reads=[s5CT.b, hr.b], writes=[PB[4].sub("y5")])
                        S.op("pe", lambda e, out=out, s=s, idx=idx: e.matmul(out, s5CT.t[:, 1, s, :], hi.t[:, idx * ntok:(idx + 1) * ntok],
                                                                             start=False, stop=True, skip_group_check=True,
                                                                             tile_position=(0, 32 * (s % 4))),
                             reads=[s5CT.b, hi.b], writes=[PB[4].sub("y5")])
                    q0 = 0 if not is_s else 2 * gi_
                    S.op("act", lambda e: e.activation(out=y5pre.t[:, q0:q0 + nq, tk0:tk0 + ntok],
                                                       in_=PB[4].t[:, y5c0:y5c0 + nq * ntok].rearrange("p (q t) -> p q t", t=ntok), func=AF.Copy),
                         reads=[PB[4].sub("y5")], writes=[y5pre.b])
                if pend_y5[0] is not None:
                    pend_y5[0]()
                    yield
                pend_y5[0] = emit_y5
            if pend_y5[0] is not None:
                pend_y5[0]()
                pend_y5[0] = None
                yield
            if ti == 7:
                S.op("dve", lambda e: e.tensor_copy(out=s5fin.t[:, :, :, 0], in_=s5cr.t[:]), reads=[s5cr.b], writes=[s5fin.b])
            if is_s:
                S.dma("sp", o_s5, s5fin.t[:].rearrange("p a s q -> p (a s q)"), reads=[s5fin.b], buf=s5fin.b)
                outbufs.append(s5fin.b)
            if ti == 0:
                dump("y5pre", y5pre.t[:].rearrange("p k t -> p (k t)"), [128, 4 * NTM], [y5pre.b])
            ckpt("E%d" % ti)
            yield
            S.op("act", lambda e: e.activation(out=g5.t[:, :, 0:NT], in_=y5pre.t[:, :, 0:NT], func=AF.Gelu), reads=[y5pre.b], writes=[g5.b])
            for m in range(4):
                yield
                pb = next_pb()
                for q in range(4):
                    S.op("pe", lambda e, m=m, q=q, pb=pb: e.matmul(pb.t[:, 0:NT], wglu_sb.t[:, q, m * 128:(m + 1) * 128], g5.t[:, q, 0:NT],
                                                                   start=(q == 0), stop=(q == 3)),
                         reads=[wglu_sb.b, g5.b], writes=[pb.b])
                S.op("act", lambda e, m=m, pb=pb: e.activation(out=sgl.t[:, 0:NT], in_=pb.t[:, 0:NT], func=AF.Sigmoid,
                                                               bias=prm.t[:, P_S5M + 4 + m:P_S5M + 5 + m]),
                     reads=[pb.b, prm.b], writes=[sgl.b])
                S.op("dve", lambda e, m=m: TT(e, mixt[ti % 2].t[:, 4 + m, 0:NT], g5.t[:, m, 0:NT], sgl.t[:, 0:NT], ALU.mult),
                     reads=[g5.b, sgl.b], writes=[mixt[ti % 2].sub("s5")])
            S.dma("sp", mixd[:, :, t0:t0 + NT], mixt[ti % 2].t[:, :, 0:NT], reads=mixt[ti % 2].allb(), writes=[mixdb[ti]], buf=mixdb[ti])
            ckpt("T%d" % ti)
            if ti == 0:
                dump("mix0", mixt[0].t[:, :, 0:NTM], [128, 8, NTM], mixt[0].allb())
            yield

        import os as _os
        RATIO = int(_os.environ.get("K_RATIO", "1"))

        def drive(gens, ada_every=0):
            gens = [g for g in gens if g is not None]
            n = 0
            while gens:
                for gi__, g in enumerate(list(gens)):
                    for _ in range((RATIO if gi__ == 0 else 1) if RATIO > 0 else (-RATIO if gi__ == 1 else 1)):
                        try:
                            next(g)
                        except StopIteration:
                            if g in gens:
                                gens.remove(g)
                            break
                n += 1
                if ada_every and n % ada_every == 0:
                    ada_step()
        ada_state[0] = 0
        drive([chain1(0)], ada_every=12)
        for ti_ in range(len(TILES_A)):
            if ti_ == 7:
                while ada_state[1] < len(ADA_CH):
                    ada_step()
                fill_x(a1x, amod.t[:, 0:8, 1:17], [amod.b])
                fill_x(sh1x, chunkmod(MOD_SH1)[:, :, 1:17], [mod.b])
                make_amod([(1, (4, 1)), (2, (7, 2))])
            drive([chain2(ti_), chain1(ti_ + 1) if ti_ + 1 < len(TILES_A) else None], ada_every=(10 if ti_ < 7 else 0))
        dump("mixS", mixt[0].t[:, :, 0:64], [128, 8, 64], mixt[0].allb())
        S.barrier()
        ckpt("1a")
        A.lo = LO_P1
        x1T = A.alloc("x1T", [128, 8, NTOK], F32, top=True)
        vT = A.alloc("vT", [128, 8, NTOK], BF16, top=True)
        wout_sb = A.alloc("wout_sb", [128, 8, D], BF16)
        wout_v = wout.rearrange("(kt p) n -> p kt n", p=128)
        for kh in range(4):
            S.dma("pool", wout_sb.t[:, 2 * kh:2 * kh + 2, :], wout_v[:, 2 * kh:2 * kh + 2, :], writes=[wout_sb.b])
        mixb = [A.alloc("mixb%d" % i, [128, 8, 512], BF16) for i in range(2)]

        def load_mix(ti):
            t0, NT, is_s = TILES_B[ti]
            tiles_a = [i for i, (a0, n0, s0_) in enumerate(TILES_A) if a0 >= t0 and a0 < t0 + NT]
            S.dma("sp", mixb[ti % 2].t[:, :, 0:NT], mixd[:, :, t0:t0 + NT], reads=[mixdb[i] for i in tiles_a], writes=[mixb[ti % 2].b])
        xtm2 = A.alloc("xtm2", [128, 4, D], F32)
        xTm = [A.alloc("xTm%d" % i, [128, 512], F32) for i in range(2)]
        sqb = [A.alloc("sqb%d" % i, [128, 512], BF16) for i in range(2)]
        onesb = A.alloc("onesb", [128, 128], BF16)
        S.op("dve", lambda e: e.memset(onesb.t[:], 1.0), writes=[onesb.b])
        tmp2 = [A.alloc("tmp2_%d" % i, [128, 512], F32) for i in range(2)]
        rstdb = [A.alloc("rstdb%d" % i, [128, 512], F32) for i in range(2)]
        g1x = expand_mod("g1x", chunkmod(MOD_G1)[:, :, 1:17], [mod.b])
        a2x = expand_mod("a2x", amod.t[:, 8:16, 1:17], [amod.b])
        sh2x = expand_mod("sh2x", chunkmod(MOD_SH2)[:, :, 1:17], [mod.b])
        print("arena p1b: lo=%d hi=%d" % (A.lo, A.hi))
        TILES_B = [(i * 512, 512, False) for i in range(4)] + [(SEQ, 64, True)]

        def load_x2(ti):
            t0, NT, is_s = TILES_B[ti]
            for blk in range((NT + 127) // 128):
                rows = min(128, NT - blk * 128)
                S.dma("sp", xtm2.t[0:rows, blk, :], xin[t0 + blk * 128:t0 + blk * 128 + rows, :], writes=[xtm2.sub(blk)])
        load_x2(0)
        load_mix(0)

        def stat_accum(src_ap, m, NT, pbs):
            sq = sqb[m % 2]
            S.op("act", lambda e: e.activation(out=sq.t[:, 0:NT], in_=src_ap, func=AF.Square), reads=[x1T.sub(m)], writes=[sq.b])
            S.op("pe", lambda e: e.matmul(pbs.t[:, 0:NT], onesb.t[:], sq.t[:, 0:NT], start=(m == 0), stop=(m == 7)),
                 reads=[onesb.b, sq.b], writes=[pbs.b])

        def stat_finish(NT, pbs, rs):
            S.op("act", lambda e: e.activation(out=rs.t[:, 0:NT], in_=pbs.t[:, 0:NT], func=AF.Sqrt, scale=1.0 / D, bias=EPS),
                 reads=[pbs.b], writes=[rs.b])
            S.op("dve", lambda e: e.reciprocal(out=rs.t[:, 0:NT], in_=rs.t[:, 0:NT]), reads=[rs.b], writes=[rs.b])

        def b_part1(ti):
            t0, NT, is_s = TILES_B[ti]
            nblk = (NT + 127) // 128
            tsl = slice(t0, t0 + NT)
            pbs = PB[4 + ti % 2]
            for m in range(8):
                pbx = PB[2 + m % 2]
                xm = xTm[m % 2]
                for blk in range(nblk):
                    rows = min(128, NT - blk * 128)
                    S.op("pe", lambda e, blk=blk, rows=rows: e.transpose(
                        pbx.t[:, blk * 128:blk * 128 + rows], xtm2.t[0:rows, blk, m * 128:(m + 1) * 128], cst.t[0:rows, C_ID:C_ID + rows]),
                        reads=[xtm2.sub(blk), cst.b], writes=[pbx.b])
                S.op("act", lambda e: e.activation(out=xm.t[:, 0:NT], in_=pbx.t[:, 0:NT], func=AF.Copy), reads=[pbx.b], writes=[xm.b])
                pb = next_pb()
                for kt in range(8):
                    S.op("pe", lambda e, kt=kt: e.matmul(pb.t[:, 0:NT], wout_sb.t[:, kt, m * 128:(m + 1) * 128], mixb[ti % 2].t[:, kt, 0:NT],
                                                         start=(kt == 0), stop=(kt == 7)),
                         reads=[wout_sb.b, mixb[ti % 2].b], writes=[pb.b])
                if m == 0 and ti + 1 < len(TILES_B):
                    load_mix(ti + 1)
                if not is_s:
                    S.op("dve", lambda e: e.scalar_tensor_tensor(
                        out=x1T.t[:, m, tsl], in0=pb.t[:, 0:NT], scalar=mod.t[:, 8 * MOD_G1 + m, 0:1], in1=xm.t[:, 0:NT],
                        op0=ALU.mult, op1=ALU.add), reads=[pb.b, mod.b, xm.b], writes=[x1T.sub(m)])
                else:
                    S.op("dve", lambda e: TT(e, tmp2[0].t[:, 0:NT], pb.t[:, 0:NT], g1x.t[:, m, :], ALU.mult),
                         reads=[pb.b, g1x.b], writes=[tmp2[0].b])
                    S.op("dve", lambda e: TT(e, x1T.t[:, m, tsl], tmp2[0].t[:, 0:NT], xm.t[:, 0:NT], ALU.add),
                         reads=[tmp2[0].b, xm.b], writes=[x1T.sub(m)])
                stat_accum(x1T.t[:, m, tsl], m, NT, pbs)
                yield
            if ti + 1 < len(TILES_B):
                load_x2(ti + 1)
            yield

        def b_part2(ti):
            t0, NT, is_s = TILES_B[ti]
            tsl = slice(t0, t0 + NT)
            rs = rstdb[ti % 2]
            stat_finish(NT, PB[4 + ti % 2], rs)
            yield
            for m in range(8):
                tq = tmp2[m % 2]
                S.op("dve", lambda e: TT(e, tq.t[:, 0:NT], x1T.t[:, m, tsl], rs.t[:, 0:NT], ALU.mult),
                     reads=[x1T.sub(m), rs.b], writes=[tq.b])
                if not is_s:
                    S.op("act", lambda e: e.activation(out=vT.t[:, m, tsl], in_=tq.t[:, 0:NT], func=AF.Identity,
                                                       scale=amod.t[:, 8 + m, 0:1], bias=mod.t[:, 8 * MOD_SH2 + m, 0:1]),
                         reads=[tq.b, amod.b, mod.b], writes=[vT.sub(m)])
                else:
                    S.op("dve", lambda e: TT(e, tq.t[:, 0:NT], tq.t[:, 0:NT], a2x.t[:, m, :], ALU.mult),
                         reads=[tq.b, a2x.b], writes=[tq.b])
                    S.op("dve", lambda e: TT(e, vT.t[:, m, tsl], tq.t[:, 0:NT], sh2x.t[:, m, :], ALU.add),
                         reads=[tq.b, sh2x.b], writes=[vT.sub(m)])
                yield
            if ti == 0:
                dump("x1p", x1T.t[:, :, 0:256], [128, 8, 256], x1T.allb())
                dump("vp", vT.t[:, :, 0:256], [128, 8, 256], vT.allb())
        drive([b_part1(0)])
        for ti_ in range(len(TILES_B)):
            drive([b_part2(ti_), b_part1(ti_ + 1) if ti_ + 1 < len(TILES_B) else None])
        S.barrier()
        ckpt("1b")

        A.lo = LO_GLOBAL
        tmp2 = [A.alloc("tmp3_%d" % i, [128, 512], F32) for i in range(2)]
        rstdb = [A.alloc("rstd3_%d" % i, [128, 512], F32) for i in range(2)]
        sqb = [A.alloc("sqb3_%d" % i, [128, 512], BF16) for i in range(2)]
        onesb = A.alloc("onesb3", [128, 128], BF16)
        S.op("dve", lambda e: e.memset(onesb.t[:], 1.0), writes=[onesb.b])
        g2x = expand_mod("g2x", chunkmod(MOD_G2)[:, :, 1:17], [mod.b])
        afx = expand_mod("afx", amod.t[:, 16:24, 1:17], [amod.b])
        shfx = expand_mod("shfx", chunkmod(MOD_SHF)[:, :, 1:17], [mod.b])
        LO_P2 = A.lo
        hT = A.alloc("hT", [128, 6, NTOK], BF16)
        wgs = [A.alloc("wgs%d" % i, [128, 8, 256], BF16) for i in range(3)]
        wus = [A.alloc("wus%d" % i, [128, 8, 256], BF16) for i in range(3)]
        wds = [A.alloc("wds%d" % i, [128, 6, D], BF16) for i in range(2)]
        sgt = [A.alloc("sgt%d" % i, [128, 512], BF16) for i in range(2)]
        print("arena p2: lo=%d hi=%d" % (A.lo, A.hi))
        wg_v = wg.rearrange("(kt p) n -> p kt n", p=128)
        wu_v = wu.rearrange("(kt p) n -> p kt n", p=128)
        wd_v = wd.rearrange("(j p) n -> p j n", p=128)
        QUARTERS = [(0, 6), (6, 12), (12, 18), (18, 22)]
        SLABS = [(q, ja + 2 * s) for q, (ja, jb) in enumerate(QUARTERS) for s in range((jb - ja) // 2)]

        def load_gu(si):
            q, j0 = SLABS[si]
            S.dma("pool", wgs[si % 3].t[:], wg_v[:, :, j0 * 128:(j0 + 2) * 128], writes=[wgs[si % 3].b])
            S.dma("pool", wus[si % 3].t[:], wu_v[:, :, j0 * 128:(j0 + 2) * 128], writes=[wus[si % 3].b])

        def load_wd(q):
            ja, jb = QUARTERS[q]
            for jh in range(0, jb - ja, 2):
                S.dma("pool", wds[q % 2].t[:, jh:jh + 2, :], wd_v[:, ja + jh:ja + jh + 2, :], writes=[wds[q % 2].b])
        load_gu(0)
        load_gu(1)
        load_wd(0)
        gbank = [0]
        si = 0
        for q, (ja, jb) in enumerate(QUARTERS):
            if q + 1 < 4:
                load_wd(q + 1)
            for s in range((jb - ja) // 2):
                if si + 2 < len(SLABS):
                    load_gu(si + 2)
                wgt, wut = wgs[si % 3], wus[si % 3]
                for jc in range(2):
                    jj = 2 * s + jc
                    for (t0, NT, is_s) in TILES_B:
                        tsl = slice(t0, t0 + NT)
                        gbank[0] ^= 1
                        pbg, pbu = PB[gbank[0]], PB[2 + gbank[0]]
                        for (wt, pb_) in ((wgt, pbg), (wut, pbu)):
                            for kt in range(8):
                                S.op("pe", lambda e, kt=kt, wt=wt, pb_=pb_: e.matmul(
                                    pb_.t[:, 0:NT], wt.t[:, kt, jc * 128:(jc + 1) * 128], vT.t[:, kt, tsl], start=(kt == 0), stop=(kt == 7)),
                                    reads=[wt.b] + vT.allb(), writes=[pb_.b])
                        sg_ = sgt[gbank[0]]
                        S.op("act", lambda e, pbg=pbg, sg_=sg_: e.activation(out=sg_.t[:, 0:NT], in_=pbg.t[:, 0:NT], func=AF.Silu),
                             reads=[pbg.b], writes=[sg_.b])
                        S.op("dve", lambda e, pbu=pbu, sg_=sg_: TT(e, hT.t[:, jj, tsl], sg_.t[:, 0:NT], pbu.t[:, 0:NT], ALU.mult),
                             reads=[sg_.b, pbu.b], writes=[hT.sub(jj)])
                si += 1
            nj = jb - ja
            wdt = wds[q % 2]
            for (t0, NT, is_s) in TILES_B:
                tsl = slice(t0, t0 + NT)
                for m in range(8):
                    pb = PB[4 + m % 2]
                    for jj in range(nj):
                        S.op("pe", lambda e, jj=jj, m=m, pb=pb: e.matmul(pb.t[:, 0:NT], wdt.t[:, jj, m * 128:(m + 1) * 128], hT.t[:, jj, tsl],
                                                                         start=(jj == 0), stop=(jj == nj - 1)),
                             reads=[wdt.b, hT.sub(jj)], writes=[pb.b])
                    if not is_s:
                        S.op("dve", lambda e, m=m, pb=pb: e.scalar_tensor_tensor(
                            out=x1T.t[:, m, tsl], in0=pb.t[:, 0:NT], scalar=mod.t[:, 8 * MOD_G2 + m, 0:1], in1=x1T.t[:, m, tsl],
                            op0=ALU.mult, op1=ALU.add), reads=[pb.b, mod.b, x1T.sub(m)], writes=[x1T.sub(m)])
                    else:
                        S.op("dve", lambda e, m=m, pb=pb: TT(e, tmp2[0].t[:, 0:NT], pb.t[:, 0:NT], g2x.t[:, m, :], ALU.mult),
                             reads=[pb.b, g2x.b], writes=[tmp2[0].b])
                        S.op("dve", lambda e, m=m: TT(e, x1T.t[:, m, tsl], tmp2[0].t[:, 0:NT], x1T.t[:, m, tsl], ALU.add),
                             reads=[tmp2[0].b, x1T.sub(m)], writes=[x1T.sub(m)])
        S.barrier()
        ckpt("ffn")
        A.lo = LO_P2
        yTs = [A.alloc("yT%d" % i, [128, 8, 512], F32) for i in range(2)]
        ytm = [A.alloc("ytm%d" % i, [128, D], F32) for i in range(2)]
        print("arena final: lo=%d hi=%d" % (A.lo, A.hi))
        oi = [0]

        def f_part1(ti):
            t0, NT, is_s = TILES_B[ti]
            tsl = slice(t0, t0 + NT)
            yT = yTs[ti % 2]
            pbs = PB[6 + ti % 2]
            rs = rstdb[ti % 2]
            for m in range(8):
                stat_accum(x1T.t[:, m, tsl], m, NT, pbs)
                if m % 2 == 1:
                    yield
            stat_finish(NT, pbs, rs)
            yield
            for m in range(8):
                tq = tmp2[m % 2]
                S.op("dve", lambda e: TT(e, tq.t[:, 0:NT], x1T.t[:, m, tsl], rs.t[:, 0:NT], ALU.mult),
                     reads=[x1T.sub(m), rs.b], writes=[tq.b])
                if not is_s:
                    S.op("act", lambda e: e.activation(out=yT.t[:, m, 0:NT], in_=tq.t[:, 0:NT], func=AF.Identity,
                                                       scale=amod.t[:, 16 + m, 0:1], bias=mod.t[:, 8 * MOD_SHF + m, 0:1]),
                         reads=[tq.b, amod.b, mod.b], writes=[yT.sub(m)])
                else:
                    S.op("dve", lambda e: TT(e, tq.t[:, 0:NT], tq.t[:, 0:NT], afx.t[:, m, :], ALU.mult),
                         reads=[tq.b, afx.b], writes=[tq.b])
                    S.op("dve", lambda e: TT(e, yT.t[:, m, 0:NT], tq.t[:, 0:NT], shfx.t[:, m, :], ALU.add),
                         reads=[tq.b, shfx.b], writes=[yT.sub(m)])
                yield

        def f_part2(ti):
            t0, NT, is_s = TILES_B[ti]
            yT = yTs[ti % 2]
            for blk in range((NT + 127) // 128):
                rows = min(128, NT - blk * 128)
                yo = ytm[oi[0] % 2]
                oi[0] += 1
                for half in range(2):
                    pbt = PB[half]
                    for k4 in range(4):
                        kt = 4 * half + k4
                        S.op("pe", lambda e, kt=kt, k4=k4: e.transpose(
                            pbt.t[0:rows, k4 * 128:(k4 + 1) * 128], yT.t[:, kt, blk * 128:blk * 128 + rows], ident),
                            reads=[yT.sub(kt), cst.b], writes=[pbt.b])
                    if half == 0:
                        S.op("act", lambda e: e.activation(out=yo.t[0:rows, 0:512], in_=pbt.t[0:rows, :], func=AF.Copy),
                             reads=[pbt.b], writes=[yo.b])
                    else:
                        S.op("dve", lambda e: e.tensor_copy(out=yo.t[0:rows, 512:1024], in_=pbt.t[0:rows, :]),
                             reads=[pbt.b], writes=[yo.b])
                    yield
                S.dma("sp", yout[t0 + blk * 128:t0 + blk * 128 + rows, :], yo.t[0:rows, :], reads=[yo.b], buf=yo.b)
        import os as _os2
        if True:
            for ti_ in range(len(TILES_B)):
                drive([f_part1(ti_)])
                drive([f_part2(ti_)])
        else:
            drive([f_part1(0)])
            for ti_ in range(len(TILES_B)):
                drive([f_part2(ti_), f_part1(ti_ + 1) if ti_ + 1 < len(TILES_B) else None])
        S.barrier()
    return nc, dumps


def _prep_inputs(inp):
    cstv = _consts()
    prmv = _params(inp)
    BT, CT = _s5mats(inp)
    maps = []
    for i in range(NCORES):
        m = {}
        m["xin"] = np.ascontiguousarray(np.concatenate(
            [inp["x_prompt"][i], inp["x_sample"][NS * i:NS * (i + 1)].reshape(NS * LS, D)], axis=0), dtype=np.float32)
        m["cin"] = np.ascontiguousarray(np.concatenate(
            [inp["c_prompt"][i:i + 1], inp["c_sample"][NS * i:NS * (i + 1)]], axis=0), dtype=np.float32)
        m["wada"] = np.ascontiguousarray(inp["w_ada"][0], dtype=np.float32)
        m["wadaf"] = np.ascontiguousarray(inp["w_ada_f"], dtype=np.float32)
        m["win"] = np.ascontiguousarray(inp["w_in"][0], dtype=np.float32)
        m["wglu"] = np.ascontiguousarray(inp["w_glu"][0], dtype=np.float32)
        m["wout"] = np.ascontiguousarray(inp["w_out"][0], dtype=np.float32)
        m["wg"] = np.ascontiguousarray(inp["w_ffn_gate"][0], dtype=np.float32)
        m["wu"] = np.ascontiguousarray(inp["w_ffn_up"][0], dtype=np.float32)
        m["wd"] = np.ascontiguousarray(inp["w_ffn_down"][0], dtype=np.float32)
        m["cst"] = cstv
        m["prm"] = prmv
        m["s5bt"] = BT.reshape(128, -1)
        m["s5ct"] = CT.reshape(128, -1)
        m["stssd"] = np.ascontiguousarray(inp["state_ssd"][0, NS * i:NS * (i + 1)], dtype=np.float32)
        sc = inp["state_conv"][0, NS * i:NS * (i + 1)]
        m["stconv"] = np.ascontiguousarray(
            sc.reshape(NS, 3, 8, 128).transpose(3, 2, 0, 1).reshape(128, -1), dtype=np.float32)
        sr = inp["state_s5_re"][0, NS * i:NS * (i + 1)]
        si = inp["state_s5_im"][0, NS * i:NS * (i + 1)]
        st = np.stack([sr, si], 0).reshape(2, NS, 16, 128).transpose(3, 0, 2, 1)
        m["sts5"] = np.ascontiguousarray(st.reshape(128, -1), dtype=np.float32)
        maps.append(m)
    return maps


_CACHE = {}


def kernel(**inputs):
    inp = {k: np.asarray(v) for k, v in inputs.items()}
    if "nc" not in _CACHE:
        _CACHE["nc"] = build()[0]
    nc = _CACHE["nc"]
    maps = _prep_inputs(inp)
    res = run_bass_kernel_spmd(nc, maps, core_ids=list(range(NCORES)))
    R = res.results
    y_p = np.stack([R[i]["yout"][:SEQ] for i in range(NCORES)], 0)
    y_s = np.concatenate([R[i]["yout"][SEQ:].reshape(NS, LS, D) for i in range(NCORES)], 0)
    ssd_p = np.stack([R[i]["o_ssdp"].reshape(128, 8, 64).transpose(1, 2, 0) for i in range(NCORES)], 0)[None]
    ssd_s = np.concatenate([R[i]["o_ssds"] for i in range(NCORES)], 0)[None]
    conv = [R[i]["o_conv"].reshape(128, 8, 17, 3).transpose(2, 3, 1, 0).reshape(17, 3, 1024) for i in range(NCORES)]
    conv_p = np.stack([c[0] for c in conv], 0)[None]
    conv_s = np.concatenate([c[1:] for c in conv], 0)[None]
    s5 = [R[i]["o_s5"].reshape(128, 2, 16, 17).transpose(1, 3, 2, 0).reshape(2, 17, 32, 64) for i in range(NCORES)]
    re_p = np.stack([s[0, 0] for s in s5], 0)[None]
    re_s = np.concatenate([s[0, 1:] for s in s5], 0)[None]
    im_p = np.stack([s[1, 0] for s in s5], 0)[None]
    im_s = np.concatenate([s[1, 1:] for s in s5], 0)[None]
    f = lambda a: np.ascontiguousarray(a, dtype=np.float32)
    return (f(y_p), f(y_s), f(ssd_p), f(ssd_s), f(conv_p), f(conv_s), f(re_p), f(re_s), f(im_p), f(im_s))
```

```python
import math
import numpy as np
from contextlib import ExitStack
import concourse.bass as bass
import concourse.mybir as mybir
from concourse.bass_utils import run_bass_kernel_spmd

F32 = mybir.dt.float32
BF16 = mybir.dt.bfloat16
I32 = mybir.dt.int32
AF = mybir.ActivationFunctionType
ALU = mybir.AluOpType

NCORES = 8
D = 1024
SEQ = 2048
NS = 16
LS = 4
NTOK = SEQ + NS * LS
DFF = 2816
NJ = DFF // 128
INP = 2056
EPS = 1e-6
T5 = 32
TILES = [(0, 512), (512, 512), (1024, 512), (1536, 512), (2048, 64)]
PI = math.pi


class Buf:
    def __init__(self, name):
        self.name = name
        self.w = None
        self.r = []
        self.dsem = None
        self.dcnt = 0


class TL:
    def __init__(self, t, name):
        self.t = t
        self.name = name
        self.b = Buf(name)
        self.subs = {}

    def sub(self, k):
        if getattr(self, "nosub", False):
            return self.b
        if k not in self.subs:
            self.subs[k] = Buf("%s_%s" % (self.name, k))
        return self.subs[k]

    def allb(self):
        return [self.b] + list(self.subs.values())

    def __getitem__(self, k):
        return self.t[k]


class Sched:
    ENG = ["pe", "act", "dve", "pool", "sp"]

    def __init__(self, nc, es):
        self.nc = nc
        self.es = es
        self.eobj = {"pe": nc.tensor, "act": nc.scalar, "dve": nc.vector, "pool": nc.gpsimd, "sp": nc.sync}
        self.cnt = {e: 0 for e in self.ENG}
        self.sem = {e: es.enter_context(nc.semaphore("s_" + e)) for e in self.ENG}
        self.seen = {e: {} for e in self.ENG}
        self.dbufs = []
        self.ninst = 0
        self.dead = False
        self.pe_pending = None

    def _flush_pe(self):
        if self.pe_pending is not None:
            self.pe_pending.then_inc(self.sem["pe"], 1)
            self.cnt["pe"] += 1
            self.pe_pending = None

    def _deps(self, eng, reads, writes):
        deps = []
        for b in reads:
            if b.w is not None:
                deps.append(b.w)
        for b in writes:
            if b.w is not None:
                deps.append(b.w)
            deps.extend(b.r)
        waits = {}
        for (sem, val, key) in deps:
            if key == "pe" and eng == "pe":
                continue
            if self.seen[eng].get(key, 0) >= val:
                continue
            if key == "pe" and val > self.cnt["pe"]:
                self._flush_pe()
            if key not in waits or waits[key][1] < val:
                waits[key] = (sem, val)
        for key, (sem, val) in waits.items():
            self.seen[eng][key] = val
        return list(waits.values())

    def op(self, eng, fn, reads=(), writes=()):
        if self.dead:
            return None
        xr = [b for b in reads if getattr(b, "excl", False)]
        if xr:
            reads = [b for b in reads if not getattr(b, "excl", False)]
            writes = list(writes) + xr
        waits = self._deps(eng, reads, writes)
        e = self.eobj[eng]
        for (s_, v_) in waits:
            e.wait_ge(s_, v_)
        if eng == "pe":
            self.pe_pending = fn(e)
            tok = (self.sem[eng], self.cnt[eng] + 1, eng)
        else:
            self.cnt[eng] += 1
            tok = (self.sem[eng], self.cnt[eng], eng)
            fn(e).then_inc(self.sem[eng], 1)
        for b in reads:
            b.r.append(tok)
        for b in writes:
            b.w = tok
            b.r = []
        self.ninst += 1
        return tok

    def dma(self, eng, out, in_, reads=(), writes=(), buf=None, **kw):
        if self.dead:
            return None
        waits = self._deps(eng, reads, writes)
        if buf is None:
            buf = writes[0] if writes else reads[0]
        if buf.dsem is None:
            buf.dsem = self.es.enter_context(self.nc.semaphore("d_" + buf.name))
            self.dbufs.append(buf)
        buf.dcnt += 16
        tok = (buf.dsem, buf.dcnt, "d_" + buf.name)
        e = self.eobj[eng]
        for (s_, v_) in waits:
            e.wait_ge(s_, v_)
        e.dma_start(out=out, in_=in_, **kw).then_inc(buf.dsem, 16)
        for b in reads:
            b.r.append(tok)
        for b in writes:
            b.w = tok
            b.r = []
        self.ninst += 1
        return tok

    def barrier(self):
        if self.dead:
            return
        self._flush_pe()
        for e in self.ENG:
            waits = []
            for o in self.ENG:
                if o != e and self.cnt[o] > self.seen[e].get(o, 0):
                    waits.append((self.sem[o], self.cnt[o]))
                    self.seen[e][o] = self.cnt[o]
            for b in self.dbufs:
                key = "d_" + b.name
                if b.dcnt > self.seen[e].get(key, 0):
                    waits.append((b.dsem, b.dcnt))
                    self.seen[e][key] = b.dcnt
            for (s_, v_) in waits:
                self.eobj[e].wait_ge(s_, v_)

    def emit(self):
        pass


C_ID = 0
C_TRI = 128
C_NEG = 256
C_TRI64 = 384
C_NEG64 = 512
C_SEG64 = 640
C_SEGI = 768
CST_W = 784

P_BMOD = 0
P_GAIN = 64
P_CONV = 88
P_SSDFM = 128
P_S5P = 136
P_S5M = 184
P_SSD8 = 192
PRM_W = 194


def _consts():
    c = np.zeros((128, CST_W), np.float32)
    c[:, C_ID:C_ID + 128] = np.eye(128, dtype=np.float32)
    s = np.arange(128)[:, None]
    l = np.arange(128)[None, :]
    c[:, C_TRI:C_TRI + 128] = (s <= l).astype(np.float32)
    c[:, C_NEG:C_NEG + 128] = np.where(l >= s, 0.0, -30000.0)
    same = (s // LS == l // LS) & (s < 64) & (l < 64)
    c[:, C_TRI64:C_TRI64 + 128] = ((s <= l) & same).astype(np.float32)
    c[:, C_NEG64:C_NEG64 + 128] = np.where((l >= s) & same, 0.0, -30000.0)
    c[:, C_SEG64:C_SEG64 + 128] = same.astype(np.float32)
    j = np.arange(16)[None, :]
    c[:, C_SEGI:C_SEGI + 16] = ((s // LS == j) & (s < 64)).astype(np.float32)
    return c


def _fm(v, nt):
    return np.ascontiguousarray(np.asarray(v, np.float32).reshape(nt, 128).T)


def _params(inp):
    p = np.zeros((128, PRM_W), np.float32)
    p[:, P_BMOD:P_BMOD + 48] = _fm(inp["b_ada"][0], 48)
    p[:, P_BMOD + 48:P_BMOD + 64] = _fm(inp["b_ada_f"], 16)
    p[:, P_GAIN:P_GAIN + 8] = _fm(inp["norm1_g"][0], 8)
    p[:, P_GAIN + 8:P_GAIN + 16] = _fm(inp["norm2_g"][0], 8)
    p[:, P_GAIN + 16:P_GAIN + 24] = _fm(inp["normf_g"], 8)
    cw = inp["conv_w"][0]
    cv = np.zeros((128, 8, 5), np.float32)
    for k in range(4):
        cv[:, :, k] = _fm(cw[k], 8)
    cv[:, :, 4] = _fm(inp["conv_b"][0], 8)
    p[:, P_CONV:P_CONV + 40] = cv.reshape(128, 40)
    Dh = inp["ssd_D"][0]
    dfm = np.zeros((128, 4), np.float32)
    for pr in range(4):
        dfm[0:64, pr] = Dh[2 * pr]
        dfm[64:128, pr] = Dh[2 * pr + 1]
    p[:, P_SSDFM:P_SSDFM + 4] = dfm
    p[:, P_SSDFM + 4:P_SSDFM + 8] = _fm(inp["ssd_norm_g"][0], 4)

    def st(a):
        return np.ascontiguousarray(np.asarray(a, np.float32).reshape(16, 128).T)
    p[:, P_S5P:P_S5P + 16] = st(inp["s5_A_re"][0])
    p[:, P_S5P + 16:P_S5P + 32] = st(inp["s5_A_im"][0])
    p[:, P_S5P + 32:P_S5P + 48] = st(np.repeat(inp["s5_log_step"][0][:, None], 64, axis=1))
    p[:, P_S5M:P_S5M + 4] = _fm(inp["s5_D"][0], 4)
    p[:, P_S5M + 4:P_S5M + 8] = _fm(inp["b_glu"][0], 4)
    p[0:8, P_SSD8] = inp["ssd_dt_bias"][0]
    p[0:8, P_SSD8 + 1] = inp["ssd_A_log"][0]
    return p


def _s5mats(inp):
    Br, Bi = inp["s5_B_re"][0], inp["s5_B_im"][0]
    Cr, Ci = inp["s5_C_re"][0], inp["s5_C_im"][0]
    BT = np.zeros((128, 2, 16, 128), np.float32)
    CT = np.zeros((128, 2, 16, 32), np.float32)
    for s in range(16):
        for gl in range(2):
            g = 2 * s + gl
            r0 = (g % 8) * 16
            BT[r0:r0 + 16, 0, s, gl * 64:(gl + 1) * 64] = Br[g].T
            BT[r0:r0 + 16, 1, s, gl * 64:(gl + 1) * 64] = Bi[g].T
            CT[gl * 64:(gl + 1) * 64, 0, s, gl * 16:(gl + 1) * 16] = Cr[g].T
            CT[gl * 64:(gl + 1) * 64, 1, s, gl * 16:(gl + 1) * 16] = Ci[g].T
    return BT, CT


class Arena:
    def __init__(self, nc, es, words):
        self.t = es.enter_context(nc.sbuf_tensor("arena", [128, words], F32))
        self.words = words
        self.lo = 0
        self.hi = words

    def alloc(self, name, shape, dt, top=False):
        n = 1
        for d in shape[1:]:
            n *= d
        w = n if dt == F32 or dt == I32 else (n + 1) // 2
        w = (w + 3) // 4 * 4
        if top:
            self.hi -= w
            off = self.hi
        else:
            off = self.lo
            self.lo += w
        assert self.lo <= self.hi, "arena overflow at %s: lo=%d hi=%d" % (name, self.lo, self.hi)
        ap = self.t[:, off:off + w]
        if dt != F32:
            ap = ap.bitcast(dt)
        ap = ap[:, 0:n]
        if len(shape) == 3:
            ap = ap.rearrange("p (a b) -> p a b", b=shape[2])
        elif len(shape) == 4:
            ap = ap.rearrange("p (a b c) -> p a b c", b=shape[2], c=shape[3])
        if shape[0] < 128:
            ap = ap[0:shape[0]]
        return TL(ap, name)


class StopBuild(Exception):
    pass


def build(dbg=None, stop_after=None):
    nc = bass.Bass("TRN2", target_bir_lowering=False)

    SH = []

    def ckpt(name):
        if stop_after == name:
            SH[0].barrier()
            SH[0].dead = True
    dt_in = lambda name, shape: nc.dram_tensor(name, list(shape), F32, kind="ExternalInput").ap()
    dt_out = lambda name, shape: nc.dram_tensor(name, list(shape), F32, kind="ExternalOutput").ap()
    xin = dt_in("xin", [NTOK, D])
    cin = dt_in("cin", [17, D])
    wada = dt_in("wada", [D, 6144])
    wadaf = dt_in("wadaf", [D, 2048])
    win = dt_in("win", [D, INP])
    wglu = dt_in("wglu", [512, 512])
    wout = dt_in("wout", [D, D])
    wg = dt_in("wg", [D, DFF])
    wu = dt_in("wu", [D, DFF])
    wd = dt_in("wd", [DFF, D])
    cst_d = dt_in("cst", [128, CST_W])
    prm_d = dt_in("prm", [128, PRM_W])
    s5bt_d = dt_in("s5bt", [128, 2 * 16 * 128])
    s5ct_d = dt_in("s5ct", [128, 2 * 16 * 32])
    stssd_d = dt_in("stssd", [NS, 8, 64, 128])
    stconv_d = dt_in("stconv", [128, 8 * NS * 3])
    sts5_d = dt_in("sts5", [128, 2 * 16 * NS])
    yout = dt_out("yout", [NTOK, D])
    o_ssdp = dt_out("o_ssdp", [128, 512])
    o_ssds = dt_out("o_ssds", [NS, 8, 64, 128])
    o_conv = dt_out("o_conv", [128, 8 * 17 * 3])
    o_s5 = dt_out("o_s5", [128, 2 * 16 * 17])
    mixd = nc.dram_tensor("mixd", [128, 8, NTOK], BF16, kind="Internal").ap()
    dumps = {}

    with ExitStack() as es:
        S = Sched(nc, es)
        SH.append(S)
        A = Arena(nc, es, 53200)
        outbufs = []

        def dump(name, ap, shape, reads):
            if dbg is None or name not in dbg:
                return
            d = dt_out("dbg_" + name, shape)
            dumps[name] = shape
            b = Buf("dbg_" + name)
            S.dma("sp" if ap.dtype == F32 else "pool", d, ap, reads=reads, buf=b)
            outbufs.append(b)

        PB = [TL(es.enter_context(nc.psum_tensor("pb%d" % i, [128, 512], F32)), "pb%d" % i) for i in range(8)]
        for pb_ in PB:
            pb_.b.excl = True
            pb_.nosub = True

        def pbf(i):
            return PB[i].t[:].bitcast(BF16)

        cst = A.alloc("cst", [128, CST_W], F32)
        prm = A.alloc("prm", [128, PRM_W], F32)
        identb = A.alloc("identb", [128, 128], BF16)
        onesf = A.alloc("onesf", [128, 128], F32)
        mod = A.alloc("mod", [128, 64, 17], F32)
        amod = A.alloc("amod", [128, 24, 17], F32)
        s5fin = A.alloc("s5fin", [128, 2, 16, 17], F32)
        scT = A.alloc("scT", [128, 8, 17], BF16)
        LO_GLOBAL = A.lo

        ident = cst.t[:, C_ID:C_ID + 128]
        S.dma("sp", cst.t[:], cst_d, writes=[cst.b])
        S.dma("sp", prm.t[:], prm_d, writes=[prm.b])
        S.op("act", lambda e: e.activation(out=identb.t[:], in_=ident, func=AF.Copy), reads=[cst.b], writes=[identb.b])
        S.op("dve", lambda e: e.memset(onesf.t[:], 1.0), writes=[onesf.b])

        def chunkmod(i):
            return mod.t[:, 8 * i:8 * i + 8, :]

        cs = A.alloc("cs", [17, D], F32)
        slabs = [A.alloc("adaslab%d" % i, [128, 8, 512], BF16) for i in range(3)]
        S.dma("sp", cs.t[:], cin, writes=[cs.b])
        S.op("act", lambda e: e.activation(out=cs.t[:], in_=cs.t[:], func=AF.Silu), reads=[cs.b], writes=[cs.b])
        for kt in range(8):
            S.op("pe", lambda e, kt=kt: e.transpose(PB[2].t[:, kt * 17:(kt + 1) * 17], cs.t[:, kt * 128:(kt + 1) * 128],
                                                    cst.t[0:17, C_ID:C_ID + 17]),
                 reads=[cs.b, cst.b], writes=[PB[2].b])
        S.op("act", lambda e: e.activation(out=scT.t[:].rearrange("p k s -> p (k s)"), in_=PB[2].t[:, 0:136], func=AF.Copy),
             reads=[PB[2].b], writes=[scT.b])
        wada_v = wada.rearrange("(kt p) n -> p kt n", p=128)
        wadaf_v = wadaf.rearrange("(kt p) n -> p kt n", p=128)

        def slab_src(i):
            if i < 12:
                return wada_v[:, :, i * 512:(i + 1) * 512]
            return wadaf_v[:, :, (i - 12) * 512:(i - 11) * 512]

        def load_slab(i):
            sl = slabs[i % 3]
            for kh in range(2):
                S.dma("pool", sl.t[:, 4 * kh:4 * kh + 4, :], slab_src(i)[:, 4 * kh:4 * kh + 4, :], writes=[sl.b])
        load_slab(0)
        load_slab(1)
        for i in range(4):
            if i + 2 < 4:
                load_slab(i + 2)
            sl = slabs[i % 3]
            pb = PB[i % 2]
            for fc in range(4):
                for kt in range(8):
                    S.op("pe", lambda e, fc=fc, kt=kt, sl=sl, pb=pb: e.matmul(
                        pb.t[:, fc * 17:(fc + 1) * 17], sl.t[:, kt, fc * 128:(fc + 1) * 128], scT.t[:, kt, :],
                        start=(kt == 0), stop=(kt == 7)), reads=[sl.b, scT.b], writes=[pb.b])
            S.op("dve", lambda e, i=i, pb=pb: e.tensor_tensor(
                out=mod.t[:, 4 * i:4 * i + 4, :], in0=pb.t[:, 0:68].rearrange("p (c s) -> p c s", s=17),
                in1=prm.t[:, P_BMOD + 4 * i:P_BMOD + 4 * i + 4].unsqueeze(2).to_broadcast([128, 4, 17]), op=ALU.add),
                reads=[pb.b, prm.b], writes=[mod.b])
        def make_amod(lst):
          for k, (sci, gi) in lst:
            S.op("dve", lambda e, k=k, sci=sci, gi=gi: e.scalar_tensor_tensor(
                out=amod.t[:, 8 * k:8 * k + 8, :], in0=chunkmod(sci), scalar=1.0,
                in1=prm.t[:, P_GAIN + 8 * gi:P_GAIN + 8 * gi + 8].unsqueeze(2).to_broadcast([128, 8, 17]),
                op0=ALU.add, op1=ALU.mult), reads=[mod.b, prm.b], writes=[amod.b])
        make_amod([(0, (1, 0))])
        dump("mod", mod.t[:].rearrange("p c s -> p (c s)"), [128, 64 * 17], [mod.b])
        S.barrier()
        S.emit()
        A.lo = LO_GLOBAL

        MOD_SH1, MOD_G1, MOD_SH2, MOD_G2, MOD_SHF = 0, 2, 3, 5, 6

        def expand_mod(name, src_ap, srcbufs):
            t = A.alloc(name, [128, 8, 64], F32)
            S.op("dve", lambda e: e.tensor_copy(out=t.t[:].rearrange("p k (s b) -> p k s b", b=LS),
                                                in_=src_ap.unsqueeze(3).to_broadcast([128, 8, NS, LS])),
                 reads=srcbufs, writes=[t.b])
            return t

        LO_P1 = A.lo
        mixt = [A.alloc("mixt%d" % i, [128, 8, 256], BF16) for i in range(2)]
        mixdb = [Buf("mixd%d" % i) for i in range(9)]
        win_sb = A.alloc("win_sb", [128, 8, INP], BF16)
        wglu_sb = A.alloc("wglu_sb", [128, 4, 512], BF16)
        s5BT = A.alloc("s5BT", [128, 2, 16, 128], BF16)
        s5CT = A.alloc("s5CT", [128, 2, 16, 32], BF16)
        win_v = win.rearrange("(kt p) n -> p kt n", p=128)
        for kh in range(4):
            for ch in range(2):
                S.dma("pool", win_sb.t[:, 2 * kh:2 * kh + 2, ch * 1028:(ch + 1) * 1028],
                      win_v[:, 2 * kh:2 * kh + 2, ch * 1028:(ch + 1) * 1028], writes=[win_sb.b])
        for a_ in range(4):
            S.dma("pool", s5BT.t[:].rearrange("p a s c -> p (a s c)")[:, a_ * 1024:(a_ + 1) * 1024],
                  s5bt_d[:, a_ * 1024:(a_ + 1) * 1024], writes=[s5BT.b])
        S.dma("pool", s5CT.t[:].rearrange("p a s c -> p (a s c)"), s5ct_d, writes=[s5CT.b])
        S.dma("pool", wglu_sb.t[:], wglu.rearrange("(kt p) n -> p kt n", p=128), writes=[wglu_sb.b])
        S.op("dve", lambda e: e.tensor_scalar(out=s5CT.t[:, 1], in0=s5CT.t[:, 1], scalar1=-1.0, scalar2=None, op0=ALU.mult),
             reads=[s5CT.b], writes=[s5CT.b])

        a1x = A.alloc("a1x", [128, 8, 64], F32)
        sh1x = A.alloc("sh1x", [128, 8, 64], F32)

        def fill_x(t, src_ap, srcbufs):
            S.op("dve", lambda e: e.tensor_copy(out=t.t[:].rearrange("p k (s b) -> p k s b", b=LS),
                                                in_=src_ap.unsqueeze(3).to_broadcast([128, 8, NS, LS])),
                 reads=srcbufs, writes=[t.b])
        adab = [TL(a1x.t[:].rearrange("p k t -> p (k t)").bitcast(BF16).rearrange("p (k c) -> p k c", c=128), "adab0"),
                TL(sh1x.t[:].rearrange("p k t -> p (k t)").bitcast(BF16).rearrange("p (k c) -> p k c", c=128), "adab1")]
        adab[0].b = a1x.b
        adab[1].b = sh1x.b
        ADA_CH = list(range(16, 64))

        def ada_load(ci):
            c = ADA_CH[ci]
            src = wada_v[:, :, c * 128:(c + 1) * 128] if c < 48 else wadaf_v[:, :, (c - 48) * 128:(c - 47) * 128]
            S.dma("pool", adab[ci % 2].t[:], src, writes=[adab[ci % 2].b])

        def ada_compute(ci):
            c = ADA_CH[ci]
            sl = adab[ci % 2]
            pb = next_pb()
            for kt in range(8):
                S.op("pe", lambda e, kt=kt: e.matmul(pb.t[:, 0:17], sl.t[:, kt, :], scT.t[:, kt, :], start=(kt == 0), stop=(kt == 7)),
                     reads=[sl.b, scT.b], writes=[pb.b])
            S.op("dve", lambda e: e.tensor_scalar(out=mod.t[:, c, :], in0=pb.t[:, 0:17], scalar1=prm.t[:, P_BMOD + c:P_BMOD + c + 1],
                                                  scalar2=None, op0=ALU.add), reads=[pb.b, prm.b], writes=[mod.b])
        ada_state = [0, 0]

        def ada_step():
            if ada_state[1] >= len(ADA_CH):
                return
            while ada_state[0] < min(len(ADA_CH), ada_state[1] + 2):
                ada_load(ada_state[0])
                ada_state[0] += 1
            ada_compute(ada_state[1])
            ada_state[1] += 1

        ssd8 = A.alloc("ssd8", [8, 4], F32)
        S.op("act", lambda e: e.activation(out=ssd8.t[:, 1:2], in_=prm.t[0:8, P_SSD8 + 1:P_SSD8 + 2], func=AF.Exp),
             reads=[prm.b], writes=[ssd8.b])
        S.op("dve", lambda e: e.tensor_scalar(out=ssd8.t[:, 1:2], in0=ssd8.t[:, 1:2], scalar1=-1.0, scalar2=None, op0=ALU.mult),
             reads=[ssd8.b], writes=[ssd8.b])
        S.op("dve", lambda e: e.tensor_copy(out=ssd8.t[:, 0:1], in_=prm.t[0:8, P_SSD8:P_SSD8 + 1]), reads=[prm.b], writes=[ssd8.b])

        Ptab = A.alloc("Ptab", [128, 2, 16, T5], F32)
        Qtab = A.alloc("Qtab", [128, 2, 16, T5], F32)
        s5t = [A.alloc("s5t%d" % i, [128, 512], F32) for i in range(2)]

        def alias(name, ap, buf):
            tl = TL(ap, name)
            tl.b = buf
            return tl
        sw = alias("s5work", s5t[1].t[:, 0:384].rearrange("p (a b) -> p a b", b=16), s5t[1].b)
        tmpA = alias("tmpA", s5t[0].t[:, 0:256].rearrange("p (a b) -> p a b", b=T5 // 2), s5t[0].b)
        tmpB = alias("tmpB", s5t[0].t[:, 256:512].rearrange("p (a b) -> p a b", b=T5 // 2), s5t[0].b)
        mask32 = A.alloc("mask32", [128, 16, T5], BF16)
        s5v = [A.alloc("s5v%d" % i, [128, 512], F32) for i in range(2)]
        qtmp = alias("qtmp", s5v[0].t[:].rearrange("p (s t) -> p s t", t=T5), s5v[0].b)
        mask4 = A.alloc("mask4", [128, 128, LS], BF16)
        s5cr = A.alloc("s5cr", [128, 2, 16], F32)
        W = lambda i: sw.t[:, i, :]
        pv = lambda i: prm.t[:, P_S5P + 16 * i:P_S5P + 16 * (i + 1)]
        swb = [sw.b, prm.b]

        def dv(fn):
            S.op("dve", fn, reads=swb, writes=[sw.b])

        def act(fn):
            S.op("act", fn, reads=swb, writes=[sw.b])
        TT = lambda e, o, a, b, op: e.tensor_tensor(out=o, in0=a, in1=b, op=op)
        def exp_acc(dst, src):
            dv(lambda e: e.tensor_scalar(out=W(22), in0=src, scalar1=1.0 / 16, scalar2=None, op0=ALU.mult))
            dv(lambda e: e.tensor_scalar(out=dst, in0=W(22), scalar1=1.0 / 7, scalar2=1.0, op0=ALU.mult, op1=ALU.add))
            for k in (6, 5, 4, 3, 2, 1):
                dv(lambda e: TT(e, dst, dst, W(22), ALU.mult))
                dv(lambda e, k=k: e.tensor_scalar(out=dst, in0=dst, scalar1=1.0 / k, scalar2=1.0, op0=ALU.mult, op1=ALU.add))
            for _ in range(4):
                dv(lambda e: TT(e, dst, dst, dst, ALU.mult))
        exp_acc(W(0), pv(2))
        dv(lambda e: TT(e, W(1), pv(0), W(0), ALU.mult))
        dv(lambda e: TT(e, W(2), pv(1), W(0), ALU.mult))
        exp_acc(W(3), W(1))

        def range_reduce(dst, src, add):
            ki = A_ki
            dv(lambda e: e.tensor_scalar(out=W(20), in0=src, scalar1=float(add), scalar2=1.0 / (2 * PI), op0=ALU.add, op1=ALU.mult))
            S.op("dve", lambda e: e.tensor_copy(out=ki.t[:], in_=W(20)), reads=swb, writes=[ki.b])
            S.op("dve", lambda e: e.tensor_copy(out=W(21), in_=ki.t[:]), reads=[ki.b], writes=[sw.b])
            dv(lambda e: e.tensor_scalar(out=W(20), in0=src, scalar1=float(add), scalar2=None, op0=ALU.add))
            dv(lambda e: e.scalar_tensor_tensor(out=dst, in0=W(21), scalar=-2 * PI, in1=W(20), op0=ALU.mult, op1=ALU.add))
            dv(lambda e: e.tensor_scalar(out=dst, in0=dst, scalar1=PI, scalar2=-PI, op0=ALU.min, op1=ALU.max))
        A_ki = A.alloc("s5ki", [128, 16], I32)
        range_reduce(W(4), W(2), 0.0)
        range_reduce(W(5), W(2), PI / 2)
        act(lambda e: e.activation(out=W(6), in_=W(4), func=AF.Sin))
        act(lambda e: e.activation(out=W(7), in_=W(5), func=AF.Sin))
        dv(lambda e: TT(e, W(8), W(3), W(7), ALU.mult))
        dv(lambda e: TT(e, W(9), W(3), W(6), ALU.mult))
        dv(lambda e: e.tensor_scalar(out=W(10), in0=W(8), scalar1=-1.0, scalar2=None, op0=ALU.add))
        dv(lambda e: TT(e, W(11), pv(0), pv(0), ALU.mult))
        dv(lambda e: TT(e, W(12), pv(1), pv(1), ALU.mult))
        dv(lambda e: TT(e, W(11), W(11), W(12), ALU.add))
        dv(lambda e: e.reciprocal(out=W(11), in_=W(11)))
        dv(lambda e: TT(e, W(12), W(10), pv(0), ALU.mult))
        dv(lambda e: TT(e, W(13), W(9), pv(1), ALU.mult))
        dv(lambda e: TT(e, W(12), W(12), W(13), ALU.add))
        dv(lambda e: TT(e, W(14), W(12), W(11), ALU.mult))
        dv(lambda e: TT(e, W(12), W(9), pv(0), ALU.mult))
        dv(lambda e: TT(e, W(13), W(10), pv(1), ALU.mult))
        dv(lambda e: TT(e, W(12), W(12), W(13), ALU.subtract))
        dv(lambda e: TT(e, W(15), W(12), W(11), ALU.mult))
        dv(lambda e: TT(e, W(12), W(8), W(8), ALU.mult))
        dv(lambda e: TT(e, W(13), W(9), W(9), ALU.mult))
        dv(lambda e: TT(e, W(12), W(12), W(13), ALU.add))
        dv(lambda e: e.reciprocal(out=W(12), in_=W(12)))
        dv(lambda e: TT(e, W(16), W(8), W(12), ALU.mult))
        dv(lambda e: e.scalar_tensor_tensor(out=W(17), in0=W(9), scalar=-1.0, in1=W(12), op0=ALU.mult, op1=ALU.mult))

        def build_pow(tab, br, bi):
            tb = [tab.b, sw.b, tmpA.b, tmpB.b]
            S.op("dve", lambda e: e.tensor_copy(out=tab.t[:, 0, :, 0], in_=br), reads=tb, writes=[tab.b])
            S.op("dve", lambda e: e.tensor_copy(out=tab.t[:, 1, :, 0], in_=bi), reads=tb, writes=[tab.b])
            n = 1
            while n < T5:
                ar, ai = tab.t[:, 0, :, 0:n], tab.t[:, 1, :, 0:n]
                sr = tab.t[:, 0, :, n - 1:n].to_broadcast([128, 16, n])
                si = tab.t[:, 1, :, n - 1:n].to_broadcast([128, 16, n])
                tA, tB = tmpA.t[:, :, 0:n], tmpB.t[:, :, 0:n]
                orr, oi = tab.t[:, 0, :, n:2 * n], tab.t[:, 1, :, n:2 * n]
                ops = [(tA, ar, sr, ALU.mult), (tB, ai, si, ALU.mult), (orr, tA, tB, ALU.subtract),
                       (tA, ar, si, ALU.mult), (tB, ai, sr, ALU.mult), (oi, tA, tB, ALU.add)]
                for (o, a, b, op) in ops:
                    S.op("dve", lambda e, o=o, a=a, b=b, op=op: TT(e, o, a, b, op), reads=tb, writes=tb[0:1] + tb[2:4])
                n *= 2
        build_pow(Ptab, W(8), W(9))
        build_pow(Qtab, W(16), W(17))
        tq = [Qtab.b, sw.b, tmpA.b, tmpB.b]
        for half in range(2):
            hs = slice(half * (T5 // 2), (half + 1) * (T5 // 2))
            qr, qi = Qtab.t[:, 0, :, hs], Qtab.t[:, 1, :, hs]
            fr = W(14).unsqueeze(2).to_broadcast([128, 16, T5 // 2])
            fi = W(15).unsqueeze(2).to_broadcast([128, 16, T5 // 2])
            ops = [(tmpA.t[:], qr, fr, ALU.mult), (tmpB.t[:], qi, fi, ALU.mult), ("R", tmpA.t[:], tmpB.t[:], ALU.subtract),
                   (tmpA.t[:], qr, fi, ALU.mult), (tmpB.t[:], qi, fr, ALU.mult), (qi, tmpA.t[:], tmpB.t[:], ALU.add)]
            for (o, a, b, op) in ops:
                if isinstance(o, str):
                    o = qtmp.t[:, :, hs]
                S.op("dve", lambda e, o=o, a=a, b=b, op=op: TT(e, o, a, b, op), reads=tq + [qtmp.b], writes=tq + [qtmp.b])
            S.op("dve", lambda e, qr=qr, hs=hs: e.tensor_copy(out=qr, in_=qtmp.t[:, :, hs]), reads=[qtmp.b], writes=[Qtab.b])
        S.op("dve", lambda e: e.memset(mask32.t[:], 1.0), reads=[Qtab.b], writes=[mask32.b])
        S.op("dve", lambda e: e.memset(mask32.t[:, :, 0:1], 0.0), writes=[mask32.b])
        S.op("dve", lambda e: e.memset(mask4.t[:], 1.0), writes=[mask4.b])
        S.op("dve", lambda e: e.memset(mask4.t[:, :, 0:1], 0.0), writes=[mask4.b])
        S.op("dve", lambda e: e.memset(s5cr.t[:], 0.0), writes=[s5cr.b])
        dump("Ptab", Ptab.t[:].rearrange("p a s t -> p (a s t)"), [128, 2 * 16 * T5], [Ptab.b])
        dump("Qtab", Qtab.t[:].rearrange("p a s t -> p (a s t)"), [128, 2 * 16 * T5], [Qtab.b])

        ckpt("setup0")
        NTM = 256
        xtm = A.alloc("xtm", [128, 2, D], F32)
        xn = A.alloc("xn", [128, 2, D], BF16)
        nstat = A.alloc("nstat", [128, 4], F32)
        uT = A.alloc("uT", [128, 8, NTM], BF16)
        xpad = A.alloc("xpad", [128, 8, NTM + 4], BF16)
        xtail = A.alloc("xtail", [128, 8, 64], F32)
        dgc = A.alloc("dgc", [128, 8, 4, 128], BF16)
        for ct_ in range(8):
            for k_ in range(4):
                S.op("act", lambda e, ct_=ct_, k_=k_: e.activation(
                    out=dgc.t[:, ct_, k_, :], in_=ident, func=AF.Copy,
                    scale=prm.t[:, P_CONV + 5 * ct_ + k_:P_CONV + 5 * ct_ + k_ + 1]), reads=[cst.b, prm.b], writes=[dgc.b])
        xsT = A.alloc("xsT", [128, 4, NTM], F32)
        BCT = A.alloc("BCT", [128, 4, NTM], BF16)
        szT = A.alloc("szT", [128, 4, NTM], BF16)
        u5Ts = [A.alloc("u5T%d" % i, [128, 4, NTM], BF16) for i in range(2)]
        dtT = A.alloc("dtT", [8, 2, NTM], F32)
        cacc = [A.alloc("cacc0", [128, NTM], F32)] * 2
        y5pre = A.alloc("y5pre", [128, 4, NTM], F32)
        g5 = A.alloc("g5", [128, 4, NTM], BF16)
        sgl = A.alloc("sgl", [128, NTM], F32)
        dtm_l = [A.alloc("dtm%d" % i, [128, 16], F32) for i in range(2)]
        acs_l = [A.alloc("acs%d" % i, [128, 8], F32) for i in range(2)]
        dec_l = [A.alloc("dec%d" % i, [128, 8], F32) for i in range(2)]
        dtdec_l = [A.alloc("dtdec%d" % i, [128, 8], F32) for i in range(2)]
        Xtm = A.alloc("Xtm", [128, 8, 64], BF16)
        Xdec = A.alloc("Xdec", [128, 8, 64], BF16)
        Btm = A.alloc("Btm", [128, 2, 128], BF16)
        big1 = A.alloc("big1", [128, 8, 128], F32)
        big2 = A.alloc("big2", [128, 8, 128], F32)
        MT = A.alloc("MT", [128, 8, 128], BF16)
        eA = A.alloc("eA", [128, 8, 128], F32)
        CdT = A.alloc("CdT", [128, 8, 128], BF16)
        ST = A.alloc("ST", [128, 8, 64], F32)
        STb = A.alloc("STb", [128, 8, 64], BF16)
        sts5 = alias("sts5", ST.t[:].rearrange("p h q -> p (h q)").rearrange("p (a s q) -> p a s q", a=2, s=16), ST.b)
        yg = A.alloc("yg", [128, 4, 128], F32)
        ysq = alias("ysq", big1.t[:, 4:8, :], big1.b)
        rsb = A.alloc("rsb", [128, 2, 128], F32)
        h0n = [alias("h0n0", xtm.t[:, 1, 0:512].rearrange("p (a n) -> p a n", n=128), xtm.sub(1)),
               alias("h0n1", xtm.t[:, 0, 0:512].rearrange("p (a n) -> p a n", n=128), xtm.sub(0))]
        h0T = [A.alloc("h0T%d" % i, [128, 8, 64], BF16) for i in range(2)]
        Bj = [A.alloc("Bj%d" % i, [128, 2, 128], BF16) for i in range(2)]
        hn = [alias("hn0", xtm.t[:, 1, 512:1024].rearrange("p (a n) -> p a n", n=128), xtm.sub(1)),
              alias("hn1", xtm.t[:, 0, 512:1024].rearrange("p (a n) -> p a n", n=128), xtm.sub(0))]
        decfm = A.alloc("decfm", [128, 4, 16], F32)
        dAx = alias("dAx", big1.t[:, 0:4, :].rearrange("p a (b c) -> p (a b) c", c=64), big1.b)
        s5g = [[A.alloc("s5g%d%d" % (j, i), [128, 512], F32) for i in range(2)] for j in range(2)]
        s5t34 = [A.alloc("s5t%d" % i, [128, 512], F32) for i in (2, 3)]
        s5vb = [A.alloc("s5vb%d" % i, [128, 512], F32) for i in range(2)]
        s5k = [0]
        s5o = [A.alloc("s5o%d" % i, [128, 512], F32) for i in range(2)]
        s5h = [[A.alloc("s5h%d%d" % (j, i), [128, 512], BF16) for i in range(2)] for j in range(2)]
        s5c = A.alloc("s5c", [128, 4, 16], F32)
        busd = [[A.alloc("bus%d%d" % (j, i), [128, 512], F32) for i in range(2)] for j in range(2)]
        dg5 = A.alloc("dg5", [128, 4, 128], BF16)
        for q_ in range(4):
            S.op("act", lambda e, q_=q_: e.activation(out=dg5.t[:, q_, :], in_=ident, func=AF.Copy,
                                                      scale=prm.t[:, P_S5M + q_:P_S5M + q_ + 1]),
                 reads=[cst.b, prm.b], writes=[dg5.b])
        print("arena after p1a allocs: lo=%d hi=%d (words)" % (A.lo, A.hi))

        S.op("dve", lambda e: e.memset(xpad.t[:, :, 0:3], 0.0), writes=[xpad.b])
        S.op("dve", lambda e: e.memset(ST.t[:], 0.0), writes=[ST.b])
        S.op("dve", lambda e: e.memset(STb.t[:], 0.0), writes=[STb.b])

        import os as _os3
        ENG_OUTROT = _os3.environ.get("K_OUTROT", "dve")
        ENG_ADDS = _os3.environ.get("K_ADDS", "dve")
        TILES_A = [(i * 256, 256, False) for i in range(8)] + [(SEQ, 64, True)]

        def load_x(ti):
            t0, NT, is_s = TILES_A[ti]
            for blk in range((NT + 127) // 128):
                rows = min(128, NT - blk * 128)
                S.dma("sp", xtm.t[0:rows, blk, :], xin[t0 + blk * 128:t0 + blk * 128 + rows, :], writes=[xtm.sub(blk)])

        a1 = lambda kt: amod.t[:, kt, 0:1]
        sh1 = lambda kt: mod.t[:, 8 * MOD_SH1 + kt, 0:1]
        cw = lambda ct, k: prm.t[:, P_CONV + 5 * ct + k:P_CONV + 5 * ct + k + 1]
        IN_CHUNKS = [("dt", 0, 1536, 8)] + [("z", i, i * 128, 128) for i in range(4)] + \
                    [("xbc", i, 512 + i * 128, 128) for i in range(8)] + [("u5", i, 1544 + i * 128, 128) for i in range(4)]

        load_x(0)
        pbi = [0]

        def next_pb():
            pbi[0] ^= 1
            return PB[pbi[0]]

        ckpt("pre")
        def chain1(ti):
            t0, NT, is_s = TILES_A[ti]
            u5T = u5Ts[ti % 2]
            nblk = (NT + 127) // 128
            T = 128 if not is_s else 64
            tri = cst.t[0:T, C_TRI:C_TRI + T] if not is_s else cst.t[0:T, C_TRI64:C_TRI64 + T]
            neg = cst.t[0:T, C_NEG:C_NEG + T] if not is_s else cst.t[0:T, C_NEG64:C_NEG64 + T]
            sego = onesf.t[0:T, 0:T] if not is_s else cst.t[0:T, C_SEG64:C_SEG64 + T]
            segi = cst.t[0:64, C_SEGI:C_SEGI + 16]

            def dt_prep(ck):
                c0 = ck * T
                cs_ = slice(c0, c0 + T)
                dtm, acs, dec, dtdec = dtm_l[ck], acs_l[ck], dec_l[ck], dtdec_l[ck]
                pc = 0 if ck == 0 else 480
                S.op("pe", lambda e: e.transpose(PB[4].t[0:T, pc:pc + 8], dtT.t[:, 0, cs_], cst.t[0:8, C_ID:C_ID + 8]),
                     reads=[dtT.b, cst.b], writes=[PB[4].sub("sm")])
                S.op("pe", lambda e: e.transpose(PB[4].t[0:T, pc + 8:pc + 16], dtT.t[:, 1, cs_], cst.t[0:8, C_ID:C_ID + 8]),
                     reads=[dtT.b, cst.b], writes=[PB[4].sub("sm")])
                S.op("act", lambda e: e.activation(out=dtm.t[0:T, :], in_=PB[4].t[0:T, pc:pc + 16], func=AF.Copy),
                     reads=[PB[4].sub("sm")], writes=[dtm.b])
                S.op("pe", lambda e: e.matmul(PB[4].t[0:T, pc + 16:pc + 24], tri, dtm.t[0:T, 8:16], start=True, stop=True),
                     reads=[dtm.b, cst.b], writes=[PB[4].sub("sm")])
                S.op("pe", lambda e: e.matmul(PB[4].t[0:T, pc + 24:pc + 32], sego, dtm.t[0:T, 8:16], start=True, stop=True),
                     reads=[dtm.b, cst.b, onesf.b], writes=[PB[4].sub("sm")])
                S.op("act", lambda e: e.activation(out=acs.t[0:T, :], in_=PB[4].t[0:T, pc + 16:pc + 24], func=AF.Copy),
                     reads=[PB[4].sub("sm")], writes=[acs.b])
                S.op("dve", lambda e: TT(e, dec.t[0:T, :], PB[4].t[0:T, pc + 24:pc + 32], acs.t[0:T, :], ALU.subtract),
                     reads=[PB[4].sub("sm"), acs.b], writes=[dec.b])
                S.op("act", lambda e: e.activation(out=dec.t[0:T, :], in_=dec.t[0:T, :], func=AF.Exp), reads=[dec.b], writes=[dec.b])
                S.op("dve", lambda e: TT(e, dtdec.t[0:T, :], dtm.t[0:T, 0:8], dec.t[0:T, :], ALU.mult),
                     reads=[dtm.b, dec.b], writes=[dtdec.b])
            for blk in range(nblk):
                rows = min(128, NT - blk * 128)
                xb = xtm.sub(blk)
                S.op("act", lambda e, blk=blk, rows=rows: e.activation(
                    out=xn.t[0:rows, blk, :], in_=xtm.t[0:rows, blk, :], func=AF.Square, accum_out=nstat.t[0:rows, blk:blk + 1]),
                    reads=[xb], writes=[xn.sub(blk), nstat.sub(blk)])
                S.op("act", lambda e, blk=blk, rows=rows: e.activation(
                    out=nstat.t[0:rows, 2 + blk:3 + blk], in_=nstat.t[0:rows, blk:blk + 1], func=AF.Sqrt, scale=1.0 / D, bias=EPS),
                    reads=[nstat.sub(blk)], writes=[nstat.sub(blk)])
                S.op("dve", lambda e, blk=blk, rows=rows: e.reciprocal(out=nstat.t[0:rows, 2 + blk:3 + blk],
                                                                        in_=nstat.t[0:rows, 2 + blk:3 + blk]),
                     reads=[nstat.sub(blk)], writes=[nstat.sub(blk)])
                S.op("act", lambda e, blk=blk, rows=rows: e.activation(
                    out=xn.t[0:rows, blk, :], in_=xtm.t[0:rows, blk, :], func=AF.Copy, scale=nstat.t[0:rows, 2 + blk:3 + blk]),
                    reads=[xb, nstat.sub(blk)], writes=[xn.sub(blk)])
            ckpt("Aa%d" % ti)
            if ti + 1 < len(TILES_A):
                load_x(ti + 1)
            ckpt("Ab%d" % ti)
            for kt in range(8):
                xb_ = 2 + (kt % 2)
                pslot = PB[xb_].b
                for blk in range(nblk):
                    rows = min(128, NT - blk * 128)
                    S.op("pe", lambda e, kt=kt, blk=blk, rows=rows: e.transpose(
                        pbf(xb_)[:, blk * 128:blk * 128 + rows],
                        xn.t[0:rows, blk, kt * 128:(kt + 1) * 128], identb.t[0:rows, 0:rows]),
                        reads=[xn.sub(blk), identb.b], writes=[pslot])
                src = pbf(xb_)[:, 0:NT]
                if not is_s:
                    S.op("act", lambda e, kt=kt, src=src: e.activation(out=uT.t[:, kt, 0:NT], in_=src, func=AF.Identity,
                                                                       scale=a1(kt), bias=sh1(kt)),
                         reads=[pslot, amod.b, mod.b], writes=[uT.sub(kt)])
                else:
                    S.op("dve", lambda e, kt=kt, src=src: TT(e, cacc[0].t[:, 0:NT], src, a1x.t[:, kt, :], ALU.mult),
                         reads=[pslot, a1x.b], writes=[cacc[0].b])
                    S.op("dve", lambda e, kt=kt: TT(e, uT.t[:, kt, 0:NT], cacc[0].t[:, 0:NT], sh1x.t[:, kt, :], ALU.add),
                         reads=[cacc[0].b, sh1x.b], writes=[uT.sub(kt)])
            ckpt("A%d" % ti)
            if ti == 0:
                dump("uT", uT.t[:].rearrange("p k t -> p (k t)"), [128, 8 * NTM], uT.allb())

            yield
            if is_s:
                xps = xpad.t[:, :, 0:NS * 7].rearrange("p c (s k) -> p c s k", k=7)
                scv = stconv_d.rearrange("p (c s k) -> p c s k", s=NS, k=3)
                for ct in range(8):
                    S.dma("pool", xps[:, ct, :, 0:3], scv[:, ct], writes=[xpad.b])
            for (kind, i, c0, M) in IN_CHUNKS:
                yield
                pb = next_pb()
                for kt in range(8):
                    S.op("pe", lambda e, kt=kt, c0=c0, M=M, pb=pb: e.matmul(
                        pb.t[0:M, 0:NT], win_sb.t[:, kt, c0:c0 + M], uT.t[:, kt, 0:NT], start=(kt == 0), stop=(kt == 7)),
                        reads=[win_sb.b, uT.sub(kt)], writes=[pb.b])
                if kind == "z":
                    S.op("act", lambda e, i=i, pb=pb: e.activation(out=szT.t[:, i, 0:NT], in_=pb.t[:, 0:NT], func=AF.Silu),
                         reads=[pb.b], writes=[szT.b])
                elif kind == "xbc":
                    if not is_s:
                        S.op("act", lambda e, i=i, pb=pb: e.activation(out=xpad.t[:, i, 3:3 + NT], in_=pb.t[:, 0:NT], func=AF.Copy),
                             reads=[pb.b], writes=[xpad.b])
                        if ti == 7:
                            S.op("act", lambda e, i=i, pb=pb: e.activation(out=xtail.t[:, i, 0:3], in_=pb.t[:, NT - 3:NT], func=AF.Copy),
                                 reads=[pb.b], writes=[xtail.b])
                    else:
                        S.op("act", lambda e, i=i, pb=pb: e.activation(
                            out=xps[:, i, :, 3:7], in_=pb.t[:, 0:NT].rearrange("p (s k) -> p s k", k=LS), func=AF.Copy),
                            reads=[pb.b], writes=[xpad.b])
                        S.op("act", lambda e, i=i, pb=pb: e.activation(out=xtail.t[:, i, 0:NT], in_=pb.t[:, 0:NT], func=AF.Copy),
                             reads=[pb.b], writes=[xtail.b])
                elif kind == "dt":
                    S.op("act", lambda e, pb=pb: e.activation(out=dtT.t[:, 1, 0:NT], in_=pb.t[0:8, 0:NT], func=AF.Exp,
                                                              bias=ssd8.t[:, 0:1]), reads=[pb.b, ssd8.b], writes=[dtT.b])
                    S.op("act", lambda e: e.activation(out=dtT.t[:, 0, 0:NT], in_=dtT.t[:, 1, 0:NT], func=AF.Ln, bias=1.0),
                         reads=[dtT.b], writes=[dtT.b])
                    S.op("dve", lambda e: e.tensor_scalar(out=dtT.t[:, 1, 0:NT], in0=dtT.t[:, 0, 0:NT], scalar1=ssd8.t[:, 1:2],
                                                          scalar2=None, op0=ALU.mult), reads=[dtT.b, ssd8.b], writes=[dtT.b])
                    for ck_ in range(NT // T):
                        yield
                        dt_prep(ck_)
                else:
                    S.op("act", lambda e, i=i, pb=pb: e.activation(out=u5T.t[:, i, 0:NT], in_=pb.t[:, 0:NT], func=AF.Copy),
                         reads=[pb.b], writes=[u5T.b])

            ckpt("B%d" % ti)
            for ct in range(8):
                yield
                pb = next_pb()
                if not is_s:
                    xin_k = lambda k, ct=ct: xpad.t[:, ct, k:k + NT]
                    pbv = pb.t[:, 0:NT]
                    dst = xsT.t[:, ct, 0:NT] if ct < 4 else BCT.t[:, ct - 4, 0:NT]
                else:
                    xin_k = lambda k, ct=ct: xps[:, ct, :, k:k + LS]
                    pbv = pb.t[:, 0:NT].rearrange("p (s k) -> p s k", k=LS)
                    dst = (xsT.t[:, ct, 0:NT] if ct < 4 else BCT.t[:, ct - 4, 0:NT]).rearrange("p (s k) -> p s k", k=LS)
                for k in range(4):
                    S.op("pe", lambda e, k=k: e.matmul(pbv, dgc.t[:, ct, k, :], xin_k(k), start=(k == 0), stop=(k == 3)),
                         reads=[dgc.b, xpad.b], writes=[pb.b])
                S.op("act", lambda e: e.activation(out=dst, in_=pbv, func=AF.Silu, bias=cw(ct, 4)),
                     reads=[pb.b, prm.b], writes=[xsT.b if ct < 4 else BCT.b])
            ocv = o_conv.rearrange("p (c s k) -> p c s k", s=17, k=3)
            if is_s:
                for ct in range(8):
                    S.dma("sp", ocv[:, ct, 1:17, :], xtail.t[:, ct, :].rearrange("p (s k) -> p s k", k=LS)[:, :, 1:4], reads=[xtail.b], buf=xtail.b)
                outbufs.append(xtail.b)
            elif ti == 7:
                S.dma("sp", ocv[:, :, 0, :], xtail.t[:, :, 0:3], reads=[xtail.b], buf=xtail.b)
            if not is_s:
                S.op("dve", lambda e: e.tensor_copy(out=xpad.t[:, :, 0:3], in_=xpad.t[:, :, NT:NT + 3]),
                     reads=[xpad.b], writes=[xpad.b])
            if is_s:
                dump("xsS", xsT.t[:, :, 0:64], [128, 4, 64], [xsT.b])
                dump("ygS", yg.t[:, :, 0:64], [128, 4, 64], [yg.b])
            if ti == 0:
                dump("xsT", xsT.t[:].rearrange("p k t -> p (k t)"), [128, 4 * NTM], [xsT.b])
                dump("dtT", dtT.t[:].rearrange("p k t -> p (k t)"), [8, 2 * NTM], [dtT.b])

            ckpt("C%d" % ti)
            for ck in range(NT // T):
                c0 = ck * T
                cs_ = slice(c0, c0 + T)
                dtm, acs, dec, dtdec = dtm_l[ck], acs_l[ck], dec_l[ck], dtdec_l[ck]
                yield
                for pr in range(4):
                    S.op("pe", lambda e, pr=pr, cs_=cs_: e.transpose(PB[3].t[0:T, pr * 128:(pr + 1) * 128], xsT.t[:, pr, cs_], ident),
                         reads=[xsT.b, cst.b], writes=[PB[3].b])
                pxs = PB[3].t[0:T, :].rearrange("p (h q) -> p h q", q=64)
                S.op("dve", lambda e: TT(e, Xtm.t[0:T], pxs, dtm.t[0:T, 0:8].unsqueeze(2).to_broadcast([T, 8, 64]), ALU.mult),
                     reads=[PB[3].b, dtm.b], writes=[Xtm.b])
                S.op("dve", lambda e: TT(e, Xdec.t[0:T], pxs, dtdec.t[0:T, :].unsqueeze(2).to_broadcast([T, 8, 64]), ALU.mult),
                     reads=[PB[3].b, dtdec.b], writes=[Xdec.b])
                for g in range(2):
                    S.op("pe", lambda e, g=g, cs_=cs_: e.transpose(pbf(2)[0:T, g * 128:(g + 1) * 128], BCT.t[:, g, cs_], identb.t[:]),
                         reads=[BCT.b, identb.b], writes=[PB[2].sub(0)])
                S.op("act", lambda e: e.activation(out=Btm.t[0:T].rearrange("p g n -> p (g n)"), in_=pbf(2)[0:T, 0:256], func=AF.Copy),
                     reads=[PB[2].sub(0)], writes=[Btm.b])
                yield
                S.op("dve", lambda e: TT(e, big1.t[0:T, :, 0:T], tri.unsqueeze(1).to_broadcast([T, 8, T]),
                                         dtm.t[0:T, 8:16].unsqueeze(2).to_broadcast([T, 8, T]), ALU.mult),
                     reads=[cst.b, dtm.b], writes=[big1.b])
                for half in range(2):
                    S.op("pe", lambda e, half=half: e.matmul(
                        PB[3].t[:, 0:4 * T].rearrange("p (h l) -> p h l", l=T), onesf.t[0:T, :],
                        big1.t[0:T, 4 * half:4 * half + 4, 0:T], start=True, stop=True),
                        reads=[big1.b, onesf.b], writes=[PB[3].b])
                    for h in range(4 * half, 4 * half + 4):
                        S.op("dve", lambda e, h=h: e.scalar_tensor_tensor(
                            out=big2.t[0:T, h, 0:T], in0=PB[3].t[0:T, (h % 4) * T:(h % 4 + 1) * T], scalar=acs.t[0:T, h:h + 1],
                            in1=neg, op0=ALU.subtract, op1=ALU.min), reads=[PB[3].b, acs.b, cst.b], writes=[big2.b])
                    S.op("act", lambda e, half=half: e.activation(
                        out=eA.t[:, 4 * half:4 * half + 4, 0:T], in_=PB[3].t[:, 0:4 * T].rearrange("p (h l) -> p h l", l=T),
                        func=AF.Exp), reads=[PB[3].b], writes=[eA.b])
                    yield
                S.op("act", lambda e: e.activation(out=big2.t[0:T, :, 0:T], in_=big2.t[0:T, :, 0:T], func=AF.Exp),
                     reads=[big2.b], writes=[big2.b])
                yield
                for g in range(2):
                    S.op("pe", lambda e, g=g, cs_=cs_: e.matmul(PB[4].t[0:T, 32 + g * 128:32 + g * 128 + T], BCT.t[:, g, cs_],
                                                                 BCT.t[:, 2 + g, cs_], start=True, stop=True),
                         reads=[BCT.b], writes=[PB[4].sub("cb")])
                cbv = PB[4].t[0:T, 32:288].rearrange("p (g l) -> p g l", l=128)[:, :, 0:T]
                S.op("dve", lambda e: TT(e, MT.t[0:T, :, 0:T].rearrange("p (g h) l -> p g h l", h=4),
                                         cbv.unsqueeze(2).to_broadcast([T, 2, 4, T]),
                                         big2.t[0:T, :, 0:T].rearrange("p (g h) l -> p g h l", h=4), ALU.mult),
                     reads=[PB[4].sub("cb"), big2.b], writes=[MT.b])
                yield
                S.op("pool", lambda e, cs_=cs_: TT(e, CdT.t[:, :, 0:T].rearrange("p (g h) l -> p g h l", h=4),
                                                   BCT.t[:, 2:4, cs_].unsqueeze(2).to_broadcast([128, 2, 4, T]),
                                                   eA.t[:, :, 0:T].rearrange("p (g h) l -> p g h l", h=4), ALU.mult),
                     reads=[BCT.b, eA.b], writes=[CdT.b])
                yield
                ypb = PB[7]
                if is_s:
                    S.op("dve", lambda e: e.tensor_copy(out=dAx.t[0:T], in_=dtm.t[0:T, 8:16].unsqueeze(2).to_broadcast([T, 8, 64])),
                         reads=[dtm.b], writes=[dAx.b])
                    for pr in range(4):
                        S.op("pe", lambda e, pr=pr: e.matmul(PB[4].t[:, 288 + pr * 16:288 + (pr + 1) * 16],
                                                             dAx.t[0:T, 2 * pr:2 * pr + 2, :], segi, start=True, stop=True),
                             reads=[dAx.b, cst.b], writes=[PB[4].sub("dec")])
                    S.op("act", lambda e: e.activation(out=decfm.t[:].rearrange("p a s -> p (a s)"), in_=PB[4].t[:, 288:352], func=AF.Exp),
                         reads=[PB[4].sub("dec")], writes=[decfm.b])
                    stv = stssd_d.rearrange("j (pr hl) p n -> j (hl p) pr n", hl=2)
                    osv = o_ssds.rearrange("j (pr hl) p n -> j (hl p) pr n", hl=2)
                    S.dma("sp", h0n[0].t[:], stv[0], writes=[h0n[0].b])
                    for j in range(NS):
                        yield
                        jj = j % 2
                        if j + 1 < NS:
                            S.dma("sp", h0n[1 - jj].t[:], stv[j + 1], writes=[h0n[1 - jj].b])
                        pbt = PB[jj]
                        for pr in range(4):
                            S.op("pe", lambda e, pr=pr, jj=jj, pbt=pbt: e.transpose(pbt.t[:, pr * 128:(pr + 1) * 128], h0n[jj].t[:, pr, :], ident),
                                 reads=[h0n[jj].b, cst.b], writes=[pbt.b])
                        S.op("act", lambda e, jj=jj, pbt=pbt: e.activation(out=h0T[jj].t[:].rearrange("p h q -> p (h q)"), in_=pbt.t[:, :], func=AF.Copy),
                             reads=[pbt.b], writes=[h0T[jj].b])
                        for h in range(8):
                            pr, hl = h // 2, h % 2
                            S.op("pe", lambda e, h=h, pr=pr, hl=hl, jj=jj, j=j: e.matmul(
                                ypb.t[64 * hl:64 * hl + 64, pr * T + LS * j:pr * T + LS * j + LS], h0T[jj].t[:, h, :],
                                CdT.t[:, h, LS * j:LS * j + LS], start=(j == 0 and pr == 0), stop=False, skip_group_check=True),
                                reads=[h0T[jj].b, CdT.b], writes=[ypb.b])
                        S.op("dve", lambda e, jj=jj, j=j: e.tensor_scalar(out=Bj[jj].t[0:T], in0=Btm.t[0:T], scalar1=segi[:, j:j + 1],
                                                                          scalar2=None, op0=ALU.mult),
                             reads=[Btm.b, cst.b], writes=[Bj[jj].b])
                        pby = PB[3]
                        for pr in range(4):
                            S.op("pe", lambda e, pr=pr, jj=jj, pby=pby: e.matmul(
                                pby.t[:, pr * 128:(pr + 1) * 128], Xdec.t[0:T, 2 * pr:2 * pr + 2, :], Bj[jj].t[0:T, pr // 2, :],
                                start=True, stop=True), reads=[Xdec.b, Bj[jj].b], writes=[pby.b])
                        S.op("dve", lambda e, jj=jj, j=j: TT(e, hn[jj].t[:], h0n[jj].t[:],
                                                             decfm.t[:, :, j:j + 1].to_broadcast([128, 4, 128]), ALU.mult),
                             reads=[h0n[jj].b, decfm.b], writes=[hn[jj].b])
                        S.op("dve", lambda e, jj=jj, pby=pby: TT(e, hn[jj].t[:], hn[jj].t[:],
                                                                 pby.t[:, :].rearrange("p (a n) -> p a n", n=128), ALU.add),
                             reads=[hn[jj].b, pby.b], writes=[hn[jj].b])
                        S.dma("sp", osv[j], hn[jj].t[:], reads=[hn[jj].b], buf=hn[jj].b)
                    outbufs.extend([hn[0].b, hn[1].b])
                for h in range(8):
                    pr, hl = h // 2, h % 2
                    out = ypb.t[64 * hl:64 * hl + 64, pr * T:(pr + 1) * T]
                    S.op("pe", lambda e, h=h, out=out, pr=pr: e.matmul(out, Xtm.t[0:T, h, :], MT.t[0:T, h, 0:T],
                                                                       start=(pr == 0 and not is_s), stop=is_s, skip_group_check=True),
                         reads=[Xtm.b, MT.b], writes=[ypb.b])
                    if not is_s:
                        S.op("pe", lambda e, h=h, out=out: e.matmul(out, STb.t[:, h, :], CdT.t[:, h, 0:T], start=False, stop=True,
                                                                    skip_group_check=True),
                             reads=[STb.b, CdT.b], writes=[ypb.b])
                yield
                for pr in range(4):
                    S.op("dve", lambda e, pr=pr, cs_=cs_: e.scalar_tensor_tensor(
                        out=yg.t[:, pr, 0:T], in0=xsT.t[:, pr, cs_], scalar=prm.t[:, P_SSDFM + pr:P_SSDFM + pr + 1],
                        in1=ypb.t[:, pr * T:(pr + 1) * T], op0=ALU.mult, op1=ALU.add),
                        reads=[xsT.b, prm.b, ypb.b], writes=[yg.b])
                S.op("pool", lambda e, cs_=cs_: TT(e, yg.t[:, :, 0:T], yg.t[:, :, 0:T], szT.t[:, :, cs_], ALU.mult),
                     reads=[yg.b, szT.b], writes=[yg.b])
                S.op("act", lambda e: e.activation(out=ysq.t[:, :, 0:T], in_=yg.t[:, :, 0:T], func=AF.Square),
                     reads=[yg.b], writes=[ysq.b])
                for g in range(2):
                    for k in range(2):
                        S.op("pe", lambda e, g=g, k=k: e.matmul(PB[3].t[:, g * T:(g + 1) * T], onesf.t[:], ysq.t[:, 2 * g + k, 0:T],
                                                                start=(k == 0), stop=(k == 1)),
                             reads=[onesf.b, ysq.b], writes=[PB[3].b])
                S.op("act", lambda e: e.activation(out=rsb.t[:, :, 0:T], in_=PB[3].t[:, 0:2 * T].rearrange("p (g l) -> p g l", l=T),
                                                   func=AF.Sqrt, scale=1.0 / 256, bias=EPS), reads=[PB[3].b], writes=[rsb.b])
                S.op("dve", lambda e: e.reciprocal(out=rsb.t[:, :, 0:T], in_=rsb.t[:, :, 0:T]), reads=[rsb.b], writes=[rsb.b])
                for pr in range(4):
                    S.op("dve", lambda e, pr=pr: e.scalar_tensor_tensor(
                        out=mixt[ti % 2].t[:, pr, c0:c0 + T], in0=yg.t[:, pr, 0:T],
                        scalar=prm.t[:, P_SSDFM + 4 + pr:P_SSDFM + 5 + pr], in1=rsb.t[:, pr // 2, 0:T], op0=ALU.mult, op1=ALU.mult),
                        reads=[yg.b, prm.b, rsb.b], writes=[mixt[ti % 2].sub("ssd")])
                yield
                if not is_s:
                    for g in range(2):
                        S.op("pe", lambda e, g=g: e.matmul(PB[6].t[:, g * 256:(g + 1) * 256], Btm.t[0:T, g, :],
                                                           Xdec.t[0:T, 4 * g:4 * g + 4, :], start=True, stop=True),
                             reads=[Btm.b, Xdec.b], writes=[PB[6].b])
                    S.op("dve", lambda e: TT(e, ST.t[:], ST.t[:], eA.t[:, :, T - 1:T].to_broadcast([128, 8, 64]), ALU.mult),
                         reads=[ST.b, eA.b], writes=[ST.b])
                    S.op("dve", lambda e: TT(e, ST.t[:], ST.t[:], PB[6].t[:, :].rearrange("p (h q) -> p h q", q=64), ALU.add),
                         reads=[ST.b, PB[6].b], writes=[ST.b])
                    S.op("act", lambda e: e.activation(out=STb.t[:], in_=ST.t[:], func=AF.Copy), reads=[ST.b], writes=[STb.b])
            if ti == 7:
                S.dma("sp", o_ssdp, ST.t[:].rearrange("p h q -> p (h q)"), reads=[ST.b], buf=ST.b)
                outbufs.append(ST.b)

            ckpt("D%d" % ti)
            yield

        def chain2(ti):
            t0, NT, is_s = TILES_A[ti]
            u5T = u5Ts[ti % 2]
            if is_s:
                S.dma("sp", sts5.t[:].rearrange("p a s q -> p (a s q)"), sts5_d, writes=[sts5.b])
            if not is_s:
                groups = [(list(range(16)), k * T5, T5) for k in range(NT // T5)]
            else:
                groups = [(list(range(8)), 0, 64), (list(range(8, 16)), 0, 64)]
            def emit_bu(g_):
                slist_, tk0_, ntok_ = groups[g_]
                bus = busd[g_ % 2]
                for part, pb in ((0, PB[5]), (1, PB[6])):
                    for idx, s in enumerate(slist_):
                        S.op("pe", lambda e, part=part, pb=pb, idx=idx, s=s: e.matmul(
                            pb.t[:, idx * ntok_:(idx + 1) * ntok_], s5BT.t[:, part, s, :], u5T.t[:, s // 4, tk0_:tk0_ + ntok_],
                            start=True, stop=True), reads=[s5BT.b, u5T.b], writes=[pb.b])
                S.op("act", lambda e: e.activation(out=bus[0].t[:], in_=PB[5].t[:, :], func=AF.Copy), reads=[PB[5].b], writes=[bus[0].b])
                S.op("act", lambda e: e.activation(out=bus[1].t[:], in_=PB[6].t[:, :], func=AF.Copy), reads=[PB[6].b], writes=[bus[1].b])
            def views(g_):
                slist_, tk0_, ntok_ = groups[g_]
                s0_ = slist_[0]
                if not is_s:
                    V3 = lambda ap: ap.rearrange("p (s t) -> p s t", t=T5)
                    QR, QI = Qtab.t[:, 0], Qtab.t[:, 1]
                    PR_, PI_ = Ptab.t[:, 0], Ptab.t[:, 1]
                    msk = mask32.t[:].rearrange("p s t -> p (s t)")
                    first = lambda ap: V3(ap)[:, :, 0]
                    cin_r, cin_i = s5cr.t[:, 0, :], s5cr.t[:, 1, :]
                else:
                    V3 = lambda ap: ap.rearrange("p (s q b) -> p s q b", q=NS, b=LS)
                    bc = lambda ap: ap.unsqueeze(2).to_broadcast([128, 8, NS, LS])
                    QR, QI = bc(Qtab.t[:, 0, s0_:s0_ + 8, 0:LS]), bc(Qtab.t[:, 1, s0_:s0_ + 8, 0:LS])
                    PR_, PI_ = bc(Ptab.t[:, 0, s0_:s0_ + 8, 0:LS]), bc(Ptab.t[:, 1, s0_:s0_ + 8, 0:LS])
                    msk = mask4.t[:].rearrange("p s t -> p (s t)")
                    first = lambda ap: V3(ap)[:, :, :, 0]
                    cin_r, cin_i = sts5.t[:, 0, s0_:s0_ + 8, :], sts5.t[:, 1, s0_:s0_ + 8, :]
                return V3, QR, QI, PR_, PI_, msk, first, cin_r, cin_i
            vsets = [[s5v[0], s5v[1]], [s5vb[0], s5vb[1]]]

            def mults_adds(g_):
                V3, QR, QI, PR_, PI_, msk, first, cin_r, cin_i = views(g_)
                bus = busd[g_ % 2]
                br, bi = V3(bus[0].t[:]), V3(bus[1].t[:])
                t1, t2, t3, t4 = s5t[0], s5t[1], s5t34[0], s5t34[1]
                vr, vi = vsets[g_ % 2]
                tb = [Qtab.b]
                for (o, a, b_, rd) in ((t1, QR, br, bus[0].b), (t2, QI, bi, bus[1].b), (t3, QR, bi, bus[1].b), (t4, QI, br, bus[0].b)):
                    S.op("dve", lambda e, o=o, a=a, b_=b_: TT(e, V3(o.t[:]), a, b_, ALU.mult), reads=tb + [rd], writes=[o.b])
                S.op(ENG_ADDS, lambda e: TT(e, vr.t[:], t1.t[:], t2.t[:], ALU.subtract), reads=[t1.b, t2.b], writes=[vr.b])
                S.op(ENG_ADDS, lambda e: TT(e, vi.t[:], t3.t[:], t4.t[:], ALU.add), reads=[t3.b, t4.b], writes=[vi.b])
            emit_bu(0)
            if len(groups) > 1:
                emit_bu(1)
            mults_adds(0)
            pend_y5 = [None]
            for gi_, (slist, tk0, ntok) in enumerate(groups):
                yield
                ns = len(slist)
                s0 = slist[0]
                V3, QR, QI, PR_, PI_, msk, first, cin_r, cin_i = views(gi_)
                vr, vi = vsets[gi_ % 2]
                if gi_ + 1 < len(groups):
                    mults_adds(gi_ + 1)
                    yield
                if gi_ + 2 < len(groups):
                    emit_bu(gi_ + 2)
                S.op("dve", lambda e: TT(e, first(vr.t[:]), first(vr.t[:]), cin_r, ALU.add), reads=[vr.b, s5cr.b, sts5.b], writes=[vr.b])
                S.op("dve", lambda e: TT(e, first(vi.t[:]), first(vi.t[:]), cin_i, ALU.add), reads=[vi.b, s5cr.b, sts5.b], writes=[vi.b])
                yield
                s5k[0] ^= 1
                gr, gi2 = s5g[s5k[0]][0], s5g[s5k[0]][1]
                S.op("dve", lambda e: e.tensor_tensor_scan(out=gr.t[:], data0=msk, data1=vr.t[:], initial=0.0, op0=ALU.mult, op1=ALU.add),
                     reads=[vr.b, mask32.b, mask4.b], writes=[gr.b])
                S.op("dve", lambda e: e.tensor_tensor_scan(out=gi2.t[:], data0=msk, data1=vi.t[:], initial=0.0, op0=ALU.mult, op1=ALU.add),
                     reads=[vi.b, mask32.b, mask4.b], writes=[gi2.b])
                yield
                o1, o2 = s5o[0], s5o[1]
                hr, hi = s5h[gi_ % 2][0], s5h[gi_ % 2][1]
                seq = [(o1, PR_, gr, ALU.mult), (o2, PI_, gi2, ALU.mult), (hr, o1, o2, ALU.subtract),
                       (o1, PR_, gi2, ALU.mult), (o2, PI_, gr, ALU.mult), (hi, o1, o2, ALU.add)]
                for (o, a, b, op) in seq:
                    a3 = a if not isinstance(a, TL) else V3(a.t[:])
                    rd = [Ptab.b, b.b] + ([a.b] if isinstance(a, TL) else [])
                    S.op(ENG_OUTROT, lambda e, o=o, a3=a3, b=b, op=op: TT(e, V3(o.t[:]), a3, V3(b.t[:]), op), reads=rd, writes=[o.b])
                yield
                if not is_s:
                    glr, gli = V3(gr.t[:])[:, :, T5 - 1], V3(gi2.t[:])[:, :, T5 - 1]
                    plr, pli = Ptab.t[:, 0, :, T5 - 1], Ptab.t[:, 1, :, T5 - 1]
                    c_ = lambda i: s5c.t[:, i, :]
                    outr, outi = s5cr.t[:, 0, :], s5cr.t[:, 1, :]
                else:
                    glr, gli = V3(gr.t[:])[:, :, :, LS - 1], V3(gi2.t[:])[:, :, :, LS - 1]
                    plr = Ptab.t[:, 0, s0:s0 + 8, LS - 1:LS].to_broadcast([128, 8, NS])
                    pli = Ptab.t[:, 1, s0:s0 + 8, LS - 1:LS].to_broadcast([128, 8, NS])
                    c_ = lambda i: hn[0].t[:, i, :].rearrange("p (s q) -> p s q", q=NS)
                    outr, outi = s5fin.t[:, 0, s0:s0 + 8, 1:17], s5fin.t[:, 1, s0:s0 + 8, 1:17]
                cb_ = [s5c.b, hn[0].b]
                cseq = [(c_(0), plr, glr, ALU.mult), (c_(1), pli, gli, ALU.mult), (c_(2), plr, gli, ALU.mult), (c_(3), pli, glr, ALU.mult)]
                for (o, a, b, op) in cseq:
                    S.op("dve", lambda e, o=o, a=a, b=b, op=op: TT(e, o, a, b, op), reads=[Ptab.b, gr.b, gi2.b] + cb_, writes=cb_)
                S.op("dve", lambda e: TT(e, outr, c_(0), c_(1), ALU.subtract), reads=cb_, writes=[s5cr.b, s5fin.b])
                S.op("dve", lambda e: TT(e, outi, c_(2), c_(3), ALU.add), reads=cb_, writes=[s5cr.b, s5fin.b])
                yield
                def emit_y5(gi_=gi_, slist=slist, tk0=tk0, ntok=ntok, hr=hr, hi=hi):
                    y5c0 = 352
                    nq = 4 if not is_s else 2
                    for qi in range(nq):
                        q = qi if not is_s else 2 * gi_ + qi
                        S.op("pe", lambda e, q=q, qi=qi: e.matmul(PB[4].t[:, y5c0 + qi * ntok:y5c0 + (qi + 1) * ntok], dg5.t[:, q, :],
                                                                  u5T.t[:, q, tk0:tk0 + ntok], start=(qi == 0), stop=False, skip_group_check=True),
                             reads=[dg5.b, u5T.b], writes=[PB[4].sub("y5")])
                    for idx, s in enumerate(slist):
                        qi = (s // 4) if not is_s else (s // 4 - 2 * gi_)
                        out = PB[4].t[32 * (s % 4):32 * (s % 4) + 32, y5c0 + qi * ntok:y5c0 + (qi + 1) * ntok]
                        S.op("pe", lambda e, out=out, s=s, idx=idx: e.matmul(out, s5CT.t[:, 0, s, :], hr.t[:, idx * ntok:(idx + 1) * ntok],
                                                                             start=False, stop=False, skip_group_check=True,
                                                                             tile_position=(0, 32 * (s % 4))),
                             reads=[s5CT.b, hr.b], writes=[PB[4].sub("y5")])
                        S.op("pe", lambda e, out=out, s=s, idx=idx: e.matmul(out, s5CT.t[:, 1, s, :], hi.t[:, idx * ntok:(idx + 1) * ntok],
                                                                             start=False, stop=True, skip_group_check=True,
                                                                             tile_position=(0, 32 * (s % 4))),
                             reads=[s5CT.b, hi.b], writes=[PB[4].sub("y5")])
                    q0 = 0 if not is_s else 2 * gi_
                    S.op("act", lambda e: e.activation(out=y5pre.t[:, q0:q0 + nq, tk0:tk0 + ntok],
                                                       in_=PB[4].t[:, y5c0:y5c0 + nq * ntok].rearrange("p (q t) -> p q t", t=ntok), func=AF.Copy),
                         reads=[PB[4].sub("y5")], writes=[y5pre.b])
                if pend_y5[0] is not None:
                    pend_y5[0]()
                    yield
                pend_y5[0] = emit_y5
            if pend_y5[0] is not None:
                pend_y5[0]()
                pend_y5[0] = None
                yield
            if ti == 7:
                S.op("dve", lambda e: e.tensor_copy(out=s5fin.t[:, :, :, 0], in_=s5cr.t[:]), reads=[s5cr.b], writes=[s5fin.b])
            if is_s:
                S.dma("sp", o_s5, s5fin.t[:].rearrange("p a s q -> p (a s q)"), reads=[s5fin.b], buf=s5fin.b)
                outbufs.append(s5fin.b)
            if ti == 0:
                dump("y5pre", y5pre.t[:].rearrange("p k t -> p (k t)"), [128, 4 * NTM], [y5pre.b])
            ckpt("E%d" % ti)
            yield
            S.op("act", lambda e: e.activation(out=g5.t[:, :, 0:NT], in_=y5pre.t[:, :, 0:NT], func=AF.Gelu), reads=[y5pre.b], writes=[g5.b])
            for m in range(4):
                yield
                pb = next_pb()
                for q in range(4):
                    S.op("pe", lambda e, m=m, q=q, pb=pb: e.matmul(pb.t[:, 0:NT], wglu_sb.t[:, q, m * 128:(m + 1) * 128], g5.t[:, q, 0:NT],
                                                                   start=(q == 0), stop=(q == 3)),
                         reads=[wglu_sb.b, g5.b], writes=[pb.b])
                S.op("act", lambda e, m=m, pb=pb: e.activation(out=sgl.t[:, 0:NT], in_=pb.t[:, 0:NT], func=AF.Sigmoid,
                                                               bias=prm.t[:, P_S5M + 4 + m:P_S5M + 5 + m]),
                     reads=[pb.b, prm.b], writes=[sgl.b])
                S.op("dve", lambda e, m=m: TT(e, mixt[ti % 2].t[:, 4 + m, 0:NT], g5.t[:, m, 0:NT], sgl.t[:, 0:NT], ALU.mult),
                     reads=[g5.b, sgl.b], writes=[mixt[ti % 2].sub("s5")])
            S.dma("sp", mixd[:, :, t0:t0 + NT], mixt[ti % 2].t[:, :, 0:NT], reads=mixt[ti % 2].allb(), writes=[mixdb[ti]], buf=mixdb[ti])
            ckpt("T%d" % ti)
            if ti == 0:
                dump("mix0", mixt[0].t[:, :, 0:NTM], [128, 8, NTM], mixt[0].allb())
            yield

        import os as _os
        RATIO = int(_os.environ.get("K_RATIO", "1"))

        def drive(gens, ada_every=0):
            gens = [g for g in gens if g is not None]
            n = 0
            while gens:
                for gi__, g in enumerate(list(gens)):
                    for _ in range((RATIO if gi__ == 0 else 1) if RATIO > 0 else (-RATIO if gi__ == 1 else 1)):
                        try:
                            next(g)
                        except StopIteration:
                            if g in gens:
                                gens.remove(g)
                            break
                n += 1
                if ada_every and n % ada_every == 0:
                    ada_step()
        ada_state[0] = 0
        drive([chain1(0)], ada_every=12)
        for ti_ in range(len(TILES_A)):
            if ti_ == 7:
                while ada_state[1] < len(ADA_CH):
                    ada_step()
                fill_x(a1x, amod.t[:, 0:8, 1:17], [amod.b])
                fill_x(sh1x, chunkmod(MOD_SH1)[:, :, 1:17], [mod.b])
                make_amod([(1, (4, 1)), (2, (7, 2))])
            drive([chain2(ti_), chain1(ti_ + 1) if ti_ + 1 < len(TILES_A) else None], ada_every=(10 if ti_ < 7 else 0))
        dump("mixS", mixt[0].t[:, :, 0:64], [128, 8, 64], mixt[0].allb())
        S.barrier()
        ckpt("1a")
        A.lo = LO_P1
        x1T = A.alloc("x1T", [128, 8, NTOK], F32, top=True)
        vT = A.alloc("vT", [128, 8, NTOK], BF16, top=True)
        wout_sb = A.alloc("wout_sb", [128, 8, D], BF16)
        wout_v = wout.rearrange("(kt p) n -> p kt n", p=128)
        for kh in range(4):
            S.dma("pool", wout_sb.t[:, 2 * kh:2 * kh + 2, :], wout_v[:, 2 * kh:2 * kh + 2, :], writes=[wout_sb.b])
        mixb = [A.alloc("mixb%d" % i, [128, 8, 512], BF16) for i in range(2)]

        def load_mix(ti):
            t0, NT, is_s = TILES_B[ti]
            tiles_a = [i for i, (a0, n0, s0_) in enumerate(TILES_A) if a0 >= t0 and a0 < t0 + NT]
            S.dma("sp", mixb[ti % 2].t[:, :, 0:NT], mixd[:, :, t0:t0 + NT], reads=[mixdb[i] for i in tiles_a], writes=[mixb[ti % 2].b])
        xtm2 = A.alloc("xtm2", [128, 4, D], F32)
        xTm = [A.alloc("xTm%d" % i, [128, 512], F32) for i in range(2)]
        sqb = [A.alloc("sqb%d" % i, [128, 512], BF16) for i in range(2)]
        onesb = A.alloc("onesb", [128, 128], BF16)
        S.op("dve", lambda e: e.memset(onesb.t[:], 1.0), writes=[onesb.b])
        tmp2 = [A.alloc("tmp2_%d" % i, [128, 512], F32) for i in range(2)]
        rstdb = [A.alloc("rstdb%d" % i, [128, 512], F32) for i in range(2)]
        g1x = expand_mod("g1x", chunkmod(MOD_G1)[:, :, 1:17], [mod.b])
        a2x = expand_mod("a2x", amod.t[:, 8:16, 1:17], [amod.b])
        sh2x = expand_mod("sh2x", chunkmod(MOD_SH2)[:, :, 1:17], [mod.b])
        print("arena p1b: lo=%d hi=%d" % (A.lo, A.hi))
        TILES_B = [(i * 512, 512, False) for i in range(4)] + [(SEQ, 64, True)]

        def load_x2(ti):
            t0, NT, is_s = TILES_B[ti]
            for blk in range((NT + 127) // 128):
                rows = min(128, NT - blk * 128)
                S.dma("sp", xtm2.t[0:rows, blk, :], xin[t0 + blk * 128:t0 + blk * 128 + rows, :], writes=[xtm2.sub(blk)])
        load_x2(0)
        load_mix(0)

        def stat_accum(src_ap, m, NT, pbs):
            sq = sqb[m % 2]
            S.op("act", lambda e: e.activation(out=sq.t[:, 0:NT], in_=src_ap, func=AF.Square), reads=[x1T.sub(m)], writes=[sq.b])
            S.op("pe", lambda e: e.matmul(pbs.t[:, 0:NT], onesb.t[:], sq.t[:, 0:NT], start=(m == 0), stop=(m == 7)),
                 reads=[onesb.b, sq.b], writes=[pbs.b])

        def stat_finish(NT, pbs, rs):
            S.op("act", lambda e: e.activation(out=rs.t[:, 0:NT], in_=pbs.t[:, 0:NT], func=AF.Sqrt, scale=1.0 / D, bias=EPS),
                 reads=[pbs.b], writes=[rs.b])
            S.op("dve", lambda e: e.reciprocal(out=rs.t[:, 0:NT], in_=rs.t[:, 0:NT]), reads=[rs.b], writes=[rs.b])

        def b_part1(ti):
            t0, NT, is_s = TILES_B[ti]
            nblk = (NT + 127) // 128
            tsl = slice(t0, t0 + NT)
            pbs = PB[4 + ti % 2]
            for m in range(8):
                pbx = PB[2 + m % 2]
                xm = xTm[m % 2]
                for blk in range(nblk):
                    rows = min(128, NT - blk * 128)
                    S.op("pe", lambda e, blk=blk, rows=rows: e.transpose(
                        pbx.t[:, blk * 128:blk * 128 + rows], xtm2.t[0:rows, blk, m * 128:(m + 1) * 128], cst.t[0:rows, C_ID:C_ID + rows]),
                        reads=[xtm2.sub(blk), cst.b], writes=[pbx.b])
                S.op("act", lambda e: e.activation(out=xm.t[:, 0:NT], in_=pbx.t[:, 0:NT], func=AF.Copy), reads=[pbx.b], writes=[xm.b])
                pb = next_pb()
                for kt in range(8):
                    S.op("pe", lambda e, kt=kt: e.matmul(pb.t[:, 0:NT], wout_sb.t[:, kt, m * 128:(m + 1) * 128], mixb[ti % 2].t[:, kt, 0:NT],
                                                         start=(kt == 0), stop=(kt == 7)),
                         reads=[wout_sb.b, mixb[ti % 2].b], writes=[pb.b])
                if m == 0 and ti + 1 < len(TILES_B):
                    load_mix(ti + 1)
                if not is_s:
                    S.op("dve", lambda e: e.scalar_tensor_tensor(
                        out=x1T.t[:, m, tsl], in0=pb.t[:, 0:NT], scalar=mod.t[:, 8 * MOD_G1 + m, 0:1], in1=xm.t[:, 0:NT],
                        op0=ALU.mult, op1=ALU.add), reads=[pb.b, mod.b, xm.b], writes=[x1T.sub(m)])
                else:
                    S.op("dve", lambda e: TT(e, tmp2[0].t[:, 0:NT], pb.t[:, 0:NT], g1x.t[:, m, :], ALU.mult),
                         reads=[pb.b, g1x.b], writes=[tmp2[0].b])
                    S.op("dve", lambda e: TT(e, x1T.t[:, m, tsl], tmp2[0].t[:, 0:NT], xm.t[:, 0:NT], ALU.add),
                         reads=[tmp2[0].b, xm.b], writes=[x1T.sub(m)])
                stat_accum(x1T.t[:, m, tsl], m, NT, pbs)
                yield
            if ti + 1 < len(TILES_B):
                load_x2(ti + 1)
            yield

        def b_part2(ti):
            t0, NT, is_s = TILES_B[ti]
            tsl = slice(t0, t0 + NT)
            rs = rstdb[ti % 2]
            stat_finish(NT, PB[4 + ti % 2], rs)
            yield
            for m in range(8):
                tq = tmp2[m % 2]
                S.op("dve", lambda e: TT(e, tq.t[:, 0:NT], x1T.t[:, m, tsl], rs.t[:, 0:NT], ALU.mult),
                     reads=[x1T.sub(m), rs.b], writes=[tq.b])
                if not is_s:
                    S.op("act", lambda e: e.activation(out=vT.t[:, m, tsl], in_=tq.t[:, 0:NT], func=AF.Identity,
                                                       scale=amod.t[:, 8 + m, 0:1], bias=mod.t[:, 8 * MOD_SH2 + m, 0:1]),
                         reads=[tq.b, amod.b, mod.b], writes=[vT.sub(m)])
                else:
                    S.op("dve", lambda e: TT(e, tq.t[:, 0:NT], tq.t[:, 0:NT], a2x.t[:, m, :], ALU.mult),
                         reads=[tq.b, a2x.b], writes=[tq.b])
                    S.op("dve", lambda e: TT(e, vT.t[:, m, tsl], tq.t[:, 0:NT], sh2x.t[:, m, :], ALU.add),
                         reads=[tq.b, sh2x.b], writes=[vT.sub(m)])
                yield
            if ti == 0:
                dump("x1p", x1T.t[:, :, 0:256], [128, 8, 256], x1T.allb())
                dump("vp", vT.t[:, :, 0:256], [128, 8, 256], vT.allb())
        drive([b_part1(0)])
        for ti_ in range(len(TILES_B)):
            drive([b_part2(ti_), b_part1(ti_ + 1) if ti_ + 1 < len(TILES_B) else None])
        S.barrier()
        ckpt("1b")

        A.lo = LO_GLOBAL
        tmp2 = [A.alloc("tmp3_%d" % i, [128, 512], F32) for i in range(2)]
        rstdb = [A.alloc("rstd3_%d" % i, [128, 512], F32) for i in range(2)]
        sqb = [A.alloc("sqb3_%d" % i, [128, 512], BF16) for i in range(2)]
        onesb = A.alloc("onesb3", [128, 128], BF16)
        S.op("dve", lambda e: e.memset(onesb.t[:], 1.0), writes=[onesb.b])
        g2x = expand_mod("g2x", chunkmod(MOD_G2)[:, :, 1:17], [mod.b])
        afx = expand_mod("afx", amod.t[:, 16:24, 1:17], [amod.b])
        shfx = expand_mod("shfx", chunkmod(MOD_SHF)[:, :, 1:17], [mod.b])
        LO_P2 = A.lo
        hT = A.alloc("hT", [128, 6, NTOK], BF16)
        wgs = [A.alloc("wgs%d" % i, [128, 8, 256], BF16) for i in range(3)]
        wus = [A.alloc("wus%d" % i, [128, 8, 256], BF16) for i in range(3)]
        wds = [A.alloc("wds%d" % i, [128, 6, D], BF16) for i in range(2)]
        sgt = [A.alloc("sgt%d" % i, [128, 512], BF16) for i in range(2)]
        print("arena p2: lo=%d hi=%d" % (A.lo, A.hi))
        wg_v = wg.rearrange("(kt p) n -> p kt n", p=128)
        wu_v = wu.rearrange("(kt p) n -> p kt n", p=128)
        wd_v = wd.rearrange("(j p) n -> p j n", p=128)
        QUARTERS = [(0, 6), (6, 12), (12, 18), (18, 22)]
        SLABS = [(q, ja + 2 * s) for q, (ja, jb) in enumerate(QUARTERS) for s in range((jb - ja) // 2)]

        def load_gu(si):
            q, j0 = SLABS[si]
            S.dma("pool", wgs[si % 3].t[:], wg_v[:, :, j0 * 128:(j0 + 2) * 128], writes=[wgs[si % 3].b])
            S.dma("pool", wus[si % 3].t[:], wu_v[:, :, j0 * 128:(j0 + 2) * 128], writes=[wus[si % 3].b])

        def load_wd(q):
            ja, jb = QUARTERS[q]
            for jh in range(0, jb - ja, 2):
                S.dma("pool", wds[q % 2].t[:, jh:jh + 2, :], wd_v[:, ja + jh:ja + jh + 2, :], writes=[wds[q % 2].b])
        load_gu(0)
        load_gu(1)
        load_wd(0)
        gbank = [0]
        si = 0
        for q, (ja, jb) in enumerate(QUARTERS):
            if q + 1 < 4:
                load_wd(q + 1)
            for s in range((jb - ja) // 2):
                if si + 2 < len(SLABS):
                    load_gu(si + 2)
                wgt, wut = wgs[si % 3], wus[si % 3]
                for jc in range(2):
                    jj = 2 * s + jc
                    for (t0, NT, is_s) in TILES_B:
                        tsl = slice(t0, t0 + NT)
                        gbank[0] ^= 1
                        pbg, pbu = PB[gbank[0]], PB[2 + gbank[0]]
                        for (wt, pb_) in ((wgt, pbg), (wut, pbu)):
                            for kt in range(8):
                                S.op("pe", lambda e, kt=kt, wt=wt, pb_=pb_: e.matmul(
                                    pb_.t[:, 0:NT], wt.t[:, kt, jc * 128:(jc + 1) * 128], vT.t[:, kt, tsl], start=(kt == 0), stop=(kt == 7)),
                                    reads=[wt.b] + vT.allb(), writes=[pb_.b])
                        sg_ = sgt[gbank[0]]
                        S.op("act", lambda e, pbg=pbg, sg_=sg_: e.activation(out=sg_.t[:, 0:NT], in_=pbg.t[:, 0:NT], func=AF.Silu),
                             reads=[pbg.b], writes=[sg_.b])
                        S.op("dve", lambda e, pbu=pbu, sg_=sg_: TT(e, hT.t[:, jj, tsl], sg_.t[:, 0:NT], pbu.t[:, 0:NT], ALU.mult),
                             reads=[sg_.b, pbu.b], writes=[hT.sub(jj)])
                si += 1
            nj = jb - ja
            wdt = wds[q % 2]
            for (t0, NT, is_s) in TILES_B:
                tsl = slice(t0, t0 + NT)
                for m in range(8):
                    pb = PB[4 + m % 2]
                    for jj in range(nj):
                        S.op("pe", lambda e, jj=jj, m=m, pb=pb: e.matmul(pb.t[:, 0:NT], wdt.t[:, jj, m * 128:(m + 1) * 128], hT.t[:, jj, tsl],
                                                                         start=(jj == 0), stop=(jj == nj - 1)),
                             reads=[wdt.b, hT.sub(jj)], writes=[pb.b])
                    if not is_s:
                        S.op("dve", lambda e, m=m, pb=pb: e.scalar_tensor_tensor(
                            out=x1T.t[:, m, tsl], in0=pb.t[:, 0:NT], scalar=mod.t[:, 8 * MOD_G2 + m, 0:1], in1=x1T.t[:, m, tsl],
                            op0=ALU.mult, op1=ALU.add), reads=[pb.b, mod.b, x1T.sub(m)], writes=[x1T.sub(m)])
                    else:
                        S.op("dve", lambda e, m=m, pb=pb: TT(e, tmp2[0].t[:, 0:NT], pb.t[:, 0:NT], g2x.t[:, m, :], ALU.mult),
                             reads=[pb.b, g2x.b], writes=[tmp2[0].b])
                        S.op("dve", lambda e, m=m: TT(e, x1T.t[:, m, tsl], tmp2[0].t[:, 0:NT], x1T.t[:, m, tsl], ALU.add),
                             reads=[tmp2[0].b, x1T.sub(m)], writes=[x1T.sub(m)])
        S.barrier()
        ckpt("ffn")
        A.lo = LO_P2
        yTs = [A.alloc("yT%d" % i, [128, 8, 512], F32) for i in range(2)]
        ytm = [A.alloc("ytm%d" % i, [128, D], F32) for i in range(2)]
        print("arena final: lo=%d hi=%d" % (A.lo, A.hi))
        oi = [0]

        def f_part1(ti):
            t0, NT, is_s = TILES_B[ti]
            tsl = slice(t0, t0 + NT)
            yT = yTs[ti % 2]
            pbs = PB[6 + ti % 2]
            rs = rstdb[ti % 2]
            for m in range(8):
                stat_accum(x1T.t[:, m, tsl], m, NT, pbs)
                if m % 2 == 1:
                    yield
            stat_finish(NT, pbs, rs)
            yield
            for m in range(8):
                tq = tmp2[m % 2]
                S.op("dve", lambda e: TT(e, tq.t[:, 0:NT], x1T.t[:, m, tsl], rs.t[:, 0:NT], ALU.mult),
                     reads=[x1T.sub(m), rs.b], writes=[tq.b])
                if not is_s:
                    S.op("act", lambda e: e.activation(out=yT.t[:, m, 0:NT], in_=tq.t[:, 0:NT], func=AF.Identity,
                                                       scale=amod.t[:, 16 + m, 0:1], bias=mod.t[:, 8 * MOD_SHF + m, 0:1]),
                         reads=[tq.b, amod.b, mod.b], writes=[yT.sub(m)])
                else:
                    S.op("dve", lambda e: TT(e, tq.t[:, 0:NT], tq.t[:, 0:NT], afx.t[:, m, :], ALU.mult),
                         reads=[tq.b, afx.b], writes=[tq.b])
                    S.op("dve", lambda e: TT(e, yT.t[:, m, 0:NT], tq.t[:, 0:NT], shfx.t[:, m, :], ALU.add),
                         reads=[tq.b, shfx.b], writes=[yT.sub(m)])
                yield

        def f_part2(ti):
            t0, NT, is_s = TILES_B[ti]
            yT = yTs[ti % 2]
            for blk in range((NT + 127) // 128):
                rows = min(128, NT - blk * 128)
                yo = ytm[oi[0] % 2]
                oi[0] += 1
                for half in range(2):
                    pbt = PB[half]
                    for k4 in range(4):
                        kt = 4 * half + k4
                        S.op("pe", lambda e, kt=kt, k4=k4: e.transpose(
                            pbt.t[0:rows, k4 * 128:(k4 + 1) * 128], yT.t[:, kt, blk * 128:blk * 128 + rows], ident),
                            reads=[yT.sub(kt), cst.b], writes=[pbt.b])
                    if half == 0:
                        S.op("act", lambda e: e.activation(out=yo.t[0:rows, 0:512], in_=pbt.t[0:rows, :], func=AF.Copy),
                             reads=[pbt.b], writes=[yo.b])
                    else:
                        S.op("dve", lambda e: e.tensor_copy(out=yo.t[0:rows, 512:1024], in_=pbt.t[0:rows, :]),
                             reads=[pbt.b], writes=[yo.b])
                    yield
                S.dma("sp", yout[t0 + blk * 128:t0 + blk * 128 + rows, :], yo.t[0:rows, :], reads=[yo.b], buf=yo.b)
        import os as _os2
        if True:
            for ti_ in range(len(TILES_B)):
                drive([f_part1(ti_)])
                drive([f_part2(ti_)])
        else:
            drive([f_part1(0)])
            for ti_ in range(len(TILES_B)):
                drive([f_part2(ti_), f_part1(ti_ + 1) if ti_ + 1 < len(TILES_B) else None])
        S.barrier()
    return nc, dumps


def _prep_inputs(inp):
    cstv = _consts()
    prmv = _params(inp)
    BT, CT = _s5mats(inp)
    maps = []
    for i in range(NCORES):
        m = {}
        m["xin"] = np.ascontiguousarray(np.concatenate(
            [inp["x_prompt"][i], inp["x_sample"][NS * i:NS * (i + 1)].reshape(NS * LS, D)], axis=0), dtype=np.float32)
        m["cin"] = np.ascontiguousarray(np.concatenate(
            [inp["c_prompt"][i:i + 1], inp["c_sample"][NS * i:NS * (i + 1)]], axis=0), dtype=np.float32)
        m["wada"] = np.ascontiguousarray(inp["w_ada"][0], dtype=np.float32)
        m["wadaf"] = np.ascontiguousarray(inp["w_ada_f"], dtype=np.float32)
        m["win"] = np.ascontiguousarray(inp["w_in"][0], dtype=np.float32)
        m["wglu"] = np.ascontiguousarray(inp["w_glu"][0], dtype=np.float32)
        m["wout"] = np.ascontiguousarray(inp["w_out"][0], dtype=np.float32)
        m["wg"] = np.ascontiguousarray(inp["w_ffn_gate"][0], dtype=np.float32)
        m["wu"] = np.ascontiguousarray(inp["w_ffn_up"][0], dtype=np.float32)
        m["wd"] = np.ascontiguousarray(inp["w_ffn_down"][0], dtype=np.float32)
        m["cst"] = cstv
        m["prm"] = prmv
        m["s5bt"] = BT.reshape(128, -1)
        m["s5ct"] = CT.reshape(128, -1)
        m["stssd"] = np.ascontiguousarray(inp["state_ssd"][0, NS * i:NS * (i + 1)], dtype=np.float32)
        sc = inp["state_conv"][0, NS * i:NS * (i + 1)]
        m["stconv"] = np.ascontiguousarray(
            sc.reshape(NS, 3, 8, 128).transpose(3, 2, 0, 1).reshape(128, -1), dtype=np.float32)
        sr = inp["state_s5_re"][0, NS * i:NS * (i + 1)]
        si = inp["state_s5_im"][0, NS * i:NS * (i + 1)]
        st = np.stack([sr, si], 0).reshape(2, NS, 16, 128).transpose(3, 0, 2, 1)
        m["sts5"] = np.ascontiguousarray(st.reshape(128, -1), dtype=np.float32)
        maps.append(m)
    return maps


_CACHE = {}


def kernel(**inputs):
    inp = {k: np.asarray(v) for k, v in inputs.items()}
    if "nc" not in _CACHE:
        _CACHE["nc"] = build()[0]
    nc = _CACHE["nc"]
    maps = _prep_inputs(inp)
    res = run_bass_kernel_spmd(nc, maps, core_ids=list(range(NCORES)))
    R = res.results
    y_p = np.stack([R[i]["yout"][:SEQ] for i in range(NCORES)], 0)
    y_s = np.concatenate([R[i]["yout"][SEQ:].reshape(NS, LS, D) for i in range(NCORES)], 0)
    ssd_p = np.stack([R[i]["o_ssdp"].reshape(128, 8, 64).transpose(1, 2, 0) for i in range(NCORES)], 0)[None]
    ssd_s = np.concatenate([R[i]["o_ssds"] for i in range(NCORES)], 0)[None]
    conv = [R[i]["o_conv"].reshape(128, 8, 17, 3).transpose(2, 3, 1, 0).reshape(17, 3, 1024) for i in range(NCORES)]
    conv_p = np.stack([c[0] for c in conv], 0)[None]
    conv_s = np.concatenate([c[1:] for c in conv], 0)[None]
    s5 = [R[i]["o_s5"].reshape(128, 2, 16, 17).transpose(1, 3, 2, 0).reshape(2, 17, 32, 64) for i in range(NCORES)]
    re_p = np.stack([s[0, 0] for s in s5], 0)[None]
    re_s = np.concatenate([s[0, 1:] for s in s5], 0)[None]
    im_p = np.stack([s[1, 0] for s in s5], 0)[None]
    im_s = np.concatenate([s[1, 1:] for s in s5], 0)[None]
    f = lambda a: np.ascontiguousarray(a, dtype=np.float32)
    return (f(y_p), f(y_s), f(ssd_p), f(ssd_s), f(conv_p), f(conv_s), f(re_p), f(re_s), f(im_p), f(im_s))
```

```python
import math
import numpy as np
from contextlib import ExitStack
import concourse.bass as bass
import concourse.mybir as mybir
from concourse.bass_utils import run_bass_kernel_spmd

F32 = mybir.dt.float32
BF16 = mybir.dt.bfloat16
I32 = mybir.dt.int32
AF = mybir.ActivationFunctionType
ALU = mybir.AluOpType

NCORES = 8
D = 1024
SEQ = 2048
NS = 16
LS = 4
NTOK = SEQ + NS * LS
DFF = 2816
NJ = DFF // 128
INP = 2056
EPS = 1e-6
T5 = 32
TILES = [(0, 512), (512, 512), (1024, 512), (1536, 512), (2048, 64)]
PI = math.pi


class Buf:
    def __init__(self, name):
        self.name = name
        self.w = None
        self.r = []
        self.dsem = None
        self.dcnt = 0


class TL:
    def __init__(self, t, name):
        self.t = t
        self.name = name
        self.b = Buf(name)
        self.subs = {}

    def sub(self, k):
        if getattr(self, "nosub", False):
            return self.b
        if k not in self.subs:
            self.subs[k] = Buf("%s_%s" % (self.name, k))
        return self.subs[k]

    def allb(self):
        return [self.b] + list(self.subs.values())

    def __getitem__(self, k):
        return self.t[k]


class Sched:
    ENG = ["pe", "act", "dve", "pool", "sp"]

    def __init__(self, nc, es):
        self.nc = nc
        self.es = es
        self.eobj = {"pe": nc.tensor, "act": nc.scalar, "dve": nc.vector, "pool": nc.gpsimd, "sp": nc.sync}
        self.cnt = {e: 0 for e in self.ENG}
        self.sem = {e: es.enter_context(nc.semaphore("s_" + e)) for e in self.ENG}
        self.seen = {e: {} for e in self.ENG}
        self.dbufs = []
        self.ninst = 0
        self.dead = False
        self.pe_pending = None

    def _flush_pe(self):
        if self.pe_pending is not None:
            self.pe_pending.then_inc(self.sem["pe"], 1)
            self.cnt["pe"] += 1
            self.pe_pending = None

    def _deps(self, eng, reads, writes):
        deps = []
        for b in reads:
            if b.w is not None:
                deps.append(b.w)
        for b in writes:
            if b.w is not None:
                deps.append(b.w)
            deps.extend(b.r)
        waits = {}
        for (sem, val, key) in deps:
            if key == "pe" and eng == "pe":
                continue
            if self.seen[eng].get(key, 0) >= val:
                continue
            if key == "pe" and val > self.cnt["pe"]:
                self._flush_pe()
            if key not in waits or waits[key][1] < val:
                waits[key] = (sem, val)
        for key, (sem, val) in waits.items():
            self.seen[eng][key] = val
        return list(waits.values())

    def op(self, eng, fn, reads=(), writes=()):
        if self.dead:
            return None
        xr = [b for b in reads if getattr(b, "excl", False)]
        if xr:
            reads = [b for b in reads if not getattr(b, "excl", False)]
            writes = list(writes) + xr
        waits = self._deps(eng, reads, writes)
        e = self.eobj[eng]
        for (s_, v_) in waits:
            e.wait_ge(s_, v_)
        if eng == "pe":
            self.pe_pending = fn(e)
            tok = (self.sem[eng], self.cnt[eng] + 1, eng)
        else:
            self.cnt[eng] += 1
            tok = (self.sem[eng], self.cnt[eng], eng)
            fn(e).then_inc(self.sem[eng], 1)
        for b in reads:
            b.r.append(tok)
        for b in writes:
            b.w = tok
            b.r = []
        self.ninst += 1
        return tok

    def dma(self, eng, out, in_, reads=(), writes=(), buf=None, **kw):
        if self.dead:
            return None
        waits = self._deps(eng, reads, writes)
        if buf is None:
            buf = writes[0] if writes else reads[0]
        if buf.dsem is None:
            buf.dsem = self.es.enter_context(self.nc.semaphore("d_" + buf.name))
            self.dbufs.append(buf)
        buf.dcnt += 16
        tok = (buf.dsem, buf.dcnt, "d_" + buf.name)
        e = self.eobj[eng]
        for (s_, v_) in waits:
            e.wait_ge(s_, v_)
        e.dma_start(out=out, in_=in_, **kw).then_inc(buf.dsem, 16)
        for b in reads:
            b.r.append(tok)
        for b in writes:
            b.w = tok
            b.r = []
        self.ninst += 1
        return tok

    def barrier(self):
        if self.dead:
            return
        self._flush_pe()
        for e in self.ENG:
            waits = []
            for o in self.ENG:
                if o != e and self.cnt[o] > self.seen[e].get(o, 0):
                    waits.append((self.sem[o], self.cnt[o]))
                    self.seen[e][o] = self.cnt[o]
            for b in self.dbufs:
                key = "d_" + b.name
                if b.dcnt > self.seen[e].get(key, 0):
                    waits.append((b.dsem, b.dcnt))
                    self.seen[e][key] = b.dcnt
            for (s_, v_) in waits:
                self.eobj[e].wait_ge(s_, v_)

    def emit(self):
        pass


C_ID = 0
C_TRI = 128
C_NEG = 256
C_TRI64 = 384
C_NEG64 = 512
C_SEG64 = 640
C_SEGI = 768
CST_W = 784

P_BMOD = 0
P_GAIN = 64
P_CONV = 88
P_SSDFM = 128
P_S5P = 136
P_S5M = 184
P_SSD8 = 192
PRM_W = 194


def _consts():
    c = np.zeros((128, CST_W), np.float32)
    c[:, C_ID:C_ID + 128] = np.eye(128, dtype=np.float32)
    s = np.arange(128)[:, None]
    l = np.arange(128)[None, :]
    c[:, C_TRI:C_TRI + 128] = (s <= l).astype(np.float32)
    c[:, C_NEG:C_NEG + 128] = np.where(l >= s, 0.0, -30000.0)
    same = (s // LS == l // LS) & (s < 64) & (l < 64)
    c[:, C_TRI64:C_TRI64 + 128] = ((s <= l) & same).astype(np.float32)
    c[:, C_NEG64:C_NEG64 + 128] = np.where((l >= s) & same, 0.0, -30000.0)
    c[:, C_SEG64:C_SEG64 + 128] = same.astype(np.float32)
    j = np.arange(16)[None, :]
    c[:, C_SEGI:C_SEGI + 16] = ((s // LS == j) & (s < 64)).astype(np.float32)
    return c


def _fm(v, nt):
    return np.ascontiguousarray(np.asarray(v, np.float32).reshape(nt, 128).T)


def _params(inp):
    p = np.zeros((128, PRM_W), np.float32)
    p[:, P_BMOD:P_BMOD + 48] = _fm(inp["b_ada"][0], 48)
    p[:, P_BMOD + 48:P_BMOD + 64] = _fm(inp["b_ada_f"], 16)
    p[:, P_GAIN:P_GAIN + 8] = _fm(inp["norm1_g"][0], 8)
    p[:, P_GAIN + 8:P_GAIN + 16] = _fm(inp["norm2_g"][0], 8)
    p[:, P_GAIN + 16:P_GAIN + 24] = _fm(inp["normf_g"], 8)
    cw = inp["conv_w"][0]
    cv = np.zeros((128, 8, 5), np.float32)
    for k in range(4):
        cv[:, :, k] = _fm(cw[k], 8)
    cv[:, :, 4] = _fm(inp["conv_b"][0], 8)
    p[:, P_CONV:P_CONV + 40] = cv.reshape(128, 40)
    Dh = inp["ssd_D"][0]
    dfm = np.zeros((128, 4), np.float32)
    for pr in range(4):
        dfm[0:64, pr] = Dh[2 * pr]
        dfm[64:128, pr] = Dh[2 * pr + 1]
    p[:, P_SSDFM:P_SSDFM + 4] = dfm
    p[:, P_SSDFM + 4:P_SSDFM + 8] = _fm(inp["ssd_norm_g"][0], 4)

    def st(a):
        return np.ascontiguousarray(np.asarray(a, np.float32).reshape(16, 128).T)
    p[:, P_S5P:P_S5P + 16] = st(inp["s5_A_re"][0])
    p[:, P_S5P + 16:P_S5P + 32] = st(inp["s5_A_im"][0])
    p[:, P_S5P + 32:P_S5P + 48] = st(np.repeat(inp["s5_log_step"][0][:, None], 64, axis=1))
    p[:, P_S5M:P_S5M + 4] = _fm(inp["s5_D"][0], 4)
    p[:, P_S5M + 4:P_S5M + 8] = _fm(inp["b_glu"][0], 4)
    p[0:8, P_SSD8] = inp["ssd_dt_bias"][0]
    p[0:8, P_SSD8 + 1] = inp["ssd_A_log"][0]
    return p


def _s5mats(inp):
    Br, Bi = inp["s5_B_re"][0], inp["s5_B_im"][0]
    Cr, Ci = inp["s5_C_re"][0], inp["s5_C_im"][0]
    BT = np.zeros((128, 2, 16, 128), np.float32)
    CT = np.zeros((128, 2, 16, 32), np.float32)
    for s in range(16):
        for gl in range(2):
            g = 2 * s + gl
            r0 = (g % 8) * 16
            BT[r0:r0 + 16, 0, s, gl * 64:(gl + 1) * 64] = Br[g].T
            BT[r0:r0 + 16, 1, s, gl * 64:(gl + 1) * 64] = Bi[g].T
            CT[gl * 64:(gl + 1) * 64, 0, s, gl * 16:(gl + 1) * 16] = Cr[g].T
            CT[gl * 64:(gl + 1) * 64, 1, s, gl * 16:(gl + 1) * 16] = Ci[g].T
    return BT, CT


class Arena:
    def __init__(self, nc, es, words):
        self.t = es.enter_context(nc.sbuf_tensor("arena", [128, words], F32))
        self.words = words
        self.lo = 0
        self.hi = words

    def alloc(self, name, shape, dt, top=False):
        n = 1
        for d in shape[1:]:
            n *= d
        w = n if dt == F32 or dt == I32 else (n + 1) // 2
        w = (w + 3) // 4 * 4
        if top:
            self.hi -= w
            off = self.hi
        else:
            off = self.lo
            self.lo += w
        assert self.lo <= self.hi, "arena overflow at %s: lo=%d hi=%d" % (name, self.lo, self.hi)
        ap = self.t[:, off:off + w]
        if dt != F32:
            ap = ap.bitcast(dt)
        ap = ap[:, 0:n]
        if len(shape) == 3:
            ap = ap.rearrange("p (a b) -> p a b", b=shape[2])
        elif len(shape) == 4:
            ap = ap.rearrange("p (a b c) -> p a b c", b=shape[2], c=shape[3])
        if shape[0] < 128:
            ap = ap[0:shape[0]]
        return TL(ap, name)


class StopBuild(Exception):
    pass


def build(dbg=None, stop_after=None):
    nc = bass.Bass("TRN2", target_bir_lowering=False)

    SH = []

    def ckpt(name):
        if stop_after == name:
            SH[0].barrier()
            SH[0].dead = True
    dt_in = lambda name, shape: nc.dram_tensor(name, list(shape), F32, kind="ExternalInput").ap()
    dt_out = lambda name, shape: nc.dram_tensor(name, list(shape), F32, kind="ExternalOutput").ap()
    xin = dt_in("xin", [NTOK, D])
    cin = dt_in("cin", [17, D])
    wada = dt_in("wada", [D, 6144])
    wadaf = dt_in("wadaf", [D, 2048])
    win = dt_in("win", [D, INP])
    wglu = dt_in("wglu", [512, 512])
    wout = dt_in("wout", [D, D])
    wg = dt_in("wg", [D, DFF])
    wu = dt_in("wu", [D, DFF])
    wd = dt_in("wd", [DFF, D])
    cst_d = dt_in("cst", [128, CST_W])
    prm_d = dt_in("prm", [128, PRM_W])
    s5bt_d = dt_in("s5bt", [128, 2 * 16 * 128])
    s5ct_d = dt_in("s5ct", [128, 2 * 16 * 32])
    stssd_d = dt_in("stssd", [NS, 8, 64, 128])
    stconv_d = dt_in("stconv", [128, 8 * NS * 3])
    sts5_d = dt_in("sts5", [128, 2 * 16 * NS])
    yout = dt_out("yout", [NTOK, D])
    o_ssdp = dt_out("o_ssdp", [128, 512])
    o_ssds = dt_out("o_ssds", [NS, 8, 64, 128])
    o_conv = dt_out("o_conv", [128, 8 * 17 * 3])
    o_s5 = dt_out("o_s5", [128, 2 * 16 * 17])
    mixd = nc.dram_tensor("mixd", [128, 8, NTOK], BF16, kind="Internal").ap()
    dumps = {}

    with ExitStack() as es:
        S = Sched(nc, es)
        NEED_CTN = []
        SH.append(S)
        A = Arena(nc, es, 53200)
        outbufs = []

        def dump(name, ap, shape, reads):
            if dbg is None or name not in dbg:
                return
            d = dt_out("dbg_" + name, shape)
            dumps[name] = shape
            b = Buf("dbg_" + name)
            S.dma("sp" if ap.dtype == F32 else "pool", d, ap, reads=reads, buf=b)
            outbufs.append(b)

        PB = [TL(es.enter_context(nc.psum_tensor("pb%d" % i, [128, 512], F32)), "pb%d" % i) for i in range(8)]
        for pb_ in PB:
            pb_.b.excl = True
            pb_.nosub = True

        def pbf(i):
            return PB[i].t[:].bitcast(BF16)

        cst = A.alloc("cst", [128, CST_W], F32)
        prm = A.alloc("prm", [128, PRM_W], F32)
        identb = A.alloc("identb", [128, 128], BF16)
        onesf = A.alloc("onesf", [128, 128], F32)
        mod = A.alloc("mod", [128, 64, 17], F32)
        amod = A.alloc("amod", [128, 24, 17], F32)
        s5fin = A.alloc("s5fin", [128, 2, 16, 17], F32)
        scT = A.alloc("scT", [128, 8, 17], BF16)
        LO_GLOBAL = A.lo

        ident = cst.t[:, C_ID:C_ID + 128]
        S.dma("sp", cst.t[:], cst_d, writes=[cst.b])
        S.dma("sp", prm.t[:], prm_d, writes=[prm.b])
        S.op("act", lambda e: e.activation(out=identb.t[:], in_=ident, func=AF.Copy), reads=[cst.b], writes=[identb.b])
        S.op("dve", lambda e: e.memset(onesf.t[:], 1.0), writes=[onesf.b])

        def chunkmod(i):
            return mod.t[:, 8 * i:8 * i + 8, :]

        cs = A.alloc("cs", [17, D], F32)
        slabs = [A.alloc("adaslab%d" % i, [128, 8, 512], BF16) for i in range(3)]
        S.dma("sp", cs.t[:], cin, writes=[cs.b])
        S.op("act", lambda e: e.activation(out=cs.t[:], in_=cs.t[:], func=AF.Silu), reads=[cs.b], writes=[cs.b])
        for kt in range(8):
            S.op("pe", lambda e, kt=kt: e.transpose(PB[2].t[:, kt * 17:(kt + 1) * 17], cs.t[:, kt * 128:(kt + 1) * 128],
                                                    cst.t[0:17, C_ID:C_ID + 17]),
                 reads=[cs.b, cst.b], writes=[PB[2].b])
        S.op("act", lambda e: e.activation(out=scT.t[:].rearrange("p k s -> p (k s)"), in_=PB[2].t[:, 0:136], func=AF.Copy),
             reads=[PB[2].b], writes=[scT.b])
        wada_v = wada.rearrange("(kt p) n -> p kt n", p=128)
        wadaf_v = wadaf.rearrange("(kt p) n -> p kt n", p=128)

        def slab_src(i):
            if i < 12:
                return wada_v[:, :, i * 512:(i + 1) * 512]
            return wadaf_v[:, :, (i - 12) * 512:(i - 11) * 512]

        def load_slab(i):
            sl = slabs[i % 3]
            for kh in range(2):
                S.dma("pool", sl.t[:, 4 * kh:4 * kh + 4, :], slab_src(i)[:, 4 * kh:4 * kh + 4, :], writes=[sl.b])
        load_slab(0)
        load_slab(1)
        for i in range(4):
            if i + 2 < 4:
                load_slab(i + 2)
            sl = slabs[i % 3]
            pb = PB[i % 2]
            for fc in range(4):
                for kt in range(8):
                    S.op("pe", lambda e, fc=fc, kt=kt, sl=sl, pb=pb: e.matmul(
                        pb.t[:, fc * 17:(fc + 1) * 17], sl.t[:, kt, fc * 128:(fc + 1) * 128], scT.t[:, kt, :],
                        start=(kt == 0), stop=(kt == 7)), reads=[sl.b, scT.b], writes=[pb.b])
            S.op("dve", lambda e, i=i, pb=pb: e.tensor_tensor(
                out=mod.t[:, 4 * i:4 * i + 4, :], in0=pb.t[:, 0:68].rearrange("p (c s) -> p c s", s=17),
                in1=prm.t[:, P_BMOD + 4 * i:P_BMOD + 4 * i + 4].unsqueeze(2).to_broadcast([128, 4, 17]), op=ALU.add),
                reads=[pb.b, prm.b], writes=[mod.b])
        def make_amod(lst):
          for k, (sci, gi) in lst:
            S.op("dve", lambda e, k=k, sci=sci, gi=gi: e.scalar_tensor_tensor(
                out=amod.t[:, 8 * k:8 * k + 8, :], in0=chunkmod(sci), scalar=1.0,
                in1=prm.t[:, P_GAIN + 8 * gi:P_GAIN + 8 * gi + 8].unsqueeze(2).to_broadcast([128, 8, 17]),
                op0=ALU.add, op1=ALU.mult), reads=[mod.b, prm.b], writes=[amod.b])
        make_amod([(0, (1, 0))])
        dump("mod", mod.t[:].rearrange("p c s -> p (c s)"), [128, 64 * 17], [mod.b])
        S.barrier()
        S.emit()
        A.lo = LO_GLOBAL

        MOD_SH1, MOD_G1, MOD_SH2, MOD_G2, MOD_SHF = 0, 2, 3, 5, 6

        def expand_mod(name, src_ap, srcbufs):
            t = A.alloc(name, [128, 8, 64], F32)
            S.op("dve", lambda e: e.tensor_copy(out=t.t[:].rearrange("p k (s b) -> p k s b", b=LS),
                                                in_=src_ap.unsqueeze(3).to_broadcast([128, 8, NS, LS])),
                 reads=srcbufs, writes=[t.b])
            return t

        LO_P1 = A.lo
        mixt = [A.alloc("mixt%d" % i, [128, 8, 256], BF16) for i in range(2)]
        mixdb = [Buf("mixd%d" % i) for i in range(9)]
        win_sb = A.alloc("win_sb", [128, 8, INP], BF16)
        wglu_sb = A.alloc("wglu_sb", [128, 4, 512], BF16)
        s5BT = A.alloc("s5BT", [128, 2, 16, 128], BF16)
        s5CT = A.alloc("s5CT", [128, 2, 16, 32], BF16)
        win_v = win.rearrange("(kt p) n -> p kt n", p=128)
        for kh in range(4):
            for ch in range(2):
                S.dma("pool", win_sb.t[:, 2 * kh:2 * kh + 2, ch * 1028:(ch + 1) * 1028],
                      win_v[:, 2 * kh:2 * kh + 2, ch * 1028:(ch + 1) * 1028], writes=[win_sb.b])
        for a_ in range(4):
            S.dma("pool", s5BT.t[:].rearrange("p a s c -> p (a s c)")[:, a_ * 1024:(a_ + 1) * 1024],
                  s5bt_d[:, a_ * 1024:(a_ + 1) * 1024], writes=[s5BT.b])
        S.dma("pool", s5CT.t[:].rearrange("p a s c -> p (a s c)"), s5ct_d, writes=[s5CT.b])
        S.dma("pool", wglu_sb.t[:], wglu.rearrange("(kt p) n -> p kt n", p=128), writes=[wglu_sb.b])
        S.op("dve", lambda e: e.tensor_scalar(out=s5CT.t[:, 1], in0=s5CT.t[:, 1], scalar1=-1.0, scalar2=None, op0=ALU.mult),
             reads=[s5CT.b], writes=[s5CT.b])
        NEED_CTN.append(1)

        a1x = A.alloc("a1x", [128, 8, 64], F32)
        sh1x = A.alloc("sh1x", [128, 8, 64], F32)

        def fill_x(t, src_ap, srcbufs):
            S.op("dve", lambda e: e.tensor_copy(out=t.t[:].rearrange("p k (s b) -> p k s b", b=LS),
                                                in_=src_ap.unsqueeze(3).to_broadcast([128, 8, NS, LS])),
                 reads=srcbufs, writes=[t.b])
        adab = [TL(a1x.t[:].rearrange("p k t -> p (k t)").bitcast(BF16).rearrange("p (k c) -> p k c", c=128), "adab0"),
                TL(sh1x.t[:].rearrange("p k t -> p (k t)").bitcast(BF16).rearrange("p (k c) -> p k c", c=128), "adab1")]
        adab[0].b = a1x.b
        adab[1].b = sh1x.b
        ADA_CH = list(range(16, 64))

        def ada_load(ci):
            c = ADA_CH[ci]
            src = wada_v[:, :, c * 128:(c + 1) * 128] if c < 48 else wadaf_v[:, :, (c - 48) * 128:(c - 47) * 128]
            S.dma("pool", adab[ci % 2].t[:], src, writes=[adab[ci % 2].b])

        def ada_compute(ci):
            c = ADA_CH[ci]
            sl = adab[ci % 2]
            pb = next_pb()
            for kt in range(8):
                S.op("pe", lambda e, kt=kt: e.matmul(pb.t[:, 0:17], sl.t[:, kt, :], scT.t[:, kt, :], start=(kt == 0), stop=(kt == 7)),
                     reads=[sl.b, scT.b], writes=[pb.b])
            S.op("dve", lambda e: e.tensor_scalar(out=mod.t[:, c, :], in0=pb.t[:, 0:17], scalar1=prm.t[:, P_BMOD + c:P_BMOD + c + 1],
                                                  scalar2=None, op0=ALU.add), reads=[pb.b, prm.b], writes=[mod.b])
        ada_state = [0, 0]

        def ada_step():
            if ada_state[1] >= len(ADA_CH):
                return
            while ada_state[0] < min(len(ADA_CH), ada_state[1] + 2):
                ada_load(ada_state[0])
                ada_state[0] += 1
            ada_compute(ada_state[1])
            ada_state[1] += 1

        ssd8 = A.alloc("ssd8", [8, 4], F32)
        S.op("act", lambda e: e.activation(out=ssd8.t[:, 1:2], in_=prm.t[0:8, P_SSD8 + 1:P_SSD8 + 2], func=AF.Exp),
             reads=[prm.b], writes=[ssd8.b])
        S.op("dve", lambda e: e.tensor_scalar(out=ssd8.t[:, 1:2], in0=ssd8.t[:, 1:2], scalar1=-1.0, scalar2=None, op0=ALU.mult),
             reads=[ssd8.b], writes=[ssd8.b])
        S.op("dve", lambda e: e.tensor_copy(out=ssd8.t[:, 0:1], in_=prm.t[0:8, P_SSD8:P_SSD8 + 1]), reads=[prm.b], writes=[ssd8.b])

        Ptab = A.alloc("Ptab", [128, 2, 16, T5], F32)
        Qtab = A.alloc("Qtab", [128, 2, 16, T5], F32)
        s5t = [A.alloc("s5t%d" % i, [128, 512], F32) for i in range(2)]

        def alias(name, ap, buf):
            tl = TL(ap, name)
            tl.b = buf
            return tl
        sw = alias("s5work", s5t[1].t[:, 0:384].rearrange("p (a b) -> p a b", b=16), s5t[1].b)
        tmpA = alias("tmpA", s5t[0].t[:, 0:256].rearrange("p (a b) -> p a b", b=T5 // 2), s5t[0].b)
        tmpB = alias("tmpB", s5t[0].t[:, 256:512].rearrange("p (a b) -> p a b", b=T5 // 2), s5t[0].b)
        mask32 = A.alloc("mask32", [128, 16, T5], BF16)
        s5v = [A.alloc("s5v%d" % i, [128, 512], F32) for i in range(2)]
        qtmp = alias("qtmp", s5v[0].t[:].rearrange("p (s t) -> p s t", t=T5), s5v[0].b)
        mask4 = A.alloc("mask4", [128, 128, LS], BF16)
        s5cr = A.alloc("s5cr", [128, 2, 16], F32)
        W = lambda i: sw.t[:, i, :]
        pv = lambda i: prm.t[:, P_S5P + 16 * i:P_S5P + 16 * (i + 1)]
        swb = [sw.b, prm.b]

        def dv(fn):
            S.op("dve", fn, reads=swb, writes=[sw.b])

        def act(fn):
            S.op("act", fn, reads=swb, writes=[sw.b])
        TT = lambda e, o, a, b, op: e.tensor_tensor(out=o, in0=a, in1=b, op=op)
        def exp_acc(dst, src):
            dv(lambda e: e.tensor_scalar(out=W(22), in0=src, scalar1=1.0 / 16, scalar2=None, op0=ALU.mult))
            dv(lambda e: e.tensor_scalar(out=dst, in0=W(22), scalar1=1.0 / 7, scalar2=1.0, op0=ALU.mult, op1=ALU.add))
            for k in (6, 5, 4, 3, 2, 1):
                dv(lambda e: TT(e, dst, dst, W(22), ALU.mult))
                dv(lambda e, k=k: e.tensor_scalar(out=dst, in0=dst, scalar1=1.0 / k, scalar2=1.0, op0=ALU.mult, op1=ALU.add))
            for _ in range(4):
                dv(lambda e: TT(e, dst, dst, dst, ALU.mult))
        exp_acc(W(0), pv(2))
        dv(lambda e: TT(e, W(1), pv(0), W(0), ALU.mult))
        dv(lambda e: TT(e, W(2), pv(1), W(0), ALU.mult))
        exp_acc(W(3), W(1))

        def range_reduce(dst, src, add):
            ki = A_ki
            dv(lambda e: e.tensor_scalar(out=W(20), in0=src, scalar1=float(add), scalar2=1.0 / (2 * PI), op0=ALU.add, op1=ALU.mult))
            S.op("dve", lambda e: e.tensor_copy(out=ki.t[:], in_=W(20)), reads=swb, writes=[ki.b])
            S.op("dve", lambda e: e.tensor_copy(out=W(21), in_=ki.t[:]), reads=[ki.b], writes=[sw.b])
            dv(lambda e: e.tensor_scalar(out=W(20), in0=src, scalar1=float(add), scalar2=None, op0=ALU.add))
            dv(lambda e: e.scalar_tensor_tensor(out=dst, in0=W(21), scalar=-2 * PI, in1=W(20), op0=ALU.mult, op1=ALU.add))
            dv(lambda e: e.tensor_scalar(out=dst, in0=dst, scalar1=PI, scalar2=-PI, op0=ALU.min, op1=ALU.max))
        A_ki = A.alloc("s5ki", [128, 16], I32)
        range_reduce(W(4), W(2), 0.0)
        range_reduce(W(5), W(2), PI / 2)
        act(lambda e: e.activation(out=W(6), in_=W(4), func=AF.Sin))
        act(lambda e: e.activation(out=W(7), in_=W(5), func=AF.Sin))
        dv(lambda e: TT(e, W(8), W(3), W(7), ALU.mult))
        dv(lambda e: TT(e, W(9), W(3), W(6), ALU.mult))
        dv(lambda e: e.tensor_scalar(out=W(10), in0=W(8), scalar1=-1.0, scalar2=None, op0=ALU.add))
        dv(lambda e: TT(e, W(11), pv(0), pv(0), ALU.mult))
        dv(lambda e: TT(e, W(12), pv(1), pv(1), ALU.mult))
        dv(lambda e: TT(e, W(11), W(11), W(12), ALU.add))
        dv(lambda e: e.reciprocal(out=W(11), in_=W(11)))
        dv(lambda e: TT(e, W(12), W(10), pv(0), ALU.mult))
        dv(lambda e: TT(e, W(13), W(9), pv(1), ALU.mult))
        dv(lambda e: TT(e, W(12), W(12), W(13), ALU.add))
        dv(lambda e: TT(e, W(14), W(12), W(11), ALU.mult))
        dv(lambda e: TT(e, W(12), W(9), pv(0), ALU.mult))
        dv(lambda e: TT(e, W(13), W(10), pv(1), ALU.mult))
        dv(lambda e: TT(e, W(12), W(12), W(13), ALU.subtract))
        dv(lambda e: TT(e, W(15), W(12), W(11), ALU.mult))
        dv(lambda e: TT(e, W(12), W(8), W(8), ALU.mult))
        dv(lambda e: TT(e, W(13), W(9), W(9), ALU.mult))
        dv(lambda e: TT(e, W(12), W(12), W(13), ALU.add))
        dv(lambda e: e.reciprocal(out=W(12), in_=W(12)))
        dv(lambda e: TT(e, W(16), W(8), W(12), ALU.mult))
        dv(lambda e: e.scalar_tensor_tensor(out=W(17), in0=W(9), scalar=-1.0, in1=W(12), op0=ALU.mult, op1=ALU.mult))

        def build_pow(tab, br, bi):
            tb = [tab.b, sw.b, tmpA.b, tmpB.b]
            S.op("dve", lambda e: e.tensor_copy(out=tab.t[:, 0, :, 0], in_=br), reads=tb, writes=[tab.b])
            S.op("dve", lambda e: e.tensor_copy(out=tab.t[:, 1, :, 0], in_=bi), reads=tb, writes=[tab.b])
            n = 1
            while n < T5:
                ar, ai = tab.t[:, 0, :, 0:n], tab.t[:, 1, :, 0:n]
                sr = tab.t[:, 0, :, n - 1:n].to_broadcast([128, 16, n])
                si = tab.t[:, 1, :, n - 1:n].to_broadcast([128, 16, n])
                tA, tB = tmpA.t[:, :, 0:n], tmpB.t[:, :, 0:n]
                orr, oi = tab.t[:, 0, :, n:2 * n], tab.t[:, 1, :, n:2 * n]
                ops = [(tA, ar, sr, ALU.mult), (tB, ai, si, ALU.mult), (orr, tA, tB, ALU.subtract),
                       (tA, ar, si, ALU.mult), (tB, ai, sr, ALU.mult), (oi, tA, tB, ALU.add)]
                for (o, a, b, op) in ops:
                    S.op("dve", lambda e, o=o, a=a, b=b, op=op: TT(e, o, a, b, op), reads=tb, writes=tb[0:1] + tb[2:4])
                n *= 2
        build_pow(Ptab, W(8), W(9))
        build_pow(Qtab, W(16), W(17))
        tq = [Qtab.b, sw.b, tmpA.b, tmpB.b]
        for half in range(2):
            hs = slice(half * (T5 // 2), (half + 1) * (T5 // 2))
            qr, qi = Qtab.t[:, 0, :, hs], Qtab.t[:, 1, :, hs]
            fr = W(14).unsqueeze(2).to_broadcast([128, 16, T5 // 2])
            fi = W(15).unsqueeze(2).to_broadcast([128, 16, T5 // 2])
            ops = [(tmpA.t[:], qr, fr, ALU.mult), (tmpB.t[:], qi, fi, ALU.mult), ("R", tmpA.t[:], tmpB.t[:], ALU.subtract),
                   (tmpA.t[:], qr, fi, ALU.mult), (tmpB.t[:], qi, fr, ALU.mult), (qi, tmpA.t[:], tmpB.t[:], ALU.add)]
            for (o, a, b, op) in ops:
                if isinstance(o, str):
                    o = qtmp.t[:, :, hs]
                S.op("dve", lambda e, o=o, a=a, b=b, op=op: TT(e, o, a, b, op), reads=tq + [qtmp.b], writes=tq + [qtmp.b])
            S.op("dve", lambda e, qr=qr, hs=hs: e.tensor_copy(out=qr, in_=qtmp.t[:, :, hs]), reads=[qtmp.b], writes=[Qtab.b])
        S.op("dve", lambda e: e.memset(mask32.t[:], 1.0), reads=[Qtab.b], writes=[mask32.b])
        S.op("dve", lambda e: e.memset(mask32.t[:, :, 0:1], 0.0), writes=[mask32.b])
        S.op("dve", lambda e: e.memset(mask4.t[:], 1.0), writes=[mask4.b])
        S.op("dve", lambda e: e.memset(mask4.t[:, :, 0:1], 0.0), writes=[mask4.b])
        S.op("dve", lambda e: e.memset(s5cr.t[:], 0.0), writes=[s5cr.b])
        dump("Ptab", Ptab.t[:].rearrange("p a s t -> p (a s t)"), [128, 2 * 16 * T5], [Ptab.b])
        dump("Qtab", Qtab.t[:].rearrange("p a s t -> p (a s t)"), [128, 2 * 16 * T5], [Qtab.b])

        ckpt("setup0")
        NTM = 256
        xtm = A.alloc("xtm", [128, 2, D], F32)
        xn = A.alloc("xn", [128, 2, D], BF16)
        nstat = A.alloc("nstat", [128, 4], F32)
        uT = A.alloc("uT", [128, 8, NTM], BF16)
        xpad = A.alloc("xpad", [128, 8, NTM + 4], BF16)
        xtail = A.alloc("xtail", [128, 8, 64], F32)
        dgc = A.alloc("dgc", [128, 8, 4, 128], BF16)
        for ct_ in range(8):
            for k_ in range(4):
                S.op("act", lambda e, ct_=ct_, k_=k_: e.activation(
                    out=dgc.t[:, ct_, k_, :], in_=ident, func=AF.Copy,
                    scale=prm.t[:, P_CONV + 5 * ct_ + k_:P_CONV + 5 * ct_ + k_ + 1]), reads=[cst.b, prm.b], writes=[dgc.b])
        xsT = A.alloc("xsT", [128, 4, NTM], F32)
        BCT = A.alloc("BCT", [128, 4, NTM], BF16)
        szT = A.alloc("szT", [128, 4, NTM], BF16)
        u5Ts = [A.alloc("u5T%d" % i, [128, 4, NTM], BF16) for i in range(2)]
        dtT = A.alloc("dtT", [8, 2, NTM], F32)
        cacc = [A.alloc("cacc0", [128, NTM], F32)] * 2
        y5pre = A.alloc("y5pre", [128, 4, NTM], F32)
        g5 = A.alloc("g5", [128, 4, NTM], BF16)
        sgl = A.alloc("sgl", [128, NTM], F32)
        dtm_l = [A.alloc("dtm%d" % i, [128, 16], F32) for i in range(2)]
        acs_l = [A.alloc("acs%d" % i, [128, 8], F32) for i in range(2)]
        dec_l = [A.alloc("dec%d" % i, [128, 8], F32) for i in range(2)]
        dtdec_l = [A.alloc("dtdec%d" % i, [128, 8], F32) for i in range(2)]
        Xtm = A.alloc("Xtm", [128, 8, 64], BF16)
        Xdec = A.alloc("Xdec", [128, 8, 64], BF16)
        Btm = A.alloc("Btm", [128, 2, 128], BF16)
        big1 = A.alloc("big1", [128, 8, 128], F32)
        big2 = A.alloc("big2", [128, 8, 128], F32)
        MT = A.alloc("MT", [128, 8, 128], BF16)
        eA = A.alloc("eA", [128, 8, 128], F32)
        CdT = A.alloc("CdT", [128, 8, 128], BF16)
        ST = A.alloc("ST", [128, 8, 64], F32)
        STb = A.alloc("STb", [128, 8, 64], BF16)
        sts5 = alias("sts5", ST.t[:].rearrange("p h q -> p (h q)").rearrange("p (a s q) -> p a s q", a=2, s=16), ST.b)
        yg = A.alloc("yg", [128, 4, 128], F32)
        ysq = alias("ysq", big1.t[:, 4:8, :], big1.b)
        rsb = A.alloc("rsb", [128, 2, 128], F32)
        ysqb = A.alloc("ysqb", [128, 4, 128], BF16)
        onesb1 = A.alloc("onesb1", [128, 128], BF16)
        S.op("dve", lambda e: e.memset(onesb1.t[:], 1.0), writes=[onesb1.b])
        h0n = [alias("h0n0", xtm.t[:, 1, 0:512].rearrange("p (a n) -> p a n", n=128), xtm.sub(1)),
               alias("h0n1", xtm.t[:, 0, 0:512].rearrange("p (a n) -> p a n", n=128), xtm.sub(0))]
        h0T = [A.alloc("h0T%d" % i, [128, 8, 64], BF16) for i in range(2)]
        Bj = [A.alloc("Bj%d" % i, [128, 2, 128], BF16) for i in range(2)]
        hn = [alias("hn0", xtm.t[:, 1, 512:1024].rearrange("p (a n) -> p a n", n=128), xtm.sub(1)),
              alias("hn1", xtm.t[:, 0, 512:1024].rearrange("p (a n) -> p a n", n=128), xtm.sub(0))]
        decfm = A.alloc("decfm", [128, 4, 16], F32)
        dAx = alias("dAx", big1.t[:, 0:4, :].rearrange("p a (b c) -> p (a b) c", c=64), big1.b)
        s5g = [[A.alloc("s5g%d%d" % (j, i), [128, 512], F32) for i in range(2)] for j in range(2)]
        s5t34 = [A.alloc("s5t%d" % i, [128, 512], F32) for i in (2, 3)]
        s5vb = [A.alloc("s5vb%d" % i, [128, 512], F32) for i in range(2)]
        s5k = [0]
        s5h = [[A.alloc("s5h%d%d" % (j, i), [128, 512], BF16) for i in range(4)] for j in range(2)]
        s5CTn = A.alloc("s5CTn", [128, 16, 32], BF16)
        s5c = A.alloc("s5c", [128, 4, 16], F32)
        busd = [[A.alloc("bus%d%d" % (j, i), [128, 512], F32) for i in range(2)] for j in range(2)]
        dg5 = A.alloc("dg5", [128, 4, 128], BF16)
        for q_ in range(4):
            S.op("act", lambda e, q_=q_: e.activation(out=dg5.t[:, q_, :], in_=ident, func=AF.Copy,
                                                      scale=prm.t[:, P_S5M + q_:P_S5M + q_ + 1]),
                 reads=[cst.b, prm.b], writes=[dg5.b])
        S.op("dve", lambda e: e.tensor_scalar(out=s5CTn.t[:], in0=s5CT.t[:, 0], scalar1=-1.0, scalar2=None, op0=ALU.mult),
             reads=[s5CT.b], writes=[s5CTn.b])
        print("arena after p1a allocs: lo=%d hi=%d (words)" % (A.lo, A.hi))

        S.op("dve", lambda e: e.memset(xpad.t[:, :, 0:3], 0.0), writes=[xpad.b])
        S.op("dve", lambda e: e.memset(ST.t[:], 0.0), writes=[ST.b])
        S.op("dve", lambda e: e.memset(STb.t[:], 0.0), writes=[STb.b])

        import os as _os3
        ENG_OUTROT = _os3.environ.get("K_OUTROT", "dve")
        ENG_ADDS = _os3.environ.get("K_ADDS", "dve")
        TILES_A = [(i * 256, 256, False) for i in range(8)] + [(SEQ, 64, True)]

        def load_x(ti):
            t0, NT, is_s = TILES_A[ti]
            for blk in range((NT + 127) // 128):
                rows = min(128, NT - blk * 128)
                S.dma("sp", xtm.t[0:rows, blk, :], xin[t0 + blk * 128:t0 + blk * 128 + rows, :], writes=[xtm.sub(blk)])

        a1 = lambda kt: amod.t[:, kt, 0:1]
        sh1 = lambda kt: mod.t[:, 8 * MOD_SH1 + kt, 0:1]
        cw = lambda ct, k: prm.t[:, P_CONV + 5 * ct + k:P_CONV + 5 * ct + k + 1]
        IN_CHUNKS = [("dt", 0, 1536, 8)] + [("z", i, i * 128, 128) for i in range(4)] + \
                    [("xbc", i, 512 + i * 128, 128) for i in range(8)] + [("u5", i, 1544 + i * 128, 128) for i in range(4)]

        load_x(0)
        pbi = [0]

        def next_pb():
            pbi[0] ^= 1
            return PB[pbi[0]]

        ckpt("pre")
        def chain1(ti):
            t0, NT, is_s = TILES_A[ti]
            u5T = u5Ts[ti % 2]
            nblk = (NT + 127) // 128
            T = 128 if not is_s else 64
            tri = cst.t[0:T, C_TRI:C_TRI + T] if not is_s else cst.t[0:T, C_TRI64:C_TRI64 + T]
            neg = cst.t[0:T, C_NEG:C_NEG + T] if not is_s else cst.t[0:T, C_NEG64:C_NEG64 + T]
            sego = onesf.t[0:T, 0:T] if not is_s else cst.t[0:T, C_SEG64:C_SEG64 + T]
            segi = cst.t[0:64, C_SEGI:C_SEGI + 16]

            def dt_prep(ck):
                c0 = ck * T
                cs_ = slice(c0, c0 + T)
                dtm, acs, dec, dtdec = dtm_l[ck], acs_l[ck], dec_l[ck], dtdec_l[ck]
                pc = 0 if ck == 0 else 480
                S.op("pe", lambda e: e.transpose(PB[4].t[0:T, pc:pc + 8], dtT.t[:, 0, cs_], cst.t[0:8, C_ID:C_ID + 8]),
                     reads=[dtT.b, cst.b], writes=[PB[4].sub("sm")])
                S.op("pe", lambda e: e.transpose(PB[4].t[0:T, pc + 8:pc + 16], dtT.t[:, 1, cs_], cst.t[0:8, C_ID:C_ID + 8]),
                     reads=[dtT.b, cst.b], writes=[PB[4].sub("sm")])
                S.op("act", lambda e: e.activation(out=dtm.t[0:T, :], in_=PB[4].t[0:T, pc:pc + 16], func=AF.Copy),
                     reads=[PB[4].sub("sm")], writes=[dtm.b])
                S.op("pe", lambda e: e.matmul(PB[4].t[0:T, pc + 16:pc + 24], tri, dtm.t[0:T, 8:16], start=True, stop=True),
                     reads=[dtm.b, cst.b], writes=[PB[4].sub("sm")])
                S.op("pe", lambda e: e.matmul(PB[4].t[0:T, pc + 24:pc + 32], sego, dtm.t[0:T, 8:16], start=True, stop=True),
                     reads=[dtm.b, cst.b, onesf.b], writes=[PB[4].sub("sm")])
                S.op("act", lambda e: e.activation(out=acs.t[0:T, :], in_=PB[4].t[0:T, pc + 16:pc + 24], func=AF.Copy),
                     reads=[PB[4].sub("sm")], writes=[acs.b])
                S.op("dve", lambda e: TT(e, dec.t[0:T, :], PB[4].t[0:T, pc + 24:pc + 32], acs.t[0:T, :], ALU.subtract),
                     reads=[PB[4].sub("sm"), acs.b], writes=[dec.b])
                S.op("act", lambda e: e.activation(out=dec.t[0:T, :], in_=dec.t[0:T, :], func=AF.Exp), reads=[dec.b], writes=[dec.b])
                S.op("dve", lambda e: TT(e, dtdec.t[0:T, :], dtm.t[0:T, 0:8], dec.t[0:T, :], ALU.mult),
                     reads=[dtm.b, dec.b], writes=[dtdec.b])
            for blk in range(nblk):
                rows = min(128, NT - blk * 128)
                xb = xtm.sub(blk)
                S.op("act", lambda e, blk=blk, rows=rows: e.activation(
                    out=xn.t[0:rows, blk, :], in_=xtm.t[0:rows, blk, :], func=AF.Square, accum_out=nstat.t[0:rows, blk:blk + 1]),
                    reads=[xb], writes=[xn.sub(blk), nstat.sub(blk)])
                S.op("act", lambda e, blk=blk, rows=rows: e.activation(
                    out=nstat.t[0:rows, 2 + blk:3 + blk], in_=nstat.t[0:rows, blk:blk + 1], func=AF.Ln, scale=1.0 / D, bias=EPS),
                    reads=[nstat.sub(blk)], writes=[nstat.sub(blk)])
                S.op("act", lambda e, blk=blk, rows=rows: e.activation(out=nstat.t[0:rows, 2 + blk:3 + blk],
                                                                        in_=nstat.t[0:rows, 2 + blk:3 + blk], func=AF.Exp, scale=-0.5),
                     reads=[nstat.sub(blk)], writes=[nstat.sub(blk)])
                S.op("act", lambda e, blk=blk, rows=rows: e.activation(
                    out=xn.t[0:rows, blk, :], in_=xtm.t[0:rows, blk, :], func=AF.Copy, scale=nstat.t[0:rows, 2 + blk:3 + blk]),
                    reads=[xb, nstat.sub(blk)], writes=[xn.sub(blk)])
            ckpt("Aa%d" % ti)
            if ti + 1 < len(TILES_A):
                load_x(ti + 1)
            ckpt("Ab%d" % ti)
            for kt in range(8):
                xb_ = 2 + (kt % 2)
                pslot = PB[xb_].b
                for blk in range(nblk):
                    rows = min(128, NT - blk * 128)
                    S.op("pe", lambda e, kt=kt, blk=blk, rows=rows: e.transpose(
                        pbf(xb_)[:, blk * 128:blk * 128 + rows],
                        xn.t[0:rows, blk, kt * 128:(kt + 1) * 128], identb.t[0:rows, 0:rows]),
                        reads=[xn.sub(blk), identb.b], writes=[pslot])
                src = pbf(xb_)[:, 0:NT]
                if not is_s:
                    S.op("act", lambda e, kt=kt, src=src: e.activation(out=uT.t[:, kt, 0:NT], in_=src, func=AF.Identity,
                                                                       scale=a1(kt), bias=sh1(kt)),
                         reads=[pslot, amod.b, mod.b], writes=[uT.sub(kt)])
                else:
                    S.op("dve", lambda e, kt=kt, src=src: TT(e, cacc[0].t[:, 0:NT], src, a1x.t[:, kt, :], ALU.mult),
                         reads=[pslot, a1x.b], writes=[cacc[0].b])
                    S.op("dve", lambda e, kt=kt: TT(e, uT.t[:, kt, 0:NT], cacc[0].t[:, 0:NT], sh1x.t[:, kt, :], ALU.add),
                         reads=[cacc[0].b, sh1x.b], writes=[uT.sub(kt)])
            ckpt("A%d" % ti)
            if ti == 0:
                dump("uT", uT.t[:].rearrange("p k t -> p (k t)"), [128, 8 * NTM], uT.allb())

            yield
            if is_s:
                xps = xpad.t[:, :, 0:NS * 7].rearrange("p c (s k) -> p c s k", k=7)
                scv = stconv_d.rearrange("p (c s k) -> p c s k", s=NS, k=3)
                for ct in range(8):
                    S.dma("pool", xps[:, ct, :, 0:3], scv[:, ct], writes=[xpad.b])
            for (kind, i, c0, M) in IN_CHUNKS:
                yield
                pb = next_pb()
                for kt in range(8):
                    S.op("pe", lambda e, kt=kt, c0=c0, M=M, pb=pb: e.matmul(
                        pb.t[0:M, 0:NT], win_sb.t[:, kt, c0:c0 + M], uT.t[:, kt, 0:NT], start=(kt == 0), stop=(kt == 7)),
                        reads=[win_sb.b, uT.sub(kt)], writes=[pb.b])
                if kind == "z":
                    S.op("act", lambda e, i=i, pb=pb: e.activation(out=szT.t[:, i, 0:NT], in_=pb.t[:, 0:NT], func=AF.Silu),
                         reads=[pb.b], writes=[szT.b])
                elif kind == "xbc":
                    if not is_s:
                        S.op("act", lambda e, i=i, pb=pb: e.activation(out=xpad.t[:, i, 3:3 + NT], in_=pb.t[:, 0:NT], func=AF.Copy),
                             reads=[pb.b], writes=[xpad.b])
                        if ti == 7:
                            S.op("act", lambda e, i=i, pb=pb: e.activation(out=xtail.t[:, i, 0:3], in_=pb.t[:, NT - 3:NT], func=AF.Copy),
                                 reads=[pb.b], writes=[xtail.b])
                    else:
                        S.op("act", lambda e, i=i, pb=pb: e.activation(
                            out=xps[:, i, :, 3:7], in_=pb.t[:, 0:NT].rearrange("p (s k) -> p s k", k=LS), func=AF.Copy),
                            reads=[pb.b], writes=[xpad.b])
                        S.op("act", lambda e, i=i, pb=pb: e.activation(out=xtail.t[:, i, 0:NT], in_=pb.t[:, 0:NT], func=AF.Copy),
                             reads=[pb.b], writes=[xtail.b])
                elif kind == "dt":
                    S.op("act", lambda e, pb=pb: e.activation(out=dtT.t[:, 1, 0:NT], in_=pb.t[0:8, 0:NT], func=AF.Exp,
                                                              bias=ssd8.t[:, 0:1]), reads=[pb.b, ssd8.b], writes=[dtT.b])
                    S.op("act", lambda e: e.activation(out=dtT.t[:, 0, 0:NT], in_=dtT.t[:, 1, 0:NT], func=AF.Ln, bias=1.0),
                         reads=[dtT.b], writes=[dtT.b])
                    S.op("dve", lambda e: e.tensor_scalar(out=dtT.t[:, 1, 0:NT], in0=dtT.t[:, 0, 0:NT], scalar1=ssd8.t[:, 1:2],
                                                          scalar2=None, op0=ALU.mult), reads=[dtT.b, ssd8.b], writes=[dtT.b])
                    for ck_ in range(NT // T):
                        yield
                        dt_prep(ck_)
                else:
                    S.op("act", lambda e, i=i, pb=pb: e.activation(out=u5T.t[:, i, 0:NT], in_=pb.t[:, 0:NT], func=AF.Copy),
                         reads=[pb.b], writes=[u5T.b])

            ckpt("B%d" % ti)
            for ct in range(8):
                yield
                pb = next_pb()
                if not is_s:
                    xin_k = lambda k, ct=ct: xpad.t[:, ct, k:k + NT]
                    pbv = pb.t[:, 0:NT]
                    dst = xsT.t[:, ct, 0:NT] if ct < 4 else BCT.t[:, ct - 4, 0:NT]
                else:
                    xin_k = lambda k, ct=ct: xps[:, ct, :, k:k + LS]
                    pbv = pb.t[:, 0:NT].rearrange("p (s k) -> p s k", k=LS)
                    dst = (xsT.t[:, ct, 0:NT] if ct < 4 else BCT.t[:, ct - 4, 0:NT]).rearrange("p (s k) -> p s k", k=LS)
                for k in range(4):
                    S.op("pe", lambda e, k=k: e.matmul(pbv, dgc.t[:, ct, k, :], xin_k(k), start=(k == 0), stop=(k == 3)),
                         reads=[dgc.b, xpad.b], writes=[pb.b])
                S.op("act", lambda e: e.activation(out=dst, in_=pbv, func=AF.Silu, bias=cw(ct, 4)),
                     reads=[pb.b, prm.b], writes=[xsT.b if ct < 4 else BCT.b])
            ocv = o_conv.rearrange("p (c s k) -> p c s k", s=17, k=3)
            if is_s:
                for ct in range(8):
                    S.dma("sp", ocv[:, ct, 1:17, :], xtail.t[:, ct, :].rearrange("p (s k) -> p s k", k=LS)[:, :, 1:4], reads=[xtail.b], buf=xtail.b)
                outbufs.append(xtail.b)
            elif ti == 7:
                S.dma("sp", ocv[:, :, 0, :], xtail.t[:, :, 0:3], reads=[xtail.b], buf=xtail.b)
            if not is_s:
                S.op("dve", lambda e: e.tensor_copy(out=xpad.t[:, :, 0:3], in_=xpad.t[:, :, NT:NT + 3]),
                     reads=[xpad.b], writes=[xpad.b])
            if is_s:
                dump("xsS", xsT.t[:, :, 0:64], [128, 4, 64], [xsT.b])
                dump("ygS", yg.t[:, :, 0:64], [128, 4, 64], [yg.b])
            if ti == 0:
                dump("xsT", xsT.t[:].rearrange("p k t -> p (k t)"), [128, 4 * NTM], [xsT.b])
                dump("dtT", dtT.t[:].rearrange("p k t -> p (k t)"), [8, 2 * NTM], [dtT.b])

            ckpt("C%d" % ti)
            for ck in range(NT // T):
                c0 = ck * T
                cs_ = slice(c0, c0 + T)
                dtm, acs, dec, dtdec = dtm_l[ck], acs_l[ck], dec_l[ck], dtdec_l[ck]
                yield
                for pr in range(4):
                    S.op("pe", lambda e, pr=pr, cs_=cs_: e.transpose(PB[3].t[0:T, pr * 128:(pr + 1) * 128], xsT.t[:, pr, cs_], ident),
                         reads=[xsT.b, cst.b], writes=[PB[3].b])
                pxs = PB[3].t[0:T, :].rearrange("p (h q) -> p h q", q=64)
                S.op("dve", lambda e: TT(e, Xtm.t[0:T], pxs, dtm.t[0:T, 0:8].unsqueeze(2).to_broadcast([T, 8, 64]), ALU.mult),
                     reads=[PB[3].b, dtm.b], writes=[Xtm.b])
                S.op("dve", lambda e: TT(e, Xdec.t[0:T], pxs, dtdec.t[0:T, :].unsqueeze(2).to_broadcast([T, 8, 64]), ALU.mult),
                     reads=[PB[3].b, dtdec.b], writes=[Xdec.b])
                for g in range(2):
                    S.op("pe", lambda e, g=g, cs_=cs_: e.transpose(pbf(2)[0:T, g * 128:(g + 1) * 128], BCT.t[:, g, cs_], identb.t[:]),
                         reads=[BCT.b, identb.b], writes=[PB[2].sub(0)])
                S.op("act", lambda e: e.activation(out=Btm.t[0:T].rearrange("p g n -> p (g n)"), in_=pbf(2)[0:T, 0:256], func=AF.Copy),
                     reads=[PB[2].sub(0)], writes=[Btm.b])
                yield
                S.op("dve", lambda e: TT(e, big1.t[0:T, :, 0:T], tri.unsqueeze(1).to_broadcast([T, 8, T]),
                                         dtm.t[0:T, 8:16].unsqueeze(2).to_broadcast([T, 8, T]), ALU.mult),
                     reads=[cst.b, dtm.b], writes=[big1.b])
                for half in range(2):
                    S.op("pe", lambda e, half=half: e.matmul(
                        PB[3].t[:, 0:4 * T].rearrange("p (h l) -> p h l", l=T), onesf.t[0:T, :],
                        big1.t[0:T, 4 * half:4 * half + 4, 0:T], start=True, stop=True),
                        reads=[big1.b, onesf.b], writes=[PB[3].b])
                    for h in range(4 * half, 4 * half + 4):
                        S.op("dve", lambda e, h=h: e.scalar_tensor_tensor(
                            out=big2.t[0:T, h, 0:T], in0=PB[3].t[0:T, (h % 4) * T:(h % 4 + 1) * T], scalar=acs.t[0:T, h:h + 1],
                            in1=neg, op0=ALU.subtract, op1=ALU.min), reads=[PB[3].b, acs.b, cst.b], writes=[big2.b])
                    S.op("act", lambda e, half=half: e.activation(
                        out=eA.t[:, 4 * half:4 * half + 4, 0:T], in_=PB[3].t[:, 0:4 * T].rearrange("p (h l) -> p h l", l=T),
                        func=AF.Exp), reads=[PB[3].b], writes=[eA.b])
                    yield
                S.op("act", lambda e: e.activation(out=big2.t[0:T, :, 0:T], in_=big2.t[0:T, :, 0:T], func=AF.Exp),
                     reads=[big2.b], writes=[big2.b])
                yield
                for g in range(2):
                    S.op("pe", lambda e, g=g, cs_=cs_: e.matmul(PB[4].t[0:T, 32 + g * 128:32 + g * 128 + T], BCT.t[:, g, cs_],
                                                                 BCT.t[:, 2 + g, cs_], start=True, stop=True),
                         reads=[BCT.b], writes=[PB[4].sub("cb")])
                cbv = PB[4].t[0:T, 32:288].rearrange("p (g l) -> p g l", l=128)[:, :, 0:T]
                S.op("dve", lambda e: TT(e, MT.t[0:T, :, 0:T].rearrange("p (g h) l -> p g h l", h=4),
                                         cbv.unsqueeze(2).to_broadcast([T, 2, 4, T]),
                                         big2.t[0:T, :, 0:T].rearrange("p (g h) l -> p g h l", h=4), ALU.mult),
                     reads=[PB[4].sub("cb"), big2.b], writes=[MT.b])
                yield
                S.op("pool", lambda e, cs_=cs_: TT(e, CdT.t[:, :, 0:T].rearrange("p (g h) l -> p g h l", h=4),
                                                   BCT.t[:, 2:4, cs_].unsqueeze(2).to_broadcast([128, 2, 4, T]),
                                                   eA.t[:, :, 0:T].rearrange("p (g h) l -> p g h l", h=4), ALU.mult),
                     reads=[BCT.b, eA.b], writes=[CdT.b])
                yield
                ypb = PB[7]
                if is_s:
                    S.op("dve", lambda e: e.tensor_copy(out=dAx.t[0:T], in_=dtm.t[0:T, 8:16].unsqueeze(2).to_broadcast([T, 8, 64])),
                         reads=[dtm.b], writes=[dAx.b])
                    for pr in range(4):
                        S.op("pe", lambda e, pr=pr: e.matmul(PB[4].t[:, 288 + pr * 16:288 + (pr + 1) * 16],
                                                             dAx.t[0:T, 2 * pr:2 * pr + 2, :], segi, start=True, stop=True),
                             reads=[dAx.b, cst.b], writes=[PB[4].sub("dec")])
                    S.op("act", lambda e: e.activation(out=decfm.t[:].rearrange("p a s -> p (a s)"), in_=PB[4].t[:, 288:352], func=AF.Exp),
                         reads=[PB[4].sub("dec")], writes=[decfm.b])
                    stv = stssd_d.rearrange("j (pr hl) p n -> j (hl p) pr n", hl=2)
                    osv = o_ssds.rearrange("j (pr hl) p n -> j (hl p) pr n", hl=2)
                    S.dma("sp", h0n[0].t[:], stv[0], writes=[h0n[0].b])
                    for j in range(NS):
                        yield
                        jj = j % 2
                        if j + 1 < NS:
                            S.dma("sp", h0n[1 - jj].t[:], stv[j + 1], writes=[h0n[1 - jj].b])
                        pbt = PB[jj]
                        for pr in range(4):
                            S.op("pe", lambda e, pr=pr, jj=jj, pbt=pbt: e.transpose(pbt.t[:, pr * 128:(pr + 1) * 128], h0n[jj].t[:, pr, :], ident),
                                 reads=[h0n[jj].b, cst.b], writes=[pbt.b])
                        S.op("act", lambda e, jj=jj, pbt=pbt: e.activation(out=h0T[jj].t[:].rearrange("p h q -> p (h q)"), in_=pbt.t[:, :], func=AF.Copy),
                             reads=[pbt.b], writes=[h0T[jj].b])
                        for h in range(8):
                            pr, hl = h // 2, h % 2
                            S.op("pe", lambda e, h=h, pr=pr, hl=hl, jj=jj, j=j: e.matmul(
                                ypb.t[64 * hl:64 * hl + 64, pr * T + LS * j:pr * T + LS * j + LS], h0T[jj].t[:, h, :],
                                CdT.t[:, h, LS * j:LS * j + LS], start=(j == 0 and pr == 0), stop=False, skip_group_check=True),
                                reads=[h0T[jj].b, CdT.b], writes=[ypb.b])
                        S.op("dve", lambda e, jj=jj, j=j: e.tensor_scalar(out=Bj[jj].t[0:T], in0=Btm.t[0:T], scalar1=segi[:, j:j + 1],
                                                                          scalar2=None, op0=ALU.mult),
                             reads=[Btm.b, cst.b], writes=[Bj[jj].b])
                        pby = PB[3]
                        for pr in range(4):
                            S.op("pe", lambda e, pr=pr, jj=jj, pby=pby: e.matmul(
                                pby.t[:, pr * 128:(pr + 1) * 128], Xdec.t[0:T, 2 * pr:2 * pr + 2, :], Bj[jj].t[0:T, pr // 2, :],
                                start=True, stop=True), reads=[Xdec.b, Bj[jj].b], writes=[pby.b])
                        S.op("dve", lambda e, jj=jj, j=j: TT(e, hn[jj].t[:], h0n[jj].t[:],
                                                             decfm.t[:, :, j:j + 1].to_broadcast([128, 4, 128]), ALU.mult),
                             reads=[h0n[jj].b, decfm.b], writes=[hn[jj].b])
                        S.op("dve", lambda e, jj=jj, pby=pby: TT(e, hn[jj].t[:], hn[jj].t[:],
                                                                 pby.t[:, :].rearrange("p (a n) -> p a n", n=128), ALU.add),
                             reads=[hn[jj].b, pby.b], writes=[hn[jj].b])
                        S.dma("sp", osv[j], hn[jj].t[:], reads=[hn[jj].b], buf=hn[jj].b)
                    outbufs.extend([hn[0].b, hn[1].b])
                for h in range(8):
                    pr, hl = h // 2, h % 2
                    out = ypb.t[64 * hl:64 * hl + 64, pr * T:(pr + 1) * T]
                    S.op("pe", lambda e, h=h, out=out, pr=pr: e.matmul(out, Xtm.t[0:T, h, :], MT.t[0:T, h, 0:T],
                                                                       start=(pr == 0 and not is_s), stop=is_s, skip_group_check=True),
                         reads=[Xtm.b, MT.b], writes=[ypb.b])
                    if not is_s:
                        S.op("pe", lambda e, h=h, out=out: e.matmul(out, STb.t[:, h, :], CdT.t[:, h, 0:T], start=False, stop=True,
                                                                    skip_group_check=True),
                             reads=[STb.b, CdT.b], writes=[ypb.b])
                yield
                for pr in range(4):
                    S.op("dve", lambda e, pr=pr, cs_=cs_: e.scalar_tensor_tensor(
                        out=yg.t[:, pr, 0:T], in0=xsT.t[:, pr, cs_], scalar=prm.t[:, P_SSDFM + pr:P_SSDFM + pr + 1],
                        in1=ypb.t[:, pr * T:(pr + 1) * T], op0=ALU.mult, op1=ALU.add),
                        reads=[xsT.b, prm.b, ypb.b], writes=[yg.b])
                S.op("dve", lambda e, cs_=cs_: TT(e, yg.t[:, :, 0:T], yg.t[:, :, 0:T], szT.t[:, :, cs_], ALU.mult),
                     reads=[yg.b, szT.b], writes=[yg.b])
                S.op("dve", lambda e: TT(e, ysqb.t[:, :, 0:T], yg.t[:, :, 0:T], yg.t[:, :, 0:T], ALU.mult),
                     reads=[yg.b], writes=[ysqb.b])
                for g in range(2):
                    for k in range(2):
                        S.op("pe", lambda e, g=g, k=k: e.matmul(PB[3].t[:, g * T:(g + 1) * T], onesb1.t[:], ysqb.t[:, 2 * g + k, 0:T],
                                                                start=(k == 0), stop=(k == 1)),
                             reads=[onesb1.b, ysqb.b], writes=[PB[3].b])
                S.op("act", lambda e: e.activation(out=rsb.t[:, :, 0:T], in_=PB[3].t[:, 0:2 * T].rearrange("p (g l) -> p g l", l=T),
                                                   func=AF.Ln, scale=1.0 / 256, bias=EPS), reads=[PB[3].b], writes=[rsb.b])
                S.op("act", lambda e: e.activation(out=rsb.t[:, :, 0:T], in_=rsb.t[:, :, 0:T], func=AF.Exp, scale=-0.5),
                     reads=[rsb.b], writes=[rsb.b])
                for pr in range(4):
                    S.op("dve", lambda e, pr=pr: e.scalar_tensor_tensor(
                        out=mixt[ti % 2].t[:, pr, c0:c0 + T], in0=yg.t[:, pr, 0:T],
                        scalar=prm.t[:, P_SSDFM + 4 + pr:P_SSDFM + 5 + pr], in1=rsb.t[:, pr // 2, 0:T], op0=ALU.mult, op1=ALU.mult),
                        reads=[yg.b, prm.b, rsb.b], writes=[mixt[ti % 2].sub("ssd")])
                yield
                if not is_s:
                    for g in range(2):
                        S.op("pe", lambda e, g=g: e.matmul(PB[6].t[:, g * 256:(g + 1) * 256], Btm.t[0:T, g, :],
                                                           Xdec.t[0:T, 4 * g:4 * g + 4, :], start=True, stop=True),
                             reads=[Btm.b, Xdec.b], writes=[PB[6].b])
                    S.op("dve", lambda e: TT(e, ST.t[:], ST.t[:], eA.t[:, :, T - 1:T].to_broadcast([128, 8, 64]), ALU.mult),
                         reads=[ST.b, eA.b], writes=[ST.b])
                    S.op("dve", lambda e: TT(e, ST.t[:], ST.t[:], PB[6].t[:, :].rearrange("p (h q) -> p h q", q=64), ALU.add),
                         reads=[ST.b, PB[6].b], writes=[ST.b])
                    S.op("act", lambda e: e.activation(out=STb.t[:], in_=ST.t[:], func=AF.Copy), reads=[ST.b], writes=[STb.b])
            if ti == 7:
                S.dma("sp", o_ssdp, ST.t[:].rearrange("p h q -> p (h q)"), reads=[ST.b], buf=ST.b)
                outbufs.append(ST.b)

            ckpt("D%d" % ti)
            yield

        def chain2(ti):
            t0, NT, is_s = TILES_A[ti]
            u5T = u5Ts[ti % 2]
            if is_s:
                S.dma("sp", sts5.t[:].rearrange("p a s q -> p (a s q)"), sts5_d, writes=[sts5.b])
            if not is_s:
                groups = [(list(range(16)), k * T5, T5) for k in range(NT // T5)]
            else:
                groups = [(list(range(8)), 0, 64), (list(range(8, 16)), 0, 64)]
            def emit_bu(g_):
                slist_, tk0_, ntok_ = groups[g_]
                bus = busd[g_ % 2]
                for part, pb in ((0, PB[5]), (1, PB[6])):
                    for idx, s in enumerate(slist_):
                        S.op("pe", lambda e, part=part, pb=pb, idx=idx, s=s: e.matmul(
                            pb.t[:, idx * ntok_:(idx + 1) * ntok_], s5BT.t[:, part, s, :], u5T.t[:, s // 4, tk0_:tk0_ + ntok_],
                            start=True, stop=True), reads=[s5BT.b, u5T.b], writes=[pb.b])
                S.op("act", lambda e: e.activation(out=bus[0].t[:], in_=PB[5].t[:, :], func=AF.Copy), reads=[PB[5].b], writes=[bus[0].b])
                S.op("act", lambda e: e.activation(out=bus[1].t[:], in_=PB[6].t[:, :], func=AF.Copy), reads=[PB[6].b], writes=[bus[1].b])
            def views(g_):
                slist_, tk0_, ntok_ = groups[g_]
                s0_ = slist_[0]
                if not is_s:
                    V3 = lambda ap: ap.rearrange("p (s t) -> p s t", t=T5)
                    QR, QI = Qtab.t[:, 0], Qtab.t[:, 1]
                    PR_, PI_ = Ptab.t[:, 0], Ptab.t[:, 1]
                    msk = mask32.t[:].rearrange("p s t -> p (s t)")
                    first = lambda ap: V3(ap)[:, :, 0]
                    cin_r, cin_i = s5cr.t[:, 0, :], s5cr.t[:, 1, :]
                else:
                    V3 = lambda ap: ap.rearrange("p (s q b) -> p s q b", q=NS, b=LS)
                    bc = lambda ap: ap.unsqueeze(2).to_broadcast([128, 8, NS, LS])
                    QR, QI = bc(Qtab.t[:, 0, s0_:s0_ + 8, 0:LS]), bc(Qtab.t[:, 1, s0_:s0_ + 8, 0:LS])
                    PR_, PI_ = bc(Ptab.t[:, 0, s0_:s0_ + 8, 0:LS]), bc(Ptab.t[:, 1, s0_:s0_ + 8, 0:LS])
                    msk = mask4.t[:].rearrange("p s t -> p (s t)")
                    first = lambda ap: V3(ap)[:, :, :, 0]
                    cin_r, cin_i = sts5.t[:, 0, s0_:s0_ + 8, :], sts5.t[:, 1, s0_:s0_ + 8, :]
                return V3, QR, QI, PR_, PI_, msk, first, cin_r, cin_i
            vsets = [[s5v[0], s5v[1]], [s5vb[0], s5vb[1]]]

            def mults_adds(g_):
                V3, QR, QI, PR_, PI_, msk, first, cin_r, cin_i = views(g_)
                bus = busd[g_ % 2]
                br, bi = V3(bus[0].t[:]), V3(bus[1].t[:])
                t1, t2, t3, t4 = s5t[0], s5t[1], s5t34[0], s5t34[1]
                vr, vi = vsets[g_ % 2]
                tb = [Qtab.b]
                for (o, a, b_, rd) in ((t1, QR, br, bus[0].b), (t2, QI, bi, bus[1].b), (t3, QR, bi, bus[1].b), (t4, QI, br, bus[0].b)):
                    S.op("dve", lambda e, o=o, a=a, b_=b_: TT(e, V3(o.t[:]), a, b_, ALU.mult), reads=tb + [rd], writes=[o.b])
                S.op(ENG_ADDS, lambda e: TT(e, vr.t[:], t1.t[:], t2.t[:], ALU.subtract), reads=[t1.b, t2.b], writes=[vr.b])
                S.op(ENG_ADDS, lambda e: TT(e, vi.t[:], t3.t[:], t4.t[:], ALU.add), reads=[t3.b, t4.b], writes=[vi.b])
            emit_bu(0)
            if len(groups) > 1:
                emit_bu(1)
            mults_adds(0)
            pend_y5 = [None]
            for gi_, (slist, tk0, ntok) in enumerate(groups):
                yield
                ns = len(slist)
                s0 = slist[0]
                V3, QR, QI, PR_, PI_, msk, first, cin_r, cin_i = views(gi_)
                vr, vi = vsets[gi_ % 2]
                if gi_ + 1 < len(groups):
                    mults_adds(gi_ + 1)
                    yield
                if gi_ + 2 < len(groups):
                    emit_bu(gi_ + 2)
                S.op("dve", lambda e: TT(e, first(vr.t[:]), first(vr.t[:]), cin_r, ALU.add), reads=[vr.b, s5cr.b, sts5.b], writes=[vr.b])
                S.op("dve", lambda e: TT(e, first(vi.t[:]), first(vi.t[:]), cin_i, ALU.add), reads=[vi.b, s5cr.b, sts5.b], writes=[vi.b])
                yield
                s5k[0] ^= 1
                gr, gi2 = s5g[s5k[0]][0], s5g[s5k[0]][1]
                S.op("dve", lambda e: e.tensor_tensor_scan(out=gr.t[:], data0=msk, data1=vr.t[:], initial=0.0, op0=ALU.mult, op1=ALU.add),
                     reads=[vr.b, mask32.b, mask4.b], writes=[gr.b])
                S.op("dve", lambda e: e.tensor_tensor_scan(out=gi2.t[:], data0=msk, data1=vi.t[:], initial=0.0, op0=ALU.mult, op1=ALU.add),
                     reads=[vi.b, mask32.b, mask4.b], writes=[gi2.b])
                yield
                hp = s5h[gi_ % 2]
                hr, hi = hp, hp
                for (o, a, b_) in ((hp[0], PR_, gr), (hp[1], PI_, gi2), (hp[2], PR_, gi2), (hp[3], PI_, gr)):
                    S.op(ENG_OUTROT, lambda e, o=o, a=a, b_=b_: TT(e, V3(o.t[:]), a, V3(b_.t[:]), ALU.mult),
                         reads=[Ptab.b, b_.b], writes=[o.b])
                yield
                if not is_s:
                    glr, gli = V3(gr.t[:])[:, :, T5 - 1], V3(gi2.t[:])[:, :, T5 - 1]
                    plr, pli = Ptab.t[:, 0, :, T5 - 1], Ptab.t[:, 1, :, T5 - 1]
                    c_ = lambda i: s5c.t[:, i, :]
                    outr, outi = s5cr.t[:, 0, :], s5cr.t[:, 1, :]
                else:
                    glr, gli = V3(gr.t[:])[:, :, :, LS - 1], V3(gi2.t[:])[:, :, :, LS - 1]
                    plr = Ptab.t[:, 0, s0:s0 + 8, LS - 1:LS].to_broadcast([128, 8, NS])
                    pli = Ptab.t[:, 1, s0:s0 + 8, LS - 1:LS].to_broadcast([128, 8, NS])
                    c_ = lambda i: hn[0].t[:, i, :].rearrange("p (s q) -> p s q", q=NS)
                    outr, outi = s5fin.t[:, 0, s0:s0 + 8, 1:17], s5fin.t[:, 1, s0:s0 + 8, 1:17]
                cb_ = [s5c.b, hn[0].b]
                cseq = [(c_(0), plr, glr, ALU.mult), (c_(1), pli, gli, ALU.mult), (c_(2), plr, gli, ALU.mult), (c_(3), pli, glr, ALU.mult)]
                for (o, a, b, op) in cseq:
                    S.op("dve", lambda e, o=o, a=a, b=b, op=op: TT(e, o, a, b, op), reads=[Ptab.b, gr.b, gi2.b] + cb_, writes=cb_)
                S.op("dve", lambda e: TT(e, outr, c_(0), c_(1), ALU.subtract), reads=cb_, writes=[s5cr.b, s5fin.b])
                S.op("dve", lambda e: TT(e, outi, c_(2), c_(3), ALU.add), reads=cb_, writes=[s5cr.b, s5fin.b])
                yield
                def emit_y5(gi_=gi_, slist=slist, tk0=tk0, ntok=ntok, hr=hr, hi=hi):
                    y5c0 = 352
                    nq = 4 if not is_s else 2
                    for qi in range(nq):
                        q = qi if not is_s else 2 * gi_ + qi
                        S.op("pe", lambda e, q=q, qi=qi: e.matmul(PB[4].t[:, y5c0 + qi * ntok:y5c0 + (qi + 1) * ntok], dg5.t[:, q, :],
                                                                  u5T.t[:, q, tk0:tk0 + ntok], start=(qi == 0), stop=False, skip_group_check=True),
                             reads=[dg5.b, u5T.b], writes=[PB[4].sub("y5")])
                    for idx, s in enumerate(slist):
                        qi = (s // 4) if not is_s else (s // 4 - 2 * gi_)
                        out = PB[4].t[32 * (s % 4):32 * (s % 4) + 32, y5c0 + qi * ntok:y5c0 + (qi + 1) * ntok]
                        for j4, lw in enumerate((s5CT.t[:, 0, s, :], s5CTn.t[:, s, :], s5CT.t[:, 1, s, :], s5CT.t[:, 1, s, :])):
                            S.op("pe", lambda e, j4=j4, lw=lw: e.matmul(out, lw, hr[j4].t[:, idx * ntok:(idx + 1) * ntok],
                                                                        start=False, stop=(j4 == 3), skip_group_check=True,
                                                                        tile_position=(0, 32 * (s % 4))),
                                 reads=[s5CT.b, s5CTn.b, hr[j4].b], writes=[PB[4].sub("y5")])
                    q0 = 0 if not is_s else 2 * gi_
                    S.op("act", lambda e: e.activation(out=y5pre.t[:, q0:q0 + nq, tk0:tk0 + ntok],
                                                       in_=PB[4].t[:, y5c0:y5c0 + nq * ntok].rearrange("p (q t) -> p q t", t=ntok), func=AF.Copy),
                         reads=[PB[4].sub("y5")], writes=[y5pre.b])
                if pend_y5[0] is not None:
                    pend_y5[0]()
                    yield
                pend_y5[0] = emit_y5
            if pend_y5[0] is not None:
                pend_y5[0]()
                pend_y5[0] = None
                yield
            if ti == 7:
                S.op("dve", lambda e: e.tensor_copy(out=s5fin.t[:, :, :, 0], in_=s5cr.t[:]), reads=[s5cr.b], writes=[s5fin.b])
            if is_s:
                S.dma("sp", o_s5, s5fin.t[:].rearrange("p a s q -> p (a s q)"), reads=[s5fin.b], buf=s5fin.b)
                outbufs.append(s5fin.b)
            if ti == 0:
                dump("y5pre", y5pre.t[:].rearrange("p k t -> p (k t)"), [128, 4 * NTM], [y5pre.b])
            ckpt("E%d" % ti)
            yield
            S.op("act", lambda e: e.activation(out=g5.t[:, :, 0:NT], in_=y5pre.t[:, :, 0:NT], func=AF.Gelu), reads=[y5pre.b], writes=[g5.b])
            for m in range(4):
                yield
                pb = next_pb()
                for q in range(4):
                    S.op("pe", lambda e, m=m, q=q, pb=pb: e.matmul(pb.t[:, 0:NT], wglu_sb.t[:, q, m * 128:(m + 1) * 128], g5.t[:, q, 0:NT],
                                                                   start=(q == 0), stop=(q == 3)),
                         reads=[wglu_sb.b, g5.b], writes=[pb.b])
                S.op("act", lambda e, m=m, pb=pb: e.activation(out=sgl.t[:, 0:NT], in_=pb.t[:, 0:NT], func=AF.Sigmoid,
                                                               bias=prm.t[:, P_S5M + 4 + m:P_S5M + 5 + m]),
                     reads=[pb.b, prm.b], writes=[sgl.b])
                S.op("dve", lambda e, m=m: TT(e, mixt[ti % 2].t[:, 4 + m, 0:NT], g5.t[:, m, 0:NT], sgl.t[:, 0:NT], ALU.mult),
                     reads=[g5.b, sgl.b], writes=[mixt[ti % 2].sub("s5")])
            S.dma("sp", mixd[:, :, t0:t0 + NT], mixt[ti % 2].t[:, :, 0:NT], reads=mixt[ti % 2].allb(), writes=[mixdb[ti]], buf=mixdb[ti])
            ckpt("T%d" % ti)
            if ti == 0:
                dump("mix0", mixt[0].t[:, :, 0:NTM], [128, 8, NTM], mixt[0].allb())
            yield

        import os as _os
        RATIO = int(_os.environ.get("K_RATIO", "1"))

        def drive(gens, ada_every=0):
            gens = [g for g in gens if g is not None]
            n = 0
            while gens:
                for gi__, g in enumerate(list(gens)):
                    for _ in range((RATIO if gi__ == 0 else 1) if RATIO > 0 else (-RATIO if gi__ == 1 else 1)):
                        try:
                            next(g)
                        except StopIteration:
                            if g in gens:
                                gens.remove(g)
                            break
                n += 1
                if ada_every and n % ada_every == 0:
                    ada_step()
        ada_state[0] = 0
        drive([chain1(0)], ada_every=12)
        for ti_ in range(len(TILES_A)):
            if ti_ == 7:
                while ada_state[1] < len(ADA_CH):
                    ada_step()
                fill_x(a1x, amod.t[:, 0:8, 1:17], [amod.b])
                fill_x(sh1x, chunkmod(MOD_SH1)[:, :, 1:17], [mod.b])
                make_amod([(1, (4, 1)), (2, (7, 2))])
            drive([chain2(ti_), chain1(ti_ + 1) if ti_ + 1 < len(TILES_A) else None], ada_every=(10 if ti_ < 7 else 0))
        dump("mixS", mixt[0].t[:, :, 0:64], [128, 8, 64], mixt[0].allb())
        S.barrier()
        ckpt("1a")
        A.lo = LO_P1
        x1T = A.alloc("x1T", [128, 8, NTOK], F32, top=True)
        vT = A.alloc("vT", [128, 8, NTOK], BF16, top=True)
        wout_sb = A.alloc("wout_sb", [128, 8, D], BF16)
        wout_v = wout.rearrange("(kt p) n -> p kt n", p=128)
        for kh in range(4):
            S.dma("pool", wout_sb.t[:, 2 * kh:2 * kh + 2, :], wout_v[:, 2 * kh:2 * kh + 2, :], writes=[wout_sb.b])
        mixb = [A.alloc("mixb%d" % i, [128, 8, 512], BF16) for i in range(2)]

        def load_mix(ti):
            t0, NT, is_s = TILES_B[ti]
            tiles_a = [i for i, (a0, n0, s0_) in enumerate(TILES_A) if a0 >= t0 and a0 < t0 + NT]
            S.dma("sp", mixb[ti % 2].t[:, :, 0:NT], mixd[:, :, t0:t0 + NT], reads=[mixdb[i] for i in tiles_a], writes=[mixb[ti % 2].b])
        xtm2 = A.alloc("xtm2", [128, 4, D], F32)
        xTm = [A.alloc("xTm%d" % i, [128, 512], F32) for i in range(2)]
        sqb = [A.alloc("sqb%d" % i, [128, 512], BF16) for i in range(2)]
        onesb = A.alloc("onesb", [128, 128], BF16)
        S.op("dve", lambda e: e.memset(onesb.t[:], 1.0), writes=[onesb.b])
        tmp2 = [A.alloc("tmp2_%d" % i, [128, 512], F32) for i in range(2)]
        rstdb = [A.alloc("rstdb%d" % i, [128, 512], F32) for i in range(2)]
        g1x = expand_mod("g1x", chunkmod(MOD_G1)[:, :, 1:17], [mod.b])
        a2x = expand_mod("a2x", amod.t[:, 8:16, 1:17], [amod.b])
        sh2x = expand_mod("sh2x", chunkmod(MOD_SH2)[:, :, 1:17], [mod.b])
        print("arena p1b: lo=%d hi=%d" % (A.lo, A.hi))
        TILES_B = [(i * 512, 512, False) for i in range(4)] + [(SEQ, 64, True)]

        def load_x2(ti):
            t0, NT, is_s = TILES_B[ti]
            for blk in range((NT + 127) // 128):
                rows = min(128, NT - blk * 128)
                S.dma("sp", xtm2.t[0:rows, blk, :], xin[t0 + blk * 128:t0 + blk * 128 + rows, :], writes=[xtm2.sub(blk)])
        load_x2(0)
        load_mix(0)

        def stat_accum(src_ap, m, NT, pbs):
            sq = sqb[m % 2]
            S.op("act", lambda e: e.activation(out=sq.t[:, 0:NT], in_=src_ap, func=AF.Square), reads=[x1T.sub(m)], writes=[sq.b])
            S.op("pe", lambda e: e.matmul(pbs.t[:, 0:NT], onesb.t[:], sq.t[:, 0:NT], start=(m == 0), stop=(m == 7)),
                 reads=[onesb.b, sq.b], writes=[pbs.b])

        def stat_finish(NT, pbs, rs):
            S.op("act", lambda e: e.activation(out=rs.t[:, 0:NT], in_=pbs.t[:, 0:NT], func=AF.Ln, scale=1.0 / D, bias=EPS),
                 reads=[pbs.b], writes=[rs.b])
            S.op("act", lambda e: e.activation(out=rs.t[:, 0:NT], in_=rs.t[:, 0:NT], func=AF.Exp, scale=-0.5), reads=[rs.b], writes=[rs.b])

        def b_part1(ti):
            t0, NT, is_s = TILES_B[ti]
            nblk = (NT + 127) // 128
            tsl = slice(t0, t0 + NT)
            pbs = PB[4 + ti % 2]
            for m in range(8):
                pbx = PB[2 + m % 2]
                xm = xTm[m % 2]
                for blk in range(nblk):
                    rows = min(128, NT - blk * 128)
                    S.op("pe", lambda e, blk=blk, rows=rows: e.transpose(
                        pbx.t[:, blk * 128:blk * 128 + rows], xtm2.t[0:rows, blk, m * 128:(m + 1) * 128], cst.t[0:rows, C_ID:C_ID + rows]),
                        reads=[xtm2.sub(blk), cst.b], writes=[pbx.b])
                S.op("act", lambda e: e.activation(out=xm.t[:, 0:NT], in_=pbx.t[:, 0:NT], func=AF.Copy), reads=[pbx.b], writes=[xm.b])
                pb = next_pb()
                for kt in range(8):
                    S.op("pe", lambda e, kt=kt: e.matmul(pb.t[:, 0:NT], wout_sb.t[:, kt, m * 128:(m + 1) * 128], mixb[ti % 2].t[:, kt, 0:NT],
                                                         start=(kt == 0), stop=(kt == 7)),
                         reads=[wout_sb.b, mixb[ti % 2].b], writes=[pb.b])
                if m == 0 and ti + 1 < len(TILES_B):
                    load_mix(ti + 1)
                if not is_s:
                    S.op("dve", lambda e: e.scalar_tensor_tensor(
                        out=x1T.t[:, m, tsl], in0=pb.t[:, 0:NT], scalar=mod.t[:, 8 * MOD_G1 + m, 0:1], in1=xm.t[:, 0:NT],
                        op0=ALU.mult, op1=ALU.add), reads=[pb.b, mod.b, xm.b], writes=[x1T.sub(m)])
                else:
                    S.op("dve", lambda e: TT(e, tmp2[0].t[:, 0:NT], pb.t[:, 0:NT], g1x.t[:, m, :], ALU.mult),
                         reads=[pb.b, g1x.b], writes=[tmp2[0].b])
                    S.op("dve", lambda e: TT(e, x1T.t[:, m, tsl], tmp2[0].t[:, 0:NT], xm.t[:, 0:NT], ALU.add),
                         reads=[tmp2[0].b, xm.b], writes=[x1T.sub(m)])
                stat_accum(x1T.t[:, m, tsl], m, NT, pbs)
                yield
            if ti + 1 < len(TILES_B):
                load_x2(ti + 1)
            yield

        def b_part2(ti):
            t0, NT, is_s = TILES_B[ti]
            tsl = slice(t0, t0 + NT)
            rs = rstdb[ti % 2]
            stat_finish(NT, PB[4 + ti % 2], rs)
            yield
            for m in range(8):
                tq = tmp2[m % 2]
                S.op("dve", lambda e: TT(e, tq.t[:, 0:NT], x1T.t[:, m, tsl], rs.t[:, 0:NT], ALU.mult),
                     reads=[x1T.sub(m), rs.b], writes=[tq.b])
                if not is_s:
                    S.op("act", lambda e: e.activation(out=vT.t[:, m, tsl], in_=tq.t[:, 0:NT], func=AF.Identity,
                                                       scale=amod.t[:, 8 + m, 0:1], bias=mod.t[:, 8 * MOD_SH2 + m, 0:1]),
                         reads=[tq.b, amod.b, mod.b], writes=[vT.sub(m)])
                else:
                    S.op("dve", lambda e: TT(e, tq.t[:, 0:NT], tq.t[:, 0:NT], a2x.t[:, m, :], ALU.mult),
                         reads=[tq.b, a2x.b], writes=[tq.b])
                    S.op("dve", lambda e: TT(e, vT.t[:, m, tsl], tq.t[:, 0:NT], sh2x.t[:, m, :], ALU.add),
                         reads=[tq.b, sh2x.b], writes=[vT.sub(m)])
                yield
            if ti == 0:
                dump("x1p", x1T.t[:, :, 0:256], [128, 8, 256], x1T.allb())
                dump("vp", vT.t[:, :, 0:256], [128, 8, 256], vT.allb())
        drive([b_part1(0)])
        for ti_ in range(len(TILES_B)):
            drive([b_part2(ti_), b_part1(ti_ + 1) if ti_ + 1 < len(TILES_B) else None])
        S.barrier()
        ckpt("1b")

        A.lo = LO_GLOBAL
        tmp2 = [A.alloc("tmp3_%d" % i, [128, 512], F32) for i in range(2)]
        rstdb = [A.alloc("rstd3_%d" % i, [128, 512], F32) for i in range(2)]
        sqb = [A.alloc("sqb3_%d" % i, [128, 512], BF16) for i in range(2)]
        onesb = A.alloc("onesb3", [128, 128], BF16)
        S.op("dve", lambda e: e.memset(onesb.t[:], 1.0), writes=[onesb.b])
        g2x = expand_mod("g2x", chunkmod(MOD_G2)[:, :, 1:17], [mod.b])
        afx = expand_mod("afx", amod.t[:, 16:24, 1:17], [amod.b])
        shfx = expand_mod("shfx", chunkmod(MOD_SHF)[:, :, 1:17], [mod.b])
        LO_P2 = A.lo
        hT = A.alloc("hT", [128, 6, NTOK], BF16)
        wgs = [A.alloc("wgs%d" % i, [128, 8, 256], BF16) for i in range(3)]
        wus = [A.alloc("wus%d" % i, [128, 8, 256], BF16) for i in range(3)]
        wds = [A.alloc("wds%d" % i, [128, 6, D], BF16) for i in range(2)]
        sgt = [A.alloc("sgt%d" % i, [128, 512], BF16) for i in range(2)]
        print("arena p2: lo=%d hi=%d" % (A.lo, A.hi))
        wg_v = wg.rearrange("(kt p) n -> p kt n", p=128)
        wu_v = wu.rearrange("(kt p) n -> p kt n", p=128)
        wd_v = wd.rearrange("(j p) n -> p j n", p=128)
        QUARTERS = [(0, 6), (6, 12), (12, 18), (18, 22)]
        SLABS = [(q, ja + 2 * s) for q, (ja, jb) in enumerate(QUARTERS) for s in range((jb - ja) // 2)]

        def load_gu(si):
            q, j0 = SLABS[si]
            S.dma("pool", wgs[si % 3].t[:], wg_v[:, :, j0 * 128:(j0 + 2) * 128], writes=[wgs[si % 3].b])
            S.dma("pool", wus[si % 3].t[:], wu_v[:, :, j0 * 128:(j0 + 2) * 128], writes=[wus[si % 3].b])

        def load_wd(q):
            ja, jb = QUARTERS[q]
            for jh in range(0, jb - ja, 2):
                S.dma("pool", wds[q % 2].t[:, jh:jh + 2, :], wd_v[:, ja + jh:ja + jh + 2, :], writes=[wds[q % 2].b])
        load_gu(0)
        load_gu(1)
        load_wd(0)
        gbank = [0]
        si = 0
        for q, (ja, jb) in enumerate(QUARTERS):
            if q + 1 < 4:
                load_wd(q + 1)
            for s in range((jb - ja) // 2):
                if si + 2 < len(SLABS):
                    load_gu(si + 2)
                wgt, wut = wgs[si % 3], wus[si % 3]
                for jc in range(2):
                    jj = 2 * s + jc
                    for (t0, NT, is_s) in TILES_B:
                        tsl = slice(t0, t0 + NT)
                        gbank[0] ^= 1
                        pbg, pbu = PB[gbank[0]], PB[2 + gbank[0]]
                        for (wt, pb_) in ((wgt, pbg), (wut, pbu)):
                            for kt in range(8):
                                S.op("pe", lambda e, kt=kt, wt=wt, pb_=pb_: e.matmul(
                                    pb_.t[:, 0:NT], wt.t[:, kt, jc * 128:(jc + 1) * 128], vT.t[:, kt, tsl], start=(kt == 0), stop=(kt == 7)),
                                    reads=[wt.b] + vT.allb(), writes=[pb_.b])
                        sg_ = sgt[gbank[0]]
                        S.op("act", lambda e, pbg=pbg, sg_=sg_: e.activation(out=sg_.t[:, 0:NT], in_=pbg.t[:, 0:NT], func=AF.Silu),
                             reads=[pbg.b], writes=[sg_.b])
                        S.op("dve", lambda e, pbu=pbu, sg_=sg_: TT(e, hT.t[:, jj, tsl], sg_.t[:, 0:NT], pbu.t[:, 0:NT], ALU.mult),
                             reads=[sg_.b, pbu.b], writes=[hT.sub(jj)])
                si += 1
            nj = jb - ja
            wdt = wds[q % 2]
            for (t0, NT, is_s) in TILES_B:
                tsl = slice(t0, t0 + NT)
                for m in range(8):
                    pb = PB[4 + m % 2]
                    for jj in range(nj):
                        S.op("pe", lambda e, jj=jj, m=m, pb=pb: e.matmul(pb.t[:, 0:NT], wdt.t[:, jj, m * 128:(m + 1) * 128], hT.t[:, jj, tsl],
                                                                         start=(jj == 0), stop=(jj == nj - 1)),
                             reads=[wdt.b, hT.sub(jj)], writes=[pb.b])
                    if not is_s:
                        S.op("dve", lambda e, m=m, pb=pb: e.scalar_tensor_tensor(
                            out=x1T.t[:, m, tsl], in0=pb.t[:, 0:NT], scalar=mod.t[:, 8 * MOD_G2 + m, 0:1], in1=x1T.t[:, m, tsl],
                            op0=ALU.mult, op1=ALU.add), reads=[pb.b, mod.b, x1T.sub(m)], writes=[x1T.sub(m)])
                    else:
                        S.op("dve", lambda e, m=m, pb=pb: TT(e, tmp2[0].t[:, 0:NT], pb.t[:, 0:NT], g2x.t[:, m, :], ALU.mult),
                             reads=[pb.b, g2x.b], writes=[tmp2[0].b])
                        S.op("dve", lambda e, m=m: TT(e, x1T.t[:, m, tsl], tmp2[0].t[:, 0:NT], x1T.t[:, m, tsl], ALU.add),
                             reads=[tmp2[0].b, x1T.sub(m)], writes=[x1T.sub(m)])
        S.barrier()
        ckpt("ffn")
        A.lo = LO_P2
        yTs = [A.alloc("yT%d" % i, [128, 8, 512], F32) for i in range(2)]
        ytm = [A.alloc("ytm%d" % i, [128, D], F32) for i in range(2)]
        print("arena final: lo=%d hi=%d" % (A.lo, A.hi))
        oi = [0]

        def f_part1(ti):
            t0, NT, is_s = TILES_B[ti]
            tsl = slice(t0, t0 + NT)
            yT = yTs[ti % 2]
            pbs = PB[6 + ti % 2]
            rs = rstdb[ti % 2]
            for m in range(8):
                stat_accum(x1T.t[:, m, tsl], m, NT, pbs)
                if m % 2 == 1:
                    yield
            stat_finish(NT, pbs, rs)
            yield
            for m in range(8):
                tq = tmp2[m % 2]
                S.op("dve", lambda e: TT(e, tq.t[:, 0:NT], x1T.t[:, m, tsl], rs.t[:, 0:NT], ALU.mult),
                     reads=[x1T.sub(m), rs.b], writes=[tq.b])
                if not is_s:
                    S.op("act", lambda e: e.activation(out=yT.t[:, m, 0:NT], in_=tq.t[:, 0:NT], func=AF.Identity,
                                                       scale=amod.t[:, 16 + m, 0:1], bias=mod.t[:, 8 * MOD_SHF + m, 0:1]),
                         reads=[tq.b, amod.b, mod.b], writes=[yT.sub(m)])
                else:
                    S.op("dve", lambda e: TT(e, tq.t[:, 0:NT], tq.t[:, 0:NT], afx.t[:, m, :], ALU.mult),
                         reads=[tq.b, afx.b], writes=[tq.b])
                    S.op("dve", lambda e: TT(e, yT.t[:, m, 0:NT], tq.t[:, 0:NT], shfx.t[:, m, :], ALU.add),
                         reads=[tq.b, shfx.b], writes=[yT.sub(m)])
                yield

        def f_part2(ti):
            t0, NT, is_s = TILES_B[ti]
            yT = yTs[ti % 2]
            for blk in range((NT + 127) // 128):
                rows = min(128, NT - blk * 128)
                yo = ytm[oi[0] % 2]
                oi[0] += 1
                for half in range(2):
                    pbt = PB[half]
                    for k4 in range(4):
                        kt = 4 * half + k4
                        S.op("pe", lambda e, kt=kt, k4=k4: e.transpose(
                            pbt.t[0:rows, k4 * 128:(k4 + 1) * 128], yT.t[:, kt, blk * 128:blk * 128 + rows], ident),
                            reads=[yT.sub(kt), cst.b], writes=[pbt.b])
                    if half == 0:
                        S.op("act", lambda e: e.activation(out=yo.t[0:rows, 0:512], in_=pbt.t[0:rows, :], func=AF.Copy),
                             reads=[pbt.b], writes=[yo.b])
                    else:
                        S.op("dve", lambda e: e.tensor_copy(out=yo.t[0:rows, 512:1024], in_=pbt.t[0:rows, :]),
                             reads=[pbt.b], writes=[yo.b])
                    yield
                S.dma("sp", yout[t0 + blk * 128:t0 + blk * 128 + rows, :], yo.t[0:rows, :], reads=[yo.b], buf=yo.b)
        import os as _os2
        if True:
            for ti_ in range(len(TILES_B)):
                drive([f_part1(ti_)])
                drive([f_part2(ti_)])
        else:
            drive([f_part1(0)])
            for ti_ in range(len(TILES_B)):
                drive([f_part2(ti_), f_part1(ti_ + 1) if ti_ + 1 < len(TILES_B) else None])
        S.barrier()
    return nc, dumps


def _prep_inputs(inp):
    cstv = _consts()
    prmv = _params(inp)
    BT, CT = _s5mats(inp)
    maps = []
    for i in range(NCORES):
        m = {}
        m["xin"] = np.ascontiguousarray(np.concatenate(
            [inp["x_prompt"][i], inp["x_sample"][NS * i:NS * (i + 1)].reshape(NS * LS, D)], axis=0), dtype=np.float32)
        m["cin"] = np.ascontiguousarray(np.concatenate(
            [inp["c_prompt"][i:i + 1], inp["c_sample"][NS * i:NS * (i + 1)]], axis=0), dtype=np.float32)
        m["wada"] = np.ascontiguousarray(inp["w_ada"][0], dtype=np.float32)
        m["wadaf"] = np.ascontiguousarray(inp["w_ada_f"], dtype=np.float32)
        m["win"] = np.ascontiguousarray(inp["w_in"][0], dtype=np.float32)
        m["wglu"] = np.ascontiguousarray(inp["w_glu"][0], dtype=np.float32)
        m["wout"] = np.ascontiguousarray(inp["w_out"][0], dtype=np.float32)
        m["wg"] = np.ascontiguousarray(inp["w_ffn_gate"][0], dtype=np.float32)
        m["wu"] = np.ascontiguousarray(inp["w_ffn_up"][0], dtype=np.float32)
        m["wd"] = np.ascontiguousarray(inp["w_ffn_down"][0], dtype=np.float32)
        m["cst"] = cstv
        m["prm"] = prmv
        m["s5bt"] = BT.reshape(128, -1)
        m["s5ct"] = CT.reshape(128, -1)
        m["stssd"] = np.ascontiguousarray(inp["state_ssd"][0, NS * i:NS * (i + 1)], dtype=np.float32)
        sc = inp["state_conv"][0, NS * i:NS * (i + 1)]
        m["stconv"] = np.ascontiguousarray(
            sc.reshape(NS, 3, 8, 128).transpose(3, 2, 0, 1).reshape(128, -1), dtype=np.float32)
        sr = inp["state_s5_re"][0, NS * i:NS * (i + 1)]
        si = inp["state_s5_im"][0, NS * i:NS * (i + 1)]
        st = np.stack([sr, si], 0).reshape(2, NS, 16, 128).transpose(3, 0, 2, 1)
        m["sts5"] = np.ascontiguousarray(st.reshape(128, -1), dtype=np.float32)
        maps.append(m)
    return maps


_CACHE = {}


def kernel(**inputs):
    inp = {k: np.asarray(v) for k, v in inputs.items()}
    if "nc" not in _CACHE:
        _CACHE["nc"] = build()[0]
    nc = _CACHE["nc"]
    maps = _prep_inputs(inp)
    res = run_bass_kernel_spmd(nc, maps, core_ids=list(range(NCORES)))
    R = res.results
    y_p = np.stack([R[i]["yout"][:SEQ] for i in range(NCORES)], 0)
    y_s = np.concatenate([R[i]["yout"][SEQ:].reshape(NS, LS, D) for i in range(NCORES)], 0)
    ssd_p = np.stack([R[i]["o_ssdp"].reshape(128, 8, 64).transpose(1, 2, 0) for i in range(NCORES)], 0)[None]
    ssd_s = np.concatenate([R[i]["o_ssds"] for i in range(NCORES)], 0)[None]
    conv = [R[i]["o_conv"].reshape(128, 8, 17, 3).transpose(2, 3, 1, 0).reshape(17, 3, 1024) for i in range(NCORES)]
    conv_p = np.stack([c[0] for c in conv], 0)[None]
    conv_s = np.concatenate([c[1:] for c in conv], 0)[None]
    s5 = [R[i]["o_s5"].reshape(128, 2, 16, 17).transpose(1, 3, 2, 0).reshape(2, 17, 32, 64) for i in range(NCORES)]
    re_p = np.stack([s[0, 0] for s in s5], 0)[None]
    re_s = np.concatenate([s[0, 1:] for s in s5], 0)[None]
    im_p = np.stack([s[1, 0] for s in s5], 0)[None]
    im_s = np.concatenate([s[1, 1:] for s in s5], 0)[None]
    f = lambda a: np.ascontiguousarray(a, dtype=np.float32)
    return (f(y_p), f(y_s), f(ssd_p), f(ssd_s), f(conv_p), f(conv_s), f(re_p), f(re_s), f(im_p), f(im_s))
```

```python
import math
import numpy as np
from contextlib import ExitStack
import concourse.bass as bass
import concourse.mybir as mybir
from concourse.bass_utils import run_bass_kernel_spmd

F32 = mybir.dt.float32
BF16 = mybir.dt.bfloat16
I32 = mybir.dt.int32
AF = mybir.ActivationFunctionType
ALU = mybir.AluOpType

NCORES = 8
D = 1024
SEQ = 2048
NS = 16
LS = 4
NTOK = SEQ + NS * LS
DFF = 2816
NJ = DFF // 128
INP = 2056
EPS = 1e-6
T5 = 32
TILES = [(0, 512), (512, 512), (1024, 512), (1536, 512), (2048, 64)]
PI = math.pi


class Buf:
    def __init__(self, name):
        self.name = name
        self.w = None
        self.r = []
        self.dsem = None
        self.dcnt = 0


class TL:
    def __init__(self, t, name):
        self.t = t
        self.name = name
        self.b = Buf(name)
        self.subs = {}

    def sub(self, k):
        if getattr(self, "nosub", False):
            return self.b
        if k not in self.subs:
            self.subs[k] = Buf("%s_%s" % (self.name, k))
        return self.subs[k]

    def allb(self):
        return [self.b] + list(self.subs.values())

    def __getitem__(self, k):
        return self.t[k]


class Sched:
    ENG = ["pe", "act", "dve", "pool", "sp"]

    def __init__(self, nc, es):
        self.nc = nc
        self.es = es
        self.eobj = {"pe": nc.tensor, "act": nc.scalar, "dve": nc.vector, "pool": nc.gpsimd, "sp": nc.sync}
        self.cnt = {e: 0 for e in self.ENG}
        self.sem = {e: es.enter_context(nc.semaphore("s_" + e)) for e in self.ENG}
        self.seen = {e: {} for e in self.ENG}
        self.dbufs = []
        self.ninst = 0
        self.dead = False
        self.pe_pending = None

    def _flush_pe(self):
        if self.pe_pending is not None:
            self.pe_pending.then_inc(self.sem["pe"], 1)
            self.cnt["pe"] += 1
            self.pe_pending = None

    def _deps(self, eng, reads, writes):
        deps = []
        for b in reads:
            if b.w is not None:
                deps.append(b.w)
        for b in writes:
            if b.w is not None:
                deps.append(b.w)
            deps.extend(b.r)
        waits = {}
        for (sem, val, key) in deps:
            if key == "pe" and eng == "pe":
                continue
            if self.seen[eng].get(key, 0) >= val:
                continue
            if key == "pe" and val > self.cnt["pe"]:
                self._flush_pe()
            if key not in waits or waits[key][1] < val:
                waits[key] = (sem, val)
        for key, (sem, val) in waits.items():
            self.seen[eng][key] = val
        return list(waits.values())

    def op(self, eng, fn, reads=(), writes=()):
        if self.dead:
            return None
        xr = [b for b in reads if getattr(b, "excl", False)]
        if xr:
            reads = [b for b in reads if not getattr(b, "excl", False)]
            writes = list(writes) + xr
        waits = self._deps(eng, reads, writes)
        e = self.eobj[eng]
        for (s_, v_) in waits:
            e.wait_ge(s_, v_)
        if eng == "pe":
            self.pe_pending = fn(e)
            tok = (self.sem[eng], self.cnt[eng] + 1, eng)
        else:
            self.cnt[eng] += 1
            tok = (self.sem[eng], self.cnt[eng], eng)
            fn(e).then_inc(self.sem[eng], 1)
        for b in reads:
            b.r.append(tok)
        for b in writes:
            b.w = tok
            b.r = []
        self.ninst += 1
        return tok

    def dma(self, eng, out, in_, reads=(), writes=(), buf=None, **kw):
        if self.dead:
            return None
        waits = self._deps(eng, reads, writes)
        if buf is None:
            buf = writes[0] if writes else reads[0]
        if buf.dsem is None:
            buf.dsem = self.es.enter_context(self.nc.semaphore("d_" + buf.name))
            self.dbufs.append(buf)
        buf.dcnt += 16
        tok = (buf.dsem, buf.dcnt, "d_" + buf.name)
        e = self.eobj[eng]
        for (s_, v_) in waits:
            e.wait_ge(s_, v_)
        e.dma_start(out=out, in_=in_, **kw).then_inc(buf.dsem, 16)
        for b in reads:
            b.r.append(tok)
        for b in writes:
            b.w = tok
            b.r = []
        self.ninst += 1
        return tok

    def barrier(self):
        if self.dead:
            return
        self._flush_pe()
        for e in self.ENG:
            waits = []
            for o in self.ENG:
                if o != e and self.cnt[o] > self.seen[e].get(o, 0):
                    waits.append((self.sem[o], self.cnt[o]))
                    self.seen[e][o] = self.cnt[o]
            for b in self.dbufs:
                key = "d_" + b.name
                if b.dcnt > self.seen[e].get(key, 0):
                    waits.append((b.dsem, b.dcnt))
                    self.seen[e][key] = b.dcnt
            for (s_, v_) in waits:
                self.eobj[e].wait_ge(s_, v_)

    def emit(self):
        pass


C_ID = 0
C_TRI = 128
C_NEG = 256
C_TRI64 = 384
C_NEG64 = 512
C_SEG64 = 640
C_SEGI = 768
CST_W = 784

P_BMOD = 0
P_GAIN = 64
P_CONV = 88
P_SSDFM = 128
P_S5P = 136
P_S5M = 184
P_SSD8 = 192
PRM_W = 194


def _consts():
    c = np.zeros((128, CST_W), np.float32)
    c[:, C_ID:C_ID + 128] = np.eye(128, dtype=np.float32)
    s = np.arange(128)[:, None]
    l = np.arange(128)[None, :]
    c[:, C_TRI:C_TRI + 128] = (s <= l).astype(np.float32)
    c[:, C_NEG:C_NEG + 128] = np.where(l >= s, 0.0, -30000.0)
    same = (s // LS == l // LS) & (s < 64) & (l < 64)
    c[:, C_TRI64:C_TRI64 + 128] = ((s <= l) & same).astype(np.float32)
    c[:, C_NEG64:C_NEG64 + 128] = np.where((l >= s) & same, 0.0, -30000.0)
    c[:, C_SEG64:C_SEG64 + 128] = same.astype(np.float32)
    j = np.arange(16)[None, :]
    c[:, C_SEGI:C_SEGI + 16] = ((s // LS == j) & (s < 64)).astype(np.float32)
    return c


def _fm(v, nt):
    return np.ascontiguousarray(np.asarray(v, np.float32).reshape(nt, 128).T)


def _params(inp):
    p = np.zeros((128, PRM_W), np.float32)
    p[:, P_BMOD:P_BMOD + 48] = _fm(inp["b_ada"][0], 48)
    p[:, P_BMOD + 48:P_BMOD + 64] = _fm(inp["b_ada_f"], 16)
    p[:, P_GAIN:P_GAIN + 8] = _fm(inp["norm1_g"][0], 8)
    p[:, P_GAIN + 8:P_GAIN + 16] = _fm(inp["norm2_g"][0], 8)
    p[:, P_GAIN + 16:P_GAIN + 24] = _fm(inp["normf_g"], 8)
    cw = inp["conv_w"][0]
    cv = np.zeros((128, 8, 5), np.float32)
    for k in range(4):
        cv[:, :, k] = _fm(cw[k], 8)
    cv[:, :, 4] = _fm(inp["conv_b"][0], 8)
    p[:, P_CONV:P_CONV + 40] = cv.reshape(128, 40)
    Dh = inp["ssd_D"][0]
    dfm = np.zeros((128, 4), np.float32)
    for pr in range(4):
        dfm[0:64, pr] = Dh[2 * pr]
        dfm[64:128, pr] = Dh[2 * pr + 1]
    p[:, P_SSDFM:P_SSDFM + 4] = dfm
    p[:, P_SSDFM + 4:P_SSDFM + 8] = _fm(inp["ssd_norm_g"][0], 4)

    def st(a):
        return np.ascontiguousarray(np.asarray(a, np.float32).reshape(16, 128).T)
    p[:, P_S5P:P_S5P + 16] = st(inp["s5_A_re"][0])
    p[:, P_S5P + 16:P_S5P + 32] = st(inp["s5_A_im"][0])
    p[:, P_S5P + 32:P_S5P + 48] = st(np.repeat(inp["s5_log_step"][0][:, None], 64, axis=1))
    p[:, P_S5M:P_S5M + 4] = _fm(inp["s5_D"][0], 4)
    p[:, P_S5M + 4:P_S5M + 8] = _fm(inp["b_glu"][0], 4)
    p[0:8, P_SSD8] = inp["ssd_dt_bias"][0]
    p[0:8, P_SSD8 + 1] = inp["ssd_A_log"][0]
    return p


def _s5mats(inp):
    Br, Bi = inp["s5_B_re"][0], inp["s5_B_im"][0]
    Cr, Ci = inp["s5_C_re"][0], inp["s5_C_im"][0]
    BT = np.zeros((128, 2, 16, 128), np.float32)
    CT = np.zeros((128, 2, 16, 32), np.float32)
    for s in range(16):
        for gl in range(2):
            g = 2 * s + gl
            r0 = (g % 8) * 16
            BT[r0:r0 + 16, 0, s, gl * 64:(gl + 1) * 64] = Br[g].T
            BT[r0:r0 + 16, 1, s, gl * 64:(gl + 1) * 64] = Bi[g].T
            CT[gl * 64:(gl + 1) * 64, 0, s, gl * 16:(gl + 1) * 16] = Cr[g].T
            CT[gl * 64:(gl + 1) * 64, 1, s, gl * 16:(gl + 1) * 16] = Ci[g].T
    return BT, CT


class Arena:
    def __init__(self, nc, es, words):
        self.t = es.enter_context(nc.sbuf_tensor("arena", [128, words], F32))
        self.words = words
        self.lo = 0
        self.hi = words

    def alloc(self, name, shape, dt, top=False):
        n = 1
        for d in shape[1:]:
            n *= d
        w = n if dt == F32 or dt == I32 else (n + 1) // 2
        w = (w + 3) // 4 * 4
        if top:
            self.hi -= w
            off = self.hi
        else:
            off = self.lo
            self.lo += w
        assert self.lo <= self.hi, "arena overflow at %s: lo=%d hi=%d" % (name, self.lo, self.hi)
        ap = self.t[:, off:off + w]
        if dt != F32:
            ap = ap.bitcast(dt)
        ap = ap[:, 0:n]
        if len(shape) == 3:
            ap = ap.rearrange("p (a b) -> p a b", b=shape[2])
        elif len(shape) == 4:
            ap = ap.rearrange("p (a b c) -> p a b c", b=shape[2], c=shape[3])
        if shape[0] < 128:
            ap = ap[0:shape[0]]
        return TL(ap, name)


class StopBuild(Exception):
    pass


def build(dbg=None, stop_after=None):
    nc = bass.Bass("TRN2", target_bir_lowering=False)

    SH = []

    def ckpt(name):
        if stop_after == name:
            SH[0].barrier()
            SH[0].dead = True
    dt_in = lambda name, shape: nc.dram_tensor(name, list(shape), F32, kind="ExternalInput").ap()
    dt_out = lambda name, shape: nc.dram_tensor(name, list(shape), F32, kind="ExternalOutput").ap()
    xin = dt_in("xin", [NTOK, D])
    cin = dt_in("cin", [17, D])
    wada = dt_in("wada", [D, 6144])
    wadaf = dt_in("wadaf", [D, 2048])
    win = dt_in("win", [D, INP])
    wglu = dt_in("wglu", [512, 512])
    wout = dt_in("wout", [D, D])
    wg = dt_in("wg", [D, DFF])
    wu = dt_in("wu", [D, DFF])
    wd = dt_in("wd", [DFF, D])
    cst_d = dt_in("cst", [128, CST_W])
    prm_d = dt_in("prm", [128, PRM_W])
    s5bt_d = dt_in("s5bt", [128, 2 * 16 * 128])
    s5ct_d = dt_in("s5ct", [128, 2 * 16 * 32])
    stssd_d = dt_in("stssd", [NS, 8, 64, 128])
    stconv_d = dt_in("stconv", [128, 8 * NS * 3])
    sts5_d = dt_in("sts5", [128, 2 * 16 * NS])
    yout = dt_out("yout", [NTOK, D])
    o_ssdp = dt_out("o_ssdp", [128, 512])
    o_ssds = dt_out("o_ssds", [NS, 8, 64, 128])
    o_conv = dt_out("o_conv", [128, 8 * 17 * 3])
    o_s5 = dt_out("o_s5", [128, 2 * 16 * 17])
    mixd = nc.dram_tensor("mixd", [128, 8, NTOK], BF16, kind="Internal").ap()
    dumps = {}

    with ExitStack() as es:
        S = Sched(nc, es)
        NEED_CTN = []
        SH.append(S)
        A = Arena(nc, es, 53200)
        outbufs = []

        def dump(name, ap, shape, reads):
            if dbg is None or name not in dbg:
                return
            d = dt_out("dbg_" + name, shape)
            dumps[name] = shape
            b = Buf("dbg_" + name)
            S.dma("sp" if ap.dtype == F32 else "pool", d, ap, reads=reads, buf=b)
            outbufs.append(b)

        PB = [TL(es.enter_context(nc.psum_tensor("pb%d" % i, [128, 512], F32)), "pb%d" % i) for i in range(8)]
        for pb_ in PB:
            pb_.b.excl = True
            pb_.nosub = True

        def pbf(i):
            return PB[i].t[:].bitcast(BF16)

        cst = A.alloc("cst", [128, CST_W], F32)
        prm = A.alloc("prm", [128, PRM_W], F32)
        identb = A.alloc("identb", [128, 128], BF16)
        onesf = A.alloc("onesf", [128, 128], F32)
        mod = A.alloc("mod", [128, 64, 17], F32)
        amod = A.alloc("amod", [128, 24, 17], F32)
        s5fin = A.alloc("s5fin", [128, 2, 16, 17], F32)
        scT = A.alloc("scT", [128, 8, 17], BF16)
        LO_GLOBAL = A.lo

        ident = cst.t[:, C_ID:C_ID + 128]
        S.dma("sp", cst.t[:], cst_d, writes=[cst.b])
        S.dma("sp", prm.t[:], prm_d, writes=[prm.b])
        S.op("act", lambda e: e.activation(out=identb.t[:], in_=ident, func=AF.Copy), reads=[cst.b], writes=[identb.b])
        S.op("dve", lambda e: e.memset(onesf.t[:], 1.0), writes=[onesf.b])

        def chunkmod(i):
            return mod.t[:, 8 * i:8 * i + 8, :]

        cs = A.alloc("cs", [17, D], F32)
        slabs = [A.alloc("adaslab%d" % i, [128, 8, 512], BF16) for i in range(3)]
        S.dma("sp", cs.t[:], cin, writes=[cs.b])
        S.op("act", lambda e: e.activation(out=cs.t[:], in_=cs.t[:], func=AF.Silu), reads=[cs.b], writes=[cs.b])
        for kt in range(8):
            S.op("pe", lambda e, kt=kt: e.transpose(PB[2].t[:, kt * 17:(kt + 1) * 17], cs.t[:, kt * 128:(kt + 1) * 128],
                                                    cst.t[0:17, C_ID:C_ID + 17]),
                 reads=[cs.b, cst.b], writes=[PB[2].b])
        S.op("act", lambda e: e.activation(out=scT.t[:].rearrange("p k s -> p (k s)"), in_=PB[2].t[:, 0:136], func=AF.Copy),
             reads=[PB[2].b], writes=[scT.b])
        wada_v = wada.rearrange("(kt p) n -> p kt n", p=128)
        wadaf_v = wadaf.rearrange("(kt p) n -> p kt n", p=128)

        def slab_src(i):
            if i < 12:
                return wada_v[:, :, i * 512:(i + 1) * 512]
            return wadaf_v[:, :, (i - 12) * 512:(i - 11) * 512]

        def load_slab(i):
            sl = slabs[i % 3]
            for kh in range(2):
                S.dma("pool", sl.t[:, 4 * kh:4 * kh + 4, :], slab_src(i)[:, 4 * kh:4 * kh + 4, :], writes=[sl.b])
        load_slab(0)
        load_slab(1)
        for i in range(4):
            if i + 2 < 4:
                load_slab(i + 2)
            sl = slabs[i % 3]
            pb = PB[i % 2]
            for fc in range(4):
                for kt in range(8):
                    S.op("pe", lambda e, fc=fc, kt=kt, sl=sl, pb=pb: e.matmul(
                        pb.t[:, fc * 17:(fc + 1) * 17], sl.t[:, kt, fc * 128:(fc + 1) * 128], scT.t[:, kt, :],
                        start=(kt == 0), stop=(kt == 7)), reads=[sl.b, scT.b], writes=[pb.b])
            S.op("dve", lambda e, i=i, pb=pb: e.tensor_tensor(
                out=mod.t[:, 4 * i:4 * i + 4, :], in0=pb.t[:, 0:68].rearrange("p (c s) -> p c s", s=17),
                in1=prm.t[:, P_BMOD + 4 * i:P_BMOD + 4 * i + 4].unsqueeze(2).to_broadcast([128, 4, 17]), op=ALU.add),
                reads=[pb.b, prm.b], writes=[mod.b])
        def make_amod(lst):
          for k, (sci, gi) in lst:
            S.op("dve", lambda e, k=k, sci=sci, gi=gi: e.scalar_tensor_tensor(
                out=amod.t[:, 8 * k:8 * k + 8, :], in0=chunkmod(sci), scalar=1.0,
                in1=prm.t[:, P_GAIN + 8 * gi:P_GAIN + 8 * gi + 8].unsqueeze(2).to_broadcast([128, 8, 17]),
                op0=ALU.add, op1=ALU.mult), reads=[mod.b, prm.b], writes=[amod.b])
        make_amod([(0, (1, 0))])
        dump("mod", mod.t[:].rearrange("p c s -> p (c s)"), [128, 64 * 17], [mod.b])
        S.barrier()
        S.emit()
        A.lo = LO_GLOBAL

        MOD_SH1, MOD_G1, MOD_SH2, MOD_G2, MOD_SHF = 0, 2, 3, 5, 6

        def expand_mod(name, src_ap, srcbufs):
            t = A.alloc(name, [128, 8, 64], F32)
            S.op("dve", lambda e: e.tensor_copy(out=t.t[:].rearrange("p k (s b) -> p k s b", b=LS),
                                                in_=src_ap.unsqueeze(3).to_broadcast([128, 8, NS, LS])),
                 reads=srcbufs, writes=[t.b])
            return t

        LO_P1 = A.lo
        mixt = [A.alloc("mixt%d" % i, [128, 8, 256], BF16) for i in range(2)]
        mixdb = [Buf("mixd%d" % i) for i in range(9)]
        win_sb = A.alloc("win_sb", [128, 8, INP], BF16)
        wglu_sb = A.alloc("wglu_sb", [128, 4, 512], BF16)
        s5BT = A.alloc("s5BT", [128, 2, 16, 128], BF16)
        s5CT = A.alloc("s5CT", [128, 2, 16, 32], BF16)
        win_v = win.rearrange("(kt p) n -> p kt n", p=128)
        for kh in range(4):
            for ch in range(2):
                S.dma("pool", win_sb.t[:, 2 * kh:2 * kh + 2, ch * 1028:(ch + 1) * 1028],
                      win_v[:, 2 * kh:2 * kh + 2, ch * 1028:(ch + 1) * 1028], writes=[win_sb.b])
        for a_ in range(4):
            S.dma("pool", s5BT.t[:].rearrange("p a s c -> p (a s c)")[:, a_ * 1024:(a_ + 1) * 1024],
                  s5bt_d[:, a_ * 1024:(a_ + 1) * 1024], writes=[s5BT.b])
        S.dma("pool", s5CT.t[:].rearrange("p a s c -> p (a s c)"), s5ct_d, writes=[s5CT.b])
        S.dma("pool", wglu_sb.t[:], wglu.rearrange("(kt p) n -> p kt n", p=128), writes=[wglu_sb.b])
        S.op("dve", lambda e: e.tensor_scalar(out=s5CT.t[:, 1], in0=s5CT.t[:, 1], scalar1=-1.0, scalar2=None, op0=ALU.mult),
             reads=[s5CT.b], writes=[s5CT.b])
        NEED_CTN.append(1)

        a1x = A.alloc("a1x", [128, 8, 64], F32)
        sh1x = A.alloc("sh1x", [128, 8, 64], F32)

        def fill_x(t, src_ap, srcbufs):
            S.op("dve", lambda e: e.tensor_copy(out=t.t[:].rearrange("p k (s b) -> p k s b", b=LS),
                                                in_=src_ap.unsqueeze(3).to_broadcast([128, 8, NS, LS])),
                 reads=srcbufs, writes=[t.b])
        adab = [TL(a1x.t[:].rearrange("p k t -> p (k t)").bitcast(BF16).rearrange("p (k c) -> p k c", c=128), "adab0"),
                TL(sh1x.t[:].rearrange("p k t -> p (k t)").bitcast(BF16).rearrange("p (k c) -> p k c", c=128), "adab1")]
        adab[0].b = a1x.b
        adab[1].b = sh1x.b
        ADA_CH = list(range(16, 64))

        def ada_load(ci):
            c = ADA_CH[ci]
            src = wada_v[:, :, c * 128:(c + 1) * 128] if c < 48 else wadaf_v[:, :, (c - 48) * 128:(c - 47) * 128]
            S.dma("pool", adab[ci % 2].t[:], src, writes=[adab[ci % 2].b])

        def ada_compute(ci):
            c = ADA_CH[ci]
            sl = adab[ci % 2]
            pb = next_pb()
            for kt in range(8):
                S.op("pe", lambda e, kt=kt: e.matmul(pb.t[:, 0:17], sl.t[:, kt, :], scT.t[:, kt, :], start=(kt == 0), stop=(kt == 7)),
                     reads=[sl.b, scT.b], writes=[pb.b])
            S.op("dve", lambda e: e.tensor_scalar(out=mod.t[:, c, :], in0=pb.t[:, 0:17], scalar1=prm.t[:, P_BMOD + c:P_BMOD + c + 1],
                                                  scalar2=None, op0=ALU.add), reads=[pb.b, prm.b], writes=[mod.b])
        ada_state = [0, 0]

        def ada_step():
            if ada_state[1] >= len(ADA_CH):
                return
            while ada_state[0] < min(len(ADA_CH), ada_state[1] + 2):
                ada_load(ada_state[0])
                ada_state[0] += 1
            ada_compute(ada_state[1])
            ada_state[1] += 1

        ssd8 = A.alloc("ssd8", [8, 4], F32)
        S.op("act", lambda e: e.activation(out=ssd8.t[:, 1:2], in_=prm.t[0:8, P_SSD8 + 1:P_SSD8 + 2], func=AF.Exp),
             reads=[prm.b], writes=[ssd8.b])
        S.op("dve", lambda e: e.tensor_scalar(out=ssd8.t[:, 1:2], in0=ssd8.t[:, 1:2], scalar1=-1.0, scalar2=None, op0=ALU.mult),
             reads=[ssd8.b], writes=[ssd8.b])
        S.op("dve", lambda e: e.tensor_copy(out=ssd8.t[:, 0:1], in_=prm.t[0:8, P_SSD8:P_SSD8 + 1]), reads=[prm.b], writes=[ssd8.b])

        Ptab = A.alloc("Ptab", [128, 2, 16, T5], F32)
        Qtab = A.alloc("Qtab", [128, 2, 16, T5], F32)
        s5t = [A.alloc("s5t%d" % i, [128, 512], F32) for i in range(2)]

        def alias(name, ap, buf):
            tl = TL(ap, name)
            tl.b = buf
            return tl
        sw = alias("s5work", s5t[1].t[:, 0:384].rearrange("p (a b) -> p a b", b=16), s5t[1].b)
        tmpA = alias("tmpA", s5t[0].t[:, 0:256].rearrange("p (a b) -> p a b", b=T5 // 2), s5t[0].b)
        tmpB = alias("tmpB", s5t[0].t[:, 256:512].rearrange("p (a b) -> p a b", b=T5 // 2), s5t[0].b)
        mask32 = A.alloc("mask32", [128, 16, T5], BF16)
        s5v = [A.alloc("s5v%d" % i, [128, 512], F32) for i in range(2)]
        qtmp = alias("qtmp", s5v[0].t[:].rearrange("p (s t) -> p s t", t=T5), s5v[0].b)
        mask4 = A.alloc("mask4", [128, 128, LS], BF16)
        s5cr = A.alloc("s5cr", [128, 2, 16], F32)
        W = lambda i: sw.t[:, i, :]
        pv = lambda i: prm.t[:, P_S5P + 16 * i:P_S5P + 16 * (i + 1)]
        swb = [sw.b, prm.b]

        def dv(fn):
            S.op("dve", fn, reads=swb, writes=[sw.b])

        def act(fn):
            S.op("act", fn, reads=swb, writes=[sw.b])
        TT = lambda e, o, a, b, op: e.tensor_tensor(out=o, in0=a, in1=b, op=op)
        def exp_acc(dst, src):
            dv(lambda e: e.tensor_scalar(out=W(22), in0=src, scalar1=1.0 / 16, scalar2=None, op0=ALU.mult))
            dv(lambda e: e.tensor_scalar(out=dst, in0=W(22), scalar1=1.0 / 7, scalar2=1.0, op0=ALU.mult, op1=ALU.add))
            for k in (6, 5, 4, 3, 2, 1):
                dv(lambda e: TT(e, dst, dst, W(22), ALU.mult))
                dv(lambda e, k=k: e.tensor_scalar(out=dst, in0=dst, scalar1=1.0 / k, scalar2=1.0, op0=ALU.mult, op1=ALU.add))
            for _ in range(4):
                dv(lambda e: TT(e, dst, dst, dst, ALU.mult))
        exp_acc(W(0), pv(2))
        dv(lambda e: TT(e, W(1), pv(0), W(0), ALU.mult))
        dv(lambda e: TT(e, W(2), pv(1), W(0), ALU.mult))
        exp_acc(W(3), W(1))

        def range_reduce(dst, src, add):
            ki = A_ki
            dv(lambda e: e.tensor_scalar(out=W(20), in0=src, scalar1=float(add), scalar2=1.0 / (2 * PI), op0=ALU.add, op1=ALU.mult))
            S.op("dve", lambda e: e.tensor_copy(out=ki.t[:], in_=W(20)), reads=swb, writes=[ki.b])
            S.op("dve", lambda e: e.tensor_copy(out=W(21), in_=ki.t[:]), reads=[ki.b], writes=[sw.b])
            dv(lambda e: e.tensor_scalar(out=W(20), in0=src, scalar1=float(add), scalar2=None, op0=ALU.add))
            dv(lambda e: e.scalar_tensor_tensor(out=dst, in0=W(21), scalar=-2 * PI, in1=W(20), op0=ALU.mult, op1=ALU.add))
            dv(lambda e: e.tensor_scalar(out=dst, in0=dst, scalar1=PI, scalar2=-PI, op0=ALU.min, op1=ALU.max))
        A_ki = A.alloc("s5ki", [128, 16], I32)
        range_reduce(W(4), W(2), 0.0)
        range_reduce(W(5), W(2), PI / 2)
        act(lambda e: e.activation(out=W(6), in_=W(4), func=AF.Sin))
        act(lambda e: e.activation(out=W(7), in_=W(5), func=AF.Sin))
        dv(lambda e: TT(e, W(8), W(3), W(7), ALU.mult))
        dv(lambda e: TT(e, W(9), W(3), W(6), ALU.mult))
        dv(lambda e: e.tensor_scalar(out=W(10), in0=W(8), scalar1=-1.0, scalar2=None, op0=ALU.add))
        dv(lambda e: TT(e, W(11), pv(0), pv(0), ALU.mult))
        dv(lambda e: TT(e, W(12), pv(1), pv(1), ALU.mult))
        dv(lambda e: TT(e, W(11), W(11), W(12), ALU.add))
        dv(lambda e: e.reciprocal(out=W(11), in_=W(11)))
        dv(lambda e: TT(e, W(12), W(10), pv(0), ALU.mult))
        dv(lambda e: TT(e, W(13), W(9), pv(1), ALU.mult))
        dv(lambda e: TT(e, W(12), W(12), W(13), ALU.add))
        dv(lambda e: TT(e, W(14), W(12), W(11), ALU.mult))
        dv(lambda e: TT(e, W(12), W(9), pv(0), ALU.mult))
        dv(lambda e: TT(e, W(13), W(10), pv(1), ALU.mult))
        dv(lambda e: TT(e, W(12), W(12), W(13), ALU.subtract))
        dv(lambda e: TT(e, W(15), W(12), W(11), ALU.mult))
        dv(lambda e: TT(e, W(12), W(8), W(8), ALU.mult))
        dv(lambda e: TT(e, W(13), W(9), W(9), ALU.mult))
        dv(lambda e: TT(e, W(12), W(12), W(13), ALU.add))
        dv(lambda e: e.reciprocal(out=W(12), in_=W(12)))
        dv(lambda e: TT(e, W(16), W(8), W(12), ALU.mult))
        dv(lambda e: e.scalar_tensor_tensor(out=W(17), in0=W(9), scalar=-1.0, in1=W(12), op0=ALU.mult, op1=ALU.mult))

        def build_pow(tab, br, bi):
            tb = [tab.b, sw.b, tmpA.b, tmpB.b]
            S.op("dve", lambda e: e.tensor_copy(out=tab.t[:, 0, :, 0], in_=br), reads=tb, writes=[tab.b])
            S.op("dve", lambda e: e.tensor_copy(out=tab.t[:, 1, :, 0], in_=bi), reads=tb, writes=[tab.b])
            n = 1
            while n < T5:
                ar, ai = tab.t[:, 0, :, 0:n], tab.t[:, 1, :, 0:n]
                sr = tab.t[:, 0, :, n - 1:n].to_broadcast([128, 16, n])
                si = tab.t[:, 1, :, n - 1:n].to_broadcast([128, 16, n])
                tA, tB = tmpA.t[:, :, 0:n], tmpB.t[:, :, 0:n]
                orr, oi = tab.t[:, 0, :, n:2 * n], tab.t[:, 1, :, n:2 * n]
                ops = [(tA, ar, sr, ALU.mult), (tB, ai, si, ALU.mult), (orr, tA, tB, ALU.subtract),
                       (tA, ar, si, ALU.mult), (tB, ai, sr, ALU.mult), (oi, tA, tB, ALU.add)]
                for (o, a, b, op) in ops:
                    S.op("dve", lambda e, o=o, a=a, b=b, op=op: TT(e, o, a, b, op), reads=tb, writes=tb[0:1] + tb[2:4])
                n *= 2
        build_pow(Ptab, W(8), W(9))
        build_pow(Qtab, W(16), W(17))
        tq = [Qtab.b, sw.b, tmpA.b, tmpB.b]
        for half in range(2):
            hs = slice(half * (T5 // 2), (half + 1) * (T5 // 2))
            qr, qi = Qtab.t[:, 0, :, hs], Qtab.t[:, 1, :, hs]
            fr = W(14).unsqueeze(2).to_broadcast([128, 16, T5 // 2])
            fi = W(15).unsqueeze(2).to_broadcast([128, 16, T5 // 2])
            ops = [(tmpA.t[:], qr, fr, ALU.mult), (tmpB.t[:], qi, fi, ALU.mult), ("R", tmpA.t[:], tmpB.t[:], ALU.subtract),
                   (tmpA.t[:], qr, fi, ALU.mult), (tmpB.t[:], qi, fr, ALU.mult), (qi, tmpA.t[:], tmpB.t[:], ALU.add)]
            for (o, a, b, op) in ops:
                if isinstance(o, str):
                    o = qtmp.t[:, :, hs]
                S.op("dve", lambda e, o=o, a=a, b=b, op=op: TT(e, o, a, b, op), reads=tq + [qtmp.b], writes=tq + [qtmp.b])
            S.op("dve", lambda e, qr=qr, hs=hs: e.tensor_copy(out=qr, in_=qtmp.t[:, :, hs]), reads=[qtmp.b], writes=[Qtab.b])
        S.op("dve", lambda e: e.memset(mask32.t[:], 1.0), reads=[Qtab.b], writes=[mask32.b])
        S.op("dve", lambda e: e.memset(mask32.t[:, :, 0:1], 0.0), writes=[mask32.b])
        S.op("dve", lambda e: e.memset(mask4.t[:], 1.0), writes=[mask4.b])
        S.op("dve", lambda e: e.memset(mask4.t[:, :, 0:1], 0.0), writes=[mask4.b])
        S.op("dve", lambda e: e.memset(s5cr.t[:], 0.0), writes=[s5cr.b])
        dump("Ptab", Ptab.t[:].rearrange("p a s t -> p (a s t)"), [128, 2 * 16 * T5], [Ptab.b])
        dump("Qtab", Qtab.t[:].rearrange("p a s t -> p (a s t)"), [128, 2 * 16 * T5], [Qtab.b])

        ckpt("setup0")
        NTM = 256
        xtm = A.alloc("xtm", [128, 2, D], F32)
        xn = A.alloc("xn", [128, 2, D], BF16)
        nstat = A.alloc("nstat", [128, 4], F32)
        uT = A.alloc("uT", [128, 8, NTM], BF16)
        xpad = A.alloc("xpad", [128, 8, NTM + 4], BF16)
        xtail = A.alloc("xtail", [128, 8, 64], F32)
        cvst = A.alloc("cvst", [128, 8, NS, 3], F32)
        S.dma("sp", cvst.t[:].rearrange("p c s k -> p (c s k)"), stconv_d, writes=[cvst.b])
        dgc = A.alloc("dgc", [128, 8, 4, 128], BF16)
        for ct_ in range(8):
            for k_ in range(4):
                S.op("act", lambda e, ct_=ct_, k_=k_: e.activation(
                    out=dgc.t[:, ct_, k_, :], in_=ident, func=AF.Copy,
                    scale=prm.t[:, P_CONV + 5 * ct_ + k_:P_CONV + 5 * ct_ + k_ + 1]), reads=[cst.b, prm.b], writes=[dgc.b])
        xsT = A.alloc("xsT", [128, 4, NTM], F32)
        BCT = A.alloc("BCT", [128, 4, NTM], BF16)
        szT = A.alloc("szT", [128, 4, NTM], BF16)
        u5Ts = [A.alloc("u5T%d" % i, [128, 4, NTM], BF16) for i in range(2)]
        dtT = A.alloc("dtT", [8, 2, NTM], F32)
        cacc = [A.alloc("cacc0", [128, NTM], F32)] * 2
        y5pre = A.alloc("y5pre", [128, 4, NTM], F32)
        g5 = A.alloc("g5", [128, 4, NTM], BF16)
        sgl = A.alloc("sgl", [128, NTM], F32)
        dtm_l = [A.alloc("dtm%d" % i, [128, 16], F32) for i in range(2)]
        acs_l = [A.alloc("acs%d" % i, [128, 8], F32) for i in range(2)]
        dec_l = [A.alloc("dec%d" % i, [128, 8], F32) for i in range(2)]
        dtdec_l = [A.alloc("dtdec%d" % i, [128, 8], F32) for i in range(2)]
        Xtm = A.alloc("Xtm", [128, 8, 64], BF16)
        Xdec = A.alloc("Xdec", [128, 8, 64], BF16)
        Btm = A.alloc("Btm", [128, 2, 128], BF16)
        big1 = A.alloc("big1", [128, 8, 128], F32)
        big2 = A.alloc("big2", [128, 8, 128], F32)
        MT = A.alloc("MT", [128, 8, 128], BF16)
        eA = A.alloc("eA", [128, 8, 128], F32)
        CdT = A.alloc("CdT", [128, 8, 128], BF16)
        ST = A.alloc("ST", [128, 8, 64], F32)
        STb = A.alloc("STb", [128, 8, 64], BF16)
        sts5 = alias("sts5", ST.t[:].rearrange("p h q -> p (h q)").rearrange("p (a s q) -> p a s q", a=2, s=16), ST.b)
        yg = A.alloc("yg", [128, 4, 128], F32)
        ysq = alias("ysq", big1.t[:, 4:8, :], big1.b)
        rsb = A.alloc("rsb", [128, 2, 128], F32)
        ysqb = A.alloc("ysqb", [128, 4, 128], BF16)
        onesb1 = A.alloc("onesb1", [128, 128], BF16)
        S.op("dve", lambda e: e.memset(onesb1.t[:], 1.0), writes=[onesb1.b])
        h0n = [alias("h0n0", xtm.t[:, 1, 0:512].rearrange("p (a n) -> p a n", n=128), xtm.sub(1)),
               alias("h0n1", xtm.t[:, 0, 0:512].rearrange("p (a n) -> p a n", n=128), xtm.sub(0))]
        h0T = [A.alloc("h0T%d" % i, [128, 8, 64], BF16) for i in range(2)]
        Bj = [A.alloc("Bj%d" % i, [128, 2, 128], BF16) for i in range(2)]
        hn = [alias("hn0", xtm.t[:, 1, 512:1024].rearrange("p (a n) -> p a n", n=128), xtm.sub(1)),
              alias("hn1", xtm.t[:, 0, 512:1024].rearrange("p (a n) -> p a n", n=128), xtm.sub(0))]
        decfm = A.alloc("decfm", [128, 4, 16], F32)
        dAx = alias("dAx", big1.t[:, 0:4, :].rearrange("p a (b c) -> p (a b) c", c=64), big1.b)
        s5g = [[A.alloc("s5g%d%d" % (j, i), [128, 512], F32) for i in range(2)] for j in range(2)]
        s5t34 = [A.alloc("s5t%d" % i, [128, 512], F32) for i in (2, 3)]
        s5vb = [A.alloc("s5vb%d" % i, [128, 512], F32) for i in range(2)]
        s5k = [0]
        s5h = [[A.alloc("s5h%d%d" % (j, i), [128, 512], BF16) for i in range(4)] for j in range(2)]
        s5CTn = A.alloc("s5CTn", [128, 16, 32], BF16)
        s5c = A.alloc("s5c", [128, 4, 16], F32)
        busd = [[A.alloc("bus%d%d" % (j, i), [128, 512], F32) for i in range(2)] for j in range(2)]
        dg5 = A.alloc("dg5", [128, 4, 128], BF16)
        for q_ in range(4):
            S.op("act", lambda e, q_=q_: e.activation(out=dg5.t[:, q_, :], in_=ident, func=AF.Copy,
                                                      scale=prm.t[:, P_S5M + q_:P_S5M + q_ + 1]),
                 reads=[cst.b, prm.b], writes=[dg5.b])
        S.op("dve", lambda e: e.tensor_scalar(out=s5CTn.t[:], in0=s5CT.t[:, 0], scalar1=-1.0, scalar2=None, op0=ALU.mult),
             reads=[s5CT.b], writes=[s5CTn.b])
        print("arena after p1a allocs: lo=%d hi=%d (words)" % (A.lo, A.hi))

        S.op("dve", lambda e: e.memset(xpad.t[:, :, 0:3], 0.0), writes=[xpad.b])
        S.op("dve", lambda e: e.memset(ST.t[:], 0.0), writes=[ST.b])
        S.op("dve", lambda e: e.memset(STb.t[:], 0.0), writes=[STb.b])

        import os as _os3
        ENG_OUTROT = _os3.environ.get("K_OUTROT", "dve")
        ENG_ADDS = _os3.environ.get("K_ADDS", "dve")
        TILES_A = [(i * 256, 256, False) for i in range(8)] + [(SEQ, 64, True)]

        def load_x(ti):
            t0, NT, is_s = TILES_A[ti]
            for blk in range((NT + 127) // 128):
                rows = min(128, NT - blk * 128)
                S.dma("sp", xtm.t[0:rows, blk, :], xin[t0 + blk * 128:t0 + blk * 128 + rows, :], writes=[xtm.sub(blk)])

        a1 = lambda kt: amod.t[:, kt, 0:1]
        sh1 = lambda kt: mod.t[:, 8 * MOD_SH1 + kt, 0:1]
        cw = lambda ct, k: prm.t[:, P_CONV + 5 * ct + k:P_CONV + 5 * ct + k + 1]
        IN_CHUNKS = [("dt", 0, 1536, 8)] + [("z", i, i * 128, 128) for i in range(4)] + \
                    [("xbc", i, 512 + i * 128, 128) for i in range(8)] + [("u5", i, 1544 + i * 128, 128) for i in range(4)]

        load_x(0)
        pbi = [0]

        def next_pb():
            pbi[0] ^= 1
            return PB[pbi[0]]

        ckpt("pre")
        def chain1(ti):
            t0, NT, is_s = TILES_A[ti]
            u5T = u5Ts[ti % 2]
            nblk = (NT + 127) // 128
            T = 128 if not is_s else 64
            tri = cst.t[0:T, C_TRI:C_TRI + T] if not is_s else cst.t[0:T, C_TRI64:C_TRI64 + T]
            neg = cst.t[0:T, C_NEG:C_NEG + T] if not is_s else cst.t[0:T, C_NEG64:C_NEG64 + T]
            sego = onesf.t[0:T, 0:T] if not is_s else cst.t[0:T, C_SEG64:C_SEG64 + T]
            segi = cst.t[0:64, C_SEGI:C_SEGI + 16]

            def dt_prep(ck):
                c0 = ck * T
                cs_ = slice(c0, c0 + T)
                dtm, acs, dec, dtdec = dtm_l[ck], acs_l[ck], dec_l[ck], dtdec_l[ck]
                pc = 0 if ck == 0 else 480
                S.op("pe", lambda e: e.transpose(PB[4].t[0:T, pc:pc + 8], dtT.t[:, 0, cs_], cst.t[0:8, C_ID:C_ID + 8]),
                     reads=[dtT.b, cst.b], writes=[PB[4].sub("sm")])
                S.op("pe", lambda e: e.transpose(PB[4].t[0:T, pc + 8:pc + 16], dtT.t[:, 1, cs_], cst.t[0:8, C_ID:C_ID + 8]),
                     reads=[dtT.b, cst.b], writes=[PB[4].sub("sm")])
                S.op("act", lambda e: e.activation(out=dtm.t[0:T, :], in_=PB[4].t[0:T, pc:pc + 16], func=AF.Copy),
                     reads=[PB[4].sub("sm")], writes=[dtm.b])
                S.op("pe", lambda e: e.matmul(PB[4].t[0:T, pc + 16:pc + 24], tri, dtm.t[0:T, 8:16], start=True, stop=True),
                     reads=[dtm.b, cst.b], writes=[PB[4].sub("sm")])
                S.op("pe", lambda e: e.matmul(PB[4].t[0:T, pc + 24:pc + 32], sego, dtm.t[0:T, 8:16], start=True, stop=True),
                     reads=[dtm.b, cst.b, onesf.b], writes=[PB[4].sub("sm")])
                S.op("act", lambda e: e.activation(out=acs.t[0:T, :], in_=PB[4].t[0:T, pc + 16:pc + 24], func=AF.Copy),
                     reads=[PB[4].sub("sm")], writes=[acs.b])
                S.op("dve", lambda e: TT(e, dec.t[0:T, :], PB[4].t[0:T, pc + 24:pc + 32], acs.t[0:T, :], ALU.subtract),
                     reads=[PB[4].sub("sm"), acs.b], writes=[dec.b])
                S.op("act", lambda e: e.activation(out=dec.t[0:T, :], in_=dec.t[0:T, :], func=AF.Exp), reads=[dec.b], writes=[dec.b])
                S.op("dve", lambda e: TT(e, dtdec.t[0:T, :], dtm.t[0:T, 0:8], dec.t[0:T, :], ALU.mult),
                     reads=[dtm.b, dec.b], writes=[dtdec.b])
            for blk in range(nblk):
                rows = min(128, NT - blk * 128)
                xb = xtm.sub(blk)
                S.op("act", lambda e, blk=blk, rows=rows: e.activation(
                    out=xn.t[0:rows, blk, :], in_=xtm.t[0:rows, blk, :], func=AF.Square, accum_out=nstat.t[0:rows, blk:blk + 1]),
                    reads=[xb], writes=[xn.sub(blk), nstat.sub(blk)])
                S.op("act", lambda e, blk=blk, rows=rows: e.activation(
                    out=nstat.t[0:rows, 2 + blk:3 + blk], in_=nstat.t[0:rows, blk:blk + 1], func=AF.Ln, scale=1.0 / D, bias=EPS),
                    reads=[nstat.sub(blk)], writes=[nstat.sub(blk)])
                S.op("act", lambda e, blk=blk, rows=rows: e.activation(out=nstat.t[0:rows, 2 + blk:3 + blk],
                                                                        in_=nstat.t[0:rows, 2 + blk:3 + blk], func=AF.Exp, scale=-0.5),
                     reads=[nstat.sub(blk)], writes=[nstat.sub(blk)])
                S.op("act", lambda e, blk=blk, rows=rows: e.activation(
                    out=xn.t[0:rows, blk, :], in_=xtm.t[0:rows, blk, :], func=AF.Copy, scale=nstat.t[0:rows, 2 + blk:3 + blk]),
                    reads=[xb, nstat.sub(blk)], writes=[xn.sub(blk)])
            ckpt("Aa%d" % ti)
            if ti + 1 < len(TILES_A):
                load_x(ti + 1)
            ckpt("Ab%d" % ti)
            for kt in range(8):
                xb_ = 2 + (kt % 2)
                pslot = PB[xb_].b
                for blk in range(nblk):
                    rows = min(128, NT - blk * 128)
                    S.op("pe", lambda e, kt=kt, blk=blk, rows=rows: e.transpose(
                        pbf(xb_)[:, blk * 128:blk * 128 + rows],
                        xn.t[0:rows, blk, kt * 128:(kt + 1) * 128], identb.t[0:rows, 0:rows]),
                        reads=[xn.sub(blk), identb.b], writes=[pslot])
                src = pbf(xb_)[:, 0:NT]
                if not is_s:
                    S.op("act", lambda e, kt=kt, src=src: e.activation(out=uT.t[:, kt, 0:NT], in_=src, func=AF.Identity,
                                                                       scale=a1(kt), bias=sh1(kt)),
                         reads=[pslot, amod.b, mod.b], writes=[uT.sub(kt)])
                else:
                    S.op("dve", lambda e, kt=kt, src=src: TT(e, cacc[0].t[:, 0:NT], src, a1x.t[:, kt, :], ALU.mult),
                         reads=[pslot, a1x.b], writes=[cacc[0].b])
                    S.op("dve", lambda e, kt=kt: TT(e, uT.t[:, kt, 0:NT], cacc[0].t[:, 0:NT], sh1x.t[:, kt, :], ALU.add),
                         reads=[cacc[0].b, sh1x.b], writes=[uT.sub(kt)])
            ckpt("A%d" % ti)
            if ti == 0:
                dump("uT", uT.t[:].rearrange("p k t -> p (k t)"), [128, 8 * NTM], uT.allb())

            yield
            if is_s:
                xps = xpad.t[:, :, 0:NS * 7].rearrange("p c (s k) -> p c s k", k=7)
                S.op("act", lambda e: e.activation(out=xps[:, :, :, 0:3], in_=cvst.t[:], func=AF.Copy), reads=[cvst.b], writes=[xpad.b])
            for (kind, i, c0, M) in IN_CHUNKS:
                yield
                pb = next_pb()
                for kt in range(8):
                    S.op("pe", lambda e, kt=kt, c0=c0, M=M, pb=pb: e.matmul(
                        pb.t[0:M, 0:NT], win_sb.t[:, kt, c0:c0 + M], uT.t[:, kt, 0:NT], start=(kt == 0), stop=(kt == 7)),
                        reads=[win_sb.b, uT.sub(kt)], writes=[pb.b])
                if kind == "z":
                    S.op("act", lambda e, i=i, pb=pb: e.activation(out=szT.t[:, i, 0:NT], in_=pb.t[:, 0:NT], func=AF.Silu),
                         reads=[pb.b], writes=[szT.b])
                elif kind == "xbc":
                    if not is_s:
                        S.op("act", lambda e, i=i, pb=pb: e.activation(out=xpad.t[:, i, 3:3 + NT], in_=pb.t[:, 0:NT], func=AF.Copy),
                             reads=[pb.b], writes=[xpad.b])
                        if ti == 7:
                            S.op("act", lambda e, i=i, pb=pb: e.activation(out=xtail.t[:, i, 0:3], in_=pb.t[:, NT - 3:NT], func=AF.Copy),
                                 reads=[pb.b], writes=[xtail.b])
                    else:
                        S.op("act", lambda e, i=i, pb=pb: e.activation(
                            out=xps[:, i, :, 3:7], in_=pb.t[:, 0:NT].rearrange("p (s k) -> p s k", k=LS), func=AF.Copy),
                            reads=[pb.b], writes=[xpad.b])
                        S.op("act", lambda e, i=i, pb=pb: e.activation(out=xtail.t[:, i, 0:NT], in_=pb.t[:, 0:NT], func=AF.Copy),
                             reads=[pb.b], writes=[xtail.b])
                elif kind == "dt":
                    S.op("act", lambda e, pb=pb: e.activation(out=dtT.t[:, 1, 0:NT], in_=pb.t[0:8, 0:NT], func=AF.Exp,
                                                              bias=ssd8.t[:, 0:1]), reads=[pb.b, ssd8.b], writes=[dtT.b])
                    S.op("act", lambda e: e.activation(out=dtT.t[:, 0, 0:NT], in_=dtT.t[:, 1, 0:NT], func=AF.Ln, bias=1.0),
                         reads=[dtT.b], writes=[dtT.b])
                    S.op("dve", lambda e: e.tensor_scalar(out=dtT.t[:, 1, 0:NT], in0=dtT.t[:, 0, 0:NT], scalar1=ssd8.t[:, 1:2],
                                                          scalar2=None, op0=ALU.mult), reads=[dtT.b, ssd8.b], writes=[dtT.b])
                    for ck_ in range(NT // T):
                        yield
                        dt_prep(ck_)
                else:
                    S.op("act", lambda e, i=i, pb=pb: e.activation(out=u5T.t[:, i, 0:NT], in_=pb.t[:, 0:NT], func=AF.Copy),
                         reads=[pb.b], writes=[u5T.b])

            ckpt("B%d" % ti)
            for ct in range(8):
                yield
                pb = next_pb()
                if not is_s:
                    xin_k = lambda k, ct=ct: xpad.t[:, ct, k:k + NT]
                    pbv = pb.t[:, 0:NT]
                    dst = xsT.t[:, ct, 0:NT] if ct < 4 else BCT.t[:, ct - 4, 0:NT]
                else:
                    xin_k = lambda k, ct=ct: xps[:, ct, :, k:k + LS]
                    pbv = pb.t[:, 0:NT].rearrange("p (s k) -> p s k", k=LS)
                    dst = (xsT.t[:, ct, 0:NT] if ct < 4 else BCT.t[:, ct - 4, 0:NT]).rearrange("p (s k) -> p s k", k=LS)
                for k in range(4):
                    S.op("pe", lambda e, k=k: e.matmul(pbv, dgc.t[:, ct, k, :], xin_k(k), start=(k == 0), stop=(k == 3)),
                         reads=[dgc.b, xpad.b], writes=[pb.b])
                S.op("act", lambda e: e.activation(out=dst, in_=pbv, func=AF.Silu, bias=cw(ct, 4)),
                     reads=[pb.b, prm.b], writes=[xsT.b if ct < 4 else BCT.b])
            ocv = o_conv.rearrange("p (c s k) -> p c s k", s=17, k=3)
            if is_s:
                S.op("act", lambda e: e.activation(out=cvst.t[:], in_=xtail.t[:].rearrange("p c (s k) -> p c s k", k=LS)[:, :, :, 1:4],
                                                   func=AF.Copy), reads=[xtail.b], writes=[cvst.b])
                S.dma("sp", ocv[:, :, 1:17, :], cvst.t[:], reads=[cvst.b], buf=cvst.b)
                outbufs.append(cvst.b)
            elif ti == 7:
                S.dma("sp", ocv[:, :, 0, :], xtail.t[:, :, 0:3], reads=[xtail.b], buf=xtail.b)
            if not is_s:
                S.op("dve", lambda e: e.tensor_copy(out=xpad.t[:, :, 0:3], in_=xpad.t[:, :, NT:NT + 3]),
                     reads=[xpad.b], writes=[xpad.b])
            if is_s:
                dump("xsS", xsT.t[:, :, 0:64], [128, 4, 64], [xsT.b])
                dump("ygS", yg.t[:, :, 0:64], [128, 4, 64], [yg.b])
            if ti == 0:
                dump("xsT", xsT.t[:].rearrange("p k t -> p (k t)"), [128, 4 * NTM], [xsT.b])
                dump("dtT", dtT.t[:].rearrange("p k t -> p (k t)"), [8, 2 * NTM], [dtT.b])

            ckpt("C%d" % ti)
            for ck in range(NT // T):
                c0 = ck * T
                cs_ = slice(c0, c0 + T)
                dtm, acs, dec, dtdec = dtm_l[ck], acs_l[ck], dec_l[ck], dtdec_l[ck]
                yield
                for pr in range(4):
                    S.op("pe", lambda e, pr=pr, cs_=cs_: e.transpose(PB[3].t[0:T, pr * 128:(pr + 1) * 128], xsT.t[:, pr, cs_], ident),
                         reads=[xsT.b, cst.b], writes=[PB[3].b])
                pxs = PB[3].t[0:T, :].rearrange("p (h q) -> p h q", q=64)
                S.op("dve", lambda e: TT(e, Xtm.t[0:T], pxs, dtm.t[0:T, 0:8].unsqueeze(2).to_broadcast([T, 8, 64]), ALU.mult),
                     reads=[PB[3].b, dtm.b], writes=[Xtm.b])
                S.op("dve", lambda e: TT(e, Xdec.t[0:T], pxs, dtdec.t[0:T, :].unsqueeze(2).to_broadcast([T, 8, 64]), ALU.mult),
                     reads=[PB[3].b, dtdec.b], writes=[Xdec.b])
                for g in range(2):
                    S.op("pe", lambda e, g=g, cs_=cs_: e.transpose(pbf(2)[0:T, g * 128:(g + 1) * 128], BCT.t[:, g, cs_], identb.t[:]),
                         reads=[BCT.b, identb.b], writes=[PB[2].sub(0)])
                S.op("act", lambda e: e.activation(out=Btm.t[0:T].rearrange("p g n -> p (g n)"), in_=pbf(2)[0:T, 0:256], func=AF.Copy),
                     reads=[PB[2].sub(0)], writes=[Btm.b])
                yield
                S.op("dve", lambda e: TT(e, big1.t[0:T, :, 0:T], tri.unsqueeze(1).to_broadcast([T, 8, T]),
                                         dtm.t[0:T, 8:16].unsqueeze(2).to_broadcast([T, 8, T]), ALU.mult),
                     reads=[cst.b, dtm.b], writes=[big1.b])
                for half in range(2):
                    S.op("pe", lambda e, half=half: e.matmul(
                        PB[3].t[:, 0:4 * T].rearrange("p (h l) -> p h l", l=T), onesf.t[0:T, :],
                        big1.t[0:T, 4 * half:4 * half + 4, 0:T], start=True, stop=True),
                        reads=[big1.b, onesf.b], writes=[PB[3].b])
                    for h in range(4 * half, 4 * half + 4):
                        S.op("dve", lambda e, h=h: e.scalar_tensor_tensor(
                            out=big2.t[0:T, h, 0:T], in0=PB[3].t[0:T, (h % 4) * T:(h % 4 + 1) * T], scalar=acs.t[0:T, h:h + 1],
                            in1=neg, op0=ALU.subtract, op1=ALU.min), reads=[PB[3].b, acs.b, cst.b], writes=[big2.b])
                    S.op("act", lambda e, half=half: e.activation(
                        out=eA.t[:, 4 * half:4 * half + 4, 0:T], in_=PB[3].t[:, 0:4 * T].rearrange("p (h l) -> p h l", l=T),
                        func=AF.Exp), reads=[PB[3].b], writes=[eA.b])
                    yield
                S.op("act", lambda e: e.activation(out=big2.t[0:T, :, 0:T], in_=big2.t[0:T, :, 0:T], func=AF.Exp),
                     reads=[big2.b], writes=[big2.b])
                yield
                for g in range(2):
                    S.op("pe", lambda e, g=g, cs_=cs_: e.matmul(PB[4].t[0:T, 32 + g * 128:32 + g * 128 + T], BCT.t[:, g, cs_],
                                                                 BCT.t[:, 2 + g, cs_], start=True, stop=True),
                         reads=[BCT.b], writes=[PB[4].sub("cb")])
                cbv = PB[4].t[0:T, 32:288].rearrange("p (g l) -> p g l", l=128)[:, :, 0:T]
                S.op("dve", lambda e: TT(e, MT.t[0:T, :, 0:T].rearrange("p (g h) l -> p g h l", h=4),
                                         cbv.unsqueeze(2).to_broadcast([T, 2, 4, T]),
                                         big2.t[0:T, :, 0:T].rearrange("p (g h) l -> p g h l", h=4), ALU.mult),
                     reads=[PB[4].sub("cb"), big2.b], writes=[MT.b])
                yield
                S.op("pool", lambda e, cs_=cs_: TT(e, CdT.t[:, :, 0:T].rearrange("p (g h) l -> p g h l", h=4),
                                                   BCT.t[:, 2:4, cs_].unsqueeze(2).to_broadcast([128, 2, 4, T]),
                                                   eA.t[:, :, 0:T].rearrange("p (g h) l -> p g h l", h=4), ALU.mult),
                     reads=[BCT.b, eA.b], writes=[CdT.b])
                yield
                ypb = PB[7]
                if is_s:
                    S.op("dve", lambda e: e.tensor_copy(out=dAx.t[0:T], in_=dtm.t[0:T, 8:16].unsqueeze(2).to_broadcast([T, 8, 64])),
                         reads=[dtm.b], writes=[dAx.b])
                    for pr in range(4):
                        S.op("pe", lambda e, pr=pr: e.matmul(PB[4].t[:, 288 + pr * 16:288 + (pr + 1) * 16],
                                                             dAx.t[0:T, 2 * pr:2 * pr + 2, :], segi, start=True, stop=True),
                             reads=[dAx.b, cst.b], writes=[PB[4].sub("dec")])
                    S.op("act", lambda e: e.activation(out=decfm.t[:].rearrange("p a s -> p (a s)"), in_=PB[4].t[:, 288:352], func=AF.Exp),
                         reads=[PB[4].sub("dec")], writes=[decfm.b])
                    stv = stssd_d.rearrange("j (pr hl) p n -> j (hl p) pr n", hl=2)
                    osv = o_ssds.rearrange("j (pr hl) p n -> j (hl p) pr n", hl=2)
                    S.dma("act", h0n[0].t[:], stv[0], writes=[h0n[0].b])
                    for j in range(NS):
                        yield
                        jj = j % 2
                        if j + 1 < NS:
                            S.dma("act", h0n[1 - jj].t[:], stv[j + 1], writes=[h0n[1 - jj].b])
                        pbt = PB[jj]
                        for pr in range(4):
                            S.op("pe", lambda e, pr=pr, jj=jj, pbt=pbt: e.transpose(pbt.t[:, pr * 128:(pr + 1) * 128], h0n[jj].t[:, pr, :], ident),
                                 reads=[h0n[jj].b, cst.b], writes=[pbt.b])
                        S.op("act", lambda e, jj=jj, pbt=pbt: e.activation(out=h0T[jj].t[:].rearrange("p h q -> p (h q)"), in_=pbt.t[:, :], func=AF.Copy),
                             reads=[pbt.b], writes=[h0T[jj].b])
                        for h in range(8):
                            pr, hl = h // 2, h % 2
                            S.op("pe", lambda e, h=h, pr=pr, hl=hl, jj=jj, j=j: e.matmul(
                                ypb.t[64 * hl:64 * hl + 64, pr * T + LS * j:pr * T + LS * j + LS], h0T[jj].t[:, h, :],
                                CdT.t[:, h, LS * j:LS * j + LS], start=(j == 0 and pr == 0), stop=False, skip_group_check=True),
                                reads=[h0T[jj].b, CdT.b], writes=[ypb.b])
                        S.op("dve", lambda e, jj=jj, j=j: e.tensor_scalar(out=Bj[jj].t[0:T], in0=Btm.t[0:T], scalar1=segi[:, j:j + 1],
                                                                          scalar2=None, op0=ALU.mult),
                             reads=[Btm.b, cst.b], writes=[Bj[jj].b])
                        pby = PB[3]
                        for pr in range(4):
                            S.op("pe", lambda e, pr=pr, jj=jj, pby=pby: e.matmul(
                                pby.t[:, pr * 128:(pr + 1) * 128], Xdec.t[0:T, 2 * pr:2 * pr + 2, :], Bj[jj].t[0:T, pr // 2, :],
                                start=True, stop=True), reads=[Xdec.b, Bj[jj].b], writes=[pby.b])
                        S.op("dve", lambda e, jj=jj, j=j: TT(e, hn[jj].t[:], h0n[jj].t[:],
                                                             decfm.t[:, :, j:j + 1].to_broadcast([128, 4, 128]), ALU.mult),
                             reads=[h0n[jj].b, decfm.b], writes=[hn[jj].b])
                        S.op("dve", lambda e, jj=jj, pby=pby: TT(e, hn[jj].t[:], hn[jj].t[:],
                                                                 pby.t[:, :].rearrange("p (a n) -> p a n", n=128), ALU.add),
                             reads=[hn[jj].b, pby.b], writes=[hn[jj].b])
                        S.dma("sp", osv[j], hn[jj].t[:], reads=[hn[jj].b], buf=hn[jj].b)
                    outbufs.extend([hn[0].b, hn[1].b])
                for h in range(8):
                    pr, hl = h // 2, h % 2
                    out = ypb.t[64 * hl:64 * hl + 64, pr * T:(pr + 1) * T]
                    S.op("pe", lambda e, h=h, out=out, pr=pr: e.matmul(out, Xtm.t[0:T, h, :], MT.t[0:T, h, 0:T],
                                                                       start=(pr == 0 and not is_s), stop=is_s, skip_group_check=True),
                         reads=[Xtm.b, MT.b], writes=[ypb.b])
                    if not is_s:
                        S.op("pe", lambda e, h=h, out=out: e.matmul(out, STb.t[:, h, :], CdT.t[:, h, 0:T], start=False, stop=True,
                                                                    skip_group_check=True),
                             reads=[STb.b, CdT.b], writes=[ypb.b])
                yield
                for pr in range(4):
                    S.op("dve", lambda e, pr=pr, cs_=cs_: e.scalar_tensor_tensor(
                        out=yg.t[:, pr, 0:T], in0=xsT.t[:, pr, cs_], scalar=prm.t[:, P_SSDFM + pr:P_SSDFM + pr + 1],
                        in1=ypb.t[:, pr * T:(pr + 1) * T], op0=ALU.mult, op1=ALU.add),
                        reads=[xsT.b, prm.b, ypb.b], writes=[yg.b])
                S.op("dve", lambda e, cs_=cs_: TT(e, yg.t[:, :, 0:T], yg.t[:, :, 0:T], szT.t[:, :, cs_], ALU.mult),
                     reads=[yg.b, szT.b], writes=[yg.b])
                S.op("dve", lambda e: TT(e, ysqb.t[:, :, 0:T], yg.t[:, :, 0:T], yg.t[:, :, 0:T], ALU.mult),
                     reads=[yg.b], writes=[ysqb.b])
                for g in range(2):
                    for k in range(2):
                        S.op("pe", lambda e, g=g, k=k: e.matmul(PB[3].t[:, g * T:(g + 1) * T], onesb1.t[:], ysqb.t[:, 2 * g + k, 0:T],
                                                                start=(k == 0), stop=(k == 1)),
                             reads=[onesb1.b, ysqb.b], writes=[PB[3].b])
                S.op("act", lambda e: e.activation(out=rsb.t[:, :, 0:T], in_=PB[3].t[:, 0:2 * T].rearrange("p (g l) -> p g l", l=T),
                                                   func=AF.Ln, scale=1.0 / 256, bias=EPS), reads=[PB[3].b], writes=[rsb.b])
                S.op("act", lambda e: e.activation(out=rsb.t[:, :, 0:T], in_=rsb.t[:, :, 0:T], func=AF.Exp, scale=-0.5),
                     reads=[rsb.b], writes=[rsb.b])
                for pr in range(4):
                    S.op("dve", lambda e, pr=pr: e.scalar_tensor_tensor(
                        out=mixt[ti % 2].t[:, pr, c0:c0 + T], in0=yg.t[:, pr, 0:T],
                        scalar=prm.t[:, P_SSDFM + 4 + pr:P_SSDFM + 5 + pr], in1=rsb.t[:, pr // 2, 0:T], op0=ALU.mult, op1=ALU.mult),
                        reads=[yg.b, prm.b, rsb.b], writes=[mixt[ti % 2].sub("ssd")])
                yield
                if not is_s:
                    for g in range(2):
                        S.op("pe", lambda e, g=g: e.matmul(PB[6].t[:, g * 256:(g + 1) * 256], Btm.t[0:T, g, :],
                                                           Xdec.t[0:T, 4 * g:4 * g + 4, :], start=True, stop=True),
                             reads=[Btm.b, Xdec.b], writes=[PB[6].b])
                    S.op("dve", lambda e: TT(e, ST.t[:], ST.t[:], eA.t[:, :, T - 1:T].to_broadcast([128, 8, 64]), ALU.mult),
                         reads=[ST.b, eA.b], writes=[ST.b])
                    S.op("dve", lambda e: TT(e, ST.t[:], ST.t[:], PB[6].t[:, :].rearrange("p (h q) -> p h q", q=64), ALU.add),
                         reads=[ST.b, PB[6].b], writes=[ST.b])
                    S.op("act", lambda e: e.activation(out=STb.t[:], in_=ST.t[:], func=AF.Copy), reads=[ST.b], writes=[STb.b])
            if ti == 7:
                S.dma("sp", o_ssdp, ST.t[:].rearrange("p h q -> p (h q)"), reads=[ST.b], buf=ST.b)
                outbufs.append(ST.b)

            ckpt("D%d" % ti)
            yield

        def chain2(ti):
            t0, NT, is_s = TILES_A[ti]
            u5T = u5Ts[ti % 2]
            if is_s:
                S.dma("sp", sts5.t[:].rearrange("p a s q -> p (a s q)"), sts5_d, writes=[sts5.b])
            if not is_s:
                groups = [(list(range(16)), k * T5, T5) for k in range(NT // T5)]
            else:
                groups = [(list(range(8)), 0, 64), (list(range(8, 16)), 0, 64)]
            def emit_bu(g_):
                slist_, tk0_, ntok_ = groups[g_]
                bus = busd[g_ % 2]
                for part, pb in ((0, PB[5]), (1, PB[6])):
                    for idx, s in enumerate(slist_):
                        S.op("pe", lambda e, part=part, pb=pb, idx=idx, s=s: e.matmul(
                            pb.t[:, idx * ntok_:(idx + 1) * ntok_], s5BT.t[:, part, s, :], u5T.t[:, s // 4, tk0_:tk0_ + ntok_],
                            start=True, stop=True), reads=[s5BT.b, u5T.b], writes=[pb.b])
                S.op("act", lambda e: e.activation(out=bus[0].t[:], in_=PB[5].t[:, :], func=AF.Copy), reads=[PB[5].b], writes=[bus[0].b])
                S.op("act", lambda e: e.activation(out=bus[1].t[:], in_=PB[6].t[:, :], func=AF.Copy), reads=[PB[6].b], writes=[bus[1].b])
            def views(g_):
                slist_, tk0_, ntok_ = groups[g_]
                s0_ = slist_[0]
                if not is_s:
                    V3 = lambda ap: ap.rearrange("p (s t) -> p s t", t=T5)
                    QR, QI = Qtab.t[:, 0], Qtab.t[:, 1]
                    PR_, PI_ = Ptab.t[:, 0], Ptab.t[:, 1]
                    msk = mask32.t[:].rearrange("p s t -> p (s t)")
                    first = lambda ap: V3(ap)[:, :, 0]
                    cin_r, cin_i = s5cr.t[:, 0, :], s5cr.t[:, 1, :]
                else:
                    V3 = lambda ap: ap.rearrange("p (s q b) -> p s q b", q=NS, b=LS)
                    bc = lambda ap: ap.unsqueeze(2).to_broadcast([128, 8, NS, LS])
                    QR, QI = bc(Qtab.t[:, 0, s0_:s0_ + 8, 0:LS]), bc(Qtab.t[:, 1, s0_:s0_ + 8, 0:LS])
                    PR_, PI_ = bc(Ptab.t[:, 0, s0_:s0_ + 8, 0:LS]), bc(Ptab.t[:, 1, s0_:s0_ + 8, 0:LS])
                    msk = mask4.t[:].rearrange("p s t -> p (s t)")
                    first = lambda ap: V3(ap)[:, :, :, 0]
                    cin_r, cin_i = sts5.t[:, 0, s0_:s0_ + 8, :], sts5.t[:, 1, s0_:s0_ + 8, :]
                return V3, QR, QI, PR_, PI_, msk, first, cin_r, cin_i
            vsets = [[s5v[0], s5v[1]], [s5vb[0], s5vb[1]]]

            def mults_adds(g_):
                V3, QR, QI, PR_, PI_, msk, first, cin_r, cin_i = views(g_)
                bus = busd[g_ % 2]
                br, bi = V3(bus[0].t[:]), V3(bus[1].t[:])
                t1, t2, t3, t4 = s5t[0], s5t[1], s5t34[0], s5t34[1]
                vr, vi = vsets[g_ % 2]
                tb = [Qtab.b]
                for (o, a, b_, rd) in ((t1, QR, br, bus[0].b), (t2, QI, bi, bus[1].b), (t3, QR, bi, bus[1].b), (t4, QI, br, bus[0].b)):
                    S.op("dve", lambda e, o=o, a=a, b_=b_: TT(e, V3(o.t[:]), a, b_, ALU.mult), reads=tb + [rd], writes=[o.b])
                S.op(ENG_ADDS, lambda e: TT(e, vr.t[:], t1.t[:], t2.t[:], ALU.subtract), reads=[t1.b, t2.b], writes=[vr.b])
                S.op(ENG_ADDS, lambda e: TT(e, vi.t[:], t3.t[:], t4.t[:], ALU.add), reads=[t3.b, t4.b], writes=[vi.b])
            emit_bu(0)
            if len(groups) > 1:
                emit_bu(1)
            mults_adds(0)
            pend_y5 = [None]
            for gi_, (slist, tk0, ntok) in enumerate(groups):
                yield
                ns = len(slist)
                s0 = slist[0]
                V3, QR, QI, PR_, PI_, msk, first, cin_r, cin_i = views(gi_)
                vr, vi = vsets[gi_ % 2]
                if gi_ + 1 < len(groups):
                    mults_adds(gi_ + 1)
                    yield
                if gi_ + 2 < len(groups):
                    emit_bu(gi_ + 2)
                S.op("dve", lambda e: TT(e, first(vr.t[:]), first(vr.t[:]), cin_r, ALU.add), reads=[vr.b, s5cr.b, sts5.b], writes=[vr.b])
                S.op("dve", lambda e: TT(e, first(vi.t[:]), first(vi.t[:]), cin_i, ALU.add), reads=[vi.b, s5cr.b, sts5.b], writes=[vi.b])
                yield
                s5k[0] ^= 1
                gr, gi2 = s5g[s5k[0]][0], s5g[s5k[0]][1]
                S.op("dve", lambda e: e.tensor_tensor_scan(out=gr.t[:], data0=msk, data1=vr.t[:], initial=0.0, op0=ALU.mult, op1=ALU.add),
                     reads=[vr.b, mask32.b, mask4.b], writes=[gr.b])
                S.op("dve", lambda e: e.tensor_tensor_scan(out=gi2.t[:], data0=msk, data1=vi.t[:], initial=0.0, op0=ALU.mult, op1=ALU.add),
                     reads=[vi.b, mask32.b, mask4.b], writes=[gi2.b])
                yield
                hp = s5h[gi_ % 2]
                hr, hi = hp, hp
                for (o, a, b_) in ((hp[0], PR_, gr), (hp[1], PI_, gi2), (hp[2], PR_, gi2), (hp[3], PI_, gr)):
                    S.op(ENG_OUTROT, lambda e, o=o, a=a, b_=b_: TT(e, V3(o.t[:]), a, V3(b_.t[:]), ALU.mult),
                         reads=[Ptab.b, b_.b], writes=[o.b])
                yield
                if not is_s:
                    glr, gli = V3(gr.t[:])[:, :, T5 - 1], V3(gi2.t[:])[:, :, T5 - 1]
                    plr, pli = Ptab.t[:, 0, :, T5 - 1], Ptab.t[:, 1, :, T5 - 1]
                    c_ = lambda i: s5c.t[:, i, :]
                    outr, outi = s5cr.t[:, 0, :], s5cr.t[:, 1, :]
                else:
                    glr, gli = V3(gr.t[:])[:, :, :, LS - 1], V3(gi2.t[:])[:, :, :, LS - 1]
                    plr = Ptab.t[:, 0, s0:s0 + 8, LS - 1:LS].to_broadcast([128, 8, NS])
                    pli = Ptab.t[:, 1, s0:s0 + 8, LS - 1:LS].to_broadcast([128, 8, NS])
                    c_ = lambda i: hn[0].t[:, i, :].rearrange("p (s q) -> p s q", q=NS)
                    outr, outi = s5fin.t[:, 0, s0:s0 + 8, 1:17], s5fin.t[:, 1, s0:s0 + 8, 1:17]
                cb_ = [s5c.b, hn[0].b]
                cseq = [(c_(0), plr, glr, ALU.mult), (c_(1), pli, gli, ALU.mult), (c_(2), plr, gli, ALU.mult), (c_(3), pli, glr, ALU.mult)]
                for (o, a, b, op) in cseq:
                    S.op("dve", lambda e, o=o, a=a, b=b, op=op: TT(e, o, a, b, op), reads=[Ptab.b, gr.b, gi2.b] + cb_, writes=cb_)
                S.op("dve", lambda e: TT(e, outr, c_(0), c_(1), ALU.subtract), reads=cb_, writes=[s5cr.b, s5fin.b])
                S.op("dve", lambda e: TT(e, outi, c_(2), c_(3), ALU.add), reads=cb_, writes=[s5cr.b, s5fin.b])
                yield
                def emit_y5(gi_=gi_, slist=slist, tk0=tk0, ntok=ntok, hr=hr, hi=hi):
                    y5c0 = 352
                    nq = 4 if not is_s else 2
                    for qi in range(nq):
                        q = qi if not is_s else 2 * gi_ + qi
                        S.op("pe", lambda e, q=q, qi=qi: e.matmul(PB[4].t[:, y5c0 + qi * ntok:y5c0 + (qi + 1) * ntok], dg5.t[:, q, :],
                                                                  u5T.t[:, q, tk0:tk0 + ntok], start=(qi == 0), stop=False, skip_group_check=True),
                             reads=[dg5.b, u5T.b], writes=[PB[4].sub("y5")])
                    for idx, s in enumerate(slist):
                        qi = (s // 4) if not is_s else (s // 4 - 2 * gi_)
                        out = PB[4].t[32 * (s % 4):32 * (s % 4) + 32, y5c0 + qi * ntok:y5c0 + (qi + 1) * ntok]
                        for j4, lw in enumerate((s5CT.t[:, 0, s, :], s5CTn.t[:, s, :], s5CT.t[:, 1, s, :], s5CT.t[:, 1, s, :])):
                            S.op("pe", lambda e, j4=j4, lw=lw: e.matmul(out, lw, hr[j4].t[:, idx * ntok:(idx + 1) * ntok],
                                                                        start=False, stop=(j4 == 3), skip_group_check=True,
                                                                        tile_position=(0, 32 * (s % 4))),
                                 reads=[s5CT.b, s5CTn.b, hr[j4].b], writes=[PB[4].sub("y5")])
                    q0 = 0 if not is_s else 2 * gi_
                    S.op("act", lambda e: e.activation(out=y5pre.t[:, q0:q0 + nq, tk0:tk0 + ntok],
                                                       in_=PB[4].t[:, y5c0:y5c0 + nq * ntok].rearrange("p (q t) -> p q t", t=ntok), func=AF.Copy),
                         reads=[PB[4].sub("y5")], writes=[y5pre.b])
                if pend_y5[0] is not None:
                    pend_y5[0]()
                    yield
                pend_y5[0] = emit_y5
            if pend_y5[0] is not None:
                pend_y5[0]()
                pend_y5[0] = None
                yield
            if ti == 7:
                S.op("dve", lambda e: e.tensor_copy(out=s5fin.t[:, :, :, 0], in_=s5cr.t[:]), reads=[s5cr.b], writes=[s5fin.b])
            if is_s:
                S.dma("sp", o_s5, s5fin.t[:].rearrange("p a s q -> p (a s q)"), reads=[s5fin.b], buf=s5fin.b)
                outbufs.append(s5fin.b)
            if ti == 0:
                dump("y5pre", y5pre.t[:].rearrange("p k t -> p (k t)"), [128, 4 * NTM], [y5pre.b])
            ckpt("E%d" % ti)
            yield
            S.op("act", lambda e: e.activation(out=g5.t[:, :, 0:NT], in_=y5pre.t[:, :, 0:NT], func=AF.Gelu), reads=[y5pre.b], writes=[g5.b])
            for m in range(4):
                yield
                pb = next_pb()
                for q in range(4):
                    S.op("pe", lambda e, m=m, q=q, pb=pb: e.matmul(pb.t[:, 0:NT], wglu_sb.t[:, q, m * 128:(m + 1) * 128], g5.t[:, q, 0:NT],
                                                                   start=(q == 0), stop=(q == 3)),
                         reads=[wglu_sb.b, g5.b], writes=[pb.b])
                S.op("act", lambda e, m=m, pb=pb: e.activation(out=sgl.t[:, 0:NT], in_=pb.t[:, 0:NT], func=AF.Sigmoid,
                                                               bias=prm.t[:, P_S5M + 4 + m:P_S5M + 5 + m]),
                     reads=[pb.b, prm.b], writes=[sgl.b])
                S.op("dve", lambda e, m=m: TT(e, mixt[ti % 2].t[:, 4 + m, 0:NT], g5.t[:, m, 0:NT], sgl.t[:, 0:NT], ALU.mult),
                     reads=[g5.b, sgl.b], writes=[mixt[ti % 2].sub("s5")])
            S.dma("sp", mixd[:, :, t0:t0 + NT], mixt[ti % 2].t[:, :, 0:NT], reads=mixt[ti % 2].allb(), writes=[mixdb[ti]], buf=mixdb[ti])
            ckpt("T%d" % ti)
            if ti == 0:
                dump("mix0", mixt[0].t[:, :, 0:NTM], [128, 8, NTM], mixt[0].allb())
            yield

        import os as _os
        RATIO = int(_os.environ.get("K_RATIO", "1"))

        def drive(gens, ada_every=0):
            gens = [g for g in gens if g is not None]
            n = 0
            while gens:
                for gi__, g in enumerate(list(gens)):
                    for _ in range((RATIO if gi__ == 0 else 1) if RATIO > 0 else (-RATIO if gi__ == 1 else 1)):
                        try:
                            next(g)
                        except StopIteration:
                            if g in gens:
                                gens.remove(g)
                            break
                n += 1
                if ada_every and n % ada_every == 0:
                    ada_step()
        ada_state[0] = 0
        drive([chain1(0)], ada_every=12)
        for ti_ in range(len(TILES_A)):
            if ti_ == 7:
                while ada_state[1] < len(ADA_CH):
                    ada_step()
                fill_x(a1x, amod.t[:, 0:8, 1:17], [amod.b])
                fill_x(sh1x, chunkmod(MOD_SH1)[:, :, 1:17], [mod.b])
                make_amod([(1, (4, 1)), (2, (7, 2))])
            drive([chain2(ti_), chain1(ti_ + 1) if ti_ + 1 < len(TILES_A) else None], ada_every=(10 if ti_ < 7 else 0))
        dump("mixS", mixt[0].t[:, :, 0:64], [128, 8, 64], mixt[0].allb())
        S.barrier()
        ckpt("1a")
        A.lo = LO_P1
        x1T = A.alloc("x1T", [128, 8, NTOK], F32, top=True)
        vT = A.alloc("vT", [128, 8, NTOK], BF16, top=True)
        wout_sb = A.alloc("wout_sb", [128, 8, D], BF16)
        wout_v = wout.rearrange("(kt p) n -> p kt n", p=128)
        for kh in range(4):
            S.dma("pool", wout_sb.t[:, 2 * kh:2 * kh + 2, :], wout_v[:, 2 * kh:2 * kh + 2, :], writes=[wout_sb.sub(kh)])
        mixb = [A.alloc("mixb%d" % i, [128, 8, 512], BF16) for i in range(2)]

        def load_mix(ti):
            t0, NT, is_s = TILES_B[ti]
            tiles_a = [i for i, (a0, n0, s0_) in enumerate(TILES_A) if a0 >= t0 and a0 < t0 + NT]
            S.dma("sp", mixb[ti % 2].t[:, :, 0:NT], mixd[:, :, t0:t0 + NT], reads=[mixdb[i] for i in tiles_a], writes=[mixb[ti % 2].b])
        xtm2 = A.alloc("xtm2", [128, 4, D], F32)
        xTm = [A.alloc("xTm%d" % i, [128, 512], F32) for i in range(2)]
        sqb = [A.alloc("sqb%d" % i, [128, 512], BF16) for i in range(2)]
        onesb = A.alloc("onesb", [128, 128], BF16)
        S.op("dve", lambda e: e.memset(onesb.t[:], 1.0), writes=[onesb.b])
        tmp2 = [A.alloc("tmp2_%d" % i, [128, 512], F32) for i in range(2)]
        rstdb = [A.alloc("rstdb%d" % i, [128, 512], F32) for i in range(2)]
        g1x = expand_mod("g1x", chunkmod(MOD_G1)[:, :, 1:17], [mod.b])
        a2x = expand_mod("a2x", amod.t[:, 8:16, 1:17], [amod.b])
        sh2x = expand_mod("sh2x", chunkmod(MOD_SH2)[:, :, 1:17], [mod.b])
        print("arena p1b: lo=%d hi=%d" % (A.lo, A.hi))
        TILES_B = [(i * 512, 512, False) for i in range(4)] + [(SEQ, 64, True)]

        def load_x2(ti):
            t0, NT, is_s = TILES_B[ti]
            for blk in range((NT + 127) // 128):
                rows = min(128, NT - blk * 128)
                S.dma("sp", xtm2.t[0:rows, blk, :], xin[t0 + blk * 128:t0 + blk * 128 + rows, :], writes=[xtm2.sub(blk)])
        load_x2(0)
        load_mix(0)

        def stat_accum(src_ap, m, NT, pbs):
            sq = sqb[m % 2]
            S.op("act", lambda e: e.activation(out=sq.t[:, 0:NT], in_=src_ap, func=AF.Square), reads=[x1T.sub(m)], writes=[sq.b])
            S.op("pe", lambda e: e.matmul(pbs.t[:, 0:NT], onesb.t[:], sq.t[:, 0:NT], start=(m == 0), stop=(m == 7)),
                 reads=[onesb.b, sq.b], writes=[pbs.b])

        def stat_finish(NT, pbs, rs):
            S.op("act", lambda e: e.activation(out=rs.t[:, 0:NT], in_=pbs.t[:, 0:NT], func=AF.Ln, scale=1.0 / D, bias=EPS),
                 reads=[pbs.b], writes=[rs.b])
            S.op("act", lambda e: e.activation(out=rs.t[:, 0:NT], in_=rs.t[:, 0:NT], func=AF.Exp, scale=-0.5), reads=[rs.b], writes=[rs.b])

        def b_part1(ti):
            t0, NT, is_s = TILES_B[ti]
            nblk = (NT + 127) // 128
            tsl = slice(t0, t0 + NT)
            pbs = PB[4 + ti % 2]
            for m in range(8):
                pbx = PB[2 + m % 2]
                xm = xTm[m % 2]
                for blk in range(nblk):
                    rows = min(128, NT - blk * 128)
                    S.op("pe", lambda e, blk=blk, rows=rows: e.transpose(
                        pbx.t[:, blk * 128:blk * 128 + rows], xtm2.t[0:rows, blk, m * 128:(m + 1) * 128], cst.t[0:rows, C_ID:C_ID + rows]),
                        reads=[xtm2.sub(blk), cst.b], writes=[pbx.b])
                S.op("act", lambda e: e.activation(out=xm.t[:, 0:NT], in_=pbx.t[:, 0:NT], func=AF.Copy), reads=[pbx.b], writes=[xm.b])
                pb = next_pb()
                for kt in range(8):
                    S.op("pe", lambda e, kt=kt: e.matmul(pb.t[:, 0:NT], wout_sb.t[:, kt, m * 128:(m + 1) * 128], mixb[ti % 2].t[:, kt, 0:NT],
                                                         start=(kt == 0), stop=(kt == 7)),
                         reads=[wout_sb.sub(kt // 2), mixb[ti % 2].b], writes=[pb.b])
                if m == 0 and ti + 1 < len(TILES_B):
                    load_mix(ti + 1)
                if not is_s:
                    S.op("dve", lambda e: e.scalar_tensor_tensor(
                        out=x1T.t[:, m, tsl], in0=pb.t[:, 0:NT], scalar=mod.t[:, 8 * MOD_G1 + m, 0:1], in1=xm.t[:, 0:NT],
                        op0=ALU.mult, op1=ALU.add), reads=[pb.b, mod.b, xm.b], writes=[x1T.sub(m)])
                else:
                    S.op("dve", lambda e: TT(e, tmp2[0].t[:, 0:NT], pb.t[:, 0:NT], g1x.t[:, m, :], ALU.mult),
                         reads=[pb.b, g1x.b], writes=[tmp2[0].b])
                    S.op("dve", lambda e: TT(e, x1T.t[:, m, tsl], tmp2[0].t[:, 0:NT], xm.t[:, 0:NT], ALU.add),
                         reads=[tmp2[0].b, xm.b], writes=[x1T.sub(m)])
                stat_accum(x1T.t[:, m, tsl], m, NT, pbs)
                yield
            if ti + 1 < len(TILES_B):
                load_x2(ti + 1)
            yield

        def b_part2(ti):
            t0, NT, is_s = TILES_B[ti]
            tsl = slice(t0, t0 + NT)
            rs = rstdb[ti % 2]
            stat_finish(NT, PB[4 + ti % 2], rs)
            yield
            for m in range(8):
                tq = tmp2[m % 2]
                S.op("dve", lambda e: TT(e, tq.t[:, 0:NT], x1T.t[:, m, tsl], rs.t[:, 0:NT], ALU.mult),
                     reads=[x1T.sub(m), rs.b], writes=[tq.b])
                if not is_s:
                    S.op("act", lambda e: e.activation(out=vT.t[:, m, tsl], in_=tq.t[:, 0:NT], func=AF.Identity,
                                                       scale=amod.t[:, 8 + m, 0:1], bias=mod.t[:, 8 * MOD_SH2 + m, 0:1]),
                         reads=[tq.b, amod.b, mod.b], writes=[vT.sub(m)])
                else:
                    S.op("dve", lambda e: TT(e, tq.t[:, 0:NT], tq.t[:, 0:NT], a2x.t[:, m, :], ALU.mult),
                         reads=[tq.b, a2x.b], writes=[tq.b])
                    S.op("dve", lambda e: TT(e, vT.t[:, m, tsl], tq.t[:, 0:NT], sh2x.t[:, m, :], ALU.add),
                         reads=[tq.b, sh2x.b], writes=[vT.sub(m)])
                yield
            if ti == 0:
                dump("x1p", x1T.t[:, :, 0:256], [128, 8, 256], x1T.allb())
                dump("vp", vT.t[:, :, 0:256], [128, 8, 256], vT.allb())
        drive([b_part1(0)])
        for ti_ in range(len(TILES_B)):
            drive([b_part2(ti_), b_part1(ti_ + 1) if ti_ + 1 < len(TILES_B) else None])
        S.barrier()
        ckpt("1b")

        A.lo = LO_GLOBAL
        tmp2 = [A.alloc("tmp3_%d" % i, [128, 512], F32) for i in range(2)]
        rstdb = [A.alloc("rstd3_%d" % i, [128, 512], F32) for i in range(2)]
        sqb = [A.alloc("sqb3_%d" % i, [128, 512], BF16) for i in range(2)]
        onesb = A.alloc("onesb3", [128, 128], BF16)
        S.op("dve", lambda e: e.memset(onesb.t[:], 1.0), writes=[onesb.b])
        g2x = expand_mod("g2x", chunkmod(MOD_G2)[:, :, 1:17], [mod.b])
        afx = expand_mod("afx", amod.t[:, 16:24, 1:17], [amod.b])
        shfx = expand_mod("shfx", chunkmod(MOD_SHF)[:, :, 1:17], [mod.b])
        LO_P2 = A.lo
        hT = A.alloc("hT", [128, 6, NTOK], BF16)
        wgs = [A.alloc("wgs%d" % i, [128, 8, 256], BF16) for i in range(3)]
        wus = [A.alloc("wus%d" % i, [128, 8, 256], BF16) for i in range(3)]
        wds = [A.alloc("wds%d" % i, [128, 6, D], BF16) for i in range(2)]
        sgt = [A.alloc("sgt%d" % i, [128, 512], BF16) for i in range(2)]
        print("arena p2: lo=%d hi=%d" % (A.lo, A.hi))
        wg_v = wg.rearrange("(kt p) n -> p kt n", p=128)
        wu_v = wu.rearrange("(kt p) n -> p kt n", p=128)
        wd_v = wd.rearrange("(j p) n -> p j n", p=128)
        QUARTERS = [(0, 6), (6, 12), (12, 18), (18, 22)]
        SLABS = [(q, ja + 2 * s) for q, (ja, jb) in enumerate(QUARTERS) for s in range((jb - ja) // 2)]

        def load_gu(si):
            q, j0 = SLABS[si]
            S.dma("pool", wgs[si % 3].t[:], wg_v[:, :, j0 * 128:(j0 + 2) * 128], writes=[wgs[si % 3].b])
            S.dma("pool", wus[si % 3].t[:], wu_v[:, :, j0 * 128:(j0 + 2) * 128], writes=[wus[si % 3].b])

        def load_wd(q):
            ja, jb = QUARTERS[q]
            for jh in range(0, jb - ja, 2):
                S.dma("pool", wds[q % 2].t[:, jh:jh + 2, :], wd_v[:, ja + jh:ja + jh + 2, :], writes=[wds[q % 2].b])
        load_gu(0)
        load_gu(1)
        load_wd(0)
        gbank = [0]
        si = 0
        for q, (ja, jb) in enumerate(QUARTERS):
            if q + 1 < 4:
                load_wd(q + 1)
            for s in range((jb - ja) // 2):
                if si + 2 < len(SLABS):
                    load_gu(si + 2)
                wgt, wut = wgs[si % 3], wus[si % 3]
                for jc in range(2):
                    jj = 2 * s + jc
                    for (t0, NT, is_s) in TILES_B:
                        tsl = slice(t0, t0 + NT)
                        gbank[0] ^= 1
                        pbg, pbu = PB[gbank[0]], PB[2 + gbank[0]]
                        for (wt, pb_) in ((wgt, pbg), (wut, pbu)):
                            for kt in range(8):
                                S.op("pe", lambda e, kt=kt, wt=wt, pb_=pb_: e.matmul(
                                    pb_.t[:, 0:NT], wt.t[:, kt, jc * 128:(jc + 1) * 128], vT.t[:, kt, tsl], start=(kt == 0), stop=(kt == 7)),
                                    reads=[wt.b] + vT.allb(), writes=[pb_.b])
                        sg_ = sgt[gbank[0]]
                        S.op("act", lambda e, pbg=pbg, sg_=sg_: e.activation(out=sg_.t[:, 0:NT], in_=pbg.t[:, 0:NT], func=AF.Silu),
                             reads=[pbg.b], writes=[sg_.b])
                        S.op("dve", lambda e, pbu=pbu, sg_=sg_: TT(e, hT.t[:, jj, tsl], sg_.t[:, 0:NT], pbu.t[:, 0:NT], ALU.mult),
                             reads=[sg_.b, pbu.b], writes=[hT.sub(jj)])
                si += 1
            nj = jb - ja
            wdt = wds[q % 2]
            for (t0, NT, is_s) in TILES_B:
                tsl = slice(t0, t0 + NT)
                for m in range(8):
                    pb = PB[4 + m % 2]
                    for jj in range(nj):
                        S.op("pe", lambda e, jj=jj, m=m, pb=pb: e.matmul(pb.t[:, 0:NT], wdt.t[:, jj, m * 128:(m + 1) * 128], hT.t[:, jj, tsl],
                                                                         start=(jj == 0), stop=(jj == nj - 1)),
                             reads=[wdt.b, hT.sub(jj)], writes=[pb.b])
                    if not is_s:
                        S.op("dve", lambda e, m=m, pb=pb: e.scalar_tensor_tensor(
                            out=x1T.t[:, m, tsl], in0=pb.t[:, 0:NT], scalar=mod.t[:, 8 * MOD_G2 + m, 0:1], in1=x1T.t[:, m, tsl],
                            op0=ALU.mult, op1=ALU.add), reads=[pb.b, mod.b, x1T.sub(m)], writes=[x1T.sub(m)])
                    else:
                        S.op("dve", lambda e, m=m, pb=pb: TT(e, tmp2[0].t[:, 0:NT], pb.t[:, 0:NT], g2x.t[:, m, :], ALU.mult),
                             reads=[pb.b, g2x.b], writes=[tmp2[0].b])
                        S.op("dve", lambda e, m=m: TT(e, x1T.t[:, m, tsl], tmp2[0].t[:, 0:NT], x1T.t[:, m, tsl], ALU.add),
                             reads=[tmp2[0].b, x1T.sub(m)], writes=[x1T.sub(m)])
        S.barrier()
        ckpt("ffn")
        A.lo = LO_P2
        yTs = [A.alloc("yT%d" % i, [128, 8, 512], F32) for i in range(2)]
        ytm = [A.alloc("ytm%d" % i, [128, D], F32) for i in range(2)]
        print("arena final: lo=%d hi=%d" % (A.lo, A.hi))
        oi = [0]

        def f_part1(ti):
            t0, NT, is_s = TILES_B[ti]
            tsl = slice(t0, t0 + NT)
            yT = yTs[ti % 2]
            pbs = PB[6 + ti % 2]
            rs = rstdb[ti % 2]
            for m in range(8):
                stat_accum(x1T.t[:, m, tsl], m, NT, pbs)
                if m % 2 == 1:
                    yield
            stat_finish(NT, pbs, rs)
            yield
            for m in range(8):
                tq = tmp2[m % 2]
                S.op("dve", lambda e: TT(e, tq.t[:, 0:NT], x1T.t[:, m, tsl], rs.t[:, 0:NT], ALU.mult),
                     reads=[x1T.sub(m), rs.b], writes=[tq.b])
                if not is_s:
                    S.op("act", lambda e: e.activation(out=yT.t[:, m, 0:NT], in_=tq.t[:, 0:NT], func=AF.Identity,
                                                       scale=amod.t[:, 16 + m, 0:1], bias=mod.t[:, 8 * MOD_SHF + m, 0:1]),
                         reads=[tq.b, amod.b, mod.b], writes=[yT.sub(m)])
                else:
                    S.op("dve", lambda e: TT(e, tq.t[:, 0:NT], tq.t[:, 0:NT], afx.t[:, m, :], ALU.mult),
                         reads=[tq.b, afx.b], writes=[tq.b])
                    S.op("dve", lambda e: TT(e, yT.t[:, m, 0:NT], tq.t[:, 0:NT], shfx.t[:, m, :], ALU.add),
                         reads=[tq.b, shfx.b], writes=[yT.sub(m)])
                yield

        def f_part2(ti):
            t0, NT, is_s = TILES_B[ti]
            yT = yTs[ti % 2]
            for blk in range((NT + 127) // 128):
                rows = min(128, NT - blk * 128)
                yo = ytm[oi[0] % 2]
                oi[0] += 1
                for half in range(2):
                    pbt = PB[half]
                    for k4 in range(4):
                        kt = 4 * half + k4
                        S.op("pe", lambda e, kt=kt, k4=k4: e.transpose(
                            pbt.t[0:rows, k4 * 128:(k4 + 1) * 128], yT.t[:, kt, blk * 128:blk * 128 + rows], ident),
                            reads=[yT.sub(kt), cst.b], writes=[pbt.b])
                    if half == 0:
                        S.op("act", lambda e: e.activation(out=yo.t[0:rows, 0:512], in_=pbt.t[0:rows, :], func=AF.Copy),
                             reads=[pbt.b], writes=[yo.b])
                    else:
                        S.op("dve", lambda e: e.tensor_copy(out=yo.t[0:rows, 512:1024], in_=pbt.t[0:rows, :]),
                             reads=[pbt.b], writes=[yo.b])
                    yield
                S.dma("sp", yout[t0 + blk * 128:t0 + blk * 128 + rows, :], yo.t[0:rows, :], reads=[yo.b], buf=yo.b)
        import os as _os2
        if True:
            for ti_ in range(len(TILES_B)):
                drive([f_part1(ti_)])
                drive([f_part2(ti_)])
        else:
            drive([f_part1(0)])
            for ti_ in range(len(TILES_B)):
                drive([f_part2(ti_), f_part1(ti_ + 1) if ti_ + 1 < len(TILES_B) else None])
        S.barrier()
    return nc, dumps


def _prep_inputs(inp):
    cstv = _consts()
    prmv = _params(inp)
    BT, CT = _s5mats(inp)
    maps = []
    for i in range(NCORES):
        m = {}
        m["xin"] = np.ascontiguousarray(np.concatenate(
            [inp["x_prompt"][i], inp["x_sample"][NS * i:NS * (i + 1)].reshape(NS * LS, D)], axis=0), dtype=np.float32)
        m["cin"] = np.ascontiguousarray(np.concatenate(
            [inp["c_prompt"][i:i + 1], inp["c_sample"][NS * i:NS * (i + 1)]], axis=0), dtype=np.float32)
        m["wada"] = np.ascontiguousarray(inp["w_ada"][0], dtype=np.float32)
        m["wadaf"] = np.ascontiguousarray(inp["w_ada_f"], dtype=np.float32)
        m["win"] = np.ascontiguousarray(inp["w_in"][0], dtype=np.float32)
        m["wglu"] = np.ascontiguousarray(inp["w_glu"][0], dtype=np.float32)
        m["wout"] = np.ascontiguousarray(inp["w_out"][0], dtype=np.float32)
        m["wg"] = np.ascontiguousarray(inp["w_ffn_gate"][0], dtype=np.float32)
        m["wu"] = np.ascontiguousarray(inp["w_ffn_up"][0], dtype=np.float32)
        m["wd"] = np.ascontiguousarray(inp["w_ffn_down"][0], dtype=np.float32)
        m["cst"] = cstv
        m["prm"] = prmv
        m["s5bt"] = BT.reshape(128, -1)
        m["s5ct"] = CT.reshape(128, -1)
        m["stssd"] = np.ascontiguousarray(inp["state_ssd"][0, NS * i:NS * (i + 1)], dtype=np.float32)
        sc = inp["state_conv"][0, NS * i:NS * (i + 1)]
        m["stconv"] = np.ascontiguousarray(
            sc.reshape(NS, 3, 8, 128).transpose(3, 2, 0, 1).reshape(128, -1), dtype=np.float32)
        sr = inp["state_s5_re"][0, NS * i:NS * (i + 1)]
        si = inp["state_s5_im"][0, NS * i:NS * (i + 1)]
        st = np.stack([sr, si], 0).reshape(2, NS, 16, 128).transpose(3, 0, 2, 1)
        m["sts5"] = np.ascontiguousarray(st.reshape(128, -1), dtype=np.float32)
        maps.append(m)
    return maps


_CACHE = {}


def kernel(**inputs):
    inp = {k: np.asarray(v) for k, v in inputs.items()}
    if "nc" not in _CACHE:
        _CACHE["nc"] = build()[0]
    nc = _CACHE["nc"]
    maps = _prep_inputs(inp)
    res = run_bass_kernel_spmd(nc, maps, core_ids=list(range(NCORES)))
    R = res.results
    y_p = np.stack([R[i]["yout"][:SEQ] for i in range(NCORES)], 0)
    y_s = np.concatenate([R[i]["yout"][SEQ:].reshape(NS, LS, D) for i in range(NCORES)], 0)
    ssd_p = np.stack([R[i]["o_ssdp"].reshape(128, 8, 64).transpose(1, 2, 0) for i in range(NCORES)], 0)[None]
    ssd_s = np.concatenate([R[i]["o_ssds"] for i in range(NCORES)], 0)[None]
    conv = [R[i]["o_conv"].reshape(128, 8, 17, 3).transpose(2, 3, 1, 0).reshape(17, 3, 1024) for i in range(NCORES)]
    conv_p = np.stack([c[0] for c in conv], 0)[None]
    conv_s = np.concatenate([c[1:] for c in conv], 0)[None]
    s5 = [R[i]["o_s5"].reshape(128, 2, 16, 17).transpose(1, 3, 2, 0).reshape(2, 17, 32, 64) for i in range(NCORES)]
    re_p = np.stack([s[0, 0] for s in s5], 0)[None]
    re_s = np.concatenate([s[0, 1:] for s in s5], 0)[None]
    im_p = np.stack([s[1, 0] for s in s5], 0)[None]
    im_s = np.concatenate([s[1, 1:] for s in s5], 0)[None]
    f = lambda a: np.ascontiguousarray(a, dtype=np.float32)
    return (f(y_p), f(y_s), f(ssd_p), f(ssd_s), f(conv_p), f(conv_s), f(re_p), f(re_s), f(im_p), f(im_s))
```

```python
import math
import numpy as np
from contextlib import ExitStack
import concourse.bass as bass
import concourse.mybir as mybir
from concourse.bass_utils import run_bass_kernel_spmd

F32 = mybir.dt.float32
BF16 = mybir.dt.bfloat16
I32 = mybir.dt.int32
AF = mybir.ActivationFunctionType
ALU = mybir.AluOpType

NCORES = 8
D = 1024
SEQ = 2048
NS = 16
LS = 4
NTOK = SEQ + NS * LS
DFF = 2816
NJ = DFF // 128
INP = 2056
EPS = 1e-6
T5 = 32
TILES = [(0, 512), (512, 512), (1024, 512), (1536, 512), (2048, 64)]
PI = math.pi


class Buf:
    def __init__(self, name):
        self.name = name
        self.w = None
        self.r = []
        self.dsem = None
        self.dcnt = 0


class TL:
    def __init__(self, t, name):
        self.t = t
        self.name = name
        self.b = Buf(name)
        self.subs = {}

    def sub(self, k):
        if getattr(self, "nosub", False):
            return self.b
        if k not in self.subs:
            self.subs[k] = Buf("%s_%s" % (self.name, k))
        return self.subs[k]

    def allb(self):
        return [self.b] + list(self.subs.values())

    def __getitem__(self, k):
        return self.t[k]


class Sched:
    ENG = ["pe", "act", "dve", "pool", "sp"]

    def __init__(self, nc, es):
        self.nc = nc
        self.es = es
        self.eobj = {"pe": nc.tensor, "act": nc.scalar, "dve": nc.vector, "pool": nc.gpsimd, "sp": nc.sync}
        self.cnt = {e: 0 for e in self.ENG}
        self.sem = {e: es.enter_context(nc.semaphore("s_" + e)) for e in self.ENG}
        self.seen = {e: {} for e in self.ENG}
        self.dbufs = []
        self.ninst = 0
        self.dead = False
        self.pe_pending = None

    def _flush_pe(self):
        if self.pe_pending is not None:
            self.pe_pending.then_inc(self.sem["pe"], 1)
            self.cnt["pe"] += 1
            self.pe_pending = None

    def _deps(self, eng, reads, writes):
        deps = []
        for b in reads:
            if b.w is not None:
                deps.append(b.w)
        for b in writes:
            if b.w is not None:
                deps.append(b.w)
            deps.extend(b.r)
        waits = {}
        for (sem, val, key) in deps:
            if key == "pe" and eng == "pe":
                continue
            if self.seen[eng].get(key, 0) >= val:
                continue
            if key == "pe" and val > self.cnt["pe"]:
                self._flush_pe()
            if key not in waits or waits[key][1] < val:
                waits[key] = (sem, val)
        for key, (sem, val) in waits.items():
            self.seen[eng][key] = val
        return list(waits.values())

    def op(self, eng, fn, reads=(), writes=()):
        if self.dead:
            return None
        xr = [b for b in reads if getattr(b, "excl", False)]
        if xr:
            reads = [b for b in reads if not getattr(b, "excl", False)]
            writes = list(writes) + xr
        waits = self._deps(eng, reads, writes)
        e = self.eobj[eng]
        for (s_, v_) in waits:
            e.wait_ge(s_, v_)
        if eng == "pe":
            self.pe_pending = fn(e)
            tok = (self.sem[eng], self.cnt[eng] + 1, eng)
        else:
            self.cnt[eng] += 1
            tok = (self.sem[eng], self.cnt[eng], eng)
            fn(e).then_inc(self.sem[eng], 1)
        for b in reads:
            b.r.append(tok)
        for b in writes:
            b.w = tok
            b.r = []
        self.ninst += 1
        return tok

    def dma(self, eng, out, in_, reads=(), writes=(), buf=None, **kw):
        if self.dead:
            return None
        waits = self._deps(eng, reads, writes)
        if buf is None:
            buf = writes[0] if writes else reads[0]
        if buf.dsem is None:
            buf.dsem = self.es.enter_context(self.nc.semaphore("d_" + buf.name))
            self.dbufs.append(buf)
        buf.dcnt += 16
        tok = (buf.dsem, buf.dcnt, "d_" + buf.name)
        e = self.eobj[eng]
        for (s_, v_) in waits:
            e.wait_ge(s_, v_)
        e.dma_start(out=out, in_=in_, **kw).then_inc(buf.dsem, 16)
        for b in reads:
            b.r.append(tok)
        for b in writes:
            b.w = tok
            b.r = []
        self.ninst += 1
        return tok

    def barrier(self):
        if self.dead:
            return
        self._flush_pe()
        for e in self.ENG:
            waits = []
            for o in self.ENG:
                if o != e and self.cnt[o] > self.seen[e].get(o, 0):
                    waits.append((self.sem[o], self.cnt[o]))
                    self.seen[e][o] = self.cnt[o]
            for b in self.dbufs:
                key = "d_" + b.name
                if b.dcnt > self.seen[e].get(key, 0):
                    waits.append((b.dsem, b.dcnt))
                    self.seen[e][key] = b.dcnt
            for (s_, v_) in waits:
                self.eobj[e].wait_ge(s_, v_)

    def emit(self):
        pass


C_ID = 0
C_TRI = 128
C_NEG = 256
C_TRI64 = 384
C_NEG64 = 512
C_SEG64 = 640
C_SEGI = 768
CST_W = 784

P_BMOD = 0
P_GAIN = 64
P_CONV = 88
P_SSDFM = 128
P_S5P = 136
P_S5M = 184
P_SSD8 = 192
PRM_W = 194


def _consts():
    c = np.zeros((128, CST_W), np.float32)
    c[:, C_ID:C_ID + 128] = np.eye(128, dtype=np.float32)
    s = np.arange(128)[:, None]
    l = np.arange(128)[None, :]
    c[:, C_TRI:C_TRI + 128] = (s <= l).astype(np.float32)
    c[:, C_NEG:C_NEG + 128] = np.where(l >= s, 0.0, -30000.0)
    same = (s // LS == l // LS) & (s < 64) & (l < 64)
    c[:, C_TRI64:C_TRI64 + 128] = ((s <= l) & same).astype(np.float32)
    c[:, C_NEG64:C_NEG64 + 128] = np.where((l >= s) & same, 0.0, -30000.0)
    c[:, C_SEG64:C_SEG64 + 128] = same.astype(np.float32)
    j = np.arange(16)[None, :]
    c[:, C_SEGI:C_SEGI + 16] = ((s // LS == j) & (s < 64)).astype(np.float32)
    return c


def _fm(v, nt):
    return np.ascontiguousarray(np.asarray(v, np.float32).reshape(nt, 128).T)


def _params(inp):
    p = np.zeros((128, PRM_W), np.float32)
    p[:, P_BMOD:P_BMOD + 48] = _fm(inp["b_ada"][0], 48)
    p[:, P_BMOD + 48:P_BMOD + 64] = _fm(inp["b_ada_f"], 16)
    p[:, P_GAIN:P_GAIN + 8] = _fm(inp["norm1_g"][0], 8)
    p[:, P_GAIN + 8:P_GAIN + 16] = _fm(inp["norm2_g"][0], 8)
    p[:, P_GAIN + 16:P_GAIN + 24] = _fm(inp["normf_g"], 8)
    cw = inp["conv_w"][0]
    cv = np.zeros((128, 8, 5), np.float32)
    for k in range(4):
        cv[:, :, k] = _fm(cw[k], 8)
    cv[:, :, 4] = _fm(inp["conv_b"][0], 8)
    p[:, P_CONV:P_CONV + 40] = cv.reshape(128, 40)
    Dh = inp["ssd_D"][0]
    dfm = np.zeros((128, 4), np.float32)
    for pr in range(4):
        dfm[0:64, pr] = Dh[2 * pr]
        dfm[64:128, pr] = Dh[2 * pr + 1]
    p[:, P_SSDFM:P_SSDFM + 4] = dfm
    p[:, P_SSDFM + 4:P_SSDFM + 8] = _fm(inp["ssd_norm_g"][0], 4)

    def st(a):
        return np.ascontiguousarray(np.asarray(a, np.float32).reshape(16, 128).T)
    p[:, P_S5P:P_S5P + 16] = st(inp["s5_A_re"][0])
    p[:, P_S5P + 16:P_S5P + 32] = st(inp["s5_A_im"][0])
    p[:, P_S5P + 32:P_S5P + 48] = st(np.repeat(inp["s5_log_step"][0][:, None], 64, axis=1))
    p[:, P_S5M:P_S5M + 4] = _fm(inp["s5_D"][0], 4)
    p[:, P_S5M + 4:P_S5M + 8] = _fm(inp["b_glu"][0], 4)
    p[0:8, P_SSD8] = inp["ssd_dt_bias"][0]
    p[0:8, P_SSD8 + 1] = inp["ssd_A_log"][0]
    return p


def _s5mats(inp):
    Br, Bi = inp["s5_B_re"][0], inp["s5_B_im"][0]
    Cr, Ci = inp["s5_C_re"][0], inp["s5_C_im"][0]
    BT = np.zeros((128, 2, 16, 128), np.float32)
    CT = np.zeros((128, 2, 16, 32), np.float32)
    for s in range(16):
        for gl in range(2):
            g = 2 * s + gl
            r0 = (g % 8) * 16
            BT[r0:r0 + 16, 0, s, gl * 64:(gl + 1) * 64] = Br[g].T
            BT[r0:r0 + 16, 1, s, gl * 64:(gl + 1) * 64] = Bi[g].T
            CT[gl * 64:(gl + 1) * 64, 0, s, gl * 16:(gl + 1) * 16] = Cr[g].T
            CT[gl * 64:(gl + 1) * 64, 1, s, gl * 16:(gl + 1) * 16] = Ci[g].T
    return BT, CT


class Arena:
    def __init__(self, nc, es, words):
        self.t = es.enter_context(nc.sbuf_tensor("arena", [128, words], F32))
        self.words = words
        self.lo = 0
        self.hi = words

    def alloc(self, name, shape, dt, top=False):
        n = 1
        for d in shape[1:]:
            n *= d
        w = n if dt == F32 or dt == I32 else (n + 1) // 2
        w = (w + 3) // 4 * 4
        if top:
            self.hi -= w
            off = self.hi
        else:
            off = self.lo
            self.lo += w
        assert self.lo <= self.hi, "arena overflow at %s: lo=%d hi=%d" % (name, self.lo, self.hi)
        ap = self.t[:, off:off + w]
        if dt != F32:
            ap = ap.bitcast(dt)
        ap = ap[:, 0:n]
        if len(shape) == 3:
            ap = ap.rearrange("p (a b) -> p a b", b=shape[2])
        elif len(shape) == 4:
            ap = ap.rearrange("p (a b c) -> p a b c", b=shape[2], c=shape[3])
        if shape[0] < 128:
            ap = ap[0:shape[0]]
        return TL(ap, name)


class StopBuild(Exception):
    pass


def build(dbg=None, stop_after=None):
    nc = bass.Bass("TRN2", target_bir_lowering=False)

    SH = []

    def ckpt(name):
        if stop_after == name:
            SH[0].barrier()
            SH[0].dead = True
    dt_in = lambda name, shape: nc.dram_tensor(name, list(shape), F32, kind="ExternalInput").ap()
    dt_out = lambda name, shape: nc.dram_tensor(name, list(shape), F32, kind="ExternalOutput").ap()
    xin = dt_in("xin", [NTOK, D])
    cin = dt_in("cin", [17, D])
    wada = dt_in("wada", [D, 6144])
    wadaf = dt_in("wadaf", [D, 2048])
    win = dt_in("win", [D, INP])
    wglu = dt_in("wglu", [512, 512])
    wout = dt_in("wout", [D, D])
    wg = dt_in("wg", [D, DFF])
    wu = dt_in("wu", [D, DFF])
    wd = dt_in("wd", [DFF, D])
    cst_d = dt_in("cst", [128, CST_W])
    prm_d = dt_in("prm", [128, PRM_W])
    s5bt_d = dt_in("s5bt", [128, 2 * 16 * 128])
    s5ct_d = dt_in("s5ct", [128, 2 * 16 * 32])
    stssd_d = dt_in("stssd", [NS, 8, 64, 128])
    stconv_d = dt_in("stconv", [128, 8 * NS * 3])
    sts5_d = dt_in("sts5", [128, 2 * 16 * NS])
    yout = dt_out("yout", [NTOK, D])
    o_ssdp = dt_out("o_ssdp", [128, 512])
    o_ssds = dt_out("o_ssds", [NS, 8, 64, 128])
    o_conv = dt_out("o_conv", [128, 8 * 17 * 3])
    o_s5 = dt_out("o_s5", [128, 2 * 16 * 17])
    mixd = nc.dram_tensor("mixd", [128, 8, NTOK], BF16, kind="Internal").ap()
    dumps = {}

    with ExitStack() as es:
        S = Sched(nc, es)
        NEED_CTN = []
        SH.append(S)
        A = Arena(nc, es, 53200)
        outbufs = []

        def dump(name, ap, shape, reads):
            if dbg is None or name not in dbg:
                return
            d = dt_out("dbg_" + name, shape)
            dumps[name] = shape
            b = Buf("dbg_" + name)
            S.dma("sp" if ap.dtype == F32 else "pool", d, ap, reads=reads, buf=b)
            outbufs.append(b)

        PB = [TL(es.enter_context(nc.psum_tensor("pb%d" % i, [128, 512], F32)), "pb%d" % i) for i in range(8)]
        for pb_ in PB:
            pb_.b.excl = True
            pb_.nosub = True

        def pbf(i):
            return PB[i].t[:].bitcast(BF16)

        cst = A.alloc("cst", [128, CST_W], F32)
        prm = A.alloc("prm", [128, PRM_W], F32)
        identb = A.alloc("identb", [128, 128], BF16)
        onesf = A.alloc("onesf", [128, 128], F32)
        mod = A.alloc("mod", [128, 64, 17], F32)
        amod = A.alloc("amod", [128, 24, 17], F32)
        s5fin = A.alloc("s5fin", [128, 2, 16, 17], F32)
        scT = A.alloc("scT", [128, 8, 17], BF16)
        LO_GLOBAL = A.lo

        ident = cst.t[:, C_ID:C_ID + 128]
        S.dma("sp", cst.t[:], cst_d, writes=[cst.b])
        S.dma("sp", prm.t[:], prm_d, writes=[prm.b])
        S.op("act", lambda e: e.activation(out=identb.t[:], in_=ident, func=AF.Copy), reads=[cst.b], writes=[identb.b])
        S.op("dve", lambda e: e.memset(onesf.t[:], 1.0), writes=[onesf.b])

        def chunkmod(i):
            return mod.t[:, 8 * i:8 * i + 8, :]

        cs = A.alloc("cs", [17, D], F32)
        slabs = [A.alloc("adaslab%d" % i, [128, 8, 512], BF16) for i in range(3)]
        S.dma("sp", cs.t[:], cin, writes=[cs.b])
        S.op("act", lambda e: e.activation(out=cs.t[:], in_=cs.t[:], func=AF.Silu), reads=[cs.b], writes=[cs.b])
        for kt in range(8):
            S.op("pe", lambda e, kt=kt: e.transpose(PB[2].t[:, kt * 17:(kt + 1) * 17], cs.t[:, kt * 128:(kt + 1) * 128],
                                                    cst.t[0:17, C_ID:C_ID + 17]),
                 reads=[cs.b, cst.b], writes=[PB[2].b])
        S.op("act", lambda e: e.activation(out=scT.t[:].rearrange("p k s -> p (k s)"), in_=PB[2].t[:, 0:136], func=AF.Copy),
             reads=[PB[2].b], writes=[scT.b])
        wada_v = wada.rearrange("(kt p) n -> p kt n", p=128)
        wadaf_v = wadaf.rearrange("(kt p) n -> p kt n", p=128)

        def slab_src(i):
            if i < 12:
                return wada_v[:, :, i * 512:(i + 1) * 512]
            return wadaf_v[:, :, (i - 12) * 512:(i - 11) * 512]

        def load_slab(i):
            sl = slabs[i % 3]
            for kh in range(2):
                S.dma("pool", sl.t[:, 4 * kh:4 * kh + 4, :], slab_src(i)[:, 4 * kh:4 * kh + 4, :], writes=[sl.b])
        load_slab(0)
        load_slab(1)
        for i in range(4):
            if i + 2 < 4:
                load_slab(i + 2)
            sl = slabs[i % 3]
            pb = PB[i % 2]
            for fc in range(4):
                for kt in range(8):
                    S.op("pe", lambda e, fc=fc, kt=kt, sl=sl, pb=pb: e.matmul(
                        pb.t[:, fc * 17:(fc + 1) * 17], sl.t[:, kt, fc * 128:(fc + 1) * 128], scT.t[:, kt, :],
                        start=(kt == 0), stop=(kt == 7)), reads=[sl.b, scT.b], writes=[pb.b])
            S.op("dve", lambda e, i=i, pb=pb: e.tensor_tensor(
                out=mod.t[:, 4 * i:4 * i + 4, :], in0=pb.t[:, 0:68].rearrange("p (c s) -> p c s", s=17),
                in1=prm.t[:, P_BMOD + 4 * i:P_BMOD + 4 * i + 4].unsqueeze(2).to_broadcast([128, 4, 17]), op=ALU.add),
                reads=[pb.b, prm.b], writes=[mod.b])
        def make_amod(lst):
          for k, (sci, gi) in lst:
            S.op("dve", lambda e, k=k, sci=sci, gi=gi: e.scalar_tensor_tensor(
                out=amod.t[:, 8 * k:8 * k + 8, :], in0=chunkmod(sci), scalar=1.0,
                in1=prm.t[:, P_GAIN + 8 * gi:P_GAIN + 8 * gi + 8].unsqueeze(2).to_broadcast([128, 8, 17]),
                op0=ALU.add, op1=ALU.mult), reads=[mod.b, prm.b], writes=[amod.b])
        make_amod([(0, (1, 0))])
        dump("mod", mod.t[:].rearrange("p c s -> p (c s)"), [128, 64 * 17], [mod.b])
        S.barrier()
        S.emit()
        A.lo = LO_GLOBAL

        MOD_SH1, MOD_G1, MOD_SH2, MOD_G2, MOD_SHF = 0, 2, 3, 5, 6

        def expand_mod(name, src_ap, srcbufs):
            t = A.alloc(name, [128, 8, 64], F32)
            S.op("dve", lambda e: e.tensor_copy(out=t.t[:].rearrange("p k (s b) -> p k s b", b=LS),
                                                in_=src_ap.unsqueeze(3).to_broadcast([128, 8, NS, LS])),
                 reads=srcbufs, writes=[t.b])
            return t

        LO_P1 = A.lo
        mixt = [A.alloc("mixt%d" % i, [128, 8, 256], BF16) for i in range(2)]
        mixdb = [Buf("mixd%d" % i) for i in range(9)]
        win_sb = A.alloc("win_sb", [128, 8, INP], BF16)
        wglu_sb = A.alloc("wglu_sb", [128, 4, 512], BF16)
        s5BT = A.alloc("s5BT", [128, 2, 16, 128], BF16)
        s5CT = A.alloc("s5CT", [128, 2, 16, 32], BF16)
        win_v = win.rearrange("(kt p) n -> p kt n", p=128)
        for kh in range(4):
            for ch in range(2):
                S.dma("pool", win_sb.t[:, 2 * kh:2 * kh + 2, ch * 1028:(ch + 1) * 1028],
                      win_v[:, 2 * kh:2 * kh + 2, ch * 1028:(ch + 1) * 1028], writes=[win_sb.b])
        for a_ in range(4):
            S.dma("pool", s5BT.t[:].rearrange("p a s c -> p (a s c)")[:, a_ * 1024:(a_ + 1) * 1024],
                  s5bt_d[:, a_ * 1024:(a_ + 1) * 1024], writes=[s5BT.b])
        S.dma("pool", s5CT.t[:].rearrange("p a s c -> p (a s c)"), s5ct_d, writes=[s5CT.b])
        S.dma("pool", wglu_sb.t[:], wglu.rearrange("(kt p) n -> p kt n", p=128), writes=[wglu_sb.b])
        S.op("dve", lambda e: e.tensor_scalar(out=s5CT.t[:, 1], in0=s5CT.t[:, 1], scalar1=-1.0, scalar2=None, op0=ALU.mult),
             reads=[s5CT.b], writes=[s5CT.b])
        NEED_CTN.append(1)

        a1x = A.alloc("a1x", [128, 8, 64], F32)
        sh1x = A.alloc("sh1x", [128, 8, 64], F32)

        def fill_x(t, src_ap, srcbufs):
            S.op("dve", lambda e: e.tensor_copy(out=t.t[:].rearrange("p k (s b) -> p k s b", b=LS),
                                                in_=src_ap.unsqueeze(3).to_broadcast([128, 8, NS, LS])),
                 reads=srcbufs, writes=[t.b])
        adab = [TL(a1x.t[:].rearrange("p k t -> p (k t)").bitcast(BF16).rearrange("p (k c) -> p k c", c=128), "adab0"),
                TL(sh1x.t[:].rearrange("p k t -> p (k t)").bitcast(BF16).rearrange("p (k c) -> p k c", c=128), "adab1")]
        adab[0].b = a1x.b
        adab[1].b = sh1x.b
        ADA_CH = list(range(16, 64))

        def ada_load(ci):
            c = ADA_CH[ci]
            src = wada_v[:, :, c * 128:(c + 1) * 128] if c < 48 else wadaf_v[:, :, (c - 48) * 128:(c - 47) * 128]
            S.dma("pool", adab[ci % 2].t[:], src, writes=[adab[ci % 2].b])

        def ada_compute(ci):
            c = ADA_CH[ci]
            sl = adab[ci % 2]
            pb = next_pb()
            for kt in range(8):
                S.op("pe", lambda e, kt=kt: e.matmul(pb.t[:, 0:17], sl.t[:, kt, :], scT.t[:, kt, :], start=(kt == 0), stop=(kt == 7)),
                     reads=[sl.b, scT.b], writes=[pb.b])
            S.op("dve", lambda e: e.tensor_scalar(out=mod.t[:, c, :], in0=pb.t[:, 0:17], scalar1=prm.t[:, P_BMOD + c:P_BMOD + c + 1],
                                                  scalar2=None, op0=ALU.add), reads=[pb.b, prm.b], writes=[mod.b])
        ada_state = [0, 0]

        def ada_step():
            if ada_state[1] >= len(ADA_CH):
                return
            while ada_state[0] < min(len(ADA_CH), ada_state[1] + 2):
                ada_load(ada_state[0])
                ada_state[0] += 1
            ada_compute(ada_state[1])
            ada_state[1] += 1

        ssd8 = A.alloc("ssd8", [8, 4], F32)
        S.op("act", lambda e: e.activation(out=ssd8.t[:, 1:2], in_=prm.t[0:8, P_SSD8 + 1:P_SSD8 + 2], func=AF.Exp),
             reads=[prm.b], writes=[ssd8.b])
        S.op("dve", lambda e: e.tensor_scalar(out=ssd8.t[:, 1:2], in0=ssd8.t[:, 1:2], scalar1=-1.0, scalar2=None, op0=ALU.mult),
             reads=[ssd8.b], writes=[ssd8.b])
        S.op("dve", lambda e: e.tensor_copy(out=ssd8.t[:, 0:1], in_=prm.t[0:8, P_SSD8:P_SSD8 + 1]), reads=[prm.b], writes=[ssd8.b])

        Ptab = A.alloc("Ptab", [128, 2, 16, T5], F32)
        Qtab = A.alloc("Qtab", [128, 2, 16, T5], F32)
        s5t = [A.alloc("s5t%d" % i, [128, 512], F32) for i in range(2)]

        def alias(name, ap, buf):
            tl = TL(ap, name)
            tl.b = buf
            return tl
        sw = alias("s5work", s5t[1].t[:, 0:384].rearrange("p (a b) -> p a b", b=16), s5t[1].b)
        tmpA = alias("tmpA", s5t[0].t[:, 0:256].rearrange("p (a b) -> p a b", b=T5 // 2), s5t[0].b)
        tmpB = alias("tmpB", s5t[0].t[:, 256:512].rearrange("p (a b) -> p a b", b=T5 // 2), s5t[0].b)
        mask32 = A.alloc("mask32", [128, 16, T5], BF16)
        s5v = [A.alloc("s5v%d" % i, [128, 512], F32) for i in range(2)]
        qtmp = alias("qtmp", s5v[0].t[:].rearrange("p (s t) -> p s t", t=T5), s5v[0].b)
        mask4 = A.alloc("mask4", [128, 128, LS], BF16)
        s5cr = A.alloc("s5cr", [128, 2, 16], F32)
        W = lambda i: sw.t[:, i, :]
        pv = lambda i: prm.t[:, P_S5P + 16 * i:P_S5P + 16 * (i + 1)]
        swb = [sw.b, prm.b]

        def dv(fn):
            S.op("dve", fn, reads=swb, writes=[sw.b])

        def act(fn):
            S.op("act", fn, reads=swb, writes=[sw.b])
        TT = lambda e, o, a, b, op: e.tensor_tensor(out=o, in0=a, in1=b, op=op)
        def exp_acc(dst, src):
            dv(lambda e: e.tensor_scalar(out=W(22), in0=src, scalar1=1.0 / 16, scalar2=None, op0=ALU.mult))
            dv(lambda e: e.tensor_scalar(out=dst, in0=W(22), scalar1=1.0 / 7, scalar2=1.0, op0=ALU.mult, op1=ALU.add))
            for k in (6, 5, 4, 3, 2, 1):
                dv(lambda e: TT(e, dst, dst, W(22), ALU.mult))
                dv(lambda e, k=k: e.tensor_scalar(out=dst, in0=dst, scalar1=1.0 / k, scalar2=1.0, op0=ALU.mult, op1=ALU.add))
            for _ in range(4):
                dv(lambda e: TT(e, dst, dst, dst, ALU.mult))
        exp_acc(W(0), pv(2))
        dv(lambda e: TT(e, W(1), pv(0), W(0), ALU.mult))
        dv(lambda e: TT(e, W(2), pv(1), W(0), ALU.mult))
        exp_acc(W(3), W(1))

        def range_reduce(dst, src, add):
            ki = A_ki
            dv(lambda e: e.tensor_scalar(out=W(20), in0=src, scalar1=float(add), scalar2=1.0 / (2 * PI), op0=ALU.add, op1=ALU.mult))
            S.op("dve", lambda e: e.tensor_copy(out=ki.t[:], in_=W(20)), reads=swb, writes=[ki.b])
            S.op("dve", lambda e: e.tensor_copy(out=W(21), in_=ki.t[:]), reads=[ki.b], writes=[sw.b])
            dv(lambda e: e.tensor_scalar(out=W(20), in0=src, scalar1=float(add), scalar2=None, op0=ALU.add))
            dv(lambda e: e.scalar_tensor_tensor(out=dst, in0=W(21), scalar=-2 * PI, in1=W(20), op0=ALU.mult, op1=ALU.add))
            dv(lambda e: e.tensor_scalar(out=dst, in0=dst, scalar1=PI, scalar2=-PI, op0=ALU.min, op1=ALU.max))
        A_ki = A.alloc("s5ki", [128, 16], I32)
        range_reduce(W(4), W(2), 0.0)
        range_reduce(W(5), W(2), PI / 2)
        act(lambda e: e.activation(out=W(6), in_=W(4), func=AF.Sin))
        act(lambda e: e.activation(out=W(7), in_=W(5), func=AF.Sin))
        dv(lambda e: TT(e, W(8), W(3), W(7), ALU.mult))
        dv(lambda e: TT(e, W(9), W(3), W(6), ALU.mult))
        dv(lambda e: e.tensor_scalar(out=W(10), in0=W(8), scalar1=-1.0, scalar2=None, op0=ALU.add))
        dv(lambda e: TT(e, W(11), pv(0), pv(0), ALU.mult))
        dv(lambda e: TT(e, W(12), pv(1), pv(1), ALU.mult))
        dv(lambda e: TT(e, W(11), W(11), W(12), ALU.add))
        dv(lambda e: e.reciprocal(out=W(11), in_=W(11)))
        dv(lambda e: TT(e, W(12), W(10), pv(0), ALU.mult))
        dv(lambda e: TT(e, W(13), W(9), pv(1), ALU.mult))
        dv(lambda e: TT(e, W(12), W(12), W(13), ALU.add))
        dv(lambda e: TT(e, W(14), W(12), W(11), ALU.mult))
        dv(lambda e: TT(e, W(12), W(9), pv(0), ALU.mult))
        dv(lambda e: TT(e, W(13), W(10), pv(1), ALU.mult))
        dv(lambda e: TT(e, W(12), W(12), W(13), ALU.subtract))
        dv(lambda e: TT(e, W(15), W(12), W(11), ALU.mult))
        dv(lambda e: TT(e, W(12), W(8), W(8), ALU.mult))
        dv(lambda e: TT(e, W(13), W(9), W(9), ALU.mult))
        dv(lambda e: TT(e, W(12), W(12), W(13), ALU.add))
        dv(lambda e: e.reciprocal(out=W(12), in_=W(12)))
        dv(lambda e: TT(e, W(16), W(8), W(12), ALU.mult))
        dv(lambda e: e.scalar_tensor_tensor(out=W(17), in0=W(9), scalar=-1.0, in1=W(12), op0=ALU.mult, op1=ALU.mult))

        def build_pow(tab, br, bi):
            tb = [tab.b, sw.b, tmpA.b, tmpB.b]
            S.op("dve", lambda e: e.tensor_copy(out=tab.t[:, 0, :, 0], in_=br), reads=tb, writes=[tab.b])
            S.op("dve", lambda e: e.tensor_copy(out=tab.t[:, 1, :, 0], in_=bi), reads=tb, writes=[tab.b])
            n = 1
            while n < T5:
                ar, ai = tab.t[:, 0, :, 0:n], tab.t[:, 1, :, 0:n]
                sr = tab.t[:, 0, :, n - 1:n].to_broadcast([128, 16, n])
                si = tab.t[:, 1, :, n - 1:n].to_broadcast([128, 16, n])
                tA, tB = tmpA.t[:, :, 0:n], tmpB.t[:, :, 0:n]
                orr, oi = tab.t[:, 0, :, n:2 * n], tab.t[:, 1, :, n:2 * n]
                ops = [(tA, ar, sr, ALU.mult), (tB, ai, si, ALU.mult), (orr, tA, tB, ALU.subtract),
                       (tA, ar, si, ALU.mult), (tB, ai, sr, ALU.mult), (oi, tA, tB, ALU.add)]
                for (o, a, b, op) in ops:
                    S.op("dve", lambda e, o=o, a=a, b=b, op=op: TT(e, o, a, b, op), reads=tb, writes=tb[0:1] + tb[2:4])
                n *= 2
        build_pow(Ptab, W(8), W(9))
        build_pow(Qtab, W(16), W(17))
        tq = [Qtab.b, sw.b, tmpA.b, tmpB.b]
        for half in range(2):
            hs = slice(half * (T5 // 2), (half + 1) * (T5 // 2))
            qr, qi = Qtab.t[:, 0, :, hs], Qtab.t[:, 1, :, hs]
            fr = W(14).unsqueeze(2).to_broadcast([128, 16, T5 // 2])
            fi = W(15).unsqueeze(2).to_broadcast([128, 16, T5 // 2])
            ops = [(tmpA.t[:], qr, fr, ALU.mult), (tmpB.t[:], qi, fi, ALU.mult), ("R", tmpA.t[:], tmpB.t[:], ALU.subtract),
                   (tmpA.t[:], qr, fi, ALU.mult), (tmpB.t[:], qi, fr, ALU.mult), (qi, tmpA.t[:], tmpB.t[:], ALU.add)]
            for (o, a, b, op) in ops:
                if isinstance(o, str):
                    o = qtmp.t[:, :, hs]
                S.op("dve", lambda e, o=o, a=a, b=b, op=op: TT(e, o, a, b, op), reads=tq + [qtmp.b], writes=tq + [qtmp.b])
            S.op("dve", lambda e, qr=qr, hs=hs: e.tensor_copy(out=qr, in_=qtmp.t[:, :, hs]), reads=[qtmp.b], writes=[Qtab.b])
        S.op("dve", lambda e: e.memset(mask32.t[:], 1.0), reads=[Qtab.b], writes=[mask32.b])
        S.op("dve", lambda e: e.memset(mask32.t[:, :, 0:1], 0.0), writes=[mask32.b])
        S.op("dve", lambda e: e.memset(mask4.t[:], 1.0), writes=[mask4.b])
        S.op("dve", lambda e: e.memset(mask4.t[:, :, 0:1], 0.0), writes=[mask4.b])
        S.op("dve", lambda e: e.memset(s5cr.t[:], 0.0), writes=[s5cr.b])
        dump("Ptab", Ptab.t[:].rearrange("p a s t -> p (a s t)"), [128, 2 * 16 * T5], [Ptab.b])
        dump("Qtab", Qtab.t[:].rearrange("p a s t -> p (a s t)"), [128, 2 * 16 * T5], [Qtab.b])

        ckpt("setup0")
        NTM = 256
        xtm = A.alloc("xtm", [128, 2, D], F32)
        xn = A.alloc("xn", [128, 2, D], BF16)
        nstat = A.alloc("nstat", [128, 4], F32)
        uT = A.alloc("uT", [128, 8, NTM], BF16)
        xpad = A.alloc("xpad", [128, 8, NTM + 4], BF16)
        xtail = A.alloc("xtail", [128, 8, 64], F32)
        cvst = A.alloc("cvst", [128, 8, NS, 3], F32)
        S.dma("sp", cvst.t[:].rearrange("p c s k -> p (c s k)"), stconv_d, writes=[cvst.b])
        dgc = A.alloc("dgc", [128, 8, 4, 128], BF16)
        for ct_ in range(8):
            for k_ in range(4):
                S.op("act", lambda e, ct_=ct_, k_=k_: e.activation(
                    out=dgc.t[:, ct_, k_, :], in_=ident, func=AF.Copy,
                    scale=prm.t[:, P_CONV + 5 * ct_ + k_:P_CONV + 5 * ct_ + k_ + 1]), reads=[cst.b, prm.b], writes=[dgc.b])
        xsT = A.alloc("xsT", [128, 4, NTM], F32)
        BCT = A.alloc("BCT", [128, 4, NTM], BF16)
        szT = A.alloc("szT", [128, 4, NTM], BF16)
        u5Ts = [A.alloc("u5T%d" % i, [128, 4, NTM], BF16) for i in range(2)]
        dtT = A.alloc("dtT", [8, 2, NTM], F32)
        cacc = [A.alloc("cacc0", [128, NTM], F32)] * 2
        y5pre = A.alloc("y5pre", [128, 4, NTM], F32)
        g5 = A.alloc("g5", [128, 4, NTM], BF16)
        sgl = A.alloc("sgl", [128, NTM], F32)
        dtm_l = [A.alloc("dtm%d" % i, [128, 16], F32) for i in range(2)]
        acs_l = [A.alloc("acs%d" % i, [128, 8], F32) for i in range(2)]
        dec_l = [A.alloc("dec%d" % i, [128, 8], F32) for i in range(2)]
        dtdec_l = [A.alloc("dtdec%d" % i, [128, 8], F32) for i in range(2)]
        Xtm = A.alloc("Xtm", [128, 8, 64], BF16)
        Xdec = A.alloc("Xdec", [128, 8, 64], BF16)
        Btm = A.alloc("Btm", [128, 2, 128], BF16)
        big1 = A.alloc("big1", [128, 8, 128], F32)
        big2 = A.alloc("big2", [128, 8, 128], F32)
        MT = A.alloc("MT", [128, 8, 128], BF16)
        eA = A.alloc("eA", [128, 8, 128], F32)
        CdT = A.alloc("CdT", [128, 8, 128], BF16)
        ST = A.alloc("ST", [128, 8, 64], F32)
        STb = A.alloc("STb", [128, 8, 64], BF16)
        sts5 = alias("sts5", ST.t[:].rearrange("p h q -> p (h q)").rearrange("p (a s q) -> p a s q", a=2, s=16), ST.b)
        yg = A.alloc("yg", [128, 4, 128], F32)
        ysq = alias("ysq", big1.t[:, 4:8, :], big1.b)
        rsb = A.alloc("rsb", [128, 2, 128], F32)
        ysqb = A.alloc("ysqb", [128, 4, 128], BF16)
        onesb1 = A.alloc("onesb1", [128, 128], BF16)
        S.op("dve", lambda e: e.memset(onesb1.t[:], 1.0), writes=[onesb1.b])
        h0n = [alias("h0n0", xtm.t[:, 1, 0:512].rearrange("p (a n) -> p a n", n=128), xtm.sub(1)),
               alias("h0n1", xtm.t[:, 0, 0:512].rearrange("p (a n) -> p a n", n=128), xtm.sub(0))]
        h0T = [A.alloc("h0T%d" % i, [128, 8, 64], BF16) for i in range(2)]
        Bj = [A.alloc("Bj%d" % i, [128, 2, 128], BF16) for i in range(2)]
        hn = [alias("hn0", xtm.t[:, 1, 512:1024].rearrange("p (a n) -> p a n", n=128), xtm.sub(1)),
              alias("hn1", xtm.t[:, 0, 512:1024].rearrange("p (a n) -> p a n", n=128), xtm.sub(0))]
        decfm = A.alloc("decfm", [128, 4, 16], F32)
        dAx = alias("dAx", big1.t[:, 0:4, :].rearrange("p a (b c) -> p (a b) c", c=64), big1.b)
        s5g = [[A.alloc("s5g%d%d" % (j, i), [128, 512], F32) for i in range(2)] for j in range(2)]
        s5t34 = [A.alloc("s5t%d" % i, [128, 512], F32) for i in (2, 3)]
        s5vb = [A.alloc("s5vb%d" % i, [128, 512], F32) for i in range(2)]
        s5k = [0]
        s5h = [[A.alloc("s5h%d%d" % (j, i), [128, 512], BF16) for i in range(4)] for j in range(2)]
        s5CTn = A.alloc("s5CTn", [128, 16, 32], BF16)
        s5c = A.alloc("s5c", [128, 4, 16], F32)
        busd = [[A.alloc("bus%d%d" % (j, i), [128, 512], F32) for i in range(2)] for j in range(2)]
        dg5 = A.alloc("dg5", [128, 4, 128], BF16)
        for q_ in range(4):
            S.op("act", lambda e, q_=q_: e.activation(out=dg5.t[:, q_, :], in_=ident, func=AF.Copy,
                                                      scale=prm.t[:, P_S5M + q_:P_S5M + q_ + 1]),
                 reads=[cst.b, prm.b], writes=[dg5.b])
        S.op("dve", lambda e: e.tensor_scalar(out=s5CTn.t[:], in0=s5CT.t[:, 0], scalar1=-1.0, scalar2=None, op0=ALU.mult),
             reads=[s5CT.b], writes=[s5CTn.b])
        print("arena after p1a allocs: lo=%d hi=%d (words)" % (A.lo, A.hi))

        S.op("dve", lambda e: e.memset(xpad.t[:, :, 0:3], 0.0), writes=[xpad.b])
        S.op("dve", lambda e: e.memset(ST.t[:], 0.0), writes=[ST.b])
        S.op("dve", lambda e: e.memset(STb.t[:], 0.0), writes=[STb.b])

        import os as _os3
        ENG_OUTROT = _os3.environ.get("K_OUTROT", "dve")
        ENG_ADDS = _os3.environ.get("K_ADDS", "dve")
        TILES_A = [(i * 256, 256, False) for i in range(8)] + [(SEQ, 64, True)]

        def load_x(ti):
            t0, NT, is_s = TILES_A[ti]
            for blk in range((NT + 127) // 128):
                rows = min(128, NT - blk * 128)
                S.dma("sp", xtm.t[0:rows, blk, :], xin[t0 + blk * 128:t0 + blk * 128 + rows, :], writes=[xtm.sub(blk)])

        a1 = lambda kt: amod.t[:, kt, 0:1]
        sh1 = lambda kt: mod.t[:, 8 * MOD_SH1 + kt, 0:1]
        cw = lambda ct, k: prm.t[:, P_CONV + 5 * ct + k:P_CONV + 5 * ct + k + 1]
        IN_CHUNKS = [("dt", 0, 1536, 8)] + [("z", i, i * 128, 128) for i in range(4)] + \
                    [("xbc", i, 512 + i * 128, 128) for i in range(8)] + [("u5", i, 1544 + i * 128, 128) for i in range(4)]

        load_x(0)
        pbi = [0]

        def next_pb():
            pbi[0] ^= 1
            return PB[pbi[0]]

        ckpt("pre")
        def chain1(ti):
            t0, NT, is_s = TILES_A[ti]
            u5T = u5Ts[ti % 2]
            nblk = (NT + 127) // 128
            T = 128 if not is_s else 64
            tri = cst.t[0:T, C_TRI:C_TRI + T] if not is_s else cst.t[0:T, C_TRI64:C_TRI64 + T]
            neg = cst.t[0:T, C_NEG:C_NEG + T] if not is_s else cst.t[0:T, C_NEG64:C_NEG64 + T]
            sego = onesf.t[0:T, 0:T] if not is_s else cst.t[0:T, C_SEG64:C_SEG64 + T]
            segi = cst.t[0:64, C_SEGI:C_SEGI + 16]

            def dt_prep(ck):
                c0 = ck * T
                cs_ = slice(c0, c0 + T)
                dtm, acs, dec, dtdec = dtm_l[ck], acs_l[ck], dec_l[ck], dtdec_l[ck]
                pc = 0 if ck == 0 else 480
                S.op("pe", lambda e: e.transpose(PB[4].t[0:T, pc:pc + 8], dtT.t[:, 0, cs_], cst.t[0:8, C_ID:C_ID + 8]),
                     reads=[dtT.b, cst.b], writes=[PB[4].sub("sm")])
                S.op("pe", lambda e: e.transpose(PB[4].t[0:T, pc + 8:pc + 16], dtT.t[:, 1, cs_], cst.t[0:8, C_ID:C_ID + 8]),
                     reads=[dtT.b, cst.b], writes=[PB[4].sub("sm")])
                S.op("act", lambda e: e.activation(out=dtm.t[0:T, :], in_=PB[4].t[0:T, pc:pc + 16], func=AF.Copy),
                     reads=[PB[4].sub("sm")], writes=[dtm.b])
                S.op("pe", lambda e: e.matmul(PB[4].t[0:T, pc + 16:pc + 24], tri, dtm.t[0:T, 8:16], start=True, stop=True),
                     reads=[dtm.b, cst.b], writes=[PB[4].sub("sm")])
                S.op("pe", lambda e: e.matmul(PB[4].t[0:T, pc + 24:pc + 32], sego, dtm.t[0:T, 8:16], start=True, stop=True),
                     reads=[dtm.b, cst.b, onesf.b], writes=[PB[4].sub("sm")])
                S.op("act", lambda e: e.activation(out=acs.t[0:T, :], in_=PB[4].t[0:T, pc + 16:pc + 24], func=AF.Copy),
                     reads=[PB[4].sub("sm")], writes=[acs.b])
                S.op("dve", lambda e: TT(e, dec.t[0:T, :], PB[4].t[0:T, pc + 24:pc + 32], acs.t[0:T, :], ALU.subtract),
                     reads=[PB[4].sub("sm"), acs.b], writes=[dec.b])
                S.op("act", lambda e: e.activation(out=dec.t[0:T, :], in_=dec.t[0:T, :], func=AF.Exp), reads=[dec.b], writes=[dec.b])
                S.op("dve", lambda e: TT(e, dtdec.t[0:T, :], dtm.t[0:T, 0:8], dec.t[0:T, :], ALU.mult),
                     reads=[dtm.b, dec.b], writes=[dtdec.b])
            for blk in range(nblk):
                rows = min(128, NT - blk * 128)
                xb = xtm.sub(blk)
                S.op("act", lambda e, blk=blk, rows=rows: e.activation(
                    out=xn.t[0:rows, blk, :], in_=xtm.t[0:rows, blk, :], func=AF.Square, accum_out=nstat.t[0:rows, blk:blk + 1]),
                    reads=[xb], writes=[xn.sub(blk), nstat.sub(blk)])
                S.op("act", lambda e, blk=blk, rows=rows: e.activation(
                    out=nstat.t[0:rows, 2 + blk:3 + blk], in_=nstat.t[0:rows, blk:blk + 1], func=AF.Ln, scale=1.0 / D, bias=EPS),
                    reads=[nstat.sub(blk)], writes=[nstat.sub(blk)])
                S.op("act", lambda e, blk=blk, rows=rows: e.activation(out=nstat.t[0:rows, 2 + blk:3 + blk],
                                                                        in_=nstat.t[0:rows, 2 + blk:3 + blk], func=AF.Exp, scale=-0.5),
                     reads=[nstat.sub(blk)], writes=[nstat.sub(blk)])
                S.op("act", lambda e, blk=blk, rows=rows: e.activation(
                    out=xn.t[0:rows, blk, :], in_=xtm.t[0:rows, blk, :], func=AF.Copy, scale=nstat.t[0:rows, 2 + blk:3 + blk]),
                    reads=[xb, nstat.sub(blk)], writes=[xn.sub(blk)])
            ckpt("Aa%d" % ti)
            if ti + 1 < len(TILES_A):
                load_x(ti + 1)
            ckpt("Ab%d" % ti)
            for kt in range(8):
                xb_ = 2 + (kt % 2)
                pslot = PB[xb_].b
                for blk in range(nblk):
                    rows = min(128, NT - blk * 128)
                    S.op("pe", lambda e, kt=kt, blk=blk, rows=rows: e.transpose(
                        pbf(xb_)[:, blk * 128:blk * 128 + rows],
                        xn.t[0:rows, blk, kt * 128:(kt + 1) * 128], identb.t[0:rows, 0:rows]),
                        reads=[xn.sub(blk), identb.b], writes=[pslot])
                src = pbf(xb_)[:, 0:NT]
                if not is_s:
                    S.op("act", lambda e, kt=kt, src=src: e.activation(out=uT.t[:, kt, 0:NT], in_=src, func=AF.Identity,
                                                                       scale=a1(kt), bias=sh1(kt)),
                         reads=[pslot, amod.b, mod.b], writes=[uT.sub(kt)])
                else:
                    S.op("dve", lambda e, kt=kt, src=src: TT(e, cacc[0].t[:, 0:NT], src, a1x.t[:, kt, :], ALU.mult),
                         reads=[pslot, a1x.b], writes=[cacc[0].b])
                    S.op("dve", lambda e, kt=kt: TT(e, uT.t[:, kt, 0:NT], cacc[0].t[:, 0:NT], sh1x.t[:, kt, :], ALU.add),
                         reads=[cacc[0].b, sh1x.b], writes=[uT.sub(kt)])
            ckpt("A%d" % ti)
            if ti == 0:
                dump("uT", uT.t[:].rearrange("p k t -> p (k t)"), [128, 8 * NTM], uT.allb())

            yield
            if is_s:
                xps = xpad.t[:, :, 0:NS * 7].rearrange("p c (s k) -> p c s k", k=7)
                S.op("act", lambda e: e.activation(out=xps[:, :, :, 0:3], in_=cvst.t[:], func=AF.Copy), reads=[cvst.b], writes=[xpad.b])
            for (kind, i, c0, M) in IN_CHUNKS:
                yield
                pb = next_pb()
                for kt in range(8):
                    S.op("pe", lambda e, kt=kt, c0=c0, M=M, pb=pb: e.matmul(
                        pb.t[0:M, 0:NT], win_sb.t[:, kt, c0:c0 + M], uT.t[:, kt, 0:NT], start=(kt == 0), stop=(kt == 7)),
                        reads=[win_sb.b, uT.sub(kt)], writes=[pb.b])
                if kind == "z":
                    S.op("act", lambda e, i=i, pb=pb: e.activation(out=szT.t[:, i, 0:NT], in_=pb.t[:, 0:NT], func=AF.Silu),
                         reads=[pb.b], writes=[szT.b])
                elif kind == "xbc":
                    if not is_s:
                        S.op("act", lambda e, i=i, pb=pb: e.activation(out=xpad.t[:, i, 3:3 + NT], in_=pb.t[:, 0:NT], func=AF.Copy),
                             reads=[pb.b], writes=[xpad.b])
                        if ti == 7:
                            S.op("act", lambda e, i=i, pb=pb: e.activation(out=xtail.t[:, i, 0:3], in_=pb.t[:, NT - 3:NT], func=AF.Copy),
                                 reads=[pb.b], writes=[xtail.b])
                    else:
                        S.op("act", lambda e, i=i, pb=pb: e.activation(
                            out=xps[:, i, :, 3:7], in_=pb.t[:, 0:NT].rearrange("p (s k) -> p s k", k=LS), func=AF.Copy),
                            reads=[pb.b], writes=[xpad.b])
                        S.op("act", lambda e, i=i, pb=pb: e.activation(out=xtail.t[:, i, 0:NT], in_=pb.t[:, 0:NT], func=AF.Copy),
                             reads=[pb.b], writes=[xtail.b])
                elif kind == "dt":
                    S.op("act", lambda e, pb=pb: e.activation(out=dtT.t[:, 1, 0:NT], in_=pb.t[0:8, 0:NT], func=AF.Exp,
                                                              bias=ssd8.t[:, 0:1]), reads=[pb.b, ssd8.b], writes=[dtT.b])
                    S.op("act", lambda e: e.activation(out=dtT.t[:, 0, 0:NT], in_=dtT.t[:, 1, 0:NT], func=AF.Ln, bias=1.0),
                         reads=[dtT.b], writes=[dtT.b])
                    S.op("dve", lambda e: e.tensor_scalar(out=dtT.t[:, 1, 0:NT], in0=dtT.t[:, 0, 0:NT], scalar1=ssd8.t[:, 1:2],
                                                          scalar2=None, op0=ALU.mult), reads=[dtT.b, ssd8.b], writes=[dtT.b])
                    for ck_ in range(NT // T):
                        yield
                        dt_prep(ck_)
                else:
                    S.op("act", lambda e, i=i, pb=pb: e.activation(out=u5T.t[:, i, 0:NT], in_=pb.t[:, 0:NT], func=AF.Copy),
                         reads=[pb.b], writes=[u5T.b])

            ckpt("B%d" % ti)
            for ct in range(8):
                yield
                pb = next_pb()
                if not is_s:
                    xin_k = lambda k, ct=ct: xpad.t[:, ct, k:k + NT]
                    pbv = pb.t[:, 0:NT]
                    dst = xsT.t[:, ct, 0:NT] if ct < 4 else BCT.t[:, ct - 4, 0:NT]
                else:
                    xin_k = lambda k, ct=ct: xps[:, ct, :, k:k + LS]
                    pbv = pb.t[:, 0:NT].rearrange("p (s k) -> p s k", k=LS)
                    dst = (xsT.t[:, ct, 0:NT] if ct < 4 else BCT.t[:, ct - 4, 0:NT]).rearrange("p (s k) -> p s k", k=LS)
                for k in range(4):
                    S.op("pe", lambda e, k=k: e.matmul(pbv, dgc.t[:, ct, k, :], xin_k(k), start=(k == 0), stop=(k == 3)),
                         reads=[dgc.b, xpad.b], writes=[pb.b])
                S.op("act", lambda e: e.activation(out=dst, in_=pbv, func=AF.Silu, bias=cw(ct, 4)),
                     reads=[pb.b, prm.b], writes=[xsT.b if ct < 4 else BCT.b])
            ocv = o_conv.rearrange("p (c s k) -> p c s k", s=17, k=3)
            if is_s:
                S.op("act", lambda e: e.activation(out=cvst.t[:], in_=xtail.t[:].rearrange("p c (s k) -> p c s k", k=LS)[:, :, :, 1:4],
                                                   func=AF.Copy), reads=[xtail.b], writes=[cvst.b])
                S.dma("sp", ocv[:, :, 1:17, :], cvst.t[:], reads=[cvst.b], buf=cvst.b)
                outbufs.append(cvst.b)
            elif ti == 7:
                S.dma("sp", ocv[:, :, 0, :], xtail.t[:, :, 0:3], reads=[xtail.b], buf=xtail.b)
            if not is_s:
                S.op("dve", lambda e: e.tensor_copy(out=xpad.t[:, :, 0:3], in_=xpad.t[:, :, NT:NT + 3]),
                     reads=[xpad.b], writes=[xpad.b])
            if is_s:
                dump("xsS", xsT.t[:, :, 0:64], [128, 4, 64], [xsT.b])
                dump("ygS", yg.t[:, :, 0:64], [128, 4, 64], [yg.b])
            if ti == 0:
                dump("xsT", xsT.t[:].rearrange("p k t -> p (k t)"), [128, 4 * NTM], [xsT.b])
                dump("dtT", dtT.t[:].rearrange("p k t -> p (k t)"), [8, 2 * NTM], [dtT.b])

            ckpt("C%d" % ti)
            for ck in range(NT // T):
                c0 = ck * T
                cs_ = slice(c0, c0 + T)
                dtm, acs, dec, dtdec = dtm_l[ck], acs_l[ck], dec_l[ck], dtdec_l[ck]
                yield
                for pr in range(4):
                    S.op("pe", lambda e, pr=pr, cs_=cs_: e.transpose(PB[3].t[0:T, pr * 128:(pr + 1) * 128], xsT.t[:, pr, cs_], ident),
                         reads=[xsT.b, cst.b], writes=[PB[3].b])
                pxs = PB[3].t[0:T, :].rearrange("p (h q) -> p h q", q=64)
                for h in range(8):
                    S.op("act", lambda e, h=h: e.activation(out=Xtm.t[0:T, h, :], in_=pxs[:, h, :], func=AF.Copy, scale=dtm.t[0:T, h:h + 1]),
                         reads=[PB[3].b, dtm.b], writes=[Xtm.b])
                    S.op("act", lambda e, h=h: e.activation(out=Xdec.t[0:T, h, :], in_=pxs[:, h, :], func=AF.Copy, scale=dtdec.t[0:T, h:h + 1]),
                         reads=[PB[3].b, dtdec.b], writes=[Xdec.b])
                for g in range(2):
                    S.op("pe", lambda e, g=g, cs_=cs_: e.transpose(pbf(2)[0:T, g * 128:(g + 1) * 128], BCT.t[:, g, cs_], identb.t[:]),
                         reads=[BCT.b, identb.b], writes=[PB[2].sub(0)])
                S.op("act", lambda e: e.activation(out=Btm.t[0:T].rearrange("p g n -> p (g n)"), in_=pbf(2)[0:T, 0:256], func=AF.Copy),
                     reads=[PB[2].sub(0)], writes=[Btm.b])
                yield
                S.op("dve", lambda e: TT(e, big1.t[0:T, :, 0:T], tri.unsqueeze(1).to_broadcast([T, 8, T]),
                                         dtm.t[0:T, 8:16].unsqueeze(2).to_broadcast([T, 8, T]), ALU.mult),
                     reads=[cst.b, dtm.b], writes=[big1.b])
                for half in range(2):
                    S.op("pe", lambda e, half=half: e.matmul(
                        PB[3].t[:, 0:4 * T].rearrange("p (h l) -> p h l", l=T), onesf.t[0:T, :],
                        big1.t[0:T, 4 * half:4 * half + 4, 0:T], start=True, stop=True),
                        reads=[big1.b, onesf.b], writes=[PB[3].b])
                    yield
                    for h in range(4 * half, 4 * half + 4):
                        S.op("dve", lambda e, h=h: e.scalar_tensor_tensor(
                            out=big2.t[0:T, h, 0:T], in0=PB[3].t[0:T, (h % 4) * T:(h % 4 + 1) * T], scalar=acs.t[0:T, h:h + 1],
                            in1=neg, op0=ALU.subtract, op1=ALU.min), reads=[PB[3].b, acs.b, cst.b], writes=[big2.b])
                    S.op("act", lambda e, half=half: e.activation(
                        out=eA.t[:, 4 * half:4 * half + 4, 0:T], in_=PB[3].t[:, 0:4 * T].rearrange("p (h l) -> p h l", l=T),
                        func=AF.Exp), reads=[PB[3].b], writes=[eA.b])
                    yield
                S.op("act", lambda e: e.activation(out=big2.t[0:T, :, 0:T], in_=big2.t[0:T, :, 0:T], func=AF.Exp),
                     reads=[big2.b], writes=[big2.b])
                yield
                for g in range(2):
                    S.op("pe", lambda e, g=g, cs_=cs_: e.matmul(PB[4].t[0:T, 32 + g * 128:32 + g * 128 + T], BCT.t[:, g, cs_],
                                                                 BCT.t[:, 2 + g, cs_], start=True, stop=True),
                         reads=[BCT.b], writes=[PB[4].sub("cb")])
                cbv = PB[4].t[0:T, 32:288].rearrange("p (g l) -> p g l", l=128)[:, :, 0:T]
                S.op("dve", lambda e: TT(e, MT.t[0:T, :, 0:T].rearrange("p (g h) l -> p g h l", h=4),
                                         cbv.unsqueeze(2).to_broadcast([T, 2, 4, T]),
                                         big2.t[0:T, :, 0:T].rearrange("p (g h) l -> p g h l", h=4), ALU.mult),
                     reads=[PB[4].sub("cb"), big2.b], writes=[MT.b])
                yield
                S.op("pool", lambda e, cs_=cs_: TT(e, CdT.t[:, :, 0:T].rearrange("p (g h) l -> p g h l", h=4),
                                                   BCT.t[:, 2:4, cs_].unsqueeze(2).to_broadcast([128, 2, 4, T]),
                                                   eA.t[:, :, 0:T].rearrange("p (g h) l -> p g h l", h=4), ALU.mult),
                     reads=[BCT.b, eA.b], writes=[CdT.b])
                yield
                ypb = PB[7]
                if is_s:
                    S.op("dve", lambda e: e.tensor_copy(out=dAx.t[0:T], in_=dtm.t[0:T, 8:16].unsqueeze(2).to_broadcast([T, 8, 64])),
                         reads=[dtm.b], writes=[dAx.b])
                    for pr in range(4):
                        S.op("pe", lambda e, pr=pr: e.matmul(PB[4].t[:, 288 + pr * 16:288 + (pr + 1) * 16],
                                                             dAx.t[0:T, 2 * pr:2 * pr + 2, :], segi, start=True, stop=True),
                             reads=[dAx.b, cst.b], writes=[PB[4].sub("dec")])
                    S.op("act", lambda e: e.activation(out=decfm.t[:].rearrange("p a s -> p (a s)"), in_=PB[4].t[:, 288:352], func=AF.Exp),
                         reads=[PB[4].sub("dec")], writes=[decfm.b])
                    stv = stssd_d.rearrange("j (pr hl) p n -> j (hl p) pr n", hl=2)
                    osv = o_ssds.rearrange("j (pr hl) p n -> j (hl p) pr n", hl=2)
                    S.dma("act", h0n[0].t[:], stv[0], writes=[h0n[0].b])
                    for j in range(NS):
                        yield
                        jj = j % 2
                        if j + 1 < NS:
                            S.dma("act", h0n[1 - jj].t[:], stv[j + 1], writes=[h0n[1 - jj].b])
                        pbt = PB[jj]
                        for pr in range(4):
                            S.op("pe", lambda e, pr=pr, jj=jj, pbt=pbt: e.transpose(pbt.t[:, pr * 128:(pr + 1) * 128], h0n[jj].t[:, pr, :], ident),
                                 reads=[h0n[jj].b, cst.b], writes=[pbt.b])
                        S.op("act", lambda e, jj=jj, pbt=pbt: e.activation(out=h0T[jj].t[:].rearrange("p h q -> p (h q)"), in_=pbt.t[:, :], func=AF.Copy),
                             reads=[pbt.b], writes=[h0T[jj].b])
                        for h in range(8):
                            pr, hl = h // 2, h % 2
                            S.op("pe", lambda e, h=h, pr=pr, hl=hl, jj=jj, j=j: e.matmul(
                                ypb.t[64 * hl:64 * hl + 64, pr * T + LS * j:pr * T + LS * j + LS], h0T[jj].t[:, h, :],
                                CdT.t[:, h, LS * j:LS * j + LS], start=(j == 0 and pr == 0), stop=False, skip_group_check=True),
                                reads=[h0T[jj].b, CdT.b], writes=[ypb.b])
                        S.op("dve", lambda e, jj=jj, j=j: e.tensor_scalar(out=Bj[jj].t[0:T], in0=Btm.t[0:T], scalar1=segi[:, j:j + 1],
                                                                          scalar2=None, op0=ALU.mult),
                             reads=[Btm.b, cst.b], writes=[Bj[jj].b])
                        pby = PB[3]
                        for pr in range(4):
                            S.op("pe", lambda e, pr=pr, jj=jj, pby=pby: e.matmul(
                                pby.t[:, pr * 128:(pr + 1) * 128], Xdec.t[0:T, 2 * pr:2 * pr + 2, :], Bj[jj].t[0:T, pr // 2, :],
                                start=True, stop=True), reads=[Xdec.b, Bj[jj].b], writes=[pby.b])
                        S.op("dve", lambda e, jj=jj, j=j: TT(e, hn[jj].t[:], h0n[jj].t[:],
                                                             decfm.t[:, :, j:j + 1].to_broadcast([128, 4, 128]), ALU.mult),
                             reads=[h0n[jj].b, decfm.b], writes=[hn[jj].b])
                        S.op("dve", lambda e, jj=jj, pby=pby: TT(e, hn[jj].t[:], hn[jj].t[:],
                                                                 pby.t[:, :].rearrange("p (a n) -> p a n", n=128), ALU.add),
                             reads=[hn[jj].b, pby.b], writes=[hn[jj].b])
                        S.dma("sp", osv[j], hn[jj].t[:], reads=[hn[jj].b], buf=hn[jj].b)
                    outbufs.extend([hn[0].b, hn[1].b])
                for h in range(8):
                    pr, hl = h // 2, h % 2
                    out = ypb.t[64 * hl:64 * hl + 64, pr * T:(pr + 1) * T]
                    S.op("pe", lambda e, h=h, out=out, pr=pr: e.matmul(out, Xtm.t[0:T, h, :], MT.t[0:T, h, 0:T],
                                                                       start=(pr == 0 and not is_s), stop=is_s, skip_group_check=True),
                         reads=[Xtm.b, MT.b], writes=[ypb.b])
                    if not is_s:
                        S.op("pe", lambda e, h=h, out=out: e.matmul(out, STb.t[:, h, :], CdT.t[:, h, 0:T], start=False, stop=True,
                                                                    skip_group_check=True),
                             reads=[STb.b, CdT.b], writes=[ypb.b])
                yield
                for pr in range(4):
                    S.op("dve", lambda e, pr=pr, cs_=cs_: e.scalar_tensor_tensor(
                        out=yg.t[:, pr, 0:T], in0=xsT.t[:, pr, cs_], scalar=prm.t[:, P_SSDFM + pr:P_SSDFM + pr + 1],
                        in1=ypb.t[:, pr * T:(pr + 1) * T], op0=ALU.mult, op1=ALU.add),
                        reads=[xsT.b, prm.b, ypb.b], writes=[yg.b])
                S.op("dve", lambda e, cs_=cs_: TT(e, yg.t[:, :, 0:T], yg.t[:, :, 0:T], szT.t[:, :, cs_], ALU.mult),
                     reads=[yg.b, szT.b], writes=[yg.b])
                S.op("dve", lambda e: TT(e, ysqb.t[:, :, 0:T], yg.t[:, :, 0:T], yg.t[:, :, 0:T], ALU.mult),
                     reads=[yg.b], writes=[ysqb.b])
                for g in range(2):
                    for k in range(2):
                        S.op("pe", lambda e, g=g, k=k: e.matmul(PB[3].t[:, g * T:(g + 1) * T], onesb1.t[:], ysqb.t[:, 2 * g + k, 0:T],
                                                                start=(k == 0), stop=(k == 1)),
                             reads=[onesb1.b, ysqb.b], writes=[PB[3].b])
                S.op("act", lambda e: e.activation(out=rsb.t[:, :, 0:T], in_=PB[3].t[:, 0:2 * T].rearrange("p (g l) -> p g l", l=T),
                                                   func=AF.Ln, scale=1.0 / 256, bias=EPS), reads=[PB[3].b], writes=[rsb.b])
                S.op("act", lambda e: e.activation(out=rsb.t[:, :, 0:T], in_=rsb.t[:, :, 0:T], func=AF.Exp, scale=-0.5),
                     reads=[rsb.b], writes=[rsb.b])
                for pr in range(4):
                    S.op("dve", lambda e, pr=pr: e.scalar_tensor_tensor(
                        out=mixt[ti % 2].t[:, pr, c0:c0 + T], in0=yg.t[:, pr, 0:T],
                        scalar=prm.t[:, P_SSDFM + 4 + pr:P_SSDFM + 5 + pr], in1=rsb.t[:, pr // 2, 0:T], op0=ALU.mult, op1=ALU.mult),
                        reads=[yg.b, prm.b, rsb.b], writes=[mixt[ti % 2].sub("ssd")])
                yield
                if not is_s:
                    for g in range(2):
                        S.op("pe", lambda e, g=g: e.matmul(PB[6].t[:, g * 256:(g + 1) * 256], Btm.t[0:T, g, :],
                                                           Xdec.t[0:T, 4 * g:4 * g + 4, :], start=True, stop=True),
                             reads=[Btm.b, Xdec.b], writes=[PB[6].b])
                    S.op("dve", lambda e: TT(e, ST.t[:], ST.t[:], eA.t[:, :, T - 1:T].to_broadcast([128, 8, 64]), ALU.mult),
                         reads=[ST.b, eA.b], writes=[ST.b])
                    S.op("dve", lambda e: TT(e, ST.t[:], ST.t[:], PB[6].t[:, :].rearrange("p (h q) -> p h q", q=64), ALU.add),
                         reads=[ST.b, PB[6].b], writes=[ST.b])
                    S.op("act", lambda e: e.activation(out=STb.t[:], in_=ST.t[:], func=AF.Copy), reads=[ST.b], writes=[STb.b])
            if ti == 7:
                S.dma("sp", o_ssdp, ST.t[:].rearrange("p h q -> p (h q)"), reads=[ST.b], buf=ST.b)
                outbufs.append(ST.b)

            ckpt("D%d" % ti)
            yield

        def chain2(ti):
            t0, NT, is_s = TILES_A[ti]
            u5T = u5Ts[ti % 2]
            if is_s:
                S.dma("sp", sts5.t[:].rearrange("p a s q -> p (a s q)"), sts5_d, writes=[sts5.b])
            if not is_s:
                groups = [(list(range(16)), k * T5, T5) for k in range(NT // T5)]
            else:
                groups = [(list(range(8)), 0, 64), (list(range(8, 16)), 0, 64)]
            def emit_bu(g_):
                slist_, tk0_, ntok_ = groups[g_]
                bus = busd[g_ % 2]
                for part, pb in ((0, PB[5]), (1, PB[6])):
                    for idx, s in enumerate(slist_):
                        S.op("pe", lambda e, part=part, pb=pb, idx=idx, s=s: e.matmul(
                            pb.t[:, idx * ntok_:(idx + 1) * ntok_], s5BT.t[:, part, s, :], u5T.t[:, s // 4, tk0_:tk0_ + ntok_],
                            start=True, stop=True), reads=[s5BT.b, u5T.b], writes=[pb.b])
                S.op("act", lambda e: e.activation(out=bus[0].t[:], in_=PB[5].t[:, :], func=AF.Copy), reads=[PB[5].b], writes=[bus[0].b])
                S.op("act", lambda e: e.activation(out=bus[1].t[:], in_=PB[6].t[:, :], func=AF.Copy), reads=[PB[6].b], writes=[bus[1].b])
            def views(g_):
                slist_, tk0_, ntok_ = groups[g_]
                s0_ = slist_[0]
                if not is_s:
                    V3 = lambda ap: ap.rearrange("p (s t) -> p s t", t=T5)
                    QR, QI = Qtab.t[:, 0], Qtab.t[:, 1]
                    PR_, PI_ = Ptab.t[:, 0], Ptab.t[:, 1]
                    msk = mask32.t[:].rearrange("p s t -> p (s t)")
                    first = lambda ap: V3(ap)[:, :, 0]
                    cin_r, cin_i = s5cr.t[:, 0, :], s5cr.t[:, 1, :]
                else:
                    V3 = lambda ap: ap.rearrange("p (s q b) -> p s q b", q=NS, b=LS)
                    bc = lambda ap: ap.unsqueeze(2).to_broadcast([128, 8, NS, LS])
                    QR, QI = bc(Qtab.t[:, 0, s0_:s0_ + 8, 0:LS]), bc(Qtab.t[:, 1, s0_:s0_ + 8, 0:LS])
                    PR_, PI_ = bc(Ptab.t[:, 0, s0_:s0_ + 8, 0:LS]), bc(Ptab.t[:, 1, s0_:s0_ + 8, 0:LS])
                    msk = mask4.t[:].rearrange("p s t -> p (s t)")
                    first = lambda ap: V3(ap)[:, :, :, 0]
                    cin_r, cin_i = sts5.t[:, 0, s0_:s0_ + 8, :], sts5.t[:, 1, s0_:s0_ + 8, :]
                return V3, QR, QI, PR_, PI_, msk, first, cin_r, cin_i
            vsets = [[s5v[0], s5v[1]], [s5vb[0], s5vb[1]]]

            def mults_adds(g_):
                V3, QR, QI, PR_, PI_, msk, first, cin_r, cin_i = views(g_)
                bus = busd[g_ % 2]
                br, bi = V3(bus[0].t[:]), V3(bus[1].t[:])
                t1, t2, t3, t4 = s5t[0], s5t[1], s5t34[0], s5t34[1]
                vr, vi = vsets[g_ % 2]
                tb = [Qtab.b]
                for (o, a, b_, rd) in ((t1, QR, br, bus[0].b), (t2, QI, bi, bus[1].b), (t3, QR, bi, bus[1].b), (t4, QI, br, bus[0].b)):
                    S.op("dve", lambda e, o=o, a=a, b_=b_: TT(e, V3(o.t[:]), a, b_, ALU.mult), reads=tb + [rd], writes=[o.b])
                S.op(ENG_ADDS, lambda e: TT(e, vr.t[:], t1.t[:], t2.t[:], ALU.subtract), reads=[t1.b, t2.b], writes=[vr.b])
                S.op(ENG_ADDS, lambda e: TT(e, vi.t[:], t3.t[:], t4.t[:], ALU.add), reads=[t3.b, t4.b], writes=[vi.b])
            emit_bu(0)
            if len(groups) > 1:
                emit_bu(1)
            mults_adds(0)
            pend_y5 = [None]
            for gi_, (slist, tk0, ntok) in enumerate(groups):
                yield
                ns = len(slist)
                s0 = slist[0]
                V3, QR, QI, PR_, PI_, msk, first, cin_r, cin_i = views(gi_)
                vr, vi = vsets[gi_ % 2]
                if gi_ + 1 < len(groups):
                    mults_adds(gi_ + 1)
                    yield
                if gi_ + 2 < len(groups):
                    emit_bu(gi_ + 2)
                S.op("dve", lambda e: TT(e, first(vr.t[:]), first(vr.t[:]), cin_r, ALU.add), reads=[vr.b, s5cr.b, sts5.b], writes=[vr.b])
                S.op("dve", lambda e: TT(e, first(vi.t[:]), first(vi.t[:]), cin_i, ALU.add), reads=[vi.b, s5cr.b, sts5.b], writes=[vi.b])
                yield
                s5k[0] ^= 1
                gr, gi2 = s5g[s5k[0]][0], s5g[s5k[0]][1]
                S.op("dve", lambda e: e.tensor_tensor_scan(out=gr.t[:], data0=msk, data1=vr.t[:], initial=0.0, op0=ALU.mult, op1=ALU.add),
                     reads=[vr.b, mask32.b, mask4.b], writes=[gr.b])
                S.op("dve", lambda e: e.tensor_tensor_scan(out=gi2.t[:], data0=msk, data1=vi.t[:], initial=0.0, op0=ALU.mult, op1=ALU.add),
                     reads=[vi.b, mask32.b, mask4.b], writes=[gi2.b])
                yield
                hp = s5h[gi_ % 2]
                hr, hi = hp, hp
                for (o, a, b_) in ((hp[0], PR_, gr), (hp[1], PI_, gi2), (hp[2], PR_, gi2), (hp[3], PI_, gr)):
                    S.op(ENG_OUTROT, lambda e, o=o, a=a, b_=b_: TT(e, V3(o.t[:]), a, V3(b_.t[:]), ALU.mult),
                         reads=[Ptab.b, b_.b], writes=[o.b])
                yield
                if not is_s:
                    glr, gli = V3(gr.t[:])[:, :, T5 - 1], V3(gi2.t[:])[:, :, T5 - 1]
                    plr, pli = Ptab.t[:, 0, :, T5 - 1], Ptab.t[:, 1, :, T5 - 1]
                    c_ = lambda i: s5c.t[:, i, :]
                    outr, outi = s5cr.t[:, 0, :], s5cr.t[:, 1, :]
                else:
                    glr, gli = V3(gr.t[:])[:, :, :, LS - 1], V3(gi2.t[:])[:, :, :, LS - 1]
                    plr = Ptab.t[:, 0, s0:s0 + 8, LS - 1:LS].to_broadcast([128, 8, NS])
                    pli = Ptab.t[:, 1, s0:s0 + 8, LS - 1:LS].to_broadcast([128, 8, NS])
                    c_ = lambda i: hn[0].t[:, i, :].rearrange("p (s q) -> p s q", q=NS)
                    outr, outi = s5fin.t[:, 0, s0:s0 + 8, 1:17], s5fin.t[:, 1, s0:s0 + 8, 1:17]
                cb_ = [s5c.b, hn[0].b]
                if not is_s:
                    pl2 = Ptab.t[:, :, :, T5 - 1]
                    ca, cb2 = s5c.t[:, 0:2, :], s5c.t[:, 2:4, :]
                    S.op("dve", lambda e: TT(e, ca, pl2, glr.unsqueeze(1).to_broadcast([128, 2, 16]), ALU.mult),
                         reads=[Ptab.b, gr.b] + cb_, writes=cb_)
                    S.op("dve", lambda e: TT(e, cb2, pl2, gli.unsqueeze(1).to_broadcast([128, 2, 16]), ALU.mult),
                         reads=[Ptab.b, gi2.b] + cb_, writes=cb_)
                    S.op("dve", lambda e: TT(e, outr, c_(0), c_(3), ALU.subtract), reads=cb_, writes=[s5cr.b, s5fin.b])
                    S.op("dve", lambda e: TT(e, outi, c_(2), c_(1), ALU.add), reads=cb_, writes=[s5cr.b, s5fin.b])
                else:
                    cseq = [(c_(0), plr, glr, ALU.mult), (c_(1), pli, gli, ALU.mult), (c_(2), plr, gli, ALU.mult), (c_(3), pli, glr, ALU.mult)]
                    for (o, a, b, op) in cseq:
                        S.op("dve", lambda e, o=o, a=a, b=b, op=op: TT(e, o, a, b, op), reads=[Ptab.b, gr.b, gi2.b] + cb_, writes=cb_)
                    S.op("dve", lambda e: TT(e, outr, c_(0), c_(1), ALU.subtract), reads=cb_, writes=[s5cr.b, s5fin.b])
                    S.op("dve", lambda e: TT(e, outi, c_(2), c_(3), ALU.add), reads=cb_, writes=[s5cr.b, s5fin.b])
                yield
                def emit_y5(gi_=gi_, slist=slist, tk0=tk0, ntok=ntok, hr=hr, hi=hi):
                    y5c0 = 352
                    nq = 4 if not is_s else 2
                    for qi in range(nq):
                        q = qi if not is_s else 2 * gi_ + qi
                        S.op("pe", lambda e, q=q, qi=qi: e.matmul(PB[4].t[:, y5c0 + qi * ntok:y5c0 + (qi + 1) * ntok], dg5.t[:, q, :],
                                                                  u5T.t[:, q, tk0:tk0 + ntok], start=(qi == 0), stop=False, skip_group_check=True),
                             reads=[dg5.b, u5T.b], writes=[PB[4].sub("y5")])
                    for idx, s in enumerate(slist):
                        qi = (s // 4) if not is_s else (s // 4 - 2 * gi_)
                        out = PB[4].t[32 * (s % 4):32 * (s % 4) + 32, y5c0 + qi * ntok:y5c0 + (qi + 1) * ntok]
                        for j4, lw in enumerate((s5CT.t[:, 0, s, :], s5CTn.t[:, s, :], s5CT.t[:, 1, s, :], s5CT.t[:, 1, s, :])):
                            S.op("pe", lambda e, j4=j4, lw=lw: e.matmul(out, lw, hr[j4].t[:, idx * ntok:(idx + 1) * ntok],
                                                                        start=False, stop=(j4 == 3), skip_group_check=True,
                                                                        tile_position=(0, 32 * (s % 4))),
                                 reads=[s5CT.b, s5CTn.b, hr[j4].b], writes=[PB[4].sub("y5")])
                    q0 = 0 if not is_s else 2 * gi_
                    S.op("act", lambda e: e.activation(out=y5pre.t[:, q0:q0 + nq, tk0:tk0 + ntok],
                                                       in_=PB[4].t[:, y5c0:y5c0 + nq * ntok].rearrange("p (q t) -> p q t", t=ntok), func=AF.Copy),
                         reads=[PB[4].sub("y5")], writes=[y5pre.b])
                if pend_y5[0] is not None:
                    pend_y5[0]()
                    yield
                pend_y5[0] = emit_y5
            if pend_y5[0] is not None:
                pend_y5[0]()
                pend_y5[0] = None
                yield
            if ti == 7:
                S.op("dve", lambda e: e.tensor_copy(out=s5fin.t[:, :, :, 0], in_=s5cr.t[:]), reads=[s5cr.b], writes=[s5fin.b])
            if is_s:
                S.dma("sp", o_s5, s5fin.t[:].rearrange("p a s q -> p (a s q)"), reads=[s5fin.b], buf=s5fin.b)
                outbufs.append(s5fin.b)
            if ti == 0:
                dump("y5pre", y5pre.t[:].rearrange("p k t -> p (k t)"), [128, 4 * NTM], [y5pre.b])
            ckpt("E%d" % ti)
            yield
            S.op("act", lambda e: e.activation(out=g5.t[:, :, 0:NT], in_=y5pre.t[:, :, 0:NT], func=AF.Gelu), reads=[y5pre.b], writes=[g5.b])
            for m in range(4):
                yield
                pb = next_pb()
                for q in range(4):
                    S.op("pe", lambda e, m=m, q=q, pb=pb: e.matmul(pb.t[:, 0:NT], wglu_sb.t[:, q, m * 128:(m + 1) * 128], g5.t[:, q, 0:NT],
                                                                   start=(q == 0), stop=(q == 3)),
                         reads=[wglu_sb.b, g5.b], writes=[pb.b])
                S.op("act", lambda e, m=m, pb=pb: e.activation(out=sgl.t[:, 0:NT], in_=pb.t[:, 0:NT], func=AF.Sigmoid,
                                                               bias=prm.t[:, P_S5M + 4 + m:P_S5M + 5 + m]),
                     reads=[pb.b, prm.b], writes=[sgl.b])
                S.op("dve", lambda e, m=m: TT(e, mixt[ti % 2].t[:, 4 + m, 0:NT], g5.t[:, m, 0:NT], sgl.t[:, 0:NT], ALU.mult),
                     reads=[g5.b, sgl.b], writes=[mixt[ti % 2].sub("s5")])
            S.dma("sp", mixd[:, :, t0:t0 + NT], mixt[ti % 2].t[:, :, 0:NT], reads=mixt[ti % 2].allb(), writes=[mixdb[ti]], buf=mixdb[ti])
            ckpt("T%d" % ti)
            if ti == 0:
                dump("mix0", mixt[0].t[:, :, 0:NTM], [128, 8, NTM], mixt[0].allb())
            yield

        import os as _os
        RATIO = int(_os.environ.get("K_RATIO", "1"))

        def drive(gens, ada_every=0):
            gens = [g for g in gens if g is not None]
            n = 0
            while gens:
                for gi__, g in enumerate(list(gens)):
                    for _ in range((RATIO if gi__ == 0 else 1) if RATIO > 0 else (-RATIO if gi__ == 1 else 1)):
                        try:
                            next(g)
                        except StopIteration:
                            if g in gens:
                                gens.remove(g)
                            break
                n += 1
                if ada_every and n % ada_every == 0:
                    ada_step()
        ada_state[0] = 0
        drive([chain1(0)], ada_every=12)
        for ti_ in range(len(TILES_A)):
            if ti_ == 7:
                while ada_state[1] < len(ADA_CH):
                    ada_step()
                fill_x(a1x, amod.t[:, 0:8, 1:17], [amod.b])
                fill_x(sh1x, chunkmod(MOD_SH1)[:, :, 1:17], [mod.b])
                make_amod([(1, (4, 1)), (2, (7, 2))])
            drive([chain2(ti_), chain1(ti_ + 1) if ti_ + 1 < len(TILES_A) else None], ada_every=(10 if ti_ < 7 else 0))
        dump("mixS", mixt[0].t[:, :, 0:64], [128, 8, 64], mixt[0].allb())
        S.barrier()
        ckpt("1a")
        A.lo = LO_P1
        x1T = A.alloc("x1T", [128, 8, NTOK], F32, top=True)
        vT = A.alloc("vT", [128, 8, NTOK], BF16, top=True)
        wout_sb = A.alloc("wout_sb", [128, 8, D], BF16)
        wout_v = wout.rearrange("(kt p) n -> p kt n", p=128)
        for kh in range(4):
            S.dma("pool", wout_sb.t[:, 2 * kh:2 * kh + 2, :], wout_v[:, 2 * kh:2 * kh + 2, :], writes=[wout_sb.sub(kh)])
        mixb = [A.alloc("mixb%d" % i, [128, 8, 512], BF16) for i in range(2)]

        def load_mix(ti):
            t0, NT, is_s = TILES_B[ti]
            tiles_a = [i for i, (a0, n0, s0_) in enumerate(TILES_A) if a0 >= t0 and a0 < t0 + NT]
            S.dma("sp", mixb[ti % 2].t[:, :, 0:NT], mixd[:, :, t0:t0 + NT], reads=[mixdb[i] for i in tiles_a], writes=[mixb[ti % 2].b])
        xtm2 = A.alloc("xtm2", [128, 4, D], F32)
        xTm = [A.alloc("xTm%d" % i, [128, 512], F32) for i in range(2)]
        sqb = [A.alloc("sqb%d" % i, [128, 512], BF16) for i in range(2)]
        onesb = A.alloc("onesb", [128, 128], BF16)
        S.op("dve", lambda e: e.memset(onesb.t[:], 1.0), writes=[onesb.b])
        tmp2 = [A.alloc("tmp2_%d" % i, [128, 512], F32) for i in range(2)]
        rstdb = [A.alloc("rstdb%d" % i, [128, 512], F32) for i in range(2)]
        g1x = expand_mod("g1x", chunkmod(MOD_G1)[:, :, 1:17], [mod.b])
        a2x = expand_mod("a2x", amod.t[:, 8:16, 1:17], [amod.b])
        sh2x = expand_mod("sh2x", chunkmod(MOD_SH2)[:, :, 1:17], [mod.b])
        print("arena p1b: lo=%d hi=%d" % (A.lo, A.hi))
        TILES_B = [(i * 512, 512, False) for i in range(4)] + [(SEQ, 64, True)]

        def load_x2(ti):
            t0, NT, is_s = TILES_B[ti]
            for blk in range((NT + 127) // 128):
                rows = min(128, NT - blk * 128)
                S.dma("sp", xtm2.t[0:rows, blk, :], xin[t0 + blk * 128:t0 + blk * 128 + rows, :], writes=[xtm2.sub(blk)])
        load_x2(0)
        load_mix(0)

        def stat_accum(src_ap, m, NT, pbs, defer=None):
            sq = sqb[m % 2]
            S.op("act", lambda e: e.activation(out=sq.t[:, 0:NT], in_=src_ap, func=AF.Square), reads=[x1T.sub(m)], writes=[sq.b])

            def mm(m=m, sq=sq):
                S.op("pe", lambda e: e.matmul(pbs.t[:, 0:NT], onesb.t[:], sq.t[:, 0:NT], start=(m == 0), stop=(m == 7)),
                     reads=[onesb.b, sq.b], writes=[pbs.b])
            if defer is None:
                mm()
            else:
                if defer[0] is not None:
                    defer[0]()
                defer[0] = mm
                if m == 7:
                    defer[0]()
                    defer[0] = None

        def stat_finish(NT, pbs, rs):
            S.op("act", lambda e: e.activation(out=rs.t[:, 0:NT], in_=pbs.t[:, 0:NT], func=AF.Ln, scale=1.0 / D, bias=EPS),
                 reads=[pbs.b], writes=[rs.b])
            S.op("act", lambda e: e.activation(out=rs.t[:, 0:NT], in_=rs.t[:, 0:NT], func=AF.Exp, scale=-0.5), reads=[rs.b], writes=[rs.b])

        def b_part1(ti):
            t0, NT, is_s = TILES_B[ti]
            nblk = (NT + 127) // 128
            tsl = slice(t0, t0 + NT)
            pbs = PB[4 + ti % 2]
            dfr = [None]
            for m in range(8):
                pbx = PB[2 + m % 2]
                xm = xTm[m % 2]
                for blk in range(nblk):
                    rows = min(128, NT - blk * 128)
                    S.op("pe", lambda e, blk=blk, rows=rows: e.transpose(
                        pbx.t[:, blk * 128:blk * 128 + rows], xtm2.t[0:rows, blk, m * 128:(m + 1) * 128], cst.t[0:rows, C_ID:C_ID + rows]),
                        reads=[xtm2.sub(blk), cst.b], writes=[pbx.b])
                S.op("act", lambda e: e.activation(out=xm.t[:, 0:NT], in_=pbx.t[:, 0:NT], func=AF.Copy), reads=[pbx.b], writes=[xm.b])
                pb = next_pb()
                for kt in range(8):
                    S.op("pe", lambda e, kt=kt: e.matmul(pb.t[:, 0:NT], wout_sb.t[:, kt, m * 128:(m + 1) * 128], mixb[ti % 2].t[:, kt, 0:NT],
                                                         start=(kt == 0), stop=(kt == 7)),
                         reads=[wout_sb.sub(kt // 2), mixb[ti % 2].b], writes=[pb.b])
                if m == 0 and ti + 1 < len(TILES_B):
                    load_mix(ti + 1)
                if not is_s:
                    S.op("dve", lambda e: e.scalar_tensor_tensor(
                        out=x1T.t[:, m, tsl], in0=pb.t[:, 0:NT], scalar=mod.t[:, 8 * MOD_G1 + m, 0:1], in1=xm.t[:, 0:NT],
                        op0=ALU.mult, op1=ALU.add), reads=[pb.b, mod.b, xm.b], writes=[x1T.sub(m)])
                else:
                    S.op("dve", lambda e: TT(e, tmp2[0].t[:, 0:NT], pb.t[:, 0:NT], g1x.t[:, m, :], ALU.mult),
                         reads=[pb.b, g1x.b], writes=[tmp2[0].b])
                    S.op("dve", lambda e: TT(e, x1T.t[:, m, tsl], tmp2[0].t[:, 0:NT], xm.t[:, 0:NT], ALU.add),
                         reads=[tmp2[0].b, xm.b], writes=[x1T.sub(m)])
                stat_accum(x1T.t[:, m, tsl], m, NT, pbs, defer=dfr)
                yield
            if ti + 1 < len(TILES_B):
                load_x2(ti + 1)
            yield

        def b_part2(ti):
            t0, NT, is_s = TILES_B[ti]
            tsl = slice(t0, t0 + NT)
            rs = rstdb[ti % 2]
            stat_finish(NT, PB[4 + ti % 2], rs)
            yield
            for m in range(8):
                tq = tmp2[m % 2]
                S.op("dve", lambda e: TT(e, tq.t[:, 0:NT], x1T.t[:, m, tsl], rs.t[:, 0:NT], ALU.mult),
                     reads=[x1T.sub(m), rs.b], writes=[tq.b])
                if not is_s:
                    S.op("act", lambda e: e.activation(out=vT.t[:, m, tsl], in_=tq.t[:, 0:NT], func=AF.Identity,
                                                       scale=amod.t[:, 8 + m, 0:1], bias=mod.t[:, 8 * MOD_SH2 + m, 0:1]),
                         reads=[tq.b, amod.b, mod.b], writes=[vT.sub(m)])
                else:
                    S.op("dve", lambda e: TT(e, tq.t[:, 0:NT], tq.t[:, 0:NT], a2x.t[:, m, :], ALU.mult),
                         reads=[tq.b, a2x.b], writes=[tq.b])
                    S.op("dve", lambda e: TT(e, vT.t[:, m, tsl], tq.t[:, 0:NT], sh2x.t[:, m, :], ALU.add),
                         reads=[tq.b, sh2x.b], writes=[vT.sub(m)])
                yield
            if ti == 0:
                dump("x1p", x1T.t[:, :, 0:256], [128, 8, 256], x1T.allb())
                dump("vp", vT.t[:, :, 0:256], [128, 8, 256], vT.allb())
        drive([b_part1(0)])
        for ti_ in range(len(TILES_B)):
            drive([b_part2(ti_), b_part1(ti_ + 1) if ti_ + 1 < len(TILES_B) else None])
        S.barrier()
        ckpt("1b")

        A.lo = LO_GLOBAL
        tmp2 = [A.alloc("tmp3_%d" % i, [128, 512], F32) for i in range(2)]
        rstdb = [A.alloc("rstd3_%d" % i, [128, 512], F32) for i in range(2)]
        sqb = [A.alloc("sqb3_%d" % i, [128, 512], BF16) for i in range(2)]
        onesb = A.alloc("onesb3", [128, 128], BF16)
        S.op("dve", lambda e: e.memset(onesb.t[:], 1.0), writes=[onesb.b])
        g2x = expand_mod("g2x", chunkmod(MOD_G2)[:, :, 1:17], [mod.b])
        afx = expand_mod("afx", amod.t[:, 16:24, 1:17], [amod.b])
        shfx = expand_mod("shfx", chunkmod(MOD_SHF)[:, :, 1:17], [mod.b])
        LO_P2 = A.lo
        hT = A.alloc("hT", [128, 6, NTOK], BF16)
        wgs = [A.alloc("wgs%d" % i, [128, 8, 256], BF16) for i in range(3)]
        wus = [A.alloc("wus%d" % i, [128, 8, 256], BF16) for i in range(3)]
        wds = [A.alloc("wds%d" % i, [128, 6, D], BF16) for i in range(2)]
        sgt = [A.alloc("sgt%d" % i, [128, 512], BF16) for i in range(2)]
        print("arena p2: lo=%d hi=%d" % (A.lo, A.hi))
        wg_v = wg.rearrange("(kt p) n -> p kt n", p=128)
        wu_v = wu.rearrange("(kt p) n -> p kt n", p=128)
        wd_v = wd.rearrange("(j p) n -> p j n", p=128)
        QUARTERS = [(0, 6), (6, 12), (12, 18), (18, 22)]
        SLABS = [(q, ja + 2 * s) for q, (ja, jb) in enumerate(QUARTERS) for s in range((jb - ja) // 2)]

        def load_gu(si):
            q, j0 = SLABS[si]
            S.dma("pool", wgs[si % 3].t[:], wg_v[:, :, j0 * 128:(j0 + 2) * 128], writes=[wgs[si % 3].b])
            S.dma("pool", wus[si % 3].t[:], wu_v[:, :, j0 * 128:(j0 + 2) * 128], writes=[wus[si % 3].b])

        def load_wd(q):
            ja, jb = QUARTERS[q]
            for jh in range(0, jb - ja, 2):
                S.dma("pool", wds[q % 2].t[:, jh:jh + 2, :], wd_v[:, ja + jh:ja + jh + 2, :], writes=[wds[q % 2].b])
        load_gu(0)
        load_gu(1)
        load_wd(0)
        gbank = [0]
        si = 0
        for q, (ja, jb) in enumerate(QUARTERS):
            if q + 1 < 4:
                load_wd(q + 1)
            for s in range((jb - ja) // 2):
                if si + 2 < len(SLABS):
                    load_gu(si + 2)
                wgt, wut = wgs[si % 3], wus[si % 3]
                for jc in range(2):
                    jj = 2 * s + jc
                    for (t0, NT, is_s) in TILES_B:
                        tsl = slice(t0, t0 + NT)
                        gbank[0] ^= 1
                        pbg, pbu = PB[gbank[0]], PB[2 + gbank[0]]
                        for (wt, pb_) in ((wgt, pbg), (wut, pbu)):
                            for kt in range(8):
                                S.op("pe", lambda e, kt=kt, wt=wt, pb_=pb_: e.matmul(
                                    pb_.t[:, 0:NT], wt.t[:, kt, jc * 128:(jc + 1) * 128], vT.t[:, kt, tsl], start=(kt == 0), stop=(kt == 7)),
                                    reads=[wt.b] + vT.allb(), writes=[pb_.b])
                        sg_ = sgt[gbank[0]]
                        S.op("act", lambda e, pbg=pbg, sg_=sg_: e.activation(out=sg_.t[:, 0:NT], in_=pbg.t[:, 0:NT], func=AF.Silu),
                             reads=[pbg.b], writes=[sg_.b])
                        S.op("dve", lambda e, pbu=pbu, sg_=sg_: TT(e, hT.t[:, jj, tsl], sg_.t[:, 0:NT], pbu.t[:, 0:NT], ALU.mult),
                             reads=[sg_.b, pbu.b], writes=[hT.sub(jj)])
                si += 1
            nj = jb - ja
            wdt = wds[q % 2]
            for (t0, NT, is_s) in TILES_B:
                tsl = slice(t0, t0 + NT)
                for m in range(8):
                    pb = PB[4 + m % 2]
                    for jj in range(nj):
                        S.op("pe", lambda e, jj=jj, m=m, pb=pb: e.matmul(pb.t[:, 0:NT], wdt.t[:, jj, m * 128:(m + 1) * 128], hT.t[:, jj, tsl],
                                                                         start=(jj == 0), stop=(jj == nj - 1)),
                             reads=[wdt.b, hT.sub(jj)], writes=[pb.b])
                    if not is_s:
                        S.op("dve", lambda e, m=m, pb=pb: e.scalar_tensor_tensor(
                            out=x1T.t[:, m, tsl], in0=pb.t[:, 0:NT], scalar=mod.t[:, 8 * MOD_G2 + m, 0:1], in1=x1T.t[:, m, tsl],
                            op0=ALU.mult, op1=ALU.add), reads=[pb.b, mod.b, x1T.sub(m)], writes=[x1T.sub(m)])
                    else:
                        S.op("dve", lambda e, m=m, pb=pb: TT(e, tmp2[0].t[:, 0:NT], pb.t[:, 0:NT], g2x.t[:, m, :], ALU.mult),
                             reads=[pb.b, g2x.b], writes=[tmp2[0].b])
                        S.op("dve", lambda e, m=m: TT(e, x1T.t[:, m, tsl], tmp2[0].t[:, 0:NT], x1T.t[:, m, tsl], ALU.add),
                             reads=[tmp2[0].b, x1T.sub(m)], writes=[x1T.sub(m)])
        S.barrier()
        ckpt("ffn")
        A.lo = LO_P2
        yTs = [A.alloc("yT%d" % i, [128, 8, 512], F32) for i in range(2)]
        ytm = [A.alloc("ytm%d" % i, [128, D], F32) for i in range(2)]
        print("arena final: lo=%d hi=%d" % (A.lo, A.hi))
        oi = [0]

        def f_part1(ti):
            t0, NT, is_s = TILES_B[ti]
            tsl = slice(t0, t0 + NT)
            yT = yTs[ti % 2]
            pbs = PB[6 + ti % 2]
            rs = rstdb[ti % 2]
            for m in range(8):
                stat_accum(x1T.t[:, m, tsl], m, NT, pbs)
                if m % 2 == 1:
                    yield
            stat_finish(NT, pbs, rs)
            yield
            for m in range(8):
                tq = tmp2[m % 2]
                S.op("dve", lambda e: TT(e, tq.t[:, 0:NT], x1T.t[:, m, tsl], rs.t[:, 0:NT], ALU.mult),
                     reads=[x1T.sub(m), rs.b], writes=[tq.b])
                if not is_s:
                    S.op("act", lambda e: e.activation(out=yT.t[:, m, 0:NT], in_=tq.t[:, 0:NT], func=AF.Identity,
                                                       scale=amod.t[:, 16 + m, 0:1], bias=mod.t[:, 8 * MOD_SHF + m, 0:1]),
                         reads=[tq.b, amod.b, mod.b], writes=[yT.sub(m)])
                else:
                    S.op("dve", lambda e: TT(e, tq.t[:, 0:NT], tq.t[:, 0:NT], afx.t[:, m, :], ALU.mult),
                         reads=[tq.b, afx.b], writes=[tq.b])
                    S.op("dve", lambda e: TT(e, yT.t[:, m, 0:NT], tq.t[:, 0:NT], shfx.t[:, m, :], ALU.add),
                         reads=[tq.b, shfx.b], writes=[yT.sub(m)])
                yield

        def f_part2(ti):
            t0, NT, is_s = TILES_B[ti]
            yT = yTs[ti % 2]
            for blk in range((NT + 127) // 128):
                rows = min(128, NT - blk * 128)
                yo = ytm[oi[0] % 2]
                oi[0] += 1
                for half in range(2):
                    pbt = PB[half]
                    for k4 in range(4):
                        kt = 4 * half + k4
                        S.op("pe", lambda e, kt=kt, k4=k4: e.transpose(
                            pbt.t[0:rows, k4 * 128:(k4 + 1) * 128], yT.t[:, kt, blk * 128:blk * 128 + rows], ident),
                            reads=[yT.sub(kt), cst.b], writes=[pbt.b])
                    if half == 0:
                        S.op("act", lambda e: e.activation(out=yo.t[0:rows, 0:512], in_=pbt.t[0:rows, :], func=AF.Copy),
                             reads=[pbt.b], writes=[yo.b])
                    else:
                        S.op("dve", lambda e: e.tensor_copy(out=yo.t[0:rows, 512:1024], in_=pbt.t[0:rows, :]),
                             reads=[pbt.b], writes=[yo.b])
                    yield
                S.dma("sp", yout[t0 + blk * 128:t0 + blk * 128 + rows, :], yo.t[0:rows, :], reads=[yo.b], buf=yo.b)
        import os as _os2
        if True:
            for ti_ in range(len(TILES_B)):
                drive([f_part1(ti_)])
                drive([f_part2(ti_)])
        else:
            drive([f_part1(0)])
            for ti_ in range(len(TILES_B)):
                drive([f_part2(ti_), f_part1(ti_ + 1) if ti_ + 1 < len(TILES_B) else None])
        S.barrier()
    return nc, dumps


def _prep_inputs(inp):
    cstv = _consts()
    prmv = _params(inp)
    BT, CT = _s5mats(inp)
    maps = []
    for i in range(NCORES):
        m = {}
        m["xin"] = np.ascontiguousarray(np.concatenate(
            [inp["x_prompt"][i], inp["x_sample"][NS * i:NS * (i + 1)].reshape(NS * LS, D)], axis=0), dtype=np.float32)
        m["cin"] = np.ascontiguousarray(np.concatenate(
            [inp["c_prompt"][i:i + 1], inp["c_sample"][NS * i:NS * (i + 1)]], axis=0), dtype=np.float32)
        m["wada"] = np.ascontiguousarray(inp["w_ada"][0], dtype=np.float32)
        m["wadaf"] = np.ascontiguousarray(inp["w_ada_f"], dtype=np.float32)
        m["win"] = np.ascontiguousarray(inp["w_in"][0], dtype=np.float32)
        m["wglu"] = np.ascontiguousarray(inp["w_glu"][0], dtype=np.float32)
        m["wout"] = np.ascontiguousarray(inp["w_out"][0], dtype=np.float32)
        m["wg"] = np.ascontiguousarray(inp["w_ffn_gate"][0], dtype=np.float32)
        m["wu"] = np.ascontiguousarray(inp["w_ffn_up"][0], dtype=np.float32)
        m["wd"] = np.ascontiguousarray(inp["w_ffn_down"][0], dtype=np.float32)
        m["cst"] = cstv
        m["prm"] = prmv
        m["s5bt"] = BT.reshape(128, -1)
        m["s5ct"] = CT.reshape(128, -1)
        m["stssd"] = np.ascontiguousarray(inp["state_ssd"][0, NS * i:NS * (i + 1)], dtype=np.float32)
        sc = inp["state_conv"][0, NS * i:NS * (i + 1)]
        m["stconv"] = np.ascontiguousarray(
            sc.reshape(NS, 3, 8, 128).transpose(3, 2, 0, 1).reshape(128, -1), dtype=np.float32)
        sr = inp["state_s5_re"][0, NS * i:NS * (i + 1)]
        si = inp["state_s5_im"][0, NS * i:NS * (i + 1)]
        st = np.stack([sr, si], 0).reshape(2, NS, 16, 128).transpose(3, 0, 2, 1)
        m["sts5"] = np.ascontiguousarray(st.reshape(128, -1), dtype=np.float32)
        maps.append(m)
    return maps


_CACHE = {}


def kernel(**inputs):
    inp = {k: np.asarray(v) for k, v in inputs.items()}
    if "nc" not in _CACHE:
        _CACHE["nc"] = build()[0]
    nc = _CACHE["nc"]
    maps = _prep_inputs(inp)
    res = run_bass_kernel_spmd(nc, maps, core_ids=list(range(NCORES)))
    R = res.results
    y_p = np.stack([R[i]["yout"][:SEQ] for i in range(NCORES)], 0)
    y_s = np.concatenate([R[i]["yout"][SEQ:].reshape(NS, LS, D) for i in range(NCORES)], 0)
    ssd_p = np.stack([R[i]["o_ssdp"].reshape(128, 8, 64).transpose(1, 2, 0) for i in range(NCORES)], 0)[None]
    ssd_s = np.concatenate([R[i]["o_ssds"] for i in range(NCORES)], 0)[None]
    conv = [R[i]["o_conv"].reshape(128, 8, 17, 3).transpose(2, 3, 1, 0).reshape(17, 3, 1024) for i in range(NCORES)]
    conv_p = np.stack([c[0] for c in conv], 0)[None]
    conv_s = np.concatenate([c[1:] for c in conv], 0)[None]
    s5 = [R[i]["o_s5"].reshape(128, 2, 16, 17).transpose(1, 3, 2, 0).reshape(2, 17, 32, 64) for i in range(NCORES)]
    re_p = np.stack([s[0, 0] for s in s5], 0)[None]
    re_s = np.concatenate([s[0, 1:] for s in s5], 0)[None]
    im_p = np.stack([s[1, 0] for s in s5], 0)[None]
    im_s = np.concatenate([s[1, 1:] for s in s5], 0)[None]
    f = lambda a: np.ascontiguousarray(a, dtype=np.float32)
    return (f(y_p), f(y_s), f(ssd_p), f(ssd_s), f(conv_p), f(conv_s), f(re_p), f(re_s), f(im_p), f(im_s))
```

```python
import math
import numpy as np
from contextlib import ExitStack
import concourse.bass as bass
import concourse.mybir as mybir
from concourse.bass_utils import run_bass_kernel_spmd

F32 = mybir.dt.float32
BF16 = mybir.dt.bfloat16
I32 = mybir.dt.int32
AF = mybir.ActivationFunctionType
ALU = mybir.AluOpType

NCORES = 8
D = 1024
SEQ = 2048
NS = 16
LS = 4
NTOK = SEQ + NS * LS
DFF = 2816
NJ = DFF // 128
INP = 2056
EPS = 1e-6
T5 = 32
TILES = [(0, 512), (512, 512), (1024, 512), (1536, 512), (2048, 64)]
PI = math.pi


class Buf:
    def __init__(self, name):
        self.name = name
        self.w = None
        self.r = []
        self.dsem = None
        self.dcnt = 0


class TL:
    def __init__(self, t, name):
        self.t = t
        self.name = name
        self.b = Buf(name)
        self.subs = {}

    def sub(self, k):
        if getattr(self, "nosub", False):
            return self.b
        if k not in self.subs:
            self.subs[k] = Buf("%s_%s" % (self.name, k))
        return self.subs[k]

    def allb(self):
        return [self.b] + list(self.subs.values())

    def __getitem__(self, k):
        return self.t[k]


class Sched:
    ENG = ["pe", "act", "dve", "pool", "sp"]

    def __init__(self, nc, es):
        self.nc = nc
        self.es = es
        self.eobj = {"pe": nc.tensor, "act": nc.scalar, "dve": nc.vector, "pool": nc.gpsimd, "sp": nc.sync}
        self.cnt = {e: 0 for e in self.ENG}
        self.sem = {e: es.enter_context(nc.semaphore("s_" + e)) for e in self.ENG}
        self.seen = {e: {} for e in self.ENG}
        self.dbufs = []
        self.ninst = 0
        self.dead = False
        self.pe_pending = None

    def _flush_pe(self):
        if self.pe_pending is not None:
            self.pe_pending.then_inc(self.sem["pe"], 1)
            self.cnt["pe"] += 1
            self.pe_pending = None

    def _deps(self, eng, reads, writes):
        deps = []
        for b in reads:
            if b.w is not None:
                deps.append(b.w)
        for b in writes:
            if b.w is not None:
                deps.append(b.w)
            deps.extend(b.r)
        waits = {}
        for (sem, val, key) in deps:
            if key == "pe" and eng == "pe":
                continue
            if self.seen[eng].get(key, 0) >= val:
                continue
            if key == "pe" and val > self.cnt["pe"]:
                self._flush_pe()
            if key not in waits or waits[key][1] < val:
                waits[key] = (sem, val)
        for key, (sem, val) in waits.items():
            self.seen[eng][key] = val
        return list(waits.values())

    def op(self, eng, fn, reads=(), writes=()):
        if self.dead:
            return None
        xr = [b for b in reads if getattr(b, "excl", False)]
        if xr:
            reads = [b for b in reads if not getattr(b, "excl", False)]
            writes = list(writes) + xr
        waits = self._deps(eng, reads, writes)
        e = self.eobj[eng]
        for (s_, v_) in waits:
            e.wait_ge(s_, v_)
        if eng == "pe":
            self.pe_pending = fn(e)
            tok = (self.sem[eng], self.cnt[eng] + 1, eng)
        else:
            self.cnt[eng] += 1
            tok = (self.sem[eng], self.cnt[eng], eng)
            fn(e).then_inc(self.sem[eng], 1)
        for b in reads:
            b.r.append(tok)
        for b in writes:
            b.w = tok
            b.r = []
        self.ninst += 1
        return tok

    def dma(self, eng, out, in_, reads=(), writes=(), buf=None, **kw):
        if self.dead:
            return None
        waits = self._deps(eng, reads, writes)
        if buf is None:
            buf = writes[0] if writes else reads[0]
        if buf.dsem is None:
            buf.dsem = self.es.enter_context(self.nc.semaphore("d_" + buf.name))
            self.dbufs.append(buf)
        buf.dcnt += 16
        tok = (buf.dsem, buf.dcnt, "d_" + buf.name)
        e = self.eobj[eng]
        for (s_, v_) in waits:
            e.wait_ge(s_, v_)
        e.dma_start(out=out, in_=in_, **kw).then_inc(buf.dsem, 16)
        for b in reads:
            b.r.append(tok)
        for b in writes:
            b.w = tok
            b.r = []
        self.ninst += 1
        return tok

    def barrier(self):
        if self.dead:
            return
        self._flush_pe()
        for e in self.ENG:
            waits = []
            for o in self.ENG:
                if o != e and self.cnt[o] > self.seen[e].get(o, 0):
                    waits.append((self.sem[o], self.cnt[o]))
                    self.seen[e][o] = self.cnt[o]
            for b in self.dbufs:
                key = "d_" + b.name
                if b.dcnt > self.seen[e].get(key, 0):
                    waits.append((b.dsem, b.dcnt))
                    self.seen[e][key] = b.dcnt
            for (s_, v_) in waits:
                self.eobj[e].wait_ge(s_, v_)

    def emit(self):
        pass


C_ID = 0
C_TRI = 128
C_NEG = 256
C_TRI64 = 384
C_NEG64 = 512
C_SEG64 = 640
C_SEGI = 768
CST_W = 784

P_BMOD = 0
P_GAIN = 64
P_CONV = 88
P_SSDFM = 128
P_S5P = 136
P_S5M = 184
P_SSD8 = 192
PRM_W = 194


def _consts():
    c = np.zeros((128, CST_W), np.float32)
    c[:, C_ID:C_ID + 128] = np.eye(128, dtype=np.float32)
    s = np.arange(128)[:, None]
    l = np.arange(128)[None, :]
    c[:, C_TRI:C_TRI + 128] = (s <= l).astype(np.float32)
    c[:, C_NEG:C_NEG + 128] = np.where(l >= s, 0.0, -30000.0)
    same = (s // LS == l // LS) & (s < 64) & (l < 64)
    c[:, C_TRI64:C_TRI64 + 128] = ((s <= l) & same).astype(np.float32)
    c[:, C_NEG64:C_NEG64 + 128] = np.where((l >= s) & same, 0.0, -30000.0)
    c[:, C_SEG64:C_SEG64 + 128] = same.astype(np.float32)
    j = np.arange(16)[None, :]
    c[:, C_SEGI:C_SEGI + 16] = ((s // LS == j) & (s < 64)).astype(np.float32)
    return c


def _fm(v, nt):
    return np.ascontiguousarray(np.asarray(v, np.float32).reshape(nt, 128).T)


def _params(inp):
    p = np.zeros((128, PRM_W), np.float32)
    p[:, P_BMOD:P_BMOD + 48] = _fm(inp["b_ada"][0], 48)
    p[:, P_BMOD + 48:P_BMOD + 64] = _fm(inp["b_ada_f"], 16)
    p[:, P_GAIN:P_GAIN + 8] = _fm(inp["norm1_g"][0], 8)
    p[:, P_GAIN + 8:P_GAIN + 16] = _fm(inp["norm2_g"][0], 8)
    p[:, P_GAIN + 16:P_GAIN + 24] = _fm(inp["normf_g"], 8)
    cw = inp["conv_w"][0]
    cv = np.zeros((128, 8, 5), np.float32)
    for k in range(4):
        cv[:, :, k] = _fm(cw[k], 8)
    cv[:, :, 4] = _fm(inp["conv_b"][0], 8)
    p[:, P_CONV:P_CONV + 40] = cv.reshape(128, 40)
    Dh = inp["ssd_D"][0]
    dfm = np.zeros((128, 4), np.float32)
    for pr in range(4):
        dfm[0:64, pr] = Dh[2 * pr]
        dfm[64:128, pr] = Dh[2 * pr + 1]
    p[:, P_SSDFM:P_SSDFM + 4] = dfm
    p[:, P_SSDFM + 4:P_SSDFM + 8] = _fm(inp["ssd_norm_g"][0], 4)

    def st(a):
        return np.ascontiguousarray(np.asarray(a, np.float32).reshape(16, 128).T)
    p[:, P_S5P:P_S5P + 16] = st(inp["s5_A_re"][0])
    p[:, P_S5P + 16:P_S5P + 32] = st(inp["s5_A_im"][0])
    p[:, P_S5P + 32:P_S5P + 48] = st(np.repeat(inp["s5_log_step"][0][:, None], 64, axis=1))
    p[:, P_S5M:P_S5M + 4] = _fm(inp["s5_D"][0], 4)
    p[:, P_S5M + 4:P_S5M + 8] = _fm(inp["b_glu"][0], 4)
    p[0:8, P_SSD8] = inp["ssd_dt_bias"][0]
    p[0:8, P_SSD8 + 1] = inp["ssd_A_log"][0]
    return p


def _s5mats(inp):
    Br, Bi = inp["s5_B_re"][0], inp["s5_B_im"][0]
    Cr, Ci = inp["s5_C_re"][0], inp["s5_C_im"][0]
    BT = np.zeros((128, 2, 16, 128), np.float32)
    CT = np.zeros((128, 2, 16, 32), np.float32)
    for s in range(16):
        for gl in range(2):
            g = 2 * s + gl
            r0 = (g % 8) * 16
            BT[r0:r0 + 16, 0, s, gl * 64:(gl + 1) * 64] = Br[g].T
            BT[r0:r0 + 16, 1, s, gl * 64:(gl + 1) * 64] = Bi[g].T
            CT[gl * 64:(gl + 1) * 64, 0, s, gl * 16:(gl + 1) * 16] = Cr[g].T
            CT[gl * 64:(gl + 1) * 64, 1, s, gl * 16:(gl + 1) * 16] = Ci[g].T
    return BT, CT


class Arena:
    def __init__(self, nc, es, words):
        self.t = es.enter_context(nc.sbuf_tensor("arena", [128, words], F32))
        self.words = words
        self.lo = 0
        self.hi = words

    def alloc(self, name, shape, dt, top=False):
        n = 1
        for d in shape[1:]:
            n *= d
        w = n if dt == F32 or dt == I32 else (n + 1) // 2
        w = (w + 3) // 4 * 4
        if top:
            self.hi -= w
            off = self.hi
        else:
            off = self.lo
            self.lo += w
        assert self.lo <= self.hi, "arena overflow at %s: lo=%d hi=%d" % (name, self.lo, self.hi)
        ap = self.t[:, off:off + w]
        if dt != F32:
            ap = ap.bitcast(dt)
        ap = ap[:, 0:n]
        if len(shape) == 3:
            ap = ap.rearrange("p (a b) -> p a b", b=shape[2])
        elif len(shape) == 4:
            ap = ap.rearrange("p (a b c) -> p a b c", b=shape[2], c=shape[3])
        if shape[0] < 128:
            ap = ap[0:shape[0]]
        return TL(ap, name)


class StopBuild(Exception):
    pass


def build(dbg=None, stop_after=None):
    nc = bass.Bass("TRN2", target_bir_lowering=False)

    SH = []

    def ckpt(name):
        if stop_after == name:
            SH[0].barrier()
            SH[0].dead = True
    dt_in = lambda name, shape: nc.dram_tensor(name, list(shape), F32, kind="ExternalInput").ap()
    dt_out = lambda name, shape: nc.dram_tensor(name, list(shape), F32, kind="ExternalOutput").ap()
    xin = dt_in("xin", [NTOK, D])
    cin = dt_in("cin", [17, D])
    wada = dt_in("wada", [D, 6144])
    wadaf = dt_in("wadaf", [D, 2048])
    win = dt_in("win", [D, INP])
    wglu = dt_in("wglu", [512, 512])
    wout = dt_in("wout", [D, D])
    wg = dt_in("wg", [D, DFF])
    wu = dt_in("wu", [D, DFF])
    wd = dt_in("wd", [DFF, D])
    cst_d = dt_in("cst", [128, CST_W])
    prm_d = dt_in("prm", [128, PRM_W])
    s5bt_d = dt_in("s5bt", [128, 2 * 16 * 128])
    s5ct_d = dt_in("s5ct", [128, 2 * 16 * 32])
    stssd_d = dt_in("stssd", [NS, 8, 64, 128])
    stconv_d = dt_in("stconv", [128, 8 * NS * 3])
    sts5_d = dt_in("sts5", [128, 2 * 16 * NS])
    yout = dt_out("yout", [NTOK, D])
    o_ssdp = dt_out("o_ssdp", [128, 512])
    o_ssds = dt_out("o_ssds", [NS, 8, 64, 128])
    o_conv = dt_out("o_conv", [128, 8 * 17 * 3])
    o_s5 = dt_out("o_s5", [128, 2 * 16 * 17])
    mixd = nc.dram_tensor("mixd", [128, 8, NTOK], BF16, kind="Internal").ap()
    dumps = {}

    with ExitStack() as es:
        S = Sched(nc, es)
        NEED_CTN = []
        SH.append(S)
        A = Arena(nc, es, 53200)
        outbufs = []

        def dump(name, ap, shape, reads):
            if dbg is None or name not in dbg:
                return
            d = dt_out("dbg_" + name, shape)
            dumps[name] = shape
            b = Buf("dbg_" + name)
            S.dma("sp" if ap.dtype == F32 else "pool", d, ap, reads=reads, buf=b)
            outbufs.append(b)

        PB = [TL(es.enter_context(nc.psum_tensor("pb%d" % i, [128, 512], F32)), "pb%d" % i) for i in range(8)]
        for pb_ in PB:
            pb_.b.excl = True
            pb_.nosub = True

        def pbf(i):
            return PB[i].t[:].bitcast(BF16)

        cst = A.alloc("cst", [128, CST_W], F32)
        prm = A.alloc("prm", [128, PRM_W], F32)
        identb = A.alloc("identb", [128, 128], BF16)
        onesf = A.alloc("onesf", [128, 128], F32)
        mod = A.alloc("mod", [128, 64, 17], F32)
        amod = A.alloc("amod", [128, 24, 17], F32)
        s5fin = A.alloc("s5fin", [128, 2, 16, 17], F32)
        scT = A.alloc("scT", [128, 8, 17], BF16)
        LO_GLOBAL = A.lo
        win_sb = A.alloc("win_sb", [128, 8, INP], BF16)
        wglu_sb = A.alloc("wglu_sb", [128, 4, 512], BF16)
        s5BT = A.alloc("s5BT", [128, 2, 16, 128], BF16)
        s5CT = A.alloc("s5CT", [128, 2, 16, 32], BF16)
        LO_W = A.lo

        def load_1a_weights():
            for a_ in range(4):
                S.dma("pool", s5BT.t[:].rearrange("p a s c -> p (a s c)")[:, a_ * 1024:(a_ + 1) * 1024],
                      s5bt_d[:, a_ * 1024:(a_ + 1) * 1024], writes=[s5BT.b])
            S.dma("pool", s5CT.t[:].rearrange("p a s c -> p (a s c)"), s5ct_d, writes=[s5CT.b])
            win_v = win.rearrange("(kt p) n -> p kt n", p=128)
            for kh in range(4):
                for ch in range(2):
                    S.dma("pool", win_sb.t[:, 2 * kh:2 * kh + 2, ch * 1028:(ch + 1) * 1028],
                          win_v[:, 2 * kh:2 * kh + 2, ch * 1028:(ch + 1) * 1028], writes=[win_sb.sub(kh)])
            S.dma("pool", wglu_sb.t[:], wglu.rearrange("(kt p) n -> p kt n", p=128), writes=[wglu_sb.b])

        ident = cst.t[:, C_ID:C_ID + 128]
        S.dma("sp", cst.t[:], cst_d, writes=[cst.b])
        S.dma("sp", prm.t[:], prm_d, writes=[prm.b])
        S.op("act", lambda e: e.activation(out=identb.t[:], in_=ident, func=AF.Copy), reads=[cst.b], writes=[identb.b])
        S.op("dve", lambda e: e.memset(onesf.t[:], 1.0), writes=[onesf.b])

        def chunkmod(i):
            return mod.t[:, 8 * i:8 * i + 8, :]

        ssd8 = A.alloc("ssd8", [8, 4], F32)
        S.op("act", lambda e: e.activation(out=ssd8.t[:, 1:2], in_=prm.t[0:8, P_SSD8 + 1:P_SSD8 + 2], func=AF.Exp),
             reads=[prm.b], writes=[ssd8.b])
        S.op("dve", lambda e: e.tensor_scalar(out=ssd8.t[:, 1:2], in0=ssd8.t[:, 1:2], scalar1=-1.0, scalar2=None, op0=ALU.mult),
             reads=[ssd8.b], writes=[ssd8.b])
        S.op("dve", lambda e: e.tensor_copy(out=ssd8.t[:, 0:1], in_=prm.t[0:8, P_SSD8:P_SSD8 + 1]), reads=[prm.b], writes=[ssd8.b])

        Ptab = A.alloc("Ptab", [128, 2, 16, T5], F32)
        Qtab = A.alloc("Qtab", [128, 2, 16, T5], F32)
        s5t = [A.alloc("s5t%d" % i, [128, 512], F32) for i in range(2)]

        def alias(name, ap, buf):
            tl = TL(ap, name)
            tl.b = buf
            return tl
        sw = alias("s5work", s5t[1].t[:, 0:384].rearrange("p (a b) -> p a b", b=16), s5t[1].b)
        tmpA = alias("tmpA", s5t[0].t[:, 0:256].rearrange("p (a b) -> p a b", b=T5 // 2), s5t[0].b)
        tmpB = alias("tmpB", s5t[0].t[:, 256:512].rearrange("p (a b) -> p a b", b=T5 // 2), s5t[0].b)
        mask32 = A.alloc("mask32", [128, 16, T5], BF16)
        s5v = [A.alloc("s5v%d" % i, [128, 512], F32) for i in range(2)]
        qtmp = alias("qtmp", s5v[0].t[:].rearrange("p (s t) -> p s t", t=T5), s5v[0].b)
        mask4 = A.alloc("mask4", [128, 128, LS], BF16)
        s5cr = A.alloc("s5cr", [128, 2, 16], F32)
        W = lambda i: sw.t[:, i, :]
        pv = lambda i: prm.t[:, P_S5P + 16 * i:P_S5P + 16 * (i + 1)]
        swb = [sw.b, prm.b]

        def dv(fn):
            S.op("dve", fn, reads=swb, writes=[sw.b])

        def act(fn):
            S.op("act", fn, reads=swb, writes=[sw.b])
        TT = lambda e, o, a, b, op: e.tensor_tensor(out=o, in0=a, in1=b, op=op)
        def exp_acc(dst, src):
            dv(lambda e: e.tensor_scalar(out=W(22), in0=src, scalar1=1.0 / 16, scalar2=None, op0=ALU.mult))
            dv(lambda e: e.tensor_scalar(out=dst, in0=W(22), scalar1=1.0 / 7, scalar2=1.0, op0=ALU.mult, op1=ALU.add))
            for k in (6, 5, 4, 3, 2, 1):
                dv(lambda e: TT(e, dst, dst, W(22), ALU.mult))
                dv(lambda e, k=k: e.tensor_scalar(out=dst, in0=dst, scalar1=1.0 / k, scalar2=1.0, op0=ALU.mult, op1=ALU.add))
            for _ in range(4):
                dv(lambda e: TT(e, dst, dst, dst, ALU.mult))
        exp_acc(W(0), pv(2))
        dv(lambda e: TT(e, W(1), pv(0), W(0), ALU.mult))
        dv(lambda e: TT(e, W(2), pv(1), W(0), ALU.mult))
        exp_acc(W(3), W(1))

        def range_reduce(dst, src, add):
            ki = A_ki
            dv(lambda e: e.tensor_scalar(out=W(20), in0=src, scalar1=float(add), scalar2=1.0 / (2 * PI), op0=ALU.add, op1=ALU.mult))
            S.op("dve", lambda e: e.tensor_copy(out=ki.t[:], in_=W(20)), reads=swb, writes=[ki.b])
            S.op("dve", lambda e: e.tensor_copy(out=W(21), in_=ki.t[:]), reads=[ki.b], writes=[sw.b])
            dv(lambda e: e.tensor_scalar(out=W(20), in0=src, scalar1=float(add), scalar2=None, op0=ALU.add))
            dv(lambda e: e.scalar_tensor_tensor(out=dst, in0=W(21), scalar=-2 * PI, in1=W(20), op0=ALU.mult, op1=ALU.add))
            dv(lambda e: e.tensor_scalar(out=dst, in0=dst, scalar1=PI, scalar2=-PI, op0=ALU.min, op1=ALU.max))
        A_ki = A.alloc("s5ki", [128, 16], I32)
        range_reduce(W(4), W(2), 0.0)
        range_reduce(W(5), W(2), PI / 2)
        act(lambda e: e.activation(out=W(6), in_=W(4), func=AF.Sin))
        act(lambda e: e.activation(out=W(7), in_=W(5), func=AF.Sin))
        dv(lambda e: TT(e, W(8), W(3), W(7), ALU.mult))
        dv(lambda e: TT(e, W(9), W(3), W(6), ALU.mult))
        dv(lambda e: e.tensor_scalar(out=W(10), in0=W(8), scalar1=-1.0, scalar2=None, op0=ALU.add))
        dv(lambda e: TT(e, W(11), pv(0), pv(0), ALU.mult))
        dv(lambda e: TT(e, W(12), pv(1), pv(1), ALU.mult))
        dv(lambda e: TT(e, W(11), W(11), W(12), ALU.add))
        dv(lambda e: e.reciprocal(out=W(11), in_=W(11)))
        dv(lambda e: TT(e, W(12), W(10), pv(0), ALU.mult))
        dv(lambda e: TT(e, W(13), W(9), pv(1), ALU.mult))
        dv(lambda e: TT(e, W(12), W(12), W(13), ALU.add))
        dv(lambda e: TT(e, W(14), W(12), W(11), ALU.mult))
        dv(lambda e: TT(e, W(12), W(9), pv(0), ALU.mult))
        dv(lambda e: TT(e, W(13), W(10), pv(1), ALU.mult))
        dv(lambda e: TT(e, W(12), W(12), W(13), ALU.subtract))
        dv(lambda e: TT(e, W(15), W(12), W(11), ALU.mult))
        dv(lambda e: TT(e, W(12), W(8), W(8), ALU.mult))
        dv(lambda e: TT(e, W(13), W(9), W(9), ALU.mult))
        dv(lambda e: TT(e, W(12), W(12), W(13), ALU.add))
        dv(lambda e: e.reciprocal(out=W(12), in_=W(12)))
        dv(lambda e: TT(e, W(16), W(8), W(12), ALU.mult))
        dv(lambda e: e.scalar_tensor_tensor(out=W(17), in0=W(9), scalar=-1.0, in1=W(12), op0=ALU.mult, op1=ALU.mult))

        def build_pow(tab, br, bi):
            tb = [tab.b, sw.b, tmpA.b, tmpB.b]
            S.op("dve", lambda e: e.tensor_copy(out=tab.t[:, 0, :, 0], in_=br), reads=tb, writes=[tab.b])
            S.op("dve", lambda e: e.tensor_copy(out=tab.t[:, 1, :, 0], in_=bi), reads=tb, writes=[tab.b])
            n = 1
            while n < T5:
                ar, ai = tab.t[:, 0, :, 0:n], tab.t[:, 1, :, 0:n]
                sr = tab.t[:, 0, :, n - 1:n].to_broadcast([128, 16, n])
                si = tab.t[:, 1, :, n - 1:n].to_broadcast([128, 16, n])
                tA, tB = tmpA.t[:, :, 0:n], tmpB.t[:, :, 0:n]
                orr, oi = tab.t[:, 0, :, n:2 * n], tab.t[:, 1, :, n:2 * n]
                ops = [(tA, ar, sr, ALU.mult), (tB, ai, si, ALU.mult), (orr, tA, tB, ALU.subtract),
                       (tA, ar, si, ALU.mult), (tB, ai, sr, ALU.mult), (oi, tA, tB, ALU.add)]
                for (o, a, b, op) in ops:
                    S.op("dve", lambda e, o=o, a=a, b=b, op=op: TT(e, o, a, b, op), reads=tb, writes=tb[0:1] + tb[2:4])
                n *= 2
        build_pow(Ptab, W(8), W(9))
        build_pow(Qtab, W(16), W(17))
        tq = [Qtab.b, sw.b, tmpA.b, tmpB.b]
        for half in range(2):
            hs = slice(half * (T5 // 2), (half + 1) * (T5 // 2))
            qr, qi = Qtab.t[:, 0, :, hs], Qtab.t[:, 1, :, hs]
            fr = W(14).unsqueeze(2).to_broadcast([128, 16, T5 // 2])
            fi = W(15).unsqueeze(2).to_broadcast([128, 16, T5 // 2])
            ops = [(tmpA.t[:], qr, fr, ALU.mult), (tmpB.t[:], qi, fi, ALU.mult), ("R", tmpA.t[:], tmpB.t[:], ALU.subtract),
                   (tmpA.t[:], qr, fi, ALU.mult), (tmpB.t[:], qi, fr, ALU.mult), (qi, tmpA.t[:], tmpB.t[:], ALU.add)]
            for (o, a, b, op) in ops:
                if isinstance(o, str):
                    o = qtmp.t[:, :, hs]
                S.op("dve", lambda e, o=o, a=a, b=b, op=op: TT(e, o, a, b, op), reads=tq + [qtmp.b], writes=tq + [qtmp.b])
            S.op("dve", lambda e, qr=qr, hs=hs: e.tensor_copy(out=qr, in_=qtmp.t[:, :, hs]), reads=[qtmp.b], writes=[Qtab.b])
        S.op("dve", lambda e: e.memset(mask32.t[:], 1.0), reads=[Qtab.b], writes=[mask32.b])
        S.op("dve", lambda e: e.memset(mask32.t[:, :, 0:1], 0.0), writes=[mask32.b])
        S.op("dve", lambda e: e.memset(mask4.t[:], 1.0), writes=[mask4.b])
        S.op("dve", lambda e: e.memset(mask4.t[:, :, 0:1], 0.0), writes=[mask4.b])
        S.op("dve", lambda e: e.memset(s5cr.t[:], 0.0), writes=[s5cr.b])
        dump("Ptab", Ptab.t[:].rearrange("p a s t -> p (a s t)"), [128, 2 * 16 * T5], [Ptab.b])
        dump("Qtab", Qtab.t[:].rearrange("p a s t -> p (a s t)"), [128, 2 * 16 * T5], [Qtab.b])

        LO_W = A.lo
        cs = A.alloc("cs", [17, D], F32)
        slabs = [A.alloc("adaslab%d" % i, [128, 8, 512], BF16) for i in range(3)]
        S.dma("sp", cs.t[:], cin, writes=[cs.b])
        S.op("act", lambda e: e.activation(out=cs.t[:], in_=cs.t[:], func=AF.Silu), reads=[cs.b], writes=[cs.b])
        for kt in range(8):
            S.op("pe", lambda e, kt=kt: e.transpose(PB[2].t[:, kt * 17:(kt + 1) * 17], cs.t[:, kt * 128:(kt + 1) * 128],
                                                    cst.t[0:17, C_ID:C_ID + 17]),
                 reads=[cs.b, cst.b], writes=[PB[2].b])
        S.op("act", lambda e: e.activation(out=scT.t[:].rearrange("p k s -> p (k s)"), in_=PB[2].t[:, 0:136], func=AF.Copy),
             reads=[PB[2].b], writes=[scT.b])
        wada_v = wada.rearrange("(kt p) n -> p kt n", p=128)
        wadaf_v = wadaf.rearrange("(kt p) n -> p kt n", p=128)

        def slab_src(i):
            if i < 12:
                return wada_v[:, :, i * 512:(i + 1) * 512]
            return wadaf_v[:, :, (i - 12) * 512:(i - 11) * 512]

        def load_slab(i):
            sl = slabs[i % 3]
            for kh in range(2):
                S.dma("pool", sl.t[:, 4 * kh:4 * kh + 4, :], slab_src(i)[:, 4 * kh:4 * kh + 4, :], writes=[sl.b])
        load_slab(0)
        load_slab(1)
        load_1a_weights()
        for i in range(4):
            if i + 2 < 4:
                load_slab(i + 2)
            sl = slabs[i % 3]
            pb = PB[i % 2]
            for fc in range(4):
                for kt in range(8):
                    S.op("pe", lambda e, fc=fc, kt=kt, sl=sl, pb=pb: e.matmul(
                        pb.t[:, fc * 17:(fc + 1) * 17], sl.t[:, kt, fc * 128:(fc + 1) * 128], scT.t[:, kt, :],
                        start=(kt == 0), stop=(kt == 7)), reads=[sl.b, scT.b], writes=[pb.b])
            S.op("dve", lambda e, i=i, pb=pb: e.tensor_tensor(
                out=mod.t[:, 4 * i:4 * i + 4, :], in0=pb.t[:, 0:68].rearrange("p (c s) -> p c s", s=17),
                in1=prm.t[:, P_BMOD + 4 * i:P_BMOD + 4 * i + 4].unsqueeze(2).to_broadcast([128, 4, 17]), op=ALU.add),
                reads=[pb.b, prm.b], writes=[mod.b])
        def make_amod(lst):
          for k, (sci, gi) in lst:
            S.op("dve", lambda e, k=k, sci=sci, gi=gi: e.scalar_tensor_tensor(
                out=amod.t[:, 8 * k:8 * k + 8, :], in0=chunkmod(sci), scalar=1.0,
                in1=prm.t[:, P_GAIN + 8 * gi:P_GAIN + 8 * gi + 8].unsqueeze(2).to_broadcast([128, 8, 17]),
                op0=ALU.add, op1=ALU.mult), reads=[mod.b, prm.b], writes=[amod.b])
        make_amod([(0, (1, 0))])
        dump("mod", mod.t[:].rearrange("p c s -> p (c s)"), [128, 64 * 17], [mod.b])
        S.barrier()
        S.emit()
        A.lo = LO_W

        MOD_SH1, MOD_G1, MOD_SH2, MOD_G2, MOD_SHF = 0, 2, 3, 5, 6

        def expand_mod(name, src_ap, srcbufs):
            t = A.alloc(name, [128, 8, 64], F32)
            S.op("dve", lambda e: e.tensor_copy(out=t.t[:].rearrange("p k (s b) -> p k s b", b=LS),
                                                in_=src_ap.unsqueeze(3).to_broadcast([128, 8, NS, LS])),
                 reads=srcbufs, writes=[t.b])
            return t

        LO_P1 = A.lo
        mixt = [A.alloc("mixt%d" % i, [128, 8, 256], BF16) for i in range(2)]
        mixdb = [Buf("mixd%d" % i) for i in range(9)]
        a1x = A.alloc("a1x", [128, 8, 64], F32)
        sh1x = A.alloc("sh1x", [128, 8, 64], F32)

        def fill_x(t, src_ap, srcbufs):
            S.op("dve", lambda e: e.tensor_copy(out=t.t[:].rearrange("p k (s b) -> p k s b", b=LS),
                                                in_=src_ap.unsqueeze(3).to_broadcast([128, 8, NS, LS])),
                 reads=srcbufs, writes=[t.b])
        adab = [TL(a1x.t[:].rearrange("p k t -> p (k t)").bitcast(BF16).rearrange("p (k c) -> p k c", c=128), "adab0"),
                TL(sh1x.t[:].rearrange("p k t -> p (k t)").bitcast(BF16).rearrange("p (k c) -> p k c", c=128), "adab1")]
        adab[0].b = a1x.b
        adab[1].b = sh1x.b
        ADA_CH = list(range(16, 64))

        def ada_load(ci):
            c = ADA_CH[ci]
            src = wada_v[:, :, c * 128:(c + 1) * 128] if c < 48 else wadaf_v[:, :, (c - 48) * 128:(c - 47) * 128]
            S.dma("pool", adab[ci % 2].t[:], src, writes=[adab[ci % 2].b])

        def ada_compute(ci):
            c = ADA_CH[ci]
            sl = adab[ci % 2]
            pb = next_pb()
            for kt in range(8):
                S.op("pe", lambda e, kt=kt: e.matmul(pb.t[:, 0:17], sl.t[:, kt, :], scT.t[:, kt, :], start=(kt == 0), stop=(kt == 7)),
                     reads=[sl.b, scT.b], writes=[pb.b])
            S.op("dve", lambda e: e.tensor_scalar(out=mod.t[:, c, :], in0=pb.t[:, 0:17], scalar1=prm.t[:, P_BMOD + c:P_BMOD + c + 1],
                                                  scalar2=None, op0=ALU.add), reads=[pb.b, prm.b], writes=[mod.b])
        ada_state = [0, 0]

        def ada_step():
            if ada_state[1] >= len(ADA_CH):
                return
            while ada_state[0] < min(len(ADA_CH), ada_state[1] + 2):
                ada_load(ada_state[0])
                ada_state[0] += 1
            ada_compute(ada_state[1])
            ada_state[1] += 1

        ckpt("setup0")
        NTM = 256
        xtm = A.alloc("xtm", [128, 2, D], F32)
        xn = A.alloc("xn", [128, 2, D], BF16)
        nstat = A.alloc("nstat", [128, 4], F32)
        uT = A.alloc("uT", [128, 8, NTM], BF16)
        xpad = A.alloc("xpad", [128, 8, NTM + 4], BF16)
        xtail = A.alloc("xtail", [128, 8, 64], F32)
        cvst = A.alloc("cvst", [128, 8, NS, 3], F32)
        S.dma("sp", cvst.t[:].rearrange("p c s k -> p (c s k)"), stconv_d, writes=[cvst.b])
        dgc = A.alloc("dgc", [128, 8, 4, 128], BF16)
        for ct_ in range(8):
            for k_ in range(4):
                S.op("act", lambda e, ct_=ct_, k_=k_: e.activation(
                    out=dgc.t[:, ct_, k_, :], in_=ident, func=AF.Copy,
                    scale=prm.t[:, P_CONV + 5 * ct_ + k_:P_CONV + 5 * ct_ + k_ + 1]), reads=[cst.b, prm.b], writes=[dgc.b])
        xsT = A.alloc("xsT", [128, 4, NTM], F32)
        BCT = A.alloc("BCT", [128, 4, NTM], BF16)
        szT = A.alloc("szT", [128, 4, NTM], BF16)
        u5Ts = [A.alloc("u5T%d" % i, [128, 4, NTM], BF16) for i in range(2)]
        dtT = A.alloc("dtT", [8, 2, NTM], F32)
        cacc = [A.alloc("cacc0", [128, NTM], F32)] * 2
        y5pre = A.alloc("y5pre", [128, 4, NTM], F32)
        g5 = A.alloc("g5", [128, 4, NTM], BF16)
        sgl = A.alloc("sgl", [128, NTM], F32)
        dtm_l = [A.alloc("dtm%d" % i, [128, 16], F32) for i in range(2)]
        acs_l = [A.alloc("acs%d" % i, [128, 8], F32) for i in range(2)]
        dec_l = [A.alloc("dec%d" % i, [128, 8], F32) for i in range(2)]
        dtdec_l = [A.alloc("dtdec%d" % i, [128, 8], F32) for i in range(2)]
        Xtm = A.alloc("Xtm", [128, 8, 64], BF16)
        Xdec = A.alloc("Xdec", [128, 8, 64], BF16)
        Btm = A.alloc("Btm", [128, 2, 128], BF16)
        big1 = A.alloc("big1", [128, 8, 128], F32)
        big2 = A.alloc("big2", [128, 8, 128], F32)
        MT = A.alloc("MT", [128, 8, 128], BF16)
        eA = A.alloc("eA", [128, 8, 128], F32)
        CdT = A.alloc("CdT", [128, 8, 128], BF16)
        ST = A.alloc("ST", [128, 8, 64], F32)
        STb = A.alloc("STb", [128, 8, 64], BF16)
        sts5 = alias("sts5", ST.t[:].rearrange("p h q -> p (h q)").rearrange("p (a s q) -> p a s q", a=2, s=16), ST.b)
        yg = A.alloc("yg", [128, 4, 128], F32)
        ysq = alias("ysq", big1.t[:, 4:8, :], big1.b)
        rsb = A.alloc("rsb", [128, 2, 128], F32)
        ysqb = A.alloc("ysqb", [128, 4, 128], BF16)
        onesb1 = A.alloc("onesb1", [128, 128], BF16)
        S.op("dve", lambda e: e.memset(onesb1.t[:], 1.0), writes=[onesb1.b])
        h0n = [alias("h0n0", xtm.t[:, 1, 0:512].rearrange("p (a n) -> p a n", n=128), xtm.sub(1)),
               alias("h0n1", xtm.t[:, 0, 0:512].rearrange("p (a n) -> p a n", n=128), xtm.sub(0))]
        h0T = [A.alloc("h0T%d" % i, [128, 8, 64], BF16) for i in range(2)]
        Bj = [A.alloc("Bj%d" % i, [128, 2, 128], BF16) for i in range(2)]
        hn = [alias("hn0", xtm.t[:, 1, 512:1024].rearrange("p (a n) -> p a n", n=128), xtm.sub(1)),
              alias("hn1", xtm.t[:, 0, 512:1024].rearrange("p (a n) -> p a n", n=128), xtm.sub(0))]
        decfm = A.alloc("decfm", [128, 4, 16], F32)
        dAx = alias("dAx", big1.t[:, 0:4, :].rearrange("p a (b c) -> p (a b) c", c=64), big1.b)
        s5g = [[A.alloc("s5g%d%d" % (j, i), [128, 512], F32) for i in range(2)] for j in range(2)]
        s5t34 = [A.alloc("s5t%d" % i, [128, 512], F32) for i in (2, 3)]
        s5vb = [A.alloc("s5vb%d" % i, [128, 512], F32) for i in range(2)]
        s5k = [0]
        s5h = [[A.alloc("s5h%d%d" % (j, i), [128, 512], BF16) for i in range(4)] for j in range(2)]
        s5CTn = A.alloc("s5CTn", [128, 16, 32], BF16)
        s5c = A.alloc("s5c", [128, 4, 16], F32)
        busd = [[A.alloc("bus%d%d" % (j, i), [128, 512], F32) for i in range(2)] for j in range(2)]
        dg5 = A.alloc("dg5", [128, 4, 128], BF16)
        for q_ in range(4):
            S.op("act", lambda e, q_=q_: e.activation(out=dg5.t[:, q_, :], in_=ident, func=AF.Copy,
                                                      scale=prm.t[:, P_S5M + q_:P_S5M + q_ + 1]),
                 reads=[cst.b, prm.b], writes=[dg5.b])
        S.op("dve", lambda e: e.tensor_scalar(out=s5CT.t[:, 1], in0=s5CT.t[:, 1], scalar1=-1.0, scalar2=None, op0=ALU.mult),
             reads=[s5CT.b], writes=[s5CT.b])
        S.op("dve", lambda e: e.tensor_scalar(out=s5CTn.t[:], in0=s5CT.t[:, 0], scalar1=-1.0, scalar2=None, op0=ALU.mult),
             reads=[s5CT.b], writes=[s5CTn.b])
        print("arena after p1a allocs: lo=%d hi=%d (words)" % (A.lo, A.hi))

        S.op("dve", lambda e: e.memset(xpad.t[:, :, 0:3], 0.0), writes=[xpad.b])
        S.op("dve", lambda e: e.memset(ST.t[:], 0.0), writes=[ST.b])
        S.op("dve", lambda e: e.memset(STb.t[:], 0.0), writes=[STb.b])

        import os as _os3
        ENG_OUTROT = _os3.environ.get("K_OUTROT", "dve")
        ENG_ADDS = _os3.environ.get("K_ADDS", "dve")
        TILES_A = [(i * 256, 256, False) for i in range(8)] + [(SEQ, 64, True)]

        def load_x(ti):
            t0, NT, is_s = TILES_A[ti]
            for blk in range((NT + 127) // 128):
                rows = min(128, NT - blk * 128)
                S.dma("sp", xtm.t[0:rows, blk, :], xin[t0 + blk * 128:t0 + blk * 128 + rows, :], writes=[xtm.sub(blk)])

        a1 = lambda kt: amod.t[:, kt, 0:1]
        sh1 = lambda kt: mod.t[:, 8 * MOD_SH1 + kt, 0:1]
        cw = lambda ct, k: prm.t[:, P_CONV + 5 * ct + k:P_CONV + 5 * ct + k + 1]
        IN_CHUNKS = [("dt", 0, 1536, 8)] + [("z", i, i * 128, 128) for i in range(4)] + \
                    [("xbc", i, 512 + i * 128, 128) for i in range(8)] + [("u5", i, 1544 + i * 128, 128) for i in range(4)]

        load_x(0)
        pbi = [0]

        def next_pb():
            pbi[0] ^= 1
            return PB[pbi[0]]

        ckpt("pre")
        def chain1(ti):
            t0, NT, is_s = TILES_A[ti]
            u5T = u5Ts[ti % 2]
            nblk = (NT + 127) // 128
            T = 128 if not is_s else 64
            tri = cst.t[0:T, C_TRI:C_TRI + T] if not is_s else cst.t[0:T, C_TRI64:C_TRI64 + T]
            neg = cst.t[0:T, C_NEG:C_NEG + T] if not is_s else cst.t[0:T, C_NEG64:C_NEG64 + T]
            sego = onesf.t[0:T, 0:T] if not is_s else cst.t[0:T, C_SEG64:C_SEG64 + T]
            segi = cst.t[0:64, C_SEGI:C_SEGI + 16]

            def dt_prep(ck):
                c0 = ck * T
                cs_ = slice(c0, c0 + T)
                dtm, acs, dec, dtdec = dtm_l[ck], acs_l[ck], dec_l[ck], dtdec_l[ck]
                pc = 0 if ck == 0 else 480
                S.op("pe", lambda e: e.transpose(PB[4].t[0:T, pc:pc + 8], dtT.t[:, 0, cs_], cst.t[0:8, C_ID:C_ID + 8]),
                     reads=[dtT.b, cst.b], writes=[PB[4].sub("sm")])
                S.op("pe", lambda e: e.transpose(PB[4].t[0:T, pc + 8:pc + 16], dtT.t[:, 1, cs_], cst.t[0:8, C_ID:C_ID + 8]),
                     reads=[dtT.b, cst.b], writes=[PB[4].sub("sm")])
                S.op("act", lambda e: e.activation(out=dtm.t[0:T, :], in_=PB[4].t[0:T, pc:pc + 16], func=AF.Copy),
                     reads=[PB[4].sub("sm")], writes=[dtm.b])
                S.op("pe", lambda e: e.matmul(PB[4].t[0:T, pc + 16:pc + 24], tri, dtm.t[0:T, 8:16], start=True, stop=True),
                     reads=[dtm.b, cst.b], writes=[PB[4].sub("sm")])
                S.op("pe", lambda e: e.matmul(PB[4].t[0:T, pc + 24:pc + 32], sego, dtm.t[0:T, 8:16], start=True, stop=True),
                     reads=[dtm.b, cst.b, onesf.b], writes=[PB[4].sub("sm")])
                S.op("act", lambda e: e.activation(out=acs.t[0:T, :], in_=PB[4].t[0:T, pc + 16:pc + 24], func=AF.Copy),
                     reads=[PB[4].sub("sm")], writes=[acs.b])
                S.op("dve", lambda e: TT(e, dec.t[0:T, :], PB[4].t[0:T, pc + 24:pc + 32], acs.t[0:T, :], ALU.subtract),
                     reads=[PB[4].sub("sm"), acs.b], writes=[dec.b])
                S.op("act", lambda e: e.activation(out=dec.t[0:T, :], in_=dec.t[0:T, :], func=AF.Exp), reads=[dec.b], writes=[dec.b])
                S.op("dve", lambda e: TT(e, dtdec.t[0:T, :], dtm.t[0:T, 0:8], dec.t[0:T, :], ALU.mult),
                     reads=[dtm.b, dec.b], writes=[dtdec.b])
            for blk in range(nblk):
                rows = min(128, NT - blk * 128)
                xb = xtm.sub(blk)
                S.op("act", lambda e, blk=blk, rows=rows: e.activation(
                    out=xn.t[0:rows, blk, :], in_=xtm.t[0:rows, blk, :], func=AF.Square, accum_out=nstat.t[0:rows, blk:blk + 1]),
                    reads=[xb], writes=[xn.sub(blk), nstat.sub(blk)])
                S.op("act", lambda e, blk=blk, rows=rows: e.activation(
                    out=nstat.t[0:rows, 2 + blk:3 + blk], in_=nstat.t[0:rows, blk:blk + 1], func=AF.Ln, scale=1.0 / D, bias=EPS),
                    reads=[nstat.sub(blk)], writes=[nstat.sub(blk)])
                S.op("act", lambda e, blk=blk, rows=rows: e.activation(out=nstat.t[0:rows, 2 + blk:3 + blk],
                                                                        in_=nstat.t[0:rows, 2 + blk:3 + blk], func=AF.Exp, scale=-0.5),
                     reads=[nstat.sub(blk)], writes=[nstat.sub(blk)])
                S.op("act", lambda e, blk=blk, rows=rows: e.activation(
                    out=xn.t[0:rows, blk, :], in_=xtm.t[0:rows, blk, :], func=AF.Copy, scale=nstat.t[0:rows, 2 + blk:3 + blk]),
                    reads=[xb, nstat.sub(blk)], writes=[xn.sub(blk)])
            ckpt("Aa%d" % ti)
            if ti + 1 < len(TILES_A):
                load_x(ti + 1)
            ckpt("Ab%d" % ti)
            for kt in range(8):
                xb_ = 2 + (kt % 2)
                pslot = PB[xb_].b
                for blk in range(nblk):
                    rows = min(128, NT - blk * 128)
                    S.op("pe", lambda e, kt=kt, blk=blk, rows=rows: e.transpose(
                        pbf(xb_)[:, blk * 128:blk * 128 + rows],
                        xn.t[0:rows, blk, kt * 128:(kt + 1) * 128], identb.t[0:rows, 0:rows]),
                        reads=[xn.sub(blk), identb.b], writes=[pslot])
                src = pbf(xb_)[:, 0:NT]
                if not is_s:
                    S.op("act", lambda e, kt=kt, src=src: e.activation(out=uT.t[:, kt, 0:NT], in_=src, func=AF.Identity,
                                                                       scale=a1(kt), bias=sh1(kt)),
                         reads=[pslot, amod.b, mod.b], writes=[uT.sub(kt)])
                else:
                    S.op("dve", lambda e, kt=kt, src=src: TT(e, cacc[0].t[:, 0:NT], src, a1x.t[:, kt, :], ALU.mult),
                         reads=[pslot, a1x.b], writes=[cacc[0].b])
                    S.op("dve", lambda e, kt=kt: TT(e, uT.t[:, kt, 0:NT], cacc[0].t[:, 0:NT], sh1x.t[:, kt, :], ALU.add),
                         reads=[cacc[0].b, sh1x.b], writes=[uT.sub(kt)])
            ckpt("A%d" % ti)
            if ti == 0:
                dump("uT", uT.t[:].rearrange("p k t -> p (k t)"), [128, 8 * NTM], uT.allb())

            yield
            if is_s:
                xps = xpad.t[:, :, 0:NS * 7].rearrange("p c (s k) -> p c s k", k=7)
                S.op("act", lambda e: e.activation(out=xps[:, :, :, 0:3], in_=cvst.t[:], func=AF.Copy), reads=[cvst.b], writes=[xpad.b])
            for (kind, i, c0, M) in IN_CHUNKS:
                yield
                pb = next_pb()
                for kt in range(8):
                    S.op("pe", lambda e, kt=kt, c0=c0, M=M, pb=pb: e.matmul(
                        pb.t[0:M, 0:NT], win_sb.t[:, kt, c0:c0 + M], uT.t[:, kt, 0:NT], start=(kt == 0), stop=(kt == 7)),
                        reads=[win_sb.sub(kt // 2), uT.sub(kt)], writes=[pb.b])
                if kind == "z":
                    S.op("act", lambda e, i=i, pb=pb: e.activation(out=szT.t[:, i, 0:NT], in_=pb.t[:, 0:NT], func=AF.Silu),
                         reads=[pb.b], writes=[szT.b])
                elif kind == "xbc":
                    if not is_s:
                        S.op("act", lambda e, i=i, pb=pb: e.activation(out=xpad.t[:, i, 3:3 + NT], in_=pb.t[:, 0:NT], func=AF.Copy),
                             reads=[pb.b], writes=[xpad.b])
                        if ti == 7:
                            S.op("act", lambda e, i=i, pb=pb: e.activation(out=xtail.t[:, i, 0:3], in_=pb.t[:, NT - 3:NT], func=AF.Copy),
                                 reads=[pb.b], writes=[xtail.b])
                    else:
                        S.op("act", lambda e, i=i, pb=pb: e.activation(
                            out=xps[:, i, :, 3:7], in_=pb.t[:, 0:NT].rearrange("p (s k) -> p s k", k=LS), func=AF.Copy),
                            reads=[pb.b], writes=[xpad.b])
                        S.op("act", lambda e, i=i, pb=pb: e.activation(out=xtail.t[:, i, 0:NT], in_=pb.t[:, 0:NT], func=AF.Copy),
                             reads=[pb.b], writes=[xtail.b])
                elif kind == "dt":
                    S.op("act", lambda e, pb=pb: e.activation(out=dtT.t[:, 1, 0:NT], in_=pb.t[0:8, 0:NT], func=AF.Exp,
                                                              bias=ssd8.t[:, 0:1]), reads=[pb.b, ssd8.b], writes=[dtT.b])
                    S.op("act", lambda e: e.activation(out=dtT.t[:, 0, 0:NT], in_=dtT.t[:, 1, 0:NT], func=AF.Ln, bias=1.0),
                         reads=[dtT.b], writes=[dtT.b])
                    S.op("dve", lambda e: e.tensor_scalar(out=dtT.t[:, 1, 0:NT], in0=dtT.t[:, 0, 0:NT], scalar1=ssd8.t[:, 1:2],
                                                          scalar2=None, op0=ALU.mult), reads=[dtT.b, ssd8.b], writes=[dtT.b])
                    for ck_ in range(NT // T):
                        yield
                        dt_prep(ck_)
                else:
                    S.op("act", lambda e, i=i, pb=pb: e.activation(out=u5T.t[:, i, 0:NT], in_=pb.t[:, 0:NT], func=AF.Copy),
                         reads=[pb.b], writes=[u5T.b])

            ckpt("B%d" % ti)
            for ct in range(8):
                yield
                pb = next_pb()
                if not is_s:
                    xin_k = lambda k, ct=ct: xpad.t[:, ct, k:k + NT]
                    pbv = pb.t[:, 0:NT]
                    dst = xsT.t[:, ct, 0:NT] if ct < 4 else BCT.t[:, ct - 4, 0:NT]
                else:
                    xin_k = lambda k, ct=ct: xps[:, ct, :, k:k + LS]
                    pbv = pb.t[:, 0:NT].rearrange("p (s k) -> p s k", k=LS)
                    dst = (xsT.t[:, ct, 0:NT] if ct < 4 else BCT.t[:, ct - 4, 0:NT]).rearrange("p (s k) -> p s k", k=LS)
                for k in range(4):
                    S.op("pe", lambda e, k=k: e.matmul(pbv, dgc.t[:, ct, k, :], xin_k(k), start=(k == 0), stop=(k == 3)),
                         reads=[dgc.b, xpad.b], writes=[pb.b])
                S.op("act", lambda e: e.activation(out=dst, in_=pbv, func=AF.Silu, bias=cw(ct, 4)),
                     reads=[pb.b, prm.b], writes=[xsT.b if ct < 4 else BCT.b])
            ocv = o_conv.rearrange("p (c s k) -> p c s k", s=17, k=3)
            if is_s:
                S.op("act", lambda e: e.activation(out=cvst.t[:], in_=xtail.t[:].rearrange("p c (s k) -> p c s k", k=LS)[:, :, :, 1:4],
                                                   func=AF.Copy), reads=[xtail.b], writes=[cvst.b])
                S.dma("sp", ocv[:, :, 1:17, :], cvst.t[:], reads=[cvst.b], buf=cvst.b)
                outbufs.append(cvst.b)
            elif ti == 7:
                S.dma("sp", ocv[:, :, 0, :], xtail.t[:, :, 0:3], reads=[xtail.b], buf=xtail.b)
            if not is_s:
                S.op("dve", lambda e: e.tensor_copy(out=xpad.t[:, :, 0:3], in_=xpad.t[:, :, NT:NT + 3]),
                     reads=[xpad.b], writes=[xpad.b])
            if is_s:
                dump("xsS", xsT.t[:, :, 0:64], [128, 4, 64], [xsT.b])
                dump("ygS", yg.t[:, :, 0:64], [128, 4, 64], [yg.b])
            if ti == 0:
                dump("xsT", xsT.t[:].rearrange("p k t -> p (k t)"), [128, 4 * NTM], [xsT.b])
                dump("dtT", dtT.t[:].rearrange("p k t -> p (k t)"), [8, 2 * NTM], [dtT.b])

            ckpt("C%d" % ti)
            for ck in range(NT // T):
                c0 = ck * T
                cs_ = slice(c0, c0 + T)
                dtm, acs, dec, dtdec = dtm_l[ck], acs_l[ck], dec_l[ck], dtdec_l[ck]
                yield
                for pr in range(4):
                    S.op("pe", lambda e, pr=pr, cs_=cs_: e.transpose(PB[3].t[0:T, pr * 128:(pr + 1) * 128], xsT.t[:, pr, cs_], ident),
                         reads=[xsT.b, cst.b], writes=[PB[3].b])
                pxs = PB[3].t[0:T, :].rearrange("p (h q) -> p h q", q=64)
                for h in range(8):
                    S.op("act", lambda e, h=h: e.activation(out=Xtm.t[0:T, h, :], in_=pxs[:, h, :], func=AF.Copy, scale=dtm.t[0:T, h:h + 1]),
                         reads=[PB[3].b, dtm.b], writes=[Xtm.b])
                    S.op("act", lambda e, h=h: e.activation(out=Xdec.t[0:T, h, :], in_=pxs[:, h, :], func=AF.Copy, scale=dtdec.t[0:T, h:h + 1]),
                         reads=[PB[3].b, dtdec.b], writes=[Xdec.b])
                for g in range(2):
                    S.op("pe", lambda e, g=g, cs_=cs_: e.transpose(pbf(2)[0:T, g * 128:(g + 1) * 128], BCT.t[:, g, cs_], identb.t[:]),
                         reads=[BCT.b, identb.b], writes=[PB[2].sub(0)])
                S.op("act", lambda e: e.activation(out=Btm.t[0:T].rearrange("p g n -> p (g n)"), in_=pbf(2)[0:T, 0:256], func=AF.Copy),
                     reads=[PB[2].sub(0)], writes=[Btm.b])
                yield
                S.op("dve", lambda e: TT(e, big1.t[0:T, :, 0:T], tri.unsqueeze(1).to_broadcast([T, 8, T]),
                                         dtm.t[0:T, 8:16].unsqueeze(2).to_broadcast([T, 8, T]), ALU.mult),
                     reads=[cst.b, dtm.b], writes=[big1.b])
                for half in range(2):
                    S.op("pe", lambda e, half=half: e.matmul(
                        PB[3].t[:, 0:4 * T].rearrange("p (h l) -> p h l", l=T), onesf.t[0:T, :],
                        big1.t[0:T, 4 * half:4 * half + 4, 0:T], start=True, stop=True),
                        reads=[big1.b, onesf.b], writes=[PB[3].b])
                    yield
                    for h in range(4 * half, 4 * half + 4):
                        S.op("dve", lambda e, h=h: e.scalar_tensor_tensor(
                            out=big2.t[0:T, h, 0:T], in0=PB[3].t[0:T, (h % 4) * T:(h % 4 + 1) * T], scalar=acs.t[0:T, h:h + 1],
                            in1=neg, op0=ALU.subtract, op1=ALU.min), reads=[PB[3].b, acs.b, cst.b], writes=[big2.b])
                    S.op("act", lambda e, half=half: e.activation(
                        out=eA.t[:, 4 * half:4 * half + 4, 0:T], in_=PB[3].t[:, 0:4 * T].rearrange("p (h l) -> p h l", l=T),
                        func=AF.Exp), reads=[PB[3].b], writes=[eA.b])
                    yield
                S.op("act", lambda e: e.activation(out=big2.t[0:T, :, 0:T], in_=big2.t[0:T, :, 0:T], func=AF.Exp),
                     reads=[big2.b], writes=[big2.b])
                yield
                for g in range(2):
                    S.op("pe", lambda e, g=g, cs_=cs_: e.matmul(PB[4].t[0:T, 32 + g * 128:32 + g * 128 + T], BCT.t[:, g, cs_],
                                                                 BCT.t[:, 2 + g, cs_], start=True, stop=True),
                         reads=[BCT.b], writes=[PB[4].sub("cb")])
                cbv = PB[4].t[0:T, 32:288].rearrange("p (g l) -> p g l", l=128)[:, :, 0:T]
                S.op("dve", lambda e: TT(e, MT.t[0:T, :, 0:T].rearrange("p (g h) l -> p g h l", h=4),
                                         cbv.unsqueeze(2).to_broadcast([T, 2, 4, T]),
                                         big2.t[0:T, :, 0:T].rearrange("p (g h) l -> p g h l", h=4), ALU.mult),
                     reads=[PB[4].sub("cb"), big2.b], writes=[MT.b])
                yield
                S.op("pool", lambda e, cs_=cs_: TT(e, CdT.t[:, :, 0:T].rearrange("p (g h) l -> p g h l", h=4),
                                                   BCT.t[:, 2:4, cs_].unsqueeze(2).to_broadcast([128, 2, 4, T]),
                                                   eA.t[:, :, 0:T].rearrange("p (g h) l -> p g h l", h=4), ALU.mult),
                     reads=[BCT.b, eA.b], writes=[CdT.b])
                yield
                ypb = PB[7]
                if is_s:
                    S.op("dve", lambda e: e.tensor_copy(out=dAx.t[0:T], in_=dtm.t[0:T, 8:16].unsqueeze(2).to_broadcast([T, 8, 64])),
                         reads=[dtm.b], writes=[dAx.b])
                    for pr in range(4):
                        S.op("pe", lambda e, pr=pr: e.matmul(PB[4].t[:, 288 + pr * 16:288 + (pr + 1) * 16],
                                                             dAx.t[0:T, 2 * pr:2 * pr + 2, :], segi, start=True, stop=True),
                             reads=[dAx.b, cst.b], writes=[PB[4].sub("dec")])
                    S.op("act", lambda e: e.activation(out=decfm.t[:].rearrange("p a s -> p (a s)"), in_=PB[4].t[:, 288:352], func=AF.Exp),
                         reads=[PB[4].sub("dec")], writes=[decfm.b])
                    stv = stssd_d.rearrange("j (pr hl) p n -> j (hl p) pr n", hl=2)
                    osv = o_ssds.rearrange("j (pr hl) p n -> j (hl p) pr n", hl=2)
                    S.dma("act", h0n[0].t[:], stv[0], writes=[h0n[0].b])
                    for j in range(NS):
                        yield
                        jj = j % 2
                        if j + 1 < NS:
                            S.dma("act", h0n[1 - jj].t[:], stv[j + 1], writes=[h0n[1 - jj].b])
                        pbt = PB[jj]
                        for pr in range(4):
                            S.op("pe", lambda e, pr=pr, jj=jj, pbt=pbt: e.transpose(pbt.t[:, pr * 128:(pr + 1) * 128], h0n[jj].t[:, pr, :], ident),
                                 reads=[h0n[jj].b, cst.b], writes=[pbt.b])
                        S.op("act", lambda e, jj=jj, pbt=pbt: e.activation(out=h0T[jj].t[:].rearrange("p h q -> p (h q)"), in_=pbt.t[:, :], func=AF.Copy),
                             reads=[pbt.b], writes=[h0T[jj].b])
                        for h in range(8):
                            pr, hl = h // 2, h % 2
                            S.op("pe", lambda e, h=h, pr=pr, hl=hl, jj=jj, j=j: e.matmul(
                                ypb.t[64 * hl:64 * hl + 64, pr * T + LS * j:pr * T + LS * j + LS], h0T[jj].t[:, h, :],
                                CdT.t[:, h, LS * j:LS * j + LS], start=(j == 0 and pr == 0), stop=False, skip_group_check=True),
                                reads=[h0T[jj].b, CdT.b], writes=[ypb.b])
                        S.op("dve", lambda e, jj=jj, j=j: e.tensor_scalar(out=Bj[jj].t[0:T], in0=Btm.t[0:T], scalar1=segi[:, j:j + 1],
                                                                          scalar2=None, op0=ALU.mult),
                             reads=[Btm.b, cst.b], writes=[Bj[jj].b])
                        pby = PB[3]
                        for pr in range(4):
                            S.op("pe", lambda e, pr=pr, jj=jj, pby=pby: e.matmul(
                                pby.t[:, pr * 128:(pr + 1) * 128], Xdec.t[0:T, 2 * pr:2 * pr + 2, :], Bj[jj].t[0:T, pr // 2, :],
                                start=True, stop=True), reads=[Xdec.b, Bj[jj].b], writes=[pby.b])
                        S.op("dve", lambda e, jj=jj, j=j: TT(e, hn[jj].t[:], h0n[jj].t[:],
                                                             decfm.t[:, :, j:j + 1].to_broadcast([128, 4, 128]), ALU.mult),
                             reads=[h0n[jj].b, decfm.b], writes=[hn[jj].b])
                        S.op("dve", lambda e, jj=jj, pby=pby: TT(e, hn[jj].t[:], hn[jj].t[:],
                                                                 pby.t[:, :].rearrange("p (a n) -> p a n", n=128), ALU.add),
                             reads=[hn[jj].b, pby.b], writes=[hn[jj].b])
                        S.dma("sp", osv[j], hn[jj].t[:], reads=[hn[jj].b], buf=hn[jj].b)
                    outbufs.extend([hn[0].b, hn[1].b])
                for h in range(8):
                    pr, hl = h // 2, h % 2
                    out = ypb.t[64 * hl:64 * hl + 64, pr * T:(pr + 1) * T]
                    S.op("pe", lambda e, h=h, out=out, pr=pr: e.matmul(out, Xtm.t[0:T, h, :], MT.t[0:T, h, 0:T],
                                                                       start=(pr == 0 and not is_s), stop=is_s, skip_group_check=True),
                         reads=[Xtm.b, MT.b], writes=[ypb.b])
                    if not is_s:
                        S.op("pe", lambda e, h=h, out=out: e.matmul(out, STb.t[:, h, :], CdT.t[:, h, 0:T], start=False, stop=True,
                                                                    skip_group_check=True),
                             reads=[STb.b, CdT.b], writes=[ypb.b])
                yield
                for pr in range(4):
                    S.op("dve", lambda e, pr=pr, cs_=cs_: e.scalar_tensor_tensor(
                        out=yg.t[:, pr, 0:T], in0=xsT.t[:, pr, cs_], scalar=prm.t[:, P_SSDFM + pr:P_SSDFM + pr + 1],
                        in1=ypb.t[:, pr * T:(pr + 1) * T], op0=ALU.mult, op1=ALU.add),
                        reads=[xsT.b, prm.b, ypb.b], writes=[yg.b])
                S.op("dve", lambda e, cs_=cs_: TT(e, yg.t[:, :, 0:T], yg.t[:, :, 0:T], szT.t[:, :, cs_], ALU.mult),
                     reads=[yg.b, szT.b], writes=[yg.b])
                S.op("dve", lambda e: TT(e, ysqb.t[:, :, 0:T], yg.t[:, :, 0:T], yg.t[:, :, 0:T], ALU.mult),
                     reads=[yg.b], writes=[ysqb.b])
                for g in range(2):
                    for k in range(2):
                        S.op("pe", lambda e, g=g, k=k: e.matmul(PB[3].t[:, g * T:(g + 1) * T], onesb1.t[:], ysqb.t[:, 2 * g + k, 0:T],
                                                                start=(k == 0), stop=(k == 1)),
                             reads=[onesb1.b, ysqb.b], writes=[PB[3].b])
                S.op("act", lambda e: e.activation(out=rsb.t[:, :, 0:T], in_=PB[3].t[:, 0:2 * T].rearrange("p (g l) -> p g l", l=T),
                                                   func=AF.Ln, scale=1.0 / 256, bias=EPS), reads=[PB[3].b], writes=[rsb.b])
                S.op("act", lambda e: e.activation(out=rsb.t[:, :, 0:T], in_=rsb.t[:, :, 0:T], func=AF.Exp, scale=-0.5),
                     reads=[rsb.b], writes=[rsb.b])
                for pr in range(4):
                    S.op("dve", lambda e, pr=pr: e.scalar_tensor_tensor(
                        out=mixt[ti % 2].t[:, pr, c0:c0 + T], in0=yg.t[:, pr, 0:T],
                        scalar=prm.t[:, P_SSDFM + 4 + pr:P_SSDFM + 5 + pr], in1=rsb.t[:, pr // 2, 0:T], op0=ALU.mult, op1=ALU.mult),
                        reads=[yg.b, prm.b, rsb.b], writes=[mixt[ti % 2].sub("ssd")])
                yield
                if not is_s:
                    for g in range(2):
                        S.op("pe", lambda e, g=g: e.matmul(PB[6].t[:, g * 256:(g + 1) * 256], Btm.t[0:T, g, :],
                                                           Xdec.t[0:T, 4 * g:4 * g + 4, :], start=True, stop=True),
                             reads=[Btm.b, Xdec.b], writes=[PB[6].b])
                    S.op("dve", lambda e: TT(e, ST.t[:], ST.t[:], eA.t[:, :, T - 1:T].to_broadcast([128, 8, 64]), ALU.mult),
                         reads=[ST.b, eA.b], writes=[ST.b])
                    S.op("dve", lambda e: TT(e, ST.t[:], ST.t[:], PB[6].t[:, :].rearrange("p (h q) -> p h q", q=64), ALU.add),
                         reads=[ST.b, PB[6].b], writes=[ST.b])
                    S.op("act", lambda e: e.activation(out=STb.t[:], in_=ST.t[:], func=AF.Copy), reads=[ST.b], writes=[STb.b])
            if ti == 7:
                S.dma("sp", o_ssdp, ST.t[:].rearrange("p h q -> p (h q)"), reads=[ST.b], buf=ST.b)
                outbufs.append(ST.b)

            ckpt("D%d" % ti)
            yield

        def chain2(ti):
            t0, NT, is_s = TILES_A[ti]
            u5T = u5Ts[ti % 2]
            if is_s:
                S.dma("sp", sts5.t[:].rearrange("p a s q -> p (a s q)"), sts5_d, writes=[sts5.b])
            if not is_s:
                groups = [(list(range(16)), k * T5, T5) for k in range(NT // T5)]
            else:
                groups = [(list(range(8)), 0, 64), (list(range(8, 16)), 0, 64)]
            def emit_bu(g_):
                slist_, tk0_, ntok_ = groups[g_]
                bus = busd[g_ % 2]
                for part, pb in ((0, PB[5]), (1, PB[6])):
                    for idx, s in enumerate(slist_):
                        S.op("pe", lambda e, part=part, pb=pb, idx=idx, s=s: e.matmul(
                            pb.t[:, idx * ntok_:(idx + 1) * ntok_], s5BT.t[:, part, s, :], u5T.t[:, s // 4, tk0_:tk0_ + ntok_],
                            start=True, stop=True), reads=[s5BT.b, u5T.b], writes=[pb.b])
                S.op("act", lambda e: e.activation(out=bus[0].t[:], in_=PB[5].t[:, :], func=AF.Copy), reads=[PB[5].b], writes=[bus[0].b])
                S.op("act", lambda e: e.activation(out=bus[1].t[:], in_=PB[6].t[:, :], func=AF.Copy), reads=[PB[6].b], writes=[bus[1].b])
            def views(g_):
                slist_, tk0_, ntok_ = groups[g_]
                s0_ = slist_[0]
                if not is_s:
                    V3 = lambda ap: ap.rearrange("p (s t) -> p s t", t=T5)
                    QR, QI = Qtab.t[:, 0], Qtab.t[:, 1]
                    PR_, PI_ = Ptab.t[:, 0], Ptab.t[:, 1]
                    msk = mask32.t[:].rearrange("p s t -> p (s t)")
                    first = lambda ap: V3(ap)[:, :, 0]
                    cin_r, cin_i = s5cr.t[:, 0, :], s5cr.t[:, 1, :]
                else:
                    V3 = lambda ap: ap.rearrange("p (s q b) -> p s q b", q=NS, b=LS)
                    bc = lambda ap: ap.unsqueeze(2).to_broadcast([128, 8, NS, LS])
                    QR, QI = bc(Qtab.t[:, 0, s0_:s0_ + 8, 0:LS]), bc(Qtab.t[:, 1, s0_:s0_ + 8, 0:LS])
                    PR_, PI_ = bc(Ptab.t[:, 0, s0_:s0_ + 8, 0:LS]), bc(Ptab.t[:, 1, s0_:s0_ + 8, 0:LS])
                    msk = mask4.t[:].rearrange("p s t -> p (s t)")
                    first = lambda ap: V3(ap)[:, :, :, 0]
                    cin_r, cin_i = sts5.t[:, 0, s0_:s0_ + 8, :], sts5.t[:, 1, s0_:s0_ + 8, :]
                return V3, QR, QI, PR_, PI_, msk, first, cin_r, cin_i
            vsets = [[s5v[0], s5v[1]], [s5vb[0], s5vb[1]]]

            def mults_adds(g_):
                V3, QR, QI, PR_, PI_, msk, first, cin_r, cin_i = views(g_)
                bus = busd[g_ % 2]
                br, bi = V3(bus[0].t[:]), V3(bus[1].t[:])
                t1, t2, t3, t4 = s5t[0], s5t[1], s5t34[0], s5t34[1]
                vr, vi = vsets[g_ % 2]
                tb = [Qtab.b]
                for (o, a, b_, rd) in ((t1, QR, br, bus[0].b), (t2, QI, bi, bus[1].b), (t3, QR, bi, bus[1].b), (t4, QI, br, bus[0].b)):
                    S.op("dve", lambda e, o=o, a=a, b_=b_: TT(e, V3(o.t[:]), a, b_, ALU.mult), reads=tb + [rd], writes=[o.b])
                S.op(ENG_ADDS, lambda e: TT(e, vr.t[:], t1.t[:], t2.t[:], ALU.subtract), reads=[t1.b, t2.b], writes=[vr.b])
                S.op(ENG_ADDS, lambda e: TT(e, vi.t[:], t3.t[:], t4.t[:], ALU.add), reads=[t3.b, t4.b], writes=[vi.b])
            emit_bu(0)
            if len(groups) > 1:
                emit_bu(1)
            mults_adds(0)
            pend_y5 = [None]
            for gi_, (slist, tk0, ntok) in enumerate(groups):
                yield
                ns = len(slist)
                s0 = slist[0]
                V3, QR, QI, PR_, PI_, msk, first, cin_r, cin_i = views(gi_)
                vr, vi = vsets[gi_ % 2]
                if gi_ + 1 < len(groups):
                    mults_adds(gi_ + 1)
                    yield
                if gi_ + 2 < len(groups):
                    emit_bu(gi_ + 2)
                S.op("dve", lambda e: TT(e, first(vr.t[:]), first(vr.t[:]), cin_r, ALU.add), reads=[vr.b, s5cr.b, sts5.b], writes=[vr.b])
                S.op("dve", lambda e: TT(e, first(vi.t[:]), first(vi.t[:]), cin_i, ALU.add), reads=[vi.b, s5cr.b, sts5.b], writes=[vi.b])
                yield
                s5k[0] ^= 1
                gr, gi2 = s5g[s5k[0]][0], s5g[s5k[0]][1]
                S.op("dve", lambda e: e.tensor_tensor_scan(out=gr.t[:], data0=msk, data1=vr.t[:], initial=0.0, op0=ALU.mult, op1=ALU.add),
                     reads=[vr.b, mask32.b, mask4.b], writes=[gr.b])
                S.op("dve", lambda e: e.tensor_tensor_scan(out=gi2.t[:], data0=msk, data1=vi.t[:], initial=0.0, op0=ALU.mult, op1=ALU.add),
                     reads=[vi.b, mask32.b, mask4.b], writes=[gi2.b])
                yield
                hp = s5h[gi_ % 2]
                hr, hi = hp, hp
                for (o, a, b_) in ((hp[0], PR_, gr), (hp[1], PI_, gi2), (hp[2], PR_, gi2), (hp[3], PI_, gr)):
                    S.op(ENG_OUTROT, lambda e, o=o, a=a, b_=b_: TT(e, V3(o.t[:]), a, V3(b_.t[:]), ALU.mult),
                         reads=[Ptab.b, b_.b], writes=[o.b])
                yield
                if not is_s:
                    glr, gli = V3(gr.t[:])[:, :, T5 - 1], V3(gi2.t[:])[:, :, T5 - 1]
                    plr, pli = Ptab.t[:, 0, :, T5 - 1], Ptab.t[:, 1, :, T5 - 1]
                    c_ = lambda i: s5c.t[:, i, :]
                    outr, outi = s5cr.t[:, 0, :], s5cr.t[:, 1, :]
                else:
                    glr, gli = V3(gr.t[:])[:, :, :, LS - 1], V3(gi2.t[:])[:, :, :, LS - 1]
                    plr = Ptab.t[:, 0, s0:s0 + 8, LS - 1:LS].to_broadcast([128, 8, NS])
                    pli = Ptab.t[:, 1, s0:s0 + 8, LS - 1:LS].to_broadcast([128, 8, NS])
                    c_ = lambda i: hn[0].t[:, i, :].rearrange("p (s q) -> p s q", q=NS)
                    outr, outi = s5fin.t[:, 0, s0:s0 + 8, 1:17], s5fin.t[:, 1, s0:s0 + 8, 1:17]
                cb_ = [s5c.b, hn[0].b]
                if not is_s:
                    pl2 = Ptab.t[:, :, :, T5 - 1]
                    ca, cb2 = s5c.t[:, 0:2, :], s5c.t[:, 2:4, :]
                    S.op("dve", lambda e: TT(e, ca, pl2, glr.unsqueeze(1).to_broadcast([128, 2, 16]), ALU.mult),
                         reads=[Ptab.b, gr.b] + cb_, writes=cb_)
                    S.op("dve", lambda e: TT(e, cb2, pl2, gli.unsqueeze(1).to_broadcast([128, 2, 16]), ALU.mult),
                         reads=[Ptab.b, gi2.b] + cb_, writes=cb_)
                    S.op("dve", lambda e: TT(e, outr, c_(0), c_(3), ALU.subtract), reads=cb_, writes=[s5cr.b, s5fin.b])
                    S.op("dve", lambda e: TT(e, outi, c_(2), c_(1), ALU.add), reads=cb_, writes=[s5cr.b, s5fin.b])
                else:
                    cseq = [(c_(0), plr, glr, ALU.mult), (c_(1), pli, gli, ALU.mult), (c_(2), plr, gli, ALU.mult), (c_(3), pli, glr, ALU.mult)]
                    for (o, a, b, op) in cseq:
                        S.op("dve", lambda e, o=o, a=a, b=b, op=op: TT(e, o, a, b, op), reads=[Ptab.b, gr.b, gi2.b] + cb_, writes=cb_)
                    S.op("dve", lambda e: TT(e, outr, c_(0), c_(1), ALU.subtract), reads=cb_, writes=[s5cr.b, s5fin.b])
                    S.op("dve", lambda e: TT(e, outi, c_(2), c_(3), ALU.add), reads=cb_, writes=[s5cr.b, s5fin.b])
                yield
                def emit_y5(gi_=gi_, slist=slist, tk0=tk0, ntok=ntok, hr=hr, hi=hi):
                    y5c0 = 352
                    nq = 4 if not is_s else 2
                    for qi in range(nq):
                        q = qi if not is_s else 2 * gi_ + qi
                        S.op("pe", lambda e, q=q, qi=qi: e.matmul(PB[4].t[:, y5c0 + qi * ntok:y5c0 + (qi + 1) * ntok], dg5.t[:, q, :],
                                                                  u5T.t[:, q, tk0:tk0 + ntok], start=(qi == 0), stop=False, skip_group_check=True),
                             reads=[dg5.b, u5T.b], writes=[PB[4].sub("y5")])
                    for idx, s in enumerate(slist):
                        qi = (s // 4) if not is_s else (s // 4 - 2 * gi_)
                        out = PB[4].t[32 * (s % 4):32 * (s % 4) + 32, y5c0 + qi * ntok:y5c0 + (qi + 1) * ntok]
                        for j4, lw in enumerate((s5CT.t[:, 0, s, :], s5CTn.t[:, s, :], s5CT.t[:, 1, s, :], s5CT.t[:, 1, s, :])):
                            S.op("pe", lambda e, j4=j4, lw=lw: e.matmul(out, lw, hr[j4].t[:, idx * ntok:(idx + 1) * ntok],
                                                                        start=False, stop=(j4 == 3), skip_group_check=True,
                                                                        tile_position=(0, 32 * (s % 4))),
                                 reads=[s5CT.b, s5CTn.b, hr[j4].b], writes=[PB[4].sub("y5")])
                    q0 = 0 if not is_s else 2 * gi_
                    S.op("act", lambda e: e.activation(out=y5pre.t[:, q0:q0 + nq, tk0:tk0 + ntok],
                                                       in_=PB[4].t[:, y5c0:y5c0 + nq * ntok].rearrange("p (q t) -> p q t", t=ntok), func=AF.Copy),
                         reads=[PB[4].sub("y5")], writes=[y5pre.b])
                if pend_y5[0] is not None:
                    pend_y5[0]()
                    yield
                pend_y5[0] = emit_y5
            if pend_y5[0] is not None:
                pend_y5[0]()
                pend_y5[0] = None
                yield
            if ti == 7:
                S.op("dve", lambda e: e.tensor_copy(out=s5fin.t[:, :, :, 0], in_=s5cr.t[:]), reads=[s5cr.b], writes=[s5fin.b])
            if is_s:
                S.dma("sp", o_s5, s5fin.t[:].rearrange("p a s q -> p (a s q)"), reads=[s5fin.b], buf=s5fin.b)
                outbufs.append(s5fin.b)
            if ti == 0:
                dump("y5pre", y5pre.t[:].rearrange("p k t -> p (k t)"), [128, 4 * NTM], [y5pre.b])
            ckpt("E%d" % ti)
            yield
            S.op("act", lambda e: e.activation(out=g5.t[:, :, 0:NT], in_=y5pre.t[:, :, 0:NT], func=AF.Gelu), reads=[y5pre.b], writes=[g5.b])
            for m in range(4):
                yield
                pb = next_pb()
                for q in range(4):
                    S.op("pe", lambda e, m=m, q=q, pb=pb: e.matmul(pb.t[:, 0:NT], wglu_sb.t[:, q, m * 128:(m + 1) * 128], g5.t[:, q, 0:NT],
                                                                   start=(q == 0), stop=(q == 3)),
                         reads=[wglu_sb.b, g5.b], writes=[pb.b])
                S.op("act", lambda e, m=m, pb=pb: e.activation(out=sgl.t[:, 0:NT], in_=pb.t[:, 0:NT], func=AF.Sigmoid,
                                                               bias=prm.t[:, P_S5M + 4 + m:P_S5M + 5 + m]),
                     reads=[pb.b, prm.b], writes=[sgl.b])
                S.op("dve", lambda e, m=m: TT(e, mixt[ti % 2].t[:, 4 + m, 0:NT], g5.t[:, m, 0:NT], sgl.t[:, 0:NT], ALU.mult),
                     reads=[g5.b, sgl.b], writes=[mixt[ti % 2].sub("s5")])
            S.dma("sp", mixd[:, :, t0:t0 + NT], mixt[ti % 2].t[:, :, 0:NT], reads=mixt[ti % 2].allb(), writes=[mixdb[ti]], buf=mixdb[ti])
            ckpt("T%d" % ti)
            if ti == 0:
                dump("mix0", mixt[0].t[:, :, 0:NTM], [128, 8, NTM], mixt[0].allb())
            yield

        import os as _os
        RATIO = int(_os.environ.get("K_RATIO", "1"))

        def drive(gens, ada_every=0):
            gens = [g for g in gens if g is not None]
            n = 0
            while gens:
                for gi__, g in enumerate(list(gens)):
                    for _ in range((RATIO if gi__ == 0 else 1) if RATIO > 0 else (-RATIO if gi__ == 1 else 1)):
                        try:
                            next(g)
                        except StopIteration:
                            if g in gens:
                                gens.remove(g)
                            break
                n += 1
                if ada_every and n % ada_every == 0:
                    ada_step()
        ada_state[0] = 0
        drive([chain1(0)], ada_every=12)
        for ti_ in range(len(TILES_A)):
            if ti_ == 7:
                while ada_state[1] < len(ADA_CH):
                    ada_step()
                fill_x(a1x, amod.t[:, 0:8, 1:17], [amod.b])
                fill_x(sh1x, chunkmod(MOD_SH1)[:, :, 1:17], [mod.b])
                make_amod([(1, (4, 1)), (2, (7, 2))])
            drive([chain2(ti_), chain1(ti_ + 1) if ti_ + 1 < len(TILES_A) else None], ada_every=(10 if ti_ < 7 else 0))
        dump("mixS", mixt[0].t[:, :, 0:64], [128, 8, 64], mixt[0].allb())
        S.barrier()
        ckpt("1a")
        A.lo = LO_GLOBAL
        x1T = A.alloc("x1T", [128, 8, NTOK], F32, top=True)
        vT = A.alloc("vT", [128, 8, NTOK], BF16, top=True)
        wout_sb = A.alloc("wout_sb", [128, 8, D], BF16)
        wout_v = wout.rearrange("(kt p) n -> p kt n", p=128)
        for kh in range(4):
            S.dma("pool", wout_sb.t[:, 2 * kh:2 * kh + 2, :], wout_v[:, 2 * kh:2 * kh + 2, :], writes=[wout_sb.sub(kh)])
        mixb = [A.alloc("mixb%d" % i, [128, 8, 512], BF16) for i in range(2)]

        def load_mix(ti):
            t0, NT, is_s = TILES_B[ti]
            tiles_a = [i for i, (a0, n0, s0_) in enumerate(TILES_A) if a0 >= t0 and a0 < t0 + NT]
            S.dma("sp", mixb[ti % 2].t[:, :, 0:NT], mixd[:, :, t0:t0 + NT], reads=[mixdb[i] for i in tiles_a], writes=[mixb[ti % 2].b])
        xtm2 = A.alloc("xtm2", [128, 4, D], F32)
        xTm = [A.alloc("xTm%d" % i, [128, 512], F32) for i in range(2)]
        sqb = [A.alloc("sqb%d" % i, [128, 512], BF16) for i in range(2)]
        onesb = A.alloc("onesb", [128, 128], BF16)
        S.op("dve", lambda e: e.memset(onesb.t[:], 1.0), writes=[onesb.b])
        tmp2 = [A.alloc("tmp2_%d" % i, [128, 512], F32) for i in range(2)]
        rstdb = [A.alloc("rstdb%d" % i, [128, 512], F32) for i in range(2)]
        g1x = expand_mod("g1x", chunkmod(MOD_G1)[:, :, 1:17], [mod.b])
        a2x = expand_mod("a2x", amod.t[:, 8:16, 1:17], [amod.b])
        sh2x = expand_mod("sh2x", chunkmod(MOD_SH2)[:, :, 1:17], [mod.b])
        print("arena p1b: lo=%d hi=%d" % (A.lo, A.hi))
        TILES_B = [(i * 512, 512, False) for i in range(4)] + [(SEQ, 64, True)]

        def load_x2(ti):
            t0, NT, is_s = TILES_B[ti]
            for blk in range((NT + 127) // 128):
                rows = min(128, NT - blk * 128)
                S.dma("sp", xtm2.t[0:rows, blk, :], xin[t0 + blk * 128:t0 + blk * 128 + rows, :], writes=[xtm2.sub(blk)])
        load_x2(0)
        load_mix(0)

        def stat_accum(src_ap, m, NT, pbs, defer=None):
            sq = sqb[m % 2]
            S.op("act", lambda e: e.activation(out=sq.t[:, 0:NT], in_=src_ap, func=AF.Square), reads=[x1T.sub(m)], writes=[sq.b])

            def mm(m=m, sq=sq):
                S.op("pe", lambda e: e.matmul(pbs.t[:, 0:NT], onesb.t[:], sq.t[:, 0:NT], start=(m == 0), stop=(m == 7)),
                     reads=[onesb.b, sq.b], writes=[pbs.b])
            if defer is None:
                mm()
            else:
                if defer[0] is not None:
                    defer[0]()
                defer[0] = mm
                if m == 7:
                    defer[0]()
                    defer[0] = None

        def stat_finish(NT, pbs, rs):
            S.op("act", lambda e: e.activation(out=rs.t[:, 0:NT], in_=pbs.t[:, 0:NT], func=AF.Ln, scale=1.0 / D, bias=EPS),
                 reads=[pbs.b], writes=[rs.b])
            S.op("act", lambda e: e.activation(out=rs.t[:, 0:NT], in_=rs.t[:, 0:NT], func=AF.Exp, scale=-0.5), reads=[rs.b], writes=[rs.b])

        def b_part1(ti):
            t0, NT, is_s = TILES_B[ti]
            nblk = (NT + 127) // 128
            tsl = slice(t0, t0 + NT)
            pbs = PB[4 + ti % 2]
            dfr = [None]
            for m in range(8):
                pbx = PB[2 + m % 2]
                xm = xTm[m % 2]
                for blk in range(nblk):
                    rows = min(128, NT - blk * 128)
                    S.op("pe", lambda e, blk=blk, rows=rows: e.transpose(
                        pbx.t[:, blk * 128:blk * 128 + rows], xtm2.t[0:rows, blk, m * 128:(m + 1) * 128], cst.t[0:rows, C_ID:C_ID + rows]),
                        reads=[xtm2.sub(blk), cst.b], writes=[pbx.b])
                S.op("act", lambda e: e.activation(out=xm.t[:, 0:NT], in_=pbx.t[:, 0:NT], func=AF.Copy), reads=[pbx.b], writes=[xm.b])
                pb = next_pb()
                for kt in range(8):
                    S.op("pe", lambda e, kt=kt: e.matmul(pb.t[:, 0:NT], wout_sb.t[:, kt, m * 128:(m + 1) * 128], mixb[ti % 2].t[:, kt, 0:NT],
                                                         start=(kt == 0), stop=(kt == 7)),
                         reads=[wout_sb.sub(kt // 2), mixb[ti % 2].b], writes=[pb.b])
                if m == 0 and ti + 1 < len(TILES_B):
                    load_mix(ti + 1)
                if not is_s:
                    S.op("dve", lambda e: e.scalar_tensor_tensor(
                        out=x1T.t[:, m, tsl], in0=pb.t[:, 0:NT], scalar=mod.t[:, 8 * MOD_G1 + m, 0:1], in1=xm.t[:, 0:NT],
                        op0=ALU.mult, op1=ALU.add), reads=[pb.b, mod.b, xm.b], writes=[x1T.sub(m)])
                else:
                    S.op("dve", lambda e: TT(e, tmp2[0].t[:, 0:NT], pb.t[:, 0:NT], g1x.t[:, m, :], ALU.mult),
                         reads=[pb.b, g1x.b], writes=[tmp2[0].b])
                    S.op("dve", lambda e: TT(e, x1T.t[:, m, tsl], tmp2[0].t[:, 0:NT], xm.t[:, 0:NT], ALU.add),
                         reads=[tmp2[0].b, xm.b], writes=[x1T.sub(m)])
                stat_accum(x1T.t[:, m, tsl], m, NT, pbs, defer=dfr)
                yield
            if ti + 1 < len(TILES_B):
                load_x2(ti + 1)
            yield

        def b_part2(ti):
            t0, NT, is_s = TILES_B[ti]
            tsl = slice(t0, t0 + NT)
            rs = rstdb[ti % 2]
            stat_finish(NT, PB[4 + ti % 2], rs)
            yield
            for m in range(8):
                tq = tmp2[m % 2]
                S.op("dve", lambda e: TT(e, tq.t[:, 0:NT], x1T.t[:, m, tsl], rs.t[:, 0:NT], ALU.mult),
                     reads=[x1T.sub(m), rs.b], writes=[tq.b])
                if not is_s:
                    S.op("act", lambda e: e.activation(out=vT.t[:, m, tsl], in_=tq.t[:, 0:NT], func=AF.Identity,
                                                       scale=amod.t[:, 8 + m, 0:1], bias=mod.t[:, 8 * MOD_SH2 + m, 0:1]),
                         reads=[tq.b, amod.b, mod.b], writes=[vT.sub(m)])
                else:
                    S.op("dve", lambda e: TT(e, tq.t[:, 0:NT], tq.t[:, 0:NT], a2x.t[:, m, :], ALU.mult),
                         reads=[tq.b, a2x.b], writes=[tq.b])
                    S.op("dve", lambda e: TT(e, vT.t[:, m, tsl], tq.t[:, 0:NT], sh2x.t[:, m, :], ALU.add),
                         reads=[tq.b, sh2x.b], writes=[vT.sub(m)])
                yield
            if ti == 0:
                dump("x1p", x1T.t[:, :, 0:256], [128, 8, 256], x1T.allb())
                dump("vp", vT.t[:, :, 0:256], [128, 8, 256], vT.allb())
        drive([b_part1(0)])
        for ti_ in range(len(TILES_B)):
            drive([b_part2(ti_), b_part1(ti_ + 1) if ti_ + 1 < len(TILES_B) else None])
        S.barrier()
        ckpt("1b")

        A.lo = LO_GLOBAL
        tmp2 = [A.alloc("tmp3_%d" % i, [128, 512], F32) for i in range(2)]
        rstdb = [A.alloc("rstd3_%d" % i, [128, 512], F32) for i in range(2)]
        sqb = [A.alloc("sqb3_%d" % i, [128, 512], BF16) for i in range(2)]
        onesb = A.alloc("onesb3", [128, 128], BF16)
        S.op("dve", lambda e: e.memset(onesb.t[:], 1.0), writes=[onesb.b])
        g2x = expand_mod("g2x", chunkmod(MOD_G2)[:, :, 1:17], [mod.b])
        afx = expand_mod("afx", amod.t[:, 16:24, 1:17], [amod.b])
        shfx = expand_mod("shfx", chunkmod(MOD_SHF)[:, :, 1:17], [mod.b])
        LO_P2 = A.lo
        hT = A.alloc("hT", [128, 6, NTOK], BF16)
        wgs = [A.alloc("wgs%d" % i, [128, 8, 256], BF16) for i in range(3)]
        wus = [A.alloc("wus%d" % i, [128, 8, 256], BF16) for i in range(3)]
        wds = [A.alloc("wds%d" % i, [128, 6, D], BF16) for i in range(2)]
        sgt = [A.alloc("sgt%d" % i, [128, 512], BF16) for i in range(2)]
        print("arena p2: lo=%d hi=%d" % (A.lo, A.hi))
        wg_v = wg.rearrange("(kt p) n -> p kt n", p=128)
        wu_v = wu.rearrange("(kt p) n -> p kt n", p=128)
        wd_v = wd.rearrange("(j p) n -> p j n", p=128)
        QUARTERS = [(0, 6), (6, 12), (12, 18), (18, 22)]
        SLABS = [(q, ja + 2 * s) for q, (ja, jb) in enumerate(QUARTERS) for s in range((jb - ja) // 2)]

        def load_gu(si):
            q, j0 = SLABS[si]
            S.dma("pool", wgs[si % 3].t[:], wg_v[:, :, j0 * 128:(j0 + 2) * 128], writes=[wgs[si % 3].b])
            S.dma("pool", wus[si % 3].t[:], wu_v[:, :, j0 * 128:(j0 + 2) * 128], writes=[wus[si % 3].b])

        def load_wd(q):
            ja, jb = QUARTERS[q]
            for jh in range(0, jb - ja, 2):
                S.dma("pool", wds[q % 2].t[:, jh:jh + 2, :], wd_v[:, ja + jh:ja + jh + 2, :], writes=[wds[q % 2].b])
        load_gu(0)
        load_gu(1)
        load_wd(0)
        gbank = [0]
        si = 0
        for q, (ja, jb) in enumerate(QUARTERS):
            if q + 1 < 4:
                load_wd(q + 1)
            for s in range((jb - ja) // 2):
                if si + 2 < len(SLABS):
                    load_gu(si + 2)
                wgt, wut = wgs[si % 3], wus[si % 3]
                for jc in range(2):
                    jj = 2 * s + jc
                    for (t0, NT, is_s) in TILES_B:
                        tsl = slice(t0, t0 + NT)
                        gbank[0] ^= 1
                        pbg, pbu = PB[gbank[0]], PB[2 + gbank[0]]
                        for (wt, pb_) in ((wgt, pbg), (wut, pbu)):
                            for kt in range(8):
                                S.op("pe", lambda e, kt=kt, wt=wt, pb_=pb_: e.matmul(
                                    pb_.t[:, 0:NT], wt.t[:, kt, jc * 128:(jc + 1) * 128], vT.t[:, kt, tsl], start=(kt == 0), stop=(kt == 7)),
                                    reads=[wt.b] + vT.allb(), writes=[pb_.b])
                        sg_ = sgt[gbank[0]]
                        S.op("act", lambda e, pbg=pbg, sg_=sg_: e.activation(out=sg_.t[:, 0:NT], in_=pbg.t[:, 0:NT], func=AF.Silu),
                             reads=[pbg.b], writes=[sg_.b])
                        S.op("dve", lambda e, pbu=pbu, sg_=sg_: TT(e, hT.t[:, jj, tsl], sg_.t[:, 0:NT], pbu.t[:, 0:NT], ALU.mult),
                             reads=[sg_.b, pbu.b], writes=[hT.sub(jj)])
                si += 1
            nj = jb - ja
            wdt = wds[q % 2]
            for (t0, NT, is_s) in TILES_B:
                tsl = slice(t0, t0 + NT)
                for m in range(8):
                    pb = PB[4 + m % 2]
                    for jj in range(nj):
                        S.op("pe", lambda e, jj=jj, m=m, pb=pb: e.matmul(pb.t[:, 0:NT], wdt.t[:, jj, m * 128:(m + 1) * 128], hT.t[:, jj, tsl],
                                                                         start=(jj == 0), stop=(jj == nj - 1)),
                             reads=[wdt.b, hT.sub(jj)], writes=[pb.b])
                    if not is_s:
                        S.op("dve", lambda e, m=m, pb=pb: e.scalar_tensor_tensor(
                            out=x1T.t[:, m, tsl], in0=pb.t[:, 0:NT], scalar=mod.t[:, 8 * MOD_G2 + m, 0:1], in1=x1T.t[:, m, tsl],
                            op0=ALU.mult, op1=ALU.add), reads=[pb.b, mod.b, x1T.sub(m)], writes=[x1T.sub(m)])
                    else:
                        S.op("dve", lambda e, m=m, pb=pb: TT(e, tmp2[0].t[:, 0:NT], pb.t[:, 0:NT], g2x.t[:, m, :], ALU.mult),
                             reads=[pb.b, g2x.b], writes=[tmp2[0].b])
                        S.op("dve", lambda e, m=m: TT(e, x1T.t[:, m, tsl], tmp2[0].t[:, 0:NT], x1T.t[:, m, tsl], ALU.add),
                             reads=[tmp2[0].b, x1T.sub(m)], writes=[x1T.sub(m)])
        S.barrier()
        ckpt("ffn")
        A.lo = LO_P2
        yTs = [A.alloc("yT%d" % i, [128, 8, 512], F32) for i in range(2)]
        ytm = [A.alloc("ytm%d" % i, [128, D], F32) for i in range(2)]
        print("arena final: lo=%d hi=%d" % (A.lo, A.hi))
        oi = [0]

        def f_part1(ti):
            t0, NT, is_s = TILES_B[ti]
            tsl = slice(t0, t0 + NT)
            yT = yTs[ti % 2]
            pbs = PB[6 + ti % 2]
            rs = rstdb[ti % 2]
            for m in range(8):
                stat_accum(x1T.t[:, m, tsl], m, NT, pbs)
                if m % 2 == 1:
                    yield
            stat_finish(NT, pbs, rs)
            yield
            for m in range(8):
                tq = tmp2[m % 2]
                S.op("dve", lambda e: TT(e, tq.t[:, 0:NT], x1T.t[:, m, tsl], rs.t[:, 0:NT], ALU.mult),
                     reads=[x1T.sub(m), rs.b], writes=[tq.b])
                if not is_s:
                    S.op("act", lambda e: e.activation(out=yT.t[:, m, 0:NT], in_=tq.t[:, 0:NT], func=AF.Identity,
                                                       scale=amod.t[:, 16 + m, 0:1], bias=mod.t[:, 8 * MOD_SHF + m, 0:1]),
                         reads=[tq.b, amod.b, mod.b], writes=[yT.sub(m)])
                else:
                    S.op("dve", lambda e: TT(e, tq.t[:, 0:NT], tq.t[:, 0:NT], afx.t[:, m, :], ALU.mult),
                         reads=[tq.b, afx.b], writes=[tq.b])
                    S.op("dve", lambda e: TT(e, yT.t[:, m, 0:NT], tq.t[:, 0:NT], shfx.t[:, m, :], ALU.add),
                         reads=[tq.b, shfx.b], writes=[yT.sub(m)])
                yield

        def f_part2(ti):
            t0, NT, is_s = TILES_B[ti]
            yT = yTs[ti % 2]
            for blk in range((NT + 127) // 128):
                rows = min(128, NT - blk * 128)
                yo = ytm[oi[0] % 2]
                oi[0] += 1
                for half in range(2):
                    pbt = PB[half]
                    for k4 in range(4):
                        kt = 4 * half + k4
                        S.op("pe", lambda e, kt=kt, k4=k4: e.transpose(
                            pbt.t[0:rows, k4 * 128:(k4 + 1) * 128], yT.t[:, kt, blk * 128:blk * 128 + rows], ident),
                            reads=[yT.sub(kt), cst.b], writes=[pbt.b])
                    if half == 0:
                        S.op("act", lambda e: e.activation(out=yo.t[0:rows, 0:512], in_=pbt.t[0:rows, :], func=AF.Copy),
                             reads=[pbt.b], writes=[yo.b])
                    else:
                        S.op("dve", lambda e: e.tensor_copy(out=yo.t[0:rows, 512:1024], in_=pbt.t[0:rows, :]),
                             reads=[pbt.b], writes=[yo.b])
                    yield
                S.dma("sp", yout[t0 + blk * 128:t0 + blk * 128 + rows, :], yo.t[0:rows, :], reads=[yo.b], buf=yo.b)
        import os as _os2
        if True:
            for ti_ in range(len(TILES_B)):
                drive([f_part1(ti_)])
                drive([f_part2(ti_)])
        else:
            drive([f_part1(0)])
            for ti_ in range(len(TILES_B)):
                drive([f_part2(ti_), f_part1(ti_ + 1) if ti_ + 1 < len(TILES_B) else None])
        S.barrier()
    return nc, dumps


def _prep_inputs(inp):
    cstv = _consts()
    prmv = _params(inp)
    BT, CT = _s5mats(inp)
    maps = []
    for i in range(NCORES):
        m = {}
        m["xin"] = np.ascontiguousarray(np.concatenate(
            [inp["x_prompt"][i], inp["x_sample"][NS * i:NS * (i + 1)].reshape(NS * LS, D)], axis=0), dtype=np.float32)
        m["cin"] = np.ascontiguousarray(np.concatenate(
            [inp["c_prompt"][i:i + 1], inp["c_sample"][NS * i:NS * (i + 1)]], axis=0), dtype=np.float32)
        m["wada"] = np.ascontiguousarray(inp["w_ada"][0], dtype=np.float32)
        m["wadaf"] = np.ascontiguousarray(inp["w_ada_f"], dtype=np.float32)
        m["win"] = np.ascontiguousarray(inp["w_in"][0], dtype=np.float32)
        m["wglu"] = np.ascontiguousarray(inp["w_glu"][0], dtype=np.float32)
        m["wout"] = np.ascontiguousarray(inp["w_out"][0], dtype=np.float32)
        m["wg"] = np.ascontiguousarray(inp["w_ffn_gate"][0], dtype=np.float32)
        m["wu"] = np.ascontiguousarray(inp["w_ffn_up"][0], dtype=np.float32)
        m["wd"] = np.ascontiguousarray(inp["w_ffn_down"][0], dtype=np.float32)
        m["cst"] = cstv
        m["prm"] = prmv
        m["s5bt"] = BT.reshape(128, -1)
        m["s5ct"] = CT.reshape(128, -1)
        m["stssd"] = np.ascontiguousarray(inp["state_ssd"][0, NS * i:NS * (i + 1)], dtype=np.float32)
        sc = inp["state_conv"][0, NS * i:NS * (i + 1)]
        m["stconv"] = np.ascontiguousarray(
            sc.reshape(NS, 3, 8, 128).transpose(3, 2, 0, 1).reshape(128, -1), dtype=np.float32)
        sr = inp["state_s5_re"][0, NS * i:NS * (i + 1)]
        si = inp["state_s5_im"][0, NS * i:NS * (i + 1)]
        st = np.stack([sr, si], 0).reshape(2, NS, 16, 128).transpose(3, 0, 2, 1)
        m["sts5"] = np.ascontiguousarray(st.reshape(128, -1), dtype=np.float32)
        maps.append(m)
    return maps


_CACHE = {}


def kernel(**inputs):
    inp = {k: np.asarray(v) for k, v in inputs.items()}
    if "nc" not in _CACHE:
        _CACHE["nc"] = build()[0]
    nc = _CACHE["nc"]
    maps = _prep_inputs(inp)
    res = run_bass_kernel_spmd(nc, maps, core_ids=list(range(NCORES)))
    R = res.results
    y_p = np.stack([R[i]["yout"][:SEQ] for i in range(NCORES)], 0)
    y_s = np.concatenate([R[i]["yout"][SEQ:].reshape(NS, LS, D) for i in range(NCORES)], 0)
    ssd_p = np.stack([R[i]["o_ssdp"].reshape(128, 8, 64).transpose(1, 2, 0) for i in range(NCORES)], 0)[None]
    ssd_s = np.concatenate([R[i]["o_ssds"] for i in range(NCORES)], 0)[None]
    conv = [R[i]["o_conv"].reshape(128, 8, 17, 3).transpose(2, 3, 1, 0).reshape(17, 3, 1024) for i in range(NCORES)]
    conv_p = np.stack([c[0] for c in conv], 0)[None]
    conv_s = np.concatenate([c[1:] for c in conv], 0)[None]
    s5 = [R[i]["o_s5"].reshape(128, 2, 16, 17).transpose(1, 3, 2, 0).reshape(2, 17, 32, 64) for i in range(NCORES)]
    re_p = np.stack([s[0, 0] for s in s5], 0)[None]
    re_s = np.concatenate([s[0, 1:] for s in s5], 0)[None]
    im_p = np.stack([s[1, 0] for s in s5], 0)[None]
    im_s = np.concatenate([s[1, 1:] for s in s5], 0)[None]
    f = lambda a: np.ascontiguousarray(a, dtype=np.float32)
    return (f(y_p), f(y_s), f(ssd_p), f(ssd_s), f(conv_p), f(conv_s), f(re_p), f(re_s), f(im_p), f(im_s))
```

```python
import math
import numpy as np
from contextlib import ExitStack
import concourse.bass as bass
import concourse.mybir as mybir
from concourse.bass_utils import run_bass_kernel_spmd

F32 = mybir.dt.float32
BF16 = mybir.dt.bfloat16
I32 = mybir.dt.int32
AF = mybir.ActivationFunctionType
ALU = mybir.AluOpType

NCORES = 8
D = 1024
SEQ = 2048
NS = 16
LS = 4
NTOK = SEQ + NS * LS
DFF = 2816
NJ = DFF // 128
INP = 2056
EPS = 1e-6
T5 = 32
TILES = [(0, 512), (512, 512), (1024, 512), (1536, 512), (2048, 64)]
PI = math.pi


class Buf:
    def __init__(self, name):
        self.name = name
        self.w = None
        self.r = []
        self.dsem = None
        self.dcnt = 0


class TL:
    def __init__(self, t, name):
        self.t = t
        self.name = name
        self.b = Buf(name)
        self.subs = {}

    def sub(self, k):
        if getattr(self, "nosub", False):
            return self.b
        if k not in self.subs:
            self.subs[k] = Buf("%s_%s" % (self.name, k))
        return self.subs[k]

    def allb(self):
        return [self.b] + list(self.subs.values())

    def __getitem__(self, k):
        return self.t[k]


class Sched:
    ENG = ["pe", "act", "dve", "pool", "sp"]

    def __init__(self, nc, es):
        self.nc = nc
        self.es = es
        self.eobj = {"pe": nc.tensor, "act": nc.scalar, "dve": nc.vector, "pool": nc.gpsimd, "sp": nc.sync}
        self.cnt = {e: 0 for e in self.ENG}
        self.sem = {e: es.enter_context(nc.semaphore("s_" + e)) for e in self.ENG}
        self.seen = {e: {} for e in self.ENG}
        self.dbufs = []
        self.ninst = 0
        self.dead = False
        self.pe_pending = None

    def _flush_pe(self):
        if self.pe_pending is not None:
            self.pe_pending.then_inc(self.sem["pe"], 1)
            self.cnt["pe"] += 1
            self.pe_pending = None

    def _deps(self, eng, reads, writes):
        deps = []
        for b in reads:
            if b.w is not None:
                deps.append(b.w)
        for b in writes:
            if b.w is not None:
                deps.append(b.w)
            deps.extend(b.r)
        waits = {}
        for (sem, val, key) in deps:
            if key == "pe" and eng == "pe":
                continue
            if self.seen[eng].get(key, 0) >= val:
                continue
            if key == "pe" and val > self.cnt["pe"]:
                self._flush_pe()
            if key not in waits or waits[key][1] < val:
                waits[key] = (sem, val)
        for key, (sem, val) in waits.items():
            self.seen[eng][key] = val
        return list(waits.values())

    def op(self, eng, fn, reads=(), writes=()):
        if self.dead:
            return None
        xr = [b for b in reads if getattr(b, "excl", False)]
        if xr:
            reads = [b for b in reads if not getattr(b, "excl", False)]
            writes = list(writes) + xr
        waits = self._deps(eng, reads, writes)
        e = self.eobj[eng]
        for (s_, v_) in waits:
            e.wait_ge(s_, v_)
        if eng == "pe":
            self.pe_pending = fn(e)
            tok = (self.sem[eng], self.cnt[eng] + 1, eng)
        else:
            self.cnt[eng] += 1
            tok = (self.sem[eng], self.cnt[eng], eng)
            fn(e).then_inc(self.sem[eng], 1)
        for b in reads:
            b.r.append(tok)
        for b in writes:
            b.w = tok
            b.r = []
        self.ninst += 1
        return tok

    def dma(self, eng, out, in_, reads=(), writes=(), buf=None, **kw):
        if self.dead:
            return None
        waits = self._deps(eng, reads, writes)
        if buf is None:
            buf = writes[0] if writes else reads[0]
        if buf.dsem is None:
            buf.dsem = self.es.enter_context(self.nc.semaphore("d_" + buf.name))
            self.dbufs.append(buf)
        buf.dcnt += 16
        tok = (buf.dsem, buf.dcnt, "d_" + buf.name)
        e = self.eobj[eng]
        for (s_, v_) in waits:
            e.wait_ge(s_, v_)
        e.dma_start(out=out, in_=in_, **kw).then_inc(buf.dsem, 16)
        for b in reads:
            b.r.append(tok)
        for b in writes:
            b.w = tok
            b.r = []
        self.ninst += 1
        return tok

    def barrier(self):
        if self.dead:
            return
        self._flush_pe()
        for e in self.ENG:
            waits = []
            for o in self.ENG:
                if o != e and self.cnt[o] > self.seen[e].get(o, 0):
                    waits.append((self.sem[o], self.cnt[o]))
                    self.seen[e][o] = self.cnt[o]
            for b in self.dbufs:
                key = "d_" + b.name
                if b.dcnt > self.seen[e].get(key, 0):
                    waits.append((b.dsem, b.dcnt))
                    self.seen[e][key] = b.dcnt
            for (s_, v_) in waits:
                self.eobj[e].wait_ge(s_, v_)

    def emit(self):
        pass


C_ID = 0
C_TRI = 128
C_NEG = 256
C_TRI64 = 384
C_NEG64 = 512
C_SEG64 = 640
C_SEGI = 768
CST_W = 784

P_BMOD = 0
P_GAIN = 64
P_CONV = 88
P_SSDFM = 128
P_S5P = 136
P_S5M = 184
P_SSD8 = 192
PRM_W = 194


def _consts():
    c = np.zeros((128, CST_W), np.float32)
    c[:, C_ID:C_ID + 128] = np.eye(128, dtype=np.float32)
    s = np.arange(128)[:, None]
    l = np.arange(128)[None, :]
    c[:, C_TRI:C_TRI + 128] = (s <= l).astype(np.float32)
    c[:, C_NEG:C_NEG + 128] = np.where(l >= s, 0.0, -30000.0)
    same = (s // LS == l // LS) & (s < 64) & (l < 64)
    c[:, C_TRI64:C_TRI64 + 128] = ((s <= l) & same).astype(np.float32)
    c[:, C_NEG64:C_NEG64 + 128] = np.where((l >= s) & same, 0.0, -30000.0)
    c[:, C_SEG64:C_SEG64 + 128] = same.astype(np.float32)
    j = np.arange(16)[None, :]
    c[:, C_SEGI:C_SEGI + 16] = ((s // LS == j) & (s < 64)).astype(np.float32)
    return c


def _fm(v, nt):
    return np.ascontiguousarray(np.asarray(v, np.float32).reshape(nt, 128).T)


def _params(inp):
    p = np.zeros((128, PRM_W), np.float32)
    p[:, P_BMOD:P_BMOD + 48] = _fm(inp["b_ada"][0], 48)
    p[:, P_BMOD + 48:P_BMOD + 64] = _fm(inp["b_ada_f"], 16)
    p[:, P_GAIN:P_GAIN + 8] = _fm(inp["norm1_g"][0], 8)
    p[:, P_GAIN + 8:P_GAIN + 16] = _fm(inp["norm2_g"][0], 8)
    p[:, P_GAIN + 16:P_GAIN + 24] = _fm(inp["normf_g"], 8)
    cw = inp["conv_w"][0]
    cv = np.zeros((128, 8, 5), np.float32)
    for k in range(4):
        cv[:, :, k] = _fm(cw[k], 8)
    cv[:, :, 4] = _fm(inp["conv_b"][0], 8)
    p[:, P_CONV:P_CONV + 40] = cv.reshape(128, 40)
    Dh = inp["ssd_D"][0]
    dfm = np.zeros((128, 4), np.float32)
    for pr in range(4):
        dfm[0:64, pr] = Dh[2 * pr]
        dfm[64:128, pr] = Dh[2 * pr + 1]
    p[:, P_SSDFM:P_SSDFM + 4] = dfm
    p[:, P_SSDFM + 4:P_SSDFM + 8] = _fm(inp["ssd_norm_g"][0], 4)

    def st(a):
        return np.ascontiguousarray(np.asarray(a, np.float32).reshape(16, 128).T)
    p[:, P_S5P:P_S5P + 16] = st(inp["s5_A_re"][0])
    p[:, P_S5P + 16:P_S5P + 32] = st(inp["s5_A_im"][0])
    p[:, P_S5P + 32:P_S5P + 48] = st(np.repeat(inp["s5_log_step"][0][:, None], 64, axis=1))
    p[:, P_S5M:P_S5M + 4] = _fm(inp["s5_D"][0], 4)
    p[:, P_S5M + 4:P_S5M + 8] = _fm(inp["b_glu"][0], 4)
    p[0:8, P_SSD8] = inp["ssd_dt_bias"][0]
    p[0:8, P_SSD8 + 1] = inp["ssd_A_log"][0]
    return p


def _s5mats(inp):
    Br, Bi = inp["s5_B_re"][0], inp["s5_B_im"][0]
    Cr, Ci = inp["s5_C_re"][0], inp["s5_C_im"][0]
    BT = np.zeros((128, 2, 16, 128), np.float32)
    CT = np.zeros((128, 2, 16, 32), np.float32)
    for s in range(16):
        for gl in range(2):
            g = 2 * s + gl
            r0 = (g % 8) * 16
            BT[r0:r0 + 16, 0, s, gl * 64:(gl + 1) * 64] = Br[g].T
            BT[r0:r0 + 16, 1, s, gl * 64:(gl + 1) * 64] = Bi[g].T
            CT[gl * 64:(gl + 1) * 64, 0, s, gl * 16:(gl + 1) * 16] = Cr[g].T
            CT[gl * 64:(gl + 1) * 64, 1, s, gl * 16:(gl + 1) * 16] = Ci[g].T
    return BT, CT


class Arena:
    def __init__(self, nc, es, words):
        self.t = es.enter_context(nc.sbuf_tensor("arena", [128, words], F32))
        self.words = words
        self.lo = 0
        self.hi = words

    def alloc(self, name, shape, dt, top=False):
        n = 1
        for d in shape[1:]:
            n *= d
        w = n if dt == F32 or dt == I32 else (n + 1) // 2
        w = (w + 3) // 4 * 4
        if top:
            self.hi -= w
            off = self.hi
        else:
            off = self.lo
            self.lo += w
        assert self.lo <= self.hi, "arena overflow at %s: lo=%d hi=%d" % (name, self.lo, self.hi)
        ap = self.t[:, off:off + w]
        if dt != F32:
            ap = ap.bitcast(dt)
        ap = ap[:, 0:n]
        if len(shape) == 3:
            ap = ap.rearrange("p (a b) -> p a b", b=shape[2])
        elif len(shape) == 4:
            ap = ap.rearrange("p (a b c) -> p a b c", b=shape[2], c=shape[3])
        if shape[0] < 128:
            ap = ap[0:shape[0]]
        return TL(ap, name)


class StopBuild(Exception):
    pass


def build(dbg=None, stop_after=None):
    nc = bass.Bass("TRN2", target_bir_lowering=False)

    SH = []

    def ckpt(name):
        if stop_after == name:
            SH[0].barrier()
            SH[0].dead = True
    dt_in = lambda name, shape: nc.dram_tensor(name, list(shape), F32, kind="ExternalInput").ap()
    dt_out = lambda name, shape: nc.dram_tensor(name, list(shape), F32, kind="ExternalOutput").ap()
    xin = dt_in("xin", [NTOK, D])
    cin = dt_in("cin", [17, D])
    wada = dt_in("wada", [D, 6144])
    wadaf = dt_in("wadaf", [D, 2048])
    win = dt_in("win", [D, INP])
    wglu = dt_in("wglu", [512, 512])
    wout = dt_in("wout", [D, D])
    wg = dt_in("wg", [D, DFF])
    wu = dt_in("wu", [D, DFF])
    wd = dt_in("wd", [DFF, D])
    cst_d = dt_in("cst", [128, CST_W])
    prm_d = dt_in("prm", [128, PRM_W])
    s5bt_d = dt_in("s5bt", [128, 2 * 16 * 128])
    s5ct_d = dt_in("s5ct", [128, 2 * 16 * 32])
    stssd_d = dt_in("stssd", [NS, 8, 64, 128])
    stconv_d = dt_in("stconv", [128, 8 * NS * 3])
    sts5_d = dt_in("sts5", [128, 2 * 16 * NS])
    yout = dt_out("yout", [NTOK, D])
    o_ssdp = dt_out("o_ssdp", [128, 512])
    o_ssds = dt_out("o_ssds", [NS, 8, 64, 128])
    o_conv = dt_out("o_conv", [128, 8 * 17 * 3])
    o_s5 = dt_out("o_s5", [128, 2 * 16 * 17])
    mixd = nc.dram_tensor("mixd", [128, 8, NTOK], BF16, kind="Internal").ap()
    dumps = {}

    with ExitStack() as es:
        S = Sched(nc, es)
        NEED_CTN = []
        SH.append(S)
        A = Arena(nc, es, 53200)
        outbufs = []

        def dump(name, ap, shape, reads):
            if dbg is None or name not in dbg:
                return
            d = dt_out("dbg_" + name, shape)
            dumps[name] = shape
            b = Buf("dbg_" + name)
            S.dma("sp" if ap.dtype == F32 else "pool", d, ap, reads=reads, buf=b)
            outbufs.append(b)

        PB = [TL(es.enter_context(nc.psum_tensor("pb%d" % i, [128, 512], F32)), "pb%d" % i) for i in range(8)]
        for pb_ in PB:
            pb_.b.excl = True
            pb_.nosub = True

        def pbf(i):
            return PB[i].t[:].bitcast(BF16)

        cst = A.alloc("cst", [128, CST_W], F32)
        prm = A.alloc("prm", [128, PRM_W], F32)
        identb = A.alloc("identb", [128, 128], BF16)
        onesf = A.alloc("onesf", [128, 128], F32)
        mod = A.alloc("mod", [128, 64, 17], F32)
        amod = A.alloc("amod", [128, 24, 17], F32)
        s5fin = A.alloc("s5fin", [128, 2, 16, 17], F32)
        scT = A.alloc("scT", [128, 8, 17], BF16)
        LO_GLOBAL = A.lo
        win_sb = A.alloc("win_sb", [128, 8, INP], BF16)
        wglu_sb = A.alloc("wglu_sb", [128, 4, 512], BF16)
        s5BT = A.alloc("s5BT", [128, 2, 16, 128], BF16)
        s5CT = A.alloc("s5CT", [128, 2, 16, 32], BF16)
        LO_W = A.lo

        def load_1a_weights():
            for a_ in range(4):
                S.dma("pool", s5BT.t[:].rearrange("p a s c -> p (a s c)")[:, a_ * 1024:(a_ + 1) * 1024],
                      s5bt_d[:, a_ * 1024:(a_ + 1) * 1024], writes=[s5BT.b])
            S.dma("pool", s5CT.t[:].rearrange("p a s c -> p (a s c)"), s5ct_d, writes=[s5CT.b])
            win_v = win.rearrange("(kt p) n -> p kt n", p=128)
            for kh in range(4):
                for ch in range(2):
                    S.dma("pool", win_sb.t[:, 2 * kh:2 * kh + 2, ch * 1028:(ch + 1) * 1028],
                          win_v[:, 2 * kh:2 * kh + 2, ch * 1028:(ch + 1) * 1028], writes=[win_sb.sub(kh)])
            S.dma("pool", wglu_sb.t[:], wglu.rearrange("(kt p) n -> p kt n", p=128), writes=[wglu_sb.b])

        ident = cst.t[:, C_ID:C_ID + 128]
        S.dma("sp", cst.t[:], cst_d, writes=[cst.b])
        S.dma("sp", prm.t[:], prm_d, writes=[prm.b])
        S.op("act", lambda e: e.activation(out=identb.t[:], in_=ident, func=AF.Copy), reads=[cst.b], writes=[identb.b])
        S.op("dve", lambda e: e.memset(onesf.t[:], 1.0), writes=[onesf.b])

        def chunkmod(i):
            return mod.t[:, 8 * i:8 * i + 8, :]

        ssd8 = A.alloc("ssd8", [8, 4], F32)
        S.op("act", lambda e: e.activation(out=ssd8.t[:, 1:2], in_=prm.t[0:8, P_SSD8 + 1:P_SSD8 + 2], func=AF.Exp),
             reads=[prm.b], writes=[ssd8.b])
        S.op("dve", lambda e: e.tensor_scalar(out=ssd8.t[:, 1:2], in0=ssd8.t[:, 1:2], scalar1=-1.0, scalar2=None, op0=ALU.mult),
             reads=[ssd8.b], writes=[ssd8.b])
        S.op("dve", lambda e: e.tensor_copy(out=ssd8.t[:, 0:1], in_=prm.t[0:8, P_SSD8:P_SSD8 + 1]), reads=[prm.b], writes=[ssd8.b])

        Ptab = A.alloc("Ptab", [128, 2, 16, T5], F32)
        Qtab = A.alloc("Qtab", [128, 2, 16, T5], F32)
        s5t = [A.alloc("s5t%d" % i, [128, 512], F32) for i in range(2)]

        def alias(name, ap, buf):
            tl = TL(ap, name)
            tl.b = buf
            return tl
        sw = alias("s5work", s5t[1].t[:, 0:384].rearrange("p (a b) -> p a b", b=16), s5t[1].b)
        tmpA = alias("tmpA", s5t[0].t[:, 0:256].rearrange("p (a b) -> p a b", b=T5 // 2), s5t[0].b)
        tmpB = alias("tmpB", s5t[0].t[:, 256:512].rearrange("p (a b) -> p a b", b=T5 // 2), s5t[0].b)
        mask32 = A.alloc("mask32", [128, 16, T5], BF16)
        s5v = [A.alloc("s5v%d" % i, [128, 512], F32) for i in range(2)]
        qtmp = alias("qtmp", s5v[0].t[:].rearrange("p (s t) -> p s t", t=T5), s5v[0].b)
        mask4 = A.alloc("mask4", [128, 128, LS], BF16)
        s5cr = A.alloc("s5cr", [128, 2, 16], F32)
        W = lambda i: sw.t[:, i, :]
        pv = lambda i: prm.t[:, P_S5P + 16 * i:P_S5P + 16 * (i + 1)]
        swb = [sw.b, prm.b]

        def dv(fn):
            S.op("dve", fn, reads=swb, writes=[sw.b])

        def act(fn):
            S.op("act", fn, reads=swb, writes=[sw.b])
        TT = lambda e, o, a, b, op: e.tensor_tensor(out=o, in0=a, in1=b, op=op)
        def exp_acc(dst, src):
            dv(lambda e: e.tensor_scalar(out=W(22), in0=src, scalar1=1.0 / 16, scalar2=None, op0=ALU.mult))
            dv(lambda e: e.tensor_scalar(out=dst, in0=W(22), scalar1=1.0 / 7, scalar2=1.0, op0=ALU.mult, op1=ALU.add))
            for k in (6, 5, 4, 3, 2, 1):
                dv(lambda e: TT(e, dst, dst, W(22), ALU.mult))
                dv(lambda e, k=k: e.tensor_scalar(out=dst, in0=dst, scalar1=1.0 / k, scalar2=1.0, op0=ALU.mult, op1=ALU.add))
            for _ in range(4):
                dv(lambda e: TT(e, dst, dst, dst, ALU.mult))
        exp_acc(W(0), pv(2))
        dv(lambda e: TT(e, W(1), pv(0), W(0), ALU.mult))
        dv(lambda e: TT(e, W(2), pv(1), W(0), ALU.mult))
        exp_acc(W(3), W(1))

        def range_reduce(dst, src, add):
            ki = A_ki
            dv(lambda e: e.tensor_scalar(out=W(20), in0=src, scalar1=float(add), scalar2=1.0 / (2 * PI), op0=ALU.add, op1=ALU.mult))
            S.op("dve", lambda e: e.tensor_copy(out=ki.t[:], in_=W(20)), reads=swb, writes=[ki.b])
            S.op("dve", lambda e: e.tensor_copy(out=W(21), in_=ki.t[:]), reads=[ki.b], writes=[sw.b])
            dv(lambda e: e.tensor_scalar(out=W(20), in0=src, scalar1=float(add), scalar2=None, op0=ALU.add))
            dv(lambda e: e.scalar_tensor_tensor(out=dst, in0=W(21), scalar=-2 * PI, in1=W(20), op0=ALU.mult, op1=ALU.add))
            dv(lambda e: e.tensor_scalar(out=dst, in0=dst, scalar1=PI, scalar2=-PI, op0=ALU.min, op1=ALU.max))
        A_ki = A.alloc("s5ki", [128, 16], I32)
        range_reduce(W(4), W(2), 0.0)
        range_reduce(W(5), W(2), PI / 2)
        act(lambda e: e.activation(out=W(6), in_=W(4), func=AF.Sin))
        act(lambda e: e.activation(out=W(7), in_=W(5), func=AF.Sin))
        dv(lambda e: TT(e, W(8), W(3), W(7), ALU.mult))
        dv(lambda e: TT(e, W(9), W(3), W(6), ALU.mult))
        dv(lambda e: e.tensor_scalar(out=W(10), in0=W(8), scalar1=-1.0, scalar2=None, op0=ALU.add))
        dv(lambda e: TT(e, W(11), pv(0), pv(0), ALU.mult))
        dv(lambda e: TT(e, W(12), pv(1), pv(1), ALU.mult))
        dv(lambda e: TT(e, W(11), W(11), W(12), ALU.add))
        dv(lambda e: e.reciprocal(out=W(11), in_=W(11)))
        dv(lambda e: TT(e, W(12), W(10), pv(0), ALU.mult))
        dv(lambda e: TT(e, W(13), W(9), pv(1), ALU.mult))
        dv(lambda e: TT(e, W(12), W(12), W(13), ALU.add))
        dv(lambda e: TT(e, W(14), W(12), W(11), ALU.mult))
        dv(lambda e: TT(e, W(12), W(9), pv(0), ALU.mult))
        dv(lambda e: TT(e, W(13), W(10), pv(1), ALU.mult))
        dv(lambda e: TT(e, W(12), W(12), W(13), ALU.subtract))
        dv(lambda e: TT(e, W(15), W(12), W(11), ALU.mult))
        dv(lambda e: TT(e, W(12), W(8), W(8), ALU.mult))
        dv(lambda e: TT(e, W(13), W(9), W(9), ALU.mult))
        dv(lambda e: TT(e, W(12), W(12), W(13), ALU.add))
        dv(lambda e: e.reciprocal(out=W(12), in_=W(12)))
        dv(lambda e: TT(e, W(16), W(8), W(12), ALU.mult))
        dv(lambda e: e.scalar_tensor_tensor(out=W(17), in0=W(9), scalar=-1.0, in1=W(12), op0=ALU.mult, op1=ALU.mult))

        def build_pow(tab, br, bi):
            tb = [tab.b, sw.b, tmpA.b, tmpB.b]
            S.op("dve", lambda e: e.tensor_copy(out=tab.t[:, 0, :, 0], in_=br), reads=tb, writes=[tab.b])
            S.op("dve", lambda e: e.tensor_copy(out=tab.t[:, 1, :, 0], in_=bi), reads=tb, writes=[tab.b])
            n = 1
            while n < T5:
                ar, ai = tab.t[:, 0, :, 0:n], tab.t[:, 1, :, 0:n]
                sr = tab.t[:, 0, :, n - 1:n].to_broadcast([128, 16, n])
                si = tab.t[:, 1, :, n - 1:n].to_broadcast([128, 16, n])
                tA, tB = tmpA.t[:, :, 0:n], tmpB.t[:, :, 0:n]
                orr, oi = tab.t[:, 0, :, n:2 * n], tab.t[:, 1, :, n:2 * n]
                ops = [(tA, ar, sr, ALU.mult), (tB, ai, si, ALU.mult), (orr, tA, tB, ALU.subtract),
                       (tA, ar, si, ALU.mult), (tB, ai, sr, ALU.mult), (oi, tA, tB, ALU.add)]
                for (o, a, b, op) in ops:
                    S.op("dve", lambda e, o=o, a=a, b=b, op=op: TT(e, o, a, b, op), reads=tb, writes=tb[0:1] + tb[2:4])
                n *= 2
        build_pow(Ptab, W(8), W(9))
        build_pow(Qtab, W(16), W(17))
        tq = [Qtab.b, sw.b, tmpA.b, tmpB.b]
        for half in range(2):
            hs = slice(half * (T5 // 2), (half + 1) * (T5 // 2))
            qr, qi = Qtab.t[:, 0, :, hs], Qtab.t[:, 1, :, hs]
            fr = W(14).unsqueeze(2).to_broadcast([128, 16, T5 // 2])
            fi = W(15).unsqueeze(2).to_broadcast([128, 16, T5 // 2])
            ops = [(tmpA.t[:], qr, fr, ALU.mult), (tmpB.t[:], qi, fi, ALU.mult), ("R", tmpA.t[:], tmpB.t[:], ALU.subtract),
                   (tmpA.t[:], qr, fi, ALU.mult), (tmpB.t[:], qi, fr, ALU.mult), (qi, tmpA.t[:], tmpB.t[:], ALU.add)]
            for (o, a, b, op) in ops:
                if isinstance(o, str):
                    o = qtmp.t[:, :, hs]
                S.op("dve", lambda e, o=o, a=a, b=b, op=op: TT(e, o, a, b, op), reads=tq + [qtmp.b], writes=tq + [qtmp.b])
            S.op("dve", lambda e, qr=qr, hs=hs: e.tensor_copy(out=qr, in_=qtmp.t[:, :, hs]), reads=[qtmp.b], writes=[Qtab.b])
        S.op("dve", lambda e: e.memset(mask32.t[:], 1.0), reads=[Qtab.b], writes=[mask32.b])
        S.op("dve", lambda e: e.memset(mask32.t[:, :, 0:1], 0.0), writes=[mask32.b])
        S.op("dve", lambda e: e.memset(mask4.t[:], 1.0), writes=[mask4.b])
        S.op("dve", lambda e: e.memset(mask4.t[:, :, 0:1], 0.0), writes=[mask4.b])
        S.op("dve", lambda e: e.memset(s5cr.t[:], 0.0), writes=[s5cr.b])
        dump("Ptab", Ptab.t[:].rearrange("p a s t -> p (a s t)"), [128, 2 * 16 * T5], [Ptab.b])
        dump("Qtab", Qtab.t[:].rearrange("p a s t -> p (a s t)"), [128, 2 * 16 * T5], [Qtab.b])

        LO_W = A.lo
        cs = A.alloc("cs", [17, D], F32)
        slabs = [A.alloc("adaslab%d" % i, [128, 8, 512], BF16) for i in range(3)]
        S.dma("sp", cs.t[:], cin, writes=[cs.b])
        S.op("act", lambda e: e.activation(out=cs.t[:], in_=cs.t[:], func=AF.Silu), reads=[cs.b], writes=[cs.b])
        for kt in range(8):
            S.op("pe", lambda e, kt=kt: e.transpose(PB[2].t[:, kt * 17:(kt + 1) * 17], cs.t[:, kt * 128:(kt + 1) * 128],
                                                    cst.t[0:17, C_ID:C_ID + 17]),
                 reads=[cs.b, cst.b], writes=[PB[2].b])
        S.op("act", lambda e: e.activation(out=scT.t[:].rearrange("p k s -> p (k s)"), in_=PB[2].t[:, 0:136], func=AF.Copy),
             reads=[PB[2].b], writes=[scT.b])
        wada_v = wada.rearrange("(kt p) n -> p kt n", p=128)
        wadaf_v = wadaf.rearrange("(kt p) n -> p kt n", p=128)

        def slab_src(i):
            if i < 12:
                return wada_v[:, :, i * 512:(i + 1) * 512]
            return wadaf_v[:, :, (i - 12) * 512:(i - 11) * 512]

        def load_slab(i):
            sl = slabs[i % 3]
            for kh in range(2):
                S.dma("pool", sl.t[:, 4 * kh:4 * kh + 4, :], slab_src(i)[:, 4 * kh:4 * kh + 4, :], writes=[sl.b])
        load_slab(0)
        load_slab(1)
        load_1a_weights()
        for i in range(4):
            if i + 2 < 4:
                load_slab(i + 2)
            sl = slabs[i % 3]
            pb = PB[i % 2]
            for fc in range(4):
                for kt in range(8):
                    S.op("pe", lambda e, fc=fc, kt=kt, sl=sl, pb=pb: e.matmul(
                        pb.t[:, fc * 17:(fc + 1) * 17], sl.t[:, kt, fc * 128:(fc + 1) * 128], scT.t[:, kt, :],
                        start=(kt == 0), stop=(kt == 7)), reads=[sl.b, scT.b], writes=[pb.b])
            S.op("dve", lambda e, i=i, pb=pb: e.tensor_tensor(
                out=mod.t[:, 4 * i:4 * i + 4, :], in0=pb.t[:, 0:68].rearrange("p (c s) -> p c s", s=17),
                in1=prm.t[:, P_BMOD + 4 * i:P_BMOD + 4 * i + 4].unsqueeze(2).to_broadcast([128, 4, 17]), op=ALU.add),
                reads=[pb.b, prm.b], writes=[mod.b])
        def make_amod(lst):
          for k, (sci, gi) in lst:
            S.op("dve", lambda e, k=k, sci=sci, gi=gi: e.scalar_tensor_tensor(
                out=amod.t[:, 8 * k:8 * k + 8, :], in0=chunkmod(sci), scalar=1.0,
                in1=prm.t[:, P_GAIN + 8 * gi:P_GAIN + 8 * gi + 8].unsqueeze(2).to_broadcast([128, 8, 17]),
                op0=ALU.add, op1=ALU.mult), reads=[mod.b, prm.b], writes=[amod.b])
        make_amod([(0, (1, 0))])
        dump("mod", mod.t[:].rearrange("p c s -> p (c s)"), [128, 64 * 17], [mod.b])
        S.barrier()
        S.emit()
        A.lo = LO_W

        MOD_SH1, MOD_G1, MOD_SH2, MOD_G2, MOD_SHF = 0, 2, 3, 5, 6

        def expand_mod(name, src_ap, srcbufs):
            t = A.alloc(name, [128, 8, 64], F32)
            S.op("dve", lambda e: e.tensor_copy(out=t.t[:].rearrange("p k (s b) -> p k s b", b=LS),
                                                in_=src_ap.unsqueeze(3).to_broadcast([128, 8, NS, LS])),
                 reads=srcbufs, writes=[t.b])
            return t

        LO_P1 = A.lo
        mixt = [A.alloc("mixt%d" % i, [128, 8, 256], BF16) for i in range(2)]
        mixdb = [Buf("mixd%d" % i) for i in range(9)]
        a1x = A.alloc("a1x", [128, 8, 64], F32)
        sh1x = A.alloc("sh1x", [128, 8, 64], F32)

        def fill_x(t, src_ap, srcbufs):
            S.op("dve", lambda e: e.tensor_copy(out=t.t[:].rearrange("p k (s b) -> p k s b", b=LS),
                                                in_=src_ap.unsqueeze(3).to_broadcast([128, 8, NS, LS])),
                 reads=srcbufs, writes=[t.b])
        adab = [TL(a1x.t[:].rearrange("p k t -> p (k t)").bitcast(BF16).rearrange("p (k c) -> p k c", c=128), "adab0"),
                TL(sh1x.t[:].rearrange("p k t -> p (k t)").bitcast(BF16).rearrange("p (k c) -> p k c", c=128), "adab1")]
        adab[0].b = a1x.b
        adab[1].b = sh1x.b
        ADA_CH = list(range(16, 64))

        def ada_load(ci):
            c = ADA_CH[ci]
            src = wada_v[:, :, c * 128:(c + 1) * 128] if c < 48 else wadaf_v[:, :, (c - 48) * 128:(c - 47) * 128]
            S.dma("pool", adab[ci % 2].t[:], src, writes=[adab[ci % 2].b])

        def ada_compute(ci):
            c = ADA_CH[ci]
            sl = adab[ci % 2]
            pb = next_pb()
            for kt in range(8):
                S.op("pe", lambda e, kt=kt: e.matmul(pb.t[:, 0:17], sl.t[:, kt, :], scT.t[:, kt, :], start=(kt == 0), stop=(kt == 7)),
                     reads=[sl.b, scT.b], writes=[pb.b])
            S.op("dve", lambda e: e.tensor_scalar(out=mod.t[:, c, :], in0=pb.t[:, 0:17], scalar1=prm.t[:, P_BMOD + c:P_BMOD + c + 1],
                                                  scalar2=None, op0=ALU.add), reads=[pb.b, prm.b], writes=[mod.b])
        ada_state = [0, 0]

        def ada_step():
            if ada_state[1] >= len(ADA_CH):
                return
            while ada_state[0] < min(len(ADA_CH), ada_state[1] + 2):
                ada_load(ada_state[0])
                ada_state[0] += 1
            ada_compute(ada_state[1])
            ada_state[1] += 1

        ckpt("setup0")
        NTM = 256
        xtm = A.alloc("xtm", [128, 2, D], F32)
        xn = A.alloc("xn", [128, 2, D], BF16)
        nstat = A.alloc("nstat", [128, 4], F32)
        uT = A.alloc("uT", [128, 8, NTM], BF16)
        xpad = A.alloc("xpad", [128, 8, NTM + 4], BF16)
        xtail = A.alloc("xtail", [128, 8, 64], F32)
        cvst = A.alloc("cvst", [128, 8, NS, 3], F32)
        S.dma("sp", cvst.t[:].rearrange("p c s k -> p (c s k)"), stconv_d, writes=[cvst.b])
        dgc = A.alloc("dgc", [128, 8, 4, 128], BF16)
        for ct_ in range(8):
            for k_ in range(4):
                S.op("act", lambda e, ct_=ct_, k_=k_: e.activation(
                    out=dgc.t[:, ct_, k_, :], in_=ident, func=AF.Copy,
                    scale=prm.t[:, P_CONV + 5 * ct_ + k_:P_CONV + 5 * ct_ + k_ + 1]), reads=[cst.b, prm.b], writes=[dgc.b])
        xsT = A.alloc("xsT", [128, 4, NTM], F32)
        BCT = A.alloc("BCT", [128, 4, NTM], BF16)
        szT = A.alloc("szT", [128, 4, NTM], BF16)
        u5Ts = [A.alloc("u5T%d" % i, [128, 4, NTM], BF16) for i in range(2)]
        dtT = A.alloc("dtT", [8, 2, NTM], F32)
        cacc = [A.alloc("cacc0", [128, NTM], F32)] * 2
        y5pre = A.alloc("y5pre", [128, 4, NTM], F32)
        g5 = A.alloc("g5", [128, 4, NTM], BF16)
        sgl = A.alloc("sgl", [128, NTM], F32)
        dtm_l = [A.alloc("dtm%d" % i, [128, 16], F32) for i in range(2)]
        acs_l = [A.alloc("acs%d" % i, [128, 8], F32) for i in range(2)]
        dec_l = [A.alloc("dec%d" % i, [128, 8], F32) for i in range(2)]
        dtdec_l = [A.alloc("dtdec%d" % i, [128, 8], F32) for i in range(2)]
        Xtm = A.alloc("Xtm", [128, 8, 64], BF16)
        Xdec = A.alloc("Xdec", [128, 8, 64], BF16)
        Btm = A.alloc("Btm", [128, 2, 128], BF16)
        big1 = A.alloc("big1", [128, 8, 128], F32)
        big2 = A.alloc("big2", [128, 8, 128], F32)
        MT = A.alloc("MT", [128, 8, 128], BF16)
        eA = A.alloc("eA", [128, 8, 128], F32)
        CdT = A.alloc("CdT", [128, 8, 128], BF16)
        ST = A.alloc("ST", [128, 8, 64], F32)
        STb = A.alloc("STb", [128, 8, 64], BF16)
        sts5 = alias("sts5", ST.t[:].rearrange("p h q -> p (h q)").rearrange("p (a s q) -> p a s q", a=2, s=16), ST.b)
        yg = A.alloc("yg", [128, 4, 128], F32)
        ysq = alias("ysq", big1.t[:, 4:8, :], big1.b)
        rsb = A.alloc("rsb", [128, 2, 128], F32)
        ysqb = A.alloc("ysqb", [128, 4, 128], BF16)
        onesb1 = A.alloc("onesb1", [128, 128], BF16)
        S.op("dve", lambda e: e.memset(onesb1.t[:], 1.0), writes=[onesb1.b])
        h0n = [alias("h0n0", xtm.t[:, 1, 0:512].rearrange("p (a n) -> p a n", n=128), xtm.sub(1)),
               alias("h0n1", xtm.t[:, 0, 0:512].rearrange("p (a n) -> p a n", n=128), xtm.sub(0))]
        h0T = [A.alloc("h0T%d" % i, [128, 8, 64], BF16) for i in range(2)]
        Bj = [A.alloc("Bj%d" % i, [128, 2, 128], BF16) for i in range(2)]
        hn = [alias("hn0", xtm.t[:, 1, 512:1024].rearrange("p (a n) -> p a n", n=128), xtm.sub(1)),
              alias("hn1", xtm.t[:, 0, 512:1024].rearrange("p (a n) -> p a n", n=128), xtm.sub(0))]
        decfm = A.alloc("decfm", [128, 4, 16], F32)
        dAx = alias("dAx", big1.t[:, 0:4, :].rearrange("p a (b c) -> p (a b) c", c=64), big1.b)
        s5g = [[A.alloc("s5g%d%d" % (j, i), [128, 512], F32) for i in range(2)] for j in range(2)]
        s5t34 = [A.alloc("s5t%d" % i, [128, 512], F32) for i in (2, 3)]
        s5vb = [A.alloc("s5vb%d" % i, [128, 512], F32) for i in range(2)]
        s5k = [0]
        s5h = [[A.alloc("s5h%d%d" % (j, i), [128, 512], BF16) for i in range(4)] for j in range(2)]
        s5CTn = A.alloc("s5CTn", [128, 16, 32], BF16)
        s5c = A.alloc("s5c", [128, 4, 16], F32)
        busd = [[A.alloc("bus%d%d" % (j, i), [128, 512], F32) for i in range(2)] for j in range(2)]
        dg5 = A.alloc("dg5", [128, 4, 128], BF16)
        for q_ in range(4):
            S.op("act", lambda e, q_=q_: e.activation(out=dg5.t[:, q_, :], in_=ident, func=AF.Copy,
                                                      scale=prm.t[:, P_S5M + q_:P_S5M + q_ + 1]),
                 reads=[cst.b, prm.b], writes=[dg5.b])
        S.op("dve", lambda e: e.tensor_scalar(out=s5CT.t[:, 1], in0=s5CT.t[:, 1], scalar1=-1.0, scalar2=None, op0=ALU.mult),
             reads=[s5CT.b], writes=[s5CT.b])
        S.op("dve", lambda e: e.tensor_scalar(out=s5CTn.t[:], in0=s5CT.t[:, 0], scalar1=-1.0, scalar2=None, op0=ALU.mult),
             reads=[s5CT.b], writes=[s5CTn.b])
        print("arena after p1a allocs: lo=%d hi=%d (words)" % (A.lo, A.hi))

        S.op("dve", lambda e: e.memset(xpad.t[:, :, 0:3], 0.0), writes=[xpad.b])
        S.op("dve", lambda e: e.memset(ST.t[:], 0.0), writes=[ST.b])
        S.op("dve", lambda e: e.memset(STb.t[:], 0.0), writes=[STb.b])

        import os as _os3
        ENG_OUTROT = _os3.environ.get("K_OUTROT", "dve")
        ENG_ADDS = _os3.environ.get("K_ADDS", "dve")
        TILES_A = [(i * 256, 256, False) for i in range(8)] + [(SEQ, 64, True)]

        def load_x(ti):
            t0, NT, is_s = TILES_A[ti]
            for blk in range((NT + 127) // 128):
                rows = min(128, NT - blk * 128)
                S.dma("sp", xtm.t[0:rows, blk, :], xin[t0 + blk * 128:t0 + blk * 128 + rows, :], writes=[xtm.sub(blk)])

        a1 = lambda kt: amod.t[:, kt, 0:1]
        sh1 = lambda kt: mod.t[:, 8 * MOD_SH1 + kt, 0:1]
        cw = lambda ct, k: prm.t[:, P_CONV + 5 * ct + k:P_CONV + 5 * ct + k + 1]
        IN_CHUNKS = [("dt", 0, 1536, 8)] + [("z", i, i * 128, 128) for i in range(4)] + \
                    [("xbc", i, 512 + i * 128, 128) for i in range(8)] + [("u5", i, 1544 + i * 128, 128) for i in range(4)]

        load_x(0)
        pbi = [0]

        def next_pb():
            pbi[0] ^= 1
            return PB[pbi[0]]

        ckpt("pre")
        def chain1(ti):
            t0, NT, is_s = TILES_A[ti]
            u5T = u5Ts[ti % 2]
            nblk = (NT + 127) // 128
            T = 128 if not is_s else 64
            tri = cst.t[0:T, C_TRI:C_TRI + T] if not is_s else cst.t[0:T, C_TRI64:C_TRI64 + T]
            neg = cst.t[0:T, C_NEG:C_NEG + T] if not is_s else cst.t[0:T, C_NEG64:C_NEG64 + T]
            sego = onesf.t[0:T, 0:T] if not is_s else cst.t[0:T, C_SEG64:C_SEG64 + T]
            segi = cst.t[0:64, C_SEGI:C_SEGI + 16]

            def dt_prep(ck):
                c0 = ck * T
                cs_ = slice(c0, c0 + T)
                dtm, acs, dec, dtdec = dtm_l[ck], acs_l[ck], dec_l[ck], dtdec_l[ck]
                pc = 0 if ck == 0 else 480
                S.op("pe", lambda e: e.transpose(PB[4].t[0:T, pc:pc + 8], dtT.t[:, 0, cs_], cst.t[0:8, C_ID:C_ID + 8]),
                     reads=[dtT.b, cst.b], writes=[PB[4].sub("sm")])
                S.op("pe", lambda e: e.transpose(PB[4].t[0:T, pc + 8:pc + 16], dtT.t[:, 1, cs_], cst.t[0:8, C_ID:C_ID + 8]),
                     reads=[dtT.b, cst.b], writes=[PB[4].sub("sm")])
                S.op("act", lambda e: e.activation(out=dtm.t[0:T, :], in_=PB[4].t[0:T, pc:pc + 16], func=AF.Copy),
                     reads=[PB[4].sub("sm")], writes=[dtm.b])
                S.op("pe", lambda e: e.matmul(PB[4].t[0:T, pc + 16:pc + 24], tri, dtm.t[0:T, 8:16], start=True, stop=True),
                     reads=[dtm.b, cst.b], writes=[PB[4].sub("sm")])
                S.op("pe", lambda e: e.matmul(PB[4].t[0:T, pc + 24:pc + 32], sego, dtm.t[0:T, 8:16], start=True, stop=True),
                     reads=[dtm.b, cst.b, onesf.b], writes=[PB[4].sub("sm")])
                S.op("act", lambda e: e.activation(out=acs.t[0:T, :], in_=PB[4].t[0:T, pc + 16:pc + 24], func=AF.Copy),
                     reads=[PB[4].sub("sm")], writes=[acs.b])
                S.op("dve", lambda e: TT(e, dec.t[0:T, :], PB[4].t[0:T, pc + 24:pc + 32], acs.t[0:T, :], ALU.subtract),
                     reads=[PB[4].sub("sm"), acs.b], writes=[dec.b])
                S.op("act", lambda e: e.activation(out=dec.t[0:T, :], in_=dec.t[0:T, :], func=AF.Exp), reads=[dec.b], writes=[dec.b])
                S.op("dve", lambda e: TT(e, dtdec.t[0:T, :], dtm.t[0:T, 0:8], dec.t[0:T, :], ALU.mult),
                     reads=[dtm.b, dec.b], writes=[dtdec.b])
            for blk in range(nblk):
                rows = min(128, NT - blk * 128)
                xb = xtm.sub(blk)
                S.op("act", lambda e, blk=blk, rows=rows: e.activation(
                    out=xn.t[0:rows, blk, :], in_=xtm.t[0:rows, blk, :], func=AF.Square, accum_out=nstat.t[0:rows, blk:blk + 1]),
                    reads=[xb], writes=[xn.sub(blk), nstat.sub(blk)])
                S.op("act", lambda e, blk=blk, rows=rows: e.activation(
                    out=nstat.t[0:rows, 2 + blk:3 + blk], in_=nstat.t[0:rows, blk:blk + 1], func=AF.Ln, scale=1.0 / D, bias=EPS),
                    reads=[nstat.sub(blk)], writes=[nstat.sub(blk)])
                S.op("act", lambda e, blk=blk, rows=rows: e.activation(out=nstat.t[0:rows, 2 + blk:3 + blk],
                                                                        in_=nstat.t[0:rows, 2 + blk:3 + blk], func=AF.Exp, scale=-0.5),
                     reads=[nstat.sub(blk)], writes=[nstat.sub(blk)])
                S.op("act", lambda e, blk=blk, rows=rows: e.activation(
                    out=xn.t[0:rows, blk, :], in_=xtm.t[0:rows, blk, :], func=AF.Copy, scale=nstat.t[0:rows, 2 + blk:3 + blk]),
                    reads=[xb, nstat.sub(blk)], writes=[xn.sub(blk)])
            ckpt("Aa%d" % ti)
            if ti + 1 < len(TILES_A):
                load_x(ti + 1)
            ckpt("Ab%d" % ti)
            for kt in range(8):
                xb_ = 2 + (kt % 2)
                pslot = PB[xb_].b
                for blk in range(nblk):
                    rows = min(128, NT - blk * 128)
                    S.op("pe", lambda e, kt=kt, blk=blk, rows=rows: e.transpose(
                        pbf(xb_)[:, blk * 128:blk * 128 + rows],
                        xn.t[0:rows, blk, kt * 128:(kt + 1) * 128], identb.t[0:rows, 0:rows]),
                        reads=[xn.sub(blk), identb.b], writes=[pslot])
                src = pbf(xb_)[:, 0:NT]
                if not is_s:
                    S.op("act", lambda e, kt=kt, src=src: e.activation(out=uT.t[:, kt, 0:NT], in_=src, func=AF.Identity,
                                                                       scale=a1(kt), bias=sh1(kt)),
                         reads=[pslot, amod.b, mod.b], writes=[uT.sub(kt)])
                else:
                    S.op("dve", lambda e, kt=kt, src=src: TT(e, cacc[0].t[:, 0:NT], src, a1x.t[:, kt, :], ALU.mult),
                         reads=[pslot, a1x.b], writes=[cacc[0].b])
                    S.op("dve", lambda e, kt=kt: TT(e, uT.t[:, kt, 0:NT], cacc[0].t[:, 0:NT], sh1x.t[:, kt, :], ALU.add),
                         reads=[cacc[0].b, sh1x.b], writes=[uT.sub(kt)])
            ckpt("A%d" % ti)
            if ti == 0:
                dump("uT", uT.t[:].rearrange("p k t -> p (k t)"), [128, 8 * NTM], uT.allb())

            yield
            if is_s:
                xps = xpad.t[:, :, 0:NS * 7].rearrange("p c (s k) -> p c s k", k=7)
                S.op("act", lambda e: e.activation(out=xps[:, :, :, 0:3], in_=cvst.t[:], func=AF.Copy), reads=[cvst.b], writes=[xpad.b])
            for (kind, i, c0, M) in IN_CHUNKS:
                yield
                pb = next_pb()
                for kt in range(8):
                    S.op("pe", lambda e, kt=kt, c0=c0, M=M, pb=pb: e.matmul(
                        pb.t[0:M, 0:NT], win_sb.t[:, kt, c0:c0 + M], uT.t[:, kt, 0:NT], start=(kt == 0), stop=(kt == 7)),
                        reads=[win_sb.sub(kt // 2), uT.sub(kt)], writes=[pb.b])
                if kind == "z":
                    S.op("act", lambda e, i=i, pb=pb: e.activation(out=szT.t[:, i, 0:NT], in_=pb.t[:, 0:NT], func=AF.Silu),
                         reads=[pb.b], writes=[szT.b])
                elif kind == "xbc":
                    if not is_s:
                        S.op("act", lambda e, i=i, pb=pb: e.activation(out=xpad.t[:, i, 3:3 + NT], in_=pb.t[:, 0:NT], func=AF.Copy),
                             reads=[pb.b], writes=[xpad.b])
                        if ti == 7:
                            S.op("act", lambda e, i=i, pb=pb: e.activation(out=xtail.t[:, i, 0:3], in_=pb.t[:, NT - 3:NT], func=AF.Copy),
                                 reads=[pb.b], writes=[xtail.b])
                    else:
                        S.op("act", lambda e, i=i, pb=pb: e.activation(
                            out=xps[:, i, :, 3:7], in_=pb.t[:, 0:NT].rearrange("p (s k) -> p s k", k=LS), func=AF.Copy),
                            reads=[pb.b], writes=[xpad.b])
                        S.op("act", lambda e, i=i, pb=pb: e.activation(out=xtail.t[:, i, 0:NT], in_=pb.t[:, 0:NT], func=AF.Copy),
                             reads=[pb.b], writes=[xtail.b])
                elif kind == "dt":
                    S.op("act", lambda e, pb=pb: e.activation(out=dtT.t[:, 1, 0:NT], in_=pb.t[0:8, 0:NT], func=AF.Exp,
                                                              bias=ssd8.t[:, 0:1]), reads=[pb.b, ssd8.b], writes=[dtT.b])
                    S.op("act", lambda e: e.activation(out=dtT.t[:, 0, 0:NT], in_=dtT.t[:, 1, 0:NT], func=AF.Ln, bias=1.0),
                         reads=[dtT.b], writes=[dtT.b])
                    S.op("dve", lambda e: e.tensor_scalar(out=dtT.t[:, 1, 0:NT], in0=dtT.t[:, 0, 0:NT], scalar1=ssd8.t[:, 1:2],
                                                          scalar2=None, op0=ALU.mult), reads=[dtT.b, ssd8.b], writes=[dtT.b])
                    for ck_ in range(NT // T):
                        yield
                        dt_prep(ck_)
                else:
                    S.op("act", lambda e, i=i, pb=pb: e.activation(out=u5T.t[:, i, 0:NT], in_=pb.t[:, 0:NT], func=AF.Copy),
                         reads=[pb.b], writes=[u5T.b])

            ckpt("B%d" % ti)
            for ct in range(8):
                yield
                pb = next_pb()
                if not is_s:
                    xin_k = lambda k, ct=ct: xpad.t[:, ct, k:k + NT]
                    pbv = pb.t[:, 0:NT]
                    dst = xsT.t[:, ct, 0:NT] if ct < 4 else BCT.t[:, ct - 4, 0:NT]
                else:
                    xin_k = lambda k, ct=ct: xps[:, ct, :, k:k + LS]
                    pbv = pb.t[:, 0:NT].rearrange("p (s k) -> p s k", k=LS)
                    dst = (xsT.t[:, ct, 0:NT] if ct < 4 else BCT.t[:, ct - 4, 0:NT]).rearrange("p (s k) -> p s k", k=LS)
                for k in range(4):
                    S.op("pe", lambda e, k=k: e.matmul(pbv, dgc.t[:, ct, k, :], xin_k(k), start=(k == 0), stop=(k == 3)),
                         reads=[dgc.b, xpad.b], writes=[pb.b])
                S.op("act", lambda e: e.activation(out=dst, in_=pbv, func=AF.Silu, bias=cw(ct, 4)),
                     reads=[pb.b, prm.b], writes=[xsT.b if ct < 4 else BCT.b])
            ocv = o_conv.rearrange("p (c s k) -> p c s k", s=17, k=3)
            if is_s:
                S.op("act", lambda e: e.activation(out=cvst.t[:], in_=xtail.t[:].rearrange("p c (s k) -> p c s k", k=LS)[:, :, :, 1:4],
                                                   func=AF.Copy), reads=[xtail.b], writes=[cvst.b])
                S.dma("sp", ocv[:, :, 1:17, :], cvst.t[:], reads=[cvst.b], buf=cvst.b)
                outbufs.append(cvst.b)
            elif ti == 7:
                S.dma("sp", ocv[:, :, 0, :], xtail.t[:, :, 0:3], reads=[xtail.b], buf=xtail.b)
            if not is_s:
                S.op("dve", lambda e: e.tensor_copy(out=xpad.t[:, :, 0:3], in_=xpad.t[:, :, NT:NT + 3]),
                     reads=[xpad.b], writes=[xpad.b])
            if is_s:
                dump("xsS", xsT.t[:, :, 0:64], [128, 4, 64], [xsT.b])
                dump("ygS", yg.t[:, :, 0:64], [128, 4, 64], [yg.b])
            if ti == 0:
                dump("xsT", xsT.t[:].rearrange("p k t -> p (k t)"), [128, 4 * NTM], [xsT.b])
                dump("dtT", dtT.t[:].rearrange("p k t -> p (k t)"), [8, 2 * NTM], [dtT.b])

            ckpt("C%d" % ti)
            for ck in range(NT // T):
                c0 = ck * T
                cs_ = slice(c0, c0 + T)
                dtm, acs, dec, dtdec = dtm_l[ck], acs_l[ck], dec_l[ck], dtdec_l[ck]
                yield
                for pr in range(4):
                    S.op("pe", lambda e, pr=pr, cs_=cs_: e.transpose(PB[3].t[0:T, pr * 128:(pr + 1) * 128], xsT.t[:, pr, cs_], ident),
                         reads=[xsT.b, cst.b], writes=[PB[3].b])
                pxs = PB[3].t[0:T, :].rearrange("p (h q) -> p h q", q=64)
                for h in range(8):
                    S.op("act", lambda e, h=h: e.activation(out=Xtm.t[0:T, h, :], in_=pxs[:, h, :], func=AF.Copy, scale=dtm.t[0:T, h:h + 1]),
                         reads=[PB[3].b, dtm.b], writes=[Xtm.b])
                    S.op("act", lambda e, h=h: e.activation(out=Xdec.t[0:T, h, :], in_=pxs[:, h, :], func=AF.Copy, scale=dtdec.t[0:T, h:h + 1]),
                         reads=[PB[3].b, dtdec.b], writes=[Xdec.b])
                for g in range(2):
                    S.op("pe", lambda e, g=g, cs_=cs_: e.transpose(pbf(2)[0:T, g * 128:(g + 1) * 128], BCT.t[:, g, cs_], identb.t[:]),
                         reads=[BCT.b, identb.b], writes=[PB[2].sub(0)])
                S.op("act", lambda e: e.activation(out=Btm.t[0:T].rearrange("p g n -> p (g n)"), in_=pbf(2)[0:T, 0:256], func=AF.Copy),
                     reads=[PB[2].sub(0)], writes=[Btm.b])
                yield
                S.op("dve", lambda e: TT(e, big1.t[0:T, :, 0:T], tri.unsqueeze(1).to_broadcast([T, 8, T]),
                                         dtm.t[0:T, 8:16].unsqueeze(2).to_broadcast([T, 8, T]), ALU.mult),
                     reads=[cst.b, dtm.b], writes=[big1.b])
                for half in range(2):
                    S.op("pe", lambda e, half=half: e.matmul(
                        PB[3].t[:, 0:4 * T].rearrange("p (h l) -> p h l", l=T), onesf.t[0:T, :],
                        big1.t[0:T, 4 * half:4 * half + 4, 0:T], start=True, stop=True),
                        reads=[big1.b, onesf.b], writes=[PB[3].b])
                    yield
                    for h in range(4 * half, 4 * half + 4):
                        S.op("dve", lambda e, h=h: e.scalar_tensor_tensor(
                            out=big2.t[0:T, h, 0:T], in0=PB[3].t[0:T, (h % 4) * T:(h % 4 + 1) * T], scalar=acs.t[0:T, h:h + 1],
                            in1=neg, op0=ALU.subtract, op1=ALU.min), reads=[PB[3].b, acs.b, cst.b], writes=[big2.b])
                    S.op("act", lambda e, half=half: e.activation(
                        out=eA.t[:, 4 * half:4 * half + 4, 0:T], in_=PB[3].t[:, 0:4 * T].rearrange("p (h l) -> p h l", l=T),
                        func=AF.Exp), reads=[PB[3].b], writes=[eA.b])
                    yield
                S.op("act", lambda e: e.activation(out=big2.t[0:T, :, 0:T], in_=big2.t[0:T, :, 0:T], func=AF.Exp),
                     reads=[big2.b], writes=[big2.b])
                yield
                for g in range(2):
                    S.op("pe", lambda e, g=g, cs_=cs_: e.matmul(PB[4].t[0:T, 32 + g * 128:32 + g * 128 + T], BCT.t[:, g, cs_],
                                                                 BCT.t[:, 2 + g, cs_], start=True, stop=True),
                         reads=[BCT.b], writes=[PB[4].sub("cb")])
                cbv = PB[4].t[0:T, 32:288].rearrange("p (g l) -> p g l", l=128)[:, :, 0:T]
                S.op("dve", lambda e: TT(e, MT.t[0:T, :, 0:T].rearrange("p (g h) l -> p g h l", h=4),
                                         cbv.unsqueeze(2).to_broadcast([T, 2, 4, T]),
                                         big2.t[0:T, :, 0:T].rearrange("p (g h) l -> p g h l", h=4), ALU.mult),
                     reads=[PB[4].sub("cb"), big2.b], writes=[MT.b])
                yield
                S.op("pool", lambda e, cs_=cs_: TT(e, CdT.t[:, :, 0:T].rearrange("p (g h) l -> p g h l", h=4),
                                                   BCT.t[:, 2:4, cs_].unsqueeze(2).to_broadcast([128, 2, 4, T]),
                                                   eA.t[:, :, 0:T].rearrange("p (g h) l -> p g h l", h=4), ALU.mult),
                     reads=[BCT.b, eA.b], writes=[CdT.b])
                yield
                ypb = PB[7]
                if is_s:
                    S.op("dve", lambda e: e.tensor_copy(out=dAx.t[0:T], in_=dtm.t[0:T, 8:16].unsqueeze(2).to_broadcast([T, 8, 64])),
                         reads=[dtm.b], writes=[dAx.b])
                    for pr in range(4):
                        S.op("pe", lambda e, pr=pr: e.matmul(PB[4].t[:, 288 + pr * 16:288 + (pr + 1) * 16],
                                                             dAx.t[0:T, 2 * pr:2 * pr + 2, :], segi, start=True, stop=True),
                             reads=[dAx.b, cst.b], writes=[PB[4].sub("dec")])
                    S.op("act", lambda e: e.activation(out=decfm.t[:].rearrange("p a s -> p (a s)"), in_=PB[4].t[:, 288:352], func=AF.Exp),
                         reads=[PB[4].sub("dec")], writes=[decfm.b])
                    stv = stssd_d.rearrange("j (pr hl) p n -> j (hl p) pr n", hl=2)
                    osv = o_ssds.rearrange("j (pr hl) p n -> j (hl p) pr n", hl=2)
                    S.dma("act", h0n[0].t[:], stv[0], writes=[h0n[0].b])
                    for j in range(NS):
                        yield
                        jj = j % 2
                        if j + 1 < NS:
                            S.dma("act", h0n[1 - jj].t[:], stv[j + 1], writes=[h0n[1 - jj].b])
                        pbt = PB[jj]
                        for pr in range(4):
                            S.op("pe", lambda e, pr=pr, jj=jj, pbt=pbt: e.transpose(pbt.t[:, pr * 128:(pr + 1) * 128], h0n[jj].t[:, pr, :], ident),
                                 reads=[h0n[jj].b, cst.b], writes=[pbt.b])
                        S.op("act", lambda e, jj=jj, pbt=pbt: e.activation(out=h0T[jj].t[:].rearrange("p h q -> p (h q)"), in_=pbt.t[:, :], func=AF.Copy),
                             reads=[pbt.b], writes=[h0T[jj].b])
                        for h in range(8):
                            pr, hl = h // 2, h % 2
                            S.op("pe", lambda e, h=h, pr=pr, hl=hl, jj=jj, j=j: e.matmul(
                                ypb.t[64 * hl:64 * hl + 64, pr * T + LS * j:pr * T + LS * j + LS], h0T[jj].t[:, h, :],
                                CdT.t[:, h, LS * j:LS * j + LS], start=(j == 0 and pr == 0), stop=False, skip_group_check=True),
                                reads=[h0T[jj].b, CdT.b], writes=[ypb.b])
                        S.op("dve", lambda e, jj=jj, j=j: e.tensor_scalar(out=Bj[jj].t[0:T], in0=Btm.t[0:T], scalar1=segi[:, j:j + 1],
                                                                          scalar2=None, op0=ALU.mult),
                             reads=[Btm.b, cst.b], writes=[Bj[jj].b])
                        pby = PB[3]
                        for pr in range(4):
                            S.op("pe", lambda e, pr=pr, jj=jj, pby=pby: e.matmul(
                                pby.t[:, pr * 128:(pr + 1) * 128], Xdec.t[0:T, 2 * pr:2 * pr + 2, :], Bj[jj].t[0:T, pr // 2, :],
                                start=True, stop=True), reads=[Xdec.b, Bj[jj].b], writes=[pby.b])
                        S.op("dve", lambda e, jj=jj, j=j: TT(e, hn[jj].t[:], h0n[jj].t[:],
                                                             decfm.t[:, :, j:j + 1].to_broadcast([128, 4, 128]), ALU.mult),
                             reads=[h0n[jj].b, decfm.b], writes=[hn[jj].b])
                        S.op("dve", lambda e, jj=jj, pby=pby: TT(e, hn[jj].t[:], hn[jj].t[:],
                                                                 pby.t[:, :].rearrange("p (a n) -> p a n", n=128), ALU.add),
                             reads=[hn[jj].b, pby.b], writes=[hn[jj].b])
                        S.dma("sp", osv[j], hn[jj].t[:], reads=[hn[jj].b], buf=hn[jj].b)
                    outbufs.extend([hn[0].b, hn[1].b])
                for h in range(8):
                    pr, hl = h // 2, h % 2
                    out = ypb.t[64 * hl:64 * hl + 64, pr * T:(pr + 1) * T]
                    S.op("pe", lambda e, h=h, out=out, pr=pr: e.matmul(out, Xtm.t[0:T, h, :], MT.t[0:T, h, 0:T],
                                                                       start=(pr == 0 and not is_s), stop=is_s, skip_group_check=True),
                         reads=[Xtm.b, MT.b], writes=[ypb.b])
                    if not is_s:
                        S.op("pe", lambda e, h=h, out=out: e.matmul(out, STb.t[:, h, :], CdT.t[:, h, 0:T], start=False, stop=True,
                                                                    skip_group_check=True),
                             reads=[STb.b, CdT.b], writes=[ypb.b])
                yield
                for pr in range(4):
                    S.op("dve", lambda e, pr=pr, cs_=cs_: e.scalar_tensor_tensor(
                        out=yg.t[:, pr, 0:T], in0=xsT.t[:, pr, cs_], scalar=prm.t[:, P_SSDFM + pr:P_SSDFM + pr + 1],
                        in1=ypb.t[:, pr * T:(pr + 1) * T], op0=ALU.mult, op1=ALU.add),
                        reads=[xsT.b, prm.b, ypb.b], writes=[yg.b])
                S.op("dve", lambda e, cs_=cs_: TT(e, yg.t[:, :, 0:T], yg.t[:, :, 0:T], szT.t[:, :, cs_], ALU.mult),
                     reads=[yg.b, szT.b], writes=[yg.b])
                S.op("dve", lambda e: TT(e, ysqb.t[:, :, 0:T], yg.t[:, :, 0:T], yg.t[:, :, 0:T], ALU.mult),
                     reads=[yg.b], writes=[ysqb.b])
                for g in range(2):
                    for k in range(2):
                        S.op("pe", lambda e, g=g, k=k: e.matmul(PB[3].t[:, g * T:(g + 1) * T], onesb1.t[:], ysqb.t[:, 2 * g + k, 0:T],
                                                                start=(k == 0), stop=(k == 1)),
                             reads=[onesb1.b, ysqb.b], writes=[PB[3].b])
                S.op("act", lambda e: e.activation(out=rsb.t[:, :, 0:T], in_=PB[3].t[:, 0:2 * T].rearrange("p (g l) -> p g l", l=T),
                                                   func=AF.Ln, scale=1.0 / 256, bias=EPS), reads=[PB[3].b], writes=[rsb.b])
                S.op("act", lambda e: e.activation(out=rsb.t[:, :, 0:T], in_=rsb.t[:, :, 0:T], func=AF.Exp, scale=-0.5),
                     reads=[rsb.b], writes=[rsb.b])
                for pr in range(4):
                    S.op("dve", lambda e, pr=pr: e.scalar_tensor_tensor(
                        out=mixt[ti % 2].t[:, pr, c0:c0 + T], in0=yg.t[:, pr, 0:T],
                        scalar=prm.t[:, P_SSDFM + 4 + pr:P_SSDFM + 5 + pr], in1=rsb.t[:, pr // 2, 0:T], op0=ALU.mult, op1=ALU.mult),
                        reads=[yg.b, prm.b, rsb.b], writes=[mixt[ti % 2].sub("ssd")])
                yield
                if not is_s:
                    for g in range(2):
                        S.op("pe", lambda e, g=g: e.matmul(PB[6].t[:, g * 256:(g + 1) * 256], Btm.t[0:T, g, :],
                                                           Xdec.t[0:T, 4 * g:4 * g + 4, :], start=True, stop=True),
                             reads=[Btm.b, Xdec.b], writes=[PB[6].b])
                    S.op("dve", lambda e: TT(e, ST.t[:], ST.t[:], eA.t[:, :, T - 1:T].to_broadcast([128, 8, 64]), ALU.mult),
                         reads=[ST.b, eA.b], writes=[ST.b])
                    S.op("dve", lambda e: TT(e, ST.t[:], ST.t[:], PB[6].t[:, :].rearrange("p (h q) -> p h q", q=64), ALU.add),
                         reads=[ST.b, PB[6].b], writes=[ST.b])
                    S.op("act", lambda e: e.activation(out=STb.t[:], in_=ST.t[:], func=AF.Copy), reads=[ST.b], writes=[STb.b])
            if ti == 7:
                S.dma("sp", o_ssdp, ST.t[:].rearrange("p h q -> p (h q)"), reads=[ST.b], buf=ST.b)
                outbufs.append(ST.b)

            ckpt("D%d" % ti)
            yield

        def chain2(ti):
            t0, NT, is_s = TILES_A[ti]
            u5T = u5Ts[ti % 2]
            if is_s:
                S.dma("sp", sts5.t[:].rearrange("p a s q -> p (a s q)"), sts5_d, writes=[sts5.b])
            if not is_s:
                groups = [(list(range(16)), k * T5, T5) for k in range(NT // T5)]
            else:
                groups = [(list(range(8)), 0, 64), (list(range(8, 16)), 0, 64)]
            def emit_bu(g_):
                slist_, tk0_, ntok_ = groups[g_]
                bus = busd[g_ % 2]
                for part, pb in ((0, PB[5]), (1, PB[6])):
                    for idx, s in enumerate(slist_):
                        S.op("pe", lambda e, part=part, pb=pb, idx=idx, s=s: e.matmul(
                            pb.t[:, idx * ntok_:(idx + 1) * ntok_], s5BT.t[:, part, s, :], u5T.t[:, s // 4, tk0_:tk0_ + ntok_],
                            start=True, stop=True), reads=[s5BT.b, u5T.b], writes=[pb.b])
                S.op("act", lambda e: e.activation(out=bus[0].t[:], in_=PB[5].t[:, :], func=AF.Copy), reads=[PB[5].b], writes=[bus[0].b])
                S.op("act", lambda e: e.activation(out=bus[1].t[:], in_=PB[6].t[:, :], func=AF.Copy), reads=[PB[6].b], writes=[bus[1].b])
            def views(g_):
                slist_, tk0_, ntok_ = groups[g_]
                s0_ = slist_[0]
                if not is_s:
                    V3 = lambda ap: ap.rearrange("p (s t) -> p s t", t=T5)
                    QR, QI = Qtab.t[:, 0], Qtab.t[:, 1]
                    PR_, PI_ = Ptab.t[:, 0], Ptab.t[:, 1]
                    msk = mask32.t[:].rearrange("p s t -> p (s t)")
                    first = lambda ap: V3(ap)[:, :, 0]
                    cin_r, cin_i = s5cr.t[:, 0, :], s5cr.t[:, 1, :]
                else:
                    V3 = lambda ap: ap.rearrange("p (s q b) -> p s q b", q=NS, b=LS)
                    bc = lambda ap: ap.unsqueeze(2).to_broadcast([128, 8, NS, LS])
                    QR, QI = bc(Qtab.t[:, 0, s0_:s0_ + 8, 0:LS]), bc(Qtab.t[:, 1, s0_:s0_ + 8, 0:LS])
                    PR_, PI_ = bc(Ptab.t[:, 0, s0_:s0_ + 8, 0:LS]), bc(Ptab.t[:, 1, s0_:s0_ + 8, 0:LS])
                    msk = mask4.t[:].rearrange("p s t -> p (s t)")
                    first = lambda ap: V3(ap)[:, :, :, 0]
                    cin_r, cin_i = sts5.t[:, 0, s0_:s0_ + 8, :], sts5.t[:, 1, s0_:s0_ + 8, :]
                return V3, QR, QI, PR_, PI_, msk, first, cin_r, cin_i
            vsets = [[s5v[0], s5v[1]], [s5vb[0], s5vb[1]]]

            def mults_adds(g_):
                V3, QR, QI, PR_, PI_, msk, first, cin_r, cin_i = views(g_)
                bus = busd[g_ % 2]
                br, bi = V3(bus[0].t[:]), V3(bus[1].t[:])
                t1, t2, t3, t4 = s5t[0], s5t[1], s5t34[0], s5t34[1]
                vr, vi = vsets[g_ % 2]
                tb = [Qtab.b]
                for (o, a, b_, rd) in ((t1, QR, br, bus[0].b), (t2, QI, bi, bus[1].b), (t3, QR, bi, bus[1].b), (t4, QI, br, bus[0].b)):
                    S.op("dve", lambda e, o=o, a=a, b_=b_: TT(e, V3(o.t[:]), a, b_, ALU.mult), reads=tb + [rd], writes=[o.b])
                S.op(ENG_ADDS, lambda e: TT(e, vr.t[:], t1.t[:], t2.t[:], ALU.subtract), reads=[t1.b, t2.b], writes=[vr.b])
                S.op(ENG_ADDS, lambda e: TT(e, vi.t[:], t3.t[:], t4.t[:], ALU.add), reads=[t3.b, t4.b], writes=[vi.b])
            emit_bu(0)
            if len(groups) > 1:
                emit_bu(1)
            mults_adds(0)
            pend_y5 = [None]
            for gi_, (slist, tk0, ntok) in enumerate(groups):
                yield
                ns = len(slist)
                s0 = slist[0]
                V3, QR, QI, PR_, PI_, msk, first, cin_r, cin_i = views(gi_)
                vr, vi = vsets[gi_ % 2]
                if gi_ + 1 < len(groups):
                    mults_adds(gi_ + 1)
                    yield
                if gi_ + 2 < len(groups):
                    emit_bu(gi_ + 2)
                S.op("dve", lambda e: TT(e, first(vr.t[:]), first(vr.t[:]), cin_r, ALU.add), reads=[vr.b, s5cr.b, sts5.b], writes=[vr.b])
                S.op("dve", lambda e: TT(e, first(vi.t[:]), first(vi.t[:]), cin_i, ALU.add), reads=[vi.b, s5cr.b, sts5.b], writes=[vi.b])
                yield
                s5k[0] ^= 1
                gr, gi2 = s5g[s5k[0]][0], s5g[s5k[0]][1]
                S.op("dve", lambda e: e.tensor_tensor_scan(out=gr.t[:], data0=msk, data1=vr.t[:], initial=0.0, op0=ALU.mult, op1=ALU.add),
                     reads=[vr.b, mask32.b, mask4.b], writes=[gr.b])
                S.op("dve", lambda e: e.tensor_tensor_scan(out=gi2.t[:], data0=msk, data1=vi.t[:], initial=0.0, op0=ALU.mult, op1=ALU.add),
                     reads=[vi.b, mask32.b, mask4.b], writes=[gi2.b])
                yield
                hp = s5h[gi_ % 2]
                hr, hi = hp, hp
                for (o, a, b_) in ((hp[0], PR_, gr), (hp[1], PI_, gi2), (hp[2], PR_, gi2), (hp[3], PI_, gr)):
                    S.op(ENG_OUTROT, lambda e, o=o, a=a, b_=b_: TT(e, V3(o.t[:]), a, V3(b_.t[:]), ALU.mult),
                         reads=[Ptab.b, b_.b], writes=[o.b])
                yield
                if not is_s:
                    glr, gli = V3(gr.t[:])[:, :, T5 - 1], V3(gi2.t[:])[:, :, T5 - 1]
                    plr, pli = Ptab.t[:, 0, :, T5 - 1], Ptab.t[:, 1, :, T5 - 1]
                    c_ = lambda i: s5c.t[:, i, :]
                    outr, outi = s5cr.t[:, 0, :], s5cr.t[:, 1, :]
                else:
                    glr, gli = V3(gr.t[:])[:, :, :, LS - 1], V3(gi2.t[:])[:, :, :, LS - 1]
                    plr = Ptab.t[:, 0, s0:s0 + 8, LS - 1:LS].to_broadcast([128, 8, NS])
                    pli = Ptab.t[:, 1, s0:s0 + 8, LS - 1:LS].to_broadcast([128, 8, NS])
                    c_ = lambda i: hn[0].t[:, i, :].rearrange("p (s q) -> p s q", q=NS)
                    outr, outi = s5fin.t[:, 0, s0:s0 + 8, 1:17], s5fin.t[:, 1, s0:s0 + 8, 1:17]
                cb_ = [s5c.b, hn[0].b]
                if not is_s:
                    pl2 = Ptab.t[:, :, :, T5 - 1]
                    ca, cb2 = s5c.t[:, 0:2, :], s5c.t[:, 2:4, :]
                    S.op("dve", lambda e: TT(e, ca, pl2, glr.unsqueeze(1).to_broadcast([128, 2, 16]), ALU.mult),
                         reads=[Ptab.b, gr.b] + cb_, writes=cb_)
                    S.op("dve", lambda e: TT(e, cb2, pl2, gli.unsqueeze(1).to_broadcast([128, 2, 16]), ALU.mult),
                         reads=[Ptab.b, gi2.b] + cb_, writes=cb_)
                    S.op("dve", lambda e: TT(e, outr, c_(0), c_(3), ALU.subtract), reads=cb_, writes=[s5cr.b, s5fin.b])
                    S.op("dve", lambda e: TT(e, outi, c_(2), c_(1), ALU.add), reads=cb_, writes=[s5cr.b, s5fin.b])
                else:
                    cseq = [(c_(0), plr, glr, ALU.mult), (c_(1), pli, gli, ALU.mult), (c_(2), plr, gli, ALU.mult), (c_(3), pli, glr, ALU.mult)]
                    for (o, a, b, op) in cseq:
                        S.op("dve", lambda e, o=o, a=a, b=b, op=op: TT(e, o, a, b, op), reads=[Ptab.b, gr.b, gi2.b] + cb_, writes=cb_)
                    S.op("dve", lambda e: TT(e, outr, c_(0), c_(1), ALU.subtract), reads=cb_, writes=[s5cr.b, s5fin.b])
                    S.op("dve", lambda e: TT(e, outi, c_(2), c_(3), ALU.add), reads=cb_, writes=[s5cr.b, s5fin.b])
                yield
                def emit_y5(gi_=gi_, slist=slist, tk0=tk0, ntok=ntok, hr=hr, hi=hi):
                    y5c0 = 352
                    nq = 4 if not is_s else 2
                    for qi in range(nq):
                        q = qi if not is_s else 2 * gi_ + qi
                        S.op("pe", lambda e, q=q, qi=qi: e.matmul(PB[4].t[:, y5c0 + qi * ntok:y5c0 + (qi + 1) * ntok], dg5.t[:, q, :],
                                                                  u5T.t[:, q, tk0:tk0 + ntok], start=(qi == 0), stop=False, skip_group_check=True),
                             reads=[dg5.b, u5T.b], writes=[PB[4].sub("y5")])
                    for idx, s in enumerate(slist):
                        qi = (s // 4) if not is_s else (s // 4 - 2 * gi_)
                        out = PB[4].t[32 * (s % 4):32 * (s % 4) + 32, y5c0 + qi * ntok:y5c0 + (qi + 1) * ntok]
                        for j4, lw in enumerate((s5CT.t[:, 0, s, :], s5CTn.t[:, s, :], s5CT.t[:, 1, s, :], s5CT.t[:, 1, s, :])):
                            S.op("pe", lambda e, j4=j4, lw=lw: e.matmul(out, lw, hr[j4].t[:, idx * ntok:(idx + 1) * ntok],
                                                                        start=False, stop=(j4 == 3), skip_group_check=True,
                                                                        tile_position=(0, 32 * (s % 4))),
                                 reads=[s5CT.b, s5CTn.b, hr[j4].b], writes=[PB[4].sub("y5")])
                    q0 = 0 if not is_s else 2 * gi_
                    S.op("act", lambda e: e.activation(out=y5pre.t[:, q0:q0 + nq, tk0:tk0 + ntok],
                                                       in_=PB[4].t[:, y5c0:y5c0 + nq * ntok].rearrange("p (q t) -> p q t", t=ntok), func=AF.Copy),
                         reads=[PB[4].sub("y5")], writes=[y5pre.b])
                if pend_y5[0] is not None:
                    pend_y5[0]()
                    yield
                pend_y5[0] = emit_y5
            if pend_y5[0] is not None:
                pend_y5[0]()
                pend_y5[0] = None
                yield
            if ti == 7:
                S.op("dve", lambda e: e.tensor_copy(out=s5fin.t[:, :, :, 0], in_=s5cr.t[:]), reads=[s5cr.b], writes=[s5fin.b])
            if is_s:
                S.dma("sp", o_s5, s5fin.t[:].rearrange("p a s q -> p (a s q)"), reads=[s5fin.b], buf=s5fin.b)
                outbufs.append(s5fin.b)
            if ti == 0:
                dump("y5pre", y5pre.t[:].rearrange("p k t -> p (k t)"), [128, 4 * NTM], [y5pre.b])
            ckpt("E%d" % ti)
            yield
            S.op("act", lambda e: e.activation(out=g5.t[:, :, 0:NT], in_=y5pre.t[:, :, 0:NT], func=AF.Gelu), reads=[y5pre.b], writes=[g5.b])
            for m in range(4):
                yield
                pb = next_pb()
                for q in range(4):
                    S.op("pe", lambda e, m=m, q=q, pb=pb: e.matmul(pb.t[:, 0:NT], wglu_sb.t[:, q, m * 128:(m + 1) * 128], g5.t[:, q, 0:NT],
                                                                   start=(q == 0), stop=(q == 3)),
                         reads=[wglu_sb.b, g5.b], writes=[pb.b])
                S.op("act", lambda e, m=m, pb=pb: e.activation(out=sgl.t[:, 0:NT], in_=pb.t[:, 0:NT], func=AF.Sigmoid,
                                                               bias=prm.t[:, P_S5M + 4 + m:P_S5M + 5 + m]),
                     reads=[pb.b, prm.b], writes=[sgl.b])
                S.op("dve", lambda e, m=m: TT(e, mixt[ti % 2].t[:, 4 + m, 0:NT], g5.t[:, m, 0:NT], sgl.t[:, 0:NT], ALU.mult),
                     reads=[g5.b, sgl.b], writes=[mixt[ti % 2].sub("s5")])
            S.dma("sp", mixd[:, :, t0:t0 + NT], mixt[ti % 2].t[:, :, 0:NT], reads=mixt[ti % 2].allb(), writes=[mixdb[ti]], buf=mixdb[ti])
            ckpt("T%d" % ti)
            if ti == 0:
                dump("mix0", mixt[0].t[:, :, 0:NTM], [128, 8, NTM], mixt[0].allb())
            yield

        import os as _os
        RATIO = int(_os.environ.get("K_RATIO", "1"))
        HEAD = int(_os.environ.get("K_HEAD", "10"))

        def drive(gens, ada_every=0):
            gens = [g for g in gens if g is not None]
            n = 0
            if len(gens) > 1:
                for _ in range(HEAD):
                    try:
                        next(gens[0])
                    except StopIteration:
                        gens.pop(0)
                        break
            while gens:
                for gi__, g in enumerate(list(gens)):
                    for _ in range((RATIO if gi__ == 0 else 1) if RATIO > 0 else (-RATIO if gi__ == 1 else 1)):
                        try:
                            next(g)
                        except StopIteration:
                            if g in gens:
                                gens.remove(g)
                            break
                n += 1
                if ada_every and n % ada_every == 0:
                    ada_step()
        ada_state[0] = 0
        drive([chain1(0)], ada_every=12)
        for ti_ in range(len(TILES_A)):
            if ti_ == 7:
                while ada_state[1] < len(ADA_CH):
                    ada_step()
                fill_x(a1x, amod.t[:, 0:8, 1:17], [amod.b])
                fill_x(sh1x, chunkmod(MOD_SH1)[:, :, 1:17], [mod.b])
                make_amod([(1, (4, 1)), (2, (7, 2))])
            drive([chain2(ti_), chain1(ti_ + 1) if ti_ + 1 < len(TILES_A) else None], ada_every=(10 if ti_ < 7 else 0))
        dump("mixS", mixt[0].t[:, :, 0:64], [128, 8, 64], mixt[0].allb())
        S.barrier()
        ckpt("1a")
        A.lo = LO_GLOBAL
        x1T = A.alloc("x1T", [128, 8, NTOK], F32, top=True)
        vT = A.alloc("vT", [128, 8, NTOK], BF16, top=True)
        wout_sb = A.alloc("wout_sb", [128, 8, D], BF16)
        wout_v = wout.rearrange("(kt p) n -> p kt n", p=128)
        for kh in range(4):
            S.dma("pool", wout_sb.t[:, 2 * kh:2 * kh + 2, :], wout_v[:, 2 * kh:2 * kh + 2, :], writes=[wout_sb.sub(kh)])
        mixb = [A.alloc("mixb%d" % i, [128, 8, 512], BF16) for i in range(2)]

        def load_mix(ti):
            t0, NT, is_s = TILES_B[ti]
            tiles_a = [i for i, (a0, n0, s0_) in enumerate(TILES_A) if a0 >= t0 and a0 < t0 + NT]
            S.dma("sp", mixb[ti % 2].t[:, :, 0:NT], mixd[:, :, t0:t0 + NT], reads=[mixdb[i] for i in tiles_a], writes=[mixb[ti % 2].b])
        xtm2 = A.alloc("xtm2", [128, 4, D], F32)
        xTm = [A.alloc("xTm%d" % i, [128, 512], F32) for i in range(2)]
        sqb = [A.alloc("sqb%d" % i, [128, 512], BF16) for i in range(2)]
        onesb = A.alloc("onesb", [128, 128], BF16)
        S.op("dve", lambda e: e.memset(onesb.t[:], 1.0), writes=[onesb.b])
        tmp2 = [A.alloc("tmp2_%d" % i, [128, 512], F32) for i in range(2)]
        rstdb = [A.alloc("rstdb%d" % i, [128, 512], F32) for i in range(2)]
        g1x = expand_mod("g1x", chunkmod(MOD_G1)[:, :, 1:17], [mod.b])
        a2x = expand_mod("a2x", amod.t[:, 8:16, 1:17], [amod.b])
        sh2x = expand_mod("sh2x", chunkmod(MOD_SH2)[:, :, 1:17], [mod.b])
        print("arena p1b: lo=%d hi=%d" % (A.lo, A.hi))
        TILES_B = [(i * 512, 512, False) for i in range(4)] + [(SEQ, 64, True)]

        def load_x2(ti):
            t0, NT, is_s = TILES_B[ti]
            for blk in range((NT + 127) // 128):
                rows = min(128, NT - blk * 128)
                S.dma("sp", xtm2.t[0:rows, blk, :], xin[t0 + blk * 128:t0 + blk * 128 + rows, :], writes=[xtm2.sub(blk)])
        load_x2(0)
        load_mix(0)

        def stat_accum(src_ap, m, NT, pbs, defer=None):
            sq = sqb[m % 2]
            S.op("act", lambda e: e.activation(out=sq.t[:, 0:NT], in_=src_ap, func=AF.Square), reads=[x1T.sub(m)], writes=[sq.b])

            def mm(m=m, sq=sq):
                S.op("pe", lambda e: e.matmul(pbs.t[:, 0:NT], onesb.t[:], sq.t[:, 0:NT], start=(m == 0), stop=(m == 7)),
                     reads=[onesb.b, sq.b], writes=[pbs.b])
            if defer is None:
                mm()
            else:
                if defer[0] is not None:
                    defer[0]()
                defer[0] = mm
                if m == 7:
                    defer[0]()
                    defer[0] = None

        def stat_finish(NT, pbs, rs):
            S.op("act", lambda e: e.activation(out=rs.t[:, 0:NT], in_=pbs.t[:, 0:NT], func=AF.Ln, scale=1.0 / D, bias=EPS),
                 reads=[pbs.b], writes=[rs.b])
            S.op("act", lambda e: e.activation(out=rs.t[:, 0:NT], in_=rs.t[:, 0:NT], func=AF.Exp, scale=-0.5), reads=[rs.b], writes=[rs.b])

        def b_part1(ti):
            t0, NT, is_s = TILES_B[ti]
            nblk = (NT + 127) // 128
            tsl = slice(t0, t0 + NT)
            pbs = PB[4 + ti % 2]
            dfr = [None]
            for m in range(8):
                pbx = PB[2 + m % 2]
                xm = xTm[m % 2]
                for blk in range(nblk):
                    rows = min(128, NT - blk * 128)
                    S.op("pe", lambda e, blk=blk, rows=rows: e.transpose(
                        pbx.t[:, blk * 128:blk * 128 + rows], xtm2.t[0:rows, blk, m * 128:(m + 1) * 128], cst.t[0:rows, C_ID:C_ID + rows]),
                        reads=[xtm2.sub(blk), cst.b], writes=[pbx.b])
                S.op("act", lambda e: e.activation(out=xm.t[:, 0:NT], in_=pbx.t[:, 0:NT], func=AF.Copy), reads=[pbx.b], writes=[xm.b])
                pb = next_pb()
                for kt in range(8):
                    S.op("pe", lambda e, kt=kt: e.matmul(pb.t[:, 0:NT], wout_sb.t[:, kt, m * 128:(m + 1) * 128], mixb[ti % 2].t[:, kt, 0:NT],
                                                         start=(kt == 0), stop=(kt == 7)),
                         reads=[wout_sb.sub(kt // 2), mixb[ti % 2].b], writes=[pb.b])
                if m == 0 and ti + 1 < len(TILES_B):
                    load_mix(ti + 1)
                if not is_s:
                    S.op("dve", lambda e: e.scalar_tensor_tensor(
                        out=x1T.t[:, m, tsl], in0=pb.t[:, 0:NT], scalar=mod.t[:, 8 * MOD_G1 + m, 0:1], in1=xm.t[:, 0:NT],
                        op0=ALU.mult, op1=ALU.add), reads=[pb.b, mod.b, xm.b], writes=[x1T.sub(m)])
                else:
                    S.op("dve", lambda e: TT(e, tmp2[0].t[:, 0:NT], pb.t[:, 0:NT], g1x.t[:, m, :], ALU.mult),
                         reads=[pb.b, g1x.b], writes=[tmp2[0].b])
                    S.op("dve", lambda e: TT(e, x1T.t[:, m, tsl], tmp2[0].t[:, 0:NT], xm.t[:, 0:NT], ALU.add),
                         reads=[tmp2[0].b, xm.b], writes=[x1T.sub(m)])
                stat_accum(x1T.t[:, m, tsl], m, NT, pbs, defer=dfr)
                yield
            if ti + 1 < len(TILES_B):
                load_x2(ti + 1)
            yield

        def b_part2(ti):
            t0, NT, is_s = TILES_B[ti]
            tsl = slice(t0, t0 + NT)
            rs = rstdb[ti % 2]
            stat_finish(NT, PB[4 + ti % 2], rs)
            yield
            for m in range(8):
                tq = tmp2[m % 2]
                S.op("dve", lambda e: TT(e, tq.t[:, 0:NT], x1T.t[:, m, tsl], rs.t[:, 0:NT], ALU.mult),
                     reads=[x1T.sub(m), rs.b], writes=[tq.b])
                if not is_s:
                    S.op("act", lambda e: e.activation(out=vT.t[:, m, tsl], in_=tq.t[:, 0:NT], func=AF.Identity,
                                                       scale=amod.t[:, 8 + m, 0:1], bias=mod.t[:, 8 * MOD_SH2 + m, 0:1]),
                         reads=[tq.b, amod.b, mod.b], writes=[vT.sub(m)])
                else:
                    S.op("dve", lambda e: TT(e, tq.t[:, 0:NT], tq.t[:, 0:NT], a2x.t[:, m, :], ALU.mult),
                         reads=[tq.b, a2x.b], writes=[tq.b])
                    S.op("dve", lambda e: TT(e, vT.t[:, m, tsl], tq.t[:, 0:NT], sh2x.t[:, m, :], ALU.add),
                         reads=[tq.b, sh2x.b], writes=[vT.sub(m)])
                yield
            if ti == 0:
                dump("x1p", x1T.t[:, :, 0:256], [128, 8, 256], x1T.allb())
                dump("vp", vT.t[:, :, 0:256], [128, 8, 256], vT.allb())
        drive([b_part1(0)])
        for ti_ in range(len(TILES_B)):
            drive([b_part2(ti_), b_part1(ti_ + 1) if ti_ + 1 < len(TILES_B) else None])
        S.barrier()
        ckpt("1b")

        A.lo = LO_GLOBAL
        tmp2 = [A.alloc("tmp3_%d" % i, [128, 512], F32) for i in range(2)]
        rstdb = [A.alloc("rstd3_%d" % i, [128, 512], F32) for i in range(2)]
        sqb = [A.alloc("sqb3_%d" % i, [128, 512], BF16) for i in range(2)]
        onesb = A.alloc("onesb3", [128, 128], BF16)
        S.op("dve", lambda e: e.memset(onesb.t[:], 1.0), writes=[onesb.b])
        g2x = expand_mod("g2x", chunkmod(MOD_G2)[:, :, 1:17], [mod.b])
        afx = expand_mod("afx", amod.t[:, 16:24, 1:17], [amod.b])
        shfx = expand_mod("shfx", chunkmod(MOD_SHF)[:, :, 1:17], [mod.b])
        LO_P2 = A.lo
        hT = A.alloc("hT", [128, 6, NTOK], BF16)
        wgs = [A.alloc("wgs%d" % i, [128, 8, 256], BF16) for i in range(3)]
        wus = [A.alloc("wus%d" % i, [128, 8, 256], BF16) for i in range(3)]
        wds = [A.alloc("wds%d" % i, [128, 6, D], BF16) for i in range(2)]
        sgt = [A.alloc("sgt%d" % i, [128, 512], BF16) for i in range(2)]
        print("arena p2: lo=%d hi=%d" % (A.lo, A.hi))
        wg_v = wg.rearrange("(kt p) n -> p kt n", p=128)
        wu_v = wu.rearrange("(kt p) n -> p kt n", p=128)
        wd_v = wd.rearrange("(j p) n -> p j n", p=128)
        QUARTERS = [(0, 6), (6, 12), (12, 18), (18, 22)]
        SLABS = [(q, ja + 2 * s) for q, (ja, jb) in enumerate(QUARTERS) for s in range((jb - ja) // 2)]

        def load_gu(si):
            q, j0 = SLABS[si]
            S.dma("pool", wgs[si % 3].t[:], wg_v[:, :, j0 * 128:(j0 + 2) * 128], writes=[wgs[si % 3].b])
            S.dma("pool", wus[si % 3].t[:], wu_v[:, :, j0 * 128:(j0 + 2) * 128], writes=[wus[si % 3].b])

        def load_wd(q):
            ja, jb = QUARTERS[q]
            for jh in range(0, jb - ja, 2):
                S.dma("pool", wds[q % 2].t[:, jh:jh + 2, :], wd_v[:, ja + jh:ja + jh + 2, :], writes=[wds[q % 2].b])
        load_gu(0)
        load_gu(1)
        load_wd(0)
        gbank = [0]
        si = 0
        for q, (ja, jb) in enumerate(QUARTERS):
            if q + 1 < 4:
                load_wd(q + 1)
            for s in range((jb - ja) // 2):
                if si + 2 < len(SLABS):
                    load_gu(si + 2)
                wgt, wut = wgs[si % 3], wus[si % 3]
                for jc in range(2):
                    jj = 2 * s + jc
                    for (t0, NT, is_s) in TILES_B:
                        tsl = slice(t0, t0 + NT)
                        gbank[0] ^= 1
                        pbg, pbu = PB[gbank[0]], PB[2 + gbank[0]]
                        for (wt, pb_) in ((wgt, pbg), (wut, pbu)):
                            for kt in range(8):
                                S.op("pe", lambda e, kt=kt, wt=wt, pb_=pb_: e.matmul(
                                    pb_.t[:, 0:NT], wt.t[:, kt, jc * 128:(jc + 1) * 128], vT.t[:, kt, tsl], start=(kt == 0), stop=(kt == 7)),
                                    reads=[wt.b] + vT.allb(), writes=[pb_.b])
                        sg_ = sgt[gbank[0]]
                        S.op("act", lambda e, pbg=pbg, sg_=sg_: e.activation(out=sg_.t[:, 0:NT], in_=pbg.t[:, 0:NT], func=AF.Silu),
                             reads=[pbg.b], writes=[sg_.b])
                        S.op("dve", lambda e, pbu=pbu, sg_=sg_: TT(e, hT.t[:, jj, tsl], sg_.t[:, 0:NT], pbu.t[:, 0:NT], ALU.mult),
                             reads=[sg_.b, pbu.b], writes=[hT.sub(jj)])
                si += 1
            nj = jb - ja
            wdt = wds[q % 2]
            for (t0, NT, is_s) in TILES_B:
                tsl = slice(t0, t0 + NT)
                for m in range(8):
                    pb = PB[4 + m % 2]
                    for jj in range(nj):
                        S.op("pe", lambda e, jj=jj, m=m, pb=pb: e.matmul(pb.t[:, 0:NT], wdt.t[:, jj, m * 128:(m + 1) * 128], hT.t[:, jj, tsl],
                                                                         start=(jj == 0), stop=(jj == nj - 1)),
                             reads=[wdt.b, hT.sub(jj)], writes=[pb.b])
                    if not is_s:
                        S.op("dve", lambda e, m=m, pb=pb: e.scalar_tensor_tensor(
                            out=x1T.t[:, m, tsl], in0=pb.t[:, 0:NT], scalar=mod.t[:, 8 * MOD_G2 + m, 0:1], in1=x1T.t[:, m, tsl],
                            op0=ALU.mult, op1=ALU.add), reads=[pb.b, mod.b, x1T.sub(m)], writes=[x1T.sub(m)])
                    else:
                        S.op("dve", lambda e, m=m, pb=pb: TT(e, tmp2[0].t[:, 0:NT], pb.t[:, 0:NT], g2x.t[:, m, :], ALU.mult),
                             reads=[pb.b, g2x.b], writes=[tmp2[0].b])
                        S.op("dve", lambda e, m=m: TT(e, x1T.t[:, m, tsl], tmp2[0].t[:, 0:NT], x1T.t[:, m, tsl], ALU.add),
                             reads=[tmp2[0].b, x1T.sub(m)], writes=[x1T.sub(m)])
        S.barrier()
        ckpt("ffn")
        A.lo = LO_P2
        yTs = [A.alloc("yT%d" % i, [128, 8, 512], F32) for i in range(2)]
        ytm = [A.alloc("ytm%d" % i, [128, D], F32) for i in range(2)]
        print("arena final: lo=%d hi=%d" % (A.lo, A.hi))
        oi = [0]

        def f_part1(ti):
            t0, NT, is_s = TILES_B[ti]
            tsl = slice(t0, t0 + NT)
            yT = yTs[ti % 2]
            pbs = PB[6 + ti % 2]
            rs = rstdb[ti % 2]
            for m in range(8):
                stat_accum(x1T.t[:, m, tsl], m, NT, pbs)
                if m % 2 == 1:
                    yield
            stat_finish(NT, pbs, rs)
            yield
            for m in range(8):
                tq = tmp2[m % 2]
                S.op("dve", lambda e: TT(e, tq.t[:, 0:NT], x1T.t[:, m, tsl], rs.t[:, 0:NT], ALU.mult),
                     reads=[x1T.sub(m), rs.b], writes=[tq.b])
                if not is_s:
                    S.op("act", lambda e: e.activation(out=yT.t[:, m, 0:NT], in_=tq.t[:, 0:NT], func=AF.Identity,
                                                       scale=amod.t[:, 16 + m, 0:1], bias=mod.t[:, 8 * MOD_SHF + m, 0:1]),
                         reads=[tq.b, amod.b, mod.b], writes=[yT.sub(m)])
                else:
                    S.op("dve", lambda e: TT(e, tq.t[:, 0:NT], tq.t[:, 0:NT], afx.t[:, m, :], ALU.mult),
                         reads=[tq.b, afx.b], writes=[tq.b])
                    S.op("dve", lambda e: TT(e, yT.t[:, m, 0:NT], tq.t[:, 0:NT], shfx.t[:, m, :], ALU.add),
                         reads=[tq.b, shfx.b], writes=[yT.sub(m)])
                yield

        def f_part2(ti):
            t0, NT, is_s = TILES_B[ti]
            yT = yTs[ti % 2]
            for blk in range((NT + 127) // 128):
                rows = min(128, NT - blk * 128)
                yo = ytm[oi[0] % 2]
                oi[0] += 1
                for half in range(2):
                    pbt = PB[half]
                    for k4 in range(4):
                        kt = 4 * half + k4
                        S.op("pe", lambda e, kt=kt, k4=k4: e.transpose(
                            pbt.t[0:rows, k4 * 128:(k4 + 1) * 128], yT.t[:, kt, blk * 128:blk * 128 + rows], ident),
                            reads=[yT.sub(kt), cst.b], writes=[pbt.b])
                    if half == 0:
                        S.op("act", lambda e: e.activation(out=yo.t[0:rows, 0:512], in_=pbt.t[0:rows, :], func=AF.Copy),
                             reads=[pbt.b], writes=[yo.b])
                    else:
                        S.op("dve", lambda e: e.tensor_copy(out=yo.t[0:rows, 512:1024], in_=pbt.t[0:rows, :]),
                             reads=[pbt.b], writes=[yo.b])
                    yield
                S.dma("sp", yout[t0 + blk * 128:t0 + blk * 128 + rows, :], yo.t[0:rows, :], reads=[yo.b], buf=yo.b)
        import os as _os2
        if True:
            for ti_ in range(len(TILES_B)):
                drive([f_part1(ti_)])
                drive([f_part2(ti_)])
        else:
            drive([f_part1(0)])
            for ti_ in range(len(TILES_B)):
                drive([f_part2(ti_), f_part1(ti_ + 1) if ti_ + 1 < len(TILES_B) else None])
        S.barrier()
    return nc, dumps


def _prep_inputs(inp):
    cstv = _consts()
    prmv = _params(inp)
    BT, CT = _s5mats(inp)
    maps = []
    for i in range(NCORES):
        m = {}
        m["xin"] = np.ascontiguousarray(np.concatenate(
            [inp["x_prompt"][i], inp["x_sample"][NS * i:NS * (i + 1)].reshape(NS * LS, D)], axis=0), dtype=np.float32)
        m["cin"] = np.ascontiguousarray(np.concatenate(
            [inp["c_prompt"][i:i + 1], inp["c_sample"][NS * i:NS * (i + 1)]], axis=0), dtype=np.float32)
        m["wada"] = np.ascontiguousarray(inp["w_ada"][0], dtype=np.float32)
        m["wadaf"] = np.ascontiguousarray(inp["w_ada_f"], dtype=np.float32)
        m["win"] = np.ascontiguousarray(inp["w_in"][0], dtype=np.float32)
        m["wglu"] = np.ascontiguousarray(inp["w_glu"][0], dtype=np.float32)
        m["wout"] = np.ascontiguousarray(inp["w_out"][0], dtype=np.float32)
        m["wg"] = np.ascontiguousarray(inp["w_ffn_gate"][0], dtype=np.float32)
        m["wu"] = np.ascontiguousarray(inp["w_ffn_up"][0], dtype=np.float32)
        m["wd"] = np.ascontiguousarray(inp["w_ffn_down"][0], dtype=np.float32)
        m["cst"] = cstv
        m["prm"] = prmv
        m["s5bt"] = BT.reshape(128, -1)
        m["s5ct"] = CT.reshape(128, -1)
        m["stssd"] = np.ascontiguousarray(inp["state_ssd"][0, NS * i:NS * (i + 1)], dtype=np.float32)
        sc = inp["state_conv"][0, NS * i:NS * (i + 1)]
        m["stconv"] = np.ascontiguousarray(
            sc.reshape(NS, 3, 8, 128).transpose(3, 2, 0, 1).reshape(128, -1), dtype=np.float32)
        sr = inp["state_s5_re"][0, NS * i:NS * (i + 1)]
        si = inp["state_s5_im"][0, NS * i:NS * (i + 1)]
        st = np.stack([sr, si], 0).reshape(2, NS, 16, 128).transpose(3, 0, 2, 1)
        m["sts5"] = np.ascontiguousarray(st.reshape(128, -1), dtype=np.float32)
        maps.append(m)
    return maps


_CACHE = {}


def kernel(**inputs):
    inp = {k: np.asarray(v) for k, v in inputs.items()}
    if "nc" not in _CACHE:
        _CACHE["nc"] = build()[0]
    nc = _CACHE["nc"]
    maps = _prep_inputs(inp)
    res = run_bass_kernel_spmd(nc, maps, core_ids=list(range(NCORES)))
    R = res.results
    y_p = np.stack([R[i]["yout"][:SEQ] for i in range(NCORES)], 0)
    y_s = np.concatenate([R[i]["yout"][SEQ:].reshape(NS, LS, D) for i in range(NCORES)], 0)
    ssd_p = np.stack([R[i]["o_ssdp"].reshape(128, 8, 64).transpose(1, 2, 0) for i in range(NCORES)], 0)[None]
    ssd_s = np.concatenate([R[i]["o_ssds"] for i in range(NCORES)], 0)[None]
    conv = [R[i]["o_conv"].reshape(128, 8, 17, 3).transpose(2, 3, 1, 0).reshape(17, 3, 1024) for i in range(NCORES)]
    conv_p = np.stack([c[0] for c in conv], 0)[None]
    conv_s = np.concatenate([c[1:] for c in conv], 0)[None]
    s5 = [R[i]["o_s5"].reshape(128, 2, 16, 17).transpose(1, 3, 2, 0).reshape(2, 17, 32, 64) for i in range(NCORES)]
    re_p = np.stack([s[0, 0] for s in s5], 0)[None]
    re_s = np.concatenate([s[0, 1:] for s in s5], 0)[None]
    im_p = np.stack([s[1, 0] for s in s5], 0)[None]
    im_s = np.concatenate([s[1, 1:] for s in s5], 0)[None]
    f = lambda a: np.ascontiguousarray(a, dtype=np.float32)
    return (f(y_p), f(y_s), f(ssd_p), f(ssd_s), f(conv_p), f(conv_s), f(re_p), f(re_s), f(im_p), f(im_s))
```

```python
import math
import numpy as np
from contextlib import ExitStack
import concourse.bass as bass
import concourse.mybir as mybir
from concourse.bass_utils import run_bass_kernel_spmd

F32 = mybir.dt.float32
BF16 = mybir.dt.bfloat16
I32 = mybir.dt.int32
AF = mybir.ActivationFunctionType
ALU = mybir.AluOpType

NCORES = 8
D = 1024
SEQ = 2048
NS = 16
LS = 4
NTOK = SEQ + NS * LS
DFF = 2816
NJ = DFF // 128
INP = 2056
EPS = 1e-6
T5 = 32
TILES = [(0, 512), (512, 512), (1024, 512), (1536, 512), (2048, 64)]
PI = math.pi


class Buf:
    def __init__(self, name):
        self.name = name
        self.w = None
        self.r = []
        self.dsem = None
        self.dcnt = 0


class TL:
    def __init__(self, t, name):
        self.t = t
        self.name = name
        self.b = Buf(name)
        self.subs = {}

    def sub(self, k):
        if getattr(self, "nosub", False):
            return self.b
        if k not in self.subs:
            self.subs[k] = Buf("%s_%s" % (self.name, k))
        return self.subs[k]

    def allb(self):
        return [self.b] + list(self.subs.values())

    def __getitem__(self, k):
        return self.t[k]


class Sched:
    ENG = ["pe", "act", "dve", "pool", "sp"]

    def __init__(self, nc, es):
        self.nc = nc
        self.es = es
        self.eobj = {"pe": nc.tensor, "act": nc.scalar, "dve": nc.vector, "pool": nc.gpsimd, "sp": nc.sync}
        self.cnt = {e: 0 for e in self.ENG}
        self.sem = {e: es.enter_context(nc.semaphore("s_" + e)) for e in self.ENG}
        self.seen = {e: {} for e in self.ENG}
        self.dbufs = []
        self.ninst = 0
        self.dead = False
        self.pe_pending = None

    def _flush_pe(self):
        if self.pe_pending is not None:
            self.pe_pending.then_inc(self.sem["pe"], 1)
            self.cnt["pe"] += 1
            self.pe_pending = None

    def _deps(self, eng, reads, writes):
        deps = []
        for b in reads:
            if b.w is not None:
                deps.append(b.w)
        for b in writes:
            if b.w is not None:
                deps.append(b.w)
            deps.extend(b.r)
        waits = {}
        for (sem, val, key) in deps:
            if key == "pe" and eng == "pe":
                continue
            if self.seen[eng].get(key, 0) >= val:
                continue
            if key == "pe" and val > self.cnt["pe"]:
                self._flush_pe()
            if key not in waits or waits[key][1] < val:
                waits[key] = (sem, val)
        for key, (sem, val) in waits.items():
            self.seen[eng][key] = val
        return list(waits.values())

    def op(self, eng, fn, reads=(), writes=()):
        if self.dead:
            return None
        xr = [b for b in reads if getattr(b, "excl", False)]
        if xr:
            reads = [b for b in reads if not getattr(b, "excl", False)]
            writes = list(writes) + xr
        waits = self._deps(eng, reads, writes)
        e = self.eobj[eng]
        for (s_, v_) in waits:
            e.wait_ge(s_, v_)
        if eng == "pe":
            self.pe_pending = fn(e)
            tok = (self.sem[eng], self.cnt[eng] + 1, eng)
        else:
            self.cnt[eng] += 1
            tok = (self.sem[eng], self.cnt[eng], eng)
            fn(e).then_inc(self.sem[eng], 1)
        for b in reads:
            b.r.append(tok)
        for b in writes:
            b.w = tok
            b.r = []
        self.ninst += 1
        return tok

    def dma(self, eng, out, in_, reads=(), writes=(), buf=None, **kw):
        if self.dead:
            return None
        waits = self._deps(eng, reads, writes)
        if buf is None:
            buf = writes[0] if writes else reads[0]
        if buf.dsem is None:
            buf.dsem = self.es.enter_context(self.nc.semaphore("d_" + buf.name))
            self.dbufs.append(buf)
        buf.dcnt += 16
        tok = (buf.dsem, buf.dcnt, "d_" + buf.name)
        e = self.eobj[eng]
        for (s_, v_) in waits:
            e.wait_ge(s_, v_)
        e.dma_start(out=out, in_=in_, **kw).then_inc(buf.dsem, 16)
        for b in reads:
            b.r.append(tok)
        for b in writes:
            b.w = tok
            b.r = []
        self.ninst += 1
        return tok

    def barrier(self):
        if self.dead:
            return
        self._flush_pe()
        for e in self.ENG:
            waits = []
            for o in self.ENG:
                if o != e and self.cnt[o] > self.seen[e].get(o, 0):
                    waits.append((self.sem[o], self.cnt[o]))
                    self.seen[e][o] = self.cnt[o]
            for b in self.dbufs:
                key = "d_" + b.name
                if b.dcnt > self.seen[e].get(key, 0):
                    waits.append((b.dsem, b.dcnt))
                    self.seen[e][key] = b.dcnt
            for (s_, v_) in waits:
                self.eobj[e].wait_ge(s_, v_)

    def emit(self):
        pass


C_ID = 0
C_TRI = 128
C_NEG = 256
C_TRI64 = 384
C_NEG64 = 512
C_SEG64 = 640
C_SEGI = 768
CST_W = 784

P_BMOD = 0
P_GAIN = 64
P_CONV = 88
P_SSDFM = 128
P_S5P = 136
P_S5M = 184
P_SSD8 = 192
PRM_W = 194


def _consts():
    c = np.zeros((128, CST_W), np.float32)
    c[:, C_ID:C_ID + 128] = np.eye(128, dtype=np.float32)
    s = np.arange(128)[:, None]
    l = np.arange(128)[None, :]
    c[:, C_TRI:C_TRI + 128] = (s <= l).astype(np.float32)
    c[:, C_NEG:C_NEG + 128] = np.where(l >= s, 0.0, -30000.0)
    same = (s // LS == l // LS) & (s < 64) & (l < 64)
    c[:, C_TRI64:C_TRI64 + 128] = ((s <= l) & same).astype(np.float32)
    c[:, C_NEG64:C_NEG64 + 128] = np.where((l >= s) & same, 0.0, -30000.0)
    c[:, C_SEG64:C_SEG64 + 128] = same.astype(np.float32)
    j = np.arange(16)[None, :]
    c[:, C_SEGI:C_SEGI + 16] = ((s // LS == j) & (s < 64)).astype(np.float32)
    return c


def _fm(v, nt):
    return np.ascontiguousarray(np.asarray(v, np.float32).reshape(nt, 128).T)


def _params(inp):
    p = np.zeros((128, PRM_W), np.float32)
    p[:, P_BMOD:P_BMOD + 48] = _fm(inp["b_ada"][0], 48)
    p[:, P_BMOD + 48:P_BMOD + 64] = _fm(inp["b_ada_f"], 16)
    p[:, P_GAIN:P_GAIN + 8] = _fm(inp["norm1_g"][0], 8)
    p[:, P_GAIN + 8:P_GAIN + 16] = _fm(inp["norm2_g"][0], 8)
    p[:, P_GAIN + 16:P_GAIN + 24] = _fm(inp["normf_g"], 8)
    cw = inp["conv_w"][0]
    cv = np.zeros((128, 8, 5), np.float32)
    for k in range(4):
        cv[:, :, k] = _fm(cw[k], 8)
    cv[:, :, 4] = _fm(inp["conv_b"][0], 8)
    p[:, P_CONV:P_CONV + 40] = cv.reshape(128, 40)
    Dh = inp["ssd_D"][0]
    dfm = np.zeros((128, 4), np.float32)
    for pr in range(4):
        dfm[0:64, pr] = Dh[2 * pr]
        dfm[64:128, pr] = Dh[2 * pr + 1]
    p[:, P_SSDFM:P_SSDFM + 4] = dfm
    p[:, P_SSDFM + 4:P_SSDFM + 8] = _fm(inp["ssd_norm_g"][0], 4)

    def st(a):
        return np.ascontiguousarray(np.asarray(a, np.float32).reshape(16, 128).T)
    p[:, P_S5P:P_S5P + 16] = st(inp["s5_A_re"][0])
    p[:, P_S5P + 16:P_S5P + 32] = st(inp["s5_A_im"][0])
    p[:, P_S5P + 32:P_S5P + 48] = st(np.repeat(inp["s5_log_step"][0][:, None], 64, axis=1))
    p[:, P_S5M:P_S5M + 4] = _fm(inp["s5_D"][0], 4)
    p[:, P_S5M + 4:P_S5M + 8] = _fm(inp["b_glu"][0], 4)
    p[0:8, P_SSD8] = inp["ssd_dt_bias"][0]
    p[0:8, P_SSD8 + 1] = inp["ssd_A_log"][0]
    return p


def _s5mats(inp):
    Br, Bi = inp["s5_B_re"][0], inp["s5_B_im"][0]
    Cr, Ci = inp["s5_C_re"][0], inp["s5_C_im"][0]
    BT = np.zeros((128, 2, 16, 128), np.float32)
    CT = np.zeros((128, 2, 16, 32), np.float32)
    for s in range(16):
        for gl in range(2):
            g = 2 * s + gl
            r0 = (g % 8) * 16
            BT[r0:r0 + 16, 0, s, gl * 64:(gl + 1) * 64] = Br[g].T
            BT[r0:r0 + 16, 1, s, gl * 64:(gl + 1) * 64] = Bi[g].T
            CT[gl * 64:(gl + 1) * 64, 0, s, gl * 16:(gl + 1) * 16] = Cr[g].T
            CT[gl * 64:(gl + 1) * 64, 1, s, gl * 16:(gl + 1) * 16] = Ci[g].T
    return BT, CT


class Arena:
    def __init__(self, nc, es, words):
        self.t = es.enter_context(nc.sbuf_tensor("arena", [128, words], F32))
        self.words = words
        self.lo = 0
        self.hi = words

    def alloc(self, name, shape, dt, top=False):
        n = 1
        for d in shape[1:]:
            n *= d
        w = n if dt == F32 or dt == I32 else (n + 1) // 2
        w = (w + 3) // 4 * 4
        if top:
            self.hi -= w
            off = self.hi
        else:
            off = self.lo
            self.lo += w
        assert self.lo <= self.hi, "arena overflow at %s: lo=%d hi=%d" % (name, self.lo, self.hi)
        ap = self.t[:, off:off + w]
        if dt != F32:
            ap = ap.bitcast(dt)
        ap = ap[:, 0:n]
        if len(shape) == 3:
            ap = ap.rearrange("p (a b) -> p a b", b=shape[2])
        elif len(shape) == 4:
            ap = ap.rearrange("p (a b c) -> p a b c", b=shape[2], c=shape[3])
        if shape[0] < 128:
            ap = ap[0:shape[0]]
        return TL(ap, name)


class StopBuild(Exception):
    pass


def build(dbg=None, stop_after=None):
    nc = bass.Bass("TRN2", target_bir_lowering=False)

    SH = []

    def ckpt(name):
        if stop_after == name:
            SH[0].barrier()
            SH[0].dead = True
    dt_in = lambda name, shape: nc.dram_tensor(name, list(shape), F32, kind="ExternalInput").ap()
    dt_out = lambda name, shape: nc.dram_tensor(name, list(shape), F32, kind="ExternalOutput").ap()
    xin = dt_in("xin", [NTOK, D])
    cin = dt_in("cin", [17, D])
    wada = dt_in("wada", [D, 6144])
    wadaf = dt_in("wadaf", [D, 2048])
    win = dt_in("win", [D, INP])
    wglu = dt_in("wglu", [512, 512])
    wout = dt_in("wout", [D, D])
    wg = dt_in("wg", [D, DFF])
    wu = dt_in("wu", [D, DFF])
    wd = dt_in("wd", [DFF, D])
    cst_d = dt_in("cst", [128, CST_W])
    prm_d = dt_in("prm", [128, PRM_W])
    s5bt_d = dt_in("s5bt", [128, 2 * 16 * 128])
    s5ct_d = dt_in("s5ct", [128, 2 * 16 * 32])
    stssd_d = dt_in("stssd", [NS, 8, 64, 128])
    stconv_d = dt_in("stconv", [128, 8 * NS * 3])
    sts5_d = dt_in("sts5", [128, 2 * 16 * NS])
    yout = dt_out("yout", [NTOK, D])
    o_ssdp = dt_out("o_ssdp", [128, 512])
    o_ssds = dt_out("o_ssds", [NS, 8, 64, 128])
    o_conv = dt_out("o_conv", [128, 8 * 17 * 3])
    o_s5 = dt_out("o_s5", [128, 2 * 16 * 17])
    mixd = nc.dram_tensor("mixd", [128, 8, NTOK], BF16, kind="Internal").ap()
    dumps = {}

    with ExitStack() as es:
        S = Sched(nc, es)
        NEED_CTN = []
        SH.append(S)
        A = Arena(nc, es, 53200)
        outbufs = []

        def dump(name, ap, shape, reads):
            if dbg is None or name not in dbg:
                return
            d = dt_out("dbg_" + name, shape)
            dumps[name] = shape
            b = Buf("dbg_" + name)
            S.dma("sp" if ap.dtype == F32 else "pool", d, ap, reads=reads, buf=b)
            outbufs.append(b)

        PB = [TL(es.enter_context(nc.psum_tensor("pb%d" % i, [128, 512], F32)), "pb%d" % i) for i in range(8)]
        for pb_ in PB:
            pb_.b.excl = True
            pb_.nosub = True

        def pbf(i):
            return PB[i].t[:].bitcast(BF16)

        cst = A.alloc("cst", [128, CST_W], F32)
        prm = A.alloc("prm", [128, PRM_W], F32)
        identb = A.alloc("identb", [128, 128], BF16)
        onesf = A.alloc("onesf", [128, 128], F32)
        mod = A.alloc("mod", [128, 64, 17], F32)
        amod = A.alloc("amod", [128, 24, 17], F32)
        s5fin = A.alloc("s5fin", [128, 2, 16, 17], F32)
        scT = A.alloc("scT", [128, 8, 17], BF16)
        LO_GLOBAL = A.lo
        win_sb = A.alloc("win_sb", [128, 8, INP], BF16)
        wglu_sb = A.alloc("wglu_sb", [128, 4, 512], BF16)
        s5BT = A.alloc("s5BT", [128, 2, 16, 128], BF16)
        s5CT = A.alloc("s5CT", [128, 2, 16, 32], BF16)
        LO_W = A.lo

        def load_1a_weights():
            for a_ in range(4):
                S.dma("pool", s5BT.t[:].rearrange("p a s c -> p (a s c)")[:, a_ * 1024:(a_ + 1) * 1024],
                      s5bt_d[:, a_ * 1024:(a_ + 1) * 1024], writes=[s5BT.b])
            S.dma("pool", s5CT.t[:].rearrange("p a s c -> p (a s c)"), s5ct_d, writes=[s5CT.b])
            win_v = win.rearrange("(kt p) n -> p kt n", p=128)
            for kh in range(4):
                for ch in range(2):
                    S.dma("pool", win_sb.t[:, 2 * kh:2 * kh + 2, ch * 1028:(ch + 1) * 1028],
                          win_v[:, 2 * kh:2 * kh + 2, ch * 1028:(ch + 1) * 1028], writes=[win_sb.sub(kh)])
            S.dma("pool", wglu_sb.t[:], wglu.rearrange("(kt p) n -> p kt n", p=128), writes=[wglu_sb.b])

        ident = cst.t[:, C_ID:C_ID + 128]
        S.dma("sp", cst.t[:], cst_d, writes=[cst.b])
        S.dma("sp", prm.t[:], prm_d, writes=[prm.b])
        S.op("act", lambda e: e.activation(out=identb.t[:], in_=ident, func=AF.Copy), reads=[cst.b], writes=[identb.b])
        S.op("dve", lambda e: e.memset(onesf.t[:], 1.0), writes=[onesf.b])

        def chunkmod(i):
            return mod.t[:, 8 * i:8 * i + 8, :]

        ssd8 = A.alloc("ssd8", [8, 4], F32)
        S.op("act", lambda e: e.activation(out=ssd8.t[:, 1:2], in_=prm.t[0:8, P_SSD8 + 1:P_SSD8 + 2], func=AF.Exp),
             reads=[prm.b], writes=[ssd8.b])
        S.op("dve", lambda e: e.tensor_scalar(out=ssd8.t[:, 1:2], in0=ssd8.t[:, 1:2], scalar1=-1.0, scalar2=None, op0=ALU.mult),
             reads=[ssd8.b], writes=[ssd8.b])
        S.op("dve", lambda e: e.tensor_copy(out=ssd8.t[:, 0:1], in_=prm.t[0:8, P_SSD8:P_SSD8 + 1]), reads=[prm.b], writes=[ssd8.b])

        Ptab = A.alloc("Ptab", [128, 2, 16, T5], F32)
        Qtab = A.alloc("Qtab", [128, 2, 16, T5], F32)
        s5t = [A.alloc("s5t%d" % i, [128, 512], F32) for i in range(2)]

        def alias(name, ap, buf):
            tl = TL(ap, name)
            tl.b = buf
            return tl
        sw = alias("s5work", s5t[1].t[:, 0:384].rearrange("p (a b) -> p a b", b=16), s5t[1].b)
        tmpA = alias("tmpA", s5t[0].t[:, 0:256].rearrange("p (a b) -> p a b", b=T5 // 2), s5t[0].b)
        tmpB = alias("tmpB", s5t[0].t[:, 256:512].rearrange("p (a b) -> p a b", b=T5 // 2), s5t[0].b)
        mask32 = A.alloc("mask32", [128, 16, T5], BF16)
        s5v = [A.alloc("s5v%d" % i, [128, 512], F32) for i in range(2)]
        qtmp = alias("qtmp", s5v[0].t[:].rearrange("p (s t) -> p s t", t=T5), s5v[0].b)
        mask4 = A.alloc("mask4", [128, 128, LS], BF16)
        s5cr = A.alloc("s5cr", [128, 2, 16], F32)
        W = lambda i: sw.t[:, i, :]
        pv = lambda i: prm.t[:, P_S5P + 16 * i:P_S5P + 16 * (i + 1)]
        swb = [sw.b, prm.b]

        def dv(fn):
            S.op("dve", fn, reads=swb, writes=[sw.b])

        def act(fn):
            S.op("act", fn, reads=swb, writes=[sw.b])
        TT = lambda e, o, a, b, op: e.tensor_tensor(out=o, in0=a, in1=b, op=op)
        def exp_acc(dst, src):
            dv(lambda e: e.tensor_scalar(out=W(22), in0=src, scalar1=1.0 / 16, scalar2=None, op0=ALU.mult))
            dv(lambda e: e.tensor_scalar(out=dst, in0=W(22), scalar1=1.0 / 7, scalar2=1.0, op0=ALU.mult, op1=ALU.add))
            for k in (6, 5, 4, 3, 2, 1):
                dv(lambda e: TT(e, dst, dst, W(22), ALU.mult))
                dv(lambda e, k=k: e.tensor_scalar(out=dst, in0=dst, scalar1=1.0 / k, scalar2=1.0, op0=ALU.mult, op1=ALU.add))
            for _ in range(4):
                dv(lambda e: TT(e, dst, dst, dst, ALU.mult))
        exp_acc(W(0), pv(2))
        dv(lambda e: TT(e, W(1), pv(0), W(0), ALU.mult))
        dv(lambda e: TT(e, W(2), pv(1), W(0), ALU.mult))
        exp_acc(W(3), W(1))

        def range_reduce(dst, src, add):
            ki = A_ki
            dv(lambda e: e.tensor_scalar(out=W(20), in0=src, scalar1=float(add), scalar2=1.0 / (2 * PI), op0=ALU.add, op1=ALU.mult))
            S.op("dve", lambda e: e.tensor_copy(out=ki.t[:], in_=W(20)), reads=swb, writes=[ki.b])
            S.op("dve", lambda e: e.tensor_copy(out=W(21), in_=ki.t[:]), reads=[ki.b], writes=[sw.b])
            dv(lambda e: e.tensor_scalar(out=W(20), in0=src, scalar1=float(add), scalar2=None, op0=ALU.add))
            dv(lambda e: e.scalar_tensor_tensor(out=dst, in0=W(21), scalar=-2 * PI, in1=W(20), op0=ALU.mult, op1=ALU.add))
            dv(lambda e: e.tensor_scalar(out=dst, in0=dst, scalar1=PI, scalar2=-PI, op0=ALU.min, op1=ALU.max))
        A_ki = A.alloc("s5ki", [128, 16], I32)
        range_reduce(W(4), W(2), 0.0)
        range_reduce(W(5), W(2), PI / 2)
        act(lambda e: e.activation(out=W(6), in_=W(4), func=AF.Sin))
        act(lambda e: e.activation(out=W(7), in_=W(5), func=AF.Sin))
        dv(lambda e: TT(e, W(8), W(3), W(7), ALU.mult))
        dv(lambda e: TT(e, W(9), W(3), W(6), ALU.mult))
        dv(lambda e: e.tensor_scalar(out=W(10), in0=W(8), scalar1=-1.0, scalar2=None, op0=ALU.add))
        dv(lambda e: TT(e, W(11), pv(0), pv(0), ALU.mult))
        dv(lambda e: TT(e, W(12), pv(1), pv(1), ALU.mult))
        dv(lambda e: TT(e, W(11), W(11), W(12), ALU.add))
        dv(lambda e: e.reciprocal(out=W(11), in_=W(11)))
        dv(lambda e: TT(e, W(12), W(10), pv(0), ALU.mult))
        dv(lambda e: TT(e, W(13), W(9), pv(1), ALU.mult))
        dv(lambda e: TT(e, W(12), W(12), W(13), ALU.add))
        dv(lambda e: TT(e, W(14), W(12), W(11), ALU.mult))
        dv(lambda e: TT(e, W(12), W(9), pv(0), ALU.mult))
        dv(lambda e: TT(e, W(13), W(10), pv(1), ALU.mult))
        dv(lambda e: TT(e, W(12), W(12), W(13), ALU.subtract))
        dv(lambda e: TT(e, W(15), W(12), W(11), ALU.mult))
        dv(lambda e: TT(e, W(12), W(8), W(8), ALU.mult))
        dv(lambda e: TT(e, W(13), W(9), W(9), ALU.mult))
        dv(lambda e: TT(e, W(12), W(12), W(13), ALU.add))
        dv(lambda e: e.reciprocal(out=W(12), in_=W(12)))
        dv(lambda e: TT(e, W(16), W(8), W(12), ALU.mult))
        dv(lambda e: e.scalar_tensor_tensor(out=W(17), in0=W(9), scalar=-1.0, in1=W(12), op0=ALU.mult, op1=ALU.mult))

        def build_pow(tab, br, bi):
            tb = [tab.b, sw.b, tmpA.b, tmpB.b]
            S.op("dve", lambda e: e.tensor_copy(out=tab.t[:, 0, :, 0], in_=br), reads=tb, writes=[tab.b])
            S.op("dve", lambda e: e.tensor_copy(out=tab.t[:, 1, :, 0], in_=bi), reads=tb, writes=[tab.b])
            n = 1
            while n < T5:
                ar, ai = tab.t[:, 0, :, 0:n], tab.t[:, 1, :, 0:n]
                sr = tab.t[:, 0, :, n - 1:n].to_broadcast([128, 16, n])
                si = tab.t[:, 1, :, n - 1:n].to_broadcast([128, 16, n])
                tA, tB = tmpA.t[:, :, 0:n], tmpB.t[:, :, 0:n]
                orr, oi = tab.t[:, 0, :, n:2 * n], tab.t[:, 1, :, n:2 * n]
                ops = [(tA, ar, sr, ALU.mult), (tB, ai, si, ALU.mult), (orr, tA, tB, ALU.subtract),
                       (tA, ar, si, ALU.mult), (tB, ai, sr, ALU.mult), (oi, tA, tB, ALU.add)]
                for (o, a, b, op) in ops:
                    S.op("dve", lambda e, o=o, a=a, b=b, op=op: TT(e, o, a, b, op), reads=tb, writes=tb[0:1] + tb[2:4])
                n *= 2
        build_pow(Ptab, W(8), W(9))
        build_pow(Qtab, W(16), W(17))
        tq = [Qtab.b, sw.b, tmpA.b, tmpB.b]
        for half in range(2):
            hs = slice(half * (T5 // 2), (half + 1) * (T5 // 2))
            qr, qi = Qtab.t[:, 0, :, hs], Qtab.t[:, 1, :, hs]
            fr = W(14).unsqueeze(2).to_broadcast([128, 16, T5 // 2])
            fi = W(15).unsqueeze(2).to_broadcast([128, 16, T5 // 2])
            ops = [(tmpA.t[:], qr, fr, ALU.mult), (tmpB.t[:], qi, fi, ALU.mult), ("R", tmpA.t[:], tmpB.t[:], ALU.subtract),
                   (tmpA.t[:], qr, fi, ALU.mult), (tmpB.t[:], qi, fr, ALU.mult), (qi, tmpA.t[:], tmpB.t[:], ALU.add)]
            for (o, a, b, op) in ops:
                if isinstance(o, str):
                    o = qtmp.t[:, :, hs]
                S.op("dve", lambda e, o=o, a=a, b=b, op=op: TT(e, o, a, b, op), reads=tq + [qtmp.b], writes=tq + [qtmp.b])
            S.op("dve", lambda e, qr=qr, hs=hs: e.tensor_copy(out=qr, in_=qtmp.t[:, :, hs]), reads=[qtmp.b], writes=[Qtab.b])
        S.op("dve", lambda e: e.memset(mask32.t[:], 1.0), reads=[Qtab.b], writes=[mask32.b])
        S.op("dve", lambda e: e.memset(mask32.t[:, :, 0:1], 0.0), writes=[mask32.b])
        S.op("dve", lambda e: e.memset(mask4.t[:], 1.0), writes=[mask4.b])
        S.op("dve", lambda e: e.memset(mask4.t[:, :, 0:1], 0.0), writes=[mask4.b])
        S.op("dve", lambda e: e.memset(s5cr.t[:], 0.0), writes=[s5cr.b])
        dump("Ptab", Ptab.t[:].rearrange("p a s t -> p (a s t)"), [128, 2 * 16 * T5], [Ptab.b])
        dump("Qtab", Qtab.t[:].rearrange("p a s t -> p (a s t)"), [128, 2 * 16 * T5], [Qtab.b])

        LO_W = A.lo
        cs = A.alloc("cs", [17, D], F32)
        slabs = [A.alloc("adaslab%d" % i, [128, 8, 512], BF16) for i in range(3)]
        S.dma("sp", cs.t[:], cin, writes=[cs.b])
        S.op("act", lambda e: e.activation(out=cs.t[:], in_=cs.t[:], func=AF.Silu), reads=[cs.b], writes=[cs.b])
        for kt in range(8):
            S.op("pe", lambda e, kt=kt: e.transpose(PB[2].t[:, kt * 17:(kt + 1) * 17], cs.t[:, kt * 128:(kt + 1) * 128],
                                                    cst.t[0:17, C_ID:C_ID + 17]),
                 reads=[cs.b, cst.b], writes=[PB[2].b])
        S.op("act", lambda e: e.activation(out=scT.t[:].rearrange("p k s -> p (k s)"), in_=PB[2].t[:, 0:136], func=AF.Copy),
             reads=[PB[2].b], writes=[scT.b])
        wada_v = wada.rearrange("(kt p) n -> p kt n", p=128)
        wadaf_v = wadaf.rearrange("(kt p) n -> p kt n", p=128)

        def slab_src(i):
            if i < 12:
                return wada_v[:, :, i * 512:(i + 1) * 512]
            return wadaf_v[:, :, (i - 12) * 512:(i - 11) * 512]

        def load_slab(i):
            sl = slabs[i % 3]
            for kh in range(2):
                S.dma("pool", sl.t[:, 4 * kh:4 * kh + 4, :], slab_src(i)[:, 4 * kh:4 * kh + 4, :], writes=[sl.b])
        load_slab(0)
        load_slab(1)
        load_1a_weights()
        for i in range(4):
            if i + 2 < 4:
                load_slab(i + 2)
            sl = slabs[i % 3]
            pb = PB[i % 2]
            for fc in range(4):
                for kt in range(8):
                    S.op("pe", lambda e, fc=fc, kt=kt, sl=sl, pb=pb: e.matmul(
                        pb.t[:, fc * 17:(fc + 1) * 17], sl.t[:, kt, fc * 128:(fc + 1) * 128], scT.t[:, kt, :],
                        start=(kt == 0), stop=(kt == 7)), reads=[sl.b, scT.b], writes=[pb.b])
            S.op("dve", lambda e, i=i, pb=pb: e.tensor_tensor(
                out=mod.t[:, 4 * i:4 * i + 4, :], in0=pb.t[:, 0:68].rearrange("p (c s) -> p c s", s=17),
                in1=prm.t[:, P_BMOD + 4 * i:P_BMOD + 4 * i + 4].unsqueeze(2).to_broadcast([128, 4, 17]), op=ALU.add),
                reads=[pb.b, prm.b], writes=[mod.b])
        def make_amod(lst):
          for k, (sci, gi) in lst:
            S.op("dve", lambda e, k=k, sci=sci, gi=gi: e.scalar_tensor_tensor(
                out=amod.t[:, 8 * k:8 * k + 8, :], in0=chunkmod(sci), scalar=1.0,
                in1=prm.t[:, P_GAIN + 8 * gi:P_GAIN + 8 * gi + 8].unsqueeze(2).to_broadcast([128, 8, 17]),
                op0=ALU.add, op1=ALU.mult), reads=[mod.b, prm.b], writes=[amod.b])
        make_amod([(0, (1, 0))])
        dump("mod", mod.t[:].rearrange("p c s -> p (c s)"), [128, 64 * 17], [mod.b])
        S.barrier()
        S.emit()
        A.lo = LO_W

        MOD_SH1, MOD_G1, MOD_SH2, MOD_G2, MOD_SHF = 0, 2, 3, 5, 6

        def expand_mod(name, src_ap, srcbufs):
            t = A.alloc(name, [128, 8, 64], F32)
            S.op("dve", lambda e: e.tensor_copy(out=t.t[:].rearrange("p k (s b) -> p k s b", b=LS),
                                                in_=src_ap.unsqueeze(3).to_broadcast([128, 8, NS, LS])),
                 reads=srcbufs, writes=[t.b])
            return t

        LO_P1 = A.lo
        mixt = [A.alloc("mixt%d" % i, [128, 8, 256], BF16) for i in range(2)]
        mixdb = [Buf("mixd%d" % i) for i in range(9)]
        a1x = A.alloc("a1x", [128, 8, 64], F32)
        sh1x = A.alloc("sh1x", [128, 8, 64], F32)

        def fill_x(t, src_ap, srcbufs):
            S.op("dve", lambda e: e.tensor_copy(out=t.t[:].rearrange("p k (s b) -> p k s b", b=LS),
                                                in_=src_ap.unsqueeze(3).to_broadcast([128, 8, NS, LS])),
                 reads=srcbufs, writes=[t.b])
        adab = [TL(a1x.t[:].rearrange("p k t -> p (k t)").bitcast(BF16).rearrange("p (k c) -> p k c", c=128), "adab0"),
                TL(sh1x.t[:].rearrange("p k t -> p (k t)").bitcast(BF16).rearrange("p (k c) -> p k c", c=128), "adab1")]
        adab[0].b = a1x.b
        adab[1].b = sh1x.b
        ADA_CH = list(range(16, 64))

        def ada_load(ci):
            c = ADA_CH[ci]
            src = wada_v[:, :, c * 128:(c + 1) * 128] if c < 48 else wadaf_v[:, :, (c - 48) * 128:(c - 47) * 128]
            S.dma("pool", adab[ci % 2].t[:], src, writes=[adab[ci % 2].b])

        def ada_compute(ci):
            c = ADA_CH[ci]
            sl = adab[ci % 2]
            pb = next_pb()
            for kt in range(8):
                S.op("pe", lambda e, kt=kt: e.matmul(pb.t[:, 0:17], sl.t[:, kt, :], scT.t[:, kt, :], start=(kt == 0), stop=(kt == 7)),
                     reads=[sl.b, scT.b], writes=[pb.b])
            S.op("dve", lambda e: e.tensor_scalar(out=mod.t[:, c, :], in0=pb.t[:, 0:17], scalar1=prm.t[:, P_BMOD + c:P_BMOD + c + 1],
                                                  scalar2=None, op0=ALU.add), reads=[pb.b, prm.b], writes=[mod.b])
        ada_state = [0, 0]

        def ada_step():
            if ada_state[1] >= len(ADA_CH):
                return
            while ada_state[0] < min(len(ADA_CH), ada_state[1] + 2):
                ada_load(ada_state[0])
                ada_state[0] += 1
            ada_compute(ada_state[1])
            ada_state[1] += 1

        ckpt("setup0")
        NTM = 256
        xtm = A.alloc("xtm", [128, 2, D], F32)
        xn = A.alloc("xn", [128, 2, D], BF16)
        nstat = A.alloc("nstat", [128, 4], F32)
        uT = A.alloc("uT", [128, 8, NTM], BF16)
        xpad = A.alloc("xpad", [128, 8, NTM + 4], BF16)
        xtail = A.alloc("xtail", [128, 8, 64], F32)
        cvst = A.alloc("cvst", [128, 8, NS, 3], F32)
        S.dma("sp", cvst.t[:].rearrange("p c s k -> p (c s k)"), stconv_d, writes=[cvst.b])
        dgc = A.alloc("dgc", [128, 8, 4, 128], BF16)
        for ct_ in range(8):
            for k_ in range(4):
                S.op("act", lambda e, ct_=ct_, k_=k_: e.activation(
                    out=dgc.t[:, ct_, k_, :], in_=ident, func=AF.Copy,
                    scale=prm.t[:, P_CONV + 5 * ct_ + k_:P_CONV + 5 * ct_ + k_ + 1]), reads=[cst.b, prm.b], writes=[dgc.b])
        xsT = A.alloc("xsT", [128, 4, NTM], F32)
        BCT = A.alloc("BCT", [128, 4, NTM], BF16)
        szT = A.alloc("szT", [128, 4, NTM], BF16)
        u5Ts = [A.alloc("u5T%d" % i, [128, 4, NTM], BF16) for i in range(2)]
        dtT = A.alloc("dtT", [8, 2, NTM], F32)
        cacc = [A.alloc("cacc0", [128, NTM], F32)] * 2
        y5pre = A.alloc("y5pre", [128, 4, NTM], F32)
        g5 = A.alloc("g5", [128, 4, NTM], BF16)
        sgl = A.alloc("sgl", [128, NTM], F32)
        dtm_l = [A.alloc("dtm%d" % i, [128, 16], F32) for i in range(2)]
        acs_l = [A.alloc("acs%d" % i, [128, 8], F32) for i in range(2)]
        dec_l = [A.alloc("dec%d" % i, [128, 8], F32) for i in range(2)]
        dtdec_l = [A.alloc("dtdec%d" % i, [128, 8], F32) for i in range(2)]
        Xtm = A.alloc("Xtm", [128, 8, 64], BF16)
        Xdec = A.alloc("Xdec", [128, 8, 64], BF16)
        Btm = A.alloc("Btm", [128, 2, 128], BF16)
        big1 = A.alloc("big1", [128, 8, 128], F32)
        big2 = A.alloc("big2", [128, 8, 128], F32)
        MT = A.alloc("MT", [128, 8, 128], BF16)
        eA = A.alloc("eA", [128, 8, 128], F32)
        CdT = A.alloc("CdT", [128, 8, 128], BF16)
        ST = A.alloc("ST", [128, 8, 64], F32)
        STb = A.alloc("STb", [128, 8, 64], BF16)
        sts5 = alias("sts5", ST.t[:].rearrange("p h q -> p (h q)").rearrange("p (a s q) -> p a s q", a=2, s=16), ST.b)
        yg = A.alloc("yg", [128, 4, 128], F32)
        ysq = alias("ysq", big1.t[:, 4:8, :], big1.b)
        rsb = A.alloc("rsb", [128, 2, 128], F32)
        ysqb = A.alloc("ysqb", [128, 4, 128], BF16)
        onesb1 = A.alloc("onesb1", [128, 128], BF16)
        S.op("dve", lambda e: e.memset(onesb1.t[:], 1.0), writes=[onesb1.b])
        h0n = [alias("h0n0", xtm.t[:, 1, 0:512].rearrange("p (a n) -> p a n", n=128), xtm.sub(1)),
               alias("h0n1", xtm.t[:, 0, 0:512].rearrange("p (a n) -> p a n", n=128), xtm.sub(0))]
        h0T = [A.alloc("h0T%d" % i, [128, 8, 64], BF16) for i in range(2)]
        Bj = [A.alloc("Bj%d" % i, [128, 2, 128], BF16) for i in range(2)]
        hn = [alias("hn0", xtm.t[:, 1, 512:1024].rearrange("p (a n) -> p a n", n=128), xtm.sub(1)),
              alias("hn1", xtm.t[:, 0, 512:1024].rearrange("p (a n) -> p a n", n=128), xtm.sub(0))]
        decfm = A.alloc("decfm", [128, 4, 16], F32)
        dAx = alias("dAx", big1.t[:, 0:4, :].rearrange("p a (b c) -> p (a b) c", c=64), big1.b)
        s5g = [[A.alloc("s5g%d%d" % (j, i), [128, 512], F32) for i in range(2)] for j in range(2)]
        s5t34 = [A.alloc("s5t%d" % i, [128, 512], F32) for i in (2, 3)]
        s5vb = [A.alloc("s5vb%d" % i, [128, 512], F32) for i in range(2)]
        s5k = [0]
        s5h = [[A.alloc("s5h%d%d" % (j, i), [128, 512], BF16) for i in range(4)] for j in range(2)]
        s5CTn = A.alloc("s5CTn", [128, 16, 32], BF16)
        s5c = A.alloc("s5c", [128, 4, 16], F32)
        busd = [[A.alloc("bus%d%d" % (j, i), [128, 512], F32) for i in range(2)] for j in range(2)]
        dg5 = A.alloc("dg5", [128, 4, 128], BF16)
        for q_ in range(4):
            S.op("act", lambda e, q_=q_: e.activation(out=dg5.t[:, q_, :], in_=ident, func=AF.Copy,
                                                      scale=prm.t[:, P_S5M + q_:P_S5M + q_ + 1]),
                 reads=[cst.b, prm.b], writes=[dg5.b])
        S.op("dve", lambda e: e.tensor_scalar(out=s5CT.t[:, 1], in0=s5CT.t[:, 1], scalar1=-1.0, scalar2=None, op0=ALU.mult),
             reads=[s5CT.b], writes=[s5CT.b])
        S.op("dve", lambda e: e.tensor_scalar(out=s5CTn.t[:], in0=s5CT.t[:, 0], scalar1=-1.0, scalar2=None, op0=ALU.mult),
             reads=[s5CT.b], writes=[s5CTn.b])
        print("arena after p1a allocs: lo=%d hi=%d (words)" % (A.lo, A.hi))

        S.op("dve", lambda e: e.memset(xpad.t[:, :, 0:3], 0.0), writes=[xpad.b])
        S.op("dve", lambda e: e.memset(ST.t[:], 0.0), writes=[ST.b])
        S.op("dve", lambda e: e.memset(STb.t[:], 0.0), writes=[STb.b])

        import os as _os3
        ENG_OUTROT = _os3.environ.get("K_OUTROT", "dve")
        ENG_ADDS = _os3.environ.get("K_ADDS", "dve")
        TILES_A = [(i * 256, 256, False) for i in range(8)] + [(SEQ, 64, True)]

        def load_x(ti):
            t0, NT, is_s = TILES_A[ti]
            for blk in range((NT + 127) // 128):
                rows = min(128, NT - blk * 128)
                S.dma("sp", xtm.t[0:rows, blk, :], xin[t0 + blk * 128:t0 + blk * 128 + rows, :], writes=[xtm.sub(blk)])

        a1 = lambda kt: amod.t[:, kt, 0:1]
        sh1 = lambda kt: mod.t[:, 8 * MOD_SH1 + kt, 0:1]
        cw = lambda ct, k: prm.t[:, P_CONV + 5 * ct + k:P_CONV + 5 * ct + k + 1]
        IN_CHUNKS = [("dt", 0, 1536, 8)] + [("z", i, i * 128, 128) for i in range(4)] + \
                    [("xbc", i, 512 + i * 128, 128) for i in range(8)] + [("u5", i, 1544 + i * 128, 128) for i in range(4)]

        load_x(0)
        pbi = [0]

        def next_pb():
            pbi[0] ^= 1
            return PB[pbi[0]]

        ckpt("pre")
        def chain1(ti):
            t0, NT, is_s = TILES_A[ti]
            u5T = u5Ts[ti % 2]
            nblk = (NT + 127) // 128
            T = 128 if not is_s else 64
            tri = cst.t[0:T, C_TRI:C_TRI + T] if not is_s else cst.t[0:T, C_TRI64:C_TRI64 + T]
            neg = cst.t[0:T, C_NEG:C_NEG + T] if not is_s else cst.t[0:T, C_NEG64:C_NEG64 + T]
            sego = onesf.t[0:T, 0:T] if not is_s else cst.t[0:T, C_SEG64:C_SEG64 + T]
            segi = cst.t[0:64, C_SEGI:C_SEGI + 16]

            def dt_prep(ck):
                c0 = ck * T
                cs_ = slice(c0, c0 + T)
                dtm, acs, dec, dtdec = dtm_l[ck], acs_l[ck], dec_l[ck], dtdec_l[ck]
                pc = 0 if ck == 0 else 480
                S.op("pe", lambda e: e.transpose(PB[4].t[0:T, pc:pc + 8], dtT.t[:, 0, cs_], cst.t[0:8, C_ID:C_ID + 8]),
                     reads=[dtT.b, cst.b], writes=[PB[4].sub("sm")])
                S.op("pe", lambda e: e.transpose(PB[4].t[0:T, pc + 8:pc + 16], dtT.t[:, 1, cs_], cst.t[0:8, C_ID:C_ID + 8]),
                     reads=[dtT.b, cst.b], writes=[PB[4].sub("sm")])
                S.op("act", lambda e: e.activation(out=dtm.t[0:T, :], in_=PB[4].t[0:T, pc:pc + 16], func=AF.Copy),
                     reads=[PB[4].sub("sm")], writes=[dtm.b])
                S.op("pe", lambda e: e.matmul(PB[4].t[0:T, pc + 16:pc + 24], tri, dtm.t[0:T, 8:16], start=True, stop=True),
                     reads=[dtm.b, cst.b], writes=[PB[4].sub("sm")])
                S.op("pe", lambda e: e.matmul(PB[4].t[0:T, pc + 24:pc + 32], sego, dtm.t[0:T, 8:16], start=True, stop=True),
                     reads=[dtm.b, cst.b, onesf.b], writes=[PB[4].sub("sm")])
                S.op("act", lambda e: e.activation(out=acs.t[0:T, :], in_=PB[4].t[0:T, pc + 16:pc + 24], func=AF.Copy),
                     reads=[PB[4].sub("sm")], writes=[acs.b])
                S.op("dve", lambda e: TT(e, dec.t[0:T, :], PB[4].t[0:T, pc + 24:pc + 32], acs.t[0:T, :], ALU.subtract),
                     reads=[PB[4].sub("sm"), acs.b], writes=[dec.b])
                S.op("act", lambda e: e.activation(out=dec.t[0:T, :], in_=dec.t[0:T, :], func=AF.Exp), reads=[dec.b], writes=[dec.b])
                S.op("dve", lambda e: TT(e, dtdec.t[0:T, :], dtm.t[0:T, 0:8], dec.t[0:T, :], ALU.mult),
                     reads=[dtm.b, dec.b], writes=[dtdec.b])
            for blk in range(nblk):
                rows = min(128, NT - blk * 128)
                xb = xtm.sub(blk)
                S.op("act", lambda e, blk=blk, rows=rows: e.activation(
                    out=xn.t[0:rows, blk, :], in_=xtm.t[0:rows, blk, :], func=AF.Square, accum_out=nstat.t[0:rows, blk:blk + 1]),
                    reads=[xb], writes=[xn.sub(blk), nstat.sub(blk)])
                S.op("act", lambda e, blk=blk, rows=rows: e.activation(
                    out=nstat.t[0:rows, 2 + blk:3 + blk], in_=nstat.t[0:rows, blk:blk + 1], func=AF.Ln, scale=1.0 / D, bias=EPS),
                    reads=[nstat.sub(blk)], writes=[nstat.sub(blk)])
                S.op("act", lambda e, blk=blk, rows=rows: e.activation(out=nstat.t[0:rows, 2 + blk:3 + blk],
                                                                        in_=nstat.t[0:rows, 2 + blk:3 + blk], func=AF.Exp, scale=-0.5),
                     reads=[nstat.sub(blk)], writes=[nstat.sub(blk)])
                S.op("act", lambda e, blk=blk, rows=rows: e.activation(
                    out=xn.t[0:rows, blk, :], in_=xtm.t[0:rows, blk, :], func=AF.Copy, scale=nstat.t[0:rows, 2 + blk:3 + blk]),
                    reads=[xb, nstat.sub(blk)], writes=[xn.sub(blk)])
            ckpt("Aa%d" % ti)
            if ti + 1 < len(TILES_A):
                load_x(ti + 1)
            ckpt("Ab%d" % ti)
            for kt in range(8):
                xb_ = 2 + (kt % 2)
                pslot = PB[xb_].b
                for blk in range(nblk):
                    rows = min(128, NT - blk * 128)
                    S.op("pe", lambda e, kt=kt, blk=blk, rows=rows: e.transpose(
                        pbf(xb_)[:, blk * 128:blk * 128 + rows],
                        xn.t[0:rows, blk, kt * 128:(kt + 1) * 128], identb.t[0:rows, 0:rows]),
                        reads=[xn.sub(blk), identb.b], writes=[pslot])
                src = pbf(xb_)[:, 0:NT]
                if not is_s:
                    S.op("act", lambda e, kt=kt, src=src: e.activation(out=uT.t[:, kt, 0:NT], in_=src, func=AF.Identity,
                                                                       scale=a1(kt), bias=sh1(kt)),
                         reads=[pslot, amod.b, mod.b], writes=[uT.sub(kt)])
                else:
                    S.op("dve", lambda e, kt=kt, src=src: TT(e, cacc[0].t[:, 0:NT], src, a1x.t[:, kt, :], ALU.mult),
                         reads=[pslot, a1x.b], writes=[cacc[0].b])
                    S.op("dve", lambda e, kt=kt: TT(e, uT.t[:, kt, 0:NT], cacc[0].t[:, 0:NT], sh1x.t[:, kt, :], ALU.add),
                         reads=[cacc[0].b, sh1x.b], writes=[uT.sub(kt)])
            ckpt("A%d" % ti)
            if ti == 0:
                dump("uT", uT.t[:].rearrange("p k t -> p (k t)"), [128, 8 * NTM], uT.allb())

            yield
            if is_s:
                xps = xpad.t[:, :, 0:NS * 7].rearrange("p c (s k) -> p c s k", k=7)
                S.op("act", lambda e: e.activation(out=xps[:, :, :, 0:3], in_=cvst.t[:], func=AF.Copy), reads=[cvst.b], writes=[xpad.b])
            for (kind, i, c0, M) in IN_CHUNKS:
                yield
                pb = next_pb()
                for kt in range(8):
                    S.op("pe", lambda e, kt=kt, c0=c0, M=M, pb=pb: e.matmul(
                        pb.t[0:M, 0:NT], win_sb.t[:, kt, c0:c0 + M], uT.t[:, kt, 0:NT], start=(kt == 0), stop=(kt == 7)),
                        reads=[win_sb.sub(kt // 2), uT.sub(kt)], writes=[pb.b])
                if kind == "z":
                    S.op("act", lambda e, i=i, pb=pb: e.activation(out=szT.t[:, i, 0:NT], in_=pb.t[:, 0:NT], func=AF.Silu),
                         reads=[pb.b], writes=[szT.b])
                elif kind == "xbc":
                    if not is_s:
                        S.op("act", lambda e, i=i, pb=pb: e.activation(out=xpad.t[:, i, 3:3 + NT], in_=pb.t[:, 0:NT], func=AF.Copy),
                             reads=[pb.b], writes=[xpad.b])
                        if ti == 7:
                            S.op("act", lambda e, i=i, pb=pb: e.activation(out=xtail.t[:, i, 0:3], in_=pb.t[:, NT - 3:NT], func=AF.Copy),
                                 reads=[pb.b], writes=[xtail.b])
                    else:
                        S.op("act", lambda e, i=i, pb=pb: e.activation(
                            out=xps[:, i, :, 3:7], in_=pb.t[:, 0:NT].rearrange("p (s k) -> p s k", k=LS), func=AF.Copy),
                            reads=[pb.b], writes=[xpad.b])
                        S.op("act", lambda e, i=i, pb=pb: e.activation(out=xtail.t[:, i, 0:NT], in_=pb.t[:, 0:NT], func=AF.Copy),
                             reads=[pb.b], writes=[xtail.b])
                elif kind == "dt":
                    S.op("act", lambda e, pb=pb: e.activation(out=dtT.t[:, 1, 0:NT], in_=pb.t[0:8, 0:NT], func=AF.Exp,
                                                              bias=ssd8.t[:, 0:1]), reads=[pb.b, ssd8.b], writes=[dtT.b])
                    S.op("act", lambda e: e.activation(out=dtT.t[:, 0, 0:NT], in_=dtT.t[:, 1, 0:NT], func=AF.Ln, bias=1.0),
                         reads=[dtT.b], writes=[dtT.b])
                    S.op("dve", lambda e: e.tensor_scalar(out=dtT.t[:, 1, 0:NT], in0=dtT.t[:, 0, 0:NT], scalar1=ssd8.t[:, 1:2],
                                                          scalar2=None, op0=ALU.mult), reads=[dtT.b, ssd8.b], writes=[dtT.b])
                    for ck_ in range(NT // T):
                        yield
                        dt_prep(ck_)
                else:
                    S.op("act", lambda e, i=i, pb=pb: e.activation(out=u5T.t[:, i, 0:NT], in_=pb.t[:, 0:NT], func=AF.Copy),
                         reads=[pb.b], writes=[u5T.b])

            ckpt("B%d" % ti)
            for ct in range(8):
                yield
                pb = next_pb()
                if not is_s:
                    xin_k = lambda k, ct=ct: xpad.t[:, ct, k:k + NT]
                    pbv = pb.t[:, 0:NT]
                    dst = xsT.t[:, ct, 0:NT] if ct < 4 else BCT.t[:, ct - 4, 0:NT]
                else:
                    xin_k = lambda k, ct=ct: xps[:, ct, :, k:k + LS]
                    pbv = pb.t[:, 0:NT].rearrange("p (s k) -> p s k", k=LS)
                    dst = (xsT.t[:, ct, 0:NT] if ct < 4 else BCT.t[:, ct - 4, 0:NT]).rearrange("p (s k) -> p s k", k=LS)
                for k in range(4):
                    S.op("pe", lambda e, k=k: e.matmul(pbv, dgc.t[:, ct, k, :], xin_k(k), start=(k == 0), stop=(k == 3)),
                         reads=[dgc.b, xpad.b], writes=[pb.b])
                S.op("act", lambda e: e.activation(out=dst, in_=pbv, func=AF.Silu, bias=cw(ct, 4)),
                     reads=[pb.b, prm.b], writes=[xsT.b if ct < 4 else BCT.b])
            ocv = o_conv.rearrange("p (c s k) -> p c s k", s=17, k=3)
            if is_s:
                S.op("act", lambda e: e.activation(out=cvst.t[:], in_=xtail.t[:].rearrange("p c (s k) -> p c s k", k=LS)[:, :, :, 1:4],
                                                   func=AF.Copy), reads=[xtail.b], writes=[cvst.b])
                S.dma("sp", ocv[:, :, 1:17, :], cvst.t[:], reads=[cvst.b], buf=cvst.b)
                outbufs.append(cvst.b)
            elif ti == 7:
                S.dma("sp", ocv[:, :, 0, :], xtail.t[:, :, 0:3], reads=[xtail.b], buf=xtail.b)
            if not is_s:
                S.op("dve", lambda e: e.tensor_copy(out=xpad.t[:, :, 0:3], in_=xpad.t[:, :, NT:NT + 3]),
                     reads=[xpad.b], writes=[xpad.b])
            if is_s:
                dump("xsS", xsT.t[:, :, 0:64], [128, 4, 64], [xsT.b])
                dump("ygS", yg.t[:, :, 0:64], [128, 4, 64], [yg.b])
            if ti == 0:
                dump("xsT", xsT.t[:].rearrange("p k t -> p (k t)"), [128, 4 * NTM], [xsT.b])
                dump("dtT", dtT.t[:].rearrange("p k t -> p (k t)"), [8, 2 * NTM], [dtT.b])

            ckpt("C%d" % ti)
            for ck in range(NT // T):
                c0 = ck * T
                cs_ = slice(c0, c0 + T)
                dtm, acs, dec, dtdec = dtm_l[ck], acs_l[ck], dec_l[ck], dtdec_l[ck]
                yield
                for pr in range(4):
                    S.op("pe", lambda e, pr=pr, cs_=cs_: e.transpose(PB[3].t[0:T, pr * 128:(pr + 1) * 128], xsT.t[:, pr, cs_], ident),
                         reads=[xsT.b, cst.b], writes=[PB[3].b])
                pxs = PB[3].t[0:T, :].rearrange("p (h q) -> p h q", q=64)
                for h in range(8):
                    S.op("act", lambda e, h=h: e.activation(out=Xtm.t[0:T, h, :], in_=pxs[:, h, :], func=AF.Copy, scale=dtm.t[0:T, h:h + 1]),
                         reads=[PB[3].b, dtm.b], writes=[Xtm.b])
                    S.op("act", lambda e, h=h: e.activation(out=Xdec.t[0:T, h, :], in_=pxs[:, h, :], func=AF.Copy, scale=dtdec.t[0:T, h:h + 1]),
                         reads=[PB[3].b, dtdec.b], writes=[Xdec.b])
                for g in range(2):
                    S.op("pe", lambda e, g=g, cs_=cs_: e.transpose(pbf(2)[0:T, g * 128:(g + 1) * 128], BCT.t[:, g, cs_], identb.t[:]),
                         reads=[BCT.b, identb.b], writes=[PB[2].sub(0)])
                S.op("act", lambda e: e.activation(out=Btm.t[0:T].rearrange("p g n -> p (g n)"), in_=pbf(2)[0:T, 0:256], func=AF.Copy),
                     reads=[PB[2].sub(0)], writes=[Btm.b])
                yield
                S.op("dve", lambda e: TT(e, big1.t[0:T, :, 0:T], tri.unsqueeze(1).to_broadcast([T, 8, T]),
                                         dtm.t[0:T, 8:16].unsqueeze(2).to_broadcast([T, 8, T]), ALU.mult),
                     reads=[cst.b, dtm.b], writes=[big1.b])
                for half in range(2):
                    S.op("pe", lambda e, half=half: e.matmul(
                        PB[3].t[:, 0:4 * T].rearrange("p (h l) -> p h l", l=T), onesf.t[0:T, :],
                        big1.t[0:T, 4 * half:4 * half + 4, 0:T], start=True, stop=True),
                        reads=[big1.b, onesf.b], writes=[PB[3].b])
                    yield
                    for h in range(4 * half, 4 * half + 4):
                        S.op("dve", lambda e, h=h: e.scalar_tensor_tensor(
                            out=big2.t[0:T, h, 0:T], in0=PB[3].t[0:T, (h % 4) * T:(h % 4 + 1) * T], scalar=acs.t[0:T, h:h + 1],
                            in1=neg, op0=ALU.subtract, op1=ALU.min), reads=[PB[3].b, acs.b, cst.b], writes=[big2.b])
                    S.op("act", lambda e, half=half: e.activation(
                        out=eA.t[:, 4 * half:4 * half + 4, 0:T], in_=PB[3].t[:, 0:4 * T].rearrange("p (h l) -> p h l", l=T),
                        func=AF.Exp), reads=[PB[3].b], writes=[eA.b])
                    yield
                S.op("act", lambda e: e.activation(out=big2.t[0:T, :, 0:T], in_=big2.t[0:T, :, 0:T], func=AF.Exp),
                     reads=[big2.b], writes=[big2.b])
                yield
                for g in range(2):
                    S.op("pe", lambda e, g=g, cs_=cs_: e.matmul(PB[4].t[0:T, 32 + g * 128:32 + g * 128 + T], BCT.t[:, g, cs_],
                                                                 BCT.t[:, 2 + g, cs_], start=True, stop=True),
                         reads=[BCT.b], writes=[PB[4].sub("cb")])
                cbv = PB[4].t[0:T, 32:288].rearrange("p (g l) -> p g l", l=128)[:, :, 0:T]
                S.op("dve", lambda e: TT(e, MT.t[0:T, :, 0:T].rearrange("p (g h) l -> p g h l", h=4),
                                         cbv.unsqueeze(2).to_broadcast([T, 2, 4, T]),
                                         big2.t[0:T, :, 0:T].rearrange("p (g h) l -> p g h l", h=4), ALU.mult),
                     reads=[PB[4].sub("cb"), big2.b], writes=[MT.b])
                yield
                S.op("pool", lambda e, cs_=cs_: TT(e, CdT.t[:, :, 0:T].rearrange("p (g h) l -> p g h l", h=4),
                                                   BCT.t[:, 2:4, cs_].unsqueeze(2).to_broadcast([128, 2, 4, T]),
                                                   eA.t[:, :, 0:T].rearrange("p (g h) l -> p g h l", h=4), ALU.mult),
                     reads=[BCT.b, eA.b], writes=[CdT.b])
                yield
                ypb = PB[7]
                if is_s:
                    S.op("dve", lambda e: e.tensor_copy(out=dAx.t[0:T], in_=dtm.t[0:T, 8:16].unsqueeze(2).to_broadcast([T, 8, 64])),
                         reads=[dtm.b], writes=[dAx.b])
                    for pr in range(4):
                        S.op("pe", lambda e, pr=pr: e.matmul(PB[4].t[:, 288 + pr * 16:288 + (pr + 1) * 16],
                                                             dAx.t[0:T, 2 * pr:2 * pr + 2, :], segi, start=True, stop=True),
                             reads=[dAx.b, cst.b], writes=[PB[4].sub("dec")])
                    S.op("act", lambda e: e.activation(out=decfm.t[:].rearrange("p a s -> p (a s)"), in_=PB[4].t[:, 288:352], func=AF.Exp),
                         reads=[PB[4].sub("dec")], writes=[decfm.b])
                    stv = stssd_d.rearrange("j (pr hl) p n -> j (hl p) pr n", hl=2)
                    osv = o_ssds.rearrange("j (pr hl) p n -> j (hl p) pr n", hl=2)
                    S.dma("act", h0n[0].t[:], stv[0], writes=[h0n[0].b])
                    for j in range(NS):
                        yield
                        jj = j % 2
                        if j + 1 < NS:
                            S.dma("act", h0n[1 - jj].t[:], stv[j + 1], writes=[h0n[1 - jj].b])
                        pbt = PB[jj]
                        for pr in range(4):
                            S.op("pe", lambda e, pr=pr, jj=jj, pbt=pbt: e.transpose(pbt.t[:, pr * 128:(pr + 1) * 128], h0n[jj].t[:, pr, :], ident),
                                 reads=[h0n[jj].b, cst.b], writes=[pbt.b])
                        S.op("act", lambda e, jj=jj, pbt=pbt: e.activation(out=h0T[jj].t[:].rearrange("p h q -> p (h q)"), in_=pbt.t[:, :], func=AF.Copy),
                             reads=[pbt.b], writes=[h0T[jj].b])
                        for h in range(8):
                            pr, hl = h // 2, h % 2
                            S.op("pe", lambda e, h=h, pr=pr, hl=hl, jj=jj, j=j: e.matmul(
                                ypb.t[64 * hl:64 * hl + 64, pr * T + LS * j:pr * T + LS * j + LS], h0T[jj].t[:, h, :],
                                CdT.t[:, h, LS * j:LS * j + LS], start=(j == 0 and pr == 0), stop=False, skip_group_check=True),
                                reads=[h0T[jj].b, CdT.b], writes=[ypb.b])
                        S.op("dve", lambda e, jj=jj, j=j: e.tensor_scalar(out=Bj[jj].t[0:T], in0=Btm.t[0:T], scalar1=segi[:, j:j + 1],
                                                                          scalar2=None, op0=ALU.mult),
                             reads=[Btm.b, cst.b], writes=[Bj[jj].b])
                        pby = PB[3]
                        for pr in range(4):
                            S.op("pe", lambda e, pr=pr, jj=jj, pby=pby: e.matmul(
                                pby.t[:, pr * 128:(pr + 1) * 128], Xdec.t[0:T, 2 * pr:2 * pr + 2, :], Bj[jj].t[0:T, pr // 2, :],
                                start=True, stop=True), reads=[Xdec.b, Bj[jj].b], writes=[pby.b])
                        S.op("dve", lambda e, jj=jj, j=j: TT(e, hn[jj].t[:], h0n[jj].t[:],
                                                             decfm.t[:, :, j:j + 1].to_broadcast([128, 4, 128]), ALU.mult),
                             reads=[h0n[jj].b, decfm.b], writes=[hn[jj].b])
                        S.op("dve", lambda e, jj=jj, pby=pby: TT(e, hn[jj].t[:], hn[jj].t[:],
                                                                 pby.t[:, :].rearrange("p (a n) -> p a n", n=128), ALU.add),
                             reads=[hn[jj].b, pby.b], writes=[hn[jj].b])
                        S.dma("sp", osv[j], hn[jj].t[:], reads=[hn[jj].b], buf=hn[jj].b)
                    outbufs.extend([hn[0].b, hn[1].b])
                for h in range(8):
                    pr, hl = h // 2, h % 2
                    out = ypb.t[64 * hl:64 * hl + 64, pr * T:(pr + 1) * T]
                    S.op("pe", lambda e, h=h, out=out, pr=pr: e.matmul(out, Xtm.t[0:T, h, :], MT.t[0:T, h, 0:T],
                                                                       start=(pr == 0 and not is_s), stop=is_s, skip_group_check=True),
                         reads=[Xtm.b, MT.b], writes=[ypb.b])
                    if not is_s:
                        S.op("pe", lambda e, h=h, out=out: e.matmul(out, STb.t[:, h, :], CdT.t[:, h, 0:T], start=False, stop=True,
                                                                    skip_group_check=True),
                             reads=[STb.b, CdT.b], writes=[ypb.b])
                yield
                for pr in range(4):
                    S.op("dve", lambda e, pr=pr, cs_=cs_: e.scalar_tensor_tensor(
                        out=yg.t[:, pr, 0:T], in0=xsT.t[:, pr, cs_], scalar=prm.t[:, P_SSDFM + pr:P_SSDFM + pr + 1],
                        in1=ypb.t[:, pr * T:(pr + 1) * T], op0=ALU.mult, op1=ALU.add),
                        reads=[xsT.b, prm.b, ypb.b], writes=[yg.b])
                S.op("dve", lambda e, cs_=cs_: TT(e, yg.t[:, :, 0:T], yg.t[:, :, 0:T], szT.t[:, :, cs_], ALU.mult),
                     reads=[yg.b, szT.b], writes=[yg.b])
                S.op("dve", lambda e: TT(e, ysqb.t[:, :, 0:T], yg.t[:, :, 0:T], yg.t[:, :, 0:T], ALU.mult),
                     reads=[yg.b], writes=[ysqb.b])
                for g in range(2):
                    for k in range(2):
                        S.op("pe", lambda e, g=g, k=k: e.matmul(PB[3].t[:, g * T:(g + 1) * T], onesb1.t[:], ysqb.t[:, 2 * g + k, 0:T],
                                                                start=(k == 0), stop=(k == 1)),
                             reads=[onesb1.b, ysqb.b], writes=[PB[3].b])
                S.op("act", lambda e: e.activation(out=rsb.t[:, :, 0:T], in_=PB[3].t[:, 0:2 * T].rearrange("p (g l) -> p g l", l=T),
                                                   func=AF.Ln, scale=1.0 / 256, bias=EPS), reads=[PB[3].b], writes=[rsb.b])
                S.op("act", lambda e: e.activation(out=rsb.t[:, :, 0:T], in_=rsb.t[:, :, 0:T], func=AF.Exp, scale=-0.5),
                     reads=[rsb.b], writes=[rsb.b])
                for pr in range(4):
                    S.op("dve", lambda e, pr=pr: e.scalar_tensor_tensor(
                        out=mixt[ti % 2].t[:, pr, c0:c0 + T], in0=yg.t[:, pr, 0:T],
                        scalar=prm.t[:, P_SSDFM + 4 + pr:P_SSDFM + 5 + pr], in1=rsb.t[:, pr // 2, 0:T], op0=ALU.mult, op1=ALU.mult),
                        reads=[yg.b, prm.b, rsb.b], writes=[mixt[ti % 2].sub("ssd")])
                yield
                if not is_s:
                    for g in range(2):
                        S.op("pe", lambda e, g=g: e.matmul(PB[6].t[:, g * 256:(g + 1) * 256], Btm.t[0:T, g, :],
                                                           Xdec.t[0:T, 4 * g:4 * g + 4, :], start=True, stop=True),
                             reads=[Btm.b, Xdec.b], writes=[PB[6].b])
                    S.op("dve", lambda e: TT(e, ST.t[:], ST.t[:], eA.t[:, :, T - 1:T].to_broadcast([128, 8, 64]), ALU.mult),
                         reads=[ST.b, eA.b], writes=[ST.b])
                    S.op("dve", lambda e: TT(e, ST.t[:], ST.t[:], PB[6].t[:, :].rearrange("p (h q) -> p h q", q=64), ALU.add),
                         reads=[ST.b, PB[6].b], writes=[ST.b])
                    S.op("act", lambda e: e.activation(out=STb.t[:], in_=ST.t[:], func=AF.Copy), reads=[ST.b], writes=[STb.b])
            if ti == 7:
                S.dma("sp", o_ssdp, ST.t[:].rearrange("p h q -> p (h q)"), reads=[ST.b], buf=ST.b)
                outbufs.append(ST.b)

            ckpt("D%d" % ti)
            yield

        def chain2(ti):
            t0, NT, is_s = TILES_A[ti]
            u5T = u5Ts[ti % 2]
            if is_s:
                S.dma("sp", sts5.t[:].rearrange("p a s q -> p (a s q)"), sts5_d, writes=[sts5.b])
            if not is_s:
                groups = [(list(range(16)), k * T5, T5) for k in range(NT // T5)]
            else:
                groups = [(list(range(8)), 0, 64), (list(range(8, 16)), 0, 64)]
            def emit_bu(g_):
                slist_, tk0_, ntok_ = groups[g_]
                bus = busd[g_ % 2]
                for part, pb in ((0, PB[5]), (1, PB[6])):
                    for idx, s in enumerate(slist_):
                        S.op("pe", lambda e, part=part, pb=pb, idx=idx, s=s: e.matmul(
                            pb.t[:, idx * ntok_:(idx + 1) * ntok_], s5BT.t[:, part, s, :], u5T.t[:, s // 4, tk0_:tk0_ + ntok_],
                            start=True, stop=True), reads=[s5BT.b, u5T.b], writes=[pb.b])
                S.op("act", lambda e: e.activation(out=bus[0].t[:], in_=PB[5].t[:, :], func=AF.Copy), reads=[PB[5].b], writes=[bus[0].b])
                S.op("act", lambda e: e.activation(out=bus[1].t[:], in_=PB[6].t[:, :], func=AF.Copy), reads=[PB[6].b], writes=[bus[1].b])
            def views(g_):
                slist_, tk0_, ntok_ = groups[g_]
                s0_ = slist_[0]
                if not is_s:
                    V3 = lambda ap: ap.rearrange("p (s t) -> p s t", t=T5)
                    QR, QI = Qtab.t[:, 0], Qtab.t[:, 1]
                    PR_, PI_ = Ptab.t[:, 0], Ptab.t[:, 1]
                    msk = mask32.t[:].rearrange("p s t -> p (s t)")
                    first = lambda ap: V3(ap)[:, :, 0]
                    cin_r, cin_i = s5cr.t[:, 0, :], s5cr.t[:, 1, :]
                else:
                    V3 = lambda ap: ap.rearrange("p (s q b) -> p s q b", q=NS, b=LS)
                    bc = lambda ap: ap.unsqueeze(2).to_broadcast([128, 8, NS, LS])
                    QR, QI = bc(Qtab.t[:, 0, s0_:s0_ + 8, 0:LS]), bc(Qtab.t[:, 1, s0_:s0_ + 8, 0:LS])
                    PR_, PI_ = bc(Ptab.t[:, 0, s0_:s0_ + 8, 0:LS]), bc(Ptab.t[:, 1, s0_:s0_ + 8, 0:LS])
                    msk = mask4.t[:].rearrange("p s t -> p (s t)")
                    first = lambda ap: V3(ap)[:, :, :, 0]
                    cin_r, cin_i = sts5.t[:, 0, s0_:s0_ + 8, :], sts5.t[:, 1, s0_:s0_ + 8, :]
                return V3, QR, QI, PR_, PI_, msk, first, cin_r, cin_i
            vsets = [[s5v[0], s5v[1]], [s5vb[0], s5vb[1]]]

            def mults_adds(g_):
                V3, QR, QI, PR_, PI_, msk, first, cin_r, cin_i = views(g_)
                bus = busd[g_ % 2]
                br, bi = V3(bus[0].t[:]), V3(bus[1].t[:])
                t1, t2, t3, t4 = s5t[0], s5t[1], s5t34[0], s5t34[1]
                vr, vi = vsets[g_ % 2]
                tb = [Qtab.b]
                for (o, a, b_, rd) in ((t1, QR, br, bus[0].b), (t2, QI, bi, bus[1].b), (t3, QR, bi, bus[1].b), (t4, QI, br, bus[0].b)):
                    S.op("dve", lambda e, o=o, a=a, b_=b_: TT(e, V3(o.t[:]), a, b_, ALU.mult), reads=tb + [rd], writes=[o.b])
                S.op(ENG_ADDS, lambda e: TT(e, vr.t[:], t1.t[:], t2.t[:], ALU.subtract), reads=[t1.b, t2.b], writes=[vr.b])
                S.op(ENG_ADDS, lambda e: TT(e, vi.t[:], t3.t[:], t4.t[:], ALU.add), reads=[t3.b, t4.b], writes=[vi.b])
            emit_bu(0)
            if len(groups) > 1:
                emit_bu(1)
            mults_adds(0)
            pend_y5 = [None]
            for gi_, (slist, tk0, ntok) in enumerate(groups):
                yield
                ns = len(slist)
                s0 = slist[0]
                V3, QR, QI, PR_, PI_, msk, first, cin_r, cin_i = views(gi_)
                vr, vi = vsets[gi_ % 2]
                if gi_ + 1 < len(groups):
                    mults_adds(gi_ + 1)
                    yield
                if gi_ + 2 < len(groups):
                    emit_bu(gi_ + 2)
                S.op("dve", lambda e: TT(e, first(vr.t[:]), first(vr.t[:]), cin_r, ALU.add), reads=[vr.b, s5cr.b, sts5.b], writes=[vr.b])
                S.op("dve", lambda e: TT(e, first(vi.t[:]), first(vi.t[:]), cin_i, ALU.add), reads=[vi.b, s5cr.b, sts5.b], writes=[vi.b])
                yield
                s5k[0] ^= 1
                gr, gi2 = s5g[s5k[0]][0], s5g[s5k[0]][1]
                S.op("dve", lambda e: e.tensor_tensor_scan(out=gr.t[:], data0=msk, data1=vr.t[:], initial=0.0, op0=ALU.mult, op1=ALU.add),
                     reads=[vr.b, mask32.b, mask4.b], writes=[gr.b])
                S.op("dve", lambda e: e.tensor_tensor_scan(out=gi2.t[:], data0=msk, data1=vi.t[:], initial=0.0, op0=ALU.mult, op1=ALU.add),
                     reads=[vi.b, mask32.b, mask4.b], writes=[gi2.b])
                yield
                hp = s5h[gi_ % 2]
                hr, hi = hp, hp
                for (o, a, b_) in ((hp[0], PR_, gr), (hp[1], PI_, gi2), (hp[2], PR_, gi2), (hp[3], PI_, gr)):
                    S.op(ENG_OUTROT, lambda e, o=o, a=a, b_=b_: TT(e, V3(o.t[:]), a, V3(b_.t[:]), ALU.mult),
                         reads=[Ptab.b, b_.b], writes=[o.b])
                yield
                if not is_s:
                    glr, gli = V3(gr.t[:])[:, :, T5 - 1], V3(gi2.t[:])[:, :, T5 - 1]
                    plr, pli = Ptab.t[:, 0, :, T5 - 1], Ptab.t[:, 1, :, T5 - 1]
                    c_ = lambda i: s5c.t[:, i, :]
                    outr, outi = s5cr.t[:, 0, :], s5cr.t[:, 1, :]
                else:
                    glr, gli = V3(gr.t[:])[:, :, :, LS - 1], V3(gi2.t[:])[:, :, :, LS - 1]
                    plr = Ptab.t[:, 0, s0:s0 + 8, LS - 1:LS].to_broadcast([128, 8, NS])
                    pli = Ptab.t[:, 1, s0:s0 + 8, LS - 1:LS].to_broadcast([128, 8, NS])
                    c_ = lambda i: hn[0].t[:, i, :].rearrange("p (s q) -> p s q", q=NS)
                    outr, outi = s5fin.t[:, 0, s0:s0 + 8, 1:17], s5fin.t[:, 1, s0:s0 + 8, 1:17]
                cb_ = [s5c.b, hn[0].b]
                if not is_s:
                    pl2 = Ptab.t[:, :, :, T5 - 1]
                    ca, cb2 = s5c.t[:, 0:2, :], s5c.t[:, 2:4, :]
                    S.op("dve", lambda e: TT(e, ca, pl2, glr.unsqueeze(1).to_broadcast([128, 2, 16]), ALU.mult),
                         reads=[Ptab.b, gr.b] + cb_, writes=cb_)
                    S.op("dve", lambda e: TT(e, cb2, pl2, gli.unsqueeze(1).to_broadcast([128, 2, 16]), ALU.mult),
                         reads=[Ptab.b, gi2.b] + cb_, writes=cb_)
                    S.op("dve", lambda e: TT(e, outr, c_(0), c_(3), ALU.subtract), reads=cb_, writes=[s5cr.b, s5fin.b])
                    S.op("dve", lambda e: TT(e, outi, c_(2), c_(1), ALU.add), reads=cb_, writes=[s5cr.b, s5fin.b])
                else:
                    cseq = [(c_(0), plr, glr, ALU.mult), (c_(1), pli, gli, ALU.mult), (c_(2), plr, gli, ALU.mult), (c_(3), pli, glr, ALU.mult)]
                    for (o, a, b, op) in cseq:
                        S.op("dve", lambda e, o=o, a=a, b=b, op=op: TT(e, o, a, b, op), reads=[Ptab.b, gr.b, gi2.b] + cb_, writes=cb_)
                    S.op("dve", lambda e: TT(e, outr, c_(0), c_(1), ALU.subtract), reads=cb_, writes=[s5cr.b, s5fin.b])
                    S.op("dve", lambda e: TT(e, outi, c_(2), c_(3), ALU.add), reads=cb_, writes=[s5cr.b, s5fin.b])
                yield
                def emit_y5(gi_=gi_, slist=slist, tk0=tk0, ntok=ntok, hr=hr, hi=hi):
                    y5c0 = 352
                    nq = 4 if not is_s else 2
                    for qi in range(nq):
                        q = qi if not is_s else 2 * gi_ + qi
                        S.op("pe", lambda e, q=q, qi=qi: e.matmul(PB[4].t[:, y5c0 + qi * ntok:y5c0 + (qi + 1) * ntok], dg5.t[:, q, :],
                                                                  u5T.t[:, q, tk0:tk0 + ntok], start=(qi == 0), stop=False, skip_group_check=True),
                             reads=[dg5.b, u5T.b], writes=[PB[4].sub("y5")])
                    for idx, s in enumerate(slist):
                        qi = (s // 4) if not is_s else (s // 4 - 2 * gi_)
                        out = PB[4].t[32 * (s % 4):32 * (s % 4) + 32, y5c0 + qi * ntok:y5c0 + (qi + 1) * ntok]
                        for j4, lw in enumerate((s5CT.t[:, 0, s, :], s5CTn.t[:, s, :], s5CT.t[:, 1, s, :], s5CT.t[:, 1, s, :])):
                            S.op("pe", lambda e, j4=j4, lw=lw: e.matmul(out, lw, hr[j4].t[:, idx * ntok:(idx + 1) * ntok],
                                                                        start=False, stop=(j4 == 3), skip_group_check=True,
                                                                        tile_position=(0, 32 * (s % 4))),
                                 reads=[s5CT.b, s5CTn.b, hr[j4].b], writes=[PB[4].sub("y5")])
                    q0 = 0 if not is_s else 2 * gi_
                    S.op("act", lambda e: e.activation(out=y5pre.t[:, q0:q0 + nq, tk0:tk0 + ntok],
                                                       in_=PB[4].t[:, y5c0:y5c0 + nq * ntok].rearrange("p (q t) -> p q t", t=ntok), func=AF.Copy),
                         reads=[PB[4].sub("y5")], writes=[y5pre.b])
                if pend_y5[0] is not None:
                    pend_y5[0]()
                    yield
                pend_y5[0] = emit_y5
            if pend_y5[0] is not None:
                pend_y5[0]()
                pend_y5[0] = None
                yield
            if ti == 7:
                S.op("dve", lambda e: e.tensor_copy(out=s5fin.t[:, :, :, 0], in_=s5cr.t[:]), reads=[s5cr.b], writes=[s5fin.b])
            if is_s:
                S.dma("sp", o_s5, s5fin.t[:].rearrange("p a s q -> p (a s q)"), reads=[s5fin.b], buf=s5fin.b)
                outbufs.append(s5fin.b)
            if ti == 0:
                dump("y5pre", y5pre.t[:].rearrange("p k t -> p (k t)"), [128, 4 * NTM], [y5pre.b])
            ckpt("E%d" % ti)
            yield
            S.op("act", lambda e: e.activation(out=g5.t[:, :, 0:NT], in_=y5pre.t[:, :, 0:NT], func=AF.Gelu), reads=[y5pre.b], writes=[g5.b])
            for m in range(4):
                yield
                pb = next_pb()
                for q in range(4):
                    S.op("pe", lambda e, m=m, q=q, pb=pb: e.matmul(pb.t[:, 0:NT], wglu_sb.t[:, q, m * 128:(m + 1) * 128], g5.t[:, q, 0:NT],
                                                                   start=(q == 0), stop=(q == 3)),
                         reads=[wglu_sb.b, g5.b], writes=[pb.b])
                S.op("act", lambda e, m=m, pb=pb: e.activation(out=sgl.t[:, 0:NT], in_=pb.t[:, 0:NT], func=AF.Sigmoid,
                                                               bias=prm.t[:, P_S5M + 4 + m:P_S5M + 5 + m]),
                     reads=[pb.b, prm.b], writes=[sgl.b])
                S.op("dve", lambda e, m=m: TT(e, mixt[ti % 2].t[:, 4 + m, 0:NT], g5.t[:, m, 0:NT], sgl.t[:, 0:NT], ALU.mult),
                     reads=[g5.b, sgl.b], writes=[mixt[ti % 2].sub("s5")])
            S.dma("sp", mixd[:, :, t0:t0 + NT], mixt[ti % 2].t[:, :, 0:NT], reads=mixt[ti % 2].allb(), writes=[mixdb[ti]], buf=mixdb[ti])
            ckpt("T%d" % ti)
            if ti == 0:
                dump("mix0", mixt[0].t[:, :, 0:NTM], [128, 8, NTM], mixt[0].allb())
            yield

        import os as _os
        RATIO = int(_os.environ.get("K_RATIO", "1"))
        HEAD = int(_os.environ.get("K_HEAD", "10"))
        HEADB = int(_os.environ.get("K_HEADB", "10"))

        def drive(gens, ada_every=0, head=0):
            gens = [g for g in gens if g is not None]
            n = 0
            if len(gens) > 1:
                for _ in range(head):
                    try:
                        next(gens[0])
                    except StopIteration:
                        gens.pop(0)
                        break
            while gens:
                for gi__, g in enumerate(list(gens)):
                    for _ in range((RATIO if gi__ == 0 else 1) if RATIO > 0 else (-RATIO if gi__ == 1 else 1)):
                        try:
                            next(g)
                        except StopIteration:
                            if g in gens:
                                gens.remove(g)
                            break
                n += 1
                if ada_every and n % ada_every == 0:
                    ada_step()
        ada_state[0] = 0
        drive([chain1(0)], ada_every=12)
        for ti_ in range(len(TILES_A)):
            if ti_ == 7:
                while ada_state[1] < len(ADA_CH):
                    ada_step()
                fill_x(a1x, amod.t[:, 0:8, 1:17], [amod.b])
                fill_x(sh1x, chunkmod(MOD_SH1)[:, :, 1:17], [mod.b])
                make_amod([(1, (4, 1)), (2, (7, 2))])
            drive([chain2(ti_), chain1(ti_ + 1) if ti_ + 1 < len(TILES_A) else None], ada_every=(10 if ti_ < 7 else 0), head=HEAD)
        dump("mixS", mixt[0].t[:, :, 0:64], [128, 8, 64], mixt[0].allb())
        S.barrier()
        ckpt("1a")
        A.lo = LO_GLOBAL
        x1T = A.alloc("x1T", [128, 8, NTOK], F32, top=True)
        vT = A.alloc("vT", [128, 8, NTOK], BF16, top=True)
        pre_g = [A.alloc("wgs%dt" % i, [128, 8, 256], BF16, top=True) for i in range(2)]
        pre_u = [A.alloc("wus%dt" % i, [128, 8, 256], BF16, top=True) for i in range(2)]
        wout_sb = A.alloc("wout_sb", [128, 8, D], BF16)
        wout_v = wout.rearrange("(kt p) n -> p kt n", p=128)
        for kh in range(4):
            S.dma("pool", wout_sb.t[:, 2 * kh:2 * kh + 2, :], wout_v[:, 2 * kh:2 * kh + 2, :], writes=[wout_sb.sub(kh)])
        mixb = [A.alloc("mixb%d" % i, [128, 8, 512], BF16) for i in range(2)]

        def load_mix(ti):
            t0, NT, is_s = TILES_B[ti]
            tiles_a = [i for i, (a0, n0, s0_) in enumerate(TILES_A) if a0 >= t0 and a0 < t0 + NT]
            S.dma("sp", mixb[ti % 2].t[:, :, 0:NT], mixd[:, :, t0:t0 + NT], reads=[mixdb[i] for i in tiles_a], writes=[mixb[ti % 2].b])
        wg_v = wg.rearrange("(kt p) n -> p kt n", p=128)
        wu_v = wu.rearrange("(kt p) n -> p kt n", p=128)
        for si_ in range(2):
            S.dma("pool", pre_g[si_].t[:], wg_v[:, :, si_ * 256:(si_ + 1) * 256], writes=[pre_g[si_].b])
            S.dma("pool", pre_u[si_].t[:], wu_v[:, :, si_ * 256:(si_ + 1) * 256], writes=[pre_u[si_].b])
        xtm2 = A.alloc("xtm2", [128, 4, D], F32)
        xTm = [A.alloc("xTm%d" % i, [128, 512], F32) for i in range(2)]
        sqb = [A.alloc("sqb%d" % i, [128, 512], BF16) for i in range(2)]
        onesb = A.alloc("onesb", [128, 128], BF16)
        S.op("dve", lambda e: e.memset(onesb.t[:], 1.0), writes=[onesb.b])
        tmp2 = [A.alloc("tmp2_%d" % i, [128, 512], F32) for i in range(2)]
        rstdb = [A.alloc("rstdb%d" % i, [128, 512], F32) for i in range(2)]
        g1x = expand_mod("g1x", chunkmod(MOD_G1)[:, :, 1:17], [mod.b])
        a2x = expand_mod("a2x", amod.t[:, 8:16, 1:17], [amod.b])
        sh2x = expand_mod("sh2x", chunkmod(MOD_SH2)[:, :, 1:17], [mod.b])
        print("arena p1b: lo=%d hi=%d" % (A.lo, A.hi))
        TILES_B = [(i * 512, 512, False) for i in range(4)] + [(SEQ, 64, True)]

        def load_x2(ti):
            t0, NT, is_s = TILES_B[ti]
            for blk in range((NT + 127) // 128):
                rows = min(128, NT - blk * 128)
                S.dma("sp", xtm2.t[0:rows, blk, :], xin[t0 + blk * 128:t0 + blk * 128 + rows, :], writes=[xtm2.sub(blk)])
        load_x2(0)
        load_mix(0)

        def stat_accum(src_ap, m, NT, pbs, defer=None):
            sq = sqb[m % 2]
            S.op("act", lambda e: e.activation(out=sq.t[:, 0:NT], in_=src_ap, func=AF.Square), reads=[x1T.sub(m)], writes=[sq.b])

            def mm(m=m, sq=sq):
                S.op("pe", lambda e: e.matmul(pbs.t[:, 0:NT], onesb.t[:], sq.t[:, 0:NT], start=(m == 0), stop=(m == 7)),
                     reads=[onesb.b, sq.b], writes=[pbs.b])
            if defer is None:
                mm()
            else:
                if defer[0] is not None:
                    defer[0]()
                defer[0] = mm
                if m == 7:
                    defer[0]()
                    defer[0] = None

        def stat_finish(NT, pbs, rs):
            S.op("act", lambda e: e.activation(out=rs.t[:, 0:NT], in_=pbs.t[:, 0:NT], func=AF.Ln, scale=1.0 / D, bias=EPS),
                 reads=[pbs.b], writes=[rs.b])
            S.op("act", lambda e: e.activation(out=rs.t[:, 0:NT], in_=rs.t[:, 0:NT], func=AF.Exp, scale=-0.5), reads=[rs.b], writes=[rs.b])

        def b_part1(ti):
            t0, NT, is_s = TILES_B[ti]
            nblk = (NT + 127) // 128
            tsl = slice(t0, t0 + NT)
            pbs = PB[4 + ti % 2]
            dfr = [None]
            for m in range(8):
                pbx = PB[2 + m % 2]
                xm = xTm[m % 2]
                for blk in range(nblk):
                    rows = min(128, NT - blk * 128)
                    S.op("pe", lambda e, blk=blk, rows=rows: e.transpose(
                        pbx.t[:, blk * 128:blk * 128 + rows], xtm2.t[0:rows, blk, m * 128:(m + 1) * 128], cst.t[0:rows, C_ID:C_ID + rows]),
                        reads=[xtm2.sub(blk), cst.b], writes=[pbx.b])
                S.op("act", lambda e: e.activation(out=xm.t[:, 0:NT], in_=pbx.t[:, 0:NT], func=AF.Copy), reads=[pbx.b], writes=[xm.b])
                pb = next_pb()
                for kt in range(8):
                    S.op("pe", lambda e, kt=kt: e.matmul(pb.t[:, 0:NT], wout_sb.t[:, kt, m * 128:(m + 1) * 128], mixb[ti % 2].t[:, kt, 0:NT],
                                                         start=(kt == 0), stop=(kt == 7)),
                         reads=[wout_sb.sub(kt // 2), mixb[ti % 2].b], writes=[pb.b])
                if m == 0 and ti + 1 < len(TILES_B):
                    load_mix(ti + 1)
                if not is_s:
                    S.op("dve", lambda e: e.scalar_tensor_tensor(
                        out=x1T.t[:, m, tsl], in0=pb.t[:, 0:NT], scalar=mod.t[:, 8 * MOD_G1 + m, 0:1], in1=xm.t[:, 0:NT],
                        op0=ALU.mult, op1=ALU.add), reads=[pb.b, mod.b, xm.b], writes=[x1T.sub(m)])
                else:
                    S.op("dve", lambda e: TT(e, tmp2[0].t[:, 0:NT], pb.t[:, 0:NT], g1x.t[:, m, :], ALU.mult),
                         reads=[pb.b, g1x.b], writes=[tmp2[0].b])
                    S.op("dve", lambda e: TT(e, x1T.t[:, m, tsl], tmp2[0].t[:, 0:NT], xm.t[:, 0:NT], ALU.add),
                         reads=[tmp2[0].b, xm.b], writes=[x1T.sub(m)])
                stat_accum(x1T.t[:, m, tsl], m, NT, pbs, defer=dfr)
                yield
            if ti + 1 < len(TILES_B):
                load_x2(ti + 1)
            yield

        def b_part2(ti):
            t0, NT, is_s = TILES_B[ti]
            tsl = slice(t0, t0 + NT)
            rs = rstdb[ti % 2]
            stat_finish(NT, PB[4 + ti % 2], rs)
            yield
            for m in range(8):
                tq = tmp2[m % 2]
                S.op("dve", lambda e: TT(e, tq.t[:, 0:NT], x1T.t[:, m, tsl], rs.t[:, 0:NT], ALU.mult),
                     reads=[x1T.sub(m), rs.b], writes=[tq.b])
                if not is_s:
                    S.op("act", lambda e: e.activation(out=vT.t[:, m, tsl], in_=tq.t[:, 0:NT], func=AF.Identity,
                                                       scale=amod.t[:, 8 + m, 0:1], bias=mod.t[:, 8 * MOD_SH2 + m, 0:1]),
                         reads=[tq.b, amod.b, mod.b], writes=[vT.sub(m)])
                else:
                    S.op("dve", lambda e: TT(e, tq.t[:, 0:NT], tq.t[:, 0:NT], a2x.t[:, m, :], ALU.mult),
                         reads=[tq.b, a2x.b], writes=[tq.b])
                    S.op("dve", lambda e: TT(e, vT.t[:, m, tsl], tq.t[:, 0:NT], sh2x.t[:, m, :], ALU.add),
                         reads=[tq.b, sh2x.b], writes=[vT.sub(m)])
                yield
            if ti == 0:
                dump("x1p", x1T.t[:, :, 0:256], [128, 8, 256], x1T.allb())
                dump("vp", vT.t[:, :, 0:256], [128, 8, 256], vT.allb())
        drive([b_part1(0)])
        for ti_ in range(len(TILES_B)):
            drive([b_part2(ti_), b_part1(ti_ + 1) if ti_ + 1 < len(TILES_B) else None], head=HEADB)
        S.barrier()
        ckpt("1b")

        A.lo = LO_GLOBAL
        tmp2 = [A.alloc("tmp3_%d" % i, [128, 512], F32) for i in range(2)]
        rstdb = [A.alloc("rstd3_%d" % i, [128, 512], F32) for i in range(2)]
        sqb = [A.alloc("sqb3_%d" % i, [128, 512], BF16) for i in range(2)]
        onesb = A.alloc("onesb3", [128, 128], BF16)
        S.op("dve", lambda e: e.memset(onesb.t[:], 1.0), writes=[onesb.b])
        g2x = expand_mod("g2x", chunkmod(MOD_G2)[:, :, 1:17], [mod.b])
        afx = expand_mod("afx", amod.t[:, 16:24, 1:17], [amod.b])
        shfx = expand_mod("shfx", chunkmod(MOD_SHF)[:, :, 1:17], [mod.b])
        LO_P2 = A.lo
        hT = A.alloc("hT", [128, 6, NTOK], BF16)
        wgs = pre_g + [A.alloc("wgs2", [128, 8, 256], BF16)]
        wus = pre_u + [A.alloc("wus2", [128, 8, 256], BF16)]
        wds = [A.alloc("wds%d" % i, [128, 6, D], BF16) for i in range(2)]
        sgt = [A.alloc("sgt%d" % i, [128, 512], BF16) for i in range(2)]
        print("arena p2: lo=%d hi=%d" % (A.lo, A.hi))
        wg_v = wg.rearrange("(kt p) n -> p kt n", p=128)
        wu_v = wu.rearrange("(kt p) n -> p kt n", p=128)
        wd_v = wd.rearrange("(j p) n -> p j n", p=128)
        QUARTERS = [(0, 6), (6, 12), (12, 18), (18, 22)]
        SLABS = [(q, ja + 2 * s) for q, (ja, jb) in enumerate(QUARTERS) for s in range((jb - ja) // 2)]

        def load_gu(si):
            q, j0 = SLABS[si]
            S.dma("pool", wgs[si % 3].t[:], wg_v[:, :, j0 * 128:(j0 + 2) * 128], writes=[wgs[si % 3].b])
            S.dma("pool", wus[si % 3].t[:], wu_v[:, :, j0 * 128:(j0 + 2) * 128], writes=[wus[si % 3].b])

        def load_wd(q):
            ja, jb = QUARTERS[q]
            for jh in range(0, jb - ja, 2):
                S.dma("pool", wds[q % 2].t[:, jh:jh + 2, :], wd_v[:, ja + jh:ja + jh + 2, :], writes=[wds[q % 2].b])
        assert SLABS[0][1] == 0 and SLABS[1][1] == 2
        load_wd(0)
        gbank = [0]
        si = 0
        for q, (ja, jb) in enumerate(QUARTERS):
            if q + 1 < 4:
                load_wd(q + 1)
            for s in range((jb - ja) // 2):
                if si + 2 < len(SLABS):
                    load_gu(si + 2)
                wgt, wut = wgs[si % 3], wus[si % 3]
                for jc in range(2):
                    jj = 2 * s + jc
                    for (t0, NT, is_s) in TILES_B:
                        tsl = slice(t0, t0 + NT)
                        gbank[0] ^= 1
                        pbg, pbu = PB[gbank[0]], PB[2 + gbank[0]]
                        for (wt, pb_) in ((wgt, pbg), (wut, pbu)):
                            for kt in range(8):
                                S.op("pe", lambda e, kt=kt, wt=wt, pb_=pb_: e.matmul(
                                    pb_.t[:, 0:NT], wt.t[:, kt, jc * 128:(jc + 1) * 128], vT.t[:, kt, tsl], start=(kt == 0), stop=(kt == 7)),
                                    reads=[wt.b] + vT.allb(), writes=[pb_.b])
                        sg_ = sgt[gbank[0]]
                        S.op("act", lambda e, pbg=pbg, sg_=sg_: e.activation(out=sg_.t[:, 0:NT], in_=pbg.t[:, 0:NT], func=AF.Silu),
                             reads=[pbg.b], writes=[sg_.b])
                        S.op("dve", lambda e, pbu=pbu, sg_=sg_: TT(e, hT.t[:, jj, tsl], sg_.t[:, 0:NT], pbu.t[:, 0:NT], ALU.mult),
                             reads=[sg_.b, pbu.b], writes=[hT.sub(jj)])
                si += 1
            nj = jb - ja
            wdt = wds[q % 2]
            for (t0, NT, is_s) in TILES_B:
                tsl = slice(t0, t0 + NT)
                for m in range(8):
                    pb = PB[4 + m % 2]
                    for jj in range(nj):
                        S.op("pe", lambda e, jj=jj, m=m, pb=pb: e.matmul(pb.t[:, 0:NT], wdt.t[:, jj, m * 128:(m + 1) * 128], hT.t[:, jj, tsl],
                                                                         start=(jj == 0), stop=(jj == nj - 1)),
                             reads=[wdt.b, hT.sub(jj)], writes=[pb.b])
                    if not is_s:
                        S.op("dve", lambda e, m=m, pb=pb: e.scalar_tensor_tensor(
                            out=x1T.t[:, m, tsl], in0=pb.t[:, 0:NT], scalar=mod.t[:, 8 * MOD_G2 + m, 0:1], in1=x1T.t[:, m, tsl],
                            op0=ALU.mult, op1=ALU.add), reads=[pb.b, mod.b, x1T.sub(m)], writes=[x1T.sub(m)])
                    else:
                        S.op("dve", lambda e, m=m, pb=pb: TT(e, tmp2[0].t[:, 0:NT], pb.t[:, 0:NT], g2x.t[:, m, :], ALU.mult),
                             reads=[pb.b, g2x.b], writes=[tmp2[0].b])
                        S.op("dve", lambda e, m=m: TT(e, x1T.t[:, m, tsl], tmp2[0].t[:, 0:NT], x1T.t[:, m, tsl], ALU.add),
                             reads=[tmp2[0].b, x1T.sub(m)], writes=[x1T.sub(m)])
        S.barrier()
        ckpt("ffn")
        A.lo = LO_P2
        yTs = [A.alloc("yT%d" % i, [128, 8, 512], F32) for i in range(2)]
        ytm = [A.alloc("ytm%d" % i, [128, D], F32) for i in range(2)]
        print("arena final: lo=%d hi=%d" % (A.lo, A.hi))
        oi = [0]

        def f_part1(ti):
            t0, NT, is_s = TILES_B[ti]
            tsl = slice(t0, t0 + NT)
            yT = yTs[ti % 2]
            pbs = PB[6 + ti % 2]
            rs = rstdb[ti % 2]
            for m in range(8):
                stat_accum(x1T.t[:, m, tsl], m, NT, pbs)
                if m % 2 == 1:
                    yield
            stat_finish(NT, pbs, rs)
            yield
            for m in range(8):
                tq = tmp2[m % 2]
                S.op("dve", lambda e: TT(e, tq.t[:, 0:NT], x1T.t[:, m, tsl], rs.t[:, 0:NT], ALU.mult),
                     reads=[x1T.sub(m), rs.b], writes=[tq.b])
                if not is_s:
                    S.op("act", lambda e: e.activation(out=yT.t[:, m, 0:NT], in_=tq.t[:, 0:NT], func=AF.Identity,
                                                       scale=amod.t[:, 16 + m, 0:1], bias=mod.t[:, 8 * MOD_SHF + m, 0:1]),
                         reads=[tq.b, amod.b, mod.b], writes=[yT.sub(m)])
                else:
                    S.op("dve", lambda e: TT(e, tq.t[:, 0:NT], tq.t[:, 0:NT], afx.t[:, m, :], ALU.mult),
                         reads=[tq.b, afx.b], writes=[tq.b])
                    S.op("dve", lambda e: TT(e, yT.t[:, m, 0:NT], tq.t[:, 0:NT], shfx.t[:, m, :], ALU.add),
                         reads=[tq.b, shfx.b], writes=[yT.sub(m)])
                yield

        def f_part2(ti):
            t0, NT, is_s = TILES_B[ti]
            yT = yTs[ti % 2]
            for blk in range((NT + 127) // 128):
                rows = min(128, NT - blk * 128)
                yo = ytm[oi[0] % 2]
                oi[0] += 1
                for half in range(2):
                    pbt = PB[half]
                    for k4 in range(4):
                        kt = 4 * half + k4
                        S.op("pe", lambda e, kt=kt, k4=k4: e.transpose(
                            pbt.t[0:rows, k4 * 128:(k4 + 1) * 128], yT.t[:, kt, blk * 128:blk * 128 + rows], ident),
                            reads=[yT.sub(kt), cst.b], writes=[pbt.b])
                    if half == 0:
                        S.op("act", lambda e: e.activation(out=yo.t[0:rows, 0:512], in_=pbt.t[0:rows, :], func=AF.Copy),
                             reads=[pbt.b], writes=[yo.b])
                    else:
                        S.op("dve", lambda e: e.tensor_copy(out=yo.t[0:rows, 512:1024], in_=pbt.t[0:rows, :]),
                             reads=[pbt.b], writes=[yo.b])
                    yield
                S.dma("sp", yout[t0 + blk * 128:t0 + blk * 128 + rows, :], yo.t[0:rows, :], reads=[yo.b], buf=yo.b)
        import os as _os2
        if True:
            for ti_ in range(len(TILES_B)):
                drive([f_part1(ti_)])
                drive([f_part2(ti_)])
        else:
            drive([f_part1(0)])
            for ti_ in range(len(TILES_B)):
                drive([f_part2(ti_), f_part1(ti_ + 1) if ti_ + 1 < len(TILES_B) else None])
        S.barrier()
    return nc, dumps


def _prep_inputs(inp):
    cstv = _consts()
    prmv = _params(inp)
    BT, CT = _s5mats(inp)
    maps = []
    for i in range(NCORES):
        m = {}
        m["xin"] = np.ascontiguousarray(np.concatenate(
            [inp["x_prompt"][i], inp["x_sample"][NS * i:NS * (i + 1)].reshape(NS * LS, D)], axis=0), dtype=np.float32)
        m["cin"] = np.ascontiguousarray(np.concatenate(
            [inp["c_prompt"][i:i + 1], inp["c_sample"][NS * i:NS * (i + 1)]], axis=0), dtype=np.float32)
        m["wada"] = np.ascontiguousarray(inp["w_ada"][0], dtype=np.float32)
        m["wadaf"] = np.ascontiguousarray(inp["w_ada_f"], dtype=np.float32)
        m["win"] = np.ascontiguousarray(inp["w_in"][0], dtype=np.float32)
        m["wglu"] = np.ascontiguousarray(inp["w_glu"][0], dtype=np.float32)
        m["wout"] = np.ascontiguousarray(inp["w_out"][0], dtype=np.float32)
        m["wg"] = np.ascontiguousarray(inp["w_ffn_gate"][0], dtype=np.float32)
        m["wu"] = np.ascontiguousarray(inp["w_ffn_up"][0], dtype=np.float32)
        m["wd"] = np.ascontiguousarray(inp["w_ffn_down"][0], dtype=np.float32)
        m["cst"] = cstv
        m["prm"] = prmv
        m["s5bt"] = BT.reshape(128, -1)
        m["s5ct"] = CT.reshape(128, -1)
        m["stssd"] = np.ascontiguousarray(inp["state_ssd"][0, NS * i:NS * (i + 1)], dtype=np.float32)
        sc = inp["state_conv"][0, NS * i:NS * (i + 1)]
        m["stconv"] = np.ascontiguousarray(
            sc.reshape(NS, 3, 8, 128).transpose(3, 2, 0, 1).reshape(128, -1), dtype=np.float32)
        sr = inp["state_s5_re"][0, NS * i:NS * (i + 1)]
        si = inp["state_s5_im"][0, NS * i:NS * (i + 1)]
        st = np.stack([sr, si], 0).reshape(2, NS, 16, 128).transpose(3, 0, 2, 1)
        m["sts5"] = np.ascontiguousarray(st.reshape(128, -1), dtype=np.float32)
        maps.append(m)
    return maps


_CACHE = {}


def kernel(**inputs):
    inp = {k: np.asarray(v) for k, v in inputs.items()}
    if "nc" not in _CACHE:
        _CACHE["nc"] = build()[0]
    nc = _CACHE["nc"]
    maps = _prep_inputs(inp)
    res = run_bass_kernel_spmd(nc, maps, core_ids=list(range(NCORES)))
    R = res.results
    y_p = np.stack([R[i]["yout"][:SEQ] for i in range(NCORES)], 0)
    y_s = np.concatenate([R[i]["yout"][SEQ:].reshape(NS, LS, D) for i in range(NCORES)], 0)
    ssd_p = np.stack([R[i]["o_ssdp"].reshape(128, 8, 64).transpose(1, 2, 0) for i in range(NCORES)], 0)[None]
    ssd_s = np.concatenate([R[i]["o_ssds"] for i in range(NCORES)], 0)[None]
    conv = [R[i]["o_conv"].reshape(128, 8, 17, 3).transpose(2, 3, 1, 0).reshape(17, 3, 1024) for i in range(NCORES)]
    conv_p = np.stack([c[0] for c in conv], 0)[None]
    conv_s = np.concatenate([c[1:] for c in conv], 0)[None]
    s5 = [R[i]["o_s5"].reshape(128, 2, 16, 17).transpose(1, 3, 2, 0).reshape(2, 17, 32, 64) for i in range(NCORES)]
    re_p = np.stack([s[0, 0] for s in s5], 0)[None]
    re_s = np.concatenate([s[0, 1:] for s in s5], 0)[None]
    im_p = np.stack([s[1, 0] for s in s5], 0)[None]
    im_s = np.concatenate([s[1, 1:] for s in s5], 0)[None]
    f = lambda a: np.ascontiguousarray(a, dtype=np.float32)
    return (f(y_p), f(y_s), f(ssd_p), f(ssd_s), f(conv_p), f(conv_s), f(re_p), f(re_s), f(im_p), f(im_s))
```

```python
import math
import numpy as np
from contextlib import ExitStack
import concourse.bass as bass
import concourse.mybir as mybir
from concourse.bass_utils import run_bass_kernel_spmd

F32 = mybir.dt.float32
BF16 = mybir.dt.bfloat16
I32 = mybir.dt.int32
AF = mybir.ActivationFunctionType
ALU = mybir.AluOpType

NCORES = 8
D = 1024
SEQ = 2048
NS = 16
LS = 4
NTOK = SEQ + NS * LS
DFF = 2816
NJ = DFF // 128
INP = 2056
EPS = 1e-6
T5 = 32
TILES = [(0, 512), (512, 512), (1024, 512), (1536, 512), (2048, 64)]
PI = math.pi


class Buf:
    def __init__(self, name):
        self.name = name
        self.w = None
        self.r = []
        self.dsem = None
        self.dcnt = 0


class TL:
    def __init__(self, t, name):
        self.t = t
        self.name = name
        self.b = Buf(name)
        self.subs = {}

    def sub(self, k):
        if getattr(self, "nosub", False):
            return self.b
        if k not in self.subs:
            self.subs[k] = Buf("%s_%s" % (self.name, k))
        return self.subs[k]

    def allb(self):
        return [self.b] + list(self.subs.values())

    def __getitem__(self, k):
        return self.t[k]


class Sched:
    ENG = ["pe", "act", "dve", "pool", "sp"]

    def __init__(self, nc, es):
        self.nc = nc
        self.es = es
        self.eobj = {"pe": nc.tensor, "act": nc.scalar, "dve": nc.vector, "pool": nc.gpsimd, "sp": nc.sync}
        self.cnt = {e: 0 for e in self.ENG}
        self.sem = {e: es.enter_context(nc.semaphore("s_" + e)) for e in self.ENG}
        self.seen = {e: {} for e in self.ENG}
        self.dbufs = []
        self.ninst = 0
        self.dead = False
        self.pe_pending = None

    def _flush_pe(self):
        if self.pe_pending is not None:
            self.pe_pending.then_inc(self.sem["pe"], 1)
            self.cnt["pe"] += 1
            self.pe_pending = None

    def _deps(self, eng, reads, writes, xreads=()):
        deps = []
        for b in reads:
            if b.w is not None:
                deps.append(b.w)
            if b in xreads:
                deps.extend(r for r in b.r if r[2] != eng)
        for b in writes:
            if b.w is not None:
                deps.append(b.w)
            deps.extend(b.r)
        waits = {}
        for (sem, val, key) in deps:
            if key == "pe" and eng == "pe":
                continue
            if self.seen[eng].get(key, 0) >= val:
                continue
            if key == "pe" and val > self.cnt["pe"]:
                self._flush_pe()
            if key not in waits or waits[key][1] < val:
                waits[key] = (sem, val)
        for key, (sem, val) in waits.items():
            self.seen[eng][key] = val
        return list(waits.values())

    def op(self, eng, fn, reads=(), writes=()):
        if self.dead:
            return None
        xr = [b for b in reads if getattr(b, "excl", False)]
        waits = self._deps(eng, reads, writes, xreads=xr)
        e = self.eobj[eng]
        for (s_, v_) in waits:
            e.wait_ge(s_, v_)
        if eng == "pe":
            self.pe_pending = fn(e)
            tok = (self.sem[eng], self.cnt[eng] + 1, eng)
        else:
            self.cnt[eng] += 1
            tok = (self.sem[eng], self.cnt[eng], eng)
            fn(e).then_inc(self.sem[eng], 1)
        for b in reads:
            b.r.append(tok)
        for b in writes:
            b.w = tok
            b.r = []
        self.ninst += 1
        return tok

    def dma(self, eng, out, in_, reads=(), writes=(), buf=None, **kw):
        if self.dead:
            return None
        waits = self._deps(eng, reads, writes)
        if buf is None:
            buf = writes[0] if writes else reads[0]
        if buf.dsem is None:
            buf.dsem = self.es.enter_context(self.nc.semaphore("d_" + buf.name))
            self.dbufs.append(buf)
        buf.dcnt += 16
        tok = (buf.dsem, buf.dcnt, "d_" + buf.name)
        e = self.eobj[eng]
        for (s_, v_) in waits:
            e.wait_ge(s_, v_)
        e.dma_start(out=out, in_=in_, **kw).then_inc(buf.dsem, 16)
        for b in reads:
            b.r.append(tok)
        for b in writes:
            b.w = tok
            b.r = []
        self.ninst += 1
        return tok

    def barrier(self):
        if self.dead:
            return
        self._flush_pe()
        for e in self.ENG:
            waits = []
            for o in self.ENG:
                if o != e and self.cnt[o] > self.seen[e].get(o, 0):
                    waits.append((self.sem[o], self.cnt[o]))
                    self.seen[e][o] = self.cnt[o]
            for b in self.dbufs:
                key = "d_" + b.name
                if b.dcnt > self.seen[e].get(key, 0):
                    waits.append((b.dsem, b.dcnt))
                    self.seen[e][key] = b.dcnt
            for (s_, v_) in waits:
                self.eobj[e].wait_ge(s_, v_)

    def emit(self):
        pass


C_ID = 0
C_TRI = 128
C_NEG = 256
C_TRI64 = 384
C_NEG64 = 512
C_SEG64 = 640
C_SEGI = 768
CST_W = 784

P_BMOD = 0
P_GAIN = 64
P_CONV = 88
P_SSDFM = 128
P_S5P = 136
P_S5M = 184
P_SSD8 = 192
PRM_W = 194


def _consts():
    c = np.zeros((128, CST_W), np.float32)
    c[:, C_ID:C_ID + 128] = np.eye(128, dtype=np.float32)
    s = np.arange(128)[:, None]
    l = np.arange(128)[None, :]
    c[:, C_TRI:C_TRI + 128] = (s <= l).astype(np.float32)
    c[:, C_NEG:C_NEG + 128] = np.where(l >= s, 0.0, -30000.0)
    same = (s // LS == l // LS) & (s < 64) & (l < 64)
    c[:, C_TRI64:C_TRI64 + 128] = ((s <= l) & same).astype(np.float32)
    c[:, C_NEG64:C_NEG64 + 128] = np.where((l >= s) & same, 0.0, -30000.0)
    c[:, C_SEG64:C_SEG64 + 128] = same.astype(np.float32)
    j = np.arange(16)[None, :]
    c[:, C_SEGI:C_SEGI + 16] = ((s // LS == j) & (s < 64)).astype(np.float32)
    return c


def _fm(v, nt):
    return np.ascontiguousarray(np.asarray(v, np.float32).reshape(nt, 128).T)


def _params(inp):
    p = np.zeros((128, PRM_W), np.float32)
    p[:, P_BMOD:P_BMOD + 48] = _fm(inp["b_ada"][0], 48)
    p[:, P_BMOD + 48:P_BMOD + 64] = _fm(inp["b_ada_f"], 16)
    p[:, P_GAIN:P_GAIN + 8] = _fm(inp["norm1_g"][0], 8)
    p[:, P_GAIN + 8:P_GAIN + 16] = _fm(inp["norm2_g"][0], 8)
    p[:, P_GAIN + 16:P_GAIN + 24] = _fm(inp["normf_g"], 8)
    cw = inp["conv_w"][0]
    cv = np.zeros((128, 8, 5), np.float32)
    for k in range(4):
        cv[:, :, k] = _fm(cw[k], 8)
    cv[:, :, 4] = _fm(inp["conv_b"][0], 8)
    p[:, P_CONV:P_CONV + 40] = cv.reshape(128, 40)
    Dh = inp["ssd_D"][0]
    dfm = np.zeros((128, 4), np.float32)
    for pr in range(4):
        dfm[0:64, pr] = Dh[2 * pr]
        dfm[64:128, pr] = Dh[2 * pr + 1]
    p[:, P_SSDFM:P_SSDFM + 4] = dfm
    p[:, P_SSDFM + 4:P_SSDFM + 8] = _fm(inp["ssd_norm_g"][0], 4)

    def st(a):
        return np.ascontiguousarray(np.asarray(a, np.float32).reshape(16, 128).T)
    p[:, P_S5P:P_S5P + 16] = st(inp["s5_A_re"][0])
    p[:, P_S5P + 16:P_S5P + 32] = st(inp["s5_A_im"][0])
    p[:, P_S5P + 32:P_S5P + 48] = st(np.repeat(inp["s5_log_step"][0][:, None], 64, axis=1))
    p[:, P_S5M:P_S5M + 4] = _fm(inp["s5_D"][0], 4)
    p[:, P_S5M + 4:P_S5M + 8] = _fm(inp["b_glu"][0], 4)
    p[0:8, P_SSD8] = inp["ssd_dt_bias"][0]
    p[0:8, P_SSD8 + 1] = inp["ssd_A_log"][0]
    return p


def _s5mats(inp):
    Br, Bi = inp["s5_B_re"][0], inp["s5_B_im"][0]
    Cr, Ci = inp["s5_C_re"][0], inp["s5_C_im"][0]
    BT = np.zeros((128, 2, 16, 128), np.float32)
    CT = np.zeros((128, 2, 16, 32), np.float32)
    for s in range(16):
        for gl in range(2):
            g = 2 * s + gl
            r0 = (g % 8) * 16
            BT[r0:r0 + 16, 0, s, gl * 64:(gl + 1) * 64] = Br[g].T
            BT[r0:r0 + 16, 1, s, gl * 64:(gl + 1) * 64] = Bi[g].T
            CT[gl * 64:(gl + 1) * 64, 0, s, gl * 16:(gl + 1) * 16] = Cr[g].T
            CT[gl * 64:(gl + 1) * 64, 1, s, gl * 16:(gl + 1) * 16] = Ci[g].T
    return BT, CT


class Arena:
    def __init__(self, nc, es, words):
        self.t = es.enter_context(nc.sbuf_tensor("arena", [128, words], F32))
        self.words = words
        self.lo = 0
        self.hi = words

    def alloc(self, name, shape, dt, top=False):
        n = 1
        for d in shape[1:]:
            n *= d
        w = n if dt == F32 or dt == I32 else (n + 1) // 2
        w = (w + 3) // 4 * 4
        if top:
            self.hi -= w
            off = self.hi
        else:
            off = self.lo
            self.lo += w
        assert self.lo <= self.hi, "arena overflow at %s: lo=%d hi=%d" % (name, self.lo, self.hi)
        ap = self.t[:, off:off + w]
        if dt != F32:
            ap = ap.bitcast(dt)
        ap = ap[:, 0:n]
        if len(shape) == 3:
            ap = ap.rearrange("p (a b) -> p a b", b=shape[2])
        elif len(shape) == 4:
            ap = ap.rearrange("p (a b c) -> p a b c", b=shape[2], c=shape[3])
        if shape[0] < 128:
            ap = ap[0:shape[0]]
        return TL(ap, name)


class StopBuild(Exception):
    pass


def build(dbg=None, stop_after=None):
    nc = bass.Bass("TRN2", target_bir_lowering=False)

    SH = []

    def ckpt(name):
        if stop_after == name:
            SH[0].barrier()
            SH[0].dead = True
    dt_in = lambda name, shape: nc.dram_tensor(name, list(shape), F32, kind="ExternalInput").ap()
    dt_out = lambda name, shape: nc.dram_tensor(name, list(shape), F32, kind="ExternalOutput").ap()
    xin = dt_in("xin", [NTOK, D])
    cin = dt_in("cin", [17, D])
    wada = dt_in("wada", [D, 6144])
    wadaf = dt_in("wadaf", [D, 2048])
    win = dt_in("win", [D, INP])
    wglu = dt_in("wglu", [512, 512])
    wout = dt_in("wout", [D, D])
    wg = dt_in("wg", [D, DFF])
    wu = dt_in("wu", [D, DFF])
    wd = dt_in("wd", [DFF, D])
    cst_d = dt_in("cst", [128, CST_W])
    prm_d = dt_in("prm", [128, PRM_W])
    s5bt_d = dt_in("s5bt", [128, 2 * 16 * 128])
    s5ct_d = dt_in("s5ct", [128, 2 * 16 * 32])
    stssd_d = dt_in("stssd", [NS, 8, 64, 128])
    stconv_d = dt_in("stconv", [128, 8 * NS * 3])
    sts5_d = dt_in("sts5", [128, 2 * 16 * NS])
    yout = dt_out("yout", [NTOK, D])
    o_ssdp = dt_out("o_ssdp", [128, 512])
    o_ssds = dt_out("o_ssds", [NS, 8, 64, 128])
    o_conv = dt_out("o_conv", [128, 8 * 17 * 3])
    o_s5 = dt_out("o_s5", [128, 2 * 16 * 17])
    mixd = nc.dram_tensor("mixd", [128, 8, NTOK], BF16, kind="Internal").ap()
    dumps = {}

    with ExitStack() as es:
        S = Sched(nc, es)
        NEED_CTN = []
        SH.append(S)
        A = Arena(nc, es, 53200)
        outbufs = []

        def dump(name, ap, shape, reads):
            if dbg is None or name not in dbg:
                return
            d = dt_out("dbg_" + name, shape)
            dumps[name] = shape
            b = Buf("dbg_" + name)
            S.dma("sp" if ap.dtype == F32 else "pool", d, ap, reads=reads, buf=b)
            outbufs.append(b)

        PB = [TL(es.enter_context(nc.psum_tensor("pb%d" % i, [128, 512], F32)), "pb%d" % i) for i in range(8)]
        for pb_ in PB:
            pb_.b.excl = True
            pb_.nosub = True

        def pbf(i):
            return PB[i].t[:].bitcast(BF16)

        cst = A.alloc("cst", [128, CST_W], F32)
        prm = A.alloc("prm", [128, PRM_W], F32)
        identb = A.alloc("identb", [128, 128], BF16)
        onesf = A.alloc("onesf", [128, 128], F32)
        mod = A.alloc("mod", [128, 64, 17], F32)
        amod = A.alloc("amod", [128, 24, 17], F32)
        s5fin = A.alloc("s5fin", [128, 2, 16, 17], F32)
        scT = A.alloc("scT", [128, 8, 17], BF16)
        LO_GLOBAL = A.lo
        win_sb = A.alloc("win_sb", [128, 8, INP], BF16)
        wglu_sb = A.alloc("wglu_sb", [128, 4, 512], BF16)
        s5BT = A.alloc("s5BT", [128, 2, 16, 128], BF16)
        s5CT = A.alloc("s5CT", [128, 2, 16, 32], BF16)
        LO_W = A.lo

        def load_1a_weights():
            for a_ in range(4):
                S.dma("pool", s5BT.t[:].rearrange("p a s c -> p (a s c)")[:, a_ * 1024:(a_ + 1) * 1024],
                      s5bt_d[:, a_ * 1024:(a_ + 1) * 1024], writes=[s5BT.b])
            S.dma("pool", s5CT.t[:].rearrange("p a s c -> p (a s c)"), s5ct_d, writes=[s5CT.b])
            win_v = win.rearrange("(kt p) n -> p kt n", p=128)
            for kh in range(4):
                for ch in range(2):
                    S.dma("pool", win_sb.t[:, 2 * kh:2 * kh + 2, ch * 1028:(ch + 1) * 1028],
                          win_v[:, 2 * kh:2 * kh + 2, ch * 1028:(ch + 1) * 1028], writes=[win_sb.sub(kh)])
            S.dma("pool", wglu_sb.t[:], wglu.rearrange("(kt p) n -> p kt n", p=128), writes=[wglu_sb.b])

        ident = cst.t[:, C_ID:C_ID + 128]
        S.dma("sp", cst.t[:], cst_d, writes=[cst.b])
        S.dma("sp", prm.t[:], prm_d, writes=[prm.b])
        S.op("act", lambda e: e.activation(out=identb.t[:], in_=ident, func=AF.Copy), reads=[cst.b], writes=[identb.b])
        S.op("dve", lambda e: e.memset(onesf.t[:], 1.0), writes=[onesf.b])

        def chunkmod(i):
            return mod.t[:, 8 * i:8 * i + 8, :]

        ssd8 = A.alloc("ssd8", [8, 4], F32)
        S.op("act", lambda e: e.activation(out=ssd8.t[:, 1:2], in_=prm.t[0:8, P_SSD8 + 1:P_SSD8 + 2], func=AF.Exp),
             reads=[prm.b], writes=[ssd8.b])
        S.op("dve", lambda e: e.tensor_scalar(out=ssd8.t[:, 1:2], in0=ssd8.t[:, 1:2], scalar1=-1.0, scalar2=None, op0=ALU.mult),
             reads=[ssd8.b], writes=[ssd8.b])
        S.op("dve", lambda e: e.tensor_copy(out=ssd8.t[:, 0:1], in_=prm.t[0:8, P_SSD8:P_SSD8 + 1]), reads=[prm.b], writes=[ssd8.b])

        Ptab = A.alloc("Ptab", [128, 2, 16, T5], F32)
        Qtab = A.alloc("Qtab", [128, 2, 16, T5], F32)
        s5t = [A.alloc("s5t%d" % i, [128, 512], F32) for i in range(2)]

        def alias(name, ap, buf):
            tl = TL(ap, name)
            tl.b = buf
            return tl
        sw = alias("s5work", s5t[1].t[:, 0:384].rearrange("p (a b) -> p a b", b=16), s5t[1].b)
        tmpA = alias("tmpA", s5t[0].t[:, 0:256].rearrange("p (a b) -> p a b", b=T5 // 2), s5t[0].b)
        tmpB = alias("tmpB", s5t[0].t[:, 256:512].rearrange("p (a b) -> p a b", b=T5 // 2), s5t[0].b)
        mask32 = A.alloc("mask32", [128, 16, T5], BF16)
        s5v = [A.alloc("s5v%d" % i, [128, 512], F32) for i in range(2)]
        qtmp = alias("qtmp", s5v[0].t[:].rearrange("p (s t) -> p s t", t=T5), s5v[0].b)
        mask4 = A.alloc("mask4", [128, 128, LS], BF16)
        s5cr = A.alloc("s5cr", [128, 2, 16], F32)
        W = lambda i: sw.t[:, i, :]
        pv = lambda i: prm.t[:, P_S5P + 16 * i:P_S5P + 16 * (i + 1)]
        swb = [sw.b, prm.b]

        def dv(fn):
            S.op("dve", fn, reads=swb, writes=[sw.b])

        def act(fn):
            S.op("act", fn, reads=swb, writes=[sw.b])
        TT = lambda e, o, a, b, op: e.tensor_tensor(out=o, in0=a, in1=b, op=op)
        def exp_acc(dst, src):
            dv(lambda e: e.tensor_scalar(out=W(22), in0=src, scalar1=1.0 / 16, scalar2=None, op0=ALU.mult))
            dv(lambda e: e.tensor_scalar(out=dst, in0=W(22), scalar1=1.0 / 7, scalar2=1.0, op0=ALU.mult, op1=ALU.add))
            for k in (6, 5, 4, 3, 2, 1):
                dv(lambda e: TT(e, dst, dst, W(22), ALU.mult))
                dv(lambda e, k=k: e.tensor_scalar(out=dst, in0=dst, scalar1=1.0 / k, scalar2=1.0, op0=ALU.mult, op1=ALU.add))
            for _ in range(4):
                dv(lambda e: TT(e, dst, dst, dst, ALU.mult))
        exp_acc(W(0), pv(2))
        dv(lambda e: TT(e, W(1), pv(0), W(0), ALU.mult))
        dv(lambda e: TT(e, W(2), pv(1), W(0), ALU.mult))
        exp_acc(W(3), W(1))

        def range_reduce(dst, src, add):
            ki = A_ki
            dv(lambda e: e.tensor_scalar(out=W(20), in0=src, scalar1=float(add), scalar2=1.0 / (2 * PI), op0=ALU.add, op1=ALU.mult))
            S.op("dve", lambda e: e.tensor_copy(out=ki.t[:], in_=W(20)), reads=swb, writes=[ki.b])
            S.op("dve", lambda e: e.tensor_copy(out=W(21), in_=ki.t[:]), reads=[ki.b], writes=[sw.b])
            dv(lambda e: e.tensor_scalar(out=W(20), in0=src, scalar1=float(add), scalar2=None, op0=ALU.add))
            dv(lambda e: e.scalar_tensor_tensor(out=dst, in0=W(21), scalar=-2 * PI, in1=W(20), op0=ALU.mult, op1=ALU.add))
            dv(lambda e: e.tensor_scalar(out=dst, in0=dst, scalar1=PI, scalar2=-PI, op0=ALU.min, op1=ALU.max))
        A_ki = A.alloc("s5ki", [128, 16], I32)
        range_reduce(W(4), W(2), 0.0)
        range_reduce(W(5), W(2), PI / 2)
        act(lambda e: e.activation(out=W(6), in_=W(4), func=AF.Sin))
        act(lambda e: e.activation(out=W(7), in_=W(5), func=AF.Sin))
        dv(lambda e: TT(e, W(8), W(3), W(7), ALU.mult))
        dv(lambda e: TT(e, W(9), W(3), W(6), ALU.mult))
        dv(lambda e: e.tensor_scalar(out=W(10), in0=W(8), scalar1=-1.0, scalar2=None, op0=ALU.add))
        dv(lambda e: TT(e, W(11), pv(0), pv(0), ALU.mult))
        dv(lambda e: TT(e, W(12), pv(1), pv(1), ALU.mult))
        dv(lambda e: TT(e, W(11), W(11), W(12), ALU.add))
        dv(lambda e: e.reciprocal(out=W(11), in_=W(11)))
        dv(lambda e: TT(e, W(12), W(10), pv(0), ALU.mult))
        dv(lambda e: TT(e, W(13), W(9), pv(1), ALU.mult))
        dv(lambda e: TT(e, W(12), W(12), W(13), ALU.add))
        dv(lambda e: TT(e, W(14), W(12), W(11), ALU.mult))
        dv(lambda e: TT(e, W(12), W(9), pv(0), ALU.mult))
        dv(lambda e: TT(e, W(13), W(10), pv(1), ALU.mult))
        dv(lambda e: TT(e, W(12), W(12), W(13), ALU.subtract))
        dv(lambda e: TT(e, W(15), W(12), W(11), ALU.mult))
        dv(lambda e: TT(e, W(12), W(8), W(8), ALU.mult))
        dv(lambda e: TT(e, W(13), W(9), W(9), ALU.mult))
        dv(lambda e: TT(e, W(12), W(12), W(13), ALU.add))
        dv(lambda e: e.reciprocal(out=W(12), in_=W(12)))
        dv(lambda e: TT(e, W(16), W(8), W(12), ALU.mult))
        dv(lambda e: e.scalar_tensor_tensor(out=W(17), in0=W(9), scalar=-1.0, in1=W(12), op0=ALU.mult, op1=ALU.mult))

        def build_pow(tab, br, bi):
            tb = [tab.b, sw.b, tmpA.b, tmpB.b]
            S.op("dve", lambda e: e.tensor_copy(out=tab.t[:, 0, :, 0], in_=br), reads=tb, writes=[tab.b])
            S.op("dve", lambda e: e.tensor_copy(out=tab.t[:, 1, :, 0], in_=bi), reads=tb, writes=[tab.b])
            n = 1
            while n < T5:
                ar, ai = tab.t[:, 0, :, 0:n], tab.t[:, 1, :, 0:n]
                sr = tab.t[:, 0, :, n - 1:n].to_broadcast([128, 16, n])
                si = tab.t[:, 1, :, n - 1:n].to_broadcast([128, 16, n])
                tA, tB = tmpA.t[:, :, 0:n], tmpB.t[:, :, 0:n]
                orr, oi = tab.t[:, 0, :, n:2 * n], tab.t[:, 1, :, n:2 * n]
                ops = [(tA, ar, sr, ALU.mult), (tB, ai, si, ALU.mult), (orr, tA, tB, ALU.subtract),
                       (tA, ar, si, ALU.mult), (tB, ai, sr, ALU.mult), (oi, tA, tB, ALU.add)]
                for (o, a, b, op) in ops:
                    S.op("dve", lambda e, o=o, a=a, b=b, op=op: TT(e, o, a, b, op), reads=tb, writes=tb[0:1] + tb[2:4])
                n *= 2
        build_pow(Ptab, W(8), W(9))
        build_pow(Qtab, W(16), W(17))
        tq = [Qtab.b, sw.b, tmpA.b, tmpB.b]
        for half in range(2):
            hs = slice(half * (T5 // 2), (half + 1) * (T5 // 2))
            qr, qi = Qtab.t[:, 0, :, hs], Qtab.t[:, 1, :, hs]
            fr = W(14).unsqueeze(2).to_broadcast([128, 16, T5 // 2])
            fi = W(15).unsqueeze(2).to_broadcast([128, 16, T5 // 2])
            ops = [(tmpA.t[:], qr, fr, ALU.mult), (tmpB.t[:], qi, fi, ALU.mult), ("R", tmpA.t[:], tmpB.t[:], ALU.subtract),
                   (tmpA.t[:], qr, fi, ALU.mult), (tmpB.t[:], qi, fr, ALU.mult), (qi, tmpA.t[:], tmpB.t[:], ALU.add)]
            for (o, a, b, op) in ops:
                if isinstance(o, str):
                    o = qtmp.t[:, :, hs]
                S.op("dve", lambda e, o=o, a=a, b=b, op=op: TT(e, o, a, b, op), reads=tq + [qtmp.b], writes=tq + [qtmp.b])
            S.op("dve", lambda e, qr=qr, hs=hs: e.tensor_copy(out=qr, in_=qtmp.t[:, :, hs]), reads=[qtmp.b], writes=[Qtab.b])
        S.op("dve", lambda e: e.memset(mask32.t[:], 1.0), reads=[Qtab.b], writes=[mask32.b])
        S.op("dve", lambda e: e.memset(mask32.t[:, :, 0:1], 0.0), writes=[mask32.b])
        S.op("dve", lambda e: e.memset(mask4.t[:], 1.0), writes=[mask4.b])
        S.op("dve", lambda e: e.memset(mask4.t[:, :, 0:1], 0.0), writes=[mask4.b])
        S.op("dve", lambda e: e.memset(s5cr.t[:], 0.0), writes=[s5cr.b])
        dump("Ptab", Ptab.t[:].rearrange("p a s t -> p (a s t)"), [128, 2 * 16 * T5], [Ptab.b])
        dump("Qtab", Qtab.t[:].rearrange("p a s t -> p (a s t)"), [128, 2 * 16 * T5], [Qtab.b])

        LO_W = A.lo
        cs = A.alloc("cs", [17, D], F32)
        slabs = [A.alloc("adaslab%d" % i, [128, 8, 512], BF16) for i in range(3)]
        S.dma("sp", cs.t[:], cin, writes=[cs.b])
        S.op("act", lambda e: e.activation(out=cs.t[:], in_=cs.t[:], func=AF.Silu), reads=[cs.b], writes=[cs.b])
        for kt in range(8):
            S.op("pe", lambda e, kt=kt: e.transpose(PB[2].t[:, kt * 17:(kt + 1) * 17], cs.t[:, kt * 128:(kt + 1) * 128],
                                                    cst.t[0:17, C_ID:C_ID + 17]),
                 reads=[cs.b, cst.b], writes=[PB[2].b])
        S.op("act", lambda e: e.activation(out=scT.t[:].rearrange("p k s -> p (k s)"), in_=PB[2].t[:, 0:136], func=AF.Copy),
             reads=[PB[2].b], writes=[scT.b])
        wada_v = wada.rearrange("(kt p) n -> p kt n", p=128)
        wadaf_v = wadaf.rearrange("(kt p) n -> p kt n", p=128)

        def slab_src(i):
            if i < 12:
                return wada_v[:, :, i * 512:(i + 1) * 512]
            return wadaf_v[:, :, (i - 12) * 512:(i - 11) * 512]

        def load_slab(i):
            sl = slabs[i % 3]
            for kh in range(2):
                S.dma("pool", sl.t[:, 4 * kh:4 * kh + 4, :], slab_src(i)[:, 4 * kh:4 * kh + 4, :], writes=[sl.b])
        load_slab(0)
        load_slab(1)
        load_1a_weights()
        for i in range(4):
            if i + 2 < 4:
                load_slab(i + 2)
            sl = slabs[i % 3]
            pb = PB[i % 2]
            for fc in range(4):
                for kt in range(8):
                    S.op("pe", lambda e, fc=fc, kt=kt, sl=sl, pb=pb: e.matmul(
                        pb.t[:, fc * 17:(fc + 1) * 17], sl.t[:, kt, fc * 128:(fc + 1) * 128], scT.t[:, kt, :],
                        start=(kt == 0), stop=(kt == 7)), reads=[sl.b, scT.b], writes=[pb.b])
            S.op("dve", lambda e, i=i, pb=pb: e.tensor_tensor(
                out=mod.t[:, 4 * i:4 * i + 4, :], in0=pb.t[:, 0:68].rearrange("p (c s) -> p c s", s=17),
                in1=prm.t[:, P_BMOD + 4 * i:P_BMOD + 4 * i + 4].unsqueeze(2).to_broadcast([128, 4, 17]), op=ALU.add),
                reads=[pb.b, prm.b], writes=[mod.b])
        def make_amod(lst):
          for k, (sci, gi) in lst:
            S.op("dve", lambda e, k=k, sci=sci, gi=gi: e.scalar_tensor_tensor(
                out=amod.t[:, 8 * k:8 * k + 8, :], in0=chunkmod(sci), scalar=1.0,
                in1=prm.t[:, P_GAIN + 8 * gi:P_GAIN + 8 * gi + 8].unsqueeze(2).to_broadcast([128, 8, 17]),
                op0=ALU.add, op1=ALU.mult), reads=[mod.b, prm.b], writes=[amod.b])
        make_amod([(0, (1, 0))])
        dump("mod", mod.t[:].rearrange("p c s -> p (c s)"), [128, 64 * 17], [mod.b])
        S.barrier()
        S.emit()
        A.lo = LO_W

        MOD_SH1, MOD_G1, MOD_SH2, MOD_G2, MOD_SHF = 0, 2, 3, 5, 6

        def expand_mod(name, src_ap, srcbufs):
            t = A.alloc(name, [128, 8, 64], F32)
            S.op("dve", lambda e: e.tensor_copy(out=t.t[:].rearrange("p k (s b) -> p k s b", b=LS),
                                                in_=src_ap.unsqueeze(3).to_broadcast([128, 8, NS, LS])),
                 reads=srcbufs, writes=[t.b])
            return t

        LO_P1 = A.lo
        mixt = [A.alloc("mixt%d" % i, [128, 8, 256], BF16) for i in range(2)]
        mixdb = [Buf("mixd%d" % i) for i in range(9)]
        a1x = A.alloc("a1x", [128, 8, 64], F32)
        sh1x = A.alloc("sh1x", [128, 8, 64], F32)

        def fill_x(t, src_ap, srcbufs):
            S.op("dve", lambda e: e.tensor_copy(out=t.t[:].rearrange("p k (s b) -> p k s b", b=LS),
                                                in_=src_ap.unsqueeze(3).to_broadcast([128, 8, NS, LS])),
                 reads=srcbufs, writes=[t.b])
        adab = [TL(a1x.t[:].rearrange("p k t -> p (k t)").bitcast(BF16).rearrange("p (k c) -> p k c", c=128), "adab0"),
                TL(sh1x.t[:].rearrange("p k t -> p (k t)").bitcast(BF16).rearrange("p (k c) -> p k c", c=128), "adab1")]
        adab[0].b = a1x.b
        adab[1].b = sh1x.b
        ADA_CH = list(range(16, 64))

        def ada_load(ci):
            c = ADA_CH[ci]
            src = wada_v[:, :, c * 128:(c + 1) * 128] if c < 48 else wadaf_v[:, :, (c - 48) * 128:(c - 47) * 128]
            S.dma("pool", adab[ci % 2].t[:], src, writes=[adab[ci % 2].b])

        def ada_compute(ci):
            c = ADA_CH[ci]
            sl = adab[ci % 2]
            pb = next_pb()
            for kt in range(8):
                S.op("pe", lambda e, kt=kt: e.matmul(pb.t[:, 0:17], sl.t[:, kt, :], scT.t[:, kt, :], start=(kt == 0), stop=(kt == 7)),
                     reads=[sl.b, scT.b], writes=[pb.b])
            S.op("dve", lambda e: e.tensor_scalar(out=mod.t[:, c, :], in0=pb.t[:, 0:17], scalar1=prm.t[:, P_BMOD + c:P_BMOD + c + 1],
                                                  scalar2=None, op0=ALU.add), reads=[pb.b, prm.b], writes=[mod.b])
        ada_state = [0, 0]

        def ada_step():
            if ada_state[1] >= len(ADA_CH):
                return
            while ada_state[0] < min(len(ADA_CH), ada_state[1] + 2):
                ada_load(ada_state[0])
                ada_state[0] += 1
            ada_compute(ada_state[1])
            ada_state[1] += 1

        ckpt("setup0")
        NTM = 256
        xtm = A.alloc("xtm", [128, 2, D], F32)
        xn = A.alloc("xn", [128, 2, D], BF16)
        nstat = A.alloc("nstat", [128, 4], F32)
        uT = A.alloc("uT", [128, 8, NTM], BF16)
        xpad = A.alloc("xpad", [128, 8, NTM + 4], BF16)
        xtail = A.alloc("xtail", [128, 8, 64], F32)
        cvst = A.alloc("cvst", [128, 8, NS, 3], F32)
        S.dma("sp", cvst.t[:].rearrange("p c s k -> p (c s k)"), stconv_d, writes=[cvst.b])
        dgc = A.alloc("dgc", [128, 8, 4, 128], BF16)
        for ct_ in range(8):
            for k_ in range(4):
                S.op("act", lambda e, ct_=ct_, k_=k_: e.activation(
                    out=dgc.t[:, ct_, k_, :], in_=ident, func=AF.Copy,
                    scale=prm.t[:, P_CONV + 5 * ct_ + k_:P_CONV + 5 * ct_ + k_ + 1]), reads=[cst.b, prm.b], writes=[dgc.b])
        xsT = A.alloc("xsT", [128, 4, NTM], F32)
        BCT = A.alloc("BCT", [128, 4, NTM], BF16)
        szT = A.alloc("szT", [128, 4, NTM], BF16)
        u5Ts = [A.alloc("u5T%d" % i, [128, 4, NTM], BF16) for i in range(2)]
        dtT = A.alloc("dtT", [8, 2, NTM], F32)
        cacc = [A.alloc("cacc0", [128, NTM], F32)] * 2
        y5pre = A.alloc("y5pre", [128, 4, NTM], F32)
        g5 = A.alloc("g5", [128, 4, NTM], BF16)
        sgl = A.alloc("sgl", [128, NTM], F32)
        dtm_l = [A.alloc("dtm%d" % i, [128, 16], F32) for i in range(2)]
        acs_l = [A.alloc("acs%d" % i, [128, 8], F32) for i in range(2)]
        dec_l = [A.alloc("dec%d" % i, [128, 8], F32) for i in range(2)]
        dtdec_l = [A.alloc("dtdec%d" % i, [128, 8], F32) for i in range(2)]
        Xtm = A.alloc("Xtm", [128, 8, 64], BF16)
        Xdec = A.alloc("Xdec", [128, 8, 64], BF16)
        Btm = A.alloc("Btm", [128, 2, 128], BF16)
        big1 = A.alloc("big1", [128, 8, 128], F32)
        big2 = A.alloc("big2", [128, 8, 128], F32)
        MT = A.alloc("MT", [128, 8, 128], BF16)
        eA = A.alloc("eA", [128, 8, 128], F32)
        CdT = A.alloc("CdT", [128, 8, 128], BF16)
        ST = A.alloc("ST", [128, 8, 64], F32)
        STb = A.alloc("STb", [128, 8, 64], BF16)
        sts5 = alias("sts5", ST.t[:].rearrange("p h q -> p (h q)").rearrange("p (a s q) -> p a s q", a=2, s=16), ST.b)
        yg = A.alloc("yg", [128, 4, 128], F32)
        ysq = alias("ysq", big1.t[:, 4:8, :], big1.b)
        rsb = A.alloc("rsb", [128, 2, 128], F32)
        ysqb = A.alloc("ysqb", [128, 4, 128], BF16)
        onesb1 = A.alloc("onesb1", [128, 128], BF16)
        S.op("dve", lambda e: e.memset(onesb1.t[:], 1.0), writes=[onesb1.b])
        h0n = [alias("h0n0", xtm.t[:, 1, 0:512].rearrange("p (a n) -> p a n", n=128), xtm.sub(1)),
               alias("h0n1", xtm.t[:, 0, 0:512].rearrange("p (a n) -> p a n", n=128), xtm.sub(0))]
        h0T = [A.alloc("h0T%d" % i, [128, 8, 64], BF16) for i in range(2)]
        Bj = [A.alloc("Bj%d" % i, [128, 2, 128], BF16) for i in range(2)]
        hn = [alias("hn0", xtm.t[:, 1, 512:1024].rearrange("p (a n) -> p a n", n=128), xtm.sub(1)),
              alias("hn1", xtm.t[:, 0, 512:1024].rearrange("p (a n) -> p a n", n=128), xtm.sub(0))]
        decfm = A.alloc("decfm", [128, 4, 16], F32)
        dAx = alias("dAx", big1.t[:, 0:4, :].rearrange("p a (b c) -> p (a b) c", c=64), big1.b)
        s5g = [[A.alloc("s5g%d%d" % (j, i), [128, 512], F32) for i in range(2)] for j in range(2)]
        s5t34 = [A.alloc("s5t%d" % i, [128, 512], F32) for i in (2, 3)]
        s5vb = [A.alloc("s5vb%d" % i, [128, 512], F32) for i in range(2)]
        s5k = [0]
        s5h = [[A.alloc("s5h%d%d" % (j, i), [128, 512], BF16) for i in range(4)] for j in range(2)]
        s5CTn = A.alloc("s5CTn", [128, 16, 32], BF16)
        s5c = A.alloc("s5c", [128, 4, 16], F32)
        busd = [[A.alloc("bus%d%d" % (j, i), [128, 512], F32) for i in range(2)] for j in range(2)]
        dg5 = A.alloc("dg5", [128, 4, 128], BF16)
        for q_ in range(4):
            S.op("act", lambda e, q_=q_: e.activation(out=dg5.t[:, q_, :], in_=ident, func=AF.Copy,
                                                      scale=prm.t[:, P_S5M + q_:P_S5M + q_ + 1]),
                 reads=[cst.b, prm.b], writes=[dg5.b])
        S.op("dve", lambda e: e.tensor_scalar(out=s5CT.t[:, 1], in0=s5CT.t[:, 1], scalar1=-1.0, scalar2=None, op0=ALU.mult),
             reads=[s5CT.b], writes=[s5CT.b])
        S.op("dve", lambda e: e.tensor_scalar(out=s5CTn.t[:], in0=s5CT.t[:, 0], scalar1=-1.0, scalar2=None, op0=ALU.mult),
             reads=[s5CT.b], writes=[s5CTn.b])
        print("arena after p1a allocs: lo=%d hi=%d (words)" % (A.lo, A.hi))

        S.op("dve", lambda e: e.memset(xpad.t[:, :, 0:3], 0.0), writes=[xpad.b])
        S.op("dve", lambda e: e.memset(ST.t[:], 0.0), writes=[ST.b])
        S.op("dve", lambda e: e.memset(STb.t[:], 0.0), writes=[STb.b])

        import os as _os3
        ENG_OUTROT = _os3.environ.get("K_OUTROT", "dve")
        ENG_ADDS = _os3.environ.get("K_ADDS", "dve")
        TILES_A = [(i * 256, 256, False) for i in range(8)] + [(SEQ, 64, True)]

        def load_x(ti):
            t0, NT, is_s = TILES_A[ti]
            for blk in range((NT + 127) // 128):
                rows = min(128, NT - blk * 128)
                S.dma("sp", xtm.t[0:rows, blk, :], xin[t0 + blk * 128:t0 + blk * 128 + rows, :], writes=[xtm.sub(blk)])

        a1 = lambda kt: amod.t[:, kt, 0:1]
        sh1 = lambda kt: mod.t[:, 8 * MOD_SH1 + kt, 0:1]
        cw = lambda ct, k: prm.t[:, P_CONV + 5 * ct + k:P_CONV + 5 * ct + k + 1]
        IN_CHUNKS = [("dt", 0, 1536, 8)] + [("z", i, i * 128, 128) for i in range(4)] + \
                    [("xbc", i, 512 + i * 128, 128) for i in range(8)] + [("u5", i, 1544 + i * 128, 128) for i in range(4)]

        load_x(0)
        pbi = [0]

        def next_pb():
            pbi[0] ^= 1
            return PB[pbi[0]]

        ckpt("pre")
        def chain1(ti):
            t0, NT, is_s = TILES_A[ti]
            u5T = u5Ts[ti % 2]
            nblk = (NT + 127) // 128
            T = 128 if not is_s else 64
            tri = cst.t[0:T, C_TRI:C_TRI + T] if not is_s else cst.t[0:T, C_TRI64:C_TRI64 + T]
            neg = cst.t[0:T, C_NEG:C_NEG + T] if not is_s else cst.t[0:T, C_NEG64:C_NEG64 + T]
            sego = onesf.t[0:T, 0:T] if not is_s else cst.t[0:T, C_SEG64:C_SEG64 + T]
            segi = cst.t[0:64, C_SEGI:C_SEGI + 16]

            def dt_prep(ck):
                c0 = ck * T
                cs_ = slice(c0, c0 + T)
                dtm, acs, dec, dtdec = dtm_l[ck], acs_l[ck], dec_l[ck], dtdec_l[ck]
                pc = 0 if ck == 0 else 480
                S.op("pe", lambda e: e.transpose(PB[4].t[0:T, pc:pc + 8], dtT.t[:, 0, cs_], cst.t[0:8, C_ID:C_ID + 8]),
                     reads=[dtT.b, cst.b], writes=[PB[4].sub("sm")])
                S.op("pe", lambda e: e.transpose(PB[4].t[0:T, pc + 8:pc + 16], dtT.t[:, 1, cs_], cst.t[0:8, C_ID:C_ID + 8]),
                     reads=[dtT.b, cst.b], writes=[PB[4].sub("sm")])
                S.op("act", lambda e: e.activation(out=dtm.t[0:T, :], in_=PB[4].t[0:T, pc:pc + 16], func=AF.Copy),
                     reads=[PB[4].sub("sm")], writes=[dtm.b])
                S.op("pe", lambda e: e.matmul(PB[4].t[0:T, pc + 16:pc + 24], tri, dtm.t[0:T, 8:16], start=True, stop=True),
                     reads=[dtm.b, cst.b], writes=[PB[4].sub("sm")])
                S.op("pe", lambda e: e.matmul(PB[4].t[0:T, pc + 24:pc + 32], sego, dtm.t[0:T, 8:16], start=True, stop=True),
                     reads=[dtm.b, cst.b, onesf.b], writes=[PB[4].sub("sm")])
                S.op("act", lambda e: e.activation(out=acs.t[0:T, :], in_=PB[4].t[0:T, pc + 16:pc + 24], func=AF.Copy),
                     reads=[PB[4].sub("sm")], writes=[acs.b])
                S.op("dve", lambda e: TT(e, dec.t[0:T, :], PB[4].t[0:T, pc + 24:pc + 32], acs.t[0:T, :], ALU.subtract),
                     reads=[PB[4].sub("sm"), acs.b], writes=[dec.b])
                S.op("act", lambda e: e.activation(out=dec.t[0:T, :], in_=dec.t[0:T, :], func=AF.Exp), reads=[dec.b], writes=[dec.b])
                S.op("dve", lambda e: TT(e, dtdec.t[0:T, :], dtm.t[0:T, 0:8], dec.t[0:T, :], ALU.mult),
                     reads=[dtm.b, dec.b], writes=[dtdec.b])
            for blk in range(nblk):
                rows = min(128, NT - blk * 128)
                xb = xtm.sub(blk)
                S.op("act", lambda e, blk=blk, rows=rows: e.activation(
                    out=xn.t[0:rows, blk, :], in_=xtm.t[0:rows, blk, :], func=AF.Square, accum_out=nstat.t[0:rows, blk:blk + 1]),
                    reads=[xb], writes=[xn.sub(blk), nstat.sub(blk)])
                S.op("act", lambda e, blk=blk, rows=rows: e.activation(
                    out=nstat.t[0:rows, 2 + blk:3 + blk], in_=nstat.t[0:rows, blk:blk + 1], func=AF.Ln, scale=1.0 / D, bias=EPS),
                    reads=[nstat.sub(blk)], writes=[nstat.sub(blk)])
                S.op("act", lambda e, blk=blk, rows=rows: e.activation(out=nstat.t[0:rows, 2 + blk:3 + blk],
                                                                        in_=nstat.t[0:rows, 2 + blk:3 + blk], func=AF.Exp, scale=-0.5),
                     reads=[nstat.sub(blk)], writes=[nstat.sub(blk)])
                S.op("act", lambda e, blk=blk, rows=rows: e.activation(
                    out=xn.t[0:rows, blk, :], in_=xtm.t[0:rows, blk, :], func=AF.Copy, scale=nstat.t[0:rows, 2 + blk:3 + blk]),
                    reads=[xb, nstat.sub(blk)], writes=[xn.sub(blk)])
            ckpt("Aa%d" % ti)
            if ti + 1 < len(TILES_A):
                load_x(ti + 1)
            ckpt("Ab%d" % ti)
            for kt in range(8):
                xb_ = 2 + (kt % 2)
                pslot = PB[xb_].b
                for blk in range(nblk):
                    rows = min(128, NT - blk * 128)
                    S.op("pe", lambda e, kt=kt, blk=blk, rows=rows: e.transpose(
                        pbf(xb_)[:, blk * 128:blk * 128 + rows],
                        xn.t[0:rows, blk, kt * 128:(kt + 1) * 128], identb.t[0:rows, 0:rows]),
                        reads=[xn.sub(blk), identb.b], writes=[pslot])
                src = pbf(xb_)[:, 0:NT]
                if not is_s:
                    S.op("act", lambda e, kt=kt, src=src: e.activation(out=uT.t[:, kt, 0:NT], in_=src, func=AF.Identity,
                                                                       scale=a1(kt), bias=sh1(kt)),
                         reads=[pslot, amod.b, mod.b], writes=[uT.sub(kt)])
                else:
                    S.op("dve", lambda e, kt=kt, src=src: TT(e, cacc[0].t[:, 0:NT], src, a1x.t[:, kt, :], ALU.mult),
                         reads=[pslot, a1x.b], writes=[cacc[0].b])
                    S.op("dve", lambda e, kt=kt: TT(e, uT.t[:, kt, 0:NT], cacc[0].t[:, 0:NT], sh1x.t[:, kt, :], ALU.add),
                         reads=[cacc[0].b, sh1x.b], writes=[uT.sub(kt)])
            ckpt("A%d" % ti)
            if ti == 0:
                dump("uT", uT.t[:].rearrange("p k t -> p (k t)"), [128, 8 * NTM], uT.allb())

            yield
            if is_s:
                xps = xpad.t[:, :, 0:NS * 7].rearrange("p c (s k) -> p c s k", k=7)
                S.op("act", lambda e: e.activation(out=xps[:, :, :, 0:3], in_=cvst.t[:], func=AF.Copy), reads=[cvst.b], writes=[xpad.b])
            for (kind, i, c0, M) in IN_CHUNKS:
                yield
                pb = next_pb()
                for kt in range(8):
                    S.op("pe", lambda e, kt=kt, c0=c0, M=M, pb=pb: e.matmul(
                        pb.t[0:M, 0:NT], win_sb.t[:, kt, c0:c0 + M], uT.t[:, kt, 0:NT], start=(kt == 0), stop=(kt == 7)),
                        reads=[win_sb.sub(kt // 2), uT.sub(kt)], writes=[pb.b])
                if kind == "z":
                    S.op("act", lambda e, i=i, pb=pb: e.activation(out=szT.t[:, i, 0:NT], in_=pb.t[:, 0:NT], func=AF.Silu),
                         reads=[pb.b], writes=[szT.b])
                elif kind == "xbc":
                    if not is_s:
                        S.op("act", lambda e, i=i, pb=pb: e.activation(out=xpad.t[:, i, 3:3 + NT], in_=pb.t[:, 0:NT], func=AF.Copy),
                             reads=[pb.b], writes=[xpad.b])
                        if ti == 7:
                            S.op("act", lambda e, i=i, pb=pb: e.activation(out=xtail.t[:, i, 0:3], in_=pb.t[:, NT - 3:NT], func=AF.Copy),
                                 reads=[pb.b], writes=[xtail.b])
                    else:
                        S.op("act", lambda e, i=i, pb=pb: e.activation(
                            out=xps[:, i, :, 3:7], in_=pb.t[:, 0:NT].rearrange("p (s k) -> p s k", k=LS), func=AF.Copy),
                            reads=[pb.b], writes=[xpad.b])
                        S.op("act", lambda e, i=i, pb=pb: e.activation(out=xtail.t[:, i, 0:NT], in_=pb.t[:, 0:NT], func=AF.Copy),
                             reads=[pb.b], writes=[xtail.b])
                elif kind == "dt":
                    S.op("act", lambda e, pb=pb: e.activation(out=dtT.t[:, 1, 0:NT], in_=pb.t[0:8, 0:NT], func=AF.Exp,
                                                              bias=ssd8.t[:, 0:1]), reads=[pb.b, ssd8.b], writes=[dtT.b])
                    S.op("act", lambda e: e.activation(out=dtT.t[:, 0, 0:NT], in_=dtT.t[:, 1, 0:NT], func=AF.Ln, bias=1.0),
                         reads=[dtT.b], writes=[dtT.b])
                    S.op("dve", lambda e: e.tensor_scalar(out=dtT.t[:, 1, 0:NT], in0=dtT.t[:, 0, 0:NT], scalar1=ssd8.t[:, 1:2],
                                                          scalar2=None, op0=ALU.mult), reads=[dtT.b, ssd8.b], writes=[dtT.b])
                    for ck_ in range(NT // T):
                        yield
                        dt_prep(ck_)
                else:
                    S.op("act", lambda e, i=i, pb=pb: e.activation(out=u5T.t[:, i, 0:NT], in_=pb.t[:, 0:NT], func=AF.Copy),
                         reads=[pb.b], writes=[u5T.b])

            ckpt("B%d" % ti)
            for ct in range(8):
                yield
                pb = next_pb()
                if not is_s:
                    xin_k = lambda k, ct=ct: xpad.t[:, ct, k:k + NT]
                    pbv = pb.t[:, 0:NT]
                    dst = xsT.t[:, ct, 0:NT] if ct < 4 else BCT.t[:, ct - 4, 0:NT]
                else:
                    xin_k = lambda k, ct=ct: xps[:, ct, :, k:k + LS]
                    pbv = pb.t[:, 0:NT].rearrange("p (s k) -> p s k", k=LS)
                    dst = (xsT.t[:, ct, 0:NT] if ct < 4 else BCT.t[:, ct - 4, 0:NT]).rearrange("p (s k) -> p s k", k=LS)
                for k in range(4):
                    S.op("pe", lambda e, k=k: e.matmul(pbv, dgc.t[:, ct, k, :], xin_k(k), start=(k == 0), stop=(k == 3)),
                         reads=[dgc.b, xpad.b], writes=[pb.b])
                S.op("act", lambda e: e.activation(out=dst, in_=pbv, func=AF.Silu, bias=cw(ct, 4)),
                     reads=[pb.b, prm.b], writes=[xsT.b if ct < 4 else BCT.b])
            ocv = o_conv.rearrange("p (c s k) -> p c s k", s=17, k=3)
            if is_s:
                S.op("act", lambda e: e.activation(out=cvst.t[:], in_=xtail.t[:].rearrange("p c (s k) -> p c s k", k=LS)[:, :, :, 1:4],
                                                   func=AF.Copy), reads=[xtail.b], writes=[cvst.b])
                S.dma("sp", ocv[:, :, 1:17, :], cvst.t[:], reads=[cvst.b], buf=cvst.b)
                outbufs.append(cvst.b)
            elif ti == 7:
                S.dma("sp", ocv[:, :, 0, :], xtail.t[:, :, 0:3], reads=[xtail.b], buf=xtail.b)
            if not is_s:
                S.op("dve", lambda e: e.tensor_copy(out=xpad.t[:, :, 0:3], in_=xpad.t[:, :, NT:NT + 3]),
                     reads=[xpad.b], writes=[xpad.b])
            if is_s:
                dump("xsS", xsT.t[:, :, 0:64], [128, 4, 64], [xsT.b])
                dump("ygS", yg.t[:, :, 0:64], [128, 4, 64], [yg.b])
            if ti == 0:
                dump("xsT", xsT.t[:].rearrange("p k t -> p (k t)"), [128, 4 * NTM], [xsT.b])
                dump("dtT", dtT.t[:].rearrange("p k t -> p (k t)"), [8, 2 * NTM], [dtT.b])

            ckpt("C%d" % ti)
            for ck in range(NT // T):
                c0 = ck * T
                cs_ = slice(c0, c0 + T)
                dtm, acs, dec, dtdec = dtm_l[ck], acs_l[ck], dec_l[ck], dtdec_l[ck]
                yield
                for pr in range(4):
                    S.op("pe", lambda e, pr=pr, cs_=cs_: e.transpose(PB[3].t[0:T, pr * 128:(pr + 1) * 128], xsT.t[:, pr, cs_], ident),
                         reads=[xsT.b, cst.b], writes=[PB[3].b])
                pxs = PB[3].t[0:T, :].rearrange("p (h q) -> p h q", q=64)
                for h in range(8):
                    S.op("act", lambda e, h=h: e.activation(out=Xtm.t[0:T, h, :], in_=pxs[:, h, :], func=AF.Copy, scale=dtm.t[0:T, h:h + 1]),
                         reads=[PB[3].b, dtm.b], writes=[Xtm.b])
                    S.op("act", lambda e, h=h: e.activation(out=Xdec.t[0:T, h, :], in_=pxs[:, h, :], func=AF.Copy, scale=dtdec.t[0:T, h:h + 1]),
                         reads=[PB[3].b, dtdec.b], writes=[Xdec.b])
                for g in range(2):
                    S.op("pe", lambda e, g=g, cs_=cs_: e.transpose(pbf(2)[0:T, g * 128:(g + 1) * 128], BCT.t[:, g, cs_], identb.t[:]),
                         reads=[BCT.b, identb.b], writes=[PB[2].sub(0)])
                S.op("act", lambda e: e.activation(out=Btm.t[0:T].rearrange("p g n -> p (g n)"), in_=pbf(2)[0:T, 0:256], func=AF.Copy),
                     reads=[PB[2].sub(0)], writes=[Btm.b])
                yield
                S.op("dve", lambda e: TT(e, big1.t[0:T, :, 0:T], tri.unsqueeze(1).to_broadcast([T, 8, T]),
                                         dtm.t[0:T, 8:16].unsqueeze(2).to_broadcast([T, 8, T]), ALU.mult),
                     reads=[cst.b, dtm.b], writes=[big1.b])
                for half in range(2):
                    S.op("pe", lambda e, half=half: e.matmul(
                        PB[3].t[:, 0:4 * T].rearrange("p (h l) -> p h l", l=T), onesf.t[0:T, :],
                        big1.t[0:T, 4 * half:4 * half + 4, 0:T], start=True, stop=True),
                        reads=[big1.b, onesf.b], writes=[PB[3].b])
                    yield
                    for h in range(4 * half, 4 * half + 4):
                        S.op("dve", lambda e, h=h: e.scalar_tensor_tensor(
                            out=big2.t[0:T, h, 0:T], in0=PB[3].t[0:T, (h % 4) * T:(h % 4 + 1) * T], scalar=acs.t[0:T, h:h + 1],
                            in1=neg, op0=ALU.subtract, op1=ALU.min), reads=[PB[3].b, acs.b, cst.b], writes=[big2.b])
                    S.op("act", lambda e, half=half: e.activation(
                        out=eA.t[:, 4 * half:4 * half + 4, 0:T], in_=PB[3].t[:, 0:4 * T].rearrange("p (h l) -> p h l", l=T),
                        func=AF.Exp), reads=[PB[3].b], writes=[eA.b])
                    yield
                S.op("act", lambda e: e.activation(out=big2.t[0:T, :, 0:T], in_=big2.t[0:T, :, 0:T], func=AF.Exp),
                     reads=[big2.b], writes=[big2.b])
                yield
                for g in range(2):
                    S.op("pe", lambda e, g=g, cs_=cs_: e.matmul(PB[4].t[0:T, 32 + g * 128:32 + g * 128 + T], BCT.t[:, g, cs_],
                                                                 BCT.t[:, 2 + g, cs_], start=True, stop=True),
                         reads=[BCT.b], writes=[PB[4].sub("cb")])
                cbv = PB[4].t[0:T, 32:288].rearrange("p (g l) -> p g l", l=128)[:, :, 0:T]
                S.op("dve", lambda e: TT(e, MT.t[0:T, :, 0:T].rearrange("p (g h) l -> p g h l", h=4),
                                         cbv.unsqueeze(2).to_broadcast([T, 2, 4, T]),
                                         big2.t[0:T, :, 0:T].rearrange("p (g h) l -> p g h l", h=4), ALU.mult),
                     reads=[PB[4].sub("cb"), big2.b], writes=[MT.b])
                yield
                S.op("pool", lambda e, cs_=cs_: TT(e, CdT.t[:, :, 0:T].rearrange("p (g h) l -> p g h l", h=4),
                                                   BCT.t[:, 2:4, cs_].unsqueeze(2).to_broadcast([128, 2, 4, T]),
                                                   eA.t[:, :, 0:T].rearrange("p (g h) l -> p g h l", h=4), ALU.mult),
                     reads=[BCT.b, eA.b], writes=[CdT.b])
                yield
                ypb = PB[7]
                if is_s:
                    S.op("dve", lambda e: e.tensor_copy(out=dAx.t[0:T], in_=dtm.t[0:T, 8:16].unsqueeze(2).to_broadcast([T, 8, 64])),
                         reads=[dtm.b], writes=[dAx.b])
                    for pr in range(4):
                        S.op("pe", lambda e, pr=pr: e.matmul(PB[4].t[:, 288 + pr * 16:288 + (pr + 1) * 16],
                                                             dAx.t[0:T, 2 * pr:2 * pr + 2, :], segi, start=True, stop=True),
                             reads=[dAx.b, cst.b], writes=[PB[4].sub("dec")])
                    S.op("act", lambda e: e.activation(out=decfm.t[:].rearrange("p a s -> p (a s)"), in_=PB[4].t[:, 288:352], func=AF.Exp),
                         reads=[PB[4].sub("dec")], writes=[decfm.b])
                    stv = stssd_d.rearrange("j (pr hl) p n -> j (hl p) pr n", hl=2)
                    osv = o_ssds.rearrange("j (pr hl) p n -> j (hl p) pr n", hl=2)
                    S.dma("act", h0n[0].t[:], stv[0], writes=[h0n[0].b])
                    for j in range(NS):
                        yield
                        jj = j % 2
                        if j + 1 < NS:
                            S.dma("act", h0n[1 - jj].t[:], stv[j + 1], writes=[h0n[1 - jj].b])
                        pbt = PB[jj]
                        for pr in range(4):
                            S.op("pe", lambda e, pr=pr, jj=jj, pbt=pbt: e.transpose(pbt.t[:, pr * 128:(pr + 1) * 128], h0n[jj].t[:, pr, :], ident),
                                 reads=[h0n[jj].b, cst.b], writes=[pbt.b])
                        S.op("act", lambda e, jj=jj, pbt=pbt: e.activation(out=h0T[jj].t[:].rearrange("p h q -> p (h q)"), in_=pbt.t[:, :], func=AF.Copy),
                             reads=[pbt.b], writes=[h0T[jj].b])
                        for h in range(8):
                            pr, hl = h // 2, h % 2
                            S.op("pe", lambda e, h=h, pr=pr, hl=hl, jj=jj, j=j: e.matmul(
                                ypb.t[64 * hl:64 * hl + 64, pr * T + LS * j:pr * T + LS * j + LS], h0T[jj].t[:, h, :],
                                CdT.t[:, h, LS * j:LS * j + LS], start=(j == 0 and pr == 0), stop=False, skip_group_check=True),
                                reads=[h0T[jj].b, CdT.b], writes=[ypb.b])
                        S.op("dve", lambda e, jj=jj, j=j: e.tensor_scalar(out=Bj[jj].t[0:T], in0=Btm.t[0:T], scalar1=segi[:, j:j + 1],
                                                                          scalar2=None, op0=ALU.mult),
                             reads=[Btm.b, cst.b], writes=[Bj[jj].b])
                        pby = PB[3]
                        for pr in range(4):
                            S.op("pe", lambda e, pr=pr, jj=jj, pby=pby: e.matmul(
                                pby.t[:, pr * 128:(pr + 1) * 128], Xdec.t[0:T, 2 * pr:2 * pr + 2, :], Bj[jj].t[0:T, pr // 2, :],
                                start=True, stop=True), reads=[Xdec.b, Bj[jj].b], writes=[pby.b])
                        S.op("dve", lambda e, jj=jj, j=j: TT(e, hn[jj].t[:], h0n[jj].t[:],
                                                             decfm.t[:, :, j:j + 1].to_broadcast([128, 4, 128]), ALU.mult),
                             reads=[h0n[jj].b, decfm.b], writes=[hn[jj].b])
                        S.op("dve", lambda e, jj=jj, pby=pby: TT(e, hn[jj].t[:], hn[jj].t[:],
                                                                 pby.t[:, :].rearrange("p (a n) -> p a n", n=128), ALU.add),
                             reads=[hn[jj].b, pby.b], writes=[hn[jj].b])
                        S.dma("sp", osv[j], hn[jj].t[:], reads=[hn[jj].b], buf=hn[jj].b)
                    outbufs.extend([hn[0].b, hn[1].b])
                for h in range(8):
                    pr, hl = h // 2, h % 2
                    out = ypb.t[64 * hl:64 * hl + 64, pr * T:(pr + 1) * T]
                    S.op("pe", lambda e, h=h, out=out, pr=pr: e.matmul(out, Xtm.t[0:T, h, :], MT.t[0:T, h, 0:T],
                                                                       start=(pr == 0 and not is_s), stop=is_s, skip_group_check=True),
                         reads=[Xtm.b, MT.b], writes=[ypb.b])
                    if not is_s:
                        S.op("pe", lambda e, h=h, out=out: e.matmul(out, STb.t[:, h, :], CdT.t[:, h, 0:T], start=False, stop=True,
                                                                    skip_group_check=True),
                             reads=[STb.b, CdT.b], writes=[ypb.b])
                yield
                for pr in range(4):
                    S.op("dve", lambda e, pr=pr, cs_=cs_: e.scalar_tensor_tensor(
                        out=yg.t[:, pr, 0:T], in0=xsT.t[:, pr, cs_], scalar=prm.t[:, P_SSDFM + pr:P_SSDFM + pr + 1],
                        in1=ypb.t[:, pr * T:(pr + 1) * T], op0=ALU.mult, op1=ALU.add),
                        reads=[xsT.b, prm.b, ypb.b], writes=[yg.b])
                S.op("dve", lambda e, cs_=cs_: TT(e, yg.t[:, :, 0:T], yg.t[:, :, 0:T], szT.t[:, :, cs_], ALU.mult),
                     reads=[yg.b, szT.b], writes=[yg.b])
                S.op("dve", lambda e: TT(e, ysqb.t[:, :, 0:T], yg.t[:, :, 0:T], yg.t[:, :, 0:T], ALU.mult),
                     reads=[yg.b], writes=[ysqb.b])
                for g in range(2):
                    for k in range(2):
                        S.op("pe", lambda e, g=g, k=k: e.matmul(PB[3].t[:, g * T:(g + 1) * T], onesb1.t[:], ysqb.t[:, 2 * g + k, 0:T],
                                                                start=(k == 0), stop=(k == 1)),
                             reads=[onesb1.b, ysqb.b], writes=[PB[3].b])
                S.op("act", lambda e: e.activation(out=rsb.t[:, :, 0:T], in_=PB[3].t[:, 0:2 * T].rearrange("p (g l) -> p g l", l=T),
                                                   func=AF.Ln, scale=1.0 / 256, bias=EPS), reads=[PB[3].b], writes=[rsb.b])
                S.op("act", lambda e: e.activation(out=rsb.t[:, :, 0:T], in_=rsb.t[:, :, 0:T], func=AF.Exp, scale=-0.5),
                     reads=[rsb.b], writes=[rsb.b])
                for pr in range(4):
                    S.op("dve", lambda e, pr=pr: e.scalar_tensor_tensor(
                        out=mixt[ti % 2].t[:, pr, c0:c0 + T], in0=yg.t[:, pr, 0:T],
                        scalar=prm.t[:, P_SSDFM + 4 + pr:P_SSDFM + 5 + pr], in1=rsb.t[:, pr // 2, 0:T], op0=ALU.mult, op1=ALU.mult),
                        reads=[yg.b, prm.b, rsb.b], writes=[mixt[ti % 2].sub("ssd")])
                yield
                if not is_s:
                    for g in range(2):
                        S.op("pe", lambda e, g=g: e.matmul(PB[6].t[:, g * 256:(g + 1) * 256], Btm.t[0:T, g, :],
                                                           Xdec.t[0:T, 4 * g:4 * g + 4, :], start=True, stop=True),
                             reads=[Btm.b, Xdec.b], writes=[PB[6].b])
                    S.op("dve", lambda e: TT(e, ST.t[:], ST.t[:], eA.t[:, :, T - 1:T].to_broadcast([128, 8, 64]), ALU.mult),
                         reads=[ST.b, eA.b], writes=[ST.b])
                    S.op("dve", lambda e: TT(e, ST.t[:], ST.t[:], PB[6].t[:, :].rearrange("p (h q) -> p h q", q=64), ALU.add),
                         reads=[ST.b, PB[6].b], writes=[ST.b])
                    S.op("act", lambda e: e.activation(out=STb.t[:], in_=ST.t[:], func=AF.Copy), reads=[ST.b], writes=[STb.b])
            if ti == 7:
                S.dma("sp", o_ssdp, ST.t[:].rearrange("p h q -> p (h q)"), reads=[ST.b], buf=ST.b)
                outbufs.append(ST.b)

            ckpt("D%d" % ti)
            yield

        def chain2(ti):
            t0, NT, is_s = TILES_A[ti]
            u5T = u5Ts[ti % 2]
            if is_s:
                S.dma("sp", sts5.t[:].rearrange("p a s q -> p (a s q)"), sts5_d, writes=[sts5.b])
            if not is_s:
                groups = [(list(range(16)), k * T5, T5) for k in range(NT // T5)]
            else:
                groups = [(list(range(8)), 0, 64), (list(range(8, 16)), 0, 64)]
            def emit_bu(g_):
                slist_, tk0_, ntok_ = groups[g_]
                bus = busd[g_ % 2]
                for part, pb in ((0, PB[5]), (1, PB[6])):
                    for idx, s in enumerate(slist_):
                        S.op("pe", lambda e, part=part, pb=pb, idx=idx, s=s: e.matmul(
                            pb.t[:, idx * ntok_:(idx + 1) * ntok_], s5BT.t[:, part, s, :], u5T.t[:, s // 4, tk0_:tk0_ + ntok_],
                            start=True, stop=True), reads=[s5BT.b, u5T.b], writes=[pb.b])
                S.op("act", lambda e: e.activation(out=bus[0].t[:], in_=PB[5].t[:, :], func=AF.Copy), reads=[PB[5].b], writes=[bus[0].b])
                S.op("act", lambda e: e.activation(out=bus[1].t[:], in_=PB[6].t[:, :], func=AF.Copy), reads=[PB[6].b], writes=[bus[1].b])
            def views(g_):
                slist_, tk0_, ntok_ = groups[g_]
                s0_ = slist_[0]
                if not is_s:
                    V3 = lambda ap: ap.rearrange("p (s t) -> p s t", t=T5)
                    QR, QI = Qtab.t[:, 0], Qtab.t[:, 1]
                    PR_, PI_ = Ptab.t[:, 0], Ptab.t[:, 1]
                    msk = mask32.t[:].rearrange("p s t -> p (s t)")
                    first = lambda ap: V3(ap)[:, :, 0]
                    cin_r, cin_i = s5cr.t[:, 0, :], s5cr.t[:, 1, :]
                else:
                    V3 = lambda ap: ap.rearrange("p (s q b) -> p s q b", q=NS, b=LS)
                    bc = lambda ap: ap.unsqueeze(2).to_broadcast([128, 8, NS, LS])
                    QR, QI = bc(Qtab.t[:, 0, s0_:s0_ + 8, 0:LS]), bc(Qtab.t[:, 1, s0_:s0_ + 8, 0:LS])
                    PR_, PI_ = bc(Ptab.t[:, 0, s0_:s0_ + 8, 0:LS]), bc(Ptab.t[:, 1, s0_:s0_ + 8, 0:LS])
                    msk = mask4.t[:].rearrange("p s t -> p (s t)")
                    first = lambda ap: V3(ap)[:, :, :, 0]
                    cin_r, cin_i = sts5.t[:, 0, s0_:s0_ + 8, :], sts5.t[:, 1, s0_:s0_ + 8, :]
                return V3, QR, QI, PR_, PI_, msk, first, cin_r, cin_i
            vsets = [[s5v[0], s5v[1]], [s5vb[0], s5vb[1]]]

            def mults_adds(g_):
                V3, QR, QI, PR_, PI_, msk, first, cin_r, cin_i = views(g_)
                bus = busd[g_ % 2]
                br, bi = V3(bus[0].t[:]), V3(bus[1].t[:])
                t1, t2, t3, t4 = s5t[0], s5t[1], s5t34[0], s5t34[1]
                vr, vi = vsets[g_ % 2]
                tb = [Qtab.b]
                for (o, a, b_, rd) in ((t1, QR, br, bus[0].b), (t2, QI, bi, bus[1].b), (t3, QR, bi, bus[1].b), (t4, QI, br, bus[0].b)):
                    S.op("dve", lambda e, o=o, a=a, b_=b_: TT(e, V3(o.t[:]), a, b_, ALU.mult), reads=tb + [rd], writes=[o.b])
                S.op(ENG_ADDS, lambda e: TT(e, vr.t[:], t1.t[:], t2.t[:], ALU.subtract), reads=[t1.b, t2.b], writes=[vr.b])
                S.op(ENG_ADDS, lambda e: TT(e, vi.t[:], t3.t[:], t4.t[:], ALU.add), reads=[t3.b, t4.b], writes=[vi.b])
            emit_bu(0)
            if len(groups) > 1:
                emit_bu(1)
            mults_adds(0)
            pend_y5 = [None]
            for gi_, (slist, tk0, ntok) in enumerate(groups):
                yield
                ns = len(slist)
                s0 = slist[0]
                V3, QR, QI, PR_, PI_, msk, first, cin_r, cin_i = views(gi_)
                vr, vi = vsets[gi_ % 2]
                if gi_ + 1 < len(groups):
                    mults_adds(gi_ + 1)
                    yield
                if gi_ + 2 < len(groups):
                    emit_bu(gi_ + 2)
                S.op("dve", lambda e: TT(e, first(vr.t[:]), first(vr.t[:]), cin_r, ALU.add), reads=[vr.b, s5cr.b, sts5.b], writes=[vr.b])
                S.op("dve", lambda e: TT(e, first(vi.t[:]), first(vi.t[:]), cin_i, ALU.add), reads=[vi.b, s5cr.b, sts5.b], writes=[vi.b])
                yield
                s5k[0] ^= 1
                gr, gi2 = s5g[s5k[0]][0], s5g[s5k[0]][1]
                S.op("dve", lambda e: e.tensor_tensor_scan(out=gr.t[:], data0=msk, data1=vr.t[:], initial=0.0, op0=ALU.mult, op1=ALU.add),
                     reads=[vr.b, mask32.b, mask4.b], writes=[gr.b])
                S.op("dve", lambda e: e.tensor_tensor_scan(out=gi2.t[:], data0=msk, data1=vi.t[:], initial=0.0, op0=ALU.mult, op1=ALU.add),
                     reads=[vi.b, mask32.b, mask4.b], writes=[gi2.b])
                yield
                hp = s5h[gi_ % 2]
                hr, hi = hp, hp
                for (o, a, b_) in ((hp[0], PR_, gr), (hp[1], PI_, gi2), (hp[2], PR_, gi2), (hp[3], PI_, gr)):
                    S.op(ENG_OUTROT, lambda e, o=o, a=a, b_=b_: TT(e, V3(o.t[:]), a, V3(b_.t[:]), ALU.mult),
                         reads=[Ptab.b, b_.b], writes=[o.b])
                yield
                if not is_s:
                    glr, gli = V3(gr.t[:])[:, :, T5 - 1], V3(gi2.t[:])[:, :, T5 - 1]
                    plr, pli = Ptab.t[:, 0, :, T5 - 1], Ptab.t[:, 1, :, T5 - 1]
                    c_ = lambda i: s5c.t[:, i, :]
                    outr, outi = s5cr.t[:, 0, :], s5cr.t[:, 1, :]
                else:
                    glr, gli = V3(gr.t[:])[:, :, :, LS - 1], V3(gi2.t[:])[:, :, :, LS - 1]
                    plr = Ptab.t[:, 0, s0:s0 + 8, LS - 1:LS].to_broadcast([128, 8, NS])
                    pli = Ptab.t[:, 1, s0:s0 + 8, LS - 1:LS].to_broadcast([128, 8, NS])
                    c_ = lambda i: hn[0].t[:, i, :].rearrange("p (s q) -> p s q", q=NS)
                    outr, outi = s5fin.t[:, 0, s0:s0 + 8, 1:17], s5fin.t[:, 1, s0:s0 + 8, 1:17]
                cb_ = [s5c.b, hn[0].b]
                if not is_s:
                    pl2 = Ptab.t[:, :, :, T5 - 1]
                    ca, cb2 = s5c.t[:, 0:2, :], s5c.t[:, 2:4, :]
                    S.op("dve", lambda e: TT(e, ca, pl2, glr.unsqueeze(1).to_broadcast([128, 2, 16]), ALU.mult),
                         reads=[Ptab.b, gr.b] + cb_, writes=cb_)
                    S.op("dve", lambda e: TT(e, cb2, pl2, gli.unsqueeze(1).to_broadcast([128, 2, 16]), ALU.mult),
                         reads=[Ptab.b, gi2.b] + cb_, writes=cb_)
                    S.op("dve", lambda e: TT(e, outr, c_(0), c_(3), ALU.subtract), reads=cb_, writes=[s5cr.b, s5fin.b])
                    S.op("dve", lambda e: TT(e, outi, c_(2), c_(1), ALU.add), reads=cb_, writes=[s5cr.b, s5fin.b])
                else:
                    cseq = [(c_(0), plr, glr, ALU.mult), (c_(1), pli, gli, ALU.mult), (c_(2), plr, gli, ALU.mult), (c_(3), pli, glr, ALU.mult)]
                    for (o, a, b, op) in cseq:
                        S.op("dve", lambda e, o=o, a=a, b=b, op=op: TT(e, o, a, b, op), reads=[Ptab.b, gr.b, gi2.b] + cb_, writes=cb_)
                    S.op("dve", lambda e: TT(e, outr, c_(0), c_(1), ALU.subtract), reads=cb_, writes=[s5cr.b, s5fin.b])
                    S.op("dve", lambda e: TT(e, outi, c_(2), c_(3), ALU.add), reads=cb_, writes=[s5cr.b, s5fin.b])
                yield
                def emit_y5(gi_=gi_, slist=slist, tk0=tk0, ntok=ntok, hr=hr, hi=hi):
                    y5c0 = 352
                    nq = 4 if not is_s else 2
                    for qi in range(nq):
                        q = qi if not is_s else 2 * gi_ + qi
                        S.op("pe", lambda e, q=q, qi=qi: e.matmul(PB[4].t[:, y5c0 + qi * ntok:y5c0 + (qi + 1) * ntok], dg5.t[:, q, :],
                                                                  u5T.t[:, q, tk0:tk0 + ntok], start=(qi == 0), stop=False, skip_group_check=True),
                             reads=[dg5.b, u5T.b], writes=[PB[4].sub("y5")])
                    for idx, s in enumerate(slist):
                        qi = (s // 4) if not is_s else (s // 4 - 2 * gi_)
                        out = PB[4].t[32 * (s % 4):32 * (s % 4) + 32, y5c0 + qi * ntok:y5c0 + (qi + 1) * ntok]
                        for j4, lw in enumerate((s5CT.t[:, 0, s, :], s5CTn.t[:, s, :], s5CT.t[:, 1, s, :], s5CT.t[:, 1, s, :])):
                            S.op("pe", lambda e, j4=j4, lw=lw: e.matmul(out, lw, hr[j4].t[:, idx * ntok:(idx + 1) * ntok],
                                                                        start=False, stop=(j4 == 3), skip_group_check=True,
                                                                        tile_position=(0, 32 * (s % 4))),
                                 reads=[s5CT.b, s5CTn.b, hr[j4].b], writes=[PB[4].sub("y5")])
                    q0 = 0 if not is_s else 2 * gi_
                    S.op("act", lambda e: e.activation(out=y5pre.t[:, q0:q0 + nq, tk0:tk0 + ntok],
                                                       in_=PB[4].t[:, y5c0:y5c0 + nq * ntok].rearrange("p (q t) -> p q t", t=ntok), func=AF.Copy),
                         reads=[PB[4].sub("y5")], writes=[y5pre.b])
                if pend_y5[0] is not None:
                    pend_y5[0]()
                    yield
                pend_y5[0] = emit_y5
            if pend_y5[0] is not None:
                pend_y5[0]()
                pend_y5[0] = None
                yield
            if ti == 7:
                S.op("dve", lambda e: e.tensor_copy(out=s5fin.t[:, :, :, 0], in_=s5cr.t[:]), reads=[s5cr.b], writes=[s5fin.b])
            if is_s:
                S.dma("sp", o_s5, s5fin.t[:].rearrange("p a s q -> p (a s q)"), reads=[s5fin.b], buf=s5fin.b)
                outbufs.append(s5fin.b)
            if ti == 0:
                dump("y5pre", y5pre.t[:].rearrange("p k t -> p (k t)"), [128, 4 * NTM], [y5pre.b])
            ckpt("E%d" % ti)
            yield
            S.op("act", lambda e: e.activation(out=g5.t[:, :, 0:NT], in_=y5pre.t[:, :, 0:NT], func=AF.Gelu), reads=[y5pre.b], writes=[g5.b])
            for m in range(4):
                yield
                pb = next_pb()
                for q in range(4):
                    S.op("pe", lambda e, m=m, q=q, pb=pb: e.matmul(pb.t[:, 0:NT], wglu_sb.t[:, q, m * 128:(m + 1) * 128], g5.t[:, q, 0:NT],
                                                                   start=(q == 0), stop=(q == 3)),
                         reads=[wglu_sb.b, g5.b], writes=[pb.b])
                S.op("act", lambda e, m=m, pb=pb: e.activation(out=sgl.t[:, 0:NT], in_=pb.t[:, 0:NT], func=AF.Sigmoid,
                                                               bias=prm.t[:, P_S5M + 4 + m:P_S5M + 5 + m]),
                     reads=[pb.b, prm.b], writes=[sgl.b])
                S.op("dve", lambda e, m=m: TT(e, mixt[ti % 2].t[:, 4 + m, 0:NT], g5.t[:, m, 0:NT], sgl.t[:, 0:NT], ALU.mult),
                     reads=[g5.b, sgl.b], writes=[mixt[ti % 2].sub("s5")])
            S.dma("sp", mixd[:, :, t0:t0 + NT], mixt[ti % 2].t[:, :, 0:NT], reads=mixt[ti % 2].allb(), writes=[mixdb[ti]], buf=mixdb[ti])
            ckpt("T%d" % ti)
            if ti == 0:
                dump("mix0", mixt[0].t[:, :, 0:NTM], [128, 8, NTM], mixt[0].allb())
            yield

        import os as _os
        RATIO = int(_os.environ.get("K_RATIO", "1"))
        HEAD = int(_os.environ.get("K_HEAD", "9"))
        HEADB = int(_os.environ.get("K_HEADB", "10"))

        def drive(gens, ada_every=0, head=0):
            gens = [g for g in gens if g is not None]
            n = 0
            if len(gens) > 1:
                for _ in range(head):
                    try:
                        next(gens[0])
                    except StopIteration:
                        gens.pop(0)
                        break
            while gens:
                for gi__, g in enumerate(list(gens)):
                    for _ in range((RATIO if gi__ == 0 else 1) if RATIO > 0 else (-RATIO if gi__ == 1 else 1)):
                        try:
                            next(g)
                        except StopIteration:
                            if g in gens:
                                gens.remove(g)
                            break
                n += 1
                if ada_every and n % ada_every == 0:
                    ada_step()
        ada_state[0] = 0
        drive([chain1(0)], ada_every=12)
        for ti_ in range(len(TILES_A)):
            if ti_ == 7:
                while ada_state[1] < len(ADA_CH):
                    ada_step()
                fill_x(a1x, amod.t[:, 0:8, 1:17], [amod.b])
                fill_x(sh1x, chunkmod(MOD_SH1)[:, :, 1:17], [mod.b])
                make_amod([(1, (4, 1)), (2, (7, 2))])
            drive([chain2(ti_), chain1(ti_ + 1) if ti_ + 1 < len(TILES_A) else None], ada_every=(8 if ti_ < 7 else 0), head=HEAD)
        dump("mixS", mixt[0].t[:, :, 0:64], [128, 8, 64], mixt[0].allb())
        S.barrier()
        ckpt("1a")
        A.lo = LO_GLOBAL
        x1T = A.alloc("x1T", [128, 8, NTOK], F32, top=True)
        vT = A.alloc("vT", [128, 8, NTOK], BF16, top=True)
        pre_g = [A.alloc("wgs%dt" % i, [128, 8, 256], BF16, top=True) for i in range(2)]
        pre_u = [A.alloc("wus%dt" % i, [128, 8, 256], BF16, top=True) for i in range(2)]
        wout_sb = A.alloc("wout_sb", [128, 8, D], BF16)
        wout_v = wout.rearrange("(kt p) n -> p kt n", p=128)
        for kh in range(4):
            S.dma("pool", wout_sb.t[:, 2 * kh:2 * kh + 2, :], wout_v[:, 2 * kh:2 * kh + 2, :], writes=[wout_sb.sub(kh)])
        mixb = [A.alloc("mixb%d" % i, [128, 8, 512], BF16) for i in range(2)]

        def load_mix(ti):
            t0, NT, is_s = TILES_B[ti]
            tiles_a = [i for i, (a0, n0, s0_) in enumerate(TILES_A) if a0 >= t0 and a0 < t0 + NT]
            S.dma("sp", mixb[ti % 2].t[:, :, 0:NT], mixd[:, :, t0:t0 + NT], reads=[mixdb[i] for i in tiles_a], writes=[mixb[ti % 2].b])
        wg_v = wg.rearrange("(kt p) n -> p kt n", p=128)
        wu_v = wu.rearrange("(kt p) n -> p kt n", p=128)
        for si_ in range(2):
            S.dma("pool", pre_g[si_].t[:], wg_v[:, :, si_ * 256:(si_ + 1) * 256], writes=[pre_g[si_].b])
            S.dma("pool", pre_u[si_].t[:], wu_v[:, :, si_ * 256:(si_ + 1) * 256], writes=[pre_u[si_].b])
        xtm2 = A.alloc("xtm2", [128, 4, D], F32)
        xTm = [A.alloc("xTm%d" % i, [128, 512], F32) for i in range(2)]
        sqb = [A.alloc("sqb%d" % i, [128, 512], BF16) for i in range(2)]
        onesb = A.alloc("onesb", [128, 128], BF16)
        S.op("dve", lambda e: e.memset(onesb.t[:], 1.0), writes=[onesb.b])
        tmp2 = [A.alloc("tmp2_%d" % i, [128, 512], F32) for i in range(2)]
        rstdb = [A.alloc("rstdb%d" % i, [128, 512], F32) for i in range(2)]
        g1x = expand_mod("g1x", chunkmod(MOD_G1)[:, :, 1:17], [mod.b])
        a2x = expand_mod("a2x", amod.t[:, 8:16, 1:17], [amod.b])
        sh2x = expand_mod("sh2x", chunkmod(MOD_SH2)[:, :, 1:17], [mod.b])
        print("arena p1b: lo=%d hi=%d" % (A.lo, A.hi))
        TILES_B = [(i * 512, 512, False) for i in range(4)] + [(SEQ, 64, True)]

        def load_x2(ti):
            t0, NT, is_s = TILES_B[ti]
            for blk in range((NT + 127) // 128):
                rows = min(128, NT - blk * 128)
                S.dma("sp", xtm2.t[0:rows, blk, :], xin[t0 + blk * 128:t0 + blk * 128 + rows, :], writes=[xtm2.sub(blk)])
        load_x2(0)
        load_mix(0)

        def stat_accum(src_ap, m, NT, pbs, defer=None):
            sq = sqb[m % 2]
            S.op("act", lambda e: e.activation(out=sq.t[:, 0:NT], in_=src_ap, func=AF.Square), reads=[x1T.sub(m)], writes=[sq.b])

            def mm(m=m, sq=sq):
                S.op("pe", lambda e: e.matmul(pbs.t[:, 0:NT], onesb.t[:], sq.t[:, 0:NT], start=(m == 0), stop=(m == 7)),
                     reads=[onesb.b, sq.b], writes=[pbs.b])
            if defer is None:
                mm()
            else:
                if defer[0] is not None:
                    defer[0]()
                defer[0] = mm
                if m == 7:
                    defer[0]()
                    defer[0] = None

        def stat_finish(NT, pbs, rs):
            S.op("act", lambda e: e.activation(out=rs.t[:, 0:NT], in_=pbs.t[:, 0:NT], func=AF.Ln, scale=1.0 / D, bias=EPS),
                 reads=[pbs.b], writes=[rs.b])
            S.op("act", lambda e: e.activation(out=rs.t[:, 0:NT], in_=rs.t[:, 0:NT], func=AF.Exp, scale=-0.5), reads=[rs.b], writes=[rs.b])

        def b_part1(ti):
            t0, NT, is_s = TILES_B[ti]
            nblk = (NT + 127) // 128
            tsl = slice(t0, t0 + NT)
            pbs = PB[4 + ti % 2]
            dfr = [None]
            for m in range(8):
                pbx = PB[2 + m % 2]
                xm = xTm[m % 2]
                for blk in range(nblk):
                    rows = min(128, NT - blk * 128)
                    S.op("pe", lambda e, blk=blk, rows=rows: e.transpose(
                        pbx.t[:, blk * 128:blk * 128 + rows], xtm2.t[0:rows, blk, m * 128:(m + 1) * 128], cst.t[0:rows, C_ID:C_ID + rows]),
                        reads=[xtm2.sub(blk), cst.b], writes=[pbx.b])
                S.op("act", lambda e: e.activation(out=xm.t[:, 0:NT], in_=pbx.t[:, 0:NT], func=AF.Copy), reads=[pbx.b], writes=[xm.b])
                pb = next_pb()
                for kt in range(8):
                    S.op("pe", lambda e, kt=kt: e.matmul(pb.t[:, 0:NT], wout_sb.t[:, kt, m * 128:(m + 1) * 128], mixb[ti % 2].t[:, kt, 0:NT],
                                                         start=(kt == 0), stop=(kt == 7)),
                         reads=[wout_sb.sub(kt // 2), mixb[ti % 2].b], writes=[pb.b])
                if m == 0 and ti + 1 < len(TILES_B):
                    load_mix(ti + 1)
                if not is_s:
                    S.op("dve", lambda e: e.scalar_tensor_tensor(
                        out=x1T.t[:, m, tsl], in0=pb.t[:, 0:NT], scalar=mod.t[:, 8 * MOD_G1 + m, 0:1], in1=xm.t[:, 0:NT],
                        op0=ALU.mult, op1=ALU.add), reads=[pb.b, mod.b, xm.b], writes=[x1T.sub(m)])
                else:
                    S.op("dve", lambda e: TT(e, tmp2[0].t[:, 0:NT], pb.t[:, 0:NT], g1x.t[:, m, :], ALU.mult),
                         reads=[pb.b, g1x.b], writes=[tmp2[0].b])
                    S.op("dve", lambda e: TT(e, x1T.t[:, m, tsl], tmp2[0].t[:, 0:NT], xm.t[:, 0:NT], ALU.add),
                         reads=[tmp2[0].b, xm.b], writes=[x1T.sub(m)])
                stat_accum(x1T.t[:, m, tsl], m, NT, pbs, defer=dfr)
                yield
            if ti + 1 < len(TILES_B):
                load_x2(ti + 1)
            yield

        def b_part2(ti):
            t0, NT, is_s = TILES_B[ti]
            tsl = slice(t0, t0 + NT)
            rs = rstdb[ti % 2]
            stat_finish(NT, PB[4 + ti % 2], rs)
            yield
            for m in range(8):
                tq = tmp2[m % 2]
                S.op("dve", lambda e: TT(e, tq.t[:, 0:NT], x1T.t[:, m, tsl], rs.t[:, 0:NT], ALU.mult),
                     reads=[x1T.sub(m), rs.b], writes=[tq.b])
                if not is_s:
                    S.op("act", lambda e: e.activation(out=vT.t[:, m, tsl], in_=tq.t[:, 0:NT], func=AF.Identity,
                                                       scale=amod.t[:, 8 + m, 0:1], bias=mod.t[:, 8 * MOD_SH2 + m, 0:1]),
                         reads=[tq.b, amod.b, mod.b], writes=[vT.sub(m)])
                else:
                    S.op("dve", lambda e: TT(e, tq.t[:, 0:NT], tq.t[:, 0:NT], a2x.t[:, m, :], ALU.mult),
                         reads=[tq.b, a2x.b], writes=[tq.b])
                    S.op("dve", lambda e: TT(e, vT.t[:, m, tsl], tq.t[:, 0:NT], sh2x.t[:, m, :], ALU.add),
                         reads=[tq.b, sh2x.b], writes=[vT.sub(m)])
                yield
            if ti == 0:
                dump("x1p", x1T.t[:, :, 0:256], [128, 8, 256], x1T.allb())
                dump("vp", vT.t[:, :, 0:256], [128, 8, 256], vT.allb())
        drive([b_part1(0)])
        for ti_ in range(len(TILES_B)):
            drive([b_part2(ti_), b_part1(ti_ + 1) if ti_ + 1 < len(TILES_B) else None], head=HEADB)
        S.barrier()
        ckpt("1b")

        A.lo = LO_GLOBAL
        tmp2 = [A.alloc("tmp3_%d" % i, [128, 512], F32) for i in range(2)]
        rstdb = [A.alloc("rstd3_%d" % i, [128, 512], F32) for i in range(2)]
        sqb = [A.alloc("sqb3_%d" % i, [128, 512], BF16) for i in range(2)]
        onesb = A.alloc("onesb3", [128, 128], BF16)
        S.op("dve", lambda e: e.memset(onesb.t[:], 1.0), writes=[onesb.b])
        g2x = expand_mod("g2x", chunkmod(MOD_G2)[:, :, 1:17], [mod.b])
        afx = expand_mod("afx", amod.t[:, 16:24, 1:17], [amod.b])
        shfx = expand_mod("shfx", chunkmod(MOD_SHF)[:, :, 1:17], [mod.b])
        LO_P2 = A.lo
        hT = A.alloc("hT", [128, 6, NTOK], BF16)
        wgs = pre_g + [A.alloc("wgs2", [128, 8, 256], BF16)]
        wus = pre_u + [A.alloc("wus2", [128, 8, 256], BF16)]
        wds = [A.alloc("wds%d" % i, [128, 6, D], BF16) for i in range(2)]
        sgt = [A.alloc("sgt%d" % i, [128, 512], BF16) for i in range(2)]
        print("arena p2: lo=%d hi=%d" % (A.lo, A.hi))
        wg_v = wg.rearrange("(kt p) n -> p kt n", p=128)
        wu_v = wu.rearrange("(kt p) n -> p kt n", p=128)
        wd_v = wd.rearrange("(j p) n -> p j n", p=128)
        QUARTERS = [(0, 6), (6, 12), (12, 18), (18, 22)]
        SLABS = [(q, ja + 2 * s) for q, (ja, jb) in enumerate(QUARTERS) for s in range((jb - ja) // 2)]

        def load_gu(si):
            q, j0 = SLABS[si]
            S.dma("pool", wgs[si % 3].t[:], wg_v[:, :, j0 * 128:(j0 + 2) * 128], writes=[wgs[si % 3].b])
            S.dma("pool", wus[si % 3].t[:], wu_v[:, :, j0 * 128:(j0 + 2) * 128], writes=[wus[si % 3].b])

        def load_wd(q):
            ja, jb = QUARTERS[q]
            for jh in range(0, jb - ja, 2):
                S.dma("pool", wds[q % 2].t[:, jh:jh + 2, :], wd_v[:, ja + jh:ja + jh + 2, :], writes=[wds[q % 2].b])
        assert SLABS[0][1] == 0 and SLABS[1][1] == 2
        load_wd(0)
        gbank = [0]
        si = 0
        for q, (ja, jb) in enumerate(QUARTERS):
            if q + 1 < 4:
                load_wd(q + 1)
            for s in range((jb - ja) // 2):
                if si + 2 < len(SLABS):
                    load_gu(si + 2)
                wgt, wut = wgs[si % 3], wus[si % 3]
                for jc in range(2):
                    jj = 2 * s + jc
                    for (t0, NT, is_s) in TILES_B:
                        tsl = slice(t0, t0 + NT)
                        gbank[0] ^= 1
                        pbg, pbu = PB[gbank[0]], PB[2 + gbank[0]]
                        for (wt, pb_) in ((wgt, pbg), (wut, pbu)):
                            for kt in range(8):
                                S.op("pe", lambda e, kt=kt, wt=wt, pb_=pb_: e.matmul(
                                    pb_.t[:, 0:NT], wt.t[:, kt, jc * 128:(jc + 1) * 128], vT.t[:, kt, tsl], start=(kt == 0), stop=(kt == 7)),
                                    reads=[wt.b] + vT.allb(), writes=[pb_.b])
                        sg_ = sgt[gbank[0]]
                        S.op("act", lambda e, pbg=pbg, sg_=sg_: e.activation(out=sg_.t[:, 0:NT], in_=pbg.t[:, 0:NT], func=AF.Silu),
                             reads=[pbg.b], writes=[sg_.b])
                        S.op("dve", lambda e, pbu=pbu, sg_=sg_: TT(e, hT.t[:, jj, tsl], sg_.t[:, 0:NT], pbu.t[:, 0:NT], ALU.mult),
                             reads=[sg_.b, pbu.b], writes=[hT.sub(jj)])
                si += 1
            nj = jb - ja
            wdt = wds[q % 2]
            for (t0, NT, is_s) in TILES_B:
                tsl = slice(t0, t0 + NT)
                for m in range(8):
                    pb = PB[4 + m % 2]
                    for jj in range(nj):
                        S.op("pe", lambda e, jj=jj, m=m, pb=pb: e.matmul(pb.t[:, 0:NT], wdt.t[:, jj, m * 128:(m + 1) * 128], hT.t[:, jj, tsl],
                                                                         start=(jj == 0), stop=(jj == nj - 1)),
                             reads=[wdt.b, hT.sub(jj)], writes=[pb.b])
                    if not is_s:
                        S.op("dve", lambda e, m=m, pb=pb: e.scalar_tensor_tensor(
                            out=x1T.t[:, m, tsl], in0=pb.t[:, 0:NT], scalar=mod.t[:, 8 * MOD_G2 + m, 0:1], in1=x1T.t[:, m, tsl],
                            op0=ALU.mult, op1=ALU.add), reads=[pb.b, mod.b, x1T.sub(m)], writes=[x1T.sub(m)])
                    else:
                        S.op("dve", lambda e, m=m, pb=pb: TT(e, tmp2[0].t[:, 0:NT], pb.t[:, 0:NT], g2x.t[:, m, :], ALU.mult),
                             reads=[pb.b, g2x.b], writes=[tmp2[0].b])
                        S.op("dve", lambda e, m=m: TT(e, x1T.t[:, m, tsl], tmp2[0].t[:, 0:NT], x1T.t[:, m, tsl], ALU.add),
                             reads=[tmp2[0].b, x1T.sub(m)], writes=[x1T.sub(m)])
        S.barrier()
        ckpt("ffn")
        A.lo = LO_P2
        yTs = [A.alloc("yT%d" % i, [128, 8, 512], F32) for i in range(2)]
        ytm = [A.alloc("ytm%d" % i, [128, D], F32) for i in range(2)]
        print("arena final: lo=%d hi=%d" % (A.lo, A.hi))
        oi = [0]

        def f_part1(ti):
            t0, NT, is_s = TILES_B[ti]
            tsl = slice(t0, t0 + NT)
            yT = yTs[ti % 2]
            pbs = PB[6 + ti % 2]
            rs = rstdb[ti % 2]
            for m in range(8):
                stat_accum(x1T.t[:, m, tsl], m, NT, pbs)
                if m % 2 == 1:
                    yield
            stat_finish(NT, pbs, rs)
            yield
            for m in range(8):
                tq = tmp2[m % 2]
                S.op("dve", lambda e: TT(e, tq.t[:, 0:NT], x1T.t[:, m, tsl], rs.t[:, 0:NT], ALU.mult),
                     reads=[x1T.sub(m), rs.b], writes=[tq.b])
                if not is_s:
                    S.op("act", lambda e: e.activation(out=yT.t[:, m, 0:NT], in_=tq.t[:, 0:NT], func=AF.Identity,
                                                       scale=amod.t[:, 16 + m, 0:1], bias=mod.t[:, 8 * MOD_SHF + m, 0:1]),
                         reads=[tq.b, amod.b, mod.b], writes=[yT.sub(m)])
                else:
                    S.op("dve", lambda e: TT(e, tq.t[:, 0:NT], tq.t[:, 0:NT], afx.t[:, m, :], ALU.mult),
                         reads=[tq.b, afx.b], writes=[tq.b])
                    S.op("dve", lambda e: TT(e, yT.t[:, m, 0:NT], tq.t[:, 0:NT], shfx.t[:, m, :], ALU.add),
                         reads=[tq.b, shfx.b], writes=[yT.sub(m)])
                yield

        def f_part2(ti):
            t0, NT, is_s = TILES_B[ti]
            yT = yTs[ti % 2]
            for blk in range((NT + 127) // 128):
                rows = min(128, NT - blk * 128)
                yo = ytm[oi[0] % 2]
                oi[0] += 1
                for half in range(2):
                    pbt = PB[half]
                    for k4 in range(4):
                        kt = 4 * half + k4
                        S.op("pe", lambda e, kt=kt, k4=k4: e.transpose(
                            pbt.t[0:rows, k4 * 128:(k4 + 1) * 128], yT.t[:, kt, blk * 128:blk * 128 + rows], ident),
                            reads=[yT.sub(kt), cst.b], writes=[pbt.b])
                    if half == 0:
                        S.op("act", lambda e: e.activation(out=yo.t[0:rows, 0:512], in_=pbt.t[0:rows, :], func=AF.Copy),
                             reads=[pbt.b], writes=[yo.b])
                    else:
                        S.op("dve", lambda e: e.tensor_copy(out=yo.t[0:rows, 512:1024], in_=pbt.t[0:rows, :]),
                             reads=[pbt.b], writes=[yo.b])
                    yield
                S.dma("sp", yout[t0 + blk * 128:t0 + blk * 128 + rows, :], yo.t[0:rows, :], reads=[yo.b], buf=yo.b)
        import os as _os2
        if True:
            for ti_ in range(len(TILES_B)):
                drive([f_part1(ti_)])
                drive([f_part2(ti_)])
        else:
            drive([f_part1(0)])
            for ti_ in range(len(TILES_B)):
                drive([f_part2(ti_), f_part1(ti_ + 1) if ti_ + 1 < len(TILES_B) else None])
        S.barrier()
    return nc, dumps


def _prep_inputs(inp):
    cstv = _consts()
    prmv = _params(inp)
    BT, CT = _s5mats(inp)
    maps = []
    for i in range(NCORES):
        m = {}
        m["xin"] = np.ascontiguousarray(np.concatenate(
            [inp["x_prompt"][i], inp["x_sample"][NS * i:NS * (i + 1)].reshape(NS * LS, D)], axis=0), dtype=np.float32)
        m["cin"] = np.ascontiguousarray(np.concatenate(
            [inp["c_prompt"][i:i + 1], inp["c_sample"][NS * i:NS * (i + 1)]], axis=0), dtype=np.float32)
        m["wada"] = np.ascontiguousarray(inp["w_ada"][0], dtype=np.float32)
        m["wadaf"] = np.ascontiguousarray(inp["w_ada_f"], dtype=np.float32)
        m["win"] = np.ascontiguousarray(inp["w_in"][0], dtype=np.float32)
        m["wglu"] = np.ascontiguousarray(inp["w_glu"][0], dtype=np.float32)
        m["wout"] = np.ascontiguousarray(inp["w_out"][0], dtype=np.float32)
        m["wg"] = np.ascontiguousarray(inp["w_ffn_gate"][0], dtype=np.float32)
        m["wu"] = np.ascontiguousarray(inp["w_ffn_up"][0], dtype=np.float32)
        m["wd"] = np.ascontiguousarray(inp["w_ffn_down"][0], dtype=np.float32)
        m["cst"] = cstv
        m["prm"] = prmv
        m["s5bt"] = BT.reshape(128, -1)
        m["s5ct"] = CT.reshape(128, -1)
        m["stssd"] = np.ascontiguousarray(inp["state_ssd"][0, NS * i:NS * (i + 1)], dtype=np.float32)
        sc = inp["state_conv"][0, NS * i:NS * (i + 1)]
        m["stconv"] = np.ascontiguousarray(
            sc.reshape(NS, 3, 8, 128).transpose(3, 2, 0, 1).reshape(128, -1), dtype=np.float32)
        sr = inp["state_s5_re"][0, NS * i:NS * (i + 1)]
        si = inp["state_s5_im"][0, NS * i:NS * (i + 1)]
        st = np.stack([sr, si], 0).reshape(2, NS, 16, 128).transpose(3, 0, 2, 1)
        m["sts5"] = np.ascontiguousarray(st.reshape(128, -1), dtype=np.float32)
        maps.append(m)
    return maps


_CACHE = {}


def kernel(**inputs):
    inp = {k: np.asarray(v) for k, v in inputs.items()}
    if "nc" not in _CACHE:
        _CACHE["nc"] = build()[0]
    nc = _CACHE["nc"]
    maps = _prep_inputs(inp)
    res = run_bass_kernel_spmd(nc, maps, core_ids=list(range(NCORES)))
    R = res.results
    y_p = np.stack([R[i]["yout"][:SEQ] for i in range(NCORES)], 0)
    y_s = np.concatenate([R[i]["yout"][SEQ:].reshape(NS, LS, D) for i in range(NCORES)], 0)
    ssd_p = np.stack([R[i]["o_ssdp"].reshape(128, 8, 64).transpose(1, 2, 0) for i in range(NCORES)], 0)[None]
    ssd_s = np.concatenate([R[i]["o_ssds"] for i in range(NCORES)], 0)[None]
    conv = [R[i]["o_conv"].reshape(128, 8, 17, 3).transpose(2, 3, 1, 0).reshape(17, 3, 1024) for i in range(NCORES)]
    conv_p = np.stack([c[0] for c in conv], 0)[None]
    conv_s = np.concatenate([c[1:] for c in conv], 0)[None]
    s5 = [R[i]["o_s5"].reshape(128, 2, 16, 17).transpose(1, 3, 2, 0).reshape(2, 17, 32, 64) for i in range(NCORES)]
    re_p = np.stack([s[0, 0] for s in s5], 0)[None]
    re_s = np.concatenate([s[0, 1:] for s in s5], 0)[None]
    im_p = np.stack([s[1, 0] for s in s5], 0)[None]
    im_s = np.concatenate([s[1, 1:] for s in s5], 0)[None]
    f = lambda a: np.ascontiguousarray(a, dtype=np.float32)
    return (f(y_p), f(y_s), f(ssd_p), f(ssd_s), f(conv_p), f(conv_s), f(re_p), f(re_s), f(im_p), f(im_s))
```

```python
import math
import numpy as np
from contextlib import ExitStack
import concourse.bass as bass
import concourse.mybir as mybir
from concourse.bass_utils import run_bass_kernel_spmd

F32 = mybir.dt.float32
BF16 = mybir.dt.bfloat16
I32 = mybir.dt.int32
AF = mybir.ActivationFunctionType
ALU = mybir.AluOpType

NCORES = 8
D = 1024
SEQ = 2048
NS = 16
LS = 4
NTOK = SEQ + NS * LS
DFF = 2816
NJ = DFF // 128
INP = 2056
EPS = 1e-6
T5 = 32
TILES = [(0, 512), (512, 512), (1024, 512), (1536, 512), (2048, 64)]
PI = math.pi


class Buf:
    def __init__(self, name):
        self.name = name
        self.w = None
        self.r = []
        self.dsem = None
        self.dcnt = 0


class TL:
    def __init__(self, t, name):
        self.t = t
        self.name = name
        self.b = Buf(name)
        self.subs = {}

    def sub(self, k):
        if getattr(self, "nosub", False):
            return self.b
        if k not in self.subs:
            self.subs[k] = Buf("%s_%s" % (self.name, k))
        return self.subs[k]

    def allb(self):
        return [self.b] + list(self.subs.values())

    def __getitem__(self, k):
        return self.t[k]


class Sched:
    ENG = ["pe", "act", "dve", "pool", "sp"]

    def __init__(self, nc, es):
        self.nc = nc
        self.es = es
        self.eobj = {"pe": nc.tensor, "act": nc.scalar, "dve": nc.vector, "pool": nc.gpsimd, "sp": nc.sync}
        self.cnt = {e: 0 for e in self.ENG}
        self.sem = {e: es.enter_context(nc.semaphore("s_" + e)) for e in self.ENG}
        self.seen = {e: {} for e in self.ENG}
        self.dbufs = []
        self.ninst = 0
        self.dead = False
        self.pe_pending = None

    def _flush_pe(self):
        if self.pe_pending is not None:
            self.pe_pending.then_inc(self.sem["pe"], 1)
            self.cnt["pe"] += 1
            self.pe_pending = None

    def _deps(self, eng, reads, writes, xreads=()):
        deps = []
        for b in reads:
            if b.w is not None:
                deps.append(b.w)
            if b in xreads:
                deps.extend(r for r in b.r if r[2] != eng)
        for b in writes:
            if b.w is not None:
                deps.append(b.w)
            deps.extend(b.r)
        waits = {}
        for (sem, val, key) in deps:
            if key == "pe" and eng == "pe":
                continue
            if self.seen[eng].get(key, 0) >= val:
                continue
            if key == "pe" and val > self.cnt["pe"]:
                self._flush_pe()
            if key not in waits or waits[key][1] < val:
                waits[key] = (sem, val)
        for key, (sem, val) in waits.items():
            self.seen[eng][key] = val
        return list(waits.values())

    def op(self, eng, fn, reads=(), writes=()):
        if self.dead:
            return None
        xr = [b for b in reads if getattr(b, "excl", False)]
        waits = self._deps(eng, reads, writes, xreads=xr)
        e = self.eobj[eng]
        for (s_, v_) in waits:
            e.wait_ge(s_, v_)
        if eng == "pe":
            self.pe_pending = fn(e)
            tok = (self.sem[eng], self.cnt[eng] + 1, eng)
        else:
            self.cnt[eng] += 1
            tok = (self.sem[eng], self.cnt[eng], eng)
            fn(e).then_inc(self.sem[eng], 1)
        for b in reads:
            b.r.append(tok)
        for b in writes:
            b.w = tok
            b.r = []
        self.ninst += 1
        return tok

    def dma(self, eng, out, in_, reads=(), writes=(), buf=None, **kw):
        if self.dead:
            return None
        waits = self._deps(eng, reads, writes)
        if buf is None:
            buf = writes[0] if writes else reads[0]
        if buf.dsem is None:
            buf.dsem = self.es.enter_context(self.nc.semaphore("d_" + buf.name))
            self.dbufs.append(buf)
        buf.dcnt += 16
        tok = (buf.dsem, buf.dcnt, "d_" + buf.name)
        e = self.eobj[eng]
        for (s_, v_) in waits:
            e.wait_ge(s_, v_)
        e.dma_start(out=out, in_=in_, **kw).then_inc(buf.dsem, 16)
        for b in reads:
            b.r.append(tok)
        for b in writes:
            b.w = tok
            b.r = []
        self.ninst += 1
        return tok

    def barrier(self):
        if self.dead:
            return
        self._flush_pe()
        for e in self.ENG:
            waits = []
            for o in self.ENG:
                if o != e and self.cnt[o] > self.seen[e].get(o, 0):
                    waits.append((self.sem[o], self.cnt[o]))
                    self.seen[e][o] = self.cnt[o]
            for b in self.dbufs:
                key = "d_" + b.name
                if b.dcnt > self.seen[e].get(key, 0):
                    waits.append((b.dsem, b.dcnt))
                    self.seen[e][key] = b.dcnt
            for (s_, v_) in waits:
                self.eobj[e].wait_ge(s_, v_)

    def emit(self):
        pass


C_ID = 0
C_TRI = 128
C_NEG = 256
C_TRI64 = 384
C_NEG64 = 512
C_SEG64 = 640
C_SEGI = 768
CST_W = 784

P_BMOD = 0
P_GAIN = 64
P_CONV = 88
P_SSDFM = 128
P_S5P = 136
P_S5M = 184
P_SSD8 = 192
PRM_W = 194


def _consts():
    c = np.zeros((128, CST_W), np.float32)
    c[:, C_ID:C_ID + 128] = np.eye(128, dtype=np.float32)
    s = np.arange(128)[:, None]
    l = np.arange(128)[None, :]
    c[:, C_TRI:C_TRI + 128] = (s <= l).astype(np.float32)
    c[:, C_NEG:C_NEG + 128] = np.where(l >= s, 0.0, -30000.0)
    same = (s // LS == l // LS) & (s < 64) & (l < 64)
    c[:, C_TRI64:C_TRI64 + 128] = ((s <= l) & same).astype(np.float32)
    c[:, C_NEG64:C_NEG64 + 128] = np.where((l >= s) & same, 0.0, -30000.0)
    c[:, C_SEG64:C_SEG64 + 128] = same.astype(np.float32)
    j = np.arange(16)[None, :]
    c[:, C_SEGI:C_SEGI + 16] = ((s // LS == j) & (s < 64)).astype(np.float32)
    return c


def _fm(v, nt):
    return np.ascontiguousarray(np.asarray(v, np.float32).reshape(nt, 128).T)


def _params(inp):
    p = np.zeros((128, PRM_W), np.float32)
    p[:, P_BMOD:P_BMOD + 48] = _fm(inp["b_ada"][0], 48)
    p[:, P_BMOD + 48:P_BMOD + 64] = _fm(inp["b_ada_f"], 16)
    p[:, P_GAIN:P_GAIN + 8] = _fm(inp["norm1_g"][0], 8)
    p[:, P_GAIN + 8:P_GAIN + 16] = _fm(inp["norm2_g"][0], 8)
    p[:, P_GAIN + 16:P_GAIN + 24] = _fm(inp["normf_g"], 8)
    cw = inp["conv_w"][0]
    cv = np.zeros((128, 8, 5), np.float32)
    for k in range(4):
        cv[:, :, k] = _fm(cw[k], 8)
    cv[:, :, 4] = _fm(inp["conv_b"][0], 8)
    p[:, P_CONV:P_CONV + 40] = cv.reshape(128, 40)
    Dh = inp["ssd_D"][0]
    dfm = np.zeros((128, 4), np.float32)
    for pr in range(4):
        dfm[0:64, pr] = Dh[2 * pr]
        dfm[64:128, pr] = Dh[2 * pr + 1]
    p[:, P_SSDFM:P_SSDFM + 4] = dfm
    p[:, P_SSDFM + 4:P_SSDFM + 8] = _fm(inp["ssd_norm_g"][0], 4)

    def st(a):
        return np.ascontiguousarray(np.asarray(a, np.float32).reshape(16, 128).T)
    p[:, P_S5P:P_S5P + 16] = st(inp["s5_A_re"][0])
    p[:, P_S5P + 16:P_S5P + 32] = st(inp["s5_A_im"][0])
    p[:, P_S5P + 32:P_S5P + 48] = st(np.repeat(inp["s5_log_step"][0][:, None], 64, axis=1))
    p[:, P_S5M:P_S5M + 4] = _fm(inp["s5_D"][0], 4)
    p[:, P_S5M + 4:P_S5M + 8] = _fm(inp["b_glu"][0], 4)
    p[0:8, P_SSD8] = inp["ssd_dt_bias"][0]
    p[0:8, P_SSD8 + 1] = inp["ssd_A_log"][0]
    return p


def _s5mats(inp):
    Br, Bi = inp["s5_B_re"][0], inp["s5_B_im"][0]
    Cr, Ci = inp["s5_C_re"][0], inp["s5_C_im"][0]
    BT = np.zeros((128, 2, 16, 128), np.float32)
    CT = np.zeros((128, 2, 16, 32), np.float32)
    for s in range(16):
        for gl in range(2):
            g = 2 * s + gl
            r0 = (g % 8) * 16
            BT[r0:r0 + 16, 0, s, gl * 64:(gl + 1) * 64] = Br[g].T
            BT[r0:r0 + 16, 1, s, gl * 64:(gl + 1) * 64] = Bi[g].T
            CT[gl * 64:(gl + 1) * 64, 0, s, gl * 16:(gl + 1) * 16] = Cr[g].T
            CT[gl * 64:(gl + 1) * 64, 1, s, gl * 16:(gl + 1) * 16] = Ci[g].T
    return BT, CT


class Arena:
    def __init__(self, nc, es, words):
        self.t = es.enter_context(nc.sbuf_tensor("arena", [128, words], F32))
        self.words = words
        self.lo = 0
        self.hi = words

    def alloc(self, name, shape, dt, top=False):
        n = 1
        for d in shape[1:]:
            n *= d
        w = n if dt == F32 or dt == I32 else (n + 1) // 2
        w = (w + 3) // 4 * 4
        if top:
            self.hi -= w
            off = self.hi
        else:
            off = self.lo
            self.lo += w
        assert self.lo <= self.hi, "arena overflow at %s: lo=%d hi=%d" % (name, self.lo, self.hi)
        ap = self.t[:, off:off + w]
        if dt != F32:
            ap = ap.bitcast(dt)
        ap = ap[:, 0:n]
        if len(shape) == 3:
            ap = ap.rearrange("p (a b) -> p a b", b=shape[2])
        elif len(shape) == 4:
            ap = ap.rearrange("p (a b c) -> p a b c", b=shape[2], c=shape[3])
        if shape[0] < 128:
            ap = ap[0:shape[0]]
        return TL(ap, name)


class StopBuild(Exception):
    pass


def build(dbg=None, stop_after=None):
    nc = bass.Bass("TRN2", target_bir_lowering=False)

    SH = []

    def ckpt(name):
        if stop_after == name:
            SH[0].barrier()
            SH[0].dead = True
    dt_in = lambda name, shape: nc.dram_tensor(name, list(shape), F32, kind="ExternalInput").ap()
    dt_out = lambda name, shape: nc.dram_tensor(name, list(shape), F32, kind="ExternalOutput").ap()
    xin = dt_in("xin", [NTOK, D])
    cin = dt_in("cin", [17, D])
    wada = dt_in("wada", [D, 6144])
    wadaf = dt_in("wadaf", [D, 2048])
    win = dt_in("win", [D, INP])
    wglu = dt_in("wglu", [512, 512])
    wout = dt_in("wout", [D, D])
    wg = dt_in("wg", [D, DFF])
    wu = dt_in("wu", [D, DFF])
    wd = dt_in("wd", [DFF, D])
    cst_d = dt_in("cst", [128, CST_W])
    prm_d = dt_in("prm", [128, PRM_W])
    s5bt_d = dt_in("s5bt", [128, 2 * 16 * 128])
    s5ct_d = dt_in("s5ct", [128, 2 * 16 * 32])
    stssd_d = dt_in("stssd", [NS, 8, 64, 128])
    stconv_d = dt_in("stconv", [128, 8 * NS * 3])
    sts5_d = dt_in("sts5", [128, 2 * 16 * NS])
    yout = dt_out("yout", [NTOK, D])
    o_ssdp = dt_out("o_ssdp", [128, 512])
    o_ssds = dt_out("o_ssds", [NS, 8, 64, 128])
    o_conv = dt_out("o_conv", [128, 8 * 17 * 3])
    o_s5 = dt_out("o_s5", [128, 2 * 16 * 17])
    mixd = nc.dram_tensor("mixd", [128, 8, NTOK], BF16, kind="Internal").ap()
    dumps = {}

    with ExitStack() as es:
        S = Sched(nc, es)
        NEED_CTN = []
        SH.append(S)
        A = Arena(nc, es, 53200)
        outbufs = []

        def dump(name, ap, shape, reads):
            if dbg is None or name not in dbg:
                return
            d = dt_out("dbg_" + name, shape)
            dumps[name] = shape
            b = Buf("dbg_" + name)
            S.dma("sp" if ap.dtype == F32 else "pool", d, ap, reads=reads, buf=b)
            outbufs.append(b)

        PB = [TL(es.enter_context(nc.psum_tensor("pb%d" % i, [128, 512], F32)), "pb%d" % i) for i in range(8)]
        for pb_ in PB:
            pb_.b.excl = True
            pb_.nosub = True

        def pbf(i):
            return PB[i].t[:].bitcast(BF16)

        cst = A.alloc("cst", [128, CST_W], F32)
        prm = A.alloc("prm", [128, PRM_W], F32)
        identb = A.alloc("identb", [128, 128], BF16)
        onesf = A.alloc("onesf", [128, 128], F32)
        mod = A.alloc("mod", [128, 64, 17], F32)
        amod = A.alloc("amod", [128, 24, 17], F32)
        s5fin = A.alloc("s5fin", [128, 2, 16, 17], F32)
        scT = A.alloc("scT", [128, 8, 17], BF16)
        LO_GLOBAL = A.lo
        win_sb = A.alloc("win_sb", [128, 8, INP], BF16)
        wglu_sb = A.alloc("wglu_sb", [128, 4, 512], BF16)
        s5BT = A.alloc("s5BT", [128, 2, 16, 128], BF16)
        s5CT = A.alloc("s5CT", [128, 2, 16, 32], BF16)
        LO_W = A.lo

        def load_1a_weights():
            for a_ in range(4):
                S.dma("pool", s5BT.t[:].rearrange("p a s c -> p (a s c)")[:, a_ * 1024:(a_ + 1) * 1024],
                      s5bt_d[:, a_ * 1024:(a_ + 1) * 1024], writes=[s5BT.b])
            S.dma("pool", s5CT.t[:].rearrange("p a s c -> p (a s c)"), s5ct_d, writes=[s5CT.b])
            win_v = win.rearrange("(kt p) n -> p kt n", p=128)
            for kh in range(4):
                for ch in range(2):
                    S.dma("pool", win_sb.t[:, 2 * kh:2 * kh + 2, ch * 1028:(ch + 1) * 1028],
                          win_v[:, 2 * kh:2 * kh + 2, ch * 1028:(ch + 1) * 1028], writes=[win_sb.sub(kh)])
            S.dma("pool", wglu_sb.t[:], wglu.rearrange("(kt p) n -> p kt n", p=128), writes=[wglu_sb.b])

        ident = cst.t[:, C_ID:C_ID + 128]
        S.dma("sp", cst.t[:], cst_d, writes=[cst.b])
        S.dma("sp", prm.t[:], prm_d, writes=[prm.b])
        S.op("act", lambda e: e.activation(out=identb.t[:], in_=ident, func=AF.Copy), reads=[cst.b], writes=[identb.b])
        S.op("dve", lambda e: e.memset(onesf.t[:], 1.0), writes=[onesf.b])

        def chunkmod(i):
            return mod.t[:, 8 * i:8 * i + 8, :]

        ssd8 = A.alloc("ssd8", [8, 4], F32)
        S.op("act", lambda e: e.activation(out=ssd8.t[:, 1:2], in_=prm.t[0:8, P_SSD8 + 1:P_SSD8 + 2], func=AF.Exp),
             reads=[prm.b], writes=[ssd8.b])
        S.op("dve", lambda e: e.tensor_scalar(out=ssd8.t[:, 1:2], in0=ssd8.t[:, 1:2], scalar1=-1.0, scalar2=None, op0=ALU.mult),
             reads=[ssd8.b], writes=[ssd8.b])
        S.op("dve", lambda e: e.tensor_copy(out=ssd8.t[:, 0:1], in_=prm.t[0:8, P_SSD8:P_SSD8 + 1]), reads=[prm.b], writes=[ssd8.b])

        Ptab = A.alloc("Ptab", [128, 2, 16, T5], F32)
        Qtab = A.alloc("Qtab", [128, 2, 16, T5], F32)
        s5t = [A.alloc("s5t%d" % i, [128, 512], F32) for i in range(2)]

        def alias(name, ap, buf):
            tl = TL(ap, name)
            tl.b = buf
            return tl
        sw = alias("s5work", s5t[1].t[:, 0:384].rearrange("p (a b) -> p a b", b=16), s5t[1].b)
        tmpA = alias("tmpA", s5t[0].t[:, 0:256].rearrange("p (a b) -> p a b", b=T5 // 2), s5t[0].b)
        tmpB = alias("tmpB", s5t[0].t[:, 256:512].rearrange("p (a b) -> p a b", b=T5 // 2), s5t[0].b)
        mask32 = A.alloc("mask32", [128, 16, T5], BF16)
        s5v = [A.alloc("s5v%d" % i, [128, 512], F32) for i in range(2)]
        qtmp = alias("qtmp", s5v[0].t[:].rearrange("p (s t) -> p s t", t=T5), s5v[0].b)
        mask4 = A.alloc("mask4", [128, 128, LS], BF16)
        s5cr = A.alloc("s5cr", [128, 2, 16], F32)
        W = lambda i: sw.t[:, i, :]
        pv = lambda i: prm.t[:, P_S5P + 16 * i:P_S5P + 16 * (i + 1)]
        swb = [sw.b, prm.b]

        def dv(fn):
            S.op("dve", fn, reads=swb, writes=[sw.b])

        def act(fn):
            S.op("act", fn, reads=swb, writes=[sw.b])
        TT = lambda e, o, a, b, op: e.tensor_tensor(out=o, in0=a, in1=b, op=op)
        def exp_acc(dst, src):
            dv(lambda e: e.tensor_scalar(out=W(22), in0=src, scalar1=1.0 / 16, scalar2=None, op0=ALU.mult))
            dv(lambda e: e.tensor_scalar(out=dst, in0=W(22), scalar1=1.0 / 7, scalar2=1.0, op0=ALU.mult, op1=ALU.add))
            for k in (6, 5, 4, 3, 2, 1):
                dv(lambda e: TT(e, dst, dst, W(22), ALU.mult))
                dv(lambda e, k=k: e.tensor_scalar(out=dst, in0=dst, scalar1=1.0 / k, scalar2=1.0, op0=ALU.mult, op1=ALU.add))
            for _ in range(4):
                dv(lambda e: TT(e, dst, dst, dst, ALU.mult))
        exp_acc(W(0), pv(2))
        dv(lambda e: TT(e, W(1), pv(0), W(0), ALU.mult))
        dv(lambda e: TT(e, W(2), pv(1), W(0), ALU.mult))
        exp_acc(W(3), W(1))

        def range_reduce(dst, src, add):
            ki = A_ki
            dv(lambda e: e.tensor_scalar(out=W(20), in0=src, scalar1=float(add), scalar2=1.0 / (2 * PI), op0=ALU.add, op1=ALU.mult))
            S.op("dve", lambda e: e.tensor_copy(out=ki.t[:], in_=W(20)), reads=swb, writes=[ki.b])
            S.op("dve", lambda e: e.tensor_copy(out=W(21), in_=ki.t[:]), reads=[ki.b], writes=[sw.b])
            dv(lambda e: e.tensor_scalar(out=W(20), in0=src, scalar1=float(add), scalar2=None, op0=ALU.add))
            dv(lambda e: e.scalar_tensor_tensor(out=dst, in0=W(21), scalar=-2 * PI, in1=W(20), op0=ALU.mult, op1=ALU.add))
            dv(lambda e: e.tensor_scalar(out=dst, in0=dst, scalar1=PI, scalar2=-PI, op0=ALU.min, op1=ALU.max))
        A_ki = A.alloc("s5ki", [128, 16], I32)
        range_reduce(W(4), W(2), 0.0)
        range_reduce(W(5), W(2), PI / 2)
        act(lambda e: e.activation(out=W(6), in_=W(4), func=AF.Sin))
        act(lambda e: e.activation(out=W(7), in_=W(5), func=AF.Sin))
        dv(lambda e: TT(e, W(8), W(3), W(7), ALU.mult))
        dv(lambda e: TT(e, W(9), W(3), W(6), ALU.mult))
        dv(lambda e: e.tensor_scalar(out=W(10), in0=W(8), scalar1=-1.0, scalar2=None, op0=ALU.add))
        dv(lambda e: TT(e, W(11), pv(0), pv(0), ALU.mult))
        dv(lambda e: TT(e, W(12), pv(1), pv(1), ALU.mult))
        dv(lambda e: TT(e, W(11), W(11), W(12), ALU.add))
        dv(lambda e: e.reciprocal(out=W(11), in_=W(11)))
        dv(lambda e: TT(e, W(12), W(10), pv(0), ALU.mult))
        dv(lambda e: TT(e, W(13), W(9), pv(1), ALU.mult))
        dv(lambda e: TT(e, W(12), W(12), W(13), ALU.add))
        dv(lambda e: TT(e, W(14), W(12), W(11), ALU.mult))
        dv(lambda e: TT(e, W(12), W(9), pv(0), ALU.mult))
        dv(lambda e: TT(e, W(13), W(10), pv(1), ALU.mult))
        dv(lambda e: TT(e, W(12), W(12), W(13), ALU.subtract))
        dv(lambda e: TT(e, W(15), W(12), W(11), ALU.mult))
        dv(lambda e: TT(e, W(12), W(8), W(8), ALU.mult))
        dv(lambda e: TT(e, W(13), W(9), W(9), ALU.mult))
        dv(lambda e: TT(e, W(12), W(12), W(13), ALU.add))
        dv(lambda e: e.reciprocal(out=W(12), in_=W(12)))
        dv(lambda e: TT(e, W(16), W(8), W(12), ALU.mult))
        dv(lambda e: e.scalar_tensor_tensor(out=W(17), in0=W(9), scalar=-1.0, in1=W(12), op0=ALU.mult, op1=ALU.mult))

        def build_pow(tab, br, bi):
            tb = [tab.b, sw.b, tmpA.b, tmpB.b]
            S.op("dve", lambda e: e.tensor_copy(out=tab.t[:, 0, :, 0], in_=br), reads=tb, writes=[tab.b])
            S.op("dve", lambda e: e.tensor_copy(out=tab.t[:, 1, :, 0], in_=bi), reads=tb, writes=[tab.b])
            n = 1
            while n < T5:
                ar, ai = tab.t[:, 0, :, 0:n], tab.t[:, 1, :, 0:n]
                sr = tab.t[:, 0, :, n - 1:n].to_broadcast([128, 16, n])
                si = tab.t[:, 1, :, n - 1:n].to_broadcast([128, 16, n])
                tA, tB = tmpA.t[:, :, 0:n], tmpB.t[:, :, 0:n]
                orr, oi = tab.t[:, 0, :, n:2 * n], tab.t[:, 1, :, n:2 * n]
                ops = [(tA, ar, sr, ALU.mult), (tB, ai, si, ALU.mult), (orr, tA, tB, ALU.subtract),
                       (tA, ar, si, ALU.mult), (tB, ai, sr, ALU.mult), (oi, tA, tB, ALU.add)]
                for (o, a, b, op) in ops:
                    S.op("dve", lambda e, o=o, a=a, b=b, op=op: TT(e, o, a, b, op), reads=tb, writes=tb[0:1] + tb[2:4])
                n *= 2
        build_pow(Ptab, W(8), W(9))
        build_pow(Qtab, W(16), W(17))
        tq = [Qtab.b, sw.b, tmpA.b, tmpB.b]
        for half in range(2):
            hs = slice(half * (T5 // 2), (half + 1) * (T5 // 2))
            qr, qi = Qtab.t[:, 0, :, hs], Qtab.t[:, 1, :, hs]
            fr = W(14).unsqueeze(2).to_broadcast([128, 16, T5 // 2])
            fi = W(15).unsqueeze(2).to_broadcast([128, 16, T5 // 2])
            ops = [(tmpA.t[:], qr, fr, ALU.mult), (tmpB.t[:], qi, fi, ALU.mult), ("R", tmpA.t[:], tmpB.t[:], ALU.subtract),
                   (tmpA.t[:], qr, fi, ALU.mult), (tmpB.t[:], qi, fr, ALU.mult), (qi, tmpA.t[:], tmpB.t[:], ALU.add)]
            for (o, a, b, op) in ops:
                if isinstance(o, str):
                    o = qtmp.t[:, :, hs]
                S.op("dve", lambda e, o=o, a=a, b=b, op=op: TT(e, o, a, b, op), reads=tq + [qtmp.b], writes=tq + [qtmp.b])
            S.op("dve", lambda e, qr=qr, hs=hs: e.tensor_copy(out=qr, in_=qtmp.t[:, :, hs]), reads=[qtmp.b], writes=[Qtab.b])
        S.op("dve", lambda e: e.memset(mask32.t[:], 1.0), reads=[Qtab.b], writes=[mask32.b])
        S.op("dve", lambda e: e.memset(mask32.t[:, :, 0:1], 0.0), writes=[mask32.b])
        S.op("dve", lambda e: e.memset(mask4.t[:], 1.0), writes=[mask4.b])
        S.op("dve", lambda e: e.memset(mask4.t[:, :, 0:1], 0.0), writes=[mask4.b])
        S.op("dve", lambda e: e.memset(s5cr.t[:], 0.0), writes=[s5cr.b])
        dump("Ptab", Ptab.t[:].rearrange("p a s t -> p (a s t)"), [128, 2 * 16 * T5], [Ptab.b])
        dump("Qtab", Qtab.t[:].rearrange("p a s t -> p (a s t)"), [128, 2 * 16 * T5], [Qtab.b])

        LO_W = A.lo
        cs = A.alloc("cs", [17, D], F32)
        slabs = [A.alloc("adaslab%d" % i, [128, 8, 512], BF16) for i in range(3)]
        S.dma("sp", cs.t[:], cin, writes=[cs.b])
        S.op("act", lambda e: e.activation(out=cs.t[:], in_=cs.t[:], func=AF.Silu), reads=[cs.b], writes=[cs.b])
        for kt in range(8):
            S.op("pe", lambda e, kt=kt: e.transpose(PB[2].t[:, kt * 17:(kt + 1) * 17], cs.t[:, kt * 128:(kt + 1) * 128],
                                                    cst.t[0:17, C_ID:C_ID + 17]),
                 reads=[cs.b, cst.b], writes=[PB[2].b])
        S.op("act", lambda e: e.activation(out=scT.t[:].rearrange("p k s -> p (k s)"), in_=PB[2].t[:, 0:136], func=AF.Copy),
             reads=[PB[2].b], writes=[scT.b])
        wada_v = wada.rearrange("(kt p) n -> p kt n", p=128)
        wadaf_v = wadaf.rearrange("(kt p) n -> p kt n", p=128)

        def slab_src(i):
            if i < 12:
                return wada_v[:, :, i * 512:(i + 1) * 512]
            return wadaf_v[:, :, (i - 12) * 512:(i - 11) * 512]

        def load_slab(i):
            sl = slabs[i % 3]
            for kh in range(2):
                S.dma("pool", sl.t[:, 4 * kh:4 * kh + 4, :], slab_src(i)[:, 4 * kh:4 * kh + 4, :], writes=[sl.b])
        load_slab(0)
        load_slab(1)
        load_1a_weights()
        for i in range(4):
            if i + 2 < 4:
                load_slab(i + 2)
            sl = slabs[i % 3]
            pb = PB[i % 2]
            for fc in range(4):
                for kt in range(8):
                    S.op("pe", lambda e, fc=fc, kt=kt, sl=sl, pb=pb: e.matmul(
                        pb.t[:, fc * 17:(fc + 1) * 17], sl.t[:, kt, fc * 128:(fc + 1) * 128], scT.t[:, kt, :],
                        start=(kt == 0), stop=(kt == 7)), reads=[sl.b, scT.b], writes=[pb.b])
            S.op("dve", lambda e, i=i, pb=pb: e.tensor_tensor(
                out=mod.t[:, 4 * i:4 * i + 4, :], in0=pb.t[:, 0:68].rearrange("p (c s) -> p c s", s=17),
                in1=prm.t[:, P_BMOD + 4 * i:P_BMOD + 4 * i + 4].unsqueeze(2).to_broadcast([128, 4, 17]), op=ALU.add),
                reads=[pb.b, prm.b], writes=[mod.b])
        def make_amod(lst):
          for k, (sci, gi) in lst:
            S.op("dve", lambda e, k=k, sci=sci, gi=gi: e.scalar_tensor_tensor(
                out=amod.t[:, 8 * k:8 * k + 8, :], in0=chunkmod(sci), scalar=1.0,
                in1=prm.t[:, P_GAIN + 8 * gi:P_GAIN + 8 * gi + 8].unsqueeze(2).to_broadcast([128, 8, 17]),
                op0=ALU.add, op1=ALU.mult), reads=[mod.b, prm.b], writes=[amod.b])
        make_amod([(0, (1, 0))])
        dump("mod", mod.t[:].rearrange("p c s -> p (c s)"), [128, 64 * 17], [mod.b])
        S.barrier()
        S.emit()
        A.lo = LO_W

        MOD_SH1, MOD_G1, MOD_SH2, MOD_G2, MOD_SHF = 0, 2, 3, 5, 6

        def expand_mod(name, src_ap, srcbufs):
            t = A.alloc(name, [128, 8, 64], F32)
            S.op("dve", lambda e: e.tensor_copy(out=t.t[:].rearrange("p k (s b) -> p k s b", b=LS),
                                                in_=src_ap.unsqueeze(3).to_broadcast([128, 8, NS, LS])),
                 reads=srcbufs, writes=[t.b])
            return t

        LO_P1 = A.lo
        mixt = [A.alloc("mixt%d" % i, [128, 8, 256], BF16) for i in range(2)]
        mixdb = [Buf("mixd%d" % i) for i in range(9)]
        a1x = A.alloc("a1x", [128, 8, 64], F32)
        sh1x = A.alloc("sh1x", [128, 8, 64], F32)

        def fill_x(t, src_ap, srcbufs):
            S.op("dve", lambda e: e.tensor_copy(out=t.t[:].rearrange("p k (s b) -> p k s b", b=LS),
                                                in_=src_ap.unsqueeze(3).to_broadcast([128, 8, NS, LS])),
                 reads=srcbufs, writes=[t.b])
        adab = [TL(a1x.t[:].rearrange("p k t -> p (k t)").bitcast(BF16).rearrange("p (k c) -> p k c", c=128), "adab0"),
                TL(sh1x.t[:].rearrange("p k t -> p (k t)").bitcast(BF16).rearrange("p (k c) -> p k c", c=128), "adab1")]
        adab[0].b = a1x.b
        adab[1].b = sh1x.b
        ADA_CH = list(range(16, 64))

        def ada_load(ci):
            c = ADA_CH[ci]
            src = wada_v[:, :, c * 128:(c + 1) * 128] if c < 48 else wadaf_v[:, :, (c - 48) * 128:(c - 47) * 128]
            S.dma("pool", adab[ci % 2].t[:], src, writes=[adab[ci % 2].b])

        def ada_compute(ci):
            c = ADA_CH[ci]
            sl = adab[ci % 2]
            pb = next_pb()
            for kt in range(8):
                S.op("pe", lambda e, kt=kt: e.matmul(pb.t[:, 0:17], sl.t[:, kt, :], scT.t[:, kt, :], start=(kt == 0), stop=(kt == 7)),
                     reads=[sl.b, scT.b], writes=[pb.b])
            S.op("dve", lambda e: e.tensor_scalar(out=mod.t[:, c, :], in0=pb.t[:, 0:17], scalar1=prm.t[:, P_BMOD + c:P_BMOD + c + 1],
                                                  scalar2=None, op0=ALU.add), reads=[pb.b, prm.b], writes=[mod.b])
        ada_state = [0, 0]

        def ada_step():
            if ada_state[1] >= len(ADA_CH):
                return
            while ada_state[0] < min(len(ADA_CH), ada_state[1] + 2):
                ada_load(ada_state[0])
                ada_state[0] += 1
            ada_compute(ada_state[1])
            ada_state[1] += 1

        ckpt("setup0")
        NTM = 256
        xtm = A.alloc("xtm", [128, 2, D], F32)
        xn = A.alloc("xn", [128, 2, D], BF16)
        nstat = A.alloc("nstat", [128, 4], F32)
        uT = A.alloc("uT", [128, 8, NTM], BF16)
        xpad = A.alloc("xpad", [128, 8, NTM + 4], BF16)
        xtail = A.alloc("xtail", [128, 8, 64], F32)
        cvst = A.alloc("cvst", [128, 8, NS, 3], F32)
        S.dma("sp", cvst.t[:].rearrange("p c s k -> p (c s k)"), stconv_d, writes=[cvst.b])
        dgc = A.alloc("dgc", [128, 8, 4, 128], BF16)
        for ct_ in range(8):
            for k_ in range(4):
                S.op("act", lambda e, ct_=ct_, k_=k_: e.activation(
                    out=dgc.t[:, ct_, k_, :], in_=ident, func=AF.Copy,
                    scale=prm.t[:, P_CONV + 5 * ct_ + k_:P_CONV + 5 * ct_ + k_ + 1]), reads=[cst.b, prm.b], writes=[dgc.b])
        xsT = A.alloc("xsT", [128, 4, NTM], F32)
        BCT = A.alloc("BCT", [128, 4, NTM], BF16)
        szT = A.alloc("szT", [128, 4, NTM], BF16)
        u5Ts = [A.alloc("u5T%d" % i, [128, 4, NTM], BF16) for i in range(2)]
        dtT = A.alloc("dtT", [8, 2, NTM], F32)
        cacc = [A.alloc("cacc0", [128, NTM], F32)] * 2
        y5pre = A.alloc("y5pre", [128, 4, NTM], F32)
        g5 = A.alloc("g5", [128, 4, NTM], BF16)
        sgl = A.alloc("sgl", [128, NTM], F32)
        dtm_l = [A.alloc("dtm%d" % i, [128, 16], F32) for i in range(2)]
        acs_l = [A.alloc("acs%d" % i, [128, 8], F32) for i in range(2)]
        dec_l = [A.alloc("dec%d" % i, [128, 8], F32) for i in range(2)]
        dtdec_l = [A.alloc("dtdec%d" % i, [128, 8], F32) for i in range(2)]
        Xtm = A.alloc("Xtm", [128, 8, 64], BF16)
        Xdec = A.alloc("Xdec", [128, 8, 64], BF16)
        Btm = A.alloc("Btm", [128, 2, 128], BF16)
        big1 = A.alloc("big1", [128, 8, 128], F32)
        big2 = A.alloc("big2", [128, 8, 128], F32)
        MT = A.alloc("MT", [128, 8, 128], BF16)
        eA = A.alloc("eA", [128, 8, 128], F32)
        CdT = A.alloc("CdT", [128, 8, 128], BF16)
        ST = A.alloc("ST", [128, 8, 64], F32)
        STb = A.alloc("STb", [128, 8, 64], BF16)
        sts5 = alias("sts5", ST.t[:].rearrange("p h q -> p (h q)").rearrange("p (a s q) -> p a s q", a=2, s=16), ST.b)
        yg = A.alloc("yg", [128, 4, 128], F32)
        ysq = alias("ysq", big1.t[:, 4:8, :], big1.b)
        rsb = A.alloc("rsb", [128, 2, 128], F32)
        ysqb = A.alloc("ysqb", [128, 4, 128], BF16)
        onesb1 = A.alloc("onesb1", [128, 128], BF16)
        S.op("dve", lambda e: e.memset(onesb1.t[:], 1.0), writes=[onesb1.b])
        h0n = [alias("h0n0", xtm.t[:, 1, 0:512].rearrange("p (a n) -> p a n", n=128), xtm.sub(1)),
               alias("h0n1", xtm.t[:, 0, 0:512].rearrange("p (a n) -> p a n", n=128), xtm.sub(0))]
        h0T = [A.alloc("h0T%d" % i, [128, 8, 64], BF16) for i in range(2)]
        Bj = [A.alloc("Bj%d" % i, [128, 2, 128], BF16) for i in range(2)]
        hn = [alias("hn0", xtm.t[:, 1, 512:1024].rearrange("p (a n) -> p a n", n=128), xtm.sub(1)),
              alias("hn1", xtm.t[:, 0, 512:1024].rearrange("p (a n) -> p a n", n=128), xtm.sub(0))]
        decfm = A.alloc("decfm", [128, 4, 16], F32)
        dAx = alias("dAx", big1.t[:, 0:4, :].rearrange("p a (b c) -> p (a b) c", c=64), big1.b)
        s5g = [[A.alloc("s5g%d%d" % (j, i), [128, 512], F32) for i in range(2)] for j in range(2)]
        s5t34 = [A.alloc("s5t%d" % i, [128, 512], F32) for i in (2, 3)]
        s5vb = [A.alloc("s5vb%d" % i, [128, 512], F32) for i in range(2)]
        s5k = [0]
        s5h = [[A.alloc("s5h%d%d" % (j, i), [128, 512], BF16) for i in range(4)] for j in range(2)]
        s5CTn = A.alloc("s5CTn", [128, 16, 32], BF16)
        s5c = A.alloc("s5c", [128, 4, 16], F32)
        busd = [[A.alloc("bus%d%d" % (j, i), [128, 512], F32) for i in range(2)] for j in range(2)]
        dg5 = A.alloc("dg5", [128, 4, 128], BF16)
        for q_ in range(4):
            S.op("act", lambda e, q_=q_: e.activation(out=dg5.t[:, q_, :], in_=ident, func=AF.Copy,
                                                      scale=prm.t[:, P_S5M + q_:P_S5M + q_ + 1]),
                 reads=[cst.b, prm.b], writes=[dg5.b])
        S.op("dve", lambda e: e.tensor_scalar(out=s5CT.t[:, 1], in0=s5CT.t[:, 1], scalar1=-1.0, scalar2=None, op0=ALU.mult),
             reads=[s5CT.b], writes=[s5CT.b])
        S.op("dve", lambda e: e.tensor_scalar(out=s5CTn.t[:], in0=s5CT.t[:, 0], scalar1=-1.0, scalar2=None, op0=ALU.mult),
             reads=[s5CT.b], writes=[s5CTn.b])
        print("arena after p1a allocs: lo=%d hi=%d (words)" % (A.lo, A.hi))

        S.op("dve", lambda e: e.memset(xpad.t[:, :, 0:3], 0.0), writes=[xpad.b])
        S.op("dve", lambda e: e.memset(ST.t[:], 0.0), writes=[ST.b])
        S.op("dve", lambda e: e.memset(STb.t[:], 0.0), writes=[STb.b])

        import os as _os3
        ENG_OUTROT = _os3.environ.get("K_OUTROT", "dve")
        ENG_ADDS = _os3.environ.get("K_ADDS", "dve")
        TILES_A = [(i * 256, 256, False) for i in range(8)] + [(SEQ, 64, True)]

        def load_x(ti):
            t0, NT, is_s = TILES_A[ti]
            for blk in range((NT + 127) // 128):
                rows = min(128, NT - blk * 128)
                S.dma("sp", xtm.t[0:rows, blk, :], xin[t0 + blk * 128:t0 + blk * 128 + rows, :], writes=[xtm.sub(blk)])

        a1 = lambda kt: amod.t[:, kt, 0:1]
        sh1 = lambda kt: mod.t[:, 8 * MOD_SH1 + kt, 0:1]
        cw = lambda ct, k: prm.t[:, P_CONV + 5 * ct + k:P_CONV + 5 * ct + k + 1]
        IN_CHUNKS = [("dt", 0, 1536, 8)] + [("z", i, i * 128, 128) for i in range(4)] + \
                    [("xbc", i, 512 + i * 128, 128) for i in range(8)] + [("u5", i, 1544 + i * 128, 128) for i in range(4)]

        load_x(0)
        pbi = [0]

        def next_pb():
            pbi[0] ^= 1
            return PB[pbi[0]]

        ckpt("pre")
        def chain1(ti):
            t0, NT, is_s = TILES_A[ti]
            u5T = u5Ts[ti % 2]
            nblk = (NT + 127) // 128
            T = 128 if not is_s else 64
            tri = cst.t[0:T, C_TRI:C_TRI + T] if not is_s else cst.t[0:T, C_TRI64:C_TRI64 + T]
            neg = cst.t[0:T, C_NEG:C_NEG + T] if not is_s else cst.t[0:T, C_NEG64:C_NEG64 + T]
            sego = onesf.t[0:T, 0:T] if not is_s else cst.t[0:T, C_SEG64:C_SEG64 + T]
            segi = cst.t[0:64, C_SEGI:C_SEGI + 16]

            def dt_prep(ck):
                c0 = ck * T
                cs_ = slice(c0, c0 + T)
                dtm, acs, dec, dtdec = dtm_l[ck], acs_l[ck], dec_l[ck], dtdec_l[ck]
                pc = 0 if ck == 0 else 480
                S.op("pe", lambda e: e.transpose(PB[4].t[0:T, pc:pc + 8], dtT.t[:, 0, cs_], cst.t[0:8, C_ID:C_ID + 8]),
                     reads=[dtT.b, cst.b], writes=[PB[4].sub("sm")])
                S.op("pe", lambda e: e.transpose(PB[4].t[0:T, pc + 8:pc + 16], dtT.t[:, 1, cs_], cst.t[0:8, C_ID:C_ID + 8]),
                     reads=[dtT.b, cst.b], writes=[PB[4].sub("sm")])
                S.op("act", lambda e: e.activation(out=dtm.t[0:T, :], in_=PB[4].t[0:T, pc:pc + 16], func=AF.Copy),
                     reads=[PB[4].sub("sm")], writes=[dtm.b])
                S.op("pe", lambda e: e.matmul(PB[4].t[0:T, pc + 16:pc + 24], tri, dtm.t[0:T, 8:16], start=True, stop=True),
                     reads=[dtm.b, cst.b], writes=[PB[4].sub("sm")])
                S.op("pe", lambda e: e.matmul(PB[4].t[0:T, pc + 24:pc + 32], sego, dtm.t[0:T, 8:16], start=True, stop=True),
                     reads=[dtm.b, cst.b, onesf.b], writes=[PB[4].sub("sm")])
                S.op("act", lambda e: e.activation(out=acs.t[0:T, :], in_=PB[4].t[0:T, pc + 16:pc + 24], func=AF.Copy),
                     reads=[PB[4].sub("sm")], writes=[acs.b])
                S.op("dve", lambda e: TT(e, dec.t[0:T, :], PB[4].t[0:T, pc + 24:pc + 32], acs.t[0:T, :], ALU.subtract),
                     reads=[PB[4].sub("sm"), acs.b], writes=[dec.b])
                S.op("act", lambda e: e.activation(out=dec.t[0:T, :], in_=dec.t[0:T, :], func=AF.Exp), reads=[dec.b], writes=[dec.b])
                S.op("dve", lambda e: TT(e, dtdec.t[0:T, :], dtm.t[0:T, 0:8], dec.t[0:T, :], ALU.mult),
                     reads=[dtm.b, dec.b], writes=[dtdec.b])
            for blk in range(nblk):
                rows = min(128, NT - blk * 128)
                xb = xtm.sub(blk)
                S.op("act", lambda e, blk=blk, rows=rows: e.activation(
                    out=xn.t[0:rows, blk, :], in_=xtm.t[0:rows, blk, :], func=AF.Square, accum_out=nstat.t[0:rows, blk:blk + 1]),
                    reads=[xb], writes=[xn.sub(blk), nstat.sub(blk)])
                S.op("act", lambda e, blk=blk, rows=rows: e.activation(
                    out=nstat.t[0:rows, 2 + blk:3 + blk], in_=nstat.t[0:rows, blk:blk + 1], func=AF.Ln, scale=1.0 / D, bias=EPS),
                    reads=[nstat.sub(blk)], writes=[nstat.sub(blk)])
                S.op("act", lambda e, blk=blk, rows=rows: e.activation(out=nstat.t[0:rows, 2 + blk:3 + blk],
                                                                        in_=nstat.t[0:rows, 2 + blk:3 + blk], func=AF.Exp, scale=-0.5),
                     reads=[nstat.sub(blk)], writes=[nstat.sub(blk)])
                S.op("act", lambda e, blk=blk, rows=rows: e.activation(
                    out=xn.t[0:rows, blk, :], in_=xtm.t[0:rows, blk, :], func=AF.Copy, scale=nstat.t[0:rows, 2 + blk:3 + blk]),
                    reads=[xb, nstat.sub(blk)], writes=[xn.sub(blk)])
            ckpt("Aa%d" % ti)
            if ti + 1 < len(TILES_A):
                load_x(ti + 1)
            ckpt("Ab%d" % ti)
            for kt in range(8):
                xb_ = 2 + (kt % 2)
                pslot = PB[xb_].b
                for blk in range(nblk):
                    rows = min(128, NT - blk * 128)
                    S.op("pe", lambda e, kt=kt, blk=blk, rows=rows: e.transpose(
                        pbf(xb_)[:, blk * 128:blk * 128 + rows],
                        xn.t[0:rows, blk, kt * 128:(kt + 1) * 128], identb.t[0:rows, 0:rows]),
                        reads=[xn.sub(blk), identb.b], writes=[pslot])
                src = pbf(xb_)[:, 0:NT]
                if not is_s:
                    S.op("act", lambda e, kt=kt, src=src: e.activation(out=uT.t[:, kt, 0:NT], in_=src, func=AF.Identity,
                                                                       scale=a1(kt), bias=sh1(kt)),
                         reads=[pslot, amod.b, mod.b], writes=[uT.sub(kt)])
                else:
                    S.op("dve", lambda e, kt=kt, src=src: TT(e, cacc[0].t[:, 0:NT], src, a1x.t[:, kt, :], ALU.mult),
                         reads=[pslot, a1x.b], writes=[cacc[0].b])
                    S.op("dve", lambda e, kt=kt: TT(e, uT.t[:, kt, 0:NT], cacc[0].t[:, 0:NT], sh1x.t[:, kt, :], ALU.add),
                         reads=[cacc[0].b, sh1x.b], writes=[uT.sub(kt)])
            ckpt("A%d" % ti)
            if ti == 0:
                dump("uT", uT.t[:].rearrange("p k t -> p (k t)"), [128, 8 * NTM], uT.allb())

            yield
            if is_s:
                xps = xpad.t[:, :, 0:NS * 7].rearrange("p c (s k) -> p c s k", k=7)
                S.op("act", lambda e: e.activation(out=xps[:, :, :, 0:3], in_=cvst.t[:], func=AF.Copy), reads=[cvst.b], writes=[xpad.b])
            for (kind, i, c0, M) in IN_CHUNKS:
                yield
                pb = next_pb()
                for kt in range(8):
                    S.op("pe", lambda e, kt=kt, c0=c0, M=M, pb=pb: e.matmul(
                        pb.t[0:M, 0:NT], win_sb.t[:, kt, c0:c0 + M], uT.t[:, kt, 0:NT], start=(kt == 0), stop=(kt == 7)),
                        reads=[win_sb.sub(kt // 2), uT.sub(kt)], writes=[pb.b])
                if kind == "z":
                    S.op("act", lambda e, i=i, pb=pb: e.activation(out=szT.t[:, i, 0:NT], in_=pb.t[:, 0:NT], func=AF.Silu),
                         reads=[pb.b], writes=[szT.b])
                elif kind == "xbc":
                    if not is_s:
                        S.op("act", lambda e, i=i, pb=pb: e.activation(out=xpad.t[:, i, 3:3 + NT], in_=pb.t[:, 0:NT], func=AF.Copy),
                             reads=[pb.b], writes=[xpad.b])
                        if ti == 7:
                            S.op("act", lambda e, i=i, pb=pb: e.activation(out=xtail.t[:, i, 0:3], in_=pb.t[:, NT - 3:NT], func=AF.Copy),
                                 reads=[pb.b], writes=[xtail.b])
                    else:
                        S.op("act", lambda e, i=i, pb=pb: e.activation(
                            out=xps[:, i, :, 3:7], in_=pb.t[:, 0:NT].rearrange("p (s k) -> p s k", k=LS), func=AF.Copy),
                            reads=[pb.b], writes=[xpad.b])
                        S.op("act", lambda e, i=i, pb=pb: e.activation(out=xtail.t[:, i, 0:NT], in_=pb.t[:, 0:NT], func=AF.Copy),
                             reads=[pb.b], writes=[xtail.b])
                elif kind == "dt":
                    S.op("act", lambda e, pb=pb: e.activation(out=dtT.t[:, 1, 0:NT], in_=pb.t[0:8, 0:NT], func=AF.Exp,
                                                              bias=ssd8.t[:, 0:1]), reads=[pb.b, ssd8.b], writes=[dtT.b])
                    S.op("act", lambda e: e.activation(out=dtT.t[:, 0, 0:NT], in_=dtT.t[:, 1, 0:NT], func=AF.Ln, bias=1.0),
                         reads=[dtT.b], writes=[dtT.b])
                    S.op("dve", lambda e: e.tensor_scalar(out=dtT.t[:, 1, 0:NT], in0=dtT.t[:, 0, 0:NT], scalar1=ssd8.t[:, 1:2],
                                                          scalar2=None, op0=ALU.mult), reads=[dtT.b, ssd8.b], writes=[dtT.b])
                    for ck_ in range(NT // T):
                        yield
                        dt_prep(ck_)
                else:
                    S.op("act", lambda e, i=i, pb=pb: e.activation(out=u5T.t[:, i, 0:NT], in_=pb.t[:, 0:NT], func=AF.Copy),
                         reads=[pb.b], writes=[u5T.b])

            ckpt("B%d" % ti)
            for ct in range(8):
                yield
                pb = next_pb()
                if not is_s:
                    xin_k = lambda k, ct=ct: xpad.t[:, ct, k:k + NT]
                    pbv = pb.t[:, 0:NT]
                    dst = xsT.t[:, ct, 0:NT] if ct < 4 else BCT.t[:, ct - 4, 0:NT]
                else:
                    xin_k = lambda k, ct=ct: xps[:, ct, :, k:k + LS]
                    pbv = pb.t[:, 0:NT].rearrange("p (s k) -> p s k", k=LS)
                    dst = (xsT.t[:, ct, 0:NT] if ct < 4 else BCT.t[:, ct - 4, 0:NT]).rearrange("p (s k) -> p s k", k=LS)
                for k in range(4):
                    S.op("pe", lambda e, k=k: e.matmul(pbv, dgc.t[:, ct, k, :], xin_k(k), start=(k == 0), stop=(k == 3)),
                         reads=[dgc.b, xpad.b], writes=[pb.b])
                S.op("act", lambda e: e.activation(out=dst, in_=pbv, func=AF.Silu, bias=cw(ct, 4)),
                     reads=[pb.b, prm.b], writes=[xsT.b if ct < 4 else BCT.b])
            ocv = o_conv.rearrange("p (c s k) -> p c s k", s=17, k=3)
            if is_s:
                S.op("act", lambda e: e.activation(out=cvst.t[:], in_=xtail.t[:].rearrange("p c (s k) -> p c s k", k=LS)[:, :, :, 1:4],
                                                   func=AF.Copy), reads=[xtail.b], writes=[cvst.b])
                S.dma("sp", ocv[:, :, 1:17, :], cvst.t[:], reads=[cvst.b], buf=cvst.b)
                outbufs.append(cvst.b)
            elif ti == 7:
                S.dma("sp", ocv[:, :, 0, :], xtail.t[:, :, 0:3], reads=[xtail.b], buf=xtail.b)
            if not is_s:
                S.op("dve", lambda e: e.tensor_copy(out=xpad.t[:, :, 0:3], in_=xpad.t[:, :, NT:NT + 3]),
                     reads=[xpad.b], writes=[xpad.b])
            if is_s:
                dump("xsS", xsT.t[:, :, 0:64], [128, 4, 64], [xsT.b])
                dump("ygS", yg.t[:, :, 0:64], [128, 4, 64], [yg.b])
            if ti == 0:
                dump("xsT", xsT.t[:].rearrange("p k t -> p (k t)"), [128, 4 * NTM], [xsT.b])
                dump("dtT", dtT.t[:].rearrange("p k t -> p (k t)"), [8, 2 * NTM], [dtT.b])

            ckpt("C%d" % ti)
            for ck in range(NT // T):
                c0 = ck * T
                cs_ = slice(c0, c0 + T)
                dtm, acs, dec, dtdec = dtm_l[ck], acs_l[ck], dec_l[ck], dtdec_l[ck]
                yield
                for pr in range(4):
                    S.op("pe", lambda e, pr=pr, cs_=cs_: e.transpose(PB[3].t[0:T, pr * 128:(pr + 1) * 128], xsT.t[:, pr, cs_], ident),
                         reads=[xsT.b, cst.b], writes=[PB[3].b])
                pxs = PB[3].t[0:T, :].rearrange("p (h q) -> p h q", q=64)
                for h in range(8):
                    S.op("act", lambda e, h=h: e.activation(out=Xtm.t[0:T, h, :], in_=pxs[:, h, :], func=AF.Copy, scale=dtm.t[0:T, h:h + 1]),
                         reads=[PB[3].b, dtm.b], writes=[Xtm.b])
                    S.op("act", lambda e, h=h: e.activation(out=Xdec.t[0:T, h, :], in_=pxs[:, h, :], func=AF.Copy, scale=dtdec.t[0:T, h:h + 1]),
                         reads=[PB[3].b, dtdec.b], writes=[Xdec.b])
                for g in range(2):
                    S.op("pe", lambda e, g=g, cs_=cs_: e.transpose(pbf(2)[0:T, g * 128:(g + 1) * 128], BCT.t[:, g, cs_], identb.t[:]),
                         reads=[BCT.b, identb.b], writes=[PB[2].sub(0)])
                S.op("act", lambda e: e.activation(out=Btm.t[0:T].rearrange("p g n -> p (g n)"), in_=pbf(2)[0:T, 0:256], func=AF.Copy),
                     reads=[PB[2].sub(0)], writes=[Btm.b])
                yield
                S.op("dve", lambda e: TT(e, big1.t[0:T, :, 0:T], tri.unsqueeze(1).to_broadcast([T, 8, T]),
                                         dtm.t[0:T, 8:16].unsqueeze(2).to_broadcast([T, 8, T]), ALU.mult),
                     reads=[cst.b, dtm.b], writes=[big1.b])
                for half in range(2):
                    S.op("pe", lambda e, half=half: e.matmul(
                        PB[3].t[:, 0:4 * T].rearrange("p (h l) -> p h l", l=T), onesf.t[0:T, :],
                        big1.t[0:T, 4 * half:4 * half + 4, 0:T], start=True, stop=True),
                        reads=[big1.b, onesf.b], writes=[PB[3].b])
                    yield
                    for h in range(4 * half, 4 * half + 4):
                        S.op("dve", lambda e, h=h: e.scalar_tensor_tensor(
                            out=big2.t[0:T, h, 0:T], in0=PB[3].t[0:T, (h % 4) * T:(h % 4 + 1) * T], scalar=acs.t[0:T, h:h + 1],
                            in1=neg, op0=ALU.subtract, op1=ALU.min), reads=[PB[3].b, acs.b, cst.b], writes=[big2.b])
                    S.op("act", lambda e, half=half: e.activation(
                        out=eA.t[:, 4 * half:4 * half + 4, 0:T], in_=PB[3].t[:, 0:4 * T].rearrange("p (h l) -> p h l", l=T),
                        func=AF.Exp), reads=[PB[3].b], writes=[eA.b])
                    yield
                S.op("act", lambda e: e.activation(out=big2.t[0:T, :, 0:T], in_=big2.t[0:T, :, 0:T], func=AF.Exp),
                     reads=[big2.b], writes=[big2.b])
                yield
                for g in range(2):
                    S.op("pe", lambda e, g=g, cs_=cs_: e.matmul(PB[4].t[0:T, 32 + g * 128:32 + g * 128 + T], BCT.t[:, g, cs_],
                                                                 BCT.t[:, 2 + g, cs_], start=True, stop=True),
                         reads=[BCT.b], writes=[PB[4].sub("cb")])
                cbv = PB[4].t[0:T, 32:288].rearrange("p (g l) -> p g l", l=128)[:, :, 0:T]
                S.op("dve", lambda e: TT(e, MT.t[0:T, :, 0:T].rearrange("p (g h) l -> p g h l", h=4),
                                         cbv.unsqueeze(2).to_broadcast([T, 2, 4, T]),
                                         big2.t[0:T, :, 0:T].rearrange("p (g h) l -> p g h l", h=4), ALU.mult),
                     reads=[PB[4].sub("cb"), big2.b], writes=[MT.b])
                yield
                S.op("pool", lambda e, cs_=cs_: TT(e, CdT.t[:, :, 0:T].rearrange("p (g h) l -> p g h l", h=4),
                                                   BCT.t[:, 2:4, cs_].unsqueeze(2).to_broadcast([128, 2, 4, T]),
                                                   eA.t[:, :, 0:T].rearrange("p (g h) l -> p g h l", h=4), ALU.mult),
                     reads=[BCT.b, eA.b], writes=[CdT.b])
                yield
                ypb = PB[7]
                if is_s:
                    S.op("dve", lambda e: e.tensor_copy(out=dAx.t[0:T], in_=dtm.t[0:T, 8:16].unsqueeze(2).to_broadcast([T, 8, 64])),
                         reads=[dtm.b], writes=[dAx.b])
                    for pr in range(4):
                        S.op("pe", lambda e, pr=pr: e.matmul(PB[4].t[:, 288 + pr * 16:288 + (pr + 1) * 16],
                                                             dAx.t[0:T, 2 * pr:2 * pr + 2, :], segi, start=True, stop=True),
                             reads=[dAx.b, cst.b], writes=[PB[4].sub("dec")])
                    S.op("act", lambda e: e.activation(out=decfm.t[:].rearrange("p a s -> p (a s)"), in_=PB[4].t[:, 288:352], func=AF.Exp),
                         reads=[PB[4].sub("dec")], writes=[decfm.b])
                    stv = stssd_d.rearrange("j (pr hl) p n -> j (hl p) pr n", hl=2)
                    osv = o_ssds.rearrange("j (pr hl) p n -> j (hl p) pr n", hl=2)
                    S.dma("act", h0n[0].t[:], stv[0], writes=[h0n[0].b])
                    for j in range(NS):
                        yield
                        jj = j % 2
                        if j + 1 < NS:
                            S.dma("act", h0n[1 - jj].t[:], stv[j + 1], writes=[h0n[1 - jj].b])
                        pbt = PB[jj]
                        for pr in range(4):
                            S.op("pe", lambda e, pr=pr, jj=jj, pbt=pbt: e.transpose(pbt.t[:, pr * 128:(pr + 1) * 128], h0n[jj].t[:, pr, :], ident),
                                 reads=[h0n[jj].b, cst.b], writes=[pbt.b])
                        S.op("act", lambda e, jj=jj, pbt=pbt: e.activation(out=h0T[jj].t[:].rearrange("p h q -> p (h q)"), in_=pbt.t[:, :], func=AF.Copy),
                             reads=[pbt.b], writes=[h0T[jj].b])
                        for h in range(8):
                            pr, hl = h // 2, h % 2
                            S.op("pe", lambda e, h=h, pr=pr, hl=hl, jj=jj, j=j: e.matmul(
                                ypb.t[64 * hl:64 * hl + 64, pr * T + LS * j:pr * T + LS * j + LS], h0T[jj].t[:, h, :],
                                CdT.t[:, h, LS * j:LS * j + LS], start=(j == 0 and pr == 0), stop=False, skip_group_check=True),
                                reads=[h0T[jj].b, CdT.b], writes=[ypb.b])
                        S.op("dve", lambda e, jj=jj, j=j: e.tensor_scalar(out=Bj[jj].t[0:T], in0=Btm.t[0:T], scalar1=segi[:, j:j + 1],
                                                                          scalar2=None, op0=ALU.mult),
                             reads=[Btm.b, cst.b], writes=[Bj[jj].b])
                        pby = PB[3]
                        for pr in range(4):
                            S.op("pe", lambda e, pr=pr, jj=jj, pby=pby: e.matmul(
                                pby.t[:, pr * 128:(pr + 1) * 128], Xdec.t[0:T, 2 * pr:2 * pr + 2, :], Bj[jj].t[0:T, pr // 2, :],
                                start=True, stop=True), reads=[Xdec.b, Bj[jj].b], writes=[pby.b])
                        S.op("dve", lambda e, jj=jj, j=j: TT(e, hn[jj].t[:], h0n[jj].t[:],
                                                             decfm.t[:, :, j:j + 1].to_broadcast([128, 4, 128]), ALU.mult),
                             reads=[h0n[jj].b, decfm.b], writes=[hn[jj].b])
                        S.op("dve", lambda e, jj=jj, pby=pby: TT(e, hn[jj].t[:], hn[jj].t[:],
                                                                 pby.t[:, :].rearrange("p (a n) -> p a n", n=128), ALU.add),
                             reads=[hn[jj].b, pby.b], writes=[hn[jj].b])
                        S.dma("sp", osv[j], hn[jj].t[:], reads=[hn[jj].b], buf=hn[jj].b)
                    outbufs.extend([hn[0].b, hn[1].b])
                for h in range(8):
                    pr, hl = h // 2, h % 2
                    out = ypb.t[64 * hl:64 * hl + 64, pr * T:(pr + 1) * T]
                    S.op("pe", lambda e, h=h, out=out, pr=pr: e.matmul(out, Xtm.t[0:T, h, :], MT.t[0:T, h, 0:T],
                                                                       start=(pr == 0 and not is_s), stop=is_s, skip_group_check=True),
                         reads=[Xtm.b, MT.b], writes=[ypb.b])
                    if not is_s:
                        S.op("pe", lambda e, h=h, out=out: e.matmul(out, STb.t[:, h, :], CdT.t[:, h, 0:T], start=False, stop=True,
                                                                    skip_group_check=True),
                             reads=[STb.b, CdT.b], writes=[ypb.b])
                yield
                for pr in range(4):
                    S.op("dve", lambda e, pr=pr, cs_=cs_: e.scalar_tensor_tensor(
                        out=yg.t[:, pr, 0:T], in0=xsT.t[:, pr, cs_], scalar=prm.t[:, P_SSDFM + pr:P_SSDFM + pr + 1],
                        in1=ypb.t[:, pr * T:(pr + 1) * T], op0=ALU.mult, op1=ALU.add),
                        reads=[xsT.b, prm.b, ypb.b], writes=[yg.b])
                S.op("dve", lambda e, cs_=cs_: TT(e, yg.t[:, :, 0:T], yg.t[:, :, 0:T], szT.t[:, :, cs_], ALU.mult),
                     reads=[yg.b, szT.b], writes=[yg.b])
                S.op("dve", lambda e: TT(e, ysqb.t[:, :, 0:T], yg.t[:, :, 0:T], yg.t[:, :, 0:T], ALU.mult),
                     reads=[yg.b], writes=[ysqb.b])
                for g in range(2):
                    for k in range(2):
                        S.op("pe", lambda e, g=g, k=k: e.matmul(PB[3].t[:, g * T:(g + 1) * T], onesb1.t[:], ysqb.t[:, 2 * g + k, 0:T],
                                                                start=(k == 0), stop=(k == 1)),
                             reads=[onesb1.b, ysqb.b], writes=[PB[3].b])
                S.op("act", lambda e: e.activation(out=rsb.t[:, :, 0:T], in_=PB[3].t[:, 0:2 * T].rearrange("p (g l) -> p g l", l=T),
                                                   func=AF.Ln, scale=1.0 / 256, bias=EPS), reads=[PB[3].b], writes=[rsb.b])
                S.op("act", lambda e: e.activation(out=rsb.t[:, :, 0:T], in_=rsb.t[:, :, 0:T], func=AF.Exp, scale=-0.5),
                     reads=[rsb.b], writes=[rsb.b])
                yield
                if not is_s:
                    for g in range(2):
                        S.op("pe", lambda e, g=g: e.matmul(PB[6].t[:, g * 256:(g + 1) * 256], Btm.t[0:T, g, :],
                                                           Xdec.t[0:T, 4 * g:4 * g + 4, :], start=True, stop=True),
                             reads=[Btm.b, Xdec.b], writes=[PB[6].b])
                    S.op("dve", lambda e: TT(e, ST.t[:], ST.t[:], eA.t[:, :, T - 1:T].to_broadcast([128, 8, 64]), ALU.mult),
                         reads=[ST.b, eA.b], writes=[ST.b])
                    S.op("dve", lambda e: TT(e, ST.t[:], ST.t[:], PB[6].t[:, :].rearrange("p (h q) -> p h q", q=64), ALU.add),
                         reads=[ST.b, PB[6].b], writes=[ST.b])
                    S.op("act", lambda e: e.activation(out=STb.t[:], in_=ST.t[:], func=AF.Copy), reads=[ST.b], writes=[STb.b])
                for pr in range(4):
                    S.op("dve", lambda e, pr=pr: e.scalar_tensor_tensor(
                        out=mixt[ti % 2].t[:, pr, c0:c0 + T], in0=yg.t[:, pr, 0:T],
                        scalar=prm.t[:, P_SSDFM + 4 + pr:P_SSDFM + 5 + pr], in1=rsb.t[:, pr // 2, 0:T], op0=ALU.mult, op1=ALU.mult),
                        reads=[yg.b, prm.b, rsb.b], writes=[mixt[ti % 2].sub("ssd")])
            if ti == 7:
                S.dma("sp", o_ssdp, ST.t[:].rearrange("p h q -> p (h q)"), reads=[ST.b], buf=ST.b)
                outbufs.append(ST.b)

            ckpt("D%d" % ti)
            yield

        def chain2(ti):
            t0, NT, is_s = TILES_A[ti]
            u5T = u5Ts[ti % 2]
            if is_s:
                S.dma("sp", sts5.t[:].rearrange("p a s q -> p (a s q)"), sts5_d, writes=[sts5.b])
            if not is_s:
                groups = [(list(range(16)), k * T5, T5) for k in range(NT // T5)]
            else:
                groups = [(list(range(8)), 0, 64), (list(range(8, 16)), 0, 64)]
            def emit_bu(g_):
                slist_, tk0_, ntok_ = groups[g_]
                bus = busd[g_ % 2]
                for part, pb in ((0, PB[5]), (1, PB[6])):
                    for idx, s in enumerate(slist_):
                        S.op("pe", lambda e, part=part, pb=pb, idx=idx, s=s: e.matmul(
                            pb.t[:, idx * ntok_:(idx + 1) * ntok_], s5BT.t[:, part, s, :], u5T.t[:, s // 4, tk0_:tk0_ + ntok_],
                            start=True, stop=True), reads=[s5BT.b, u5T.b], writes=[pb.b])
                S.op("act", lambda e: e.activation(out=bus[0].t[:], in_=PB[5].t[:, :], func=AF.Copy), reads=[PB[5].b], writes=[bus[0].b])
                S.op("act", lambda e: e.activation(out=bus[1].t[:], in_=PB[6].t[:, :], func=AF.Copy), reads=[PB[6].b], writes=[bus[1].b])
            def views(g_):
                slist_, tk0_, ntok_ = groups[g_]
                s0_ = slist_[0]
                if not is_s:
                    V3 = lambda ap: ap.rearrange("p (s t) -> p s t", t=T5)
                    QR, QI = Qtab.t[:, 0], Qtab.t[:, 1]
                    PR_, PI_ = Ptab.t[:, 0], Ptab.t[:, 1]
                    msk = mask32.t[:].rearrange("p s t -> p (s t)")
                    first = lambda ap: V3(ap)[:, :, 0]
                    cin_r, cin_i = s5cr.t[:, 0, :], s5cr.t[:, 1, :]
                else:
                    V3 = lambda ap: ap.rearrange("p (s q b) -> p s q b", q=NS, b=LS)
                    bc = lambda ap: ap.unsqueeze(2).to_broadcast([128, 8, NS, LS])
                    QR, QI = bc(Qtab.t[:, 0, s0_:s0_ + 8, 0:LS]), bc(Qtab.t[:, 1, s0_:s0_ + 8, 0:LS])
                    PR_, PI_ = bc(Ptab.t[:, 0, s0_:s0_ + 8, 0:LS]), bc(Ptab.t[:, 1, s0_:s0_ + 8, 0:LS])
                    msk = mask4.t[:].rearrange("p s t -> p (s t)")
                    first = lambda ap: V3(ap)[:, :, :, 0]
                    cin_r, cin_i = sts5.t[:, 0, s0_:s0_ + 8, :], sts5.t[:, 1, s0_:s0_ + 8, :]
                return V3, QR, QI, PR_, PI_, msk, first, cin_r, cin_i
            vsets = [[s5v[0], s5v[1]], [s5vb[0], s5vb[1]]]

            def mults_adds(g_):
                V3, QR, QI, PR_, PI_, msk, first, cin_r, cin_i = views(g_)
                bus = busd[g_ % 2]
                br, bi = V3(bus[0].t[:]), V3(bus[1].t[:])
                t1, t2, t3, t4 = s5t[0], s5t[1], s5t34[0], s5t34[1]
                vr, vi = vsets[g_ % 2]
                tb = [Qtab.b]
                for (o, a, b_, rd) in ((t1, QR, br, bus[0].b), (t2, QI, bi, bus[1].b), (t3, QR, bi, bus[1].b), (t4, QI, br, bus[0].b)):
                    S.op("dve", lambda e, o=o, a=a, b_=b_: TT(e, V3(o.t[:]), a, b_, ALU.mult), reads=tb + [rd], writes=[o.b])
                S.op(ENG_ADDS, lambda e: TT(e, vr.t[:], t1.t[:], t2.t[:], ALU.subtract), reads=[t1.b, t2.b], writes=[vr.b])
                S.op(ENG_ADDS, lambda e: TT(e, vi.t[:], t3.t[:], t4.t[:], ALU.add), reads=[t3.b, t4.b], writes=[vi.b])
            emit_bu(0)
            if len(groups) > 1:
                emit_bu(1)
            mults_adds(0)
            pend_y5 = [None]
            for gi_, (slist, tk0, ntok) in enumerate(groups):
                yield
                ns = len(slist)
                s0 = slist[0]
                V3, QR, QI, PR_, PI_, msk, first, cin_r, cin_i = views(gi_)
                vr, vi = vsets[gi_ % 2]
                if gi_ + 1 < len(groups):
                    mults_adds(gi_ + 1)
                    yield
                if gi_ + 2 < len(groups):
                    emit_bu(gi_ + 2)
                S.op("dve", lambda e: TT(e, first(vr.t[:]), first(vr.t[:]), cin_r, ALU.add), reads=[vr.b, s5cr.b, sts5.b], writes=[vr.b])
                S.op("dve", lambda e: TT(e, first(vi.t[:]), first(vi.t[:]), cin_i, ALU.add), reads=[vi.b, s5cr.b, sts5.b], writes=[vi.b])
                yield
                s5k[0] ^= 1
                gr, gi2 = s5g[s5k[0]][0], s5g[s5k[0]][1]
                S.op("dve", lambda e: e.tensor_tensor_scan(out=gr.t[:], data0=msk, data1=vr.t[:], initial=0.0, op0=ALU.mult, op1=ALU.add),
                     reads=[vr.b, mask32.b, mask4.b], writes=[gr.b])
                S.op("dve", lambda e: e.tensor_tensor_scan(out=gi2.t[:], data0=msk, data1=vi.t[:], initial=0.0, op0=ALU.mult, op1=ALU.add),
                     reads=[vi.b, mask32.b, mask4.b], writes=[gi2.b])
                yield
                hp = s5h[gi_ % 2]
                hr, hi = hp, hp
                for (o, a, b_) in ((hp[0], PR_, gr), (hp[1], PI_, gi2), (hp[2], PR_, gi2), (hp[3], PI_, gr)):
                    S.op(ENG_OUTROT, lambda e, o=o, a=a, b_=b_: TT(e, V3(o.t[:]), a, V3(b_.t[:]), ALU.mult),
                         reads=[Ptab.b, b_.b], writes=[o.b])
                yield
                if not is_s:
                    glr, gli = V3(gr.t[:])[:, :, T5 - 1], V3(gi2.t[:])[:, :, T5 - 1]
                    plr, pli = Ptab.t[:, 0, :, T5 - 1], Ptab.t[:, 1, :, T5 - 1]
                    c_ = lambda i: s5c.t[:, i, :]
                    outr, outi = s5cr.t[:, 0, :], s5cr.t[:, 1, :]
                else:
                    glr, gli = V3(gr.t[:])[:, :, :, LS - 1], V3(gi2.t[:])[:, :, :, LS - 1]
                    plr = Ptab.t[:, 0, s0:s0 + 8, LS - 1:LS].to_broadcast([128, 8, NS])
                    pli = Ptab.t[:, 1, s0:s0 + 8, LS - 1:LS].to_broadcast([128, 8, NS])
                    c_ = lambda i: hn[0].t[:, i, :].rearrange("p (s q) -> p s q", q=NS)
                    outr, outi = s5fin.t[:, 0, s0:s0 + 8, 1:17], s5fin.t[:, 1, s0:s0 + 8, 1:17]
                cb_ = [s5c.b, hn[0].b]
                if not is_s:
                    pl2 = Ptab.t[:, :, :, T5 - 1]
                    ca, cb2 = s5c.t[:, 0:2, :], s5c.t[:, 2:4, :]
                    S.op("dve", lambda e: TT(e, ca, pl2, glr.unsqueeze(1).to_broadcast([128, 2, 16]), ALU.mult),
                         reads=[Ptab.b, gr.b] + cb_, writes=cb_)
                    S.op("dve", lambda e: TT(e, cb2, pl2, gli.unsqueeze(1).to_broadcast([128, 2, 16]), ALU.mult),
                         reads=[Ptab.b, gi2.b] + cb_, writes=cb_)
                    S.op("dve", lambda e: TT(e, outr, c_(0), c_(3), ALU.subtract), reads=cb_, writes=[s5cr.b, s5fin.b])
                    S.op("dve", lambda e: TT(e, outi, c_(2), c_(1), ALU.add), reads=cb_, writes=[s5cr.b, s5fin.b])
                else:
                    cseq = [(c_(0), plr, glr, ALU.mult), (c_(1), pli, gli, ALU.mult), (c_(2), plr, gli, ALU.mult), (c_(3), pli, glr, ALU.mult)]
                    for (o, a, b, op) in cseq:
                        S.op("dve", lambda e, o=o, a=a, b=b, op=op: TT(e, o, a, b, op), reads=[Ptab.b, gr.b, gi2.b] + cb_, writes=cb_)
                    S.op("dve", lambda e: TT(e, outr, c_(0), c_(1), ALU.subtract), reads=cb_, writes=[s5cr.b, s5fin.b])
                    S.op("dve", lambda e: TT(e, outi, c_(2), c_(3), ALU.add), reads=cb_, writes=[s5cr.b, s5fin.b])
                yield
                def emit_y5(gi_=gi_, slist=slist, tk0=tk0, ntok=ntok, hr=hr, hi=hi):
                    y5c0 = 352
                    nq = 4 if not is_s else 2
                    for qi in range(nq):
                        q = qi if not is_s else 2 * gi_ + qi
                        S.op("pe", lambda e, q=q, qi=qi: e.matmul(PB[4].t[:, y5c0 + qi * ntok:y5c0 + (qi + 1) * ntok], dg5.t[:, q, :],
                                                                  u5T.t[:, q, tk0:tk0 + ntok], start=(qi == 0), stop=False, skip_group_check=True),
                             reads=[dg5.b, u5T.b], writes=[PB[4].sub("y5")])
                    for idx, s in enumerate(slist):
                        qi = (s // 4) if not is_s else (s // 4 - 2 * gi_)
                        out = PB[4].t[32 * (s % 4):32 * (s % 4) + 32, y5c0 + qi * ntok:y5c0 + (qi + 1) * ntok]
                        for j4, lw in enumerate((s5CT.t[:, 0, s, :], s5CTn.t[:, s, :], s5CT.t[:, 1, s, :], s5CT.t[:, 1, s, :])):
                            S.op("pe", lambda e, j4=j4, lw=lw: e.matmul(out, lw, hr[j4].t[:, idx * ntok:(idx + 1) * ntok],
                                                                        start=False, stop=(j4 == 3), skip_group_check=True,
                                                                        tile_position=(0, 32 * (s % 4))),
                                 reads=[s5CT.b, s5CTn.b, hr[j4].b], writes=[PB[4].sub("y5")])
                    q0 = 0 if not is_s else 2 * gi_
                    S.op("act", lambda e: e.activation(out=y5pre.t[:, q0:q0 + nq, tk0:tk0 + ntok],
                                                       in_=PB[4].t[:, y5c0:y5c0 + nq * ntok].rearrange("p (q t) -> p q t", t=ntok), func=AF.Copy),
                         reads=[PB[4].sub("y5")], writes=[y5pre.b])
                if pend_y5[0] is not None:
                    pend_y5[0]()
                    yield
                pend_y5[0] = emit_y5
            if pend_y5[0] is not None:
                pend_y5[0]()
                pend_y5[0] = None
                yield
            if ti == 7:
                S.op("dve", lambda e: e.tensor_copy(out=s5fin.t[:, :, :, 0], in_=s5cr.t[:]), reads=[s5cr.b], writes=[s5fin.b])
            if is_s:
                S.dma("sp", o_s5, s5fin.t[:].rearrange("p a s q -> p (a s q)"), reads=[s5fin.b], buf=s5fin.b)
                outbufs.append(s5fin.b)
            if ti == 0:
                dump("y5pre", y5pre.t[:].rearrange("p k t -> p (k t)"), [128, 4 * NTM], [y5pre.b])
            ckpt("E%d" % ti)
            yield
            S.op("act", lambda e: e.activation(out=g5.t[:, :, 0:NT], in_=y5pre.t[:, :, 0:NT], func=AF.Gelu), reads=[y5pre.b], writes=[g5.b])
            for m in range(4):
                yield
                pb = next_pb()
                for q in range(4):
                    S.op("pe", lambda e, m=m, q=q, pb=pb: e.matmul(pb.t[:, 0:NT], wglu_sb.t[:, q, m * 128:(m + 1) * 128], g5.t[:, q, 0:NT],
                                                                   start=(q == 0), stop=(q == 3)),
                         reads=[wglu_sb.b, g5.b], writes=[pb.b])
                S.op("act", lambda e, m=m, pb=pb: e.activation(out=sgl.t[:, 0:NT], in_=pb.t[:, 0:NT], func=AF.Sigmoid,
                                                               bias=prm.t[:, P_S5M + 4 + m:P_S5M + 5 + m]),
                     reads=[pb.b, prm.b], writes=[sgl.b])
                S.op("dve", lambda e, m=m: TT(e, mixt[ti % 2].t[:, 4 + m, 0:NT], g5.t[:, m, 0:NT], sgl.t[:, 0:NT], ALU.mult),
                     reads=[g5.b, sgl.b], writes=[mixt[ti % 2].sub("s5")])
            S.dma("sp", mixd[:, :, t0:t0 + NT], mixt[ti % 2].t[:, :, 0:NT], reads=mixt[ti % 2].allb(), writes=[mixdb[ti]], buf=mixdb[ti])
            ckpt("T%d" % ti)
            if ti == 0:
                dump("mix0", mixt[0].t[:, :, 0:NTM], [128, 8, NTM], mixt[0].allb())
            yield

        import os as _os
        RATIO = int(_os.environ.get("K_RATIO", "1"))
        HEAD = int(_os.environ.get("K_HEAD", "9"))
        HEADB = int(_os.environ.get("K_HEADB", "10"))

        def drive(gens, ada_every=0, head=0):
            gens = [g for g in gens if g is not None]
            n = 0
            if len(gens) > 1:
                for _ in range(head):
                    try:
                        next(gens[0])
                    except StopIteration:
                        gens.pop(0)
                        break
            while gens:
                for gi__, g in enumerate(list(gens)):
                    for _ in range((RATIO if gi__ == 0 else 1) if RATIO > 0 else (-RATIO if gi__ == 1 else 1)):
                        try:
                            next(g)
                        except StopIteration:
                            if g in gens:
                                gens.remove(g)
                            break
                n += 1
                if ada_every and n % ada_every == 0:
                    ada_step()
        ada_state[0] = 0
        drive([chain1(0)], ada_every=12)
        for ti_ in range(len(TILES_A)):
            if ti_ == 7:
                while ada_state[1] < len(ADA_CH):
                    ada_step()
                fill_x(a1x, amod.t[:, 0:8, 1:17], [amod.b])
                fill_x(sh1x, chunkmod(MOD_SH1)[:, :, 1:17], [mod.b])
                make_amod([(1, (4, 1)), (2, (7, 2))])
            drive([chain2(ti_), chain1(ti_ + 1) if ti_ + 1 < len(TILES_A) else None], ada_every=(8 if ti_ < 7 else 0), head=HEAD)
        dump("mixS", mixt[0].t[:, :, 0:64], [128, 8, 64], mixt[0].allb())
        S.barrier()
        ckpt("1a")
        A.lo = LO_GLOBAL
        x1T = A.alloc("x1T", [128, 8, NTOK], F32, top=True)
        vT = A.alloc("vT", [128, 8, NTOK], BF16, top=True)
        pre_g = [A.alloc("wgs%dt" % i, [128, 8, 256], BF16, top=True) for i in range(2)]
        pre_u = [A.alloc("wus%dt" % i, [128, 8, 256], BF16, top=True) for i in range(2)]
        wout_sb = A.alloc("wout_sb", [128, 8, D], BF16)
        wout_v = wout.rearrange("(kt p) n -> p kt n", p=128)
        for kh in range(4):
            S.dma("pool", wout_sb.t[:, 2 * kh:2 * kh + 2, :], wout_v[:, 2 * kh:2 * kh + 2, :], writes=[wout_sb.sub(kh)])
        mixb = [A.alloc("mixb%d" % i, [128, 8, 512], BF16) for i in range(2)]

        def load_mix(ti):
            t0, NT, is_s = TILES_B[ti]
            tiles_a = [i for i, (a0, n0, s0_) in enumerate(TILES_A) if a0 >= t0 and a0 < t0 + NT]
            S.dma("sp", mixb[ti % 2].t[:, :, 0:NT], mixd[:, :, t0:t0 + NT], reads=[mixdb[i] for i in tiles_a], writes=[mixb[ti % 2].b])
        wg_v = wg.rearrange("(kt p) n -> p kt n", p=128)
        wu_v = wu.rearrange("(kt p) n -> p kt n", p=128)
        for si_ in range(2):
            S.dma("pool", pre_g[si_].t[:], wg_v[:, :, si_ * 256:(si_ + 1) * 256], writes=[pre_g[si_].b])
            S.dma("pool", pre_u[si_].t[:], wu_v[:, :, si_ * 256:(si_ + 1) * 256], writes=[pre_u[si_].b])
        xtm2 = A.alloc("xtm2", [128, 4, D], F32)
        xTm = [A.alloc("xTm%d" % i, [128, 512], F32) for i in range(2)]
        sqb = [A.alloc("sqb%d" % i, [128, 512], BF16) for i in range(2)]
        onesb = A.alloc("onesb", [128, 128], BF16)
        S.op("dve", lambda e: e.memset(onesb.t[:], 1.0), writes=[onesb.b])
        tmp2 = [A.alloc("tmp2_%d" % i, [128, 512], F32) for i in range(2)]
        rstdb = [A.alloc("rstdb%d" % i, [128, 512], F32) for i in range(2)]
        g1x = expand_mod("g1x", chunkmod(MOD_G1)[:, :, 1:17], [mod.b])
        a2x = expand_mod("a2x", amod.t[:, 8:16, 1:17], [amod.b])
        sh2x = expand_mod("sh2x", chunkmod(MOD_SH2)[:, :, 1:17], [mod.b])
        print("arena p1b: lo=%d hi=%d" % (A.lo, A.hi))
        TILES_B = [(i * 512, 512, False) for i in range(4)] + [(SEQ, 64, True)]

        def load_x2(ti):
            t0, NT, is_s = TILES_B[ti]
            for blk in range((NT + 127) // 128):
                rows = min(128, NT - blk * 128)
                S.dma("sp", xtm2.t[0:rows, blk, :], xin[t0 + blk * 128:t0 + blk * 128 + rows, :], writes=[xtm2.sub(blk)])
        load_x2(0)
        load_mix(0)

        def stat_accum(src_ap, m, NT, pbs, defer=None):
            sq = sqb[m % 2]
            S.op("act", lambda e: e.activation(out=sq.t[:, 0:NT], in_=src_ap, func=AF.Square), reads=[x1T.sub(m)], writes=[sq.b])

            def mm(m=m, sq=sq):
                S.op("pe", lambda e: e.matmul(pbs.t[:, 0:NT], onesb.t[:], sq.t[:, 0:NT], start=(m == 0), stop=(m == 7)),
                     reads=[onesb.b, sq.b], writes=[pbs.b])
            if defer is None:
                mm()
            else:
                if defer[0] is not None:
                    defer[0]()
                defer[0] = mm
                if m == 7:
                    defer[0]()
                    defer[0] = None

        def stat_finish(NT, pbs, rs):
            S.op("act", lambda e: e.activation(out=rs.t[:, 0:NT], in_=pbs.t[:, 0:NT], func=AF.Ln, scale=1.0 / D, bias=EPS),
                 reads=[pbs.b], writes=[rs.b])
            S.op("act", lambda e: e.activation(out=rs.t[:, 0:NT], in_=rs.t[:, 0:NT], func=AF.Exp, scale=-0.5), reads=[rs.b], writes=[rs.b])

        def b_part1(ti):
            t0, NT, is_s = TILES_B[ti]
            nblk = (NT + 127) // 128
            tsl = slice(t0, t0 + NT)
            pbs = PB[4 + ti % 2]
            dfr = [None]
            for m in range(8):
                pbx = PB[2 + m % 2]
                xm = xTm[m % 2]
                for blk in range(nblk):
                    rows = min(128, NT - blk * 128)
                    S.op("pe", lambda e, blk=blk, rows=rows: e.transpose(
                        pbx.t[:, blk * 128:blk * 128 + rows], xtm2.t[0:rows, blk, m * 128:(m + 1) * 128], cst.t[0:rows, C_ID:C_ID + rows]),
                        reads=[xtm2.sub(blk), cst.b], writes=[pbx.b])
                S.op("act", lambda e: e.activation(out=xm.t[:, 0:NT], in_=pbx.t[:, 0:NT], func=AF.Copy), reads=[pbx.b], writes=[xm.b])
                pb = next_pb()
                for kt in range(8):
                    S.op("pe", lambda e, kt=kt: e.matmul(pb.t[:, 0:NT], wout_sb.t[:, kt, m * 128:(m + 1) * 128], mixb[ti % 2].t[:, kt, 0:NT],
                                                         start=(kt == 0), stop=(kt == 7)),
                         reads=[wout_sb.sub(kt // 2), mixb[ti % 2].b], writes=[pb.b])
                if m == 0 and ti + 1 < len(TILES_B):
                    load_mix(ti + 1)
                if not is_s:
                    S.op("dve", lambda e: e.scalar_tensor_tensor(
                        out=x1T.t[:, m, tsl], in0=pb.t[:, 0:NT], scalar=mod.t[:, 8 * MOD_G1 + m, 0:1], in1=xm.t[:, 0:NT],
                        op0=ALU.mult, op1=ALU.add), reads=[pb.b, mod.b, xm.b], writes=[x1T.sub(m)])
                else:
                    S.op("dve", lambda e: TT(e, tmp2[0].t[:, 0:NT], pb.t[:, 0:NT], g1x.t[:, m, :], ALU.mult),
                         reads=[pb.b, g1x.b], writes=[tmp2[0].b])
                    S.op("dve", lambda e: TT(e, x1T.t[:, m, tsl], tmp2[0].t[:, 0:NT], xm.t[:, 0:NT], ALU.add),
                         reads=[tmp2[0].b, xm.b], writes=[x1T.sub(m)])
                stat_accum(x1T.t[:, m, tsl], m, NT, pbs, defer=dfr)
                yield
            if ti + 1 < len(TILES_B):
                load_x2(ti + 1)
            yield

        def b_part2(ti):
            t0, NT, is_s = TILES_B[ti]
            tsl = slice(t0, t0 + NT)
            rs = rstdb[ti % 2]
            stat_finish(NT, PB[4 + ti % 2], rs)
            yield
            for m in range(8):
                tq = tmp2[m % 2]
                S.op("dve", lambda e: TT(e, tq.t[:, 0:NT], x1T.t[:, m, tsl], rs.t[:, 0:NT], ALU.mult),
                     reads=[x1T.sub(m), rs.b], writes=[tq.b])
                if not is_s:
                    S.op("act", lambda e: e.activation(out=vT.t[:, m, tsl], in_=tq.t[:, 0:NT], func=AF.Identity,
                                                       scale=amod.t[:, 8 + m, 0:1], bias=mod.t[:, 8 * MOD_SH2 + m, 0:1]),
                         reads=[tq.b, amod.b, mod.b], writes=[vT.sub(m)])
                else:
                    S.op("dve", lambda e: TT(e, tq.t[:, 0:NT], tq.t[:, 0:NT], a2x.t[:, m, :], ALU.mult),
                         reads=[tq.b, a2x.b], writes=[tq.b])
                    S.op("dve", lambda e: TT(e, vT.t[:, m, tsl], tq.t[:, 0:NT], sh2x.t[:, m, :], ALU.add),
                         reads=[tq.b, sh2x.b], writes=[vT.sub(m)])
                yield
            if ti == 0:
                dump("x1p", x1T.t[:, :, 0:256], [128, 8, 256], x1T.allb())
                dump("vp", vT.t[:, :, 0:256], [128, 8, 256], vT.allb())
        drive([b_part1(0)])
        for ti_ in range(len(TILES_B)):
            drive([b_part2(ti_), b_part1(ti_ + 1) if ti_ + 1 < len(TILES_B) else None], head=HEADB)
        S.barrier()
        ckpt("1b")

        A.lo = LO_GLOBAL
        tmp2 = [A.alloc("tmp3_%d" % i, [128, 512], F32) for i in range(2)]
        rstdb = [A.alloc("rstd3_%d" % i, [128, 512], F32) for i in range(2)]
        sqb = [A.alloc("sqb3_%d" % i, [128, 512], BF16) for i in range(2)]
        onesb = A.alloc("onesb3", [128, 128], BF16)
        S.op("dve", lambda e: e.memset(onesb.t[:], 1.0), writes=[onesb.b])
        g2x = expand_mod("g2x", chunkmod(MOD_G2)[:, :, 1:17], [mod.b])
        afx = expand_mod("afx", amod.t[:, 16:24, 1:17], [amod.b])
        shfx = expand_mod("shfx", chunkmod(MOD_SHF)[:, :, 1:17], [mod.b])
        LO_P2 = A.lo
        hT = A.alloc("hT", [128, 6, NTOK], BF16)
        wgs = pre_g + [A.alloc("wgs2", [128, 8, 256], BF16)]
        wus = pre_u + [A.alloc("wus2", [128, 8, 256], BF16)]
        wds = [A.alloc("wds%d" % i, [128, 6, D], BF16) for i in range(2)]
        sgt = [A.alloc("sgt%d" % i, [128, 512], BF16) for i in range(2)]
        print("arena p2: lo=%d hi=%d" % (A.lo, A.hi))
        wg_v = wg.rearrange("(kt p) n -> p kt n", p=128)
        wu_v = wu.rearrange("(kt p) n -> p kt n", p=128)
        wd_v = wd.rearrange("(j p) n -> p j n", p=128)
        QUARTERS = [(0, 6), (6, 12), (12, 18), (18, 22)]
        SLABS = [(q, ja + 2 * s) for q, (ja, jb) in enumerate(QUARTERS) for s in range((jb - ja) // 2)]

        def load_gu(si):
            q, j0 = SLABS[si]
            S.dma("pool", wgs[si % 3].t[:], wg_v[:, :, j0 * 128:(j0 + 2) * 128], writes=[wgs[si % 3].b])
            S.dma("pool", wus[si % 3].t[:], wu_v[:, :, j0 * 128:(j0 + 2) * 128], writes=[wus[si % 3].b])

        def load_wd(q):
            ja, jb = QUARTERS[q]
            for jh in range(0, jb - ja, 2):
                S.dma("pool", wds[q % 2].t[:, jh:jh + 2, :], wd_v[:, ja + jh:ja + jh + 2, :], writes=[wds[q % 2].b])
        assert SLABS[0][1] == 0 and SLABS[1][1] == 2
        load_wd(0)
        gbank = [0]
        si = 0
        for q, (ja, jb) in enumerate(QUARTERS):
            if q + 1 < 4:
                load_wd(q + 1)
            for s in range((jb - ja) // 2):
                if si + 2 < len(SLABS):
                    load_gu(si + 2)
                wgt, wut = wgs[si % 3], wus[si % 3]
                for jc in range(2):
                    jj = 2 * s + jc
                    for (t0, NT, is_s) in TILES_B:
                        tsl = slice(t0, t0 + NT)
                        gbank[0] ^= 1
                        pbg, pbu = PB[gbank[0]], PB[2 + gbank[0]]
                        for (wt, pb_) in ((wgt, pbg), (wut, pbu)):
                            for kt in range(8):
                                S.op("pe", lambda e, kt=kt, wt=wt, pb_=pb_: e.matmul(
                                    pb_.t[:, 0:NT], wt.t[:, kt, jc * 128:(jc + 1) * 128], vT.t[:, kt, tsl], start=(kt == 0), stop=(kt == 7)),
                                    reads=[wt.b] + vT.allb(), writes=[pb_.b])
                        sg_ = sgt[gbank[0]]
                        S.op("act", lambda e, pbg=pbg, sg_=sg_: e.activation(out=sg_.t[:, 0:NT], in_=pbg.t[:, 0:NT], func=AF.Silu),
                             reads=[pbg.b], writes=[sg_.b])
                        S.op("dve", lambda e, pbu=pbu, sg_=sg_: TT(e, hT.t[:, jj, tsl], sg_.t[:, 0:NT], pbu.t[:, 0:NT], ALU.mult),
                             reads=[sg_.b, pbu.b], writes=[hT.sub(jj)])
                si += 1
            nj = jb - ja
            wdt = wds[q % 2]
            for (t0, NT, is_s) in TILES_B:
                tsl = slice(t0, t0 + NT)
                for m in range(8):
                    pb = PB[4 + m % 2]
                    for jj in range(nj):
                        S.op("pe", lambda e, jj=jj, m=m, pb=pb: e.matmul(pb.t[:, 0:NT], wdt.t[:, jj, m * 128:(m + 1) * 128], hT.t[:, jj, tsl],
                                                                         start=(jj == 0), stop=(jj == nj - 1)),
                             reads=[wdt.b, hT.sub(jj)], writes=[pb.b])
                    if not is_s:
                        S.op("dve", lambda e, m=m, pb=pb: e.scalar_tensor_tensor(
                            out=x1T.t[:, m, tsl], in0=pb.t[:, 0:NT], scalar=mod.t[:, 8 * MOD_G2 + m, 0:1], in1=x1T.t[:, m, tsl],
                            op0=ALU.mult, op1=ALU.add), reads=[pb.b, mod.b, x1T.sub(m)], writes=[x1T.sub(m)])
                    else:
                        S.op("dve", lambda e, m=m, pb=pb: TT(e, tmp2[0].t[:, 0:NT], pb.t[:, 0:NT], g2x.t[:, m, :], ALU.mult),
                             reads=[pb.b, g2x.b], writes=[tmp2[0].b])
                        S.op("dve", lambda e, m=m: TT(e, x1T.t[:, m, tsl], tmp2[0].t[:, 0:NT], x1T.t[:, m, tsl], ALU.add),
                             reads=[tmp2[0].b, x1T.sub(m)], writes=[x1T.sub(m)])
        S.barrier()
        ckpt("ffn")
        A.lo = LO_P2
        yTs = [A.alloc("yT%d" % i, [128, 8, 512], F32) for i in range(2)]
        ytm = [A.alloc("ytm%d" % i, [128, D], F32) for i in range(2)]
        print("arena final: lo=%d hi=%d" % (A.lo, A.hi))
        oi = [0]

        def f_part1(ti):
            t0, NT, is_s = TILES_B[ti]
            tsl = slice(t0, t0 + NT)
            yT = yTs[ti % 2]
            pbs = PB[6 + ti % 2]
            rs = rstdb[ti % 2]
            for m in range(8):
                stat_accum(x1T.t[:, m, tsl], m, NT, pbs)
                if m % 2 == 1:
                    yield
            stat_finish(NT, pbs, rs)
            yield
            for m in range(8):
                tq = tmp2[m % 2]
                S.op("dve", lambda e: TT(e, tq.t[:, 0:NT], x1T.t[:, m, tsl], rs.t[:, 0:NT], ALU.mult),
                     reads=[x1T.sub(m), rs.b], writes=[tq.b])
                if not is_s:
                    S.op("act", lambda e: e.activation(out=yT.t[:, m, 0:NT], in_=tq.t[:, 0:NT], func=AF.Identity,
                                                       scale=amod.t[:, 16 + m, 0:1], bias=mod.t[:, 8 * MOD_SHF + m, 0:1]),
                         reads=[tq.b, amod.b, mod.b], writes=[yT.sub(m)])
                else:
                    S.op("dve", lambda e: TT(e, tq.t[:, 0:NT], tq.t[:, 0:NT], afx.t[:, m, :], ALU.mult),
                         reads=[tq.b, afx.b], writes=[tq.b])
                    S.op("dve", lambda e: TT(e, yT.t[:, m, 0:NT], tq.t[:, 0:NT], shfx.t[:, m, :], ALU.add),
                         reads=[tq.b, shfx.b], writes=[yT.sub(m)])
                yield

        def f_part2(ti):
            t0, NT, is_s = TILES_B[ti]
            yT = yTs[ti % 2]
            for blk in range((NT + 127) // 128):
                rows = min(128, NT - blk * 128)
                yo = ytm[oi[0] % 2]
                oi[0] += 1
                for half in range(2):
                    pbt = PB[half]
                    for k4 in range(4):
                        kt = 4 * half + k4
                        S.op("pe", lambda e, kt=kt, k4=k4: e.transpose(
                            pbt.t[0:rows, k4 * 128:(k4 + 1) * 128], yT.t[:, kt, blk * 128:blk * 128 + rows], ident),
                            reads=[yT.sub(kt), cst.b], writes=[pbt.b])
                    if half == 0:
                        S.op("act", lambda e: e.activation(out=yo.t[0:rows, 0:512], in_=pbt.t[0:rows, :], func=AF.Copy),
                             reads=[pbt.b], writes=[yo.b])
                    else:
                        S.op("dve", lambda e: e.tensor_copy(out=yo.t[0:rows, 512:1024], in_=pbt.t[0:rows, :]),
                             reads=[pbt.b], writes=[yo.b])
                    yield
                S.dma("sp", yout[t0 + blk * 128:t0 + blk * 128 + rows, :], yo.t[0:rows, :], reads=[yo.b], buf=yo.b)
        import os as _os2
        if True:
            for ti_ in range(len(TILES_B)):
                drive([f_part1(ti_)])
                drive([f_part2(ti_)])
        else:
            drive([f_part1(0)])
            for ti_ in range(len(TILES_B)):
                drive([f_part2(ti_), f_part1(ti_ + 1) if ti_ + 1 < len(TILES_B) else None])
        S.barrier()
    return nc, dumps


def _prep_inputs(inp):
    cstv = _consts()
    prmv = _params(inp)
    BT, CT = _s5mats(inp)
    maps = []
    for i in range(NCORES):
        m = {}
        m["xin"] = np.ascontiguousarray(np.concatenate(
            [inp["x_prompt"][i], inp["x_sample"][NS * i:NS * (i + 1)].reshape(NS * LS, D)], axis=0), dtype=np.float32)
        m["cin"] = np.ascontiguousarray(np.concatenate(
            [inp["c_prompt"][i:i + 1], inp["c_sample"][NS * i:NS * (i + 1)]], axis=0), dtype=np.float32)
        m["wada"] = np.ascontiguousarray(inp["w_ada"][0], dtype=np.float32)
        m["wadaf"] = np.ascontiguousarray(inp["w_ada_f"], dtype=np.float32)
        m["win"] = np.ascontiguousarray(inp["w_in"][0], dtype=np.float32)
        m["wglu"] = np.ascontiguousarray(inp["w_glu"][0], dtype=np.float32)
        m["wout"] = np.ascontiguousarray(inp["w_out"][0], dtype=np.float32)
        m["wg"] = np.ascontiguousarray(inp["w_ffn_gate"][0], dtype=np.float32)
        m["wu"] = np.ascontiguousarray(inp["w_ffn_up"][0], dtype=np.float32)
        m["wd"] = np.ascontiguousarray(inp["w_ffn_down"][0], dtype=np.float32)
        m["cst"] = cstv
        m["prm"] = prmv
        m["s5bt"] = BT.reshape(128, -1)
        m["s5ct"] = CT.reshape(128, -1)
        m["stssd"] = np.ascontiguousarray(inp["state_ssd"][0, NS * i:NS * (i + 1)], dtype=np.float32)
        sc = inp["state_conv"][0, NS * i:NS * (i + 1)]
        m["stconv"] = np.ascontiguousarray(
            sc.reshape(NS, 3, 8, 128).transpose(3, 2, 0, 1).reshape(128, -1), dtype=np.float32)
        sr = inp["state_s5_re"][0, NS * i:NS * (i + 1)]
        si = inp["state_s5_im"][0, NS * i:NS * (i + 1)]
        st = np.stack([sr, si], 0).reshape(2, NS, 16, 128).transpose(3, 0, 2, 1)
        m["sts5"] = np.ascontiguousarray(st.reshape(128, -1), dtype=np.float32)
        maps.append(m)
    return maps


_CACHE = {}


def kernel(**inputs):
    inp = {k: np.asarray(v) for k, v in inputs.items()}
    if "nc" not in _CACHE:
        _CACHE["nc"] = build()[0]
    nc = _CACHE["nc"]
    maps = _prep_inputs(inp)
    res = run_bass_kernel_spmd(nc, maps, core_ids=list(range(NCORES)))
    R = res.results
    y_p = np.stack([R[i]["yout"][:SEQ] for i in range(NCORES)], 0)
    y_s = np.concatenate([R[i]["yout"][SEQ:].reshape(NS, LS, D) for i in range(NCORES)], 0)
    ssd_p = np.stack([R[i]["o_ssdp"].reshape(128, 8, 64).transpose(1, 2, 0) for i in range(NCORES)], 0)[None]
    ssd_s = np.concatenate([R[i]["o_ssds"] for i in range(NCORES)], 0)[None]
    conv = [R[i]["o_conv"].reshape(128, 8, 17, 3).transpose(2, 3, 1, 0).reshape(17, 3, 1024) for i in range(NCORES)]
    conv_p = np.stack([c[0] for c in conv], 0)[None]
    conv_s = np.concatenate([c[1:] for c in conv], 0)[None]
    s5 = [R[i]["o_s5"].reshape(128, 2, 16, 17).transpose(1, 3, 2, 0).reshape(2, 17, 32, 64) for i in range(NCORES)]
    re_p = np.stack([s[0, 0] for s in s5], 0)[None]
    re_s = np.concatenate([s[0, 1:] for s in s5], 0)[None]
    im_p = np.stack([s[1, 0] for s in s5], 0)[None]
    im_s = np.concatenate([s[1, 1:] for s in s5], 0)[None]
    f = lambda a: np.ascontiguousarray(a, dtype=np.float32)
    return (f(y_p), f(y_s), f(ssd_p), f(ssd_s), f(conv_p), f(conv_s), f(re_p), f(re_s), f(im_p), f(im_s))
```

```python
import math
import numpy as np
from contextlib import ExitStack
import concourse.bass as bass
import concourse.mybir as mybir
from concourse.bass_utils import run_bass_kernel_spmd

F32 = mybir.dt.float32
BF16 = mybir.dt.bfloat16
I32 = mybir.dt.int32
AF = mybir.ActivationFunctionType
ALU = mybir.AluOpType

NCORES = 8
D = 1024
SEQ = 2048
NS = 16
LS = 4
NTOK = SEQ + NS * LS
DFF = 2816
NJ = DFF // 128
INP = 2056
EPS = 1e-6
T5 = 32
TILES = [(0, 512), (512, 512), (1024, 512), (1536, 512), (2048, 64)]
PI = math.pi


class Buf:
    def __init__(self, name):
        self.name = name
        self.w = None
        self.r = []
        self.dsem = None
        self.dcnt = 0


class TL:
    def __init__(self, t, name):
        self.t = t
        self.name = name
        self.b = Buf(name)
        self.subs = {}

    def sub(self, k):
        if getattr(self, "nosub", False):
            return self.b
        if k not in self.subs:
            self.subs[k] = Buf("%s_%s" % (self.name, k))
        return self.subs[k]

    def allb(self):
        return [self.b] + list(self.subs.values())

    def __getitem__(self, k):
        return self.t[k]


class Sched:
    ENG = ["pe", "act", "dve", "pool", "sp"]

    def __init__(self, nc, es):
        self.nc = nc
        self.es = es
        self.eobj = {"pe": nc.tensor, "act": nc.scalar, "dve": nc.vector, "pool": nc.gpsimd, "sp": nc.sync}
        self.cnt = {e: 0 for e in self.ENG}
        self.sem = {e: es.enter_context(nc.semaphore("s_" + e)) for e in self.ENG}
        self.seen = {e: {} for e in self.ENG}
        self.dbufs = []
        self.ninst = 0
        self.dead = False
        self.pe_pending = None

    def _flush_pe(self):
        if self.pe_pending is not None:
            self.pe_pending.then_inc(self.sem["pe"], 1)
            self.cnt["pe"] += 1
            self.pe_pending = None

    def _deps(self, eng, reads, writes, xreads=()):
        deps = []
        for b in reads:
            if b.w is not None:
                deps.append(b.w)
            if b in xreads:
                deps.extend(r for r in b.r if r[2] != eng)
        for b in writes:
            if b.w is not None:
                deps.append(b.w)
            deps.extend(b.r)
        waits = {}
        for (sem, val, key) in deps:
            if key == "pe" and eng == "pe":
                continue
            if self.seen[eng].get(key, 0) >= val:
                continue
            if key == "pe" and val > self.cnt["pe"]:
                self._flush_pe()
            if key not in waits or waits[key][1] < val:
                waits[key] = (sem, val)
        for key, (sem, val) in waits.items():
            self.seen[eng][key] = val
        return list(waits.values())

    def op(self, eng, fn, reads=(), writes=()):
        if self.dead:
            return None
        xr = [b for b in reads if getattr(b, "excl", False)]
        waits = self._deps(eng, reads, writes, xreads=xr)
        e = self.eobj[eng]
        for (s_, v_) in waits:
            e.wait_ge(s_, v_)
        if eng == "pe":
            self.pe_pending = fn(e)
            tok = (self.sem[eng], self.cnt[eng] + 1, eng)
        else:
            self.cnt[eng] += 1
            tok = (self.sem[eng], self.cnt[eng], eng)
            fn(e).then_inc(self.sem[eng], 1)
        for b in reads:
            b.r.append(tok)
        for b in writes:
            b.w = tok
            b.r = []
        self.ninst += 1
        return tok

    def dma(self, eng, out, in_, reads=(), writes=(), buf=None, **kw):
        if self.dead:
            return None
        waits = self._deps(eng, reads, writes)
        if buf is None:
            buf = writes[0] if writes else reads[0]
        if buf.dsem is None:
            buf.dsem = self.es.enter_context(self.nc.semaphore("d_" + buf.name))
            self.dbufs.append(buf)
        buf.dcnt += 16
        tok = (buf.dsem, buf.dcnt, "d_" + buf.name)
        e = self.eobj[eng]
        for (s_, v_) in waits:
            e.wait_ge(s_, v_)
        e.dma_start(out=out, in_=in_, **kw).then_inc(buf.dsem, 16)
        for b in reads:
            b.r.append(tok)
        for b in writes:
            b.w = tok
            b.r = []
        self.ninst += 1
        return tok

    def barrier(self):
        if self.dead:
            return
        self._flush_pe()
        for e in self.ENG:
            waits = []
            for o in self.ENG:
                if o != e and self.cnt[o] > self.seen[e].get(o, 0):
                    waits.append((self.sem[o], self.cnt[o]))
                    self.seen[e][o] = self.cnt[o]
            for b in self.dbufs:
                key = "d_" + b.name
                if b.dcnt > self.seen[e].get(key, 0):
                    waits.append((b.dsem, b.dcnt))
                    self.seen[e][key] = b.dcnt
            for (s_, v_) in waits:
                self.eobj[e].wait_ge(s_, v_)

    def emit(self):
        pass


C_ID = 0
C_TRI = 128
C_NEG = 256
C_TRI64 = 384
C_NEG64 = 512
C_SEG64 = 640
C_SEGI = 768
CST_W = 784

P_BMOD = 0
P_GAIN = 64
P_CONV = 88
P_SSDFM = 128
P_S5P = 136
P_S5M = 184
P_SSD8 = 192
PRM_W = 194


def _consts():
    c = np.zeros((128, CST_W), np.float32)
    c[:, C_ID:C_ID + 128] = np.eye(128, dtype=np.float32)
    s = np.arange(128)[:, None]
    l = np.arange(128)[None, :]
    c[:, C_TRI:C_TRI + 128] = (s <= l).astype(np.float32)
    c[:, C_NEG:C_NEG + 128] = np.where(l >= s, 0.0, -30000.0)
    same = (s // LS == l // LS) & (s < 64) & (l < 64)
    c[:, C_TRI64:C_TRI64 + 128] = ((s <= l) & same).astype(np.float32)
    c[:, C_NEG64:C_NEG64 + 128] = np.where((l >= s) & same, 0.0, -30000.0)
    c[:, C_SEG64:C_SEG64 + 128] = same.astype(np.float32)
    j = np.arange(16)[None, :]
    c[:, C_SEGI:C_SEGI + 16] = ((s // LS == j) & (s < 64)).astype(np.float32)
    return c


def _fm(v, nt):
    return np.ascontiguousarray(np.asarray(v, np.float32).reshape(nt, 128).T)


def _params(inp):
    p = np.zeros((128, PRM_W), np.float32)
    p[:, P_BMOD:P_BMOD + 48] = _fm(inp["b_ada"][0], 48)
    p[:, P_BMOD + 48:P_BMOD + 64] = _fm(inp["b_ada_f"], 16)
    p[:, P_GAIN:P_GAIN + 8] = _fm(inp["norm1_g"][0], 8)
    p[:, P_GAIN + 8:P_GAIN + 16] = _fm(inp["norm2_g"][0], 8)
    p[:, P_GAIN + 16:P_GAIN + 24] = _fm(inp["normf_g"], 8)
    cw = inp["conv_w"][0]
    cv = np.zeros((128, 8, 5), np.float32)
    for k in range(4):
        cv[:, :, k] = _fm(cw[k], 8)
    cv[:, :, 4] = _fm(inp["conv_b"][0], 8)
    p[:, P_CONV:P_CONV + 40] = cv.reshape(128, 40)
    Dh = inp["ssd_D"][0]
    dfm = np.zeros((128, 4), np.float32)
    for pr in range(4):
        dfm[0:64, pr] = Dh[2 * pr]
        dfm[64:128, pr] = Dh[2 * pr + 1]
    p[:, P_SSDFM:P_SSDFM + 4] = dfm
    p[:, P_SSDFM + 4:P_SSDFM + 8] = _fm(inp["ssd_norm_g"][0], 4)

    def st(a):
        return np.ascontiguousarray(np.asarray(a, np.float32).reshape(16, 128).T)
    p[:, P_S5P:P_S5P + 16] = st(inp["s5_A_re"][0])
    p[:, P_S5P + 16:P_S5P + 32] = st(inp["s5_A_im"][0])
    p[:, P_S5P + 32:P_S5P + 48] = st(np.repeat(inp["s5_log_step"][0][:, None], 64, axis=1))
    p[:, P_S5M:P_S5M + 4] = _fm(inp["s5_D"][0], 4)
    p[:, P_S5M + 4:P_S5M + 8] = _fm(inp["b_glu"][0], 4)
    p[0:8, P_SSD8] = inp["ssd_dt_bias"][0]
    p[0:8, P_SSD8 + 1] = inp["ssd_A_log"][0]
    return p


def _s5mats(inp):
    Br, Bi = inp["s5_B_re"][0], inp["s5_B_im"][0]
    Cr, Ci = inp["s5_C_re"][0], inp["s5_C_im"][0]
    BT = np.zeros((128, 2, 16, 128), np.float32)
    CT = np.zeros((128, 2, 16, 32), np.float32)
    for s in range(16):
        for gl in range(2):
            g = 2 * s + gl
            r0 = (g % 8) * 16
            BT[r0:r0 + 16, 0, s, gl * 64:(gl + 1) * 64] = Br[g].T
            BT[r0:r0 + 16, 1, s, gl * 64:(gl + 1) * 64] = Bi[g].T
            CT[gl * 64:(gl + 1) * 64, 0, s, gl * 16:(gl + 1) * 16] = Cr[g].T
            CT[gl * 64:(gl + 1) * 64, 1, s, gl * 16:(gl + 1) * 16] = Ci[g].T
    return BT, CT


class Arena:
    def __init__(self, nc, es, words):
        self.t = es.enter_context(nc.sbuf_tensor("arena", [128, words], F32))
        self.words = words
        self.lo = 0
        self.hi = words

    def alloc(self, name, shape, dt, top=False):
        n = 1
        for d in shape[1:]:
            n *= d
        w = n if dt == F32 or dt == I32 else (n + 1) // 2
        w = (w + 3) // 4 * 4
        if top:
            self.hi -= w
            off = self.hi
        else:
            off = self.lo
            self.lo += w
        assert self.lo <= self.hi, "arena overflow at %s: lo=%d hi=%d" % (name, self.lo, self.hi)
        ap = self.t[:, off:off + w]
        if dt != F32:
            ap = ap.bitcast(dt)
        ap = ap[:, 0:n]
        if len(shape) == 3:
            ap = ap.rearrange("p (a b) -> p a b", b=shape[2])
        elif len(shape) == 4:
            ap = ap.rearrange("p (a b c) -> p a b c", b=shape[2], c=shape[3])
        if shape[0] < 128:
            ap = ap[0:shape[0]]
        return TL(ap, name)


class StopBuild(Exception):
    pass


def build(dbg=None, stop_after=None):
    nc = bass.Bass("TRN2", target_bir_lowering=False)

    SH = []

    def ckpt(name):
        if stop_after == name:
            SH[0].barrier()
            SH[0].dead = True
    dt_in = lambda name, shape: nc.dram_tensor(name, list(shape), F32, kind="ExternalInput").ap()
    dt_out = lambda name, shape: nc.dram_tensor(name, list(shape), F32, kind="ExternalOutput").ap()
    xin = dt_in("xin", [NTOK, D])
    cin = dt_in("cin", [17, D])
    wada = dt_in("wada", [D, 6144])
    wadaf = dt_in("wadaf", [D, 2048])
    win = dt_in("win", [D, INP])
    wglu = dt_in("wglu", [512, 512])
    wout = dt_in("wout", [D, D])
    wg = dt_in("wg", [D, DFF])
    wu = dt_in("wu", [D, DFF])
    wd = dt_in("wd", [DFF, D])
    cst_d = dt_in("cst", [128, CST_W])
    prm_d = dt_in("prm", [128, PRM_W])
    s5bt_d = dt_in("s5bt", [128, 2 * 16 * 128])
    s5ct_d = dt_in("s5ct", [128, 2 * 16 * 32])
    stssd_d = dt_in("stssd", [NS, 8, 64, 128])
    stconv_d = dt_in("stconv", [128, 8 * NS * 3])
    sts5_d = dt_in("sts5", [128, 2 * 16 * NS])
    yout = dt_out("yout", [NTOK, D])
    o_ssdp = dt_out("o_ssdp", [128, 512])
    o_ssds = dt_out("o_ssds", [NS, 8, 64, 128])
    o_conv = dt_out("o_conv", [128, 8 * 17 * 3])
    o_s5 = dt_out("o_s5", [128, 2 * 16 * 17])
    mixd = nc.dram_tensor("mixd", [128, 8, NTOK], BF16, kind="Internal").ap()
    dumps = {}

    with ExitStack() as es:
        S = Sched(nc, es)
        NEED_CTN = []
        SH.append(S)
        A = Arena(nc, es, 53200)
        outbufs = []

        def dump(name, ap, shape, reads):
            if dbg is None or name not in dbg:
                return
            d = dt_out("dbg_" + name, shape)
            dumps[name] = shape
            b = Buf("dbg_" + name)
            S.dma("sp" if ap.dtype == F32 else "pool", d, ap, reads=reads, buf=b)
            outbufs.append(b)

        PB = [TL(es.enter_context(nc.psum_tensor("pb%d" % i, [128, 512], F32)), "pb%d" % i) for i in range(8)]
        for pb_ in PB:
            pb_.b.excl = True
            pb_.nosub = True

        def pbf(i):
            return PB[i].t[:].bitcast(BF16)

        cst = A.alloc("cst", [128, CST_W], F32)
        prm = A.alloc("prm", [128, PRM_W], F32)
        identb = A.alloc("identb", [128, 128], BF16)
        onesf = A.alloc("onesf", [128, 128], F32)
        mod = A.alloc("mod", [128, 64, 17], F32)
        amod = A.alloc("amod", [128, 24, 17], F32)
        s5fin = A.alloc("s5fin", [128, 2, 16, 17], F32)
        scT = A.alloc("scT", [128, 8, 17], BF16)
        LO_GLOBAL = A.lo
        win_sb = A.alloc("win_sb", [128, 8, INP], BF16)
        wglu_sb = A.alloc("wglu_sb", [128, 4, 512], BF16)
        s5BT = A.alloc("s5BT", [128, 2, 16, 128], BF16)
        s5CT = A.alloc("s5CT", [128, 2, 16, 32], BF16)
        LO_W = A.lo

        def load_1a_weights():
            for a_ in range(4):
                S.dma("pool", s5BT.t[:].rearrange("p a s c -> p (a s c)")[:, a_ * 1024:(a_ + 1) * 1024],
                      s5bt_d[:, a_ * 1024:(a_ + 1) * 1024], writes=[s5BT.b])
            S.dma("pool", s5CT.t[:].rearrange("p a s c -> p (a s c)"), s5ct_d, writes=[s5CT.b])
            win_v = win.rearrange("(kt p) n -> p kt n", p=128)
            for kh in range(4):
                for ch in range(2):
                    S.dma("pool", win_sb.t[:, 2 * kh:2 * kh + 2, ch * 1028:(ch + 1) * 1028],
                          win_v[:, 2 * kh:2 * kh + 2, ch * 1028:(ch + 1) * 1028], writes=[win_sb.sub(kh)])
            S.dma("pool", wglu_sb.t[:], wglu.rearrange("(kt p) n -> p kt n", p=128), writes=[wglu_sb.b])

        ident = cst.t[:, C_ID:C_ID + 128]
        S.dma("sp", cst.t[:], cst_d, writes=[cst.b])
        S.dma("sp", prm.t[:], prm_d, writes=[prm.b])
        S.op("act", lambda e: e.activation(out=identb.t[:], in_=ident, func=AF.Copy), reads=[cst.b], writes=[identb.b])
        S.op("dve", lambda e: e.memset(onesf.t[:], 1.0), writes=[onesf.b])

        def chunkmod(i):
            return mod.t[:, 8 * i:8 * i + 8, :]

        ssd8 = A.alloc("ssd8", [8, 4], F32)
        S.op("act", lambda e: e.activation(out=ssd8.t[:, 1:2], in_=prm.t[0:8, P_SSD8 + 1:P_SSD8 + 2], func=AF.Exp),
             reads=[prm.b], writes=[ssd8.b])
        S.op("dve", lambda e: e.tensor_scalar(out=ssd8.t[:, 1:2], in0=ssd8.t[:, 1:2], scalar1=-1.0, scalar2=None, op0=ALU.mult),
             reads=[ssd8.b], writes=[ssd8.b])
        S.op("dve", lambda e: e.tensor_copy(out=ssd8.t[:, 0:1], in_=prm.t[0:8, P_SSD8:P_SSD8 + 1]), reads=[prm.b], writes=[ssd8.b])

        Ptab = A.alloc("Ptab", [128, 2, 16, T5], F32)
        Qtab = A.alloc("Qtab", [128, 2, 16, T5], F32)
        s5t = [A.alloc("s5t%d" % i, [128, 512], F32) for i in range(2)]

        def alias(name, ap, buf):
            tl = TL(ap, name)
            tl.b = buf
            return tl
        sw = alias("s5work", s5t[1].t[:, 0:384].rearrange("p (a b) -> p a b", b=16), s5t[1].b)
        tmpA = alias("tmpA", s5t[0].t[:, 0:256].rearrange("p (a b) -> p a b", b=T5 // 2), s5t[0].b)
        tmpB = alias("tmpB", s5t[0].t[:, 256:512].rearrange("p (a b) -> p a b", b=T5 // 2), s5t[0].b)
        mask32 = A.alloc("mask32", [128, 16, T5], BF16)
        s5v = [A.alloc("s5v%d" % i, [128, 512], F32) for i in range(2)]
        qtmp = alias("qtmp", s5v[0].t[:].rearrange("p (s t) -> p s t", t=T5), s5v[0].b)
        mask4 = A.alloc("mask4", [128, 128, LS], BF16)
        s5cr = A.alloc("s5cr", [128, 2, 16], F32)
        W = lambda i: sw.t[:, i, :]
        pv = lambda i: prm.t[:, P_S5P + 16 * i:P_S5P + 16 * (i + 1)]
        swb = [sw.b, prm.b]

        def dv(fn):
            S.op("dve", fn, reads=swb, writes=[sw.b])

        def act(fn):
            S.op("act", fn, reads=swb, writes=[sw.b])
        TT = lambda e, o, a, b, op: e.tensor_tensor(out=o, in0=a, in1=b, op=op)
        def exp_acc(dst, src):
            dv(lambda e: e.tensor_scalar(out=W(22), in0=src, scalar1=1.0 / 16, scalar2=None, op0=ALU.mult))
            dv(lambda e: e.tensor_scalar(out=dst, in0=W(22), scalar1=1.0 / 7, scalar2=1.0, op0=ALU.mult, op1=ALU.add))
            for k in (6, 5, 4, 3, 2, 1):
                dv(lambda e: TT(e, dst, dst, W(22), ALU.mult))
                dv(lambda e, k=k: e.tensor_scalar(out=dst, in0=dst, scalar1=1.0 / k, scalar2=1.0, op0=ALU.mult, op1=ALU.add))
            for _ in range(4):
                dv(lambda e: TT(e, dst, dst, dst, ALU.mult))
        exp_acc(W(0), pv(2))
        dv(lambda e: TT(e, W(1), pv(0), W(0), ALU.mult))
        dv(lambda e: TT(e, W(2), pv(1), W(0), ALU.mult))
        exp_acc(W(3), W(1))

        def range_reduce(dst, src, add):
            ki = A_ki
            dv(lambda e: e.tensor_scalar(out=W(20), in0=src, scalar1=float(add), scalar2=1.0 / (2 * PI), op0=ALU.add, op1=ALU.mult))
            S.op("dve", lambda e: e.tensor_copy(out=ki.t[:], in_=W(20)), reads=swb, writes=[ki.b])
            S.op("dve", lambda e: e.tensor_copy(out=W(21), in_=ki.t[:]), reads=[ki.b], writes=[sw.b])
            dv(lambda e: e.tensor_scalar(out=W(20), in0=src, scalar1=float(add), scalar2=None, op0=ALU.add))
            dv(lambda e: e.scalar_tensor_tensor(out=dst, in0=W(21), scalar=-2 * PI, in1=W(20), op0=ALU.mult, op1=ALU.add))
            dv(lambda e: e.tensor_scalar(out=dst, in0=dst, scalar1=PI, scalar2=-PI, op0=ALU.min, op1=ALU.max))
        A_ki = A.alloc("s5ki", [128, 16], I32)
        range_reduce(W(4), W(2), 0.0)
        range_reduce(W(5), W(2), PI / 2)
        act(lambda e: e.activation(out=W(6), in_=W(4), func=AF.Sin))
        act(lambda e: e.activation(out=W(7), in_=W(5), func=AF.Sin))
        dv(lambda e: TT(e, W(8), W(3), W(7), ALU.mult))
        dv(lambda e: TT(e, W(9), W(3), W(6), ALU.mult))
        dv(lambda e: e.tensor_scalar(out=W(10), in0=W(8), scalar1=-1.0, scalar2=None, op0=ALU.add))
        dv(lambda e: TT(e, W(11), pv(0), pv(0), ALU.mult))
        dv(lambda e: TT(e, W(12), pv(1), pv(1), ALU.mult))
        dv(lambda e: TT(e, W(11), W(11), W(12), ALU.add))
        dv(lambda e: e.reciprocal(out=W(11), in_=W(11)))
        dv(lambda e: TT(e, W(12), W(10), pv(0), ALU.mult))
        dv(lambda e: TT(e, W(13), W(9), pv(1), ALU.mult))
        dv(lambda e: TT(e, W(12), W(12), W(13), ALU.add))
        dv(lambda e: TT(e, W(14), W(12), W(11), ALU.mult))
        dv(lambda e: TT(e, W(12), W(9), pv(0), ALU.mult))
        dv(lambda e: TT(e, W(13), W(10), pv(1), ALU.mult))
        dv(lambda e: TT(e, W(12), W(12), W(13), ALU.subtract))
        dv(lambda e: TT(e, W(15), W(12), W(11), ALU.mult))
        dv(lambda e: TT(e, W(12), W(8), W(8), ALU.mult))
        dv(lambda e: TT(e, W(13), W(9), W(9), ALU.mult))
        dv(lambda e: TT(e, W(12), W(12), W(13), ALU.add))
        dv(lambda e: e.reciprocal(out=W(12), in_=W(12)))
        dv(lambda e: TT(e, W(16), W(8), W(12), ALU.mult))
        dv(lambda e: e.scalar_tensor_tensor(out=W(17), in0=W(9), scalar=-1.0, in1=W(12), op0=ALU.mult, op1=ALU.mult))

        def build_pow(tab, br, bi):
            tb = [tab.b, sw.b, tmpA.b, tmpB.b]
            S.op("dve", lambda e: e.tensor_copy(out=tab.t[:, 0, :, 0], in_=br), reads=tb, writes=[tab.b])
            S.op("dve", lambda e: e.tensor_copy(out=tab.t[:, 1, :, 0], in_=bi), reads=tb, writes=[tab.b])
            n = 1
            while n < T5:
                ar, ai = tab.t[:, 0, :, 0:n], tab.t[:, 1, :, 0:n]
                sr = tab.t[:, 0, :, n - 1:n].to_broadcast([128, 16, n])
                si = tab.t[:, 1, :, n - 1:n].to_broadcast([128, 16, n])
                tA, tB = tmpA.t[:, :, 0:n], tmpB.t[:, :, 0:n]
                orr, oi = tab.t[:, 0, :, n:2 * n], tab.t[:, 1, :, n:2 * n]
                ops = [(tA, ar, sr, ALU.mult), (tB, ai, si, ALU.mult), (orr, tA, tB, ALU.subtract),
                       (tA, ar, si, ALU.mult), (tB, ai, sr, ALU.mult), (oi, tA, tB, ALU.add)]
                for (o, a, b, op) in ops:
                    S.op("dve", lambda e, o=o, a=a, b=b, op=op: TT(e, o, a, b, op), reads=tb, writes=tb[0:1] + tb[2:4])
                n *= 2
        build_pow(Ptab, W(8), W(9))
        build_pow(Qtab, W(16), W(17))
        tq = [Qtab.b, sw.b, tmpA.b, tmpB.b]
        for half in range(2):
            hs = slice(half * (T5 // 2), (half + 1) * (T5 // 2))
            qr, qi = Qtab.t[:, 0, :, hs], Qtab.t[:, 1, :, hs]
            fr = W(14).unsqueeze(2).to_broadcast([128, 16, T5 // 2])
            fi = W(15).unsqueeze(2).to_broadcast([128, 16, T5 // 2])
            ops = [(tmpA.t[:], qr, fr, ALU.mult), (tmpB.t[:], qi, fi, ALU.mult), ("R", tmpA.t[:], tmpB.t[:], ALU.subtract),
                   (tmpA.t[:], qr, fi, ALU.mult), (tmpB.t[:], qi, fr, ALU.mult), (qi, tmpA.t[:], tmpB.t[:], ALU.add)]
            for (o, a, b, op) in ops:
                if isinstance(o, str):
                    o = qtmp.t[:, :, hs]
                S.op("dve", lambda e, o=o, a=a, b=b, op=op: TT(e, o, a, b, op), reads=tq + [qtmp.b], writes=tq + [qtmp.b])
            S.op("dve", lambda e, qr=qr, hs=hs: e.tensor_copy(out=qr, in_=qtmp.t[:, :, hs]), reads=[qtmp.b], writes=[Qtab.b])
        S.op("dve", lambda e: e.memset(mask32.t[:], 1.0), reads=[Qtab.b], writes=[mask32.b])
        S.op("dve", lambda e: e.memset(mask32.t[:, :, 0:1], 0.0), writes=[mask32.b])
        S.op("dve", lambda e: e.memset(mask4.t[:], 1.0), writes=[mask4.b])
        S.op("dve", lambda e: e.memset(mask4.t[:, :, 0:1], 0.0), writes=[mask4.b])
        S.op("dve", lambda e: e.memset(s5cr.t[:], 0.0), writes=[s5cr.b])
        dump("Ptab", Ptab.t[:].rearrange("p a s t -> p (a s t)"), [128, 2 * 16 * T5], [Ptab.b])
        dump("Qtab", Qtab.t[:].rearrange("p a s t -> p (a s t)"), [128, 2 * 16 * T5], [Qtab.b])

        LO_W = A.lo
        cs = A.alloc("cs", [17, D], F32)
        slabs = [A.alloc("adaslab%d" % i, [128, 8, 512], BF16) for i in range(3)]
        S.dma("sp", cs.t[:], cin, writes=[cs.b])
        S.op("act", lambda e: e.activation(out=cs.t[:], in_=cs.t[:], func=AF.Silu), reads=[cs.b], writes=[cs.b])
        for kt in range(8):
            S.op("pe", lambda e, kt=kt: e.transpose(PB[2].t[:, kt * 17:(kt + 1) * 17], cs.t[:, kt * 128:(kt + 1) * 128],
                                                    cst.t[0:17, C_ID:C_ID + 17]),
                 reads=[cs.b, cst.b], writes=[PB[2].b])
        S.op("act", lambda e: e.activation(out=scT.t[:].rearrange("p k s -> p (k s)"), in_=PB[2].t[:, 0:136], func=AF.Copy),
             reads=[PB[2].b], writes=[scT.b])
        wada_v = wada.rearrange("(kt p) n -> p kt n", p=128)
        wadaf_v = wadaf.rearrange("(kt p) n -> p kt n", p=128)

        def slab_src(i):
            if i < 12:
                return wada_v[:, :, i * 512:(i + 1) * 512]
            return wadaf_v[:, :, (i - 12) * 512:(i - 11) * 512]

        def load_slab(i):
            sl = slabs[i % 3]
            for kh in range(2):
                S.dma("pool", sl.t[:, 4 * kh:4 * kh + 4, :], slab_src(i)[:, 4 * kh:4 * kh + 4, :], writes=[sl.b])
        load_slab(0)
        load_slab(1)
        load_1a_weights()
        for i in range(4):
            if i + 2 < 4:
                load_slab(i + 2)
            sl = slabs[i % 3]
            pb = PB[i % 2]
            for fc in range(4):
                for kt in range(8):
                    S.op("pe", lambda e, fc=fc, kt=kt, sl=sl, pb=pb: e.matmul(
                        pb.t[:, fc * 17:(fc + 1) * 17], sl.t[:, kt, fc * 128:(fc + 1) * 128], scT.t[:, kt, :],
                        start=(kt == 0), stop=(kt == 7)), reads=[sl.b, scT.b], writes=[pb.b])
            S.op("dve", lambda e, i=i, pb=pb: e.tensor_tensor(
                out=mod.t[:, 4 * i:4 * i + 4, :], in0=pb.t[:, 0:68].rearrange("p (c s) -> p c s", s=17),
                in1=prm.t[:, P_BMOD + 4 * i:P_BMOD + 4 * i + 4].unsqueeze(2).to_broadcast([128, 4, 17]), op=ALU.add),
                reads=[pb.b, prm.b], writes=[mod.b])
        def make_amod(lst):
          for k, (sci, gi) in lst:
            S.op("dve", lambda e, k=k, sci=sci, gi=gi: e.scalar_tensor_tensor(
                out=amod.t[:, 8 * k:8 * k + 8, :], in0=chunkmod(sci), scalar=1.0,
                in1=prm.t[:, P_GAIN + 8 * gi:P_GAIN + 8 * gi + 8].unsqueeze(2).to_broadcast([128, 8, 17]),
                op0=ALU.add, op1=ALU.mult), reads=[mod.b, prm.b], writes=[amod.b])
        make_amod([(0, (1, 0))])
        dump("mod", mod.t[:].rearrange("p c s -> p (c s)"), [128, 64 * 17], [mod.b])
        S.barrier()
        S.emit()
        A.lo = LO_W

        MOD_SH1, MOD_G1, MOD_SH2, MOD_G2, MOD_SHF = 0, 2, 3, 5, 6

        def expand_mod(name, src_ap, srcbufs):
            t = A.alloc(name, [128, 8, 64], F32)
            S.op("dve", lambda e: e.tensor_copy(out=t.t[:].rearrange("p k (s b) -> p k s b", b=LS),
                                                in_=src_ap.unsqueeze(3).to_broadcast([128, 8, NS, LS])),
                 reads=srcbufs, writes=[t.b])
            return t

        LO_P1 = A.lo
        mixt = [A.alloc("mixt%d" % i, [128, 8, 256], BF16) for i in range(2)]
        mixdb = [Buf("mixd%d" % i) for i in range(9)]
        a1x = A.alloc("a1x", [128, 8, 64], F32)
        sh1x = A.alloc("sh1x", [128, 8, 64], F32)

        def fill_x(t, src_ap, srcbufs):
            S.op("dve", lambda e: e.tensor_copy(out=t.t[:].rearrange("p k (s b) -> p k s b", b=LS),
                                                in_=src_ap.unsqueeze(3).to_broadcast([128, 8, NS, LS])),
                 reads=srcbufs, writes=[t.b])
        adab = [TL(a1x.t[:].rearrange("p k t -> p (k t)").bitcast(BF16).rearrange("p (k c) -> p k c", c=128), "adab0"),
                TL(sh1x.t[:].rearrange("p k t -> p (k t)").bitcast(BF16).rearrange("p (k c) -> p k c", c=128), "adab1")]
        adab[0].b = a1x.b
        adab[1].b = sh1x.b
        ADA_CH = list(range(16, 64))

        def ada_load(ci):
            c = ADA_CH[ci]
            src = wada_v[:, :, c * 128:(c + 1) * 128] if c < 48 else wadaf_v[:, :, (c - 48) * 128:(c - 47) * 128]
            S.dma("pool", adab[ci % 2].t[:], src, writes=[adab[ci % 2].b])

        def ada_compute(ci):
            c = ADA_CH[ci]
            sl = adab[ci % 2]
            pb = next_pb()
            for kt in range(8):
                S.op("pe", lambda e, kt=kt: e.matmul(pb.t[:, 0:17], sl.t[:, kt, :], scT.t[:, kt, :], start=(kt == 0), stop=(kt == 7)),
                     reads=[sl.b, scT.b], writes=[pb.b])
            S.op("dve", lambda e: e.tensor_scalar(out=mod.t[:, c, :], in0=pb.t[:, 0:17], scalar1=prm.t[:, P_BMOD + c:P_BMOD + c + 1],
                                                  scalar2=None, op0=ALU.add), reads=[pb.b, prm.b], writes=[mod.b])
        ada_state = [0, 0]

        def ada_step():
            if ada_state[1] >= len(ADA_CH):
                return
            while ada_state[0] < min(len(ADA_CH), ada_state[1] + 2):
                ada_load(ada_state[0])
                ada_state[0] += 1
            ada_compute(ada_state[1])
            ada_state[1] += 1

        ckpt("setup0")
        NTM = 256
        xtm = A.alloc("xtm", [128, 2, D], F32)
        xn = A.alloc("xn", [128, 2, D], BF16)
        nstat = A.alloc("nstat", [128, 4], F32)
        uT = A.alloc("uT", [128, 8, NTM], BF16)
        xpad = A.alloc("xpad", [128, 8, NTM + 4], BF16)
        xtail = A.alloc("xtail", [128, 8, 64], F32)
        cvst = A.alloc("cvst", [128, 8, NS, 3], F32)
        S.dma("sp", cvst.t[:].rearrange("p c s k -> p (c s k)"), stconv_d, writes=[cvst.b])
        dgc = A.alloc("dgc", [128, 8, 4, 128], BF16)
        for ct_ in range(8):
            for k_ in range(4):
                S.op("act", lambda e, ct_=ct_, k_=k_: e.activation(
                    out=dgc.t[:, ct_, k_, :], in_=ident, func=AF.Copy,
                    scale=prm.t[:, P_CONV + 5 * ct_ + k_:P_CONV + 5 * ct_ + k_ + 1]), reads=[cst.b, prm.b], writes=[dgc.b])
        xsT = A.alloc("xsT", [128, 4, NTM], F32)
        BCT = A.alloc("BCT", [128, 4, NTM], BF16)
        szT = A.alloc("szT", [128, 4, NTM], BF16)
        u5Ts = [A.alloc("u5T%d" % i, [128, 4, NTM], BF16) for i in range(2)]
        dtT = A.alloc("dtT", [8, 2, NTM], F32)
        cacc = [A.alloc("cacc0", [128, NTM], F32)] * 2
        y5pre = A.alloc("y5pre", [128, 4, NTM], F32)
        g5 = A.alloc("g5", [128, 4, NTM], BF16)
        sgl2 = [A.alloc("sgl", [128, NTM], F32), A.alloc("sgl1", [128, NTM], F32)]
        dtm_l = [A.alloc("dtm%d" % i, [128, 16], F32) for i in range(2)]
        acs_l = [A.alloc("acs%d" % i, [128, 8], F32) for i in range(2)]
        dec_l = [A.alloc("dec%d" % i, [128, 8], F32) for i in range(2)]
        dtdec_l = [A.alloc("dtdec%d" % i, [128, 8], F32) for i in range(2)]
        Xtm = A.alloc("Xtm", [128, 8, 64], BF16)
        Xdec = A.alloc("Xdec", [128, 8, 64], BF16)
        Btm = A.alloc("Btm", [128, 2, 128], BF16)
        big1 = A.alloc("big1", [128, 8, 128], F32)
        big2 = A.alloc("big2", [128, 8, 128], F32)
        MT = A.alloc("MT", [128, 8, 128], BF16)
        eA = A.alloc("eA", [128, 8, 128], F32)
        CdT = A.alloc("CdT", [128, 8, 128], BF16)
        ST = A.alloc("ST", [128, 8, 64], F32)
        STb = A.alloc("STb", [128, 8, 64], BF16)
        sts5 = alias("sts5", ST.t[:].rearrange("p h q -> p (h q)").rearrange("p (a s q) -> p a s q", a=2, s=16), ST.b)
        yg = A.alloc("yg", [128, 4, 128], F32)
        ysq = alias("ysq", big1.t[:, 4:8, :], big1.b)
        rsb = A.alloc("rsb", [128, 2, 128], F32)
        ysqb = A.alloc("ysqb", [128, 4, 128], BF16)
        onesb1 = A.alloc("onesb1", [128, 128], BF16)
        S.op("dve", lambda e: e.memset(onesb1.t[:], 1.0), writes=[onesb1.b])
        h0n = [alias("h0n0", xtm.t[:, 1, 0:512].rearrange("p (a n) -> p a n", n=128), xtm.sub(1)),
               alias("h0n1", xtm.t[:, 0, 0:512].rearrange("p (a n) -> p a n", n=128), xtm.sub(0))]
        h0T = [A.alloc("h0T%d" % i, [128, 8, 64], BF16) for i in range(2)]
        Bj = [A.alloc("Bj%d" % i, [128, 2, 128], BF16) for i in range(2)]
        hn = [alias("hn0", xtm.t[:, 1, 512:1024].rearrange("p (a n) -> p a n", n=128), xtm.sub(1)),
              alias("hn1", xtm.t[:, 0, 512:1024].rearrange("p (a n) -> p a n", n=128), xtm.sub(0))]
        decfm = A.alloc("decfm", [128, 4, 16], F32)
        dAx = alias("dAx", big1.t[:, 0:4, :].rearrange("p a (b c) -> p (a b) c", c=64), big1.b)
        s5g = [[A.alloc("s5g%d%d" % (j, i), [128, 512], F32) for i in range(2)] for j in range(2)]
        s5t34 = [A.alloc("s5t%d" % i, [128, 512], F32) for i in (2, 3)]
        s5vb = [A.alloc("s5vb%d" % i, [128, 512], F32) for i in range(2)]
        s5k = [0]
        s5h = [[A.alloc("s5h%d%d" % (j, i), [128, 512], BF16) for i in range(4)] for j in range(2)]
        s5CTn = A.alloc("s5CTn", [128, 16, 32], BF16)
        s5c = A.alloc("s5c", [128, 4, 16], F32)
        busd = [[A.alloc("bus%d%d" % (j, i), [128, 512], F32) for i in range(2)] for j in range(2)]
        dg5 = A.alloc("dg5", [128, 4, 128], BF16)
        for q_ in range(4):
            S.op("act", lambda e, q_=q_: e.activation(out=dg5.t[:, q_, :], in_=ident, func=AF.Copy,
                                                      scale=prm.t[:, P_S5M + q_:P_S5M + q_ + 1]),
                 reads=[cst.b, prm.b], writes=[dg5.b])
        S.op("dve", lambda e: e.tensor_scalar(out=s5CT.t[:, 1], in0=s5CT.t[:, 1], scalar1=-1.0, scalar2=None, op0=ALU.mult),
             reads=[s5CT.b], writes=[s5CT.b])
        S.op("dve", lambda e: e.tensor_scalar(out=s5CTn.t[:], in0=s5CT.t[:, 0], scalar1=-1.0, scalar2=None, op0=ALU.mult),
             reads=[s5CT.b], writes=[s5CTn.b])
        print("arena after p1a allocs: lo=%d hi=%d (words)" % (A.lo, A.hi))

        S.op("dve", lambda e: e.memset(xpad.t[:, :, 0:3], 0.0), writes=[xpad.b])
        S.op("dve", lambda e: e.memset(ST.t[:], 0.0), writes=[ST.b])
        S.op("dve", lambda e: e.memset(STb.t[:], 0.0), writes=[STb.b])

        import os as _os3
        ENG_OUTROT = _os3.environ.get("K_OUTROT", "dve")
        ENG_ADDS = _os3.environ.get("K_ADDS", "dve")
        TILES_A = [(i * 256, 256, False) for i in range(8)] + [(SEQ, 64, True)]

        def load_x(ti):
            t0, NT, is_s = TILES_A[ti]
            for blk in range((NT + 127) // 128):
                rows = min(128, NT - blk * 128)
                S.dma("sp", xtm.t[0:rows, blk, :], xin[t0 + blk * 128:t0 + blk * 128 + rows, :], writes=[xtm.sub(blk)])

        a1 = lambda kt: amod.t[:, kt, 0:1]
        sh1 = lambda kt: mod.t[:, 8 * MOD_SH1 + kt, 0:1]
        cw = lambda ct, k: prm.t[:, P_CONV + 5 * ct + k:P_CONV + 5 * ct + k + 1]
        IN_CHUNKS = [("dt", 0, 1536, 8)] + [("z", i, i * 128, 128) for i in range(4)] + \
                    [("xbc", i, 512 + i * 128, 128) for i in range(8)] + [("u5", i, 1544 + i * 128, 128) for i in range(4)]

        load_x(0)
        pbi = [0]

        def next_pb():
            pbi[0] ^= 1
            return PB[pbi[0]]

        ckpt("pre")
        def chain1(ti):
            t0, NT, is_s = TILES_A[ti]
            u5T = u5Ts[ti % 2]
            nblk = (NT + 127) // 128
            T = 128 if not is_s else 64
            tri = cst.t[0:T, C_TRI:C_TRI + T] if not is_s else cst.t[0:T, C_TRI64:C_TRI64 + T]
            neg = cst.t[0:T, C_NEG:C_NEG + T] if not is_s else cst.t[0:T, C_NEG64:C_NEG64 + T]
            sego = onesf.t[0:T, 0:T] if not is_s else cst.t[0:T, C_SEG64:C_SEG64 + T]
            segi = cst.t[0:64, C_SEGI:C_SEGI + 16]

            def dt_prep(ck):
                c0 = ck * T
                cs_ = slice(c0, c0 + T)
                dtm, acs, dec, dtdec = dtm_l[ck], acs_l[ck], dec_l[ck], dtdec_l[ck]
                pc = 0 if ck == 0 else 480
                S.op("pe", lambda e: e.transpose(PB[4].t[0:T, pc:pc + 8], dtT.t[:, 0, cs_], cst.t[0:8, C_ID:C_ID + 8]),
                     reads=[dtT.b, cst.b], writes=[PB[4].sub("sm")])
                S.op("pe", lambda e: e.transpose(PB[4].t[0:T, pc + 8:pc + 16], dtT.t[:, 1, cs_], cst.t[0:8, C_ID:C_ID + 8]),
                     reads=[dtT.b, cst.b], writes=[PB[4].sub("sm")])
                S.op("act", lambda e: e.activation(out=dtm.t[0:T, :], in_=PB[4].t[0:T, pc:pc + 16], func=AF.Copy),
                     reads=[PB[4].sub("sm")], writes=[dtm.b])
                S.op("pe", lambda e: e.matmul(PB[4].t[0:T, pc + 16:pc + 24], tri, dtm.t[0:T, 8:16], start=True, stop=True),
                     reads=[dtm.b, cst.b], writes=[PB[4].sub("sm")])
                S.op("pe", lambda e: e.matmul(PB[4].t[0:T, pc + 24:pc + 32], sego, dtm.t[0:T, 8:16], start=True, stop=True),
                     reads=[dtm.b, cst.b, onesf.b], writes=[PB[4].sub("sm")])
                S.op("act", lambda e: e.activation(out=acs.t[0:T, :], in_=PB[4].t[0:T, pc + 16:pc + 24], func=AF.Copy),
                     reads=[PB[4].sub("sm")], writes=[acs.b])
                S.op("dve", lambda e: TT(e, dec.t[0:T, :], PB[4].t[0:T, pc + 24:pc + 32], acs.t[0:T, :], ALU.subtract),
                     reads=[PB[4].sub("sm"), acs.b], writes=[dec.b])
                S.op("act", lambda e: e.activation(out=dec.t[0:T, :], in_=dec.t[0:T, :], func=AF.Exp), reads=[dec.b], writes=[dec.b])
                S.op("dve", lambda e: TT(e, dtdec.t[0:T, :], dtm.t[0:T, 0:8], dec.t[0:T, :], ALU.mult),
                     reads=[dtm.b, dec.b], writes=[dtdec.b])
            for blk in range(nblk):
                rows = min(128, NT - blk * 128)
                xb = xtm.sub(blk)
                S.op("act", lambda e, blk=blk, rows=rows: e.activation(
                    out=xn.t[0:rows, blk, :], in_=xtm.t[0:rows, blk, :], func=AF.Square, accum_out=nstat.t[0:rows, blk:blk + 1]),
                    reads=[xb], writes=[xn.sub(blk), nstat.sub(blk)])
                S.op("act", lambda e, blk=blk, rows=rows: e.activation(
                    out=nstat.t[0:rows, 2 + blk:3 + blk], in_=nstat.t[0:rows, blk:blk + 1], func=AF.Ln, scale=1.0 / D, bias=EPS),
                    reads=[nstat.sub(blk)], writes=[nstat.sub(blk)])
                S.op("act", lambda e, blk=blk, rows=rows: e.activation(out=nstat.t[0:rows, 2 + blk:3 + blk],
                                                                        in_=nstat.t[0:rows, 2 + blk:3 + blk], func=AF.Exp, scale=-0.5),
                     reads=[nstat.sub(blk)], writes=[nstat.sub(blk)])
                S.op("act", lambda e, blk=blk, rows=rows: e.activation(
                    out=xn.t[0:rows, blk, :], in_=xtm.t[0:rows, blk, :], func=AF.Copy, scale=nstat.t[0:rows, 2 + blk:3 + blk]),
                    reads=[xb, nstat.sub(blk)], writes=[xn.sub(blk)])
            ckpt("Aa%d" % ti)
            if ti + 1 < len(TILES_A):
                load_x(ti + 1)
            ckpt("Ab%d" % ti)
            for kt in range(8):
                xb_ = 2 + (kt % 2)
                pslot = PB[xb_].b
                for blk in range(nblk):
                    rows = min(128, NT - blk * 128)
                    S.op("pe", lambda e, kt=kt, blk=blk, rows=rows: e.transpose(
                        pbf(xb_)[:, blk * 128:blk * 128 + rows],
                        xn.t[0:rows, blk, kt * 128:(kt + 1) * 128], identb.t[0:rows, 0:rows]),
                        reads=[xn.sub(blk), identb.b], writes=[pslot])
                src = pbf(xb_)[:, 0:NT]
                if not is_s:
                    S.op("act", lambda e, kt=kt, src=src: e.activation(out=uT.t[:, kt, 0:NT], in_=src, func=AF.Identity,
                                                                       scale=a1(kt), bias=sh1(kt)),
                         reads=[pslot, amod.b, mod.b], writes=[uT.sub(kt)])
                else:
                    S.op("dve", lambda e, kt=kt, src=src: TT(e, cacc[0].t[:, 0:NT], src, a1x.t[:, kt, :], ALU.mult),
                         reads=[pslot, a1x.b], writes=[cacc[0].b])
                    S.op("dve", lambda e, kt=kt: TT(e, uT.t[:, kt, 0:NT], cacc[0].t[:, 0:NT], sh1x.t[:, kt, :], ALU.add),
                         reads=[cacc[0].b, sh1x.b], writes=[uT.sub(kt)])
            ckpt("A%d" % ti)
            if ti == 0:
                dump("uT", uT.t[:].rearrange("p k t -> p (k t)"), [128, 8 * NTM], uT.allb())

            yield
            if is_s:
                xps = xpad.t[:, :, 0:NS * 7].rearrange("p c (s k) -> p c s k", k=7)
                S.op("act", lambda e: e.activation(out=xps[:, :, :, 0:3], in_=cvst.t[:], func=AF.Copy), reads=[cvst.b], writes=[xpad.b])
            for (kind, i, c0, M) in IN_CHUNKS:
                yield
                pb = next_pb()
                for kt in range(8):
                    S.op("pe", lambda e, kt=kt, c0=c0, M=M, pb=pb: e.matmul(
                        pb.t[0:M, 0:NT], win_sb.t[:, kt, c0:c0 + M], uT.t[:, kt, 0:NT], start=(kt == 0), stop=(kt == 7)),
                        reads=[win_sb.sub(kt // 2), uT.sub(kt)], writes=[pb.b])
                if kind == "z":
                    S.op("act", lambda e, i=i, pb=pb: e.activation(out=szT.t[:, i, 0:NT], in_=pb.t[:, 0:NT], func=AF.Silu),
                         reads=[pb.b], writes=[szT.b])
                elif kind == "xbc":
                    if not is_s:
                        S.op("act", lambda e, i=i, pb=pb: e.activation(out=xpad.t[:, i, 3:3 + NT], in_=pb.t[:, 0:NT], func=AF.Copy),
                             reads=[pb.b], writes=[xpad.b])
                        if ti == 7:
                            S.op("act", lambda e, i=i, pb=pb: e.activation(out=xtail.t[:, i, 0:3], in_=pb.t[:, NT - 3:NT], func=AF.Copy),
                                 reads=[pb.b], writes=[xtail.b])
                    else:
                        S.op("act", lambda e, i=i, pb=pb: e.activation(
                            out=xps[:, i, :, 3:7], in_=pb.t[:, 0:NT].rearrange("p (s k) -> p s k", k=LS), func=AF.Copy),
                            reads=[pb.b], writes=[xpad.b])
                        S.op("act", lambda e, i=i, pb=pb: e.activation(out=xtail.t[:, i, 0:NT], in_=pb.t[:, 0:NT], func=AF.Copy),
                             reads=[pb.b], writes=[xtail.b])
                elif kind == "dt":
                    S.op("act", lambda e, pb=pb: e.activation(out=dtT.t[:, 1, 0:NT], in_=pb.t[0:8, 0:NT], func=AF.Exp,
                                                              bias=ssd8.t[:, 0:1]), reads=[pb.b, ssd8.b], writes=[dtT.b])
                    S.op("act", lambda e: e.activation(out=dtT.t[:, 0, 0:NT], in_=dtT.t[:, 1, 0:NT], func=AF.Ln, bias=1.0),
                         reads=[dtT.b], writes=[dtT.b])
                    S.op("dve", lambda e: e.tensor_scalar(out=dtT.t[:, 1, 0:NT], in0=dtT.t[:, 0, 0:NT], scalar1=ssd8.t[:, 1:2],
                                                          scalar2=None, op0=ALU.mult), reads=[dtT.b, ssd8.b], writes=[dtT.b])
                    for ck_ in range(NT // T):
                        yield
                        dt_prep(ck_)
                else:
                    S.op("act", lambda e, i=i, pb=pb: e.activation(out=u5T.t[:, i, 0:NT], in_=pb.t[:, 0:NT], func=AF.Copy),
                         reads=[pb.b], writes=[u5T.b])

            ckpt("B%d" % ti)
            for ct in range(8):
                yield
                pb = next_pb()
                if not is_s:
                    xin_k = lambda k, ct=ct: xpad.t[:, ct, k:k + NT]
                    pbv = pb.t[:, 0:NT]
                    dst = xsT.t[:, ct, 0:NT] if ct < 4 else BCT.t[:, ct - 4, 0:NT]
                else:
                    xin_k = lambda k, ct=ct: xps[:, ct, :, k:k + LS]
                    pbv = pb.t[:, 0:NT].rearrange("p (s k) -> p s k", k=LS)
                    dst = (xsT.t[:, ct, 0:NT] if ct < 4 else BCT.t[:, ct - 4, 0:NT]).rearrange("p (s k) -> p s k", k=LS)
                for k in range(4):
                    S.op("pe", lambda e, k=k: e.matmul(pbv, dgc.t[:, ct, k, :], xin_k(k), start=(k == 0), stop=(k == 3)),
                         reads=[dgc.b, xpad.b], writes=[pb.b])
                S.op("act", lambda e: e.activation(out=dst, in_=pbv, func=AF.Silu, bias=cw(ct, 4)),
                     reads=[pb.b, prm.b], writes=[xsT.b if ct < 4 else BCT.b])
            ocv = o_conv.rearrange("p (c s k) -> p c s k", s=17, k=3)
            if is_s:
                S.op("act", lambda e: e.activation(out=cvst.t[:], in_=xtail.t[:].rearrange("p c (s k) -> p c s k", k=LS)[:, :, :, 1:4],
                                                   func=AF.Copy), reads=[xtail.b], writes=[cvst.b])
                S.dma("sp", ocv[:, :, 1:17, :], cvst.t[:], reads=[cvst.b], buf=cvst.b)
                outbufs.append(cvst.b)
            elif ti == 7:
                S.dma("sp", ocv[:, :, 0, :], xtail.t[:, :, 0:3], reads=[xtail.b], buf=xtail.b)
            if not is_s:
                S.op("dve", lambda e: e.tensor_copy(out=xpad.t[:, :, 0:3], in_=xpad.t[:, :, NT:NT + 3]),
                     reads=[xpad.b], writes=[xpad.b])
            if is_s:
                dump("xsS", xsT.t[:, :, 0:64], [128, 4, 64], [xsT.b])
                dump("ygS", yg.t[:, :, 0:64], [128, 4, 64], [yg.b])
            if ti == 0:
                dump("xsT", xsT.t[:].rearrange("p k t -> p (k t)"), [128, 4 * NTM], [xsT.b])
                dump("dtT", dtT.t[:].rearrange("p k t -> p (k t)"), [8, 2 * NTM], [dtT.b])

            ckpt("C%d" % ti)
            for ck in range(NT // T):
                c0 = ck * T
                cs_ = slice(c0, c0 + T)
                dtm, acs, dec, dtdec = dtm_l[ck], acs_l[ck], dec_l[ck], dtdec_l[ck]
                yield
                for pr in range(4):
                    S.op("pe", lambda e, pr=pr, cs_=cs_: e.transpose(PB[3].t[0:T, pr * 128:(pr + 1) * 128], xsT.t[:, pr, cs_], ident),
                         reads=[xsT.b, cst.b], writes=[PB[3].b])
                pxs = PB[3].t[0:T, :].rearrange("p (h q) -> p h q", q=64)
                for h in range(8):
                    S.op("act", lambda e, h=h: e.activation(out=Xtm.t[0:T, h, :], in_=pxs[:, h, :], func=AF.Copy, scale=dtm.t[0:T, h:h + 1]),
                         reads=[PB[3].b, dtm.b], writes=[Xtm.b])
                    S.op("act", lambda e, h=h: e.activation(out=Xdec.t[0:T, h, :], in_=pxs[:, h, :], func=AF.Copy, scale=dtdec.t[0:T, h:h + 1]),
                         reads=[PB[3].b, dtdec.b], writes=[Xdec.b])
                for g in range(2):
                    S.op("pe", lambda e, g=g, cs_=cs_: e.transpose(pbf(2)[0:T, g * 128:(g + 1) * 128], BCT.t[:, g, cs_], identb.t[:]),
                         reads=[BCT.b, identb.b], writes=[PB[2].sub(0)])
                S.op("act", lambda e: e.activation(out=Btm.t[0:T].rearrange("p g n -> p (g n)"), in_=pbf(2)[0:T, 0:256], func=AF.Copy),
                     reads=[PB[2].sub(0)], writes=[Btm.b])
                yield
                S.op("dve", lambda e: TT(e, big1.t[0:T, :, 0:T], tri.unsqueeze(1).to_broadcast([T, 8, T]),
                                         dtm.t[0:T, 8:16].unsqueeze(2).to_broadcast([T, 8, T]), ALU.mult),
                     reads=[cst.b, dtm.b], writes=[big1.b])
                for half in range(2):
                    S.op("pe", lambda e, half=half: e.matmul(
                        PB[3].t[:, 0:4 * T].rearrange("p (h l) -> p h l", l=T), onesf.t[0:T, :],
                        big1.t[0:T, 4 * half:4 * half + 4, 0:T], start=True, stop=True),
                        reads=[big1.b, onesf.b], writes=[PB[3].b])
                    yield
                    for h in range(4 * half, 4 * half + 4):
                        S.op("dve", lambda e, h=h: e.scalar_tensor_tensor(
                            out=big2.t[0:T, h, 0:T], in0=PB[3].t[0:T, (h % 4) * T:(h % 4 + 1) * T], scalar=acs.t[0:T, h:h + 1],
                            in1=neg, op0=ALU.subtract, op1=ALU.min), reads=[PB[3].b, acs.b, cst.b], writes=[big2.b])
                    S.op("act", lambda e, half=half: e.activation(
                        out=eA.t[:, 4 * half:4 * half + 4, 0:T], in_=PB[3].t[:, 0:4 * T].rearrange("p (h l) -> p h l", l=T),
                        func=AF.Exp), reads=[PB[3].b], writes=[eA.b])
                    yield
                S.op("act", lambda e: e.activation(out=big2.t[0:T, :, 0:T], in_=big2.t[0:T, :, 0:T], func=AF.Exp),
                     reads=[big2.b], writes=[big2.b])
                yield
                for g in range(2):
                    S.op("pe", lambda e, g=g, cs_=cs_: e.matmul(PB[4].t[0:T, 32 + g * 128:32 + g * 128 + T], BCT.t[:, g, cs_],
                                                                 BCT.t[:, 2 + g, cs_], start=True, stop=True),
                         reads=[BCT.b], writes=[PB[4].sub("cb")])
                cbv = PB[4].t[0:T, 32:288].rearrange("p (g l) -> p g l", l=128)[:, :, 0:T]
                S.op("dve", lambda e: TT(e, MT.t[0:T, :, 0:T].rearrange("p (g h) l -> p g h l", h=4),
                                         cbv.unsqueeze(2).to_broadcast([T, 2, 4, T]),
                                         big2.t[0:T, :, 0:T].rearrange("p (g h) l -> p g h l", h=4), ALU.mult),
                     reads=[PB[4].sub("cb"), big2.b], writes=[MT.b])
                yield
                S.op("pool", lambda e, cs_=cs_: TT(e, CdT.t[:, :, 0:T].rearrange("p (g h) l -> p g h l", h=4),
                                                   BCT.t[:, 2:4, cs_].unsqueeze(2).to_broadcast([128, 2, 4, T]),
                                                   eA.t[:, :, 0:T].rearrange("p (g h) l -> p g h l", h=4), ALU.mult),
                     reads=[BCT.b, eA.b], writes=[CdT.b])
                yield
                ypb = PB[7]
                if is_s:
                    S.op("dve", lambda e: e.tensor_copy(out=dAx.t[0:T], in_=dtm.t[0:T, 8:16].unsqueeze(2).to_broadcast([T, 8, 64])),
                         reads=[dtm.b], writes=[dAx.b])
                    for pr in range(4):
                        S.op("pe", lambda e, pr=pr: e.matmul(PB[4].t[:, 288 + pr * 16:288 + (pr + 1) * 16],
                                                             dAx.t[0:T, 2 * pr:2 * pr + 2, :], segi, start=True, stop=True),
                             reads=[dAx.b, cst.b], writes=[PB[4].sub("dec")])
                    S.op("act", lambda e: e.activation(out=decfm.t[:].rearrange("p a s -> p (a s)"), in_=PB[4].t[:, 288:352], func=AF.Exp),
                         reads=[PB[4].sub("dec")], writes=[decfm.b])
                    stv = stssd_d.rearrange("j (pr hl) p n -> j (hl p) pr n", hl=2)
                    osv = o_ssds.rearrange("j (pr hl) p n -> j (hl p) pr n", hl=2)
                    S.dma("act", h0n[0].t[:], stv[0], writes=[h0n[0].b])
                    for j in range(NS):
                        yield
                        jj = j % 2
                        if j + 1 < NS:
                            S.dma("act", h0n[1 - jj].t[:], stv[j + 1], writes=[h0n[1 - jj].b])
                        pbt = PB[jj]
                        for pr in range(4):
                            S.op("pe", lambda e, pr=pr, jj=jj, pbt=pbt: e.transpose(pbt.t[:, pr * 128:(pr + 1) * 128], h0n[jj].t[:, pr, :], ident),
                                 reads=[h0n[jj].b, cst.b], writes=[pbt.b])
                        S.op("act", lambda e, jj=jj, pbt=pbt: e.activation(out=h0T[jj].t[:].rearrange("p h q -> p (h q)"), in_=pbt.t[:, :], func=AF.Copy),
                             reads=[pbt.b], writes=[h0T[jj].b])
                        for h in range(8):
                            pr, hl = h // 2, h % 2
                            S.op("pe", lambda e, h=h, pr=pr, hl=hl, jj=jj, j=j: e.matmul(
                                ypb.t[64 * hl:64 * hl + 64, pr * T + LS * j:pr * T + LS * j + LS], h0T[jj].t[:, h, :],
                                CdT.t[:, h, LS * j:LS * j + LS], start=(j == 0 and pr == 0), stop=False, skip_group_check=True),
                                reads=[h0T[jj].b, CdT.b], writes=[ypb.b])
                        S.op("dve", lambda e, jj=jj, j=j: e.tensor_scalar(out=Bj[jj].t[0:T], in0=Btm.t[0:T], scalar1=segi[:, j:j + 1],
                                                                          scalar2=None, op0=ALU.mult),
                             reads=[Btm.b, cst.b], writes=[Bj[jj].b])
                        pby = PB[3]
                        for pr in range(4):
                            S.op("pe", lambda e, pr=pr, jj=jj, pby=pby: e.matmul(
                                pby.t[:, pr * 128:(pr + 1) * 128], Xdec.t[0:T, 2 * pr:2 * pr + 2, :], Bj[jj].t[0:T, pr // 2, :],
                                start=True, stop=True), reads=[Xdec.b, Bj[jj].b], writes=[pby.b])
                        S.op("dve", lambda e, jj=jj, j=j: TT(e, hn[jj].t[:], h0n[jj].t[:],
                                                             decfm.t[:, :, j:j + 1].to_broadcast([128, 4, 128]), ALU.mult),
                             reads=[h0n[jj].b, decfm.b], writes=[hn[jj].b])
                        S.op("dve", lambda e, jj=jj, pby=pby: TT(e, hn[jj].t[:], hn[jj].t[:],
                                                                 pby.t[:, :].rearrange("p (a n) -> p a n", n=128), ALU.add),
                             reads=[hn[jj].b, pby.b], writes=[hn[jj].b])
                        S.dma("sp", osv[j], hn[jj].t[:], reads=[hn[jj].b], buf=hn[jj].b)
                    outbufs.extend([hn[0].b, hn[1].b])
                for h in range(8):
                    pr, hl = h // 2, h % 2
                    out = ypb.t[64 * hl:64 * hl + 64, pr * T:(pr + 1) * T]
                    S.op("pe", lambda e, h=h, out=out, pr=pr: e.matmul(out, Xtm.t[0:T, h, :], MT.t[0:T, h, 0:T],
                                                                       start=(pr == 0 and not is_s), stop=is_s, skip_group_check=True),
                         reads=[Xtm.b, MT.b], writes=[ypb.b])
                    if not is_s:
                        S.op("pe", lambda e, h=h, out=out: e.matmul(out, STb.t[:, h, :], CdT.t[:, h, 0:T], start=False, stop=True,
                                                                    skip_group_check=True),
                             reads=[STb.b, CdT.b], writes=[ypb.b])
                yield
                for pr in range(4):
                    S.op("dve", lambda e, pr=pr, cs_=cs_: e.scalar_tensor_tensor(
                        out=yg.t[:, pr, 0:T], in0=xsT.t[:, pr, cs_], scalar=prm.t[:, P_SSDFM + pr:P_SSDFM + pr + 1],
                        in1=ypb.t[:, pr * T:(pr + 1) * T], op0=ALU.mult, op1=ALU.add),
                        reads=[xsT.b, prm.b, ypb.b], writes=[yg.b])
                S.op("dve", lambda e, cs_=cs_: TT(e, yg.t[:, :, 0:T], yg.t[:, :, 0:T], szT.t[:, :, cs_], ALU.mult),
                     reads=[yg.b, szT.b], writes=[yg.b])
                S.op("dve", lambda e: TT(e, ysqb.t[:, :, 0:T], yg.t[:, :, 0:T], yg.t[:, :, 0:T], ALU.mult),
                     reads=[yg.b], writes=[ysqb.b])
                for g in range(2):
                    for k in range(2):
                        S.op("pe", lambda e, g=g, k=k: e.matmul(PB[3].t[:, g * T:(g + 1) * T], onesb1.t[:], ysqb.t[:, 2 * g + k, 0:T],
                                                                start=(k == 0), stop=(k == 1)),
                             reads=[onesb1.b, ysqb.b], writes=[PB[3].b])
                S.op("act", lambda e: e.activation(out=rsb.t[:, :, 0:T], in_=PB[3].t[:, 0:2 * T].rearrange("p (g l) -> p g l", l=T),
                                                   func=AF.Ln, scale=1.0 / 256, bias=EPS), reads=[PB[3].b], writes=[rsb.b])
                S.op("act", lambda e: e.activation(out=rsb.t[:, :, 0:T], in_=rsb.t[:, :, 0:T], func=AF.Exp, scale=-0.5),
                     reads=[rsb.b], writes=[rsb.b])
                yield
                if not is_s:
                    for g in range(2):
                        S.op("pe", lambda e, g=g: e.matmul(PB[6].t[:, g * 256:(g + 1) * 256], Btm.t[0:T, g, :],
                                                           Xdec.t[0:T, 4 * g:4 * g + 4, :], start=True, stop=True),
                             reads=[Btm.b, Xdec.b], writes=[PB[6].b])
                    S.op("dve", lambda e: TT(e, ST.t[:], ST.t[:], eA.t[:, :, T - 1:T].to_broadcast([128, 8, 64]), ALU.mult),
                         reads=[ST.b, eA.b], writes=[ST.b])
                    S.op("dve", lambda e: TT(e, ST.t[:], ST.t[:], PB[6].t[:, :].rearrange("p (h q) -> p h q", q=64), ALU.add),
                         reads=[ST.b, PB[6].b], writes=[ST.b])
                    S.op("act", lambda e: e.activation(out=STb.t[:], in_=ST.t[:], func=AF.Copy), reads=[ST.b], writes=[STb.b])
                for pr in range(4):
                    S.op("dve", lambda e, pr=pr: e.scalar_tensor_tensor(
                        out=mixt[ti % 2].t[:, pr, c0:c0 + T], in0=yg.t[:, pr, 0:T],
                        scalar=prm.t[:, P_SSDFM + 4 + pr:P_SSDFM + 5 + pr], in1=rsb.t[:, pr // 2, 0:T], op0=ALU.mult, op1=ALU.mult),
                        reads=[yg.b, prm.b, rsb.b], writes=[mixt[ti % 2].sub("ssd")])
            if ti == 7:
                S.dma("sp", o_ssdp, ST.t[:].rearrange("p h q -> p (h q)"), reads=[ST.b], buf=ST.b)
                outbufs.append(ST.b)

            ckpt("D%d" % ti)
            yield

        def chain2(ti):
            t0, NT, is_s = TILES_A[ti]
            u5T = u5Ts[ti % 2]
            if is_s:
                S.dma("sp", sts5.t[:].rearrange("p a s q -> p (a s q)"), sts5_d, writes=[sts5.b])
            if not is_s:
                groups = [(list(range(16)), k * T5, T5) for k in range(NT // T5)]
            else:
                groups = [(list(range(8)), 0, 64), (list(range(8, 16)), 0, 64)]
            def emit_bu(g_):
                slist_, tk0_, ntok_ = groups[g_]
                bus = busd[g_ % 2]
                for part, pb in ((0, PB[5]), (1, PB[6])):
                    for idx, s in enumerate(slist_):
                        S.op("pe", lambda e, part=part, pb=pb, idx=idx, s=s: e.matmul(
                            pb.t[:, idx * ntok_:(idx + 1) * ntok_], s5BT.t[:, part, s, :], u5T.t[:, s // 4, tk0_:tk0_ + ntok_],
                            start=True, stop=True), reads=[s5BT.b, u5T.b], writes=[pb.b])
                S.op("act", lambda e: e.activation(out=bus[0].t[:], in_=PB[5].t[:, :], func=AF.Copy), reads=[PB[5].b], writes=[bus[0].b])
                S.op("act", lambda e: e.activation(out=bus[1].t[:], in_=PB[6].t[:, :], func=AF.Copy), reads=[PB[6].b], writes=[bus[1].b])
            def views(g_):
                slist_, tk0_, ntok_ = groups[g_]
                s0_ = slist_[0]
                if not is_s:
                    V3 = lambda ap: ap.rearrange("p (s t) -> p s t", t=T5)
                    QR, QI = Qtab.t[:, 0], Qtab.t[:, 1]
                    PR_, PI_ = Ptab.t[:, 0], Ptab.t[:, 1]
                    msk = mask32.t[:].rearrange("p s t -> p (s t)")
                    first = lambda ap: V3(ap)[:, :, 0]
                    cin_r, cin_i = s5cr.t[:, 0, :], s5cr.t[:, 1, :]
                else:
                    V3 = lambda ap: ap.rearrange("p (s q b) -> p s q b", q=NS, b=LS)
                    bc = lambda ap: ap.unsqueeze(2).to_broadcast([128, 8, NS, LS])
                    QR, QI = bc(Qtab.t[:, 0, s0_:s0_ + 8, 0:LS]), bc(Qtab.t[:, 1, s0_:s0_ + 8, 0:LS])
                    PR_, PI_ = bc(Ptab.t[:, 0, s0_:s0_ + 8, 0:LS]), bc(Ptab.t[:, 1, s0_:s0_ + 8, 0:LS])
                    msk = mask4.t[:].rearrange("p s t -> p (s t)")
                    first = lambda ap: V3(ap)[:, :, :, 0]
                    cin_r, cin_i = sts5.t[:, 0, s0_:s0_ + 8, :], sts5.t[:, 1, s0_:s0_ + 8, :]
                return V3, QR, QI, PR_, PI_, msk, first, cin_r, cin_i
            vsets = [[s5v[0], s5v[1]], [s5vb[0], s5vb[1]]]

            def mults_adds(g_):
                V3, QR, QI, PR_, PI_, msk, first, cin_r, cin_i = views(g_)
                bus = busd[g_ % 2]
                br, bi = V3(bus[0].t[:]), V3(bus[1].t[:])
                t1, t2, t3, t4 = s5t[0], s5t[1], s5t34[0], s5t34[1]
                vr, vi = vsets[g_ % 2]
                tb = [Qtab.b]
                for (o, a, b_, rd) in ((t1, QR, br, bus[0].b), (t2, QI, bi, bus[1].b), (t3, QR, bi, bus[1].b), (t4, QI, br, bus[0].b)):
                    S.op("dve", lambda e, o=o, a=a, b_=b_: TT(e, V3(o.t[:]), a, b_, ALU.mult), reads=tb + [rd], writes=[o.b])
                S.op(ENG_ADDS, lambda e: TT(e, vr.t[:], t1.t[:], t2.t[:], ALU.subtract), reads=[t1.b, t2.b], writes=[vr.b])
                S.op(ENG_ADDS, lambda e: TT(e, vi.t[:], t3.t[:], t4.t[:], ALU.add), reads=[t3.b, t4.b], writes=[vi.b])
            emit_bu(0)
            if len(groups) > 1:
                emit_bu(1)
            mults_adds(0)
            pend_y5 = [None]
            for gi_, (slist, tk0, ntok) in enumerate(groups):
                yield
                ns = len(slist)
                s0 = slist[0]
                V3, QR, QI, PR_, PI_, msk, first, cin_r, cin_i = views(gi_)
                vr, vi = vsets[gi_ % 2]
                if gi_ + 1 < len(groups):
                    mults_adds(gi_ + 1)
                    yield
                if gi_ + 2 < len(groups):
                    emit_bu(gi_ + 2)
                S.op("dve", lambda e: TT(e, first(vr.t[:]), first(vr.t[:]), cin_r, ALU.add), reads=[vr.b, s5cr.b, sts5.b], writes=[vr.b])
                S.op("dve", lambda e: TT(e, first(vi.t[:]), first(vi.t[:]), cin_i, ALU.add), reads=[vi.b, s5cr.b, sts5.b], writes=[vi.b])
                yield
                s5k[0] ^= 1
                gr, gi2 = s5g[s5k[0]][0], s5g[s5k[0]][1]
                S.op("dve", lambda e: e.tensor_tensor_scan(out=gr.t[:], data0=msk, data1=vr.t[:], initial=0.0, op0=ALU.mult, op1=ALU.add),
                     reads=[vr.b, mask32.b, mask4.b], writes=[gr.b])
                S.op("dve", lambda e: e.tensor_tensor_scan(out=gi2.t[:], data0=msk, data1=vi.t[:], initial=0.0, op0=ALU.mult, op1=ALU.add),
                     reads=[vi.b, mask32.b, mask4.b], writes=[gi2.b])
                yield
                hp = s5h[gi_ % 2]
                hr, hi = hp, hp
                for (o, a, b_) in ((hp[0], PR_, gr), (hp[1], PI_, gi2), (hp[2], PR_, gi2), (hp[3], PI_, gr)):
                    S.op(ENG_OUTROT, lambda e, o=o, a=a, b_=b_: TT(e, V3(o.t[:]), a, V3(b_.t[:]), ALU.mult),
                         reads=[Ptab.b, b_.b], writes=[o.b])
                yield
                if not is_s:
                    glr, gli = V3(gr.t[:])[:, :, T5 - 1], V3(gi2.t[:])[:, :, T5 - 1]
                    plr, pli = Ptab.t[:, 0, :, T5 - 1], Ptab.t[:, 1, :, T5 - 1]
                    c_ = lambda i: s5c.t[:, i, :]
                    outr, outi = s5cr.t[:, 0, :], s5cr.t[:, 1, :]
                else:
                    glr, gli = V3(gr.t[:])[:, :, :, LS - 1], V3(gi2.t[:])[:, :, :, LS - 1]
                    plr = Ptab.t[:, 0, s0:s0 + 8, LS - 1:LS].to_broadcast([128, 8, NS])
                    pli = Ptab.t[:, 1, s0:s0 + 8, LS - 1:LS].to_broadcast([128, 8, NS])
                    c_ = lambda i: hn[0].t[:, i, :].rearrange("p (s q) -> p s q", q=NS)
                    outr, outi = s5fin.t[:, 0, s0:s0 + 8, 1:17], s5fin.t[:, 1, s0:s0 + 8, 1:17]
                cb_ = [s5c.b, hn[0].b]
                if not is_s:
                    pl2 = Ptab.t[:, :, :, T5 - 1]
                    ca, cb2 = s5c.t[:, 0:2, :], s5c.t[:, 2:4, :]
                    S.op("dve", lambda e: TT(e, ca, pl2, glr.unsqueeze(1).to_broadcast([128, 2, 16]), ALU.mult),
                         reads=[Ptab.b, gr.b] + cb_, writes=cb_)
                    S.op("dve", lambda e: TT(e, cb2, pl2, gli.unsqueeze(1).to_broadcast([128, 2, 16]), ALU.mult),
                         reads=[Ptab.b, gi2.b] + cb_, writes=cb_)
                    S.op("dve", lambda e: TT(e, outr, c_(0), c_(3), ALU.subtract), reads=cb_, writes=[s5cr.b, s5fin.b])
                    S.op("dve", lambda e: TT(e, outi, c_(2), c_(1), ALU.add), reads=cb_, writes=[s5cr.b, s5fin.b])
                else:
                    cseq = [(c_(0), plr, glr, ALU.mult), (c_(1), pli, gli, ALU.mult), (c_(2), plr, gli, ALU.mult), (c_(3), pli, glr, ALU.mult)]
                    for (o, a, b, op) in cseq:
                        S.op("dve", lambda e, o=o, a=a, b=b, op=op: TT(e, o, a, b, op), reads=[Ptab.b, gr.b, gi2.b] + cb_, writes=cb_)
                    S.op("dve", lambda e: TT(e, outr, c_(0), c_(1), ALU.subtract), reads=cb_, writes=[s5cr.b, s5fin.b])
                    S.op("dve", lambda e: TT(e, outi, c_(2), c_(3), ALU.add), reads=cb_, writes=[s5cr.b, s5fin.b])
                yield
                def emit_y5(gi_=gi_, slist=slist, tk0=tk0, ntok=ntok, hr=hr, hi=hi):
                    y5c0 = 352
                    nq = 4 if not is_s else 2
                    for qi in range(nq):
                        q = qi if not is_s else 2 * gi_ + qi
                        S.op("pe", lambda e, q=q, qi=qi: e.matmul(PB[4].t[:, y5c0 + qi * ntok:y5c0 + (qi + 1) * ntok], dg5.t[:, q, :],
                                                                  u5T.t[:, q, tk0:tk0 + ntok], start=(qi == 0), stop=False, skip_group_check=True),
                             reads=[dg5.b, u5T.b], writes=[PB[4].sub("y5")])
                    for idx, s in enumerate(slist):
                        qi = (s // 4) if not is_s else (s // 4 - 2 * gi_)
                        out = PB[4].t[32 * (s % 4):32 * (s % 4) + 32, y5c0 + qi * ntok:y5c0 + (qi + 1) * ntok]
                        for j4, lw in enumerate((s5CT.t[:, 0, s, :], s5CTn.t[:, s, :], s5CT.t[:, 1, s, :], s5CT.t[:, 1, s, :])):
                            S.op("pe", lambda e, j4=j4, lw=lw: e.matmul(out, lw, hr[j4].t[:, idx * ntok:(idx + 1) * ntok],
                                                                        start=False, stop=(j4 == 3), skip_group_check=True,
                                                                        tile_position=(0, 32 * (s % 4))),
                                 reads=[s5CT.b, s5CTn.b, hr[j4].b], writes=[PB[4].sub("y5")])
                    q0 = 0 if not is_s else 2 * gi_
                    S.op("act", lambda e: e.activation(out=y5pre.t[:, q0:q0 + nq, tk0:tk0 + ntok],
                                                       in_=PB[4].t[:, y5c0:y5c0 + nq * ntok].rearrange("p (q t) -> p q t", t=ntok), func=AF.Copy),
                         reads=[PB[4].sub("y5")], writes=[y5pre.b])
                if pend_y5[0] is not None:
                    pend_y5[0]()
                    yield
                pend_y5[0] = emit_y5
            if pend_y5[0] is not None:
                pend_y5[0]()
                pend_y5[0] = None
                yield
            if ti == 7:
                S.op("dve", lambda e: e.tensor_copy(out=s5fin.t[:, :, :, 0], in_=s5cr.t[:]), reads=[s5cr.b], writes=[s5fin.b])
            if is_s:
                S.dma("sp", o_s5, s5fin.t[:].rearrange("p a s q -> p (a s q)"), reads=[s5fin.b], buf=s5fin.b)
                outbufs.append(s5fin.b)
            if ti == 0:
                dump("y5pre", y5pre.t[:].rearrange("p k t -> p (k t)"), [128, 4 * NTM], [y5pre.b])
            ckpt("E%d" % ti)
            yield
            S.op("act", lambda e: e.activation(out=g5.t[:, :, 0:NT], in_=y5pre.t[:, :, 0:NT], func=AF.Gelu), reads=[y5pre.b], writes=[g5.b])
            pend_glu = [None]
            for m in range(4):
                yield
                pb = next_pb()
                for q in range(4):
                    S.op("pe", lambda e, m=m, q=q, pb=pb: e.matmul(pb.t[:, 0:NT], wglu_sb.t[:, q, m * 128:(m + 1) * 128], g5.t[:, q, 0:NT],
                                                                   start=(q == 0), stop=(q == 3)),
                         reads=[wglu_sb.b, g5.b], writes=[pb.b])
                sgl = sgl2[m % 2]
                S.op("act", lambda e, m=m, pb=pb: e.activation(out=sgl.t[:, 0:NT], in_=pb.t[:, 0:NT], func=AF.Sigmoid,
                                                               bias=prm.t[:, P_S5M + 4 + m:P_S5M + 5 + m]),
                     reads=[pb.b, prm.b], writes=[sgl.b])

                def glu_mul(m=m, sgl=sgl):
                    S.op("dve", lambda e: TT(e, mixt[ti % 2].t[:, 4 + m, 0:NT], g5.t[:, m, 0:NT], sgl.t[:, 0:NT], ALU.mult),
                         reads=[g5.b, sgl.b], writes=[mixt[ti % 2].sub("s5")])
                if pend_glu[0] is not None:
                    pend_glu[0]()
                pend_glu[0] = glu_mul
            pend_glu[0]()
            pend_glu[0] = None
            S.dma("sp", mixd[:, :, t0:t0 + NT], mixt[ti % 2].t[:, :, 0:NT], reads=mixt[ti % 2].allb(), writes=[mixdb[ti]], buf=mixdb[ti])
            ckpt("T%d" % ti)
            if ti == 0:
                dump("mix0", mixt[0].t[:, :, 0:NTM], [128, 8, NTM], mixt[0].allb())
            yield

        import os as _os
        RATIO = int(_os.environ.get("K_RATIO", "1"))
        HEAD = int(_os.environ.get("K_HEAD", "9"))
        HEADB = int(_os.environ.get("K_HEADB", "10"))

        def drive(gens, ada_every=0, head=0):
            gens = [g for g in gens if g is not None]
            n = 0
            if len(gens) > 1:
                for _ in range(head):
                    try:
                        next(gens[0])
                    except StopIteration:
                        gens.pop(0)
                        break
            while gens:
                for gi__, g in enumerate(list(gens)):
                    for _ in range((RATIO if gi__ == 0 else 1) if RATIO > 0 else (-RATIO if gi__ == 1 else 1)):
                        try:
                            next(g)
                        except StopIteration:
                            if g in gens:
                                gens.remove(g)
                            break
                n += 1
                if ada_every and n % ada_every == 0:
                    ada_step()
        ada_state[0] = 0
        drive([chain1(0)], ada_every=12)
        for ti_ in range(len(TILES_A)):
            if ti_ == 7:
                while ada_state[1] < len(ADA_CH):
                    ada_step()
                fill_x(a1x, amod.t[:, 0:8, 1:17], [amod.b])
                fill_x(sh1x, chunkmod(MOD_SH1)[:, :, 1:17], [mod.b])
                make_amod([(1, (4, 1)), (2, (7, 2))])
            drive([chain2(ti_), chain1(ti_ + 1) if ti_ + 1 < len(TILES_A) else None], ada_every=(8 if ti_ < 7 else 0), head=HEAD)
        dump("mixS", mixt[0].t[:, :, 0:64], [128, 8, 64], mixt[0].allb())
        S.barrier()
        ckpt("1a")
        A.lo = LO_GLOBAL
        x1T = A.alloc("x1T", [128, 8, NTOK], F32, top=True)
        vT = A.alloc("vT", [128, 8, NTOK], BF16, top=True)
        pre_g = [A.alloc("wgs%dt" % i, [128, 8, 256], BF16, top=True) for i in range(2)]
        pre_u = [A.alloc("wus%dt" % i, [128, 8, 256], BF16, top=True) for i in range(2)]
        wout_sb = A.alloc("wout_sb", [128, 8, D], BF16)
        wout_v = wout.rearrange("(kt p) n -> p kt n", p=128)
        for kh in range(4):
            S.dma("pool", wout_sb.t[:, 2 * kh:2 * kh + 2, :], wout_v[:, 2 * kh:2 * kh + 2, :], writes=[wout_sb.sub(kh)])
        mixb = [A.alloc("mixb%d" % i, [128, 8, 512], BF16) for i in range(2)]

        def load_mix(ti):
            t0, NT, is_s = TILES_B[ti]
            tiles_a = [i for i, (a0, n0, s0_) in enumerate(TILES_A) if a0 >= t0 and a0 < t0 + NT]
            S.dma("sp", mixb[ti % 2].t[:, :, 0:NT], mixd[:, :, t0:t0 + NT], reads=[mixdb[i] for i in tiles_a], writes=[mixb[ti % 2].b])
        wg_v = wg.rearrange("(kt p) n -> p kt n", p=128)
        wu_v = wu.rearrange("(kt p) n -> p kt n", p=128)
        for si_ in range(2):
            S.dma("pool", pre_g[si_].t[:], wg_v[:, :, si_ * 256:(si_ + 1) * 256], writes=[pre_g[si_].b])
            S.dma("pool", pre_u[si_].t[:], wu_v[:, :, si_ * 256:(si_ + 1) * 256], writes=[pre_u[si_].b])
        xtm2 = A.alloc("xtm2", [128, 4, D], F32)
        xTm = [A.alloc("xTm%d" % i, [128, 512], F32) for i in range(2)]
        sqb = [A.alloc("sqb%d" % i, [128, 512], BF16) for i in range(2)]
        onesb = A.alloc("onesb", [128, 128], BF16)
        S.op("dve", lambda e: e.memset(onesb.t[:], 1.0), writes=[onesb.b])
        tmp2 = [A.alloc("tmp2_%d" % i, [128, 512], F32) for i in range(2)]
        rstdb = [A.alloc("rstdb%d" % i, [128, 512], F32) for i in range(2)]
        g1x = expand_mod("g1x", chunkmod(MOD_G1)[:, :, 1:17], [mod.b])
        a2x = expand_mod("a2x", amod.t[:, 8:16, 1:17], [amod.b])
        sh2x = expand_mod("sh2x", chunkmod(MOD_SH2)[:, :, 1:17], [mod.b])
        print("arena p1b: lo=%d hi=%d" % (A.lo, A.hi))
        TILES_B = [(i * 512, 512, False) for i in range(4)] + [(SEQ, 64, True)]

        def load_x2(ti):
            t0, NT, is_s = TILES_B[ti]
            for blk in range((NT + 127) // 128):
                rows = min(128, NT - blk * 128)
                S.dma("sp", xtm2.t[0:rows, blk, :], xin[t0 + blk * 128:t0 + blk * 128 + rows, :], writes=[xtm2.sub(blk)])
        load_x2(0)
        load_mix(0)

        def stat_accum(src_ap, m, NT, pbs, defer=None):
            sq = sqb[m % 2]
            S.op("act", lambda e: e.activation(out=sq.t[:, 0:NT], in_=src_ap, func=AF.Square), reads=[x1T.sub(m)], writes=[sq.b])

            def mm(m=m, sq=sq):
                S.op("pe", lambda e: e.matmul(pbs.t[:, 0:NT], onesb.t[:], sq.t[:, 0:NT], start=(m == 0), stop=(m == 7)),
                     reads=[onesb.b, sq.b], writes=[pbs.b])
            if defer is None:
                mm()
            else:
                if defer[0] is not None:
                    defer[0]()
                defer[0] = mm
                if m == 7:
                    defer[0]()
                    defer[0] = None

        def stat_finish(NT, pbs, rs):
            S.op("act", lambda e: e.activation(out=rs.t[:, 0:NT], in_=pbs.t[:, 0:NT], func=AF.Ln, scale=1.0 / D, bias=EPS),
                 reads=[pbs.b], writes=[rs.b])
            S.op("act", lambda e: e.activation(out=rs.t[:, 0:NT], in_=rs.t[:, 0:NT], func=AF.Exp, scale=-0.5), reads=[rs.b], writes=[rs.b])

        def b_part1(ti):
            t0, NT, is_s = TILES_B[ti]
            nblk = (NT + 127) // 128
            tsl = slice(t0, t0 + NT)
            pbs = PB[4 + ti % 2]
            dfr = [None]
            for m in range(8):
                pbx = PB[2 + m % 2]
                xm = xTm[m % 2]
                for blk in range(nblk):
                    rows = min(128, NT - blk * 128)
                    S.op("pe", lambda e, blk=blk, rows=rows: e.transpose(
                        pbx.t[:, blk * 128:blk * 128 + rows], xtm2.t[0:rows, blk, m * 128:(m + 1) * 128], cst.t[0:rows, C_ID:C_ID + rows]),
                        reads=[xtm2.sub(blk), cst.b], writes=[pbx.b])
                S.op("act", lambda e: e.activation(out=xm.t[:, 0:NT], in_=pbx.t[:, 0:NT], func=AF.Copy), reads=[pbx.b], writes=[xm.b])
                pb = next_pb()
                for kt in range(8):
                    S.op("pe", lambda e, kt=kt: e.matmul(pb.t[:, 0:NT], wout_sb.t[:, kt, m * 128:(m + 1) * 128], mixb[ti % 2].t[:, kt, 0:NT],
                                                         start=(kt == 0), stop=(kt == 7)),
                         reads=[wout_sb.sub(kt // 2), mixb[ti % 2].b], writes=[pb.b])
                if m == 0 and ti + 1 < len(TILES_B):
                    load_mix(ti + 1)
                if not is_s:
                    S.op("dve", lambda e: e.scalar_tensor_tensor(
                        out=x1T.t[:, m, tsl], in0=pb.t[:, 0:NT], scalar=mod.t[:, 8 * MOD_G1 + m, 0:1], in1=xm.t[:, 0:NT],
                        op0=ALU.mult, op1=ALU.add), reads=[pb.b, mod.b, xm.b], writes=[x1T.sub(m)])
                else:
                    S.op("dve", lambda e: TT(e, tmp2[0].t[:, 0:NT], pb.t[:, 0:NT], g1x.t[:, m, :], ALU.mult),
                         reads=[pb.b, g1x.b], writes=[tmp2[0].b])
                    S.op("dve", lambda e: TT(e, x1T.t[:, m, tsl], tmp2[0].t[:, 0:NT], xm.t[:, 0:NT], ALU.add),
                         reads=[tmp2[0].b, xm.b], writes=[x1T.sub(m)])
                stat_accum(x1T.t[:, m, tsl], m, NT, pbs, defer=dfr)
                yield
            if ti + 1 < len(TILES_B):
                load_x2(ti + 1)
            yield

        def b_part2(ti):
            t0, NT, is_s = TILES_B[ti]
            tsl = slice(t0, t0 + NT)
            rs = rstdb[ti % 2]
            stat_finish(NT, PB[4 + ti % 2], rs)
            yield
            for m in range(8):
                tq = tmp2[m % 2]
                S.op("dve", lambda e: TT(e, tq.t[:, 0:NT], x1T.t[:, m, tsl], rs.t[:, 0:NT], ALU.mult),
                     reads=[x1T.sub(m), rs.b], writes=[tq.b])
                if not is_s:
                    S.op("act", lambda e: e.activation(out=vT.t[:, m, tsl], in_=tq.t[:, 0:NT], func=AF.Identity,
                                                       scale=amod.t[:, 8 + m, 0:1], bias=mod.t[:, 8 * MOD_SH2 + m, 0:1]),
                         reads=[tq.b, amod.b, mod.b], writes=[vT.sub(m)])
                else:
                    S.op("dve", lambda e: TT(e, tq.t[:, 0:NT], tq.t[:, 0:NT], a2x.t[:, m, :], ALU.mult),
                         reads=[tq.b, a2x.b], writes=[tq.b])
                    S.op("dve", lambda e: TT(e, vT.t[:, m, tsl], tq.t[:, 0:NT], sh2x.t[:, m, :], ALU.add),
                         reads=[tq.b, sh2x.b], writes=[vT.sub(m)])
                yield
            if ti == 0:
                dump("x1p", x1T.t[:, :, 0:256], [128, 8, 256], x1T.allb())
                dump("vp", vT.t[:, :, 0:256], [128, 8, 256], vT.allb())
        drive([b_part1(0)])
        for ti_ in range(len(TILES_B)):
            drive([b_part2(ti_), b_part1(ti_ + 1) if ti_ + 1 < len(TILES_B) else None], head=HEADB)
        S.barrier()
        ckpt("1b")

        A.lo = LO_GLOBAL
        tmp2 = [A.alloc("tmp3_%d" % i, [128, 512], F32) for i in range(2)]
        rstdb = [A.alloc("rstd3_%d" % i, [128, 512], F32) for i in range(2)]
        sqb = [A.alloc("sqb3_%d" % i, [128, 512], BF16) for i in range(2)]
        onesb = A.alloc("onesb3", [128, 128], BF16)
        S.op("dve", lambda e: e.memset(onesb.t[:], 1.0), writes=[onesb.b])
        g2x = expand_mod("g2x", chunkmod(MOD_G2)[:, :, 1:17], [mod.b])
        afx = expand_mod("afx", amod.t[:, 16:24, 1:17], [amod.b])
        shfx = expand_mod("shfx", chunkmod(MOD_SHF)[:, :, 1:17], [mod.b])
        LO_P2 = A.lo
        hT = A.alloc("hT", [128, 6, NTOK], BF16)
        wgs = pre_g + [A.alloc("wgs2", [128, 8, 256], BF16)]
        wus = pre_u + [A.alloc("wus2", [128, 8, 256], BF16)]
        wds = [A.alloc("wds%d" % i, [128, 6, D], BF16) for i in range(2)]
        sgt = [A.alloc("sgt%d" % i, [128, 512], BF16) for i in range(2)]
        print("arena p2: lo=%d hi=%d" % (A.lo, A.hi))
        wg_v = wg.rearrange("(kt p) n -> p kt n", p=128)
        wu_v = wu.rearrange("(kt p) n -> p kt n", p=128)
        wd_v = wd.rearrange("(j p) n -> p j n", p=128)
        QUARTERS = [(0, 6), (6, 12), (12, 18), (18, 22)]
        SLABS = [(q, ja + 2 * s) for q, (ja, jb) in enumerate(QUARTERS) for s in range((jb - ja) // 2)]

        def load_gu(si):
            q, j0 = SLABS[si]
            S.dma("pool", wgs[si % 3].t[:], wg_v[:, :, j0 * 128:(j0 + 2) * 128], writes=[wgs[si % 3].b])
            S.dma("pool", wus[si % 3].t[:], wu_v[:, :, j0 * 128:(j0 + 2) * 128], writes=[wus[si % 3].b])

        def load_wd(q):
            ja, jb = QUARTERS[q]
            for jh in range(0, jb - ja, 2):
                S.dma("pool", wds[q % 2].t[:, jh:jh + 2, :], wd_v[:, ja + jh:ja + jh + 2, :], writes=[wds[q % 2].b])
        assert SLABS[0][1] == 0 and SLABS[1][1] == 2
        load_wd(0)
        gbank = [0]
        si = 0
        for q, (ja, jb) in enumerate(QUARTERS):
            if q + 1 < 4:
                load_wd(q + 1)
            for s in range((jb - ja) // 2):
                if si + 2 < len(SLABS):
                    load_gu(si + 2)
                wgt, wut = wgs[si % 3], wus[si % 3]
                for jc in range(2):
                    jj = 2 * s + jc
                    for (t0, NT, is_s) in TILES_B:
                        tsl = slice(t0, t0 + NT)
                        gbank[0] ^= 1
                        pbg, pbu = PB[gbank[0]], PB[2 + gbank[0]]
                        for (wt, pb_) in ((wgt, pbg), (wut, pbu)):
                            for kt in range(8):
                                S.op("pe", lambda e, kt=kt, wt=wt, pb_=pb_: e.matmul(
                                    pb_.t[:, 0:NT], wt.t[:, kt, jc * 128:(jc + 1) * 128], vT.t[:, kt, tsl], start=(kt == 0), stop=(kt == 7)),
                                    reads=[wt.b] + vT.allb(), writes=[pb_.b])
                        sg_ = sgt[gbank[0]]
                        S.op("act", lambda e, pbg=pbg, sg_=sg_: e.activation(out=sg_.t[:, 0:NT], in_=pbg.t[:, 0:NT], func=AF.Silu),
                             reads=[pbg.b], writes=[sg_.b])
                        S.op("dve", lambda e, pbu=pbu, sg_=sg_: TT(e, hT.t[:, jj, tsl], sg_.t[:, 0:NT], pbu.t[:, 0:NT], ALU.mult),
                             reads=[sg_.b, pbu.b], writes=[hT.sub(jj)])
                si += 1
            nj = jb - ja
            wdt = wds[q % 2]
            for (t0, NT, is_s) in TILES_B:
                tsl = slice(t0, t0 + NT)
                for m in range(8):
                    pb = PB[4 + m % 2]
                    for jj in range(nj):
                        S.op("pe", lambda e, jj=jj, m=m, pb=pb: e.matmul(pb.t[:, 0:NT], wdt.t[:, jj, m * 128:(m + 1) * 128], hT.t[:, jj, tsl],
                                                                         start=(jj == 0), stop=(jj == nj - 1)),
                             reads=[wdt.b, hT.sub(jj)], writes=[pb.b])
                    if not is_s:
                        S.op("dve", lambda e, m=m, pb=pb: e.scalar_tensor_tensor(
                            out=x1T.t[:, m, tsl], in0=pb.t[:, 0:NT], scalar=mod.t[:, 8 * MOD_G2 + m, 0:1], in1=x1T.t[:, m, tsl],
                            op0=ALU.mult, op1=ALU.add), reads=[pb.b, mod.b, x1T.sub(m)], writes=[x1T.sub(m)])
                    else:
                        S.op("dve", lambda e, m=m, pb=pb: TT(e, tmp2[0].t[:, 0:NT], pb.t[:, 0:NT], g2x.t[:, m, :], ALU.mult),
                             reads=[pb.b, g2x.b], writes=[tmp2[0].b])
                        S.op("dve", lambda e, m=m: TT(e, x1T.t[:, m, tsl], tmp2[0].t[:, 0:NT], x1T.t[:, m, tsl], ALU.add),
                             reads=[tmp2[0].b, x1T.sub(m)], writes=[x1T.sub(m)])
        S.barrier()
        ckpt("ffn")
        A.lo = LO_P2
        yTs = [A.alloc("yT%d" % i, [128, 8, 512], F32) for i in range(2)]
        ytm = [A.alloc("ytm%d" % i, [128, D], F32) for i in range(2)]
        print("arena final: lo=%d hi=%d" % (A.lo, A.hi))
        oi = [0]

        def f_part1(ti):
            t0, NT, is_s = TILES_B[ti]
            tsl = slice(t0, t0 + NT)
            yT = yTs[ti % 2]
            pbs = PB[6 + ti % 2]
            rs = rstdb[ti % 2]
            for m in range(8):
                stat_accum(x1T.t[:, m, tsl], m, NT, pbs)
                if m % 2 == 1:
                    yield
            stat_finish(NT, pbs, rs)
            yield
            for m in range(8):
                tq = tmp2[m % 2]
                S.op("dve", lambda e: TT(e, tq.t[:, 0:NT], x1T.t[:, m, tsl], rs.t[:, 0:NT], ALU.mult),
                     reads=[x1T.sub(m), rs.b], writes=[tq.b])
                if not is_s:
                    S.op("act", lambda e: e.activation(out=yT.t[:, m, 0:NT], in_=tq.t[:, 0:NT], func=AF.Identity,
                                                       scale=amod.t[:, 16 + m, 0:1], bias=mod.t[:, 8 * MOD_SHF + m, 0:1]),
                         reads=[tq.b, amod.b, mod.b], writes=[yT.sub(m)])
                else:
                    S.op("dve", lambda e: TT(e, tq.t[:, 0:NT], tq.t[:, 0:NT], afx.t[:, m, :], ALU.mult),
                         reads=[tq.b, afx.b], writes=[tq.b])
                    S.op("dve", lambda e: TT(e, yT.t[:, m, 0:NT], tq.t[:, 0:NT], shfx.t[:, m, :], ALU.add),
                         reads=[tq.b, shfx.b], writes=[yT.sub(m)])
                yield

        def f_part2(ti):
            t0, NT, is_s = TILES_B[ti]
            yT = yTs[ti % 2]
            for blk in range((NT + 127) // 128):
                rows = min(128, NT - blk * 128)
                yo = ytm[oi[0] % 2]
                oi[0] += 1
                for half in range(2):
                    pbt = PB[half]
                    for k4 in range(4):
                        kt = 4 * half + k4
                        S.op("pe", lambda e, kt=kt, k4=k4: e.transpose(
                            pbt.t[0:rows, k4 * 128:(k4 + 1) * 128], yT.t[:, kt, blk * 128:blk * 128 + rows], ident),
                            reads=[yT.sub(kt), cst.b], writes=[pbt.b])
                    if half == 0:
                        S.op("act", lambda e: e.activation(out=yo.t[0:rows, 0:512], in_=pbt.t[0:rows, :], func=AF.Copy),
                             reads=[pbt.b], writes=[yo.b])
                    else:
                        S.op("dve", lambda e: e.tensor_copy(out=yo.t[0:rows, 512:1024], in_=pbt.t[0:rows, :]),
                             reads=[pbt.b], writes=[yo.b])
                    yield
                S.dma("sp", yout[t0 + blk * 128:t0 + blk * 128 + rows, :], yo.t[0:rows, :], reads=[yo.b], buf=yo.b)
        import os as _os2
        if True:
            for ti_ in range(len(TILES_B)):
                drive([f_part1(ti_)])
                drive([f_part2(ti_)])
        else:
            drive([f_part1(0)])
            for ti_ in range(len(TILES_B)):
                drive([f_part2(ti_), f_part1(ti_ + 1) if ti_ + 1 < len(TILES_B) else None])
        S.barrier()
    return nc, dumps


def _prep_inputs(inp):
    cstv = _consts()
    prmv = _params(inp)
    BT, CT = _s5mats(inp)
    maps = []
    for i in range(NCORES):
        m = {}
        m["xin"] = np.ascontiguousarray(np.concatenate(
            [inp["x_prompt"][i], inp["x_sample"][NS * i:NS * (i + 1)].reshape(NS * LS, D)], axis=0), dtype=np.float32)
        m["cin"] = np.ascontiguousarray(np.concatenate(
            [inp["c_prompt"][i:i + 1], inp["c_sample"][NS * i:NS * (i + 1)]], axis=0), dtype=np.float32)
        m["wada"] = np.ascontiguousarray(inp["w_ada"][0], dtype=np.float32)
        m["wadaf"] = np.ascontiguousarray(inp["w_ada_f"], dtype=np.float32)
        m["win"] = np.ascontiguousarray(inp["w_in"][0], dtype=np.float32)
        m["wglu"] = np.ascontiguousarray(inp["w_glu"][0], dtype=np.float32)
        m["wout"] = np.ascontiguousarray(inp["w_out"][0], dtype=np.float32)
        m["wg"] = np.ascontiguousarray(inp["w_ffn_gate"][0], dtype=np.float32)
        m["wu"] = np.ascontiguousarray(inp["w_ffn_up"][0], dtype=np.float32)
        m["wd"] = np.ascontiguousarray(inp["w_ffn_down"][0], dtype=np.float32)
        m["cst"] = cstv
        m["prm"] = prmv
        m["s5bt"] = BT.reshape(128, -1)
        m["s5ct"] = CT.reshape(128, -1)
        m["stssd"] = np.ascontiguousarray(inp["state_ssd"][0, NS * i:NS * (i + 1)], dtype=np.float32)
        sc = inp["state_conv"][0, NS * i:NS * (i + 1)]
        m["stconv"] = np.ascontiguousarray(
            sc.reshape(NS, 3, 8, 128).transpose(3, 2, 0, 1).reshape(128, -1), dtype=np.float32)
        sr = inp["state_s5_re"][0, NS * i:NS * (i + 1)]
        si = inp["state_s5_im"][0, NS * i:NS * (i + 1)]
        st = np.stack([sr, si], 0).reshape(2, NS, 16, 128).transpose(3, 0, 2, 1)
        m["sts5"] = np.ascontiguousarray(st.reshape(128, -1), dtype=np.float32)
        maps.append(m)
    return maps


_CACHE = {}


def kernel(**inputs):
    inp = {k: np.asarray(v) for k, v in inputs.items()}
    if "nc" not in _CACHE:
        _CACHE["nc"] = build()[0]
    nc = _CACHE["nc"]
    maps = _prep_inputs(inp)
    res = run_bass_kernel_spmd(nc, maps, core_ids=list(range(NCORES)))
    R = res.results
    y_p = np.stack([R[i]["yout"][:SEQ] for i in range(NCORES)], 0)
    y_s = np.concatenate([R[i]["yout"][SEQ:].reshape(NS, LS, D) for i in range(NCORES)], 0)
    ssd_p = np.stack([R[i]["o_ssdp"].reshape(128, 8, 64).transpose(1, 2, 0) for i in range(NCORES)], 0)[None]
    ssd_s = np.concatenate([R[i]["o_ssds"] for i in range(NCORES)], 0)[None]
    conv = [R[i]["o_conv"].reshape(128, 8, 17, 3).transpose(2, 3, 1, 0).reshape(17, 3, 1024) for i in range(NCORES)]
    conv_p = np.stack([c[0] for c in conv], 0)[None]
    conv_s = np.concatenate([c[1:] for c in conv], 0)[None]
    s5 = [R[i]["o_s5"].reshape(128, 2, 16, 17).transpose(1, 3, 2, 0).reshape(2, 17, 32, 64) for i in range(NCORES)]
    re_p = np.stack([s[0, 0] for s in s5], 0)[None]
    re_s = np.concatenate([s[0, 1:] for s in s5], 0)[None]
    im_p = np.stack([s[1, 0] for s in s5], 0)[None]
    im_s = np.concatenate([s[1, 1:] for s in s5], 0)[None]
    f = lambda a: np.ascontiguousarray(a, dtype=np.float32)
    return (f(y_p), f(y_s), f(ssd_p), f(ssd_s), f(conv_p), f(conv_s), f(re_p), f(re_s), f(im_p), f(im_s))
```

```python
import math
import numpy as np
from contextlib import ExitStack
import concourse.bass as bass
import concourse.mybir as mybir
from concourse.bass_utils import run_bass_kernel_spmd

F32 = mybir.dt.float32
BF16 = mybir.dt.bfloat16
I32 = mybir.dt.int32
AF = mybir.ActivationFunctionType
ALU = mybir.AluOpType

NCORES = 8
D = 1024
SEQ = 2048
NS = 16
LS = 4
NTOK = SEQ + NS * LS
DFF = 2816
NJ = DFF // 128
INP = 2056
EPS = 1e-6
T5 = 32
TILES = [(0, 512), (512, 512), (1024, 512), (1536, 512), (2048, 64)]
PI = math.pi


class Buf:
    def __init__(self, name):
        self.name = name
        self.w = None
        self.r = []
        self.dsem = None
        self.dcnt = 0


class TL:
    def __init__(self, t, name):
        self.t = t
        self.name = name
        self.b = Buf(name)
        self.subs = {}

    def sub(self, k):
        if getattr(self, "nosub", False):
            return self.b
        if k not in self.subs:
            self.subs[k] = Buf("%s_%s" % (self.name, k))
        return self.subs[k]

    def allb(self):
        return [self.b] + list(self.subs.values())

    def __getitem__(self, k):
        return self.t[k]


class Sched:
    ENG = ["pe", "act", "dve", "pool", "sp"]

    def __init__(self, nc, es):
        self.nc = nc
        self.es = es
        self.eobj = {"pe": nc.tensor, "act": nc.scalar, "dve": nc.vector, "pool": nc.gpsimd, "sp": nc.sync}
        self.cnt = {e: 0 for e in self.ENG}
        self.sem = {e: es.enter_context(nc.semaphore("s_" + e)) for e in self.ENG}
        self.seen = {e: {} for e in self.ENG}
        self.dbufs = []
        self.ninst = 0
        self.dead = False
        self.pe_pending = None

    def _flush_pe(self):
        if self.pe_pending is not None:
            self.pe_pending.then_inc(self.sem["pe"], 1)
            self.cnt["pe"] += 1
            self.pe_pending = None

    def _deps(self, eng, reads, writes, xreads=()):
        deps = []
        for b in reads:
            if b.w is not None:
                deps.append(b.w)
            if b in xreads:
                deps.extend(r for r in b.r if r[2] != eng)
        for b in writes:
            if b.w is not None:
                deps.append(b.w)
            deps.extend(b.r)
        waits = {}
        for (sem, val, key) in deps:
            if key == "pe" and eng == "pe":
                continue
            if self.seen[eng].get(key, 0) >= val:
                continue
            if key == "pe" and val > self.cnt["pe"]:
                self._flush_pe()
            if key not in waits or waits[key][1] < val:
                waits[key] = (sem, val)
        for key, (sem, val) in waits.items():
            self.seen[eng][key] = val
        return list(waits.values())

    def op(self, eng, fn, reads=(), writes=()):
        if self.dead:
            return None
        xr = [b for b in reads if getattr(b, "excl", False)]
        waits = self._deps(eng, reads, writes, xreads=xr)
        e = self.eobj[eng]
        for (s_, v_) in waits:
            e.wait_ge(s_, v_)
        if eng == "pe":
            self.pe_pending = fn(e)
            tok = (self.sem[eng], self.cnt[eng] + 1, eng)
        else:
            self.cnt[eng] += 1
            tok = (self.sem[eng], self.cnt[eng], eng)
            fn(e).then_inc(self.sem[eng], 1)
        for b in reads:
            b.r.append(tok)
        for b in writes:
            b.w = tok
            b.r = []
        self.ninst += 1
        return tok

    def dma(self, eng, out, in_, reads=(), writes=(), buf=None, **kw):
        if self.dead:
            return None
        waits = self._deps(eng, reads, writes)
        if buf is None:
            buf = writes[0] if writes else reads[0]
        if buf.dsem is None:
            buf.dsem = self.es.enter_context(self.nc.semaphore("d_" + buf.name))
            self.dbufs.append(buf)
        buf.dcnt += 16
        tok = (buf.dsem, buf.dcnt, "d_" + buf.name)
        e = self.eobj[eng]
        for (s_, v_) in waits:
            e.wait_ge(s_, v_)
        e.dma_start(out=out, in_=in_, **kw).then_inc(buf.dsem, 16)
        for b in reads:
            b.r.append(tok)
        for b in writes:
            b.w = tok
            b.r = []
        self.ninst += 1
        return tok

    def barrier(self):
        if self.dead:
            return
        self._flush_pe()
        for e in self.ENG:
            waits = []
            for o in self.ENG:
                if o != e and self.cnt[o] > self.seen[e].get(o, 0):
                    waits.append((self.sem[o], self.cnt[o]))
                    self.seen[e][o] = self.cnt[o]
            for b in self.dbufs:
                key = "d_" + b.name
                if b.dcnt > self.seen[e].get(key, 0):
                    waits.append((b.dsem, b.dcnt))
                    self.seen[e][key] = b.dcnt
            for (s_, v_) in waits:
                self.eobj[e].wait_ge(s_, v_)

    def emit(self):
        pass


C_ID = 0
C_TRI = 128
C_NEG = 256
C_TRI64 = 384
C_NEG64 = 512
C_SEG64 = 640
C_SEGI = 768
CST_W = 784

P_BMOD = 0
P_GAIN = 64
P_CONV = 88
P_SSDFM = 128
P_S5P = 136
P_S5M = 184
P_SSD8 = 192
PRM_W = 194


def _consts():
    c = np.zeros((128, CST_W), np.float32)
    c[:, C_ID:C_ID + 128] = np.eye(128, dtype=np.float32)
    s = np.arange(128)[:, None]
    l = np.arange(128)[None, :]
    c[:, C_TRI:C_TRI + 128] = (s <= l).astype(np.float32)
    c[:, C_NEG:C_NEG + 128] = np.where(l >= s, 0.0, -30000.0)
    same = (s // LS == l // LS) & (s < 64) & (l < 64)
    c[:, C_TRI64:C_TRI64 + 128] = ((s <= l) & same).astype(np.float32)
    c[:, C_NEG64:C_NEG64 + 128] = np.where((l >= s) & same, 0.0, -30000.0)
    c[:, C_SEG64:C_SEG64 + 128] = same.astype(np.float32)
    j = np.arange(16)[None, :]
    c[:, C_SEGI:C_SEGI + 16] = ((s // LS == j) & (s < 64)).astype(np.float32)
    return c


def _fm(v, nt):
    return np.ascontiguousarray(np.asarray(v, np.float32).reshape(nt, 128).T)


def _params(inp):
    p = np.zeros((128, PRM_W), np.float32)
    p[:, P_BMOD:P_BMOD + 48] = _fm(inp["b_ada"][0], 48)
    p[:, P_BMOD + 48:P_BMOD + 64] = _fm(inp["b_ada_f"], 16)
    p[:, P_GAIN:P_GAIN + 8] = _fm(inp["norm1_g"][0], 8)
    p[:, P_GAIN + 8:P_GAIN + 16] = _fm(inp["norm2_g"][0], 8)
    p[:, P_GAIN + 16:P_GAIN + 24] = _fm(inp["normf_g"], 8)
    cw = inp["conv_w"][0]
    cv = np.zeros((128, 8, 5), np.float32)
    for k in range(4):
        cv[:, :, k] = _fm(cw[k], 8)
    cv[:, :, 4] = _fm(inp["conv_b"][0], 8)
    p[:, P_CONV:P_CONV + 40] = cv.reshape(128, 40)
    Dh = inp["ssd_D"][0]
    dfm = np.zeros((128, 4), np.float32)
    for pr in range(4):
        dfm[0:64, pr] = Dh[2 * pr]
        dfm[64:128, pr] = Dh[2 * pr + 1]
    p[:, P_SSDFM:P_SSDFM + 4] = dfm
    p[:, P_SSDFM + 4:P_SSDFM + 8] = _fm(inp["ssd_norm_g"][0], 4)

    def st(a):
        return np.ascontiguousarray(np.asarray(a, np.float32).reshape(16, 128).T)
    p[:, P_S5P:P_S5P + 16] = st(inp["s5_A_re"][0])
    p[:, P_S5P + 16:P_S5P + 32] = st(inp["s5_A_im"][0])
    p[:, P_S5P + 32:P_S5P + 48] = st(np.repeat(inp["s5_log_step"][0][:, None], 64, axis=1))
    p[:, P_S5M:P_S5M + 4] = _fm(inp["s5_D"][0], 4)
    p[:, P_S5M + 4:P_S5M + 8] = _fm(inp["b_glu"][0], 4)
    p[0:8, P_SSD8] = inp["ssd_dt_bias"][0]
    p[0:8, P_SSD8 + 1] = inp["ssd_A_log"][0]
    return p


def _s5mats(inp):
    Br, Bi = inp["s5_B_re"][0], inp["s5_B_im"][0]
    Cr, Ci = inp["s5_C_re"][0], inp["s5_C_im"][0]
    BT = np.zeros((128, 2, 16, 128), np.float32)
    CT = np.zeros((128, 2, 16, 32), np.float32)
    for s in range(16):
        for gl in range(2):
            g = 2 * s + gl
            r0 = (g % 8) * 16
            BT[r0:r0 + 16, 0, s, gl * 64:(gl + 1) * 64] = Br[g].T
            BT[r0:r0 + 16, 1, s, gl * 64:(gl + 1) * 64] = Bi[g].T
            CT[gl * 64:(gl + 1) * 64, 0, s, gl * 16:(gl + 1) * 16] = Cr[g].T
            CT[gl * 64:(gl + 1) * 64, 1, s, gl * 16:(gl + 1) * 16] = Ci[g].T
    return BT, CT


class Arena:
    def __init__(self, nc, es, words):
        self.t = es.enter_context(nc.sbuf_tensor("arena", [128, words], F32))
        self.words = words
        self.lo = 0
        self.hi = words

    def alloc(self, name, shape, dt, top=False):
        n = 1
        for d in shape[1:]:
            n *= d
        w = n if dt == F32 or dt == I32 else (n + 1) // 2
        w = (w + 3) // 4 * 4
        if top:
            self.hi -= w
            off = self.hi
        else:
            off = self.lo
            self.lo += w
        assert self.lo <= self.hi, "arena overflow at %s: lo=%d hi=%d" % (name, self.lo, self.hi)
        ap = self.t[:, off:off + w]
        if dt != F32:
            ap = ap.bitcast(dt)
        ap = ap[:, 0:n]
        if len(shape) == 3:
            ap = ap.rearrange("p (a b) -> p a b", b=shape[2])
        elif len(shape) == 4:
            ap = ap.rearrange("p (a b c) -> p a b c", b=shape[2], c=shape[3])
        if shape[0] < 128:
            ap = ap[0:shape[0]]
        return TL(ap, name)


class StopBuild(Exception):
    pass


def build(dbg=None, stop_after=None):
    nc = bass.Bass("TRN2", target_bir_lowering=False)

    SH = []

    def ckpt(name):
        if stop_after == name:
            SH[0].barrier()
            SH[0].dead = True
    dt_in = lambda name, shape: nc.dram_tensor(name, list(shape), F32, kind="ExternalInput").ap()
    dt_out = lambda name, shape: nc.dram_tensor(name, list(shape), F32, kind="ExternalOutput").ap()
    xin = dt_in("xin", [NTOK, D])
    cin = dt_in("cin", [17, D])
    wada = dt_in("wada", [D, 6144])
    wadaf = dt_in("wadaf", [D, 2048])
    win = dt_in("win", [D, INP])
    wglu = dt_in("wglu", [512, 512])
    wout = dt_in("wout", [D, D])
    wg = dt_in("wg", [D, DFF])
    wu = dt_in("wu", [D, DFF])
    wd = dt_in("wd", [DFF, D])
    cst_d = dt_in("cst", [128, CST_W])
    prm_d = dt_in("prm", [128, PRM_W])
    s5bt_d = dt_in("s5bt", [128, 2 * 16 * 128])
    s5ct_d = dt_in("s5ct", [128, 2 * 16 * 32])
    stssd_d = dt_in("stssd", [NS, 8, 64, 128])
    stconv_d = dt_in("stconv", [128, 8 * NS * 3])
    sts5_d = dt_in("sts5", [128, 2 * 16 * NS])
    yout = dt_out("yout", [NTOK, D])
    o_ssdp = dt_out("o_ssdp", [128, 512])
    o_ssds = dt_out("o_ssds", [NS, 8, 64, 128])
    o_conv = dt_out("o_conv", [128, 8 * 17 * 3])
    o_s5 = dt_out("o_s5", [128, 2 * 16 * 17])
    mixd = nc.dram_tensor("mixd", [128, 8, NTOK], BF16, kind="Internal").ap()
    dumps = {}

    with ExitStack() as es:
        S = Sched(nc, es)
        NEED_CTN = []
        SH.append(S)
        A = Arena(nc, es, 53200)
        outbufs = []

        def dump(name, ap, shape, reads):
            if dbg is None or name not in dbg:
                return
            d = dt_out("dbg_" + name, shape)
            dumps[name] = shape
            b = Buf("dbg_" + name)
            S.dma("sp" if ap.dtype == F32 else "pool", d, ap, reads=reads, buf=b)
            outbufs.append(b)

        PB = [TL(es.enter_context(nc.psum_tensor("pb%d" % i, [128, 512], F32)), "pb%d" % i) for i in range(8)]
        for pb_ in PB:
            pb_.b.excl = True
            pb_.nosub = True

        def pbf(i):
            return PB[i].t[:].bitcast(BF16)

        cst = A.alloc("cst", [128, CST_W], F32)
        prm = A.alloc("prm", [128, PRM_W], F32)
        identb = A.alloc("identb", [128, 128], BF16)
        onesf = A.alloc("onesf", [128, 128], F32)
        mod = A.alloc("mod", [128, 64, 17], F32)
        amod = A.alloc("amod", [128, 24, 17], F32)
        s5fin = A.alloc("s5fin", [128, 2, 16, 17], F32)
        scT = A.alloc("scT", [128, 8, 17], BF16)
        LO_GLOBAL = A.lo
        win_sb = A.alloc("win_sb", [128, 8, INP], BF16)
        wglu_sb = A.alloc("wglu_sb", [128, 4, 512], BF16)
        s5BT = A.alloc("s5BT", [128, 2, 16, 128], BF16)
        s5CT = A.alloc("s5CT", [128, 2, 16, 32], BF16)
        LO_W = A.lo

        def load_1a_weights():
            for a_ in range(4):
                S.dma("pool", s5BT.t[:].rearrange("p a s c -> p (a s c)")[:, a_ * 1024:(a_ + 1) * 1024],
                      s5bt_d[:, a_ * 1024:(a_ + 1) * 1024], writes=[s5BT.b])
            S.dma("pool", s5CT.t[:].rearrange("p a s c -> p (a s c)"), s5ct_d, writes=[s5CT.b])
            win_v = win.rearrange("(kt p) n -> p kt n", p=128)
            for kh in range(4):
                for ch in range(2):
                    S.dma("pool", win_sb.t[:, 2 * kh:2 * kh + 2, ch * 1028:(ch + 1) * 1028],
                          win_v[:, 2 * kh:2 * kh + 2, ch * 1028:(ch + 1) * 1028], writes=[win_sb.sub(kh)])
            S.dma("pool", wglu_sb.t[:], wglu.rearrange("(kt p) n -> p kt n", p=128), writes=[wglu_sb.b])

        ident = cst.t[:, C_ID:C_ID + 128]
        S.dma("sp", cst.t[:], cst_d, writes=[cst.b])
        S.dma("sp", prm.t[:], prm_d, writes=[prm.b])
        S.op("act", lambda e: e.activation(out=identb.t[:], in_=ident, func=AF.Copy), reads=[cst.b], writes=[identb.b])
        S.op("dve", lambda e: e.memset(onesf.t[:], 1.0), writes=[onesf.b])

        def chunkmod(i):
            return mod.t[:, 8 * i:8 * i + 8, :]

        ssd8 = A.alloc("ssd8", [8, 4], F32)
        S.op("act", lambda e: e.activation(out=ssd8.t[:, 1:2], in_=prm.t[0:8, P_SSD8 + 1:P_SSD8 + 2], func=AF.Exp),
             reads=[prm.b], writes=[ssd8.b])
        S.op("dve", lambda e: e.tensor_scalar(out=ssd8.t[:, 1:2], in0=ssd8.t[:, 1:2], scalar1=-1.0, scalar2=None, op0=ALU.mult),
             reads=[ssd8.b], writes=[ssd8.b])
        S.op("dve", lambda e: e.tensor_copy(out=ssd8.t[:, 0:1], in_=prm.t[0:8, P_SSD8:P_SSD8 + 1]), reads=[prm.b], writes=[ssd8.b])

        Ptab = A.alloc("Ptab", [128, 2, 16, T5], F32)
        Qtab = A.alloc("Qtab", [128, 2, 16, T5], F32)
        s5t = [A.alloc("s5t%d" % i, [128, 512], F32) for i in range(2)]

        def alias(name, ap, buf):
            tl = TL(ap, name)
            tl.b = buf
            return tl
        sw = alias("s5work", s5t[1].t[:, 0:384].rearrange("p (a b) -> p a b", b=16), s5t[1].b)
        tmpA = alias("tmpA", s5t[0].t[:, 0:256].rearrange("p (a b) -> p a b", b=T5 // 2), s5t[0].b)
        tmpB = alias("tmpB", s5t[0].t[:, 256:512].rearrange("p (a b) -> p a b", b=T5 // 2), s5t[0].b)
        mask32 = A.alloc("mask32", [128, 16, T5], BF16)
        s5v = [A.alloc("s5v%d" % i, [128, 512], F32) for i in range(2)]
        qtmp = alias("qtmp", s5v[0].t[:].rearrange("p (s t) -> p s t", t=T5), s5v[0].b)
        mask4 = A.alloc("mask4", [128, 128, LS], BF16)
        s5cr = A.alloc("s5cr", [128, 2, 16], F32)
        W = lambda i: sw.t[:, i, :]
        pv = lambda i: prm.t[:, P_S5P + 16 * i:P_S5P + 16 * (i + 1)]
        swb = [sw.b, prm.b]

        def dv(fn):
            S.op("dve", fn, reads=swb, writes=[sw.b])

        def act(fn):
            S.op("act", fn, reads=swb, writes=[sw.b])
        TT = lambda e, o, a, b, op: e.tensor_tensor(out=o, in0=a, in1=b, op=op)
        def exp_acc(dst, src):
            dv(lambda e: e.tensor_scalar(out=W(22), in0=src, scalar1=1.0 / 16, scalar2=None, op0=ALU.mult))
            dv(lambda e: e.tensor_scalar(out=dst, in0=W(22), scalar1=1.0 / 7, scalar2=1.0, op0=ALU.mult, op1=ALU.add))
            for k in (6, 5, 4, 3, 2, 1):
                dv(lambda e: TT(e, dst, dst, W(22), ALU.mult))
                dv(lambda e, k=k: e.tensor_scalar(out=dst, in0=dst, scalar1=1.0 / k, scalar2=1.0, op0=ALU.mult, op1=ALU.add))
            for _ in range(4):
                dv(lambda e: TT(e, dst, dst, dst, ALU.mult))
        exp_acc(W(0), pv(2))
        dv(lambda e: TT(e, W(1), pv(0), W(0), ALU.mult))
        dv(lambda e: TT(e, W(2), pv(1), W(0), ALU.mult))
        exp_acc(W(3), W(1))

        def range_reduce(dst, src, add):
            ki = A_ki
            dv(lambda e: e.tensor_scalar(out=W(20), in0=src, scalar1=float(add), scalar2=1.0 / (2 * PI), op0=ALU.add, op1=ALU.mult))
            S.op("dve", lambda e: e.tensor_copy(out=ki.t[:], in_=W(20)), reads=swb, writes=[ki.b])
            S.op("dve", lambda e: e.tensor_copy(out=W(21), in_=ki.t[:]), reads=[ki.b], writes=[sw.b])
            dv(lambda e: e.tensor_scalar(out=W(20), in0=src, scalar1=float(add), scalar2=None, op0=ALU.add))
            dv(lambda e: e.scalar_tensor_tensor(out=dst, in0=W(21), scalar=-2 * PI, in1=W(20), op0=ALU.mult, op1=ALU.add))
            dv(lambda e: e.tensor_scalar(out=dst, in0=dst, scalar1=PI, scalar2=-PI, op0=ALU.min, op1=ALU.max))
        A_ki = A.alloc("s5ki", [128, 16], I32)
        range_reduce(W(4), W(2), 0.0)
        range_reduce(W(5), W(2), PI / 2)
        act(lambda e: e.activation(out=W(6), in_=W(4), func=AF.Sin))
        act(lambda e: e.activation(out=W(7), in_=W(5), func=AF.Sin))
        dv(lambda e: TT(e, W(8), W(3), W(7), ALU.mult))
        dv(lambda e: TT(e, W(9), W(3), W(6), ALU.mult))
        dv(lambda e: e.tensor_scalar(out=W(10), in0=W(8), scalar1=-1.0, scalar2=None, op0=ALU.add))
        dv(lambda e: TT(e, W(11), pv(0), pv(0), ALU.mult))
        dv(lambda e: TT(e, W(12), pv(1), pv(1), ALU.mult))
        dv(lambda e: TT(e, W(11), W(11), W(12), ALU.add))
        dv(lambda e: e.reciprocal(out=W(11), in_=W(11)))
        dv(lambda e: TT(e, W(12), W(10), pv(0), ALU.mult))
        dv(lambda e: TT(e, W(13), W(9), pv(1), ALU.mult))
        dv(lambda e: TT(e, W(12), W(12), W(13), ALU.add))
        dv(lambda e: TT(e, W(14), W(12), W(11), ALU.mult))
        dv(lambda e: TT(e, W(12), W(9), pv(0), ALU.mult))
        dv(lambda e: TT(e, W(13), W(10), pv(1), ALU.mult))
        dv(lambda e: TT(e, W(12), W(12), W(13), ALU.subtract))
        dv(lambda e: TT(e, W(15), W(12), W(11), ALU.mult))
        dv(lambda e: TT(e, W(12), W(8), W(8), ALU.mult))
        dv(lambda e: TT(e, W(13), W(9), W(9), ALU.mult))
        dv(lambda e: TT(e, W(12), W(12), W(13), ALU.add))
        dv(lambda e: e.reciprocal(out=W(12), in_=W(12)))
        dv(lambda e: TT(e, W(16), W(8), W(12), ALU.mult))
        dv(lambda e: e.scalar_tensor_tensor(out=W(17), in0=W(9), scalar=-1.0, in1=W(12), op0=ALU.mult, op1=ALU.mult))

        def build_pow(tab, br, bi):
            tb = [tab.b, sw.b, tmpA.b, tmpB.b]
            S.op("dve", lambda e: e.tensor_copy(out=tab.t[:, 0, :, 0], in_=br), reads=tb, writes=[tab.b])
            S.op("dve", lambda e: e.tensor_copy(out=tab.t[:, 1, :, 0], in_=bi), reads=tb, writes=[tab.b])
            n = 1
            while n < T5:
                ar, ai = tab.t[:, 0, :, 0:n], tab.t[:, 1, :, 0:n]
                sr = tab.t[:, 0, :, n - 1:n].to_broadcast([128, 16, n])
                si = tab.t[:, 1, :, n - 1:n].to_broadcast([128, 16, n])
                tA, tB = tmpA.t[:, :, 0:n], tmpB.t[:, :, 0:n]
                orr, oi = tab.t[:, 0, :, n:2 * n], tab.t[:, 1, :, n:2 * n]
                ops = [(tA, ar, sr, ALU.mult), (tB, ai, si, ALU.mult), (orr, tA, tB, ALU.subtract),
                       (tA, ar, si, ALU.mult), (tB, ai, sr, ALU.mult), (oi, tA, tB, ALU.add)]
                for (o, a, b, op) in ops:
                    S.op("dve", lambda e, o=o, a=a, b=b, op=op: TT(e, o, a, b, op), reads=tb, writes=tb[0:1] + tb[2:4])
                n *= 2
        build_pow(Ptab, W(8), W(9))
        build_pow(Qtab, W(16), W(17))
        tq = [Qtab.b, sw.b, tmpA.b, tmpB.b]
        for half in range(2):
            hs = slice(half * (T5 // 2), (half + 1) * (T5 // 2))
            qr, qi = Qtab.t[:, 0, :, hs], Qtab.t[:, 1, :, hs]
            fr = W(14).unsqueeze(2).to_broadcast([128, 16, T5 // 2])
            fi = W(15).unsqueeze(2).to_broadcast([128, 16, T5 // 2])
            ops = [(tmpA.t[:], qr, fr, ALU.mult), (tmpB.t[:], qi, fi, ALU.mult), ("R", tmpA.t[:], tmpB.t[:], ALU.subtract),
                   (tmpA.t[:], qr, fi, ALU.mult), (tmpB.t[:], qi, fr, ALU.mult), (qi, tmpA.t[:], tmpB.t[:], ALU.add)]
            for (o, a, b, op) in ops:
                if isinstance(o, str):
                    o = qtmp.t[:, :, hs]
                S.op("dve", lambda e, o=o, a=a, b=b, op=op: TT(e, o, a, b, op), reads=tq + [qtmp.b], writes=tq + [qtmp.b])
            S.op("dve", lambda e, qr=qr, hs=hs: e.tensor_copy(out=qr, in_=qtmp.t[:, :, hs]), reads=[qtmp.b], writes=[Qtab.b])
        S.op("dve", lambda e: e.memset(mask32.t[:], 1.0), reads=[Qtab.b], writes=[mask32.b])
        S.op("dve", lambda e: e.memset(mask32.t[:, :, 0:1], 0.0), writes=[mask32.b])
        S.op("dve", lambda e: e.memset(mask4.t[:], 1.0), writes=[mask4.b])
        S.op("dve", lambda e: e.memset(mask4.t[:, :, 0:1], 0.0), writes=[mask4.b])
        S.op("dve", lambda e: e.memset(s5cr.t[:], 0.0), writes=[s5cr.b])
        dump("Ptab", Ptab.t[:].rearrange("p a s t -> p (a s t)"), [128, 2 * 16 * T5], [Ptab.b])
        dump("Qtab", Qtab.t[:].rearrange("p a s t -> p (a s t)"), [128, 2 * 16 * T5], [Qtab.b])

        LO_W = A.lo
        cs = A.alloc("cs", [17, D], F32)
        slabs = [A.alloc("adaslab%d" % i, [128, 8, 512], BF16) for i in range(3)]
        S.dma("sp", cs.t[:], cin, writes=[cs.b])
        S.op("act", lambda e: e.activation(out=cs.t[:], in_=cs.t[:], func=AF.Silu), reads=[cs.b], writes=[cs.b])
        for kt in range(8):
            S.op("pe", lambda e, kt=kt: e.transpose(PB[2].t[:, kt * 17:(kt + 1) * 17], cs.t[:, kt * 128:(kt + 1) * 128],
                                                    cst.t[0:17, C_ID:C_ID + 17]),
                 reads=[cs.b, cst.b], writes=[PB[2].b])
        S.op("act", lambda e: e.activation(out=scT.t[:].rearrange("p k s -> p (k s)"), in_=PB[2].t[:, 0:136], func=AF.Copy),
             reads=[PB[2].b], writes=[scT.b])
        wada_v = wada.rearrange("(kt p) n -> p kt n", p=128)
        wadaf_v = wadaf.rearrange("(kt p) n -> p kt n", p=128)

        def slab_src(i):
            if i < 12:
                return wada_v[:, :, i * 512:(i + 1) * 512]
            return wadaf_v[:, :, (i - 12) * 512:(i - 11) * 512]

        def load_slab(i):
            sl = slabs[i % 3]
            for kh in range(2):
                S.dma("pool", sl.t[:, 4 * kh:4 * kh + 4, :], slab_src(i)[:, 4 * kh:4 * kh + 4, :], writes=[sl.b])
        load_slab(0)
        load_slab(1)
        load_1a_weights()
        for i in range(4):
            if i + 2 < 4:
                load_slab(i + 2)
            sl = slabs[i % 3]
            pb = PB[i % 2]
            for fc in range(4):
                for kt in range(8):
                    S.op("pe", lambda e, fc=fc, kt=kt, sl=sl, pb=pb: e.matmul(
                        pb.t[:, fc * 17:(fc + 1) * 17], sl.t[:, kt, fc * 128:(fc + 1) * 128], scT.t[:, kt, :],
                        start=(kt == 0), stop=(kt == 7)), reads=[sl.b, scT.b], writes=[pb.b])
            S.op("dve", lambda e, i=i, pb=pb: e.tensor_tensor(
                out=mod.t[:, 4 * i:4 * i + 4, :], in0=pb.t[:, 0:68].rearrange("p (c s) -> p c s", s=17),
                in1=prm.t[:, P_BMOD + 4 * i:P_BMOD + 4 * i + 4].unsqueeze(2).to_broadcast([128, 4, 17]), op=ALU.add),
                reads=[pb.b, prm.b], writes=[mod.b])
        def make_amod(lst):
          for k, (sci, gi) in lst:
            S.op("dve", lambda e, k=k, sci=sci, gi=gi: e.scalar_tensor_tensor(
                out=amod.t[:, 8 * k:8 * k + 8, :], in0=chunkmod(sci), scalar=1.0,
                in1=prm.t[:, P_GAIN + 8 * gi:P_GAIN + 8 * gi + 8].unsqueeze(2).to_broadcast([128, 8, 17]),
                op0=ALU.add, op1=ALU.mult), reads=[mod.b, prm.b], writes=[amod.b])
        make_amod([(0, (1, 0))])
        dump("mod", mod.t[:].rearrange("p c s -> p (c s)"), [128, 64 * 17], [mod.b])
        S.barrier()
        S.emit()
        A.lo = LO_W

        MOD_SH1, MOD_G1, MOD_SH2, MOD_G2, MOD_SHF = 0, 2, 3, 5, 6

        def expand_mod(name, src_ap, srcbufs):
            t = A.alloc(name, [128, 8, 64], F32)
            S.op("dve", lambda e: e.tensor_copy(out=t.t[:].rearrange("p k (s b) -> p k s b", b=LS),
                                                in_=src_ap.unsqueeze(3).to_broadcast([128, 8, NS, LS])),
                 reads=srcbufs, writes=[t.b])
            return t

        LO_P1 = A.lo
        mixt = [A.alloc("mixt%d" % i, [128, 8, 256], BF16) for i in range(2)]
        mixdb = [Buf("mixd%d" % i) for i in range(9)]
        a1x = A.alloc("a1x", [128, 8, 64], F32)
        sh1x = A.alloc("sh1x", [128, 8, 64], F32)

        def fill_x(t, src_ap, srcbufs):
            S.op("dve", lambda e: e.tensor_copy(out=t.t[:].rearrange("p k (s b) -> p k s b", b=LS),
                                                in_=src_ap.unsqueeze(3).to_broadcast([128, 8, NS, LS])),
                 reads=srcbufs, writes=[t.b])
        adab = [TL(a1x.t[:].rearrange("p k t -> p (k t)").bitcast(BF16).rearrange("p (k c) -> p k c", c=128), "adab0"),
                TL(sh1x.t[:].rearrange("p k t -> p (k t)").bitcast(BF16).rearrange("p (k c) -> p k c", c=128), "adab1")]
        adab[0].b = a1x.b
        adab[1].b = sh1x.b
        ADA_CH = list(range(16, 64))

        def ada_load(ci):
            c = ADA_CH[ci]
            src = wada_v[:, :, c * 128:(c + 1) * 128] if c < 48 else wadaf_v[:, :, (c - 48) * 128:(c - 47) * 128]
            S.dma("pool", adab[ci % 2].t[:], src, writes=[adab[ci % 2].b])

        def ada_compute(ci):
            c = ADA_CH[ci]
            sl = adab[ci % 2]
            pb = next_pb()
            for kt in range(8):
                S.op("pe", lambda e, kt=kt: e.matmul(pb.t[:, 0:17], sl.t[:, kt, :], scT.t[:, kt, :], start=(kt == 0), stop=(kt == 7)),
                     reads=[sl.b, scT.b], writes=[pb.b])
            S.op("dve", lambda e: e.tensor_scalar(out=mod.t[:, c, :], in0=pb.t[:, 0:17], scalar1=prm.t[:, P_BMOD + c:P_BMOD + c + 1],
                                                  scalar2=None, op0=ALU.add), reads=[pb.b, prm.b], writes=[mod.b])
        ada_state = [0, 0]

        def ada_step():
            if ada_state[1] >= len(ADA_CH):
                return
            while ada_state[0] < min(len(ADA_CH), ada_state[1] + 2):
                ada_load(ada_state[0])
                ada_state[0] += 1
            ada_compute(ada_state[1])
            ada_state[1] += 1

        ckpt("setup0")
        NTM = 256
        xtm = A.alloc("xtm", [128, 2, D], F32)
        xn = A.alloc("xn", [128, 2, D], BF16)
        nstat = A.alloc("nstat", [128, 4], F32)
        uT = A.alloc("uT", [128, 8, NTM], BF16)
        xpad = A.alloc("xpad", [128, 8, NTM + 4], BF16)
        xtail = A.alloc("xtail", [128, 8, 64], F32)
        cvst = A.alloc("cvst", [128, 8, NS, 3], F32)
        S.dma("sp", cvst.t[:].rearrange("p c s k -> p (c s k)"), stconv_d, writes=[cvst.b])
        dgc = A.alloc("dgc", [128, 8, 4, 128], BF16)
        for ct_ in range(8):
            for k_ in range(4):
                S.op("act", lambda e, ct_=ct_, k_=k_: e.activation(
                    out=dgc.t[:, ct_, k_, :], in_=ident, func=AF.Copy,
                    scale=prm.t[:, P_CONV + 5 * ct_ + k_:P_CONV + 5 * ct_ + k_ + 1]), reads=[cst.b, prm.b], writes=[dgc.b])
        xsT = A.alloc("xsT", [128, 4, NTM], F32)
        BCT = A.alloc("BCT", [128, 4, NTM], BF16)
        szT = A.alloc("szT", [128, 4, NTM], BF16)
        u5Ts = [A.alloc("u5T%d" % i, [128, 4, NTM], BF16) for i in range(2)]
        dtT = A.alloc("dtT", [8, 2, NTM], F32)
        cacc = [A.alloc("cacc0", [128, NTM], F32)] * 2
        y5pre = A.alloc("y5pre", [128, 4, NTM], F32)
        g5 = A.alloc("g5", [128, 4, NTM], BF16)
        sgl2 = [A.alloc("sgl", [128, NTM], F32), A.alloc("sgl1", [128, NTM], F32)]
        dtm_l = [A.alloc("dtm%d" % i, [128, 16], F32) for i in range(2)]
        acs_l = [A.alloc("acs%d" % i, [128, 8], F32) for i in range(2)]
        dec_l = [A.alloc("dec%d" % i, [128, 8], F32) for i in range(2)]
        dtdec_l = [A.alloc("dtdec%d" % i, [128, 8], F32) for i in range(2)]
        Xtm = A.alloc("Xtm", [128, 8, 64], BF16)
        Xdec = A.alloc("Xdec", [128, 8, 64], BF16)
        Btm = A.alloc("Btm", [128, 2, 128], BF16)
        big1 = A.alloc("big1", [128, 8, 128], F32)
        big2 = A.alloc("big2", [128, 8, 128], F32)
        MT = A.alloc("MT", [128, 8, 128], BF16)
        eA = A.alloc("eA", [128, 8, 128], F32)
        CdT = A.alloc("CdT", [128, 8, 128], BF16)
        ST = A.alloc("ST", [128, 8, 64], F32)
        STb = A.alloc("STb", [128, 8, 64], BF16)
        sts5 = alias("sts5", ST.t[:].rearrange("p h q -> p (h q)").rearrange("p (a s q) -> p a s q", a=2, s=16), ST.b)
        yg = A.alloc("yg", [128, 4, 128], F32)
        ysq = alias("ysq", big1.t[:, 4:8, :], big1.b)
        rsb = A.alloc("rsb", [128, 2, 128], F32)
        ysqb = A.alloc("ysqb", [128, 4, 128], BF16)
        onesb1 = A.alloc("onesb1", [128, 128], BF16)
        S.op("dve", lambda e: e.memset(onesb1.t[:], 1.0), writes=[onesb1.b])
        h0n = [alias("h0n0", xtm.t[:, 1, 0:512].rearrange("p (a n) -> p a n", n=128), xtm.sub(1)),
               alias("h0n1", xtm.t[:, 0, 0:512].rearrange("p (a n) -> p a n", n=128), xtm.sub(0))]
        h0T = [A.alloc("h0T%d" % i, [128, 8, 64], BF16) for i in range(2)]
        Bj = [A.alloc("Bj%d" % i, [128, 2, 128], BF16) for i in range(2)]
        hn = [alias("hn0", xtm.t[:, 1, 512:1024].rearrange("p (a n) -> p a n", n=128), xtm.sub(1)),
              alias("hn1", xtm.t[:, 0, 512:1024].rearrange("p (a n) -> p a n", n=128), xtm.sub(0))]
        decfm = A.alloc("decfm", [128, 4, 16], F32)
        dAx = alias("dAx", big1.t[:, 0:4, :].rearrange("p a (b c) -> p (a b) c", c=64), big1.b)
        s5g = [[A.alloc("s5g%d%d" % (j, i), [128, 512], F32) for i in range(2)] for j in range(2)]
        s5t34 = [A.alloc("s5t%d" % i, [128, 512], F32) for i in (2, 3)]
        s5vb = [A.alloc("s5vb%d" % i, [128, 512], F32) for i in range(2)]
        s5k = [0]
        s5h = [[A.alloc("s5h%d%d" % (j, i), [128, 512], BF16) for i in range(4)] for j in range(2)]
        s5CTn = A.alloc("s5CTn", [128, 16, 32], BF16)
        s5c = A.alloc("s5c", [128, 4, 16], F32)
        busd = [[A.alloc("bus%d%d" % (j, i), [128, 512], F32) for i in range(2)] for j in range(2)]
        dg5 = A.alloc("dg5", [128, 4, 128], BF16)
        for q_ in range(4):
            S.op("act", lambda e, q_=q_: e.activation(out=dg5.t[:, q_, :], in_=ident, func=AF.Copy,
                                                      scale=prm.t[:, P_S5M + q_:P_S5M + q_ + 1]),
                 reads=[cst.b, prm.b], writes=[dg5.b])
        S.op("dve", lambda e: e.tensor_scalar(out=s5CT.t[:, 1], in0=s5CT.t[:, 1], scalar1=-1.0, scalar2=None, op0=ALU.mult),
             reads=[s5CT.b], writes=[s5CT.b])
        S.op("dve", lambda e: e.tensor_scalar(out=s5CTn.t[:], in0=s5CT.t[:, 0], scalar1=-1.0, scalar2=None, op0=ALU.mult),
             reads=[s5CT.b], writes=[s5CTn.b])
        print("arena after p1a allocs: lo=%d hi=%d (words)" % (A.lo, A.hi))

        S.op("dve", lambda e: e.memset(xpad.t[:, :, 0:3], 0.0), writes=[xpad.b])
        S.op("dve", lambda e: e.memset(ST.t[:], 0.0), writes=[ST.b])
        S.op("dve", lambda e: e.memset(STb.t[:], 0.0), writes=[STb.b])

        import os as _os3
        ENG_OUTROT = _os3.environ.get("K_OUTROT", "dve")
        ENG_ADDS = _os3.environ.get("K_ADDS", "dve")
        TILES_A = [(i * 256, 256, False) for i in range(8)] + [(SEQ, 64, True)]

        def load_x(ti):
            t0, NT, is_s = TILES_A[ti]
            for blk in range((NT + 127) // 128):
                rows = min(128, NT - blk * 128)
                S.dma("sp", xtm.t[0:rows, blk, :], xin[t0 + blk * 128:t0 + blk * 128 + rows, :], writes=[xtm.sub(blk)])

        a1 = lambda kt: amod.t[:, kt, 0:1]
        sh1 = lambda kt: mod.t[:, 8 * MOD_SH1 + kt, 0:1]
        cw = lambda ct, k: prm.t[:, P_CONV + 5 * ct + k:P_CONV + 5 * ct + k + 1]
        IN_CHUNKS = [("dt", 0, 1536, 8)] + [("z", i, i * 128, 128) for i in range(4)] + \
                    [("xbc", i, 512 + i * 128, 128) for i in range(8)] + [("u5", i, 1544 + i * 128, 128) for i in range(4)]

        load_x(0)
        pbi = [0]

        def next_pb():
            pbi[0] ^= 1
            return PB[pbi[0]]

        ckpt("pre")
        def chain1(ti):
            t0, NT, is_s = TILES_A[ti]
            u5T = u5Ts[ti % 2]
            nblk = (NT + 127) // 128
            T = 128 if not is_s else 64
            tri = cst.t[0:T, C_TRI:C_TRI + T] if not is_s else cst.t[0:T, C_TRI64:C_TRI64 + T]
            neg = cst.t[0:T, C_NEG:C_NEG + T] if not is_s else cst.t[0:T, C_NEG64:C_NEG64 + T]
            sego = onesf.t[0:T, 0:T] if not is_s else cst.t[0:T, C_SEG64:C_SEG64 + T]
            segi = cst.t[0:64, C_SEGI:C_SEGI + 16]

            def dt_prep(ck):
                c0 = ck * T
                cs_ = slice(c0, c0 + T)
                dtm, acs, dec, dtdec = dtm_l[ck], acs_l[ck], dec_l[ck], dtdec_l[ck]
                pc = 0 if ck == 0 else 480
                S.op("pe", lambda e: e.transpose(PB[4].t[0:T, pc:pc + 8], dtT.t[:, 0, cs_], cst.t[0:8, C_ID:C_ID + 8]),
                     reads=[dtT.b, cst.b], writes=[PB[4].sub("sm")])
                S.op("pe", lambda e: e.transpose(PB[4].t[0:T, pc + 8:pc + 16], dtT.t[:, 1, cs_], cst.t[0:8, C_ID:C_ID + 8]),
                     reads=[dtT.b, cst.b], writes=[PB[4].sub("sm")])
                S.op("act", lambda e: e.activation(out=dtm.t[0:T, :], in_=PB[4].t[0:T, pc:pc + 16], func=AF.Copy),
                     reads=[PB[4].sub("sm")], writes=[dtm.b])
                S.op("pe", lambda e: e.matmul(PB[4].t[0:T, pc + 16:pc + 24], tri, dtm.t[0:T, 8:16], start=True, stop=True),
                     reads=[dtm.b, cst.b], writes=[PB[4].sub("sm")])
                S.op("pe", lambda e: e.matmul(PB[4].t[0:T, pc + 24:pc + 32], sego, dtm.t[0:T, 8:16], start=True, stop=True),
                     reads=[dtm.b, cst.b, onesf.b], writes=[PB[4].sub("sm")])
                S.op("act", lambda e: e.activation(out=acs.t[0:T, :], in_=PB[4].t[0:T, pc + 16:pc + 24], func=AF.Copy),
                     reads=[PB[4].sub("sm")], writes=[acs.b])
                S.op("dve", lambda e: TT(e, dec.t[0:T, :], PB[4].t[0:T, pc + 24:pc + 32], acs.t[0:T, :], ALU.subtract),
                     reads=[PB[4].sub("sm"), acs.b], writes=[dec.b])
                S.op("act", lambda e: e.activation(out=dec.t[0:T, :], in_=dec.t[0:T, :], func=AF.Exp), reads=[dec.b], writes=[dec.b])
                S.op("dve", lambda e: TT(e, dtdec.t[0:T, :], dtm.t[0:T, 0:8], dec.t[0:T, :], ALU.mult),
                     reads=[dtm.b, dec.b], writes=[dtdec.b])
            for blk in range(nblk):
                rows = min(128, NT - blk * 128)
                xb = xtm.sub(blk)
                S.op("act", lambda e, blk=blk, rows=rows: e.activation(
                    out=xn.t[0:rows, blk, :], in_=xtm.t[0:rows, blk, :], func=AF.Square, accum_out=nstat.t[0:rows, blk:blk + 1]),
                    reads=[xb], writes=[xn.sub(blk), nstat.sub(blk)])
                S.op("act", lambda e, blk=blk, rows=rows: e.activation(
                    out=nstat.t[0:rows, 2 + blk:3 + blk], in_=nstat.t[0:rows, blk:blk + 1], func=AF.Ln, scale=1.0 / D, bias=EPS),
                    reads=[nstat.sub(blk)], writes=[nstat.sub(blk)])
                S.op("act", lambda e, blk=blk, rows=rows: e.activation(out=nstat.t[0:rows, 2 + blk:3 + blk],
                                                                        in_=nstat.t[0:rows, 2 + blk:3 + blk], func=AF.Exp, scale=-0.5),
                     reads=[nstat.sub(blk)], writes=[nstat.sub(blk)])
                S.op("act", lambda e, blk=blk, rows=rows: e.activation(
                    out=xn.t[0:rows, blk, :], in_=xtm.t[0:rows, blk, :], func=AF.Copy, scale=nstat.t[0:rows, 2 + blk:3 + blk]),
                    reads=[xb, nstat.sub(blk)], writes=[xn.sub(blk)])
            ckpt("Aa%d" % ti)
            if ti + 1 < len(TILES_A):
                load_x(ti + 1)
            ckpt("Ab%d" % ti)
            for kt in range(8):
                xb_ = 2 + (kt % 2)
                pslot = PB[xb_].b
                for blk in range(nblk):
                    rows = min(128, NT - blk * 128)
                    S.op("pe", lambda e, kt=kt, blk=blk, rows=rows: e.transpose(
                        pbf(xb_)[:, blk * 128:blk * 128 + rows],
                        xn.t[0:rows, blk, kt * 128:(kt + 1) * 128], identb.t[0:rows, 0:rows]),
                        reads=[xn.sub(blk), identb.b], writes=[pslot])
                src = pbf(xb_)[:, 0:NT]
                if not is_s:
                    S.op("act", lambda e, kt=kt, src=src: e.activation(out=uT.t[:, kt, 0:NT], in_=src, func=AF.Identity,
                                                                       scale=a1(kt), bias=sh1(kt)),
                         reads=[pslot, amod.b, mod.b], writes=[uT.sub(kt)])
                else:
                    S.op("dve", lambda e, kt=kt, src=src: TT(e, cacc[0].t[:, 0:NT], src, a1x.t[:, kt, :], ALU.mult),
                         reads=[pslot, a1x.b], writes=[cacc[0].b])
                    S.op("dve", lambda e, kt=kt: TT(e, uT.t[:, kt, 0:NT], cacc[0].t[:, 0:NT], sh1x.t[:, kt, :], ALU.add),
                         reads=[cacc[0].b, sh1x.b], writes=[uT.sub(kt)])
            ckpt("A%d" % ti)
            if ti == 0:
                dump("uT", uT.t[:].rearrange("p k t -> p (k t)"), [128, 8 * NTM], uT.allb())

            yield
            if is_s:
                xps = xpad.t[:, :, 0:NS * 7].rearrange("p c (s k) -> p c s k", k=7)
                S.op("act", lambda e: e.activation(out=xps[:, :, :, 0:3], in_=cvst.t[:], func=AF.Copy), reads=[cvst.b], writes=[xpad.b])
            for (kind, i, c0, M) in IN_CHUNKS:
                yield
                pb = next_pb()
                for kt in range(8):
                    S.op("pe", lambda e, kt=kt, c0=c0, M=M, pb=pb: e.matmul(
                        pb.t[0:M, 0:NT], win_sb.t[:, kt, c0:c0 + M], uT.t[:, kt, 0:NT], start=(kt == 0), stop=(kt == 7)),
                        reads=[win_sb.sub(kt // 2), uT.sub(kt)], writes=[pb.b])
                if kind == "z":
                    S.op("act", lambda e, i=i, pb=pb: e.activation(out=szT.t[:, i, 0:NT], in_=pb.t[:, 0:NT], func=AF.Silu),
                         reads=[pb.b], writes=[szT.b])
                elif kind == "xbc":
                    if not is_s:
                        S.op("act", lambda e, i=i, pb=pb: e.activation(out=xpad.t[:, i, 3:3 + NT], in_=pb.t[:, 0:NT], func=AF.Copy),
                             reads=[pb.b], writes=[xpad.b])
                        if ti == 7:
                            S.op("act", lambda e, i=i, pb=pb: e.activation(out=xtail.t[:, i, 0:3], in_=pb.t[:, NT - 3:NT], func=AF.Copy),
                                 reads=[pb.b], writes=[xtail.b])
                    else:
                        S.op("act", lambda e, i=i, pb=pb: e.activation(
                            out=xps[:, i, :, 3:7], in_=pb.t[:, 0:NT].rearrange("p (s k) -> p s k", k=LS), func=AF.Copy),
                            reads=[pb.b], writes=[xpad.b])
                        S.op("act", lambda e, i=i, pb=pb: e.activation(out=xtail.t[:, i, 0:NT], in_=pb.t[:, 0:NT], func=AF.Copy),
                             reads=[pb.b], writes=[xtail.b])
                elif kind == "dt":
                    S.op("act", lambda e, pb=pb: e.activation(out=dtT.t[:, 1, 0:NT], in_=pb.t[0:8, 0:NT], func=AF.Exp,
                                                              bias=ssd8.t[:, 0:1]), reads=[pb.b, ssd8.b], writes=[dtT.b])
                    S.op("act", lambda e: e.activation(out=dtT.t[:, 0, 0:NT], in_=dtT.t[:, 1, 0:NT], func=AF.Ln, bias=1.0),
                         reads=[dtT.b], writes=[dtT.b])
                    S.op("dve", lambda e: e.tensor_scalar(out=dtT.t[:, 1, 0:NT], in0=dtT.t[:, 0, 0:NT], scalar1=ssd8.t[:, 1:2],
                                                          scalar2=None, op0=ALU.mult), reads=[dtT.b, ssd8.b], writes=[dtT.b])
                    for ck_ in range(NT // T):
                        yield
                        dt_prep(ck_)
                else:
                    S.op("act", lambda e, i=i, pb=pb: e.activation(out=u5T.t[:, i, 0:NT], in_=pb.t[:, 0:NT], func=AF.Copy),
                         reads=[pb.b], writes=[u5T.b])

            ckpt("B%d" % ti)
            for ct in range(8):
                yield
                pb = next_pb()
                if not is_s:
                    xin_k = lambda k, ct=ct: xpad.t[:, ct, k:k + NT]
                    pbv = pb.t[:, 0:NT]
                    dst = xsT.t[:, ct, 0:NT] if ct < 4 else BCT.t[:, ct - 4, 0:NT]
                else:
                    xin_k = lambda k, ct=ct: xps[:, ct, :, k:k + LS]
                    pbv = pb.t[:, 0:NT].rearrange("p (s k) -> p s k", k=LS)
                    dst = (xsT.t[:, ct, 0:NT] if ct < 4 else BCT.t[:, ct - 4, 0:NT]).rearrange("p (s k) -> p s k", k=LS)
                for k in range(4):
                    S.op("pe", lambda e, k=k: e.matmul(pbv, dgc.t[:, ct, k, :], xin_k(k), start=(k == 0), stop=(k == 3)),
                         reads=[dgc.b, xpad.b], writes=[pb.b])
                S.op("act", lambda e: e.activation(out=dst, in_=pbv, func=AF.Silu, bias=cw(ct, 4)),
                     reads=[pb.b, prm.b], writes=[xsT.b if ct < 4 else BCT.b])
            ocv = o_conv.rearrange("p (c s k) -> p c s k", s=17, k=3)
            if is_s:
                S.op("act", lambda e: e.activation(out=cvst.t[:], in_=xtail.t[:].rearrange("p c (s k) -> p c s k", k=LS)[:, :, :, 1:4],
                                                   func=AF.Copy), reads=[xtail.b], writes=[cvst.b])
                S.dma("sp", ocv[:, :, 1:17, :], cvst.t[:], reads=[cvst.b], buf=cvst.b)
                outbufs.append(cvst.b)
            elif ti == 7:
                S.dma("sp", ocv[:, :, 0, :], xtail.t[:, :, 0:3], reads=[xtail.b], buf=xtail.b)
            if not is_s:
                S.op("dve", lambda e: e.tensor_copy(out=xpad.t[:, :, 0:3], in_=xpad.t[:, :, NT:NT + 3]),
                     reads=[xpad.b], writes=[xpad.b])
            if is_s:
                dump("xsS", xsT.t[:, :, 0:64], [128, 4, 64], [xsT.b])
                dump("ygS", yg.t[:, :, 0:64], [128, 4, 64], [yg.b])
            if ti == 0:
                dump("xsT", xsT.t[:].rearrange("p k t -> p (k t)"), [128, 4 * NTM], [xsT.b])
                dump("dtT", dtT.t[:].rearrange("p k t -> p (k t)"), [8, 2 * NTM], [dtT.b])

            ckpt("C%d" % ti)
            for ck in range(NT // T):
                c0 = ck * T
                cs_ = slice(c0, c0 + T)
                dtm, acs, dec, dtdec = dtm_l[ck], acs_l[ck], dec_l[ck], dtdec_l[ck]
                yield
                for pr in range(4):
                    S.op("pe", lambda e, pr=pr, cs_=cs_: e.transpose(PB[3].t[0:T, pr * 128:(pr + 1) * 128], xsT.t[:, pr, cs_], ident),
                         reads=[xsT.b, cst.b], writes=[PB[3].b])
                pxs = PB[3].t[0:T, :].rearrange("p (h q) -> p h q", q=64)
                for h in range(8):
                    S.op("act", lambda e, h=h: e.activation(out=Xtm.t[0:T, h, :], in_=pxs[:, h, :], func=AF.Copy, scale=dtm.t[0:T, h:h + 1]),
                         reads=[PB[3].b, dtm.b], writes=[Xtm.b])
                    S.op("act", lambda e, h=h: e.activation(out=Xdec.t[0:T, h, :], in_=pxs[:, h, :], func=AF.Copy, scale=dtdec.t[0:T, h:h + 1]),
                         reads=[PB[3].b, dtdec.b], writes=[Xdec.b])
                for g in range(2):
                    S.op("pe", lambda e, g=g, cs_=cs_: e.transpose(pbf(2)[0:T, g * 128:(g + 1) * 128], BCT.t[:, g, cs_], identb.t[:]),
                         reads=[BCT.b, identb.b], writes=[PB[2].sub(0)])
                S.op("act", lambda e: e.activation(out=Btm.t[0:T].rearrange("p g n -> p (g n)"), in_=pbf(2)[0:T, 0:256], func=AF.Copy),
                     reads=[PB[2].sub(0)], writes=[Btm.b])
                yield
                S.op("dve", lambda e: TT(e, big1.t[0:T, :, 0:T], tri.unsqueeze(1).to_broadcast([T, 8, T]),
                                         dtm.t[0:T, 8:16].unsqueeze(2).to_broadcast([T, 8, T]), ALU.mult),
                     reads=[cst.b, dtm.b], writes=[big1.b])
                for half in range(2):
                    S.op("pe", lambda e, half=half: e.matmul(
                        PB[3].t[:, 0:4 * T].rearrange("p (h l) -> p h l", l=T), onesf.t[0:T, :],
                        big1.t[0:T, 4 * half:4 * half + 4, 0:T], start=True, stop=True),
                        reads=[big1.b, onesf.b], writes=[PB[3].b])
                    yield
                    for h in range(4 * half, 4 * half + 4):
                        S.op("dve", lambda e, h=h: e.scalar_tensor_tensor(
                            out=big2.t[0:T, h, 0:T], in0=PB[3].t[0:T, (h % 4) * T:(h % 4 + 1) * T], scalar=acs.t[0:T, h:h + 1],
                            in1=neg, op0=ALU.subtract, op1=ALU.min), reads=[PB[3].b, acs.b, cst.b], writes=[big2.b])
                    S.op("act", lambda e, half=half: e.activation(
                        out=eA.t[:, 4 * half:4 * half + 4, 0:T], in_=PB[3].t[:, 0:4 * T].rearrange("p (h l) -> p h l", l=T),
                        func=AF.Exp), reads=[PB[3].b], writes=[eA.b])
                    yield
                S.op("act", lambda e: e.activation(out=big2.t[0:T, :, 0:T], in_=big2.t[0:T, :, 0:T], func=AF.Exp),
                     reads=[big2.b], writes=[big2.b])
                yield
                for g in range(2):
                    S.op("pe", lambda e, g=g, cs_=cs_: e.matmul(PB[4].t[0:T, 32 + g * 128:32 + g * 128 + T], BCT.t[:, g, cs_],
                                                                 BCT.t[:, 2 + g, cs_], start=True, stop=True),
                         reads=[BCT.b], writes=[PB[4].sub("cb")])
                cbv = PB[4].t[0:T, 32:288].rearrange("p (g l) -> p g l", l=128)[:, :, 0:T]
                S.op("dve", lambda e: TT(e, MT.t[0:T, :, 0:T].rearrange("p (g h) l -> p g h l", h=4),
                                         cbv.unsqueeze(2).to_broadcast([T, 2, 4, T]),
                                         big2.t[0:T, :, 0:T].rearrange("p (g h) l -> p g h l", h=4), ALU.mult),
                     reads=[PB[4].sub("cb"), big2.b], writes=[MT.b])
                yield
                S.op("pool", lambda e, cs_=cs_: TT(e, CdT.t[:, :, 0:T].rearrange("p (g h) l -> p g h l", h=4),
                                                   BCT.t[:, 2:4, cs_].unsqueeze(2).to_broadcast([128, 2, 4, T]),
                                                   eA.t[:, :, 0:T].rearrange("p (g h) l -> p g h l", h=4), ALU.mult),
                     reads=[BCT.b, eA.b], writes=[CdT.b])
                yield
                ypb = PB[7]
                if is_s:
                    S.op("dve", lambda e: e.tensor_copy(out=dAx.t[0:T], in_=dtm.t[0:T, 8:16].unsqueeze(2).to_broadcast([T, 8, 64])),
                         reads=[dtm.b], writes=[dAx.b])
                    for pr in range(4):
                        S.op("pe", lambda e, pr=pr: e.matmul(PB[4].t[:, 288 + pr * 16:288 + (pr + 1) * 16],
                                                             dAx.t[0:T, 2 * pr:2 * pr + 2, :], segi, start=True, stop=True),
                             reads=[dAx.b, cst.b], writes=[PB[4].sub("dec")])
                    S.op("act", lambda e: e.activation(out=decfm.t[:].rearrange("p a s -> p (a s)"), in_=PB[4].t[:, 288:352], func=AF.Exp),
                         reads=[PB[4].sub("dec")], writes=[decfm.b])
                    stv = stssd_d.rearrange("j (pr hl) p n -> j (hl p) pr n", hl=2)
                    osv = o_ssds.rearrange("j (pr hl) p n -> j (hl p) pr n", hl=2)
                    S.dma("act", h0n[0].t[:], stv[0], writes=[h0n[0].b])
                    for j in range(NS):
                        yield
                        jj = j % 2
                        if j + 1 < NS:
                            S.dma("act", h0n[1 - jj].t[:], stv[j + 1], writes=[h0n[1 - jj].b])
                        pbt = PB[jj]
                        for pr in range(4):
                            S.op("pe", lambda e, pr=pr, jj=jj, pbt=pbt: e.transpose(pbt.t[:, pr * 128:(pr + 1) * 128], h0n[jj].t[:, pr, :], ident),
                                 reads=[h0n[jj].b, cst.b], writes=[pbt.b])
                        S.op("act", lambda e, jj=jj, pbt=pbt: e.activation(out=h0T[jj].t[:].rearrange("p h q -> p (h q)"), in_=pbt.t[:, :], func=AF.Copy),
                             reads=[pbt.b], writes=[h0T[jj].b])
                        for h in range(8):
                            pr, hl = h // 2, h % 2
                            S.op("pe", lambda e, h=h, pr=pr, hl=hl, jj=jj, j=j: e.matmul(
                                ypb.t[64 * hl:64 * hl + 64, pr * T + LS * j:pr * T + LS * j + LS], h0T[jj].t[:, h, :],
                                CdT.t[:, h, LS * j:LS * j + LS], start=(j == 0 and pr == 0), stop=False, skip_group_check=True),
                                reads=[h0T[jj].b, CdT.b], writes=[ypb.b])
                        S.op("dve", lambda e, jj=jj, j=j: e.tensor_scalar(out=Bj[jj].t[0:T], in0=Btm.t[0:T], scalar1=segi[:, j:j + 1],
                                                                          scalar2=None, op0=ALU.mult),
                             reads=[Btm.b, cst.b], writes=[Bj[jj].b])
                        pby = PB[3]
                        for pr in range(4):
                            S.op("pe", lambda e, pr=pr, jj=jj, pby=pby: e.matmul(
                                pby.t[:, pr * 128:(pr + 1) * 128], Xdec.t[0:T, 2 * pr:2 * pr + 2, :], Bj[jj].t[0:T, pr // 2, :],
                                start=True, stop=True), reads=[Xdec.b, Bj[jj].b], writes=[pby.b])
                        S.op("dve", lambda e, jj=jj, j=j: TT(e, hn[jj].t[:], h0n[jj].t[:],
                                                             decfm.t[:, :, j:j + 1].to_broadcast([128, 4, 128]), ALU.mult),
                             reads=[h0n[jj].b, decfm.b], writes=[hn[jj].b])
                        S.op("dve", lambda e, jj=jj, pby=pby: TT(e, hn[jj].t[:], hn[jj].t[:],
                                                                 pby.t[:, :].rearrange("p (a n) -> p a n", n=128), ALU.add),
                             reads=[hn[jj].b, pby.b], writes=[hn[jj].b])
                        S.dma("sp", osv[j], hn[jj].t[:], reads=[hn[jj].b], buf=hn[jj].b)
                    outbufs.extend([hn[0].b, hn[1].b])
                for h in range(8):
                    pr, hl = h // 2, h % 2
                    out = ypb.t[64 * hl:64 * hl + 64, pr * T:(pr + 1) * T]
                    S.op("pe", lambda e, h=h, out=out, pr=pr: e.matmul(out, Xtm.t[0:T, h, :], MT.t[0:T, h, 0:T],
                                                                       start=(pr == 0 and not is_s), stop=is_s, skip_group_check=True),
                         reads=[Xtm.b, MT.b], writes=[ypb.b])
                    if not is_s:
                        S.op("pe", lambda e, h=h, out=out: e.matmul(out, STb.t[:, h, :], CdT.t[:, h, 0:T], start=False, stop=True,
                                                                    skip_group_check=True),
                             reads=[STb.b, CdT.b], writes=[ypb.b])
                yield
                for pr in range(4):
                    S.op("dve", lambda e, pr=pr, cs_=cs_: e.scalar_tensor_tensor(
                        out=yg.t[:, pr, 0:T], in0=xsT.t[:, pr, cs_], scalar=prm.t[:, P_SSDFM + pr:P_SSDFM + pr + 1],
                        in1=ypb.t[:, pr * T:(pr + 1) * T], op0=ALU.mult, op1=ALU.add),
                        reads=[xsT.b, prm.b, ypb.b], writes=[yg.b])
                S.op("dve", lambda e, cs_=cs_: TT(e, yg.t[:, :, 0:T], yg.t[:, :, 0:T], szT.t[:, :, cs_], ALU.mult),
                     reads=[yg.b, szT.b], writes=[yg.b])
                S.op("dve", lambda e: TT(e, ysqb.t[:, :, 0:T], yg.t[:, :, 0:T], yg.t[:, :, 0:T], ALU.mult),
                     reads=[yg.b], writes=[ysqb.b])
                for g in range(2):
                    for k in range(2):
                        S.op("pe", lambda e, g=g, k=k: e.matmul(PB[3].t[:, g * T:(g + 1) * T], onesb1.t[:], ysqb.t[:, 2 * g + k, 0:T],
                                                                start=(k == 0), stop=(k == 1)),
                             reads=[onesb1.b, ysqb.b], writes=[PB[3].b])
                S.op("act", lambda e: e.activation(out=rsb.t[:, :, 0:T], in_=PB[3].t[:, 0:2 * T].rearrange("p (g l) -> p g l", l=T),
                                                   func=AF.Ln, scale=1.0 / 256, bias=EPS), reads=[PB[3].b], writes=[rsb.b])
                S.op("act", lambda e: e.activation(out=rsb.t[:, :, 0:T], in_=rsb.t[:, :, 0:T], func=AF.Exp, scale=-0.5),
                     reads=[rsb.b], writes=[rsb.b])
                yield
                if not is_s:
                    for g in range(2):
                        S.op("pe", lambda e, g=g: e.matmul(PB[6].t[:, g * 256:(g + 1) * 256], Btm.t[0:T, g, :],
                                                           Xdec.t[0:T, 4 * g:4 * g + 4, :], start=True, stop=True),
                             reads=[Btm.b, Xdec.b], writes=[PB[6].b])
                    S.op("dve", lambda e: TT(e, ST.t[:], ST.t[:], eA.t[:, :, T - 1:T].to_broadcast([128, 8, 64]), ALU.mult),
                         reads=[ST.b, eA.b], writes=[ST.b])
                    S.op("dve", lambda e: TT(e, ST.t[:], ST.t[:], PB[6].t[:, :].rearrange("p (h q) -> p h q", q=64), ALU.add),
                         reads=[ST.b, PB[6].b], writes=[ST.b])
                    S.op("act", lambda e: e.activation(out=STb.t[:], in_=ST.t[:], func=AF.Copy), reads=[ST.b], writes=[STb.b])
                for pr in range(4):
                    S.op("dve", lambda e, pr=pr: e.scalar_tensor_tensor(
                        out=mixt[ti % 2].t[:, pr, c0:c0 + T], in0=yg.t[:, pr, 0:T],
                        scalar=prm.t[:, P_SSDFM + 4 + pr:P_SSDFM + 5 + pr], in1=rsb.t[:, pr // 2, 0:T], op0=ALU.mult, op1=ALU.mult),
                        reads=[yg.b, prm.b, rsb.b], writes=[mixt[ti % 2].sub("ssd")])
            if ti == 7:
                S.dma("sp", o_ssdp, ST.t[:].rearrange("p h q -> p (h q)"), reads=[ST.b], buf=ST.b)
                outbufs.append(ST.b)

            ckpt("D%d" % ti)
            yield

        def chain2(ti):
            t0, NT, is_s = TILES_A[ti]
            u5T = u5Ts[ti % 2]
            if is_s:
                S.dma("sp", sts5.t[:].rearrange("p a s q -> p (a s q)"), sts5_d, writes=[sts5.b])
            if not is_s:
                groups = [(list(range(16)), k * T5, T5) for k in range(NT // T5)]
            else:
                groups = [(list(range(8)), 0, 64), (list(range(8, 16)), 0, 64)]
            def emit_bu(g_):
                slist_, tk0_, ntok_ = groups[g_]
                bus = busd[g_ % 2]
                for part, pb in ((0, PB[5]), (1, PB[6])):
                    for idx, s in enumerate(slist_):
                        S.op("pe", lambda e, part=part, pb=pb, idx=idx, s=s: e.matmul(
                            pb.t[:, idx * ntok_:(idx + 1) * ntok_], s5BT.t[:, part, s, :], u5T.t[:, s // 4, tk0_:tk0_ + ntok_],
                            start=True, stop=True), reads=[s5BT.b, u5T.b], writes=[pb.b])
                S.op("act", lambda e: e.activation(out=bus[0].t[:], in_=PB[5].t[:, :], func=AF.Copy), reads=[PB[5].b], writes=[bus[0].b])
                S.op("act", lambda e: e.activation(out=bus[1].t[:], in_=PB[6].t[:, :], func=AF.Copy), reads=[PB[6].b], writes=[bus[1].b])
            def views(g_):
                slist_, tk0_, ntok_ = groups[g_]
                s0_ = slist_[0]
                if not is_s:
                    V3 = lambda ap: ap.rearrange("p (s t) -> p s t", t=T5)
                    QR, QI = Qtab.t[:, 0], Qtab.t[:, 1]
                    PR_, PI_ = Ptab.t[:, 0], Ptab.t[:, 1]
                    msk = mask32.t[:].rearrange("p s t -> p (s t)")
                    first = lambda ap: V3(ap)[:, :, 0]
                    cin_r, cin_i = s5cr.t[:, 0, :], s5cr.t[:, 1, :]
                else:
                    V3 = lambda ap: ap.rearrange("p (s q b) -> p s q b", q=NS, b=LS)
                    bc = lambda ap: ap.unsqueeze(2).to_broadcast([128, 8, NS, LS])
                    QR, QI = bc(Qtab.t[:, 0, s0_:s0_ + 8, 0:LS]), bc(Qtab.t[:, 1, s0_:s0_ + 8, 0:LS])
                    PR_, PI_ = bc(Ptab.t[:, 0, s0_:s0_ + 8, 0:LS]), bc(Ptab.t[:, 1, s0_:s0_ + 8, 0:LS])
                    msk = mask4.t[:].rearrange("p s t -> p (s t)")
                    first = lambda ap: V3(ap)[:, :, :, 0]
                    cin_r, cin_i = sts5.t[:, 0, s0_:s0_ + 8, :], sts5.t[:, 1, s0_:s0_ + 8, :]
                return V3, QR, QI, PR_, PI_, msk, first, cin_r, cin_i
            vsets = [[s5v[0], s5v[1]], [s5vb[0], s5vb[1]]]

            def mults_adds(g_):
                V3, QR, QI, PR_, PI_, msk, first, cin_r, cin_i = views(g_)
                bus = busd[g_ % 2]
                br, bi = V3(bus[0].t[:]), V3(bus[1].t[:])
                t1, t2, t3, t4 = s5t[0], s5t[1], s5t34[0], s5t34[1]
                vr, vi = vsets[g_ % 2]
                tb = [Qtab.b]
                for (o, a, b_, rd) in ((t1, QR, br, bus[0].b), (t2, QI, bi, bus[1].b), (t3, QR, bi, bus[1].b), (t4, QI, br, bus[0].b)):
                    S.op("dve", lambda e, o=o, a=a, b_=b_: TT(e, V3(o.t[:]), a, b_, ALU.mult), reads=tb + [rd], writes=[o.b])
                S.op(ENG_ADDS, lambda e: TT(e, vr.t[:], t1.t[:], t2.t[:], ALU.subtract), reads=[t1.b, t2.b], writes=[vr.b])
                S.op(ENG_ADDS, lambda e: TT(e, vi.t[:], t3.t[:], t4.t[:], ALU.add), reads=[t3.b, t4.b], writes=[vi.b])
            emit_bu(0)
            if len(groups) > 1:
                emit_bu(1)
            mults_adds(0)
            pend_y5 = [None]
            for gi_, (slist, tk0, ntok) in enumerate(groups):
                yield
                ns = len(slist)
                s0 = slist[0]
                V3, QR, QI, PR_, PI_, msk, first, cin_r, cin_i = views(gi_)
                vr, vi = vsets[gi_ % 2]
                if gi_ + 1 < len(groups):
                    mults_adds(gi_ + 1)
                    yield
                if gi_ + 2 < len(groups):
                    emit_bu(gi_ + 2)
                S.op("dve", lambda e: TT(e, first(vr.t[:]), first(vr.t[:]), cin_r, ALU.add), reads=[vr.b, s5cr.b, sts5.b], writes=[vr.b])
                S.op("dve", lambda e: TT(e, first(vi.t[:]), first(vi.t[:]), cin_i, ALU.add), reads=[vi.b, s5cr.b, sts5.b], writes=[vi.b])
                yield
                s5k[0] ^= 1
                gr, gi2 = s5g[s5k[0]][0], s5g[s5k[0]][1]
                S.op("dve", lambda e: e.tensor_tensor_scan(out=gr.t[:], data0=msk, data1=vr.t[:], initial=0.0, op0=ALU.mult, op1=ALU.add),
                     reads=[vr.b, mask32.b, mask4.b], writes=[gr.b])
                S.op("dve", lambda e: e.tensor_tensor_scan(out=gi2.t[:], data0=msk, data1=vi.t[:], initial=0.0, op0=ALU.mult, op1=ALU.add),
                     reads=[vi.b, mask32.b, mask4.b], writes=[gi2.b])
                yield
                hp = s5h[gi_ % 2]
                hr, hi = hp, hp
                for (o, a, b_) in ((hp[0], PR_, gr), (hp[1], PI_, gi2), (hp[2], PR_, gi2), (hp[3], PI_, gr)):
                    S.op(ENG_OUTROT, lambda e, o=o, a=a, b_=b_: TT(e, V3(o.t[:]), a, V3(b_.t[:]), ALU.mult),
                         reads=[Ptab.b, b_.b], writes=[o.b])
                yield
                if not is_s:
                    glr, gli = V3(gr.t[:])[:, :, T5 - 1], V3(gi2.t[:])[:, :, T5 - 1]
                    plr, pli = Ptab.t[:, 0, :, T5 - 1], Ptab.t[:, 1, :, T5 - 1]
                    c_ = lambda i: s5c.t[:, i, :]
                    outr, outi = s5cr.t[:, 0, :], s5cr.t[:, 1, :]
                else:
                    glr, gli = V3(gr.t[:])[:, :, :, LS - 1], V3(gi2.t[:])[:, :, :, LS - 1]
                    plr = Ptab.t[:, 0, s0:s0 + 8, LS - 1:LS].to_broadcast([128, 8, NS])
                    pli = Ptab.t[:, 1, s0:s0 + 8, LS - 1:LS].to_broadcast([128, 8, NS])
                    c_ = lambda i: hn[0].t[:, i, :].rearrange("p (s q) -> p s q", q=NS)
                    outr, outi = s5fin.t[:, 0, s0:s0 + 8, 1:17], s5fin.t[:, 1, s0:s0 + 8, 1:17]
                cb_ = [s5c.b, hn[0].b]
                if not is_s:
                    pl2 = Ptab.t[:, :, :, T5 - 1]
                    ca, cb2 = s5c.t[:, 0:2, :], s5c.t[:, 2:4, :]
                    S.op("dve", lambda e: TT(e, ca, pl2, glr.unsqueeze(1).to_broadcast([128, 2, 16]), ALU.mult),
                         reads=[Ptab.b, gr.b] + cb_, writes=cb_)
                    S.op("dve", lambda e: TT(e, cb2, pl2, gli.unsqueeze(1).to_broadcast([128, 2, 16]), ALU.mult),
                         reads=[Ptab.b, gi2.b] + cb_, writes=cb_)
                    S.op("dve", lambda e: TT(e, outr, c_(0), c_(3), ALU.subtract), reads=cb_, writes=[s5cr.b, s5fin.b])
                    S.op("dve", lambda e: TT(e, outi, c_(2), c_(1), ALU.add), reads=cb_, writes=[s5cr.b, s5fin.b])
                else:
                    cseq = [(c_(0), plr, glr, ALU.mult), (c_(1), pli, gli, ALU.mult), (c_(2), plr, gli, ALU.mult), (c_(3), pli, glr, ALU.mult)]
                    for (o, a, b, op) in cseq:
                        S.op("dve", lambda e, o=o, a=a, b=b, op=op: TT(e, o, a, b, op), reads=[Ptab.b, gr.b, gi2.b] + cb_, writes=cb_)
                    S.op("dve", lambda e: TT(e, outr, c_(0), c_(1), ALU.subtract), reads=cb_, writes=[s5cr.b, s5fin.b])
                    S.op("dve", lambda e: TT(e, outi, c_(2), c_(3), ALU.add), reads=cb_, writes=[s5cr.b, s5fin.b])
                yield
                def emit_y5(gi_=gi_, slist=slist, tk0=tk0, ntok=ntok, hr=hr, hi=hi):
                    y5c0 = 352
                    nq = 4 if not is_s else 2
                    for qi in range(nq):
                        q = qi if not is_s else 2 * gi_ + qi
                        S.op("pe", lambda e, q=q, qi=qi: e.matmul(PB[4].t[:, y5c0 + qi * ntok:y5c0 + (qi + 1) * ntok], dg5.t[:, q, :],
                                                                  u5T.t[:, q, tk0:tk0 + ntok], start=(qi == 0), stop=False, skip_group_check=True),
                             reads=[dg5.b, u5T.b], writes=[PB[4].sub("y5")])
                    for idx, s in enumerate(slist):
                        qi = (s // 4) if not is_s else (s // 4 - 2 * gi_)
                        out = PB[4].t[32 * (s % 4):32 * (s % 4) + 32, y5c0 + qi * ntok:y5c0 + (qi + 1) * ntok]
                        for j4, lw in enumerate((s5CT.t[:, 0, s, :], s5CTn.t[:, s, :], s5CT.t[:, 1, s, :], s5CT.t[:, 1, s, :])):
                            S.op("pe", lambda e, j4=j4, lw=lw: e.matmul(out, lw, hr[j4].t[:, idx * ntok:(idx + 1) * ntok],
                                                                        start=False, stop=(j4 == 3), skip_group_check=True,
                                                                        tile_position=(0, 32 * (s % 4))),
                                 reads=[s5CT.b, s5CTn.b, hr[j4].b], writes=[PB[4].sub("y5")])
                    q0 = 0 if not is_s else 2 * gi_
                    S.op("act", lambda e: e.activation(out=y5pre.t[:, q0:q0 + nq, tk0:tk0 + ntok],
                                                       in_=PB[4].t[:, y5c0:y5c0 + nq * ntok].rearrange("p (q t) -> p q t", t=ntok), func=AF.Copy),
                         reads=[PB[4].sub("y5")], writes=[y5pre.b])
                if pend_y5[0] is not None:
                    pend_y5[0]()
                    yield
                pend_y5[0] = emit_y5
            if pend_y5[0] is not None:
                pend_y5[0]()
                pend_y5[0] = None
                yield
            if ti == 7:
                S.op("dve", lambda e: e.tensor_copy(out=s5fin.t[:, :, :, 0], in_=s5cr.t[:]), reads=[s5cr.b], writes=[s5fin.b])
            if is_s:
                S.dma("sp", o_s5, s5fin.t[:].rearrange("p a s q -> p (a s q)"), reads=[s5fin.b], buf=s5fin.b)
                outbufs.append(s5fin.b)
            if ti == 0:
                dump("y5pre", y5pre.t[:].rearrange("p k t -> p (k t)"), [128, 4 * NTM], [y5pre.b])
            ckpt("E%d" % ti)
            yield
            S.op("act", lambda e: e.activation(out=g5.t[:, :, 0:NT], in_=y5pre.t[:, :, 0:NT], func=AF.Gelu), reads=[y5pre.b], writes=[g5.b])
            pend_glu = [None]
            for m in range(4):
                yield
                pb = next_pb()
                for q in range(4):
                    S.op("pe", lambda e, m=m, q=q, pb=pb: e.matmul(pb.t[:, 0:NT], wglu_sb.t[:, q, m * 128:(m + 1) * 128], g5.t[:, q, 0:NT],
                                                                   start=(q == 0), stop=(q == 3)),
                         reads=[wglu_sb.b, g5.b], writes=[pb.b])
                sgl = sgl2[m % 2]
                S.op("act", lambda e, m=m, pb=pb: e.activation(out=sgl.t[:, 0:NT], in_=pb.t[:, 0:NT], func=AF.Sigmoid,
                                                               bias=prm.t[:, P_S5M + 4 + m:P_S5M + 5 + m]),
                     reads=[pb.b, prm.b], writes=[sgl.b])

                def glu_mul(m=m, sgl=sgl):
                    S.op("dve", lambda e: TT(e, mixt[ti % 2].t[:, 4 + m, 0:NT], g5.t[:, m, 0:NT], sgl.t[:, 0:NT], ALU.mult),
                         reads=[g5.b, sgl.b], writes=[mixt[ti % 2].sub("s5")])
                if pend_glu[0] is not None:
                    pend_glu[0]()
                pend_glu[0] = glu_mul
            pend_glu[0]()
            pend_glu[0] = None
            S.dma("sp", mixd[:, :, t0:t0 + NT], mixt[ti % 2].t[:, :, 0:NT], reads=mixt[ti % 2].allb(), writes=[mixdb[ti]], buf=mixdb[ti])
            ckpt("T%d" % ti)
            if ti == 0:
                dump("mix0", mixt[0].t[:, :, 0:NTM], [128, 8, NTM], mixt[0].allb())
            yield

        import os as _os
        RATIO = int(_os.environ.get("K_RATIO", "1"))
        HEAD = int(_os.environ.get("K_HEAD", "9"))
        HEADB = int(_os.environ.get("K_HEADB", "10"))

        def drive(gens, ada_every=0, head=0):
            gens = [g for g in gens if g is not None]
            n = 0
            if len(gens) > 1:
                for _ in range(head):
                    try:
                        next(gens[0])
                    except StopIteration:
                        gens.pop(0)
                        break
            while gens:
                for gi__, g in enumerate(list(gens)):
                    for _ in range((RATIO if gi__ == 0 else 1) if RATIO > 0 else (-RATIO if gi__ == 1 else 1)):
                        try:
                            next(g)
                        except StopIteration:
                            if g in gens:
                                gens.remove(g)
                            break
                n += 1
                if ada_every and n % ada_every == 0:
                    ada_step()
        ada_state[0] = 0
        drive([chain1(0)], ada_every=12)
        for ti_ in range(len(TILES_A)):
            if ti_ == 7:
                while ada_state[1] < len(ADA_CH):
                    ada_step()
                fill_x(a1x, amod.t[:, 0:8, 1:17], [amod.b])
                fill_x(sh1x, chunkmod(MOD_SH1)[:, :, 1:17], [mod.b])
                make_amod([(1, (4, 1)), (2, (7, 2))])
            drive([chain2(ti_), chain1(ti_ + 1) if ti_ + 1 < len(TILES_A) else None], ada_every=(8 if ti_ < 7 else 0), head=HEAD)
        dump("mixS", mixt[0].t[:, :, 0:64], [128, 8, 64], mixt[0].allb())
        S.barrier()
        ckpt("1a")
        A.lo = LO_GLOBAL
        x1T = A.alloc("x1T", [128, 8, NTOK], F32, top=True)
        vT = A.alloc("vT", [128, 8, NTOK], BF16, top=True)
        pre_g = [A.alloc("wgs%dt" % i, [128, 8, 256], BF16, top=True) for i in range(2)]
        pre_u = [A.alloc("wus%dt" % i, [128, 8, 256], BF16, top=True) for i in range(2)]
        wout_sb = A.alloc("wout_sb", [128, 8, D], BF16)
        wout_v = wout.rearrange("(kt p) n -> p kt n", p=128)
        for kh in range(4):
            S.dma("pool", wout_sb.t[:, 2 * kh:2 * kh + 2, :], wout_v[:, 2 * kh:2 * kh + 2, :], writes=[wout_sb.sub(kh)])
        mixb = [A.alloc("mixb%d" % i, [128, 8, 512], BF16) for i in range(2)]

        def load_mix(ti):
            t0, NT, is_s = TILES_B[ti]
            tiles_a = [i for i, (a0, n0, s0_) in enumerate(TILES_A) if a0 >= t0 and a0 < t0 + NT]
            S.dma("sp", mixb[ti % 2].t[:, :, 0:NT], mixd[:, :, t0:t0 + NT], reads=[mixdb[i] for i in tiles_a], writes=[mixb[ti % 2].b])
        wg_v = wg.rearrange("(kt p) n -> p kt n", p=128)
        wu_v = wu.rearrange("(kt p) n -> p kt n", p=128)
        for si_ in range(2):
            S.dma("pool", pre_g[si_].t[:], wg_v[:, :, si_ * 256:(si_ + 1) * 256], writes=[pre_g[si_].b])
            S.dma("pool", pre_u[si_].t[:], wu_v[:, :, si_ * 256:(si_ + 1) * 256], writes=[pre_u[si_].b])
        xtm2 = A.alloc("xtm2", [128, 4, D], F32)
        xTm = [A.alloc("xTm%d" % i, [128, 512], F32) for i in range(2)]
        sqb = [A.alloc("sqb%d" % i, [128, 512], BF16) for i in range(2)]
        onesb = A.alloc("onesb", [128, 128], BF16)
        S.op("dve", lambda e: e.memset(onesb.t[:], 1.0), writes=[onesb.b])
        tmp2 = [A.alloc("tmp2_%d" % i, [128, 512], F32) for i in range(2)]
        rstdb = [A.alloc("rstdb%d" % i, [128, 512], F32) for i in range(2)]
        g1x = expand_mod("g1x", chunkmod(MOD_G1)[:, :, 1:17], [mod.b])
        a2x = expand_mod("a2x", amod.t[:, 8:16, 1:17], [amod.b])
        sh2x = expand_mod("sh2x", chunkmod(MOD_SH2)[:, :, 1:17], [mod.b])
        print("arena p1b: lo=%d hi=%d" % (A.lo, A.hi))
        TILES_B = [(i * 512, 512, False) for i in range(4)] + [(SEQ, 64, True)]

        def load_x2(ti):
            t0, NT, is_s = TILES_B[ti]
            for blk in range((NT + 127) // 128):
                rows = min(128, NT - blk * 128)
                S.dma("sp", xtm2.t[0:rows, blk, :], xin[t0 + blk * 128:t0 + blk * 128 + rows, :], writes=[xtm2.sub(blk)])
        load_x2(0)
        load_mix(0)

        def stat_accum(src_ap, m, NT, pbs, defer=None):
            sq = sqb[m % 2]
            S.op("act", lambda e: e.activation(out=sq.t[:, 0:NT], in_=src_ap, func=AF.Square), reads=[x1T.sub(m)], writes=[sq.b])

            def mm(m=m, sq=sq):
                S.op("pe", lambda e: e.matmul(pbs.t[:, 0:NT], onesb.t[:], sq.t[:, 0:NT], start=(m == 0), stop=(m == 7)),
                     reads=[onesb.b, sq.b], writes=[pbs.b])
            if defer is None:
                mm()
            else:
                if defer[0] is not None:
                    defer[0]()
                defer[0] = mm
                if m == 7:
                    defer[0]()
                    defer[0] = None

        def stat_finish(NT, pbs, rs):
            S.op("act", lambda e: e.activation(out=rs.t[:, 0:NT], in_=pbs.t[:, 0:NT], func=AF.Ln, scale=1.0 / D, bias=EPS),
                 reads=[pbs.b], writes=[rs.b])
            S.op("act", lambda e: e.activation(out=rs.t[:, 0:NT], in_=rs.t[:, 0:NT], func=AF.Exp, scale=-0.5), reads=[rs.b], writes=[rs.b])

        def b_part1(ti):
            t0, NT, is_s = TILES_B[ti]
            nblk = (NT + 127) // 128
            tsl = slice(t0, t0 + NT)
            pbs = PB[4 + ti % 2]
            dfr = [None]
            for m in range(8):
                pbx = PB[2 + m % 2]
                xm = xTm[m % 2]
                for blk in range(nblk):
                    rows = min(128, NT - blk * 128)
                    S.op("pe", lambda e, blk=blk, rows=rows: e.transpose(
                        pbx.t[:, blk * 128:blk * 128 + rows], xtm2.t[0:rows, blk, m * 128:(m + 1) * 128], cst.t[0:rows, C_ID:C_ID + rows]),
                        reads=[xtm2.sub(blk), cst.b], writes=[pbx.b])
                S.op("act", lambda e: e.activation(out=xm.t[:, 0:NT], in_=pbx.t[:, 0:NT], func=AF.Copy), reads=[pbx.b], writes=[xm.b])
                pb = next_pb()
                for kt in range(8):
                    S.op("pe", lambda e, kt=kt: e.matmul(pb.t[:, 0:NT], wout_sb.t[:, kt, m * 128:(m + 1) * 128], mixb[ti % 2].t[:, kt, 0:NT],
                                                         start=(kt == 0), stop=(kt == 7)),
                         reads=[wout_sb.sub(kt // 2), mixb[ti % 2].b], writes=[pb.b])
                if m == 0 and ti + 1 < len(TILES_B):
                    load_mix(ti + 1)
                if not is_s:
                    S.op("dve", lambda e: e.scalar_tensor_tensor(
                        out=x1T.t[:, m, tsl], in0=pb.t[:, 0:NT], scalar=mod.t[:, 8 * MOD_G1 + m, 0:1], in1=xm.t[:, 0:NT],
                        op0=ALU.mult, op1=ALU.add), reads=[pb.b, mod.b, xm.b], writes=[x1T.sub(m)])
                else:
                    S.op("dve", lambda e: TT(e, tmp2[0].t[:, 0:NT], pb.t[:, 0:NT], g1x.t[:, m, :], ALU.mult),
                         reads=[pb.b, g1x.b], writes=[tmp2[0].b])
                    S.op("dve", lambda e: TT(e, x1T.t[:, m, tsl], tmp2[0].t[:, 0:NT], xm.t[:, 0:NT], ALU.add),
                         reads=[tmp2[0].b, xm.b], writes=[x1T.sub(m)])
                stat_accum(x1T.t[:, m, tsl], m, NT, pbs, defer=dfr)
                yield
            if ti + 1 < len(TILES_B):
                load_x2(ti + 1)
            yield

        def b_part2(ti):
            t0, NT, is_s = TILES_B[ti]
            tsl = slice(t0, t0 + NT)
            rs = rstdb[ti % 2]
            stat_finish(NT, PB[4 + ti % 2], rs)
            yield
            for m in range(8):
                tq = tmp2[m % 2]
                S.op("dve", lambda e: TT(e, tq.t[:, 0:NT], x1T.t[:, m, tsl], rs.t[:, 0:NT], ALU.mult),
                     reads=[x1T.sub(m), rs.b], writes=[tq.b])
                if not is_s:
                    S.op("act", lambda e: e.activation(out=vT.t[:, m, tsl], in_=tq.t[:, 0:NT], func=AF.Identity,
                                                       scale=amod.t[:, 8 + m, 0:1], bias=mod.t[:, 8 * MOD_SH2 + m, 0:1]),
                         reads=[tq.b, amod.b, mod.b], writes=[vT.sub(m)])
                else:
                    S.op("dve", lambda e: TT(e, tq.t[:, 0:NT], tq.t[:, 0:NT], a2x.t[:, m, :], ALU.mult),
                         reads=[tq.b, a2x.b], writes=[tq.b])
                    S.op("dve", lambda e: TT(e, vT.t[:, m, tsl], tq.t[:, 0:NT], sh2x.t[:, m, :], ALU.add),
                         reads=[tq.b, sh2x.b], writes=[vT.sub(m)])
                yield
            if ti == 0:
                dump("x1p", x1T.t[:, :, 0:256], [128, 8, 256], x1T.allb())
                dump("vp", vT.t[:, :, 0:256], [128, 8, 256], vT.allb())
        drive([b_part1(0)])
        for ti_ in range(len(TILES_B)):
            drive([b_part2(ti_), b_part1(ti_ + 1) if ti_ + 1 < len(TILES_B) else None], head=HEADB)
        S.barrier()
        ckpt("1b")

        A.lo = LO_GLOBAL
        tmp2 = [A.alloc("tmp3_%d" % i, [128, 512], F32) for i in range(2)]
        rstdb = [A.alloc("rstd3_%d" % i, [128, 512], F32) for i in range(2)]
        sqb = [A.alloc("sqb3_%d" % i, [128, 512], BF16) for i in range(2)]
        onesb = A.alloc("onesb3", [128, 128], BF16)
        S.op("dve", lambda e: e.memset(onesb.t[:], 1.0), writes=[onesb.b])
        g2x = expand_mod("g2x", chunkmod(MOD_G2)[:, :, 1:17], [mod.b])
        afx = expand_mod("afx", amod.t[:, 16:24, 1:17], [amod.b])
        shfx = expand_mod("shfx", chunkmod(MOD_SHF)[:, :, 1:17], [mod.b])
        LO_P2 = A.lo
        hT = A.alloc("hT", [128, 6, NTOK], BF16)
        wgs = pre_g + [A.alloc("wgs2", [128, 8, 256], BF16)]
        wus = pre_u + [A.alloc("wus2", [128, 8, 256], BF16)]
        wds = [A.alloc("wds%d" % i, [128, 6, D], BF16) for i in range(2)]
        sgt = [A.alloc("sgt%d" % i, [128, 512], BF16) for i in range(2)]
        print("arena p2: lo=%d hi=%d" % (A.lo, A.hi))
        wg_v = wg.rearrange("(kt p) n -> p kt n", p=128)
        wu_v = wu.rearrange("(kt p) n -> p kt n", p=128)
        wd_v = wd.rearrange("(j p) n -> p j n", p=128)
        QUARTERS = [(0, 6), (6, 12), (12, 18), (18, 22)]
        SLABS = [(q, ja + 2 * s) for q, (ja, jb) in enumerate(QUARTERS) for s in range((jb - ja) // 2)]

        def load_gu(si):
            q, j0 = SLABS[si]
            S.dma("pool", wgs[si % 3].t[:], wg_v[:, :, j0 * 128:(j0 + 2) * 128], writes=[wgs[si % 3].b])
            S.dma("pool", wus[si % 3].t[:], wu_v[:, :, j0 * 128:(j0 + 2) * 128], writes=[wus[si % 3].b])

        def load_wd(q):
            ja, jb = QUARTERS[q]
            for jh in range(0, jb - ja, 2):
                S.dma("pool", wds[q % 2].t[:, jh:jh + 2, :], wd_v[:, ja + jh:ja + jh + 2, :], writes=[wds[q % 2].b])
        assert SLABS[0][1] == 0 and SLABS[1][1] == 2
        load_wd(0)
        gbank = [0]
        si = 0
        for q, (ja, jb) in enumerate(QUARTERS):
            if q + 1 < 4:
                load_wd(q + 1)
            for s in range((jb - ja) // 2):
                if si + 2 < len(SLABS):
                    load_gu(si + 2)
                wgt, wut = wgs[si % 3], wus[si % 3]
                for jc in range(2):
                    jj = 2 * s + jc
                    for (t0, NT, is_s) in TILES_B:
                        tsl = slice(t0, t0 + NT)
                        gbank[0] ^= 1
                        pbg, pbu = PB[gbank[0]], PB[2 + gbank[0]]
                        for (wt, pb_) in ((wgt, pbg), (wut, pbu)):
                            for kt in range(8):
                                S.op("pe", lambda e, kt=kt, wt=wt, pb_=pb_: e.matmul(
                                    pb_.t[:, 0:NT], wt.t[:, kt, jc * 128:(jc + 1) * 128], vT.t[:, kt, tsl], start=(kt == 0), stop=(kt == 7)),
                                    reads=[wt.b] + vT.allb(), writes=[pb_.b])
                        sg_ = sgt[gbank[0]]
                        S.op("act", lambda e, pbg=pbg, sg_=sg_: e.activation(out=sg_.t[:, 0:NT], in_=pbg.t[:, 0:NT], func=AF.Silu),
                             reads=[pbg.b], writes=[sg_.b])
                        S.op("dve", lambda e, pbu=pbu, sg_=sg_: TT(e, hT.t[:, jj, tsl], sg_.t[:, 0:NT], pbu.t[:, 0:NT], ALU.mult),
                             reads=[sg_.b, pbu.b], writes=[hT.sub(jj)])
                si += 1
            nj = jb - ja
            wdt = wds[q % 2]
            for (t0, NT, is_s) in TILES_B:
                tsl = slice(t0, t0 + NT)
                for m in range(8):
                    pb = PB[4 + m % 2]
                    for jj in range(nj):
                        S.op("pe", lambda e, jj=jj, m=m, pb=pb: e.matmul(pb.t[:, 0:NT], wdt.t[:, jj, m * 128:(m + 1) * 128], hT.t[:, jj, tsl],
                                                                         start=(jj == 0), stop=(jj == nj - 1)),
                             reads=[wdt.b, hT.sub(jj)], writes=[pb.b])
                    if not is_s:
                        S.op("dve", lambda e, m=m, pb=pb: e.scalar_tensor_tensor(
                            out=x1T.t[:, m, tsl], in0=pb.t[:, 0:NT], scalar=mod.t[:, 8 * MOD_G2 + m, 0:1], in1=x1T.t[:, m, tsl],
                            op0=ALU.mult, op1=ALU.add), reads=[pb.b, mod.b, x1T.sub(m)], writes=[x1T.sub(m)])
                    else:
                        S.op("dve", lambda e, m=m, pb=pb: TT(e, tmp2[0].t[:, 0:NT], pb.t[:, 0:NT], g2x.t[:, m, :], ALU.mult),
                             reads=[pb.b, g2x.b], writes=[tmp2[0].b])
                        S.op("dve", lambda e, m=m: TT(e, x1T.t[:, m, tsl], tmp2[0].t[:, 0:NT], x1T.t[:, m, tsl], ALU.add),
                             reads=[tmp2[0].b, x1T.sub(m)], writes=[x1T.sub(m)])
        S.barrier()
        ckpt("ffn")
        A.lo = LO_P2
        yTs = [A.alloc("yT%d" % i, [128, 8, 512], F32) for i in range(2)]
        ytm = [A.alloc("ytm%d" % i, [128, D], F32) for i in range(2)]
        print("arena final: lo=%d hi=%d" % (A.lo, A.hi))
        oi = [0]

        def f_part1(ti):
            t0, NT, is_s = TILES_B[ti]
            tsl = slice(t0, t0 + NT)
            yT = yTs[ti % 2]
            pbs = PB[6 + ti % 2]
            rs = rstdb[ti % 2]
            dfr = [None]
            for m in range(8):
                stat_accum(x1T.t[:, m, tsl], m, NT, pbs, defer=dfr)
                if m % 2 == 1:
                    yield
            stat_finish(NT, pbs, rs)
            yield
            for m in range(8):
                tq = tmp2[m % 2]
                S.op("dve", lambda e: TT(e, tq.t[:, 0:NT], x1T.t[:, m, tsl], rs.t[:, 0:NT], ALU.mult),
                     reads=[x1T.sub(m), rs.b], writes=[tq.b])
                if not is_s:
                    S.op("act", lambda e: e.activation(out=yT.t[:, m, 0:NT], in_=tq.t[:, 0:NT], func=AF.Identity,
                                                       scale=amod.t[:, 16 + m, 0:1], bias=mod.t[:, 8 * MOD_SHF + m, 0:1]),
                         reads=[tq.b, amod.b, mod.b], writes=[yT.sub(m)])
                else:
                    S.op("dve", lambda e: TT(e, tq.t[:, 0:NT], tq.t[:, 0:NT], afx.t[:, m, :], ALU.mult),
                         reads=[tq.b, afx.b], writes=[tq.b])
                    S.op("dve", lambda e: TT(e, yT.t[:, m, 0:NT], tq.t[:, 0:NT], shfx.t[:, m, :], ALU.add),
                         reads=[tq.b, shfx.b], writes=[yT.sub(m)])
                yield

        def f_part2(ti):
            t0, NT, is_s = TILES_B[ti]
            yT = yTs[ti % 2]
            for blk in range((NT + 127) // 128):
                rows = min(128, NT - blk * 128)
                yo = ytm[oi[0] % 2]
                oi[0] += 1
                for half in range(2):
                    pbt = PB[half]
                    for k4 in range(4):
                        kt = 4 * half + k4
                        S.op("pe", lambda e, kt=kt, k4=k4: e.transpose(
                            pbt.t[0:rows, k4 * 128:(k4 + 1) * 128], yT.t[:, kt, blk * 128:blk * 128 + rows], ident),
                            reads=[yT.sub(kt), cst.b], writes=[pbt.b])
                    if half == 0:
                        S.op("act", lambda e: e.activation(out=yo.t[0:rows, 0:512], in_=pbt.t[0:rows, :], func=AF.Copy),
                             reads=[pbt.b], writes=[yo.b])
                    else:
                        S.op("dve", lambda e: e.tensor_copy(out=yo.t[0:rows, 512:1024], in_=pbt.t[0:rows, :]),
                             reads=[pbt.b], writes=[yo.b])
                    yield
                S.dma("sp", yout[t0 + blk * 128:t0 + blk * 128 + rows, :], yo.t[0:rows, :], reads=[yo.b], buf=yo.b)
        import os as _os2
        if True:
            for ti_ in range(len(TILES_B)):
                drive([f_part1(ti_)])
                drive([f_part2(ti_)])
        else:
            drive([f_part1(0)])
            for ti_ in range(len(TILES_B)):
                drive([f_part2(ti_), f_part1(ti_ + 1) if ti_ + 1 < len(TILES_B) else None])
        S.barrier()
    return nc, dumps


def _prep_inputs(inp):
    cstv = _consts()
    prmv = _params(inp)
    BT, CT = _s5mats(inp)
    maps = []
    for i in range(NCORES):
        m = {}
        m["xin"] = np.ascontiguousarray(np.concatenate(
            [inp["x_prompt"][i], inp["x_sample"][NS * i:NS * (i + 1)].reshape(NS * LS, D)], axis=0), dtype=np.float32)
        m["cin"] = np.ascontiguousarray(np.concatenate(
            [inp["c_prompt"][i:i + 1], inp["c_sample"][NS * i:NS * (i + 1)]], axis=0), dtype=np.float32)
        m["wada"] = np.ascontiguousarray(inp["w_ada"][0], dtype=np.float32)
        m["wadaf"] = np.ascontiguousarray(inp["w_ada_f"], dtype=np.float32)
        m["win"] = np.ascontiguousarray(inp["w_in"][0], dtype=np.float32)
        m["wglu"] = np.ascontiguousarray(inp["w_glu"][0], dtype=np.float32)
        m["wout"] = np.ascontiguousarray(inp["w_out"][0], dtype=np.float32)
        m["wg"] = np.ascontiguousarray(inp["w_ffn_gate"][0], dtype=np.float32)
        m["wu"] = np.ascontiguousarray(inp["w_ffn_up"][0], dtype=np.float32)
        m["wd"] = np.ascontiguousarray(inp["w_ffn_down"][0], dtype=np.float32)
        m["cst"] = cstv
        m["prm"] = prmv
        m["s5bt"] = BT.reshape(128, -1)
        m["s5ct"] = CT.reshape(128, -1)
        m["stssd"] = np.ascontiguousarray(inp["state_ssd"][0, NS * i:NS * (i + 1)], dtype=np.float32)
        sc = inp["state_conv"][0, NS * i:NS * (i + 1)]
        m["stconv"] = np.ascontiguousarray(
            sc.reshape(NS, 3, 8, 128).transpose(3, 2, 0, 1).reshape(128, -1), dtype=np.float32)
        sr = inp["state_s5_re"][0, NS * i:NS * (i + 1)]
        si = inp["state_s5_im"][0, NS * i:NS * (i + 1)]
        st = np.stack([sr, si], 0).reshape(2, NS, 16, 128).transpose(3, 0, 2, 1)
        m["sts5"] = np.ascontiguousarray(st.reshape(128, -1), dtype=np.float32)
        maps.append(m)
    return maps


_CACHE = {}


def kernel(**inputs):
    inp = {k: np.asarray(v) for k, v in inputs.items()}
    if "nc" not in _CACHE:
        _CACHE["nc"] = build()[0]
    nc = _CACHE["nc"]
    maps = _prep_inputs(inp)
    res = run_bass_kernel_spmd(nc, maps, core_ids=list(range(NCORES)))
    R = res.results
    y_p = np.stack([R[i]["yout"][:SEQ] for i in range(NCORES)], 0)
    y_s = np.concatenate([R[i]["yout"][SEQ:].reshape(NS, LS, D) for i in range(NCORES)], 0)
    ssd_p = np.stack([R[i]["o_ssdp"].reshape(128, 8, 64).transpose(1, 2, 0) for i in range(NCORES)], 0)[None]
    ssd_s = np.concatenate([R[i]["o_ssds"] for i in range(NCORES)], 0)[None]
    conv = [R[i]["o_conv"].reshape(128, 8, 17, 3).transpose(2, 3, 1, 0).reshape(17, 3, 1024) for i in range(NCORES)]
    conv_p = np.stack([c[0] for c in conv], 0)[None]
    conv_s = np.concatenate([c[1:] for c in conv], 0)[None]
    s5 = [R[i]["o_s5"].reshape(128, 2, 16, 17).transpose(1, 3, 2, 0).reshape(2, 17, 32, 64) for i in range(NCORES)]
    re_p = np.stack([s[0, 0] for s in s5], 0)[None]
    re_s = np.concatenate([s[0, 1:] for s in s5], 0)[None]
    im_p = np.stack([s[1, 0] for s in s5], 0)[None]
    im_s = np.concatenate([s[1, 1:] for s in s5], 0)[None]
    f = lambda a: np.ascontiguousarray(a, dtype=np.float32)
    return (f(y_p), f(y_s), f(ssd_p), f(ssd_s), f(conv_p), f(conv_s), f(re_p), f(re_s), f(im_p), f(im_s))
```
